# Optimizing a Trainium2 kernel written in Bass

```python
import math
import jax, jax.numpy as jnp
from jax import lax
import numpy as np

D_MODEL = 1024
BATCH = 2
SEQ = 8192
DEPTH = 1

MIX_WIDTH = D_MODEL
ATTN_WIDTH = D_MODEL // 2
HYENA_WIDTH = MIX_WIDTH - ATTN_WIDTH
HEAD_DIM = 64
N_HEADS = ATTN_WIDTH // HEAD_DIM
DILATED_BRANCHES = ((128, 1), (512, 4), (2048, 16))
Q_BLOCK = 128
ROPE_THETA = 10000.0
HYENA_ORDER = 2
SHORT_CONV = 3
FILTER_BANDS = 16
FILTER_EMB_DIM = 1 + 2 * FILTER_BANDS
FILTER_WIDTH = 64
DECAY_TARGET = 1e-2
FAST_DECAY_PCT = 0.3
SLOW_DECAY_PCT = 1.5
D_FF = 4 * D_MODEL
N_MOD = 6
EPS = 1e-6
PROJ_WIDTH = 3 * ATTN_WIDTH + (HYENA_ORDER + 1) * HYENA_WIDTH

kernel_name = "hymba_hyena_dilated_attn_encoder_layer"

F32 = jnp.float32


def rmsnorm(x, g):
    xf = x.astype(F32)
    var = jnp.mean(xf * xf, axis=-1, keepdims=True)
    return (xf * lax.rsqrt(var + EPS) * g.astype(F32)).astype(x.dtype)


def rope(x, positions):
    half = HEAD_DIM // 2
    inv = ROPE_THETA ** (-jnp.arange(half, dtype=F32) / half)
    ang = positions[:, None] * inv[None, :]
    cos = jnp.cos(ang)[None, :, None, :]
    sin = jnp.sin(ang)[None, :, None, :]
    xf = x.astype(F32)
    x1, x2 = xf[..., :half], xf[..., half:]
    return jnp.concatenate([x1 * cos - x2 * sin, x2 * cos + x1 * sin], axis=-1).astype(x.dtype)


def dilated_window_attention(q, k, v):
    B, S, H, Dh = q.shape
    scale = Dh ** -0.5
    n_blocks = S // Q_BLOCK

    def block(start):
        q_idx = start + jnp.arange(Q_BLOCK)
        qb = lax.dynamic_slice_in_dim(q, start, Q_BLOCK, axis=1).astype(F32) * scale
        outs, lses = [], []
        for window, dil in DILATED_BRANCHES:
            half = window // 2 // dil
            offs = dil * jnp.arange(-half, half + 1)
            kv_idx = q_idx[:, None] + offs[None, :]
            valid = (kv_idx >= 0) & (kv_idx < S)
            kv_idx = jnp.clip(kv_idx, 0, S - 1)
            kg = k[:, kv_idx].astype(F32)
            vg = v[:, kv_idx].astype(F32)
            s = jnp.einsum('bqhd,bqkhd->bhqk', qb, kg)
            s = jnp.where(valid[None, None], s, -jnp.inf)
            m = jnp.max(s, axis=-1, keepdims=True)
            p = jnp.exp(s - m)
            l = jnp.sum(p, axis=-1, keepdims=True)
            o = jnp.einsum('bhqk,bqkhd->bqhd', p, vg) / jnp.transpose(l, (0, 2, 1, 3))
            outs.append(o)
            lses.append(jnp.transpose(m + jnp.log(l), (0, 2, 1, 3)))
        w = jax.nn.softmax(jnp.stack(lses, axis=0), axis=0)
        out = jnp.sum(w * jnp.stack(outs, axis=0), axis=0)
        return out.astype(q.dtype)

    blocks = lax.map(block, jnp.arange(n_blocks) * Q_BLOCK)
    return jnp.transpose(blocks, (1, 0, 2, 3, 4)).reshape(B, S, H, Dh)


def short_conv(u, w, b):
    up = jnp.pad(u, ((0, 0), (1, 1), (0, 0)))
    return up[:, :-2] * w[0] + up[:, 1:-1] * w[1] + up[:, 2:] * w[2] + b


def hyena_filters(L, f_w1, f_b1, f_fr1, f_w2, f_b2, f_fr2, f_w3):
    t = jnp.linspace(0.0, 1.0, L, dtype=F32)[:, None]
    w = 2.0 * math.pi * jnp.arange(L, dtype=F32)[:, None] / L
    f = jnp.linspace(1e-4, FILTER_BANDS - 1, FILTER_BANDS, dtype=F32)[None, :]
    z = jnp.concatenate([t, jnp.cos(f * w), -jnp.sin(f * w)], axis=-1)
    h = jnp.sin(f_fr1.astype(F32) * (z @ f_w1.astype(F32) + f_b1.astype(F32)))
    h = jnp.sin(f_fr2.astype(F32) * (h @ f_w2.astype(F32) + f_b2.astype(F32)))
    h = (h @ f_w3.astype(F32)).reshape(L, HYENA_ORDER, 2, HYENA_WIDTH)
    max_decay = math.log(DECAY_TARGET) / FAST_DECAY_PCT
    min_decay = math.log(DECAY_TARGET) / SLOW_DECAY_PCT
    deltas = jnp.abs(jnp.linspace(min_decay, max_decay, HYENA_WIDTH, dtype=F32))
    h = h * jnp.exp(-t[:, :, None, None] * deltas)
    fwd, bwd = h[:, :, 0], h[:, :, 1]
    two_sided = jnp.concatenate([fwd, jnp.zeros((1, HYENA_ORDER, HYENA_WIDTH), F32), bwd[:0:-1]], axis=0)
    return jnp.fft.rfft(two_sided, axis=0)


def long_conv(z, hf, bias):
    L = z.shape[1]
    Z = jnp.fft.rfft(z, n=2 * L, axis=1)
    y = jnp.fft.irfft(Z * hf[None], n=2 * L, axis=1)[:, :L]
    return y + z * bias


def hyena_mixer(u, conv_w, conv_b, f_w1, f_b1, f_fr1, f_w2, f_b2, f_fr2, f_w3, hbias):
    B, L, _ = u.shape
    u = short_conv(u.astype(F32), conv_w.astype(F32), conv_b.astype(F32))
    v, x1, x2 = jnp.split(u, HYENA_ORDER + 1, axis=-1)
    hf = hyena_filters(L, f_w1, f_b1, f_fr1, f_w2, f_b2, f_fr2, f_w3)
    hb = hbias.astype(F32)
    z = x1 * long_conv(v, hf[:, 0], hb[0])
    y = x2 * long_conv(z, hf[:, 1], hb[1])
    return y


def setup_inputs(seed: int = 0) -> dict:
    key = jax.random.key(seed)
    ks = jax.random.split(key, 24)
    nrm = lambda k, shape, s: jax.random.normal(k, shape, F32) * s
    gain = lambda k, n: 1.0 + 0.05 * jax.random.normal(k, (DEPTH, n), F32)
    return {
        "x": nrm(ks[0], (BATCH, SEQ, D_MODEL), 1.0),
        "c": nrm(ks[1], (BATCH, D_MODEL), 1.0),
        "w_ada": nrm(ks[2], (DEPTH, D_MODEL, N_MOD * D_MODEL), D_MODEL ** -0.5),
        "b_ada": nrm(ks[3], (DEPTH, N_MOD * D_MODEL), 0.01),
        "g_pre_mix": gain(ks[4], D_MODEL),
        "g_post_mix": gain(ks[5], D_MODEL),
        "g_pre_mlp": gain(ks[6], D_MODEL),
        "g_post_mlp": gain(ks[7], D_MODEL),
        "w_in": nrm(ks[8], (DEPTH, D_MODEL, PROJ_WIDTH), D_MODEL ** -0.5),
        "conv_w": nrm(ks[9], (DEPTH, SHORT_CONV, (HYENA_ORDER + 1) * HYENA_WIDTH), SHORT_CONV ** -0.5),
        "conv_b": nrm(ks[10], (DEPTH, (HYENA_ORDER + 1) * HYENA_WIDTH), 0.01),
        "filt_w1": nrm(ks[11], (DEPTH, FILTER_EMB_DIM, FILTER_WIDTH), FILTER_EMB_DIM ** -0.5),
        "filt_b1": nrm(ks[12], (DEPTH, FILTER_WIDTH), 0.1),
        "filt_freq1": gain(ks[13], FILTER_WIDTH),
        "filt_w2": nrm(ks[14], (DEPTH, FILTER_WIDTH, FILTER_WIDTH), FILTER_WIDTH ** -0.5),
        "filt_b2": nrm(ks[15], (DEPTH, FILTER_WIDTH), 0.1),
        "filt_freq2": gain(ks[16], FILTER_WIDTH),
        "filt_w3": nrm(ks[17], (DEPTH, FILTER_WIDTH, HYENA_ORDER * 2 * HYENA_WIDTH), FILTER_WIDTH ** -0.5),
        "hyena_bias": nrm(ks[18], (DEPTH, HYENA_ORDER, HYENA_WIDTH), 1.0),
        "g_attn_out": gain(ks[19], ATTN_WIDTH),
        "g_hyena_out": gain(ks[20], HYENA_WIDTH),
        "w_out": nrm(ks[21], (DEPTH, MIX_WIDTH, D_MODEL), MIX_WIDTH ** -0.5),
        "w_mlp1": nrm(ks[22], (DEPTH, D_MODEL, D_FF), D_MODEL ** -0.5),
        "w_mlp2": nrm(ks[23], (DEPTH, D_FF, D_MODEL), D_FF ** -0.5),
    }


def reference(x, c, w_ada, b_ada, g_pre_mix, g_post_mix, g_pre_mlp, g_post_mlp, w_in,
              conv_w, conv_b, filt_w1, filt_b1, filt_freq1, filt_w2, filt_b2, filt_freq2,
              filt_w3, hyena_bias, g_attn_out, g_hyena_out, w_out, w_mlp1, w_mlp2):
    B, S, D = x.shape
    positions = jnp.arange(S, dtype=F32)
    for l in range(DEPTH):
        mod = jax.nn.silu(c) @ w_ada[l] + b_ada[l]
        sh1, sc1, gt1, sh2, sc2, gt2 = [m[:, None, :] for m in jnp.split(mod, N_MOD, axis=-1)]

        h = rmsnorm(x, g_pre_mix[l]) * (1.0 + sc1) + sh1
        proj = h @ w_in[l]
        qkv, hy_in = proj[..., :3 * ATTN_WIDTH], proj[..., 3 * ATTN_WIDTH:]
        q, k, v = [t.reshape(B, S, N_HEADS, HEAD_DIM) for t in jnp.split(qkv, 3, axis=-1)]
        q = rope(q, positions)
        k = rope(k, positions)
        attn = dilated_window_attention(q, k, v).reshape(B, S, ATTN_WIDTH)
        hy = hyena_mixer(hy_in, conv_w[l], conv_b[l], filt_w1[l], filt_b1[l], filt_freq1[l],
                         filt_w2[l], filt_b2[l], filt_freq2[l], filt_w3[l], hyena_bias[l]).astype(x.dtype)
        mixed = jnp.concatenate([rmsnorm(attn, g_attn_out[l]), rmsnorm(hy, g_hyena_out[l])], axis=-1)
        y = mixed @ w_out[l]
        x = x + gt1 * rmsnorm(y, g_post_mix[l])

        h = rmsnorm(x, g_pre_mlp[l]) * (1.0 + sc2) + sh2
        f = jnp.square(jax.nn.relu(h @ w_mlp1[l])) @ w_mlp2[l]
        x = x + gt2 * rmsnorm(f, g_post_mlp[l])
    return x
```

```python
import math
import numpy as np
from concourse.bass_utils import run_bass_kernel_spmd

from contextlib import ExitStack
import concourse.bass as bass
import concourse.mybir as mybir

F32 = mybir.dt.float32
BF16 = mybir.dt.bfloat16
ALU = mybir.AluOpType
AF = mybir.ActivationFunctionType
AX = mybir.AxisListType

ENGS = ("pe", "act", "dve", "pool", "sp")


class Prog:
    def __init__(self, nc, sem_stack=None, prefix=""):
        self.nc = nc
        self.ops = []
        self.state = {}
        self.es = ExitStack()
        self.sem_stack = sem_stack if sem_stack is not None else self.es
        self.prefix = prefix
        self.ndma = 0

    def sb(self, name, shape, dt):
        return self.es.enter_context(self.nc.sbuf_tensor(self.prefix + "sb_" + name, list(shape), dt))

    def ps(self, name, shape, dt):
        return self.es.enter_context(self.nc.psum_tensor(self.prefix + "ps_" + name, list(shape), dt))

    def op(self, eng, fn, reads=(), writes=(), dma=False, grp=None):
        deps = set()
        psum_r = [b for b in reads if len(b) > 1 and b[0] == "p" and b[1].isupper()]
        if psum_r:
            reads = [b for b in reads if b not in psum_r]
            writes = list(writes) + [b for b in psum_r if b not in writes]
        for b in reads:
            st = self.state.setdefault(b, [None, []])
            if st[0] is not None:
                deps.add(st[0])
        for b in writes:
            st = self.state.setdefault(b, [None, []])
            if st[0] is not None:
                deps.add(st[0])
            deps.update(st[1])
        idx = len(self.ops)
        if dma and grp is None:
            grp = "dma%d" % (self.ndma % 12)
            self.ndma += 1
        self.ops.append(dict(eng=eng, fn=fn, deps=deps, dma=dma, grp=grp, sig=dma))
        for b in reads:
            self.state[b][1].append(idx)
        for b in writes:
            self.state[b] = [idx, []]
        return idx

    def dma(self, eng, out, in_, reads=(), writes=(), grp=None, key=None):
        if grp is None:
            grp = "dma_" + str(key if key is not None else (reads[0] if reads else writes[0]))
        return self.op(eng, lambda e: e.dma_start(out=out, in_=in_), reads, writes, dma=True, grp=grp)

    def build(self):
        nc = self.nc
        ops = self.ops

        def needs_wait(o, d):
            if d["dma"] or o["dma"]:
                return True
            if o["eng"] != d["eng"]:
                return True
            return o["eng"] != "pe"

        for o in ops:
            for di in o["deps"]:
                if needs_wait(o, ops[di]):
                    ops[di]["sig"] = True
        cnt = {}
        sems = {}

        def getsem(key):
            if key not in sems:
                sems[key] = self.sem_stack.enter_context(nc.semaphore(self.prefix + "s_" + str(key).replace(" ", "")))
            return sems[key]

        for o in ops:
            if not o["sig"]:
                continue
            if o["dma"]:
                key = o["grp"]
                cnt[key] = cnt.get(key, 0) + 16
                o["tok"] = (key, cnt[key], 16)
            else:
                base = "e_" + o["eng"]
                gen = cnt.get(base + "_gen", 0)
                key = "%s_%d" % (base, gen)
                cnt[key] = cnt.get(key, 0) + 1
                o["tok"] = (key, cnt[key], 1)
                if cnt[key] >= 30000:
                    cnt[base + "_gen"] = gen + 1
        for key in list(cnt.keys()):
            if not key.endswith("_gen"):
                getsem(key)
        per = {e: [] for e in ENGS}
        for o in ops:
            per[o["eng"]].append(o)
        final_dma = {k: v for k, v in cnt.items() if k.startswith("dma")}
        self.n_sems = len(sems)

        def replay(eng_name, e):
            waited = {}
            for o in per[eng_name]:
                for di in sorted(o["deps"]):
                    d = ops[di]
                    if not needs_wait(o, d):
                        continue
                    key, val, _ = d["tok"]
                    if waited.get(key, 0) < val:
                        e.wait_ge(sems[key], val)
                        waited[key] = val
                inst = o["fn"](e)
                if o["sig"]:
                    key, val, inc = o["tok"]
                    inst.then_inc(sems[key], inc)
            if eng_name == "sp":
                for key, val in final_dma.items():
                    if waited.get(key, 0) < val:
                        e.wait_ge(sems[key], val)

        with nc.Block() as block:
            @block.sync
            def _(e):
                replay("sp", e)

            @block.tensor
            def _(e):
                replay("pe", e)

            @block.scalar
            def _(e):
                replay("act", e)

            @block.vector
            def _(e):
                replay("dve", e)

            @block.gpsimd
            def _(e):
                replay("pool", e)
        self.es.close()


EPS = 1e-6


def _rs(P, eng_stats, ss, r, n, nm, reads, key):
    P.op("dve", lambda e: e.tensor_scalar(out=r, in0=ss, scalar1=1.0 / n, scalar2=EPS, op0=ALU.mult, op1=ALU.add),
         reads=reads, writes=[key])
    P.op("act", lambda e: e.activation(out=r, in_=r, func=AF.Sqrt), reads=[key], writes=[key])
    P.op("dve", lambda e: e.reciprocal(out=r, in_=r), reads=[key], writes=[key])


def build_phase2(NT=2048):
    nc = bass.Bass("TRN2", target_bir_lowering=False)
    P = Prog(nc)
    D = lambda name, shape, kind="ExternalInput": nc.dram_tensor(name, list(shape), F32, kind=kind).ap()
    x_tok = D("x_tok", [NT, 1024])
    mix_tok = D("mix_tok", [NT, 1024])
    mixT = D("mixT", [1024, NT])
    cvec = D("cvec", [128, 8])
    w_ada = D("w_ada", [1024, 4096])
    b_ada = D("b_ada", [1, 4096])
    gvec = D("gvec", [3, 1024])
    gmix = D("gmix", [128, 8])
    w_out = D("w_out", [1024, 1024])
    w1 = D("w1", [1024, 4096])
    w2 = D("w2", [4096, 1024])
    out = D("out", [NT, 1024], kind="ExternalOutput")
    x1s = D("x1s", [NT, 1024], kind="Internal")

    sb, ps = P.sb, P.ps
    wout_b = sb("wout_b", [128, 8, 1024], BF16)
    mods = sb("mods", [128, 4096], F32)
    sbc = sb("sbc", [128, 8, 128], F32)
    ones = sb("ones", [128, 128], F32)
    ident = sb("ident", [128, 128], BF16)
    identf = sb("identf", [128, 128], F32)
    stage = [sb("stage%d" % i, [128, 2048], F32) for i in range(2)]
    w1b = [sb("w1b%d" % i, [128, 8, 512], BF16) for i in range(2)]
    w2b = [sb("w2b%d" % i, [128, 4, 1024], BF16) for i in range(2)]
    h2T = sb("h2T", [128, 8, 1024], BF16)
    f2acc = sb("f2acc", [128, 8, 1024], F32)
    xt = [sb("xt%d" % i, [128, 1024], F32) for i in range(2)]
    mt = sb("mt", [128, 1024], F32)
    mTs = sb("mTs", [128, 8, 128], F32)
    mTb = [sb("mTb%d" % i, [128, 8, 128], BF16) for i in range(2)]
    ysb = sb("ysb", [128, 1024], F32)
    tmp = sb("tmp", [128, 1024], F32)
    tmp2 = sb("tmp2", [128, 1024], F32)
    h2b = sb("h2b", [128, 1024], BF16)
    rl = [sb("rl%d" % i, [128, 512], F32) for i in range(2)]
    aT = [sb("aT%d" % i, [128, 4, 512], BF16) for i in range(2)]
    small = sb("small", [128, 32], F32)
    csb = sb("csb", [128, 8], F32)
    gmx = sb("gmx", [128, 8], F32)
    brow = sb("brow", [1, 4096], F32)

    pA = ps("pA", [128, 1024], F32)
    pH = ps("pH", [128, 1024], F32)
    pT = ps("pT", [128, 1024], BF16)
    pM = [ps("pM%d" % i, [128, 512], F32) for i in range(2)]

    P.op("pool", lambda e: e.memset(ones[:, :], 1.0), writes=["ones"])
    P.op("pool", lambda e: e.memset(identf[:, :], 0.0), writes=["identf"])
    identd = D("identd", [128, 128])
    P.dma("sp", identf[:, :], identd[:, :], writes=["identf"])
    P.op("dve", lambda e: e.tensor_copy(out=ident[:, :], in_=identf[:, :]), reads=["identf"], writes=["ident"])
    P.dma("sp", csb[:, :], cvec[:, :], writes=["csb"])
    P.dma("sp", gmx[:, :], gmix[:, :], writes=["gmx"])
    P.dma("sp", brow[:, :], b_ada[:, :], writes=["brow"])
    P.op("act", lambda e: e.activation(out=csb[:, :], in_=csb[:, :], func=AF.Silu), reads=["csb"], writes=["csb"])
    for k in range(8):
        P.op("dve", lambda e, k=k: e.tensor_scalar(out=sbc[:, k, :], in0=ones[:, :], scalar1=csb[:, k:k + 1],
                                                    scalar2=None, op0=ALU.mult), reads=["ones", "csb"], writes=["sbc"])
    wa = w_ada.rearrange("(k p) n -> p k n", p=128)
    for blk in range(16):
        st = stage[blk % 2]
        skey = "stage%d" % (blk % 2)
        stv = st[:, :].rearrange("p (k n) -> p k n", k=8)
        P.dma("sp", stv, wa[:, :, blk * 256:(blk + 1) * 256], writes=[skey])
        pm = pM[blk % 2]
        pkey = "pM%d" % (blk % 2)
        for k in range(8):
            P.op("pe", lambda e, k=k, pm=pm, stv=stv: e.matmul(pm[:, 0:256], lhsT=sbc[:, k, :], rhs=stv[:, k, :],
                                                               start=(k == 0), stop=False),
                 reads=["sbc", skey], writes=[pkey])
        P.op("pe", lambda e, pm=pm, blk=blk: e.matmul(pm[:, 0:256], lhsT=ones[0:1, :], rhs=brow[0:1, blk * 256:(blk + 1) * 256],
                                                     start=False, stop=True), reads=["ones", "brow"], writes=[pkey])
        P.op("act", lambda e, pm=pm, blk=blk: e.activation(out=mods[:, blk * 256:(blk + 1) * 256], in_=pm[:, 0:256], func=AF.Copy),
             reads=[pkey], writes=["mods"])
    gb = stage[0][:, :]
    P.dma("sp", gb[:, 0:1024], gvec[0:1, :].to_broadcast((128, 1024)), writes=["stage0"])
    gb1 = stage[1][:, :]
    P.dma("sp", gb1[:, 0:1024], gvec[1:2, :].to_broadcast((128, 1024)), writes=["stage1"])
    P.dma("sp", gb1[:, 1024:2048], gvec[2:3, :].to_broadcast((128, 1024)), writes=["stage1"])
    G1, SH2, G2, G3 = mods[:, 0:1024], mods[:, 1024:2048], mods[:, 2048:3072], mods[:, 3072:4096]
    P.op("dve", lambda e: e.tensor_tensor(out=G1, in0=G1, in1=gb[:, 0:1024], op=ALU.mult), reads=["mods", "stage0"], writes=["mods"])
    P.op("dve", lambda e: e.scalar_tensor_tensor(out=G2, in0=G2, scalar=1.0, in1=gb1[:, 0:1024], op0=ALU.add, op1=ALU.mult),
         reads=["mods", "stage1"], writes=["mods"])
    P.op("dve", lambda e: e.tensor_tensor(out=G3, in0=G3, in1=gb1[:, 1024:2048], op=ALU.mult), reads=["mods", "stage1"], writes=["mods"])
    wo = w_out.rearrange("(k p) n -> p k n", p=128)
    for c4 in range(4):
        st = stage[c4 % 2]
        skey = "stage%d" % (c4 % 2)
        stv = st[:, :].rearrange("p (k n) -> p k n", k=2)
        P.dma("sp", stv, wo[:, 2 * c4:2 * c4 + 2, :], writes=[skey])
        for kk in range(2):
            k = 2 * c4 + kk
            P.op("dve" if kk == 0 else "pool", lambda e, k=k, kk=kk, stv=stv: e.tensor_scalar(
                out=wout_b[:, k, :], in0=stv[:, kk, :], scalar1=gmx[:, k:k + 1], scalar2=None, op0=ALU.mult),
                reads=[skey, "gmx"], writes=["wout_b"])

    def sumsq(src, dst, reads, key, eng="dve"):
        w = src.shape[-1]
        P.op("pool", lambda e: e.tensor_tensor(out=tmp2[:, 0:w], in0=src, in1=src, op=ALU.mult), reads=reads, writes=["tmp2"])
        P.op(eng, lambda e: e.tensor_reduce(out=dst, in_=tmp2[:, 0:w], axis=AX.X, op=ALU.add), reads=["tmp2"] + list(reads), writes=[key])

    nsm = [0]

    def smallcol(n=1):
        c = nsm[0] % 16
        nsm[0] += 1
        return small[:, 2 * c:2 * c + n], "small%d" % c

    w1v = w1.rearrange("(k p) n -> p k n", p=128)
    w2v = w2.rearrange("(c p) n -> p c n", p=128)
    mTv = mixT.rearrange("(k p) t -> p k t", p=128)
    it = [0]
    for half in range(NT // 1024):
        for tile in range(8):
            t0 = half * 1024 + tile * 128
            i = it[0]
            it[0] += 1
            X, xk = xt[i % 2], "xt%d" % (i % 2)
            MB, mbk = mTb[i % 2], "mTb%d" % (i % 2)
            P.dma("sp", X[:, :], x_tok[t0:t0 + 128, :], writes=[xk])
            P.dma("sp", mt[:, :], mix_tok[t0:t0 + 128, :], writes=["mt"])
            P.dma("sp", mTs[:, :, :], mTv[:, :, t0:t0 + 128], writes=["mTs"])
            P.op("pool", lambda e, MB=MB: e.tensor_copy(out=MB[:, :, :], in_=mTs[:, :, :]), reads=["mTs"], writes=[mbk])
            ss, ssk = smallcol(2)
            rr, rrk = smallcol(2)
            sumsq(mt[:, 0:512], ss[:, 0:1], ["mt"], ssk)
            sumsq(mt[:, 512:1024], ss[:, 1:2], ["mt", ssk], ssk)
            _rs(P, None, ss, rr, 512.0, "rmix", [ssk], rrk)
            for nh in range(2):
                for k in range(4):
                    P.op("pe", lambda e, k=k, nh=nh, MB=MB: e.matmul(pA[:, nh * 512:(nh + 1) * 512], lhsT=MB[:, k, :],
                                                                     rhs=wout_b[:, k, nh * 512:(nh + 1) * 512],
                                                                     start=(k == 0), stop=(k == 3)),
                         reads=[mbk, "wout_b"], writes=["pA%d" % nh])
                for k in range(4, 8):
                    P.op("pe", lambda e, k=k, nh=nh, MB=MB: e.matmul(pH[:, nh * 512:(nh + 1) * 512], lhsT=MB[:, k, :],
                                                                     rhs=wout_b[:, k, nh * 512:(nh + 1) * 512],
                                                                     start=(k == 4), stop=(k == 7)),
                         reads=[mbk, "wout_b"], writes=["pH%d" % nh])
            P.op("act", lambda e, rr=rr: e.activation(out=ysb[:, :], in_=pA[:, :], func=AF.Copy, scale=rr[:, 0:1]),
                 reads=["pA0", "pA1", rrk], writes=["ysb"])
            P.op("dve", lambda e, rr=rr: e.scalar_tensor_tensor(out=ysb[:, :], in0=pH[:, :], scalar=rr[:, 1:2], in1=ysb[:, :],
                                                                op0=ALU.mult, op1=ALU.add),
                 reads=["pH0", "pH1", rrk, "ysb"], writes=["ysb"])
            ss2, ss2k = smallcol(1)
            r2, r2k = smallcol(1)
            sumsq(ysb[:, :], ss2[:, 0:1], ["ysb"], ss2k)
            _rs(P, None, ss2, r2, 1024.0, "ry", [ss2k], r2k)
            P.op("dve", lambda e, r2=r2: e.scalar_tensor_tensor(out=tmp[:, :], in0=ysb[:, :], scalar=r2[:, 0:1], in1=G1,
                                                                op0=ALU.mult, op1=ALU.mult),
                 reads=["ysb", r2k, "mods"], writes=["tmp"])
            P.op("pool", lambda e, X=X: e.tensor_tensor(out=X[:, :], in0=tmp[:, :], in1=X[:, :], op=ALU.add),
                 reads=["tmp", xk], writes=[xk])
            P.dma("pool", x1s[t0:t0 + 128, :], X[:, :], reads=[xk], writes=["x1s%d" % (t0 // 128)])
            ss3, ss3k = smallcol(1)
            r3, r3k = smallcol(1)
            sumsq(X[:, :], ss3[:, 0:1], [xk], ss3k)
            _rs(P, None, ss3, r3, 1024.0, "r1", [ss3k], r3k)
            P.op("dve", lambda e, r3=r3, X=X: e.scalar_tensor_tensor(out=tmp[:, :], in0=X[:, :], scalar=r3[:, 0:1], in1=G2,
                                                                     op0=ALU.mult, op1=ALU.mult),
                 reads=[xk, r3k, "mods"], writes=["tmp"])
            P.op("pool", lambda e: e.tensor_tensor(out=h2b[:, :], in0=tmp[:, :], in1=SH2, op=ALU.add),
                 reads=["tmp", "mods"], writes=["h2b"])
            for k in range(8):
                P.op("pe", lambda e, k=k: e.transpose(out=pT[:, k * 128:(k + 1) * 128], in_=h2b[:, k * 128:(k + 1) * 128],
                                                      identity=ident[:, :]), reads=["h2b", "ident"], writes=["pT"])
            P.op("act", lambda e, tile=tile: e.activation(out=h2T[:, :, tile * 128:(tile + 1) * 128],
                                                          in_=pT[:, :].rearrange("p (k t) -> p k t", k=8), func=AF.Copy),
                 reads=["pT"], writes=["h2T"])
        for j in range(8):
            W1B, w1k = w1b[j % 2], "w1b%d" % (j % 2)
            W2B, w2k = w2b[j % 2], "w2b%d" % (j % 2)
            for hh in range(2):
                st, skey = stage[hh], "stage%d" % hh
                stv = st[:, :].rearrange("p (k n) -> p k n", k=8)
                P.dma("sp", stv, w1v[:, :, j * 512 + hh * 256:j * 512 + (hh + 1) * 256], writes=[skey])
                P.op("dve" if hh == 0 else "pool", lambda e, W1B=W1B, stv=stv, hh=hh: e.tensor_copy(
                    out=W1B[:, :, hh * 256:(hh + 1) * 256], in_=stv), reads=[skey], writes=[w1k])
            for hh in range(2):
                st, skey = stage[hh], "stage%d" % hh
                stv = st[:, :].rearrange("p (c n) -> p c n", c=2)
                P.dma("sp", stv, w2v[:, j * 4 + hh * 2:j * 4 + hh * 2 + 2, :], writes=[skey])
                P.op("dve" if hh == 0 else "pool", lambda e, W2B=W2B, stv=stv, hh=hh: e.tensor_copy(
                    out=W2B[:, hh * 2:hh * 2 + 2, :], in_=stv), reads=[skey], writes=[w2k])
            for tg in range(2):
                AT, atk = aT[tg], "aT%d" % tg
                for hc in range(4):
                    pm, pkey = pM[hc % 2], "pM%d" % (hc % 2)
                    RL, rlk = rl[hc % 2], "rl%d" % (hc % 2)
                    for k in range(8):
                        P.op("pe", lambda e, k=k, hc=hc, tg=tg, pm=pm, W1B=W1B: e.matmul(
                            pm[:, :], lhsT=W1B[:, k, hc * 128:(hc + 1) * 128], rhs=h2T[:, k, tg * 512:(tg + 1) * 512],
                            start=(k == 0), stop=(k == 7)), reads=[w1k, "h2T"], writes=[pkey])
                    P.op("act", lambda e, pm=pm, RL=RL: e.activation(out=RL[:, :], in_=pm[:, :], func=AF.Relu),
                         reads=[pkey], writes=[rlk])
                    P.op("pool", lambda e, RL=RL, AT=AT, hc=hc: e.tensor_tensor(out=AT[:, hc, :], in0=RL[:, :], in1=RL[:, :], op=ALU.mult),
                         reads=[rlk], writes=[atk])
                for tt in range(4):
                    tile = tg * 4 + tt
                    for nh in range(2):
                        pp, ppk = (pA, "pA%d" % nh) if tt % 2 == 0 else (pH, "pH%d" % nh)
                        for hc in range(4):
                            P.op("pe", lambda e, hc=hc, tt=tt, nh=nh, pp=pp, AT=AT, W2B=W2B: e.matmul(
                                pp[:, nh * 512:(nh + 1) * 512], lhsT=AT[:, hc, tt * 128:(tt + 1) * 128],
                                rhs=W2B[:, hc, nh * 512:(nh + 1) * 512], start=(hc == 0), stop=(hc == 3)),
                                reads=[atk, w2k], writes=[ppk])
                        fv = f2acc[:, tile, nh * 512:(nh + 1) * 512]
                        fk = "f2_%d_%d" % (tile, nh)
                        if j == 0:
                            P.op("dve", lambda e, fv=fv, pp=pp, nh=nh: e.tensor_copy(out=fv, in_=pp[:, nh * 512:(nh + 1) * 512]),
                                 reads=[ppk], writes=[fk])
                        else:
                            P.op("dve", lambda e, fv=fv, pp=pp, nh=nh: e.tensor_tensor(out=fv, in0=pp[:, nh * 512:(nh + 1) * 512], in1=fv, op=ALU.add),
                                 reads=[ppk, fk], writes=[fk])
        for tile in range(8):
            t0 = half * 1024 + tile * 128
            i = it[0]
            it[0] += 1
            X, xk = xt[i % 2], "xt%d" % (i % 2)
            P.dma("sp", X[:, :], x1s[t0:t0 + 128, :], reads=["x1s%d" % (t0 // 128)], writes=[xk], key=xk)
            fkeys = ["f2_%d_%d" % (tile, nh) for nh in range(2)]
            ss4, ss4k = smallcol(1)
            r4, r4k = smallcol(1)
            sumsq(f2acc[:, tile, :], ss4[:, 0:1], fkeys, ss4k)
            _rs(P, None, ss4, r4, 1024.0, "rf", [ss4k], r4k)
            P.op("dve", lambda e, r4=r4, tile=tile: e.scalar_tensor_tensor(out=tmp[:, :], in0=f2acc[:, tile, :], scalar=r4[:, 0:1], in1=G3,
                                                                          op0=ALU.mult, op1=ALU.mult),
                 reads=fkeys + [r4k, "mods"], writes=["tmp"])
            P.op("pool", lambda e, X=X: e.tensor_tensor(out=X[:, :], in0=tmp[:, :], in1=X[:, :], op=ALU.add),
                 reads=["tmp", xk], writes=[xk])
            P.dma("pool", out[t0:t0 + 128, :], X[:, :], reads=[xk], writes=["out%d" % (t0 // 128)])
    P.build()
    return nc


def run_phase2(inp, mixed):
    f = lambda a: np.ascontiguousarray(a, dtype=np.float32)
    nc = build_phase2()
    in_maps = []
    cols = np.r_[2048:3072, 3072:6144]
    for core in range(8):
        b, j = core // 4, core % 4
        sl = slice(j * 2048, (j + 1) * 2048)
        in_maps.append({
            "x_tok": f(inp["x"][b, sl]),
            "mix_tok": f(mixed[b, sl]),
            "mixT": f(mixed[b, sl].T),
            "cvec": f(inp["c"][b].reshape(8, 128).T),
            "w_ada": f(inp["w_ada"][0][:, cols]),
            "b_ada": f(inp["b_ada"][0][cols][None, :]),
            "gvec": f(np.stack([inp["g_post_mix"][0], inp["g_pre_mlp"][0], inp["g_post_mlp"][0]])),
            "gmix": f(np.concatenate([inp["g_attn_out"][0], inp["g_hyena_out"][0]]).reshape(8, 128).T),
            "w_out": f(inp["w_out"][0]),
            "w1": f(inp["w_mlp1"][0]),
            "w2": f(inp["w_mlp2"][0]),
            "identd": np.eye(128, dtype=np.float32),
        })
    res = run_bass_kernel_spmd(nc, in_maps, core_ids=list(range(8)))
    out = np.zeros((2, 8192, 1024), np.float32)
    for core in range(8):
        b, j = core // 4, core % 4
        out[b, j * 2048:(j + 1) * 2048] = res.results[core]["out"]
    return out


ST = 512
NTOK = 8192
NST = NTOK // ST


def _p1_common(P, nc, ncols_w, ST=512, SW=256):
    D = lambda name, shape, kind="ExternalInput": nc.dram_tensor(name, list(shape), F32, kind=kind).ap()
    H = {}
    H["xT"] = D("xT", [1024, NTOK])
    cvec = D("cvec", [128, 8])
    w_ada1 = D("w_ada1", [1024, 2048])
    b_ada1 = D("b_ada1", [128, 16])
    gpre = D("gpre", [128, 8])
    w_inc = D("w_inc", [1024, ncols_w])
    sb, ps = P.sb, P.ps
    H["wb"] = wb = sb("wb", [128, 8, ncols_w], BF16)
    stage = [sb("stage%d" % i, [128, 8, SW], F32) for i in range(2)]
    H["ST"] = ST
    H["stage"] = stage
    csb = sb("csb", [128, 8], F32)
    gp = sb("gp", [128, 8], F32)
    bsb = sb("bsb", [128, 16], F32)
    H["modc"] = modc = sb("modc", [128, 16], F32)
    H["G0"] = G0 = sb("G0", [128, 8], F32)
    H["onesb"] = onesb = sb("onesb", [128, 128], BF16)
    H["xs"] = [sb("xs%d" % i, [128, 8, ST], F32) for i in range(2)]
    H["sq"] = sb("sq", [128, 8, ST], BF16)
    H["hT"] = sb("hT", [128, 8, ST], BF16)
    H["rbc"] = sb("rbc", [128, ST], F32)
    H["pSS"] = pSS = ps("pSS", [128, 512], F32)

    P.op("pool", lambda e: e.memset(onesb[:, :], 1.0), writes=["onesb"])
    P.dma("sp", csb[:, :], cvec[:, :], writes=["csb"])
    P.dma("sp", gp[:, :], gpre[:, :], writes=["gp"])
    P.dma("sp", bsb[:, :], b_ada1[:, :], writes=["bsb"])
    P.op("act", lambda e: e.activation(out=csb[:, :], in_=csb[:, :], func=AF.Silu), reads=["csb"], writes=["csb"])
    wa = w_ada1.rearrange("(k p) n -> p k n", p=128)
    for blk in range(2048 // SW):
        st, skey = stage[blk % 2], "stage%d" % (blk % 2)
        P.dma("sp", st[:, :, :], wa[:, :, blk * SW:(blk + 1) * SW], writes=[skey])
        for jj in range(SW // 128):
            j = (SW // 128) * blk + jj
            for k in range(8):
                P.op("pe", lambda e, k=k, j=j, jj=jj, st=st: e.matmul(pSS[:, j:j + 1], lhsT=st[:, k, jj * 128:(jj + 1) * 128],
                                                                      rhs=csb[:, k:k + 1], start=(k == 0), stop=(k == 7)),
                     reads=[skey, "csb"], writes=["pSS"])
    P.op("dve", lambda e: e.tensor_tensor(out=modc[:, :], in0=pSS[:, 0:16], in1=bsb[:, :], op=ALU.add),
         reads=["pSS", "bsb"], writes=["modc"])
    P.op("dve", lambda e: e.scalar_tensor_tensor(out=G0[:, :], in0=modc[:, 8:16], scalar=1.0, in1=gp[:, :], op0=ALU.add, op1=ALU.mult),
         reads=["modc", "gp"], writes=["G0"])
    wv = w_inc.rearrange("(k p) n -> p k n", p=128)
    nb = ncols_w // SW
    for blk in range(nb):
        st, skey = stage[blk % 2], "stage%d" % (blk % 2)
        P.dma("sp", st[:, :, :], wv[:, :, blk * SW:(blk + 1) * SW], writes=[skey])
        P.op("dve" if blk % 2 == 0 else "pool", lambda e, st=st, blk=blk: e.tensor_copy(out=wb[:, :, blk * SW:(blk + 1) * SW], in_=st[:, :, :]),
             reads=[skey], writes=["wb"])
    return H


def _p1_load(P, H, st):
    ST = H["ST"]
    xs, xk = H["xs"][st % 2], "xs%d" % (st % 2)
    xTv = H["xT"].rearrange("(k p) t -> p k t", p=128)
    P.dma("sp", xs[:, :, :], xTv[:, :, st * ST:(st + 1) * ST], writes=[xk])


def _p1_norm(P, H, st):
    ST = H["ST"]
    xs, xk = H["xs"][st % 2], "xs%d" % (st % 2)
    sq, hT, rbc, pSS, onesb, modc, G0 = H["sq"], H["hT"], H["rbc"], H["pSS"], H["onesb"], H["modc"], H["G0"]
    P.op("act", lambda e: e.activation(out=sq[:, :, :], in_=xs[:, :, :], func=AF.Square), reads=[xk], writes=["sq"])
    for k in range(8):
        P.op("pe", lambda e, k=k: e.matmul(pSS[:, 0:ST], lhsT=onesb[:, :], rhs=sq[:, k, :], start=(k == 0), stop=(k == 7)),
             reads=["onesb", "sq"], writes=["pSS"])
    P.op("dve", lambda e: e.tensor_scalar(out=rbc[:, :], in0=pSS[:, 0:ST], scalar1=1.0 / 1024, scalar2=EPS, op0=ALU.mult, op1=ALU.add),
         reads=["pSS"], writes=["rbc"])
    P.op("act", lambda e: e.activation(out=rbc[:, :], in_=rbc[:, :], func=AF.Sqrt), reads=["rbc"], writes=["rbc"])
    P.op("dve", lambda e: e.reciprocal(out=rbc[:, :], in_=rbc[:, :]), reads=["rbc"], writes=["rbc"])
    for k in range(8):
        P.op("dve", lambda e, k=k: e.scalar_tensor_tensor(out=xs[:, k, :], in0=xs[:, k, :], scalar=G0[:, k:k + 1], in1=rbc[:, :],
                                                          op0=ALU.mult, op1=ALU.mult), reads=[xk, "G0", "rbc"], writes=[xk])
        P.op("act", lambda e, k=k: e.activation(out=hT[:, k, :], in_=xs[:, k, :], func=AF.Identity, bias=modc[:, k:k + 1], scale=1.0),
             reads=[xk, "modc"], writes=["hT"])


def build_attn():
    nc = bass.Bass("TRN2", target_bir_lowering=False)
    P = Prog(nc)
    D = lambda name, shape, kind="ExternalInput": nc.dram_tensor(name, list(shape), F32, kind=kind).ap()
    H = _p1_common(P, nc, 768)
    ropeC = D("ropeC", [128, NTOK])
    ropeS = D("ropeS", [128, NTOK])
    maskd = D("maskd", [128, 17 * 128])
    hseld = D("hseld", [128, 64])
    attn_o = D("attn_o", [NTOK, 128], kind="ExternalOutput")
    sb, ps = P.sb, P.ps
    wb, hT = H["wb"], H["hT"]
    QT = sb("QT", [128, NTOK], BF16)
    KT = sb("KT", [128, NTOK], BF16)
    Vaug = sb("Vaug", [128, 64, 2, 65], BF16)
    Mall = sb("Mall", [128, 17 * 128], BF16)
    mst = sb("mst", [128, 17 * 128], F32)
    hself = sb("hself", [128, 64], F32)
    hsel = sb("hsel", [128, 64], BF16)
    onesrow = sb("onesrow", [64, 128], BF16)
    rc = [sb("rc%d" % i, [128, ST], F32) for i in range(2)]
    rs_ = [sb("rs%d" % i, [128, ST], F32) for i in range(2)]
    t1 = sb("t1", [128, ST], F32)
    t2 = sb("t2", [128, ST], F32)
    sqk = sb("sqk", [128, ST], BF16)
    kmx = sb("kmx", [64, 2], F32)
    qn = sb("qn", [64, 128], F32)
    negm = sb("negm", [64, 128], BF16)
    PT = [sb("PT%d" % i, [128, 512], BF16) for i in range(2)]
    ao = [sb("ao%d" % i, [128, 128], F32) for i in range(2)]
    rec = sb("rec", [128, 4], F32)
    pA = ps("pA", [128, 512], F32)
    pB = ps("pB", [128, 512], F32)
    pV = ps("pV", [128, 512], F32)
    pN = H["pSS"]
    pS = [ps("pS%d" % i, [128, 512], F32) for i in range(2)]
    pO = [ps("pO%d" % i, [128, 2, 128], F32) for i in range(2)]

    P.dma("sp", mst[:, :], maskd[:, :], writes=["mst"])
    P.op("pool", lambda e: e.tensor_copy(out=Mall[:, :], in_=mst[:, :]), reads=["mst"], writes=["Mall"])
    P.dma("sp", hself[:, :], hseld[:, :], writes=["hself"])
    P.op("pool", lambda e: e.tensor_copy(out=hsel[:, :], in_=hself[:, :]), reads=["hself"], writes=["hsel"])
    P.op("pool", lambda e: e.memset(onesrow[:, :], 1.0), writes=["onesrow"])
    P.op("pool", lambda e: e.memset(kmx[:, :], 0.0), writes=["kmx"])
    P.op("pool", lambda e: e.memset(Vaug[:, :, :, 64:65], 1.0), writes=["Vaug"])

    _p1_load(P, H, 0)
    for st in range(NST):
        if st + 1 < NST:
            _p1_load(P, H, st + 1)
        C, ck = rc[st % 2], "rc%d" % (st % 2)
        S, sk = rs_[st % 2], "rs%d" % (st % 2)
        P.dma("sp", C[:, :], ropeC[:, st * ST:(st + 1) * ST], writes=[ck])
        P.dma("sp", S[:, :], ropeS[:, st * ST:(st + 1) * ST], writes=[sk])
        _p1_norm(P, H, st)
        for which, dst in ((0, QT), (1, KT)):
            dk = "QT" if which == 0 else "KT"
            c0 = which * 256
            for k in range(8):
                P.op("pe", lambda e, k=k, c0=c0: e.matmul(pA[:, :], lhsT=wb[:, k, c0:c0 + 128], rhs=hT[:, k, :], start=(k == 0), stop=(k == 7)),
                     reads=["wb", "hT"], writes=["pA"])
            for k in range(8):
                P.op("pe", lambda e, k=k, c0=c0: e.matmul(pB[:, :], lhsT=wb[:, k, c0 + 128:c0 + 256], rhs=hT[:, k, :], start=(k == 0), stop=(k == 7)),
                     reads=["wb", "hT"], writes=["pB"])
            P.op("dve", lambda e, C=C: e.tensor_tensor(out=t1[:, :], in0=pA[:, :], in1=C[:, :], op=ALU.mult), reads=["pA", ck], writes=["t1"])
            P.op("dve", lambda e, S=S: e.tensor_tensor(out=t2[:, :], in0=pB[:, :], in1=S[:, :], op=ALU.mult), reads=["pB", sk], writes=["t2"])
            P.op("pool", lambda e, dst=dst, st=st: e.tensor_tensor(out=dst[:, st * ST:(st + 1) * ST], in0=t1[:, :], in1=t2[:, :], op=ALU.add),
                 reads=["t1", "t2"], writes=[dk])
        P.op("pool", lambda e, st=st: e.tensor_tensor(out=sqk[:, :], in0=KT[:, st * ST:(st + 1) * ST], in1=KT[:, st * ST:(st + 1) * ST], op=ALU.mult),
             reads=["KT"], writes=["sqk"])
        P.op("pe", lambda e: e.matmul(pN[0:64, :], lhsT=hsel[:, :], rhs=sqk[:, :], start=True, stop=True), reads=["hsel", "sqk"], writes=["pSS"])
        P.op("dve", lambda e: e.tensor_reduce(out=kmx[:, 1:2], in_=pN[0:64, :], axis=AX.X, op=ALU.max), reads=["pSS", "kmx"], writes=["kmx"])
        P.op("dve", lambda e: e.tensor_tensor(out=kmx[:, 0:1], in0=kmx[:, 0:1], in1=kmx[:, 1:2], op=ALU.max), reads=["kmx"], writes=["kmx"])
        for tt in range(4):
            for k in range(8):
                P.op("pe", lambda e, k=k, tt=tt: e.matmul(pV[:, tt * 128:(tt + 1) * 128], lhsT=hT[:, k, tt * 128:(tt + 1) * 128],
                                                          rhs=wb[:, k, 512:640], start=(k == 0), stop=(k == 7)),
                     reads=["wb", "hT"], writes=["pV"])
        P.op("act", lambda e, st=st: e.activation(out=Vaug[:, st * 4:(st + 1) * 4, :, 0:64],
                                                  in_=pV[:, :].rearrange("p (t h d) -> p t h d", t=4, h=2), func=AF.Copy),
             reads=["pV"], writes=["Vaug"])
    P.op("act", lambda e: e.activation(out=kmx[:, 0:1], in_=kmx[:, 0:1], func=AF.Sqrt), reads=["kmx"], writes=["kmx"])
    cnt = [0]
    for jb in range(64):
        qs = slice(jb * 128, (jb + 1) * 128)
        P.op("pool", lambda e, qs=qs: e.tensor_tensor(out=sqk[:, 0:128], in0=QT[:, qs], in1=QT[:, qs], op=ALU.mult), reads=["QT"], writes=["sqk"])
        P.op("pe", lambda e: e.matmul(pN[0:64, 0:128], lhsT=hsel[:, :], rhs=sqk[:, 0:128], start=True, stop=True), reads=["hsel", "sqk"], writes=["pSS"])
        P.op("act", lambda e: e.activation(out=qn[:, :], in_=pN[0:64, 0:128], func=AF.Sqrt), reads=["pSS"], writes=["qn"])
        P.op("dve", lambda e: e.tensor_scalar(out=negm[:, :], in0=qn[:, :], scalar1=kmx[:, 0:1], scalar2=-1.0, op0=ALU.mult, op1=ALU.mult),
             reads=["qn", "kmx"], writes=["negm"])
        AO, aok = ao[jb % 2], "ao%d" % (jb % 2)
        PO, pok = pO[jb % 2], "pO%d" % (jb % 2)
        for h in range(2):
            hs = slice(64 * h, 64 * h + 64)
            dms = [dm for dm in range(-8, 9) if 0 <= jb + dm < 64]
            groups = [dms[i:i + 4] for i in range(0, len(dms), 4)]
            nmm = 0
            for grp in groups:
                g = cnt[0]
                cnt[0] += 1
                psx, psk = pS[g % 2], "pS%d" % (g % 2)
                ptx, ptk = PT[g % 2], "PT%d" % (g % 2)
                n = len(grp)
                for i, dm in enumerate(grp):
                    kc = jb + dm
                    P.op("pe", lambda e, i=i, kc=kc, hs=hs, qs=qs, psx=psx: e.matmul(psx[:, i * 128:(i + 1) * 128], lhsT=KT[hs, kc * 128:(kc + 1) * 128],
                                                                                      rhs=QT[hs, qs], start=True, stop=False),
                         reads=["KT", "QT"], writes=[psk])
                    P.op("pe", lambda e, i=i, h=h, psx=psx: e.matmul(psx[:, i * 128:(i + 1) * 128], lhsT=onesrow[32 * h:32 * h + 1, :],
                                                                     rhs=negm[32 * h:32 * h + 1, :], start=False, stop=True),
                         reads=["onesrow", "negm"], writes=[psk])
                P.op("act", lambda e, psx=psx, ptx=ptx, n=n: e.activation(out=ptx[:, 0:n * 128], in_=psx[:, 0:n * 128], func=AF.Exp, scale=0.125),
                     reads=[psk], writes=[ptk])
                m0 = (grp[0] + 8) * 128
                P.op("dve" if g % 2 == 0 else "pool", lambda e, ptx=ptx, n=n, m0=m0: e.tensor_tensor(
                    out=ptx[:, 0:n * 128], in0=ptx[:, 0:n * 128], in1=Mall[:, m0:m0 + n * 128], op=ALU.mult),
                    reads=[ptk, "Mall"], writes=[ptk])
                for i, dm in enumerate(grp):
                    kc = jb + dm
                    P.op("pe", lambda e, i=i, kc=kc, h=h, ptx=ptx, PO=PO, first=(nmm == 0), last=(nmm == len(dms) - 1): e.matmul(
                        PO[:, h, 0:65], lhsT=ptx[:, i * 128:(i + 1) * 128], rhs=Vaug[:, kc, h, :], start=first, stop=last),
                        reads=[ptk, "Vaug"], writes=[pok])
                    nmm += 1
            P.op("dve", lambda e, h=h, PO=PO: e.reciprocal(out=rec[:, h:h + 1], in_=PO[:, h, 64:65]), reads=[pok], writes=["rec%d" % h])
            P.op("dve", lambda e, h=h, PO=PO, AO=AO: e.tensor_scalar(out=AO[:, 64 * h:64 * h + 64], in0=PO[:, h, 0:64], scalar1=rec[:, h:h + 1],
                                                                     scalar2=None, op0=ALU.mult),
                 reads=[pok, "rec%d" % h], writes=[aok])
        P.dma("pool", attn_o[qs, :], AO[:, :], reads=[aok], writes=["attn_o%d" % jb])
    P.build()
    return nc


def _rope_tables():
    half = 32
    inv = (10000.0 ** (-np.arange(half, dtype=np.float32) / half)).astype(np.float32)
    pos = np.arange(NTOK, dtype=np.float32)
    ang = (pos[:, None] * inv[None, :]).astype(np.float32)
    cos = np.cos(ang).astype(np.float32).T
    sin = np.sin(ang).astype(np.float32).T
    C = np.concatenate([cos, cos, cos, cos], axis=0)
    S = np.concatenate([-sin, sin, -sin, sin], axis=0)
    return np.ascontiguousarray(C), np.ascontiguousarray(S)


def _attn_masks():
    o = np.arange(-8 * 128 - 127, 8 * 128 + 128)
    mult = ((np.abs(o) <= 64).astype(np.float32) + ((np.abs(o) <= 256) & (o % 4 == 0)).astype(np.float32)
            + ((np.abs(o) <= 1024) & (o % 16 == 0)).astype(np.float32))
    off0 = -(8 * 128 + 127)
    M = np.zeros((128, 17, 128), np.float32)
    kl = np.arange(128)[:, None]
    ql = np.arange(128)[None, :]
    for i, dm in enumerate(range(-8, 9)):
        M[:, i, :] = mult[(128 * dm + kl - ql) - off0]
    return M.reshape(128, 17 * 128)


def _perm_cols(w):
    w = w.reshape(w.shape[0], -1, 2, 32)
    return w[:, :, ::-1, :].reshape(w.shape[0], -1)


def run_attn(inp):
    f = lambda a: np.ascontiguousarray(a, dtype=np.float32)
    nc = build_attn()
    C, S = _rope_tables()
    M = _attn_masks()
    hsel = np.zeros((128, 64), np.float32)
    hsel[0:64, 0] = 1.0
    hsel[64:128, 32] = 1.0
    w_in = inp["w_in"][0]
    in_maps = []
    for core in range(8):
        b, g = core // 4, core % 4
        cs = slice(128 * g, 128 * g + 128)
        wq, wk, wv = w_in[:, 0:512][:, cs], w_in[:, 512:1024][:, cs], w_in[:, 1024:1536][:, cs]
        w_inc = np.concatenate([wq, _perm_cols(wq), wk, _perm_cols(wk), wv, np.zeros((1024, 128), np.float32)], axis=1)
        in_maps.append({
            "xT": f(inp["x"][b].T), "cvec": f(inp["c"][b].reshape(8, 128).T),
            "w_ada1": f(inp["w_ada"][0][:, 0:2048]), "b_ada1": f(inp["b_ada"][0][0:2048].reshape(16, 128).T),
            "gpre": f(inp["g_pre_mix"][0].reshape(8, 128).T), "w_inc": f(w_inc),
            "ropeC": C, "ropeS": S, "maskd": f(M), "hseld": hsel,
        })
    res = run_bass_kernel_spmd(nc, in_maps, core_ids=list(range(8)))
    attn = np.zeros((2, NTOK, 512), np.float32)
    for core in range(8):
        b, g = core // 4, core % 4
        attn[b, :, 128 * g:128 * g + 128] = res.results[core]["attn_o"]
    return attn


NFFT = 16384
NPQ = 4
NPQH = 8


def _hy_consts():
    N = NFFT
    a = np.arange(128)[:, None].astype(np.float64)
    k1 = np.arange(256)[None, :].astype(np.float64)
    F1cat = np.concatenate([np.cos(2 * np.pi * a * k1 / 256), -np.sin(2 * np.pi * a * k1 / 256)], 1)
    th = 2 * np.pi / N
    pm = (np.arange(128) % 64)[:, None].astype(np.float64)
    Tr, Ti = np.cos(th * pm * k1), -np.sin(th * pm * k1)
    TrTr, TiTi = np.concatenate([Tr, Tr], 1), np.concatenate([Ti, Ti], 1)
    pp = np.arange(64)[:, None].astype(np.float64)
    k2 = np.arange(64)[None, :].astype(np.float64)
    g2r, g2i = np.cos(2 * np.pi * pp * k2 / 64), -np.sin(2 * np.pi * pp * k2 / 64)
    Z0 = np.zeros((64, 64))
    G2r = np.block([[g2r, Z0], [Z0, g2r]])
    G2i = np.block([[g2i, Z0], [Z0, g2i]])
    R12 = np.concatenate([G2r, -G2i, G2i, G2r], 1)
    k1c = np.arange(256)[:, None].astype(np.float64)
    pcol = (np.arange(128) % 64)[None, :].astype(np.float64)
    T2r, T2i = np.cos(th * pcol * k1c), np.sin(th * pcol * k1c)
    TT2r = np.concatenate([T2r[0:128], T2r[0:128], T2r[128:256], T2r[128:256]], 1)
    TT2i = np.concatenate([T2i[0:128], T2i[0:128], T2i[128:256], T2i[128:256]], 1)
    aa = np.arange(128)[None, :].astype(np.float64)
    IF1c, IF1s = np.cos(2 * np.pi * aa * k1c / 256), -np.sin(2 * np.pi * aa * k1c / 256)
    IF1 = np.concatenate([IF1c[0:128], IF1s[0:128], IF1c[128:256], IF1s[128:256]], 1)
    t_lin = np.linspace(0.0, 1.0, 8192, dtype=np.float32)
    Swin = -t_lin.reshape(128, 64)
    L = 8192
    t = np.linspace(0.0, 1.0, L, dtype=np.float32)[:, None]
    w = (np.float32(2.0 * math.pi) * np.arange(L, dtype=np.float32)[:, None] / np.float32(L)).astype(np.float32)
    f = np.linspace(1e-4, 15, 16, dtype=np.float32)[None, :]
    zemb = np.concatenate([t, np.cos(f * w), -np.sin(f * w)], axis=-1).astype(np.float32)
    max_decay = math.log(1e-2) / 0.3
    min_decay = math.log(1e-2) / 1.5
    deltas = np.abs(np.linspace(min_decay, max_decay, 512, dtype=np.float32)).astype(np.float32)
    f32 = lambda x: np.ascontiguousarray(x, dtype=np.float32)
    return dict(F1cat=f32(F1cat), TrTr=f32(TrTr), TiTi=f32(TiTi), R12=f32(R12), TT2r=f32(TT2r), TT2i=f32(TT2i),
                IF1=f32(IF1), Swin=f32(Swin), zembT=f32(zemb.T), deltas=deltas)


def _hy_consts_h():
    N = NFFT
    a = np.arange(128)[:, None].astype(np.float64)
    k1 = np.arange(128)[None, :].astype(np.float64) + 0.5
    F1cat = np.concatenate([np.cos(2 * np.pi * a * k1 / 256), -np.sin(2 * np.pi * a * k1 / 256)], 1)
    th = 2 * np.pi / N
    pm = (np.arange(128) % 64)[:, None].astype(np.float64)
    Tr, Ti = np.cos(th * pm * k1), -np.sin(th * pm * k1)
    TrTr, TiTi = np.concatenate([Tr] * 4, 1), np.concatenate([Ti] * 4, 1)
    k1c = (np.arange(128)[:, None].astype(np.float64) + 0.5)
    pcol = (np.arange(128) % 64)[None, :].astype(np.float64)
    T2r, T2i = np.cos(th * pcol * k1c), np.sin(th * pcol * k1c)
    TT2r, TT2i = np.concatenate([T2r] * 4, 1), np.concatenate([T2i] * 4, 1)
    aa = np.arange(128)[None, :].astype(np.float64)
    IF1 = np.concatenate([np.cos(2 * np.pi * aa * k1c / 256), -np.sin(2 * np.pi * aa * k1c / 256)], 1)
    f32 = lambda x: np.ascontiguousarray(x, dtype=np.float32)
    return dict(F1cat_h=f32(F1cat), TrTr_h=f32(TrTr), TiTi_h=f32(TiTi), TT2r_h=f32(TT2r), TT2i_h=f32(TT2i), IF1_h=f32(IF1))


def build_hyena(stop=99):
    nc = bass.Bass("TRN2", target_bir_lowering=False)
    P = Prog(nc)
    D = lambda name, shape, kind="ExternalInput": nc.dram_tensor(name, list(shape), F32, kind=kind).ap()
    ST = 256
    H = _p1_common(P, nc, 512, ST=ST, SW=128)
    sb, ps = P.sb, P.ps
    wb, hT, pSS = H["wb"], H["hT"], H["pSS"]
    dF1cat, dTrTr, dTiTi, dR12 = D("F1cat", [128, 512]), D("TrTr", [128, 512]), D("TiTi", [128, 512]), D("R12", [128, 512])
    dTT2r, dTT2i, dIF1, dSwin = D("TT2r", [128, 512]), D("TT2i", [128, 512]), D("IF1", [128, 512]), D("Swin", [128, 64])
    dzemb = D("zembT", [33, NTOK])
    ddelta = D("delta", [128, 128])
    dhbcol = D("hbcol", [128, 128])
    dfw1, dfw2, dfw3, dfpar = D("fw1", [33, 64]), D("fw2", [64, 64]), D("fw3c", [64, 512]), D("fpar", [64, 4])
    dcw, dcb = D("cw", [128, 9]), D("cb", [128, 3])
    hy_oz = D("hy_oz", [128, 128, 64], kind="ExternalOutput")

    U = [sb("U%d" % i, [128, NTOK + 2], BF16) for i in range(3)]
    CV = sb("CV", [128, NTOK], BF16)
    Ob = sb("Ob", [128, NTOK], BF16)
    h2T = sb("h2T", [64, NTOK], BF16)
    Ap = sb("Ap", [128, 2, NPQ, 256], BF16)
    Hs = sb("Hs", [128, 2, NPQ, 256], BF16)
    Yb = sb("Yb", [128, 2, NPQ, 256], BF16)
    Bp = sb("Bp", [128, 2, 2, NPQ, 128], BF16)
    W1 = [sb("W1_%d" % i, [128, 512], F32) for i in range(2)]
    W2 = [sb("W2_%d" % i, [128, 512], F32) for i in range(2)]
    cpy = [sb("cpy%d" % i, [128, 512], F32) for i in range(2)]
    cpy2 = sb("cpy2", [128, 512], F32)
    tmpc = sb("tmpc", [128, 1024], F32)
    arg = tmpc[0:64, 0:512]
    argi = sb("argi", [64, 512], mybir.dt.int32)
    h1 = tmpc[0:64, 512:1024]
    F1b = sb("F1b", [128, 512], BF16)
    R12b = sb("R12b", [128, 512], BF16)
    IF1b = sb("IF1b", [128, 512], BF16)
    TrTr, TiTi = sb("TrTr", [128, 512], F32), sb("TiTi", [128, 512], F32)
    TT2r, TT2i = sb("TT2r", [128, 512], F32), sb("TT2i", [128, 512], F32)
    Swin = sb("Swin", [128, 64], F32)
    delta = sb("delta", [128, 128], F32)
    hbcol = sb("hbcol", [128, 128], F32)
    fw1, fw2 = sb("fw1", [33, 64], F32), sb("fw2", [64, 64], F32)
    fw3b = sb("fw3b", [64, 512], BF16)
    fpar = sb("fpar", [64, 8], F32)
    cw, cb = sb("cw", [128, 9], F32), sb("cb", [128, 3], F32)
    zc = [H["stage"][i][0:33, 0:4, :].rearrange("p k n -> p (k n)") for i in range(2)]
    win = sb("win", [128, 128], F32)
    fwbw = sb("fwbw", [128, 2, 128], F32)
    ot = [H["xs"][i][:, 0:2, :].rearrange("p k n -> p (k n)") for i in range(2)]

    pA1 = [ps("pA1_%d" % i, [128, 512], F32) for i in range(2)]
    pXr, pXi = ps("pXr", [128, 512], F32), ps("pXi", [128, 512], F32)
    pB = ps("pB", [128, 512], F32)
    pY = ps("pY", [128, 512], F32)
    pTz = ps("pTz", [128, 8, 128], BF16)
    ident = sb("ident", [128, 128], BF16)
    identf = W1[0]
    didn = D("identd", [128, 128])

    def ldcast(dst, dkey, src, n=512, parts=128):
        P.dma("sp", cpy2[0:parts, 0:n], src, writes=["cpy2"])
        P.op("dve", lambda e: e.tensor_copy(out=dst, in_=cpy2[0:parts, 0:n]), reads=["cpy2"], writes=[dkey])
    ldcast(F1b[:, :], "F1b", dF1cat[:, :])
    ldcast(R12b[:, :], "R12b", dR12[:, :])
    ldcast(IF1b[:, :], "IF1b", dIF1[:, :])
    ldcast(ident[:, :], "ident", didn[:, :], n=128)
    ldcast(fw3b[:, :], "fw3b", dfw3[:, :], parts=64)
    for dst, key, src in ((TrTr, "TrTr", dTrTr), (TiTi, "TiTi", dTiTi), (TT2r, "TT2r", dTT2r), (TT2i, "TT2i", dTT2i), (Swin, "Swin", dSwin),
                          (delta, "delta", ddelta), (hbcol, "hbcol", dhbcol), (fw1, "fw1", dfw1), (fw2, "fw2", dfw2), (cw, "cw", dcw), (cb, "cb", dcb)):
        P.dma("sp", dst[:, :], src[:, :], writes=[key])
    P.dma("sp", fpar[:, 0:4], dfpar[:, :], writes=["fpar"])
    i2p = 1.0 / (2.0 * math.pi)
    for (bc, fc, o0) in ((0, 1, 4), (2, 3, 6)):
        P.op("dve", lambda e, bc=bc, fc=fc, o0=o0: e.tensor_tensor(out=fpar[:, o0 + 1:o0 + 2], in0=fpar[:, bc:bc + 1], in1=fpar[:, fc:fc + 1], op=ALU.mult),
             reads=["fpar"], writes=["fpar"])
        P.op("dve", lambda e, o0=o0: e.tensor_scalar(out=fpar[:, o0 + 1:o0 + 2], in0=fpar[:, o0 + 1:o0 + 2], scalar1=i2p, scalar2=16.0, op0=ALU.mult, op1=ALU.add),
             reads=["fpar"], writes=["fpar"])
        P.op("dve", lambda e, fc=fc, o0=o0: e.tensor_scalar(out=fpar[:, o0:o0 + 1], in0=fpar[:, fc:fc + 1], scalar1=i2p, scalar2=None, op0=ALU.mult),
             reads=["fpar"], writes=["fpar"])
    for i in range(3):
        P.op("pool", lambda e, i=i: e.memset(U[i][:, 0:1], 0.0), writes=["U%d" % i])
        P.op("pool", lambda e, i=i: e.memset(U[i][:, NTOK + 1:NTOK + 2], 0.0), writes=["U%d" % i])

    nst = NTOK // ST
    _p1_load(P, H, 0)
    pU = [pXr, pXi, pY]
    for st in range(nst):
        if st + 1 < nst:
            _p1_load(P, H, st + 1)
        _p1_norm(P, H, st)
        for i in range(3):
            pk = ["pXr", "pXi", "pY"][i]
            for k in range(8):
                P.op("pe", lambda e, k=k, i=i: e.matmul(pU[i][:, 0:ST], lhsT=wb[:, k, i * 128:(i + 1) * 128], rhs=hT[:, k, :],
                                                        start=(k == 0), stop=(k == 7)), reads=["wb", "hT"], writes=[pk])
            P.op("act", lambda e, i=i, st=st: e.activation(out=U[i][:, 1 + st * ST:1 + (st + 1) * ST], in_=pU[i][:, 0:ST], func=AF.Copy),
                 reads=[pk], writes=["U%d" % i])

    if stop <= 1:
        P.dma("pool", hy_oz[:, 0:4, :], U[0][:, 1:257].rearrange("a (c p) -> a c p", p=64).bitcast(F32) if False else H["xs"][0][:, 0, :].rearrange("a (c p) -> a c p", p=64), reads=["U0", "U1", "U2", "xs0"], writes=["dbg"])
        P.build()
        return nc
    Zkeys = lambda i: ["Z%d_%d" % (i, s) for s in range(128 // (2 * NPQ))]
    Z = [U[i][:, 0:NTOK].rearrange("a (c p) -> a c p", p=64) for i in range(3)]
    CVs = CV[:, :].rearrange("c (a p) -> c a p", p=64)
    for i in range(3):
        for ch in range(8):
            j0 = ch * 1024
            P.op("dve", lambda e, i=i, j0=j0: e.tensor_scalar(out=tmpc[:, :], in0=U[i][:, j0:j0 + 1024], scalar1=cw[:, 3 * i:3 * i + 1],
                                                              scalar2=cb[:, i:i + 1], op0=ALU.mult, op1=ALU.add),
                 reads=["U%d" % i, "cw", "cb"], writes=["tmpc"])
            P.op("dve", lambda e, i=i, j0=j0: e.scalar_tensor_tensor(out=tmpc[:, :], in0=U[i][:, j0 + 1:j0 + 1025], scalar=cw[:, 3 * i + 1:3 * i + 2],
                                                                     in1=tmpc[:, :], op0=ALU.mult, op1=ALU.add),
                 reads=["U%d" % i, "cw", "tmpc"], writes=["tmpc"])
            P.op("dve", lambda e, i=i, j0=j0: e.scalar_tensor_tensor(out=CV[:, j0:j0 + 1024], in0=U[i][:, j0 + 2:j0 + 1026], scalar=cw[:, 3 * i + 2:3 * i + 3],
                                                                     in1=tmpc[:, :], op0=ALU.mult, op1=ALU.add),
                 reads=["U%d" % i, "cw", "tmpc"], writes=["CV"])
        for pg in range(8):
            for pi in range(8):
                p = pg * 8 + pi
                P.op("pe", lambda e, p=p, pi=pi: e.transpose(out=pTz[:, pi, :], in_=CVs[:, :, p], identity=ident[:, :]),
                     reads=["CV", "ident"], writes=["pTz"])
            P.op("act", lambda e, i=i, pg=pg: e.activation(out=Z[i][:, :, pg * 8:(pg + 1) * 8].rearrange("a c p -> a p c"), in_=pTz[:, :, :], func=AF.Copy),
                 reads=["pTz"], writes=["U%d" % i] + Zkeys(i))

    if stop <= 2:
        P.dma("pool", hy_oz[:, 0:4, :], H["xs"][0][:, 0, :].rearrange("a (c p) -> a c p", p=64), reads=["U0", "U1", "U2", "xs0"], writes=["dbg"])
        P.build()
        return nc
    for ch in range(16):
        zt, zk = zc[ch % 2], "stage%d" % (ch % 2)
        P.dma("sp", zt[:, :], dzemb[:, ch * 512:(ch + 1) * 512], writes=[zk])
        for layer in range(2):
            if layer == 0:
                P.op("pe", lambda e, zt=zt: e.matmul(pSS[0:64, :], lhsT=fw1[:, :], rhs=zt[:, :], start=True, stop=True), reads=["fw1", zk], writes=["pSS"])
            else:
                P.op("pe", lambda e: e.matmul(pSS[0:64, :], lhsT=fw2[:, :], rhs=h1[:, :], start=True, stop=True), reads=["fw2", "tmpc"], writes=["pSS"])
            fr, fb = (4, 5) if layer == 0 else (6, 7)
            P.op("dve", lambda e, fr=fr, fb=fb: e.tensor_scalar(out=arg[:, :], in0=pSS[0:64, :], scalar1=fpar[:, fr:fr + 1], scalar2=fpar[:, fb:fb + 1],
                                                                op0=ALU.mult, op1=ALU.add), reads=["pSS", "fpar"], writes=["tmpc"])
            P.op("dve", lambda e: e.tensor_copy(out=argi[:, :], in_=arg[:, :]), reads=["tmpc"], writes=["argi"])
            P.op("dve", lambda e: e.tensor_copy(out=h1[:, :], in_=argi[:, :]), reads=["argi", "tmpc"], writes=["tmpc"])
            P.op("dve", lambda e: e.tensor_tensor(out=arg[:, :], in0=arg[:, :], in1=h1[:, :], op=ALU.subtract), reads=["tmpc"], writes=["tmpc"])
            P.op("dve", lambda e: e.scalar_tensor_tensor(out=arg[:, :], in0=arg[:, :], scalar=0.5, in1=arg[:, :], op0=ALU.is_gt, op1=ALU.subtract),
                 reads=["tmpc"], writes=["tmpc"])
            if layer == 0:
                P.op("act", lambda e: e.activation(out=h1[:, :], in_=arg[:, :], func=AF.Sin, scale=-2.0 * math.pi), reads=["tmpc"], writes=["tmpc"])
            else:
                P.op("act", lambda e, ch=ch: e.activation(out=h2T[:, ch * 512:(ch + 1) * 512], in_=arg[:, :], func=AF.Sin, scale=-2.0 * math.pi),
                     reads=["tmpc"], writes=["h2T"])

    if stop <= 3:
        P.dma("pool", hy_oz[:, 0:4, :], H["xs"][0][:, 0, :].rearrange("a (c p) -> a c p", p=64), reads=["h2T", "xs0"], writes=["dbg"])
        P.build()
        return nc
    E3 = CV[:, :].rearrange("a (c p) -> a c p", p=64)
    O3 = Ob[:, :].rearrange("a (c p) -> a c p", p=64)
    h2s = h2T[:, :].rearrange("j (a p) -> j a p", p=64)
    cnt = [0]

    def twiddle(psrc, pkey, Tr_, Ti_, trk, tik, outr, outi, okey, view):
        g = cnt[0]
        cnt[0] += 1
        w1, w1k = W1[g % 2], "W1_%d" % (g % 2)
        w2, w2k = W2[g % 2], "W2_%d" % (g % 2)
        cp_, cpk = cpy[g % 2], "cpy%d" % (g % 2)
        P.op("act", lambda e: e.activation(out=cp_[:, :], in_=psrc[:, :], func=AF.Copy), reads=[pkey], writes=[cpk])
        P.op("dve", lambda e: e.tensor_tensor(out=w1[:, :], in0=psrc[:, :], in1=Tr_[:, :], op=ALU.mult), reads=[pkey, trk], writes=[w1k])
        P.op("pool", lambda e: e.tensor_tensor(out=w2[:, :], in0=cp_[:, :], in1=Ti_[:, :], op=ALU.mult), reads=[cpk, tik], writes=[w2k])
        w1r, w1i = view(w1)
        w2r, w2i = view(w2)
        P.op("dve", lambda e: e.tensor_tensor(out=outr, in0=w1r, in1=w2i, op=ALU.subtract), reads=[w1k, w2k], writes=[okey])
        P.op("pool", lambda e: e.tensor_tensor(out=outi, in0=w2r, in1=w1i, op=ALU.add), reads=[w1k, w2k], writes=[okey])

    v_fwd = lambda t: (t[:, 0:256], t[:, 256:512])
    v_inv = lambda t: (t[:, :].rearrange("k (c r x) -> k c r x", c=2, r=2)[:, :, 0, :], t[:, :].rearrange("k (c r x) -> k c r x", c=2, r=2)[:, :, 1, :])

    def fwd_stage1(src3, skey, c0):
        for q in range(NPQ):
            g = cnt[0]
            pa, pak = pA1[g % 2], "pA1_%d" % (g % 2)
            c = c0 + 2 * q
            P.op("pe", lambda e, c=c, pa=pa: e.matmul(pa[:, :], lhsT=src3[:, c:c + 2, :], rhs=F1b[:, :], start=True, stop=True),
                 reads=[skey, "F1b"], writes=[pak])
            twiddle(pa, pak, TrTr, TiTi, "TrTr", "TiTi", Ap[:, 0, q, :], Ap[:, 1, q, :], "Ap", v_fwd)

    G2r, G2in, G2i = R12b[:, 0:128], R12b[:, 128:256], R12b[:, 256:384]
    R1, R2 = R12b[:, 0:256], R12b[:, 256:512]

    for o in range(2):
        for p in range(64):
            P.op("pe", lambda e, p=p, o=o: e.matmul(pSS[:, 0:256], lhsT=h2s[:, :, p], rhs=fw3b[:, o * 256:(o + 1) * 256], start=True, stop=True),
                 reads=["h2T", "fw3b"], writes=["pSS"])
            P.op("act", lambda e, p=p: e.activation(out=win[:, :], in_=delta[:, :], func=AF.Exp, scale=Swin[:, p:p + 1]), reads=["delta", "Swin"], writes=["win"])
            for d in range(2):
                P.op("dve", lambda e, d=d: e.tensor_tensor(out=fwbw[:, d, :], in0=pSS[:, d * 128:(d + 1) * 128], in1=win[:, :], op=ALU.mult),
                     reads=["pSS", "win"], writes=["fwbw"])
            P.op("pool", lambda e, p=p: e.tensor_tensor(out=E3[:, :, p], in0=fwbw[:, 0, :], in1=fwbw[:, 1, :], op=ALU.add), reads=["fwbw"], writes=["CV"])
            P.op("pool", lambda e, p=p: e.tensor_tensor(out=O3[:, :, p], in0=fwbw[:, 0, :], in1=fwbw[:, 1, :], op=ALU.subtract), reads=["fwbw"], writes=["Ob"])
            if p == 0:
                P.op("pool", lambda e: e.tensor_copy(out=E3[0:1, :, 0], in_=fwbw[0:1, 0, :]), reads=["fwbw"], writes=["CV"])
                P.op("pool", lambda e: e.tensor_copy(out=O3[0:1, :, 0], in_=fwbw[0:1, 0, :]), reads=["fwbw"], writes=["Ob"])
        if stop <= 4:
            P.dma("pool", hy_oz[:, 0:4, :], H["xs"][0][:, 0, :].rearrange("a (c p) -> a c p", p=64), reads=["CV", "Ob", "xs0"], writes=["dbg"])
            P.build()
            return nc
        for sbt in range(128 // (2 * NPQ)):
            c0 = sbt * 2 * NPQ
            zk = "Z0_%d" % sbt
            for which, (src3, skey) in enumerate(((E3, "CV"), (O3, "Ob"))):
                fwd_stage1(src3, skey, c0)
                for qq in range(NPQ // 2):
                    rr = Ap[:, 0, 2 * qq:2 * qq + 2, :]
                    ri = Ap[:, 1, 2 * qq:2 * qq + 2, :]
                    if which == 0:
                        P.op("pe", lambda e, rr=rr: e.matmul(pXr[:, :], lhsT=G2r, rhs=rr, start=True, stop=False), reads=["R12b", "Ap"], writes=["pXr"])
                        P.op("pe", lambda e, ri=ri: e.matmul(pXr[:, :], lhsT=G2in, rhs=ri, start=False, stop=True), reads=["R12b", "Ap"], writes=["pXr"])
                        for j in range(2):
                            qg = sbt * NPQ + 2 * qq + j
                            P.op("act", lambda e, j=j, qq=qq, qg=qg, o=o: e.activation(out=Hs[:, 0, 2 * qq + j, :], in_=pXr[:, j * 256:(j + 1) * 256], func=AF.Identity,
                                                                                        bias=hbcol[:, o * 64 + qg:o * 64 + qg + 1], scale=1.0),
                                 reads=["pXr", "hbcol"], writes=["Hs"])
                    else:
                        P.op("pe", lambda e, rr=rr: e.matmul(pXi[:, :], lhsT=G2i, rhs=rr, start=True, stop=False), reads=["R12b", "Ap"], writes=["pXi"])
                        P.op("pe", lambda e, ri=ri: e.matmul(pXi[:, :], lhsT=G2r, rhs=ri, start=False, stop=True), reads=["R12b", "Ap"], writes=["pXi"])
                        P.op("act", lambda e, qq=qq: e.activation(out=Hs[:, 1, 2 * qq:2 * qq + 2, :], in_=pXi[:, :].rearrange("k (q x) -> k q x", q=2), func=AF.Copy),
                             reads=["pXi"], writes=["Hs"])
            if stop <= 6:
                P.dma("pool", hy_oz[:, 0:4, :], H["xs"][0][:, 0, :].rearrange("a (c p) -> a c p", p=64), reads=["CV", "Ob", "xs0", "Ap", "Hs", "Yb", "Bp", "pY"], writes=["dbg"])
                P.build()
                return nc
            fwd_stage1(Z[0], zk, c0)
            for qq in range(NPQ // 2):
                rr = Ap[:, 0, 2 * qq:2 * qq + 2, :]
                ri = Ap[:, 1, 2 * qq:2 * qq + 2, :]
                P.op("pe", lambda e, rr=rr: e.matmul(pXr[:, :], lhsT=G2r, rhs=rr, start=True, stop=False), reads=["R12b", "Ap"], writes=["pXr"])
                P.op("pe", lambda e, ri=ri: e.matmul(pXr[:, :], lhsT=G2in, rhs=ri, start=False, stop=True), reads=["R12b", "Ap"], writes=["pXr"])
                P.op("pe", lambda e, rr=rr: e.matmul(pXi[:, :], lhsT=G2i, rhs=rr, start=True, stop=False), reads=["R12b", "Ap"], writes=["pXi"])
                P.op("pe", lambda e, ri=ri: e.matmul(pXi[:, :], lhsT=G2r, rhs=ri, start=False, stop=True), reads=["R12b", "Ap"], writes=["pXi"])
                g = cnt[0]
                cnt[0] += 1
                w1, w1k = W1[g % 2], "W1_%d" % (g % 2)
                w2, w2k = W2[g % 2], "W2_%d" % (g % 2)
                cx, cxk = cpy[g % 2], "cpy%d" % (g % 2)
                hr = Hs[:, 0, 2 * qq:2 * qq + 2, :].rearrange("k q x -> k (q x)")
                hi = Hs[:, 1, 2 * qq:2 * qq + 2, :].rearrange("k q x -> k (q x)")
                yr = Yb[:, 0, 2 * qq:2 * qq + 2, :].rearrange("k q x -> k (q x)")
                yi = Yb[:, 1, 2 * qq:2 * qq + 2, :].rearrange("k q x -> k (q x)")
                P.op("act", lambda e, cx=cx: e.activation(out=cx[:, :], in_=pXi[:, :], func=AF.Copy), reads=["pXi"], writes=[cxk])
                P.op("act", lambda e: e.activation(out=cpy2[:, :], in_=pXr[:, :], func=AF.Copy), reads=["pXr"], writes=["cpy2"])
                P.op("dve", lambda e, w1=w1, hr=hr: e.tensor_tensor(out=w1[:, :], in0=pXr[:, :], in1=hr, op=ALU.mult), reads=["pXr", "Hs"], writes=[w1k])
                P.op("pool", lambda e, w2=w2, cx=cx, hi=hi: e.tensor_tensor(out=w2[:, :], in0=cx[:, :], in1=hi, op=ALU.mult), reads=[cxk, "Hs"], writes=[w2k])
                P.op("dve", lambda e, w1=w1, w2=w2, yr=yr: e.tensor_tensor(out=yr, in0=w1[:, :], in1=w2[:, :], op=ALU.subtract), reads=[w1k, w2k], writes=["Yb"])
                P.op("dve", lambda e, w1=w1, hr=hr: e.tensor_tensor(out=w1[:, :], in0=pXi[:, :], in1=hr, op=ALU.mult), reads=["pXi", "Hs", "Yb"], writes=[w1k])
                P.op("pool", lambda e, w2=w2, hi=hi: e.tensor_tensor(out=w2[:, :], in0=cpy2[:, :], in1=hi, op=ALU.mult), reads=["cpy2", "Hs", "Yb"], writes=[w2k])
                P.op("pool", lambda e, w1=w1, w2=w2, yi=yi: e.tensor_tensor(out=yi, in0=w1[:, :], in1=w2[:, :], op=ALU.add), reads=[w1k, w2k], writes=["Yb"])
            if stop <= 7:
                P.dma("pool", hy_oz[:, 0:4, :], H["xs"][0][:, 0, :].rearrange("a (c p) -> a c p", p=64), reads=["CV", "Ob", "xs0", "Ap", "Hs", "Yb", "Bp", "pY"], writes=["dbg"])
                P.build()
                return nc
            for q in range(NPQ):
                for kc in range(2):
                    P.op("pe", lambda e, q=q, kc=kc: e.matmul(pB[:, kc * 256:(kc + 1) * 256], lhsT=Yb[:, 0, q, kc * 128:(kc + 1) * 128], rhs=R1, start=True, stop=False),
                         reads=["Yb", "R12b"], writes=["pB"])
                    P.op("pe", lambda e, q=q, kc=kc: e.matmul(pB[:, kc * 256:(kc + 1) * 256], lhsT=Yb[:, 1, q, kc * 128:(kc + 1) * 128], rhs=R2, start=False, stop=True),
                         reads=["Yb", "R12b"], writes=["pB"])
                twiddle(pB, "pB", TT2r, TT2i, "TT2r", "TT2i", Bp[:, :, 0, q, :], Bp[:, :, 1, q, :], "Bp", v_inv)
            if stop <= 8:
                P.dma("pool", hy_oz[:, 0:4, :], H["xs"][0][:, 0, :].rearrange("a (c p) -> a c p", p=64), reads=["CV", "Ob", "xs0", "Ap", "Hs", "Yb", "Bp", "pY"], writes=["dbg"])
                P.build()
                return nc
            for hh in range(NPQ // 4):
                n = 0
                for kc in range(2):
                    for r in range(2):
                        rhs = Bp[:, kc, r, 4 * hh:4 * hh + 4, :]
                        lt = IF1b[:, (2 * kc + r) * 128:(2 * kc + r + 1) * 128]
                        P.op("pe", lambda e, rhs=rhs, lt=lt, n=n: e.matmul(pY[:, :], lhsT=lt, rhs=rhs, start=(n == 0), stop=(n == 3)),
                             reads=["Bp", "IF1b"], writes=["pY"])
                        n += 1
                cc = c0 + 8 * hh
                if o == 0:
                    P.op("dve", lambda e, cc=cc: e.scalar_tensor_tensor(out=Z[0][:, cc:cc + 8, :], in0=pY[:, :].rearrange("a (c p) -> a c p", p=64), scalar=1.0 / NFFT,
                                                                        in1=Z[1][:, cc:cc + 8, :], op0=ALU.mult, op1=ALU.mult),
                         reads=["pY", "U1"] + Zkeys(1), writes=[zk])
                else:
                    g = cnt[0]
                    cnt[0] += 1
                    OT, otk = ot[g % 2], "xs%d" % (g % 2)
                    P.op("dve", lambda e, cc=cc, OT=OT: e.scalar_tensor_tensor(out=OT[:, :].rearrange("a (c p) -> a c p", p=64), in0=pY[:, :].rearrange("a (c p) -> a c p", p=64),
                                                                               scalar=1.0 / NFFT, in1=Z[2][:, cc:cc + 8, :], op0=ALU.mult, op1=ALU.mult),
                         reads=["pY", "U2"] + Zkeys(2), writes=[otk])
                    P.dma("pool", hy_oz[:, cc:cc + 8, :], OT[:, :].rearrange("a (c p) -> a c p", p=64), reads=[otk], writes=["hy_%d" % cc])
    P.build()
    return nc


def run_hyena(inp, stop=99):
    f = lambda a: np.ascontiguousarray(a, dtype=np.float32)
    nc = build_hyena(stop)
    K = _hy_consts()
    w_in = inp["w_in"][0]
    in_maps = []
    for core in range(8):
        b, g = core // 4, core % 4
        cs = slice(128 * g, 128 * g + 128)
        hy_w = w_in[:, 1536:]
        w_inc = np.concatenate([hy_w[:, 0:512][:, cs], hy_w[:, 512:1024][:, cs], hy_w[:, 1024:1536][:, cs], np.zeros((1024, 128), np.float32)], axis=1)
        cwv = inp["conv_w"][0].reshape(3, 3, 512)[:, :, cs]
        cbv = inp["conv_b"][0].reshape(3, 512)[:, cs]
        w3 = inp["filt_w3"][0].reshape(64, 2, 2, 512)[:, :, :, cs].reshape(64, 512)
        hb = inp["hyena_bias"][0][:, cs]
        hbcol = np.zeros((128, 2, 64), np.float32)
        for cp in range(2):
            hbcol[64 * cp:64 * cp + 64, :, :] = hb[:, cp::2][None, :, :]
        in_maps.append({
            "xT": f(inp["x"][b].T), "cvec": f(inp["c"][b].reshape(8, 128).T),
            "w_ada1": f(inp["w_ada"][0][:, 0:2048]), "b_ada1": f(inp["b_ada"][0][0:2048].reshape(16, 128).T),
            "gpre": f(inp["g_pre_mix"][0].reshape(8, 128).T), "w_inc": f(w_inc),
            "F1cat": K["F1cat"], "TrTr": K["TrTr"], "TiTi": K["TiTi"], "R12": K["R12"], "TT2r": K["TT2r"], "TT2i": K["TT2i"],
            "IF1": K["IF1"], "Swin": K["Swin"], "zembT": K["zembT"],
            "delta": f(np.broadcast_to(K["deltas"][cs][None, :], (128, 128))),
            "hbcol": f(hbcol.reshape(128, 128)),
            "fw1": f(inp["filt_w1"][0]), "fw2": f(inp["filt_w2"][0]), "fw3c": f(w3),
            "fpar": f(np.stack([inp["filt_b1"][0], inp["filt_freq1"][0], inp["filt_b2"][0], inp["filt_freq2"][0]], axis=1)),
            "cw": f(cwv.transpose(2, 1, 0).reshape(128, 9)), "cb": f(cbv.T),
            "identd": np.eye(128, dtype=np.float32),
        })
    res = run_bass_kernel_spmd(nc, in_maps, core_ids=list(range(8)))
    hy = np.zeros((2, NTOK, 512), np.float32)
    for core in range(8):
        b, g = core // 4, core % 4
        oz = res.results[core]["hy_oz"]
        hy[b, :, 128 * g:128 * g + 128] = oz.transpose(0, 2, 1).reshape(NTOK, 128)
    return hy


NW = 4096
NOWN = 2048


class DramReg:
    def __init__(self, nc):
        self.nc = nc
        self.t = {}

    def __call__(self, name, shape, kind="ExternalInput", dt=None):
        if name not in self.t:
            self.t[name] = self.nc.dram_tensor(name, list(shape), dt or F32, kind=kind).ap()
        return self.t[name]


def _p1_common_f(P, D, ncols_w, wname, xname, ntok, ST=512, SW=256, cast_w=True):
    H = {}
    H["xT"] = D(xname, [1024, ntok])
    cvec = D("cvec", [128, 8])
    w_ada1 = D("w_ada1", [1024, 2048])
    b_ada1 = D("b_ada1", [128, 16])
    gpre = D("gpre", [128, 8])
    H["w_inc"] = D(wname[0], wname[1])
    sb, ps = P.sb, P.ps
    H["wb"] = sb("wb", [128, 8, ncols_w], BF16)
    H["stage"] = stage = [sb("stage%d" % i, [128, 8, SW], F32) for i in range(2)]
    H["ST"], H["SW"] = ST, SW
    csb = sb("csb", [128, 8], F32)
    gp = sb("gp", [128, 8], F32)
    bsb = sb("bsb", [128, 16], F32)
    H["modc"] = modc = sb("modc", [128, 16], F32)
    H["G0"] = G0 = sb("G0", [128, 8], F32)
    H["onesb"] = onesb = sb("onesb", [128, 128], BF16)
    H["xs"] = [sb("xs%d" % i, [128, 8, ST], F32) for i in range(2)]
    H["sqR"] = [sb("sq%d" % i, [128, 8, ST], BF16) for i in range(2)]
    H["hTR"] = [sb("hT%d" % i, [128, 8, ST], BF16) for i in range(2)]
    H["rbcR"] = [sb("rbc%d" % i, [128, ST], F32) for i in range(2)]
    H["sq"], H["hT"], H["rbc"] = H["sqR"][0], H["hTR"][0], H["rbcR"][0]
    H["pSS"] = pSS = ps("pSS", [128, 512], F32)
    P.op("pool", lambda e: e.memset(onesb[:, :], 1.0), writes=["onesb"])
    P.dma("sp", csb[:, :], cvec[:, :], writes=["csb"])
    P.dma("sp", gp[:, :], gpre[:, :], writes=["gp"])
    P.dma("sp", bsb[:, :], b_ada1[:, :], writes=["bsb"])
    P.op("act", lambda e: e.activation(out=csb[:, :], in_=csb[:, :], func=AF.Silu), reads=["csb"], writes=["csb"])
    wa = w_ada1.rearrange("(k p) n -> p k n", p=128)
    for blk in range(2048 // SW):
        st, skey = stage[blk % 2], "stage%d" % (blk % 2)
        P.dma("sp", st[:, :, :], wa[:, :, blk * SW:(blk + 1) * SW], writes=[skey])
        for jj in range(SW // 128):
            j = (SW // 128) * blk + jj
            for k in range(8):
                P.op("pe", lambda e, k=k, j=j, jj=jj, st=st: e.matmul(pSS[:, j:j + 1], lhsT=st[:, k, jj * 128:(jj + 1) * 128],
                                                                      rhs=csb[:, k:k + 1], start=(k == 0), stop=(k == 7)),
                     reads=[skey, "csb"], writes=["pSS"])
    P.op("dve", lambda e: e.tensor_tensor(out=modc[:, :], in0=pSS[:, 0:16], in1=bsb[:, :], op=ALU.add),
         reads=["pSS", "bsb"], writes=["modc"])
    P.op("dve", lambda e: e.scalar_tensor_tensor(out=G0[:, :], in0=modc[:, 8:16], scalar=1.0, in1=gp[:, :], op0=ALU.add, op1=ALU.mult),
         reads=["modc", "gp"], writes=["G0"])
    return H


def _p1_norm_r(P, H, st):
    ST = H["ST"]
    r = st % 2
    xs, xk = H["xs"][r], "xs%d" % r
    sq, hT, rbc = H["sqR"][r], H["hTR"][r], H["rbcR"][r]
    sqk, hk, rk = "sq%d" % r, "hT%d" % r, "rbc%d" % r
    pSS, onesb, modc, G0 = H["pSS"], H["onesb"], H["modc"], H["G0"]
    P.op("act", lambda e: e.activation(out=sq[:, :, :], in_=xs[:, :, :], func=AF.Square), reads=[xk], writes=[sqk])
    for k in range(8):
        P.op("pe", lambda e, k=k: e.matmul(pSS[:, 0:ST], lhsT=onesb[:, :], rhs=sq[:, k, :], start=(k == 0), stop=(k == 7)),
             reads=["onesb", sqk], writes=["pSS"])
    P.op("dve", lambda e: e.tensor_scalar(out=rbc[:, :], in0=pSS[:, 0:ST], scalar1=1.0 / 1024, scalar2=EPS, op0=ALU.mult, op1=ALU.add),
         reads=["pSS"], writes=[rk])
    P.op("act", lambda e: e.activation(out=rbc[:, :], in_=rbc[:, :], func=AF.Sqrt), reads=[rk], writes=[rk])
    P.op("dve", lambda e: e.reciprocal(out=rbc[:, :], in_=rbc[:, :]), reads=[rk], writes=[rk])
    for k in range(8):
        P.op("dve", lambda e, k=k: e.scalar_tensor_tensor(out=xs[:, k, :], in0=xs[:, k, :], scalar=G0[:, k:k + 1], in1=rbc[:, :],
                                                          op0=ALU.mult, op1=ALU.mult), reads=[xk, "G0", rk], writes=[xk])
        P.op("act", lambda e, k=k: e.activation(out=hT[:, k, :], in_=xs[:, k, :], func=AF.Identity, bias=modc[:, k:k + 1], scale=1.0),
             reads=[xk, "modc"], writes=[hk])
    return hT, hk


def _cast_w(P, H, wv, ncols, col0=0):
    SW, stage, wb = H["SW"], H["stage"], H["wb"]
    for blk in range(ncols // SW):
        st, skey = stage[blk % 2], "stage%d" % (blk % 2)
        P.dma("sp", st[:, :, :], wv[:, :, blk * SW:(blk + 1) * SW], writes=[skey])
        P.op("dve" if blk % 2 == 0 else "pool", lambda e, st=st, blk=blk: e.tensor_copy(
            out=wb[:, :, col0 + blk * SW:col0 + (blk + 1) * SW], in_=st[:, :, :]), reads=[skey], writes=["wb"])


def emit_attn_norm_f(nc, outer, D, hTw):
    P = Prog(nc, sem_stack=outer, prefix="A0_")
    H = _p1_common_f(P, D, 128, ("w_incA", [4, 1024, 768]), "xTw", NW)
    ST = 512
    for st in range(NW // ST):
        _p1_load(P, H, st)
        hT, hk = _p1_norm_r(P, H, st)
        P.op("pool", lambda e, st=st, hT=hT: e.tensor_copy(out=hTw[:, :, st * ST:(st + 1) * ST], in_=hT[:, :, :]), reads=[hk], writes=["hTw"])
    P.build()


def emit_attn_f(nc, outer, D, mixT, hTw):
    P = Prog(nc, sem_stack=outer, prefix="A_")
    ST = 512
    H = {"SW": 256, "w_inc": D("w_incA", [4, 1024, 768])}
    ropeC, ropeS = D("ropeCw", [128, NW]), D("ropeSw", [128, NW])
    maskd, hseld, validd, identd = D("maskd", [128, 17 * 128]), D("hseld", [128, 64]), D("validw", [128, 32]), D("identd", [128, 128])
    sb, ps = P.sb, P.ps
    H["wb"] = wb = sb("wb", [128, 8, 768], BF16)
    H["stage"] = [sb("stage%d" % i, [128, 8, 256], F32) for i in range(2)]
    H["pSS"] = ps("pSS", [128, 512], F32)
    QT, KT = sb("QT", [128, NW], BF16), sb("KT", [128, NW], BF16)
    Vaug = sb("Vaug", [128, 32, 2, 65], BF16)
    Mall = sb("Mall", [128, 17 * 128], BF16)
    mst = sb("mst", [128, 17 * 128], F32)
    hself, hsel = sb("hself", [128, 64], F32), sb("hsel", [128, 64], BF16)
    valid = sb("valid", [128, 32], F32)
    identf = sb("identf", [128, 128], F32)
    onesrow = sb("onesrow", [64, 128], BF16)
    rc = [sb("rc%d" % i, [128, ST], F32) for i in range(2)]
    rs_ = [sb("rs%d" % i, [128, ST], F32) for i in range(2)]
    t1, t2 = sb("t1", [128, ST], F32), sb("t2", [128, ST], F32)
    sqk = sb("sqk", [128, ST], BF16)
    kmx = sb("kmx", [64, 2], F32)
    qn = sb("qn", [64, 128], F32)
    negm = sb("negm", [64, 128], BF16)
    PT = [sb("PT%d" % i, [128, 512], BF16) for i in range(4)]
    ao = [sb("ao%d" % i, [128, 128], F32) for i in range(2)]
    rec = sb("rec", [128, 4], F32)
    pA, pB, pV = ps("pA", [128, 512], F32), ps("pB", [128, 512], F32), ps("pV", [128, 512], F32)
    pN = H["pSS"]
    pS = [ps("pS%d" % i, [128, 512], F32) for i in range(2)]
    pO = [ps("pO%d" % i, [128, 2, 128], F32) for i in range(2)]

    P.dma("sp", mst[:, :], maskd[:, :], writes=["mst"])
    P.op("pool", lambda e: e.tensor_copy(out=Mall[:, :], in_=mst[:, :]), reads=["mst"], writes=["Mall"])
    P.dma("sp", hself[:, :], hseld[:, :], writes=["hself"])
    P.op("pool", lambda e: e.tensor_copy(out=hsel[:, :], in_=hself[:, :]), reads=["hself"], writes=["hsel"])
    P.dma("sp", valid[:, :], validd[:, :], writes=["valid"])
    P.dma("sp", identf[:, :], identd[:, :], writes=["identf"])
    P.op("pool", lambda e: e.memset(onesrow[:, :], 1.0), writes=["onesrow"])
    for h in range(2):
        P.op("pool", lambda e, h=h: e.tensor_copy(out=Vaug[:, :, h, 64], in_=valid[:, :]), reads=["valid"], writes=["Vaug"])
    cnt = [0]
    xl = [0]
    wv_all = H["w_inc"]
    for hp in range(4):
        _cast_w(P, H, wv_all[hp].rearrange("(k p) n -> p k n", p=128), 768)
        P.op("pool", lambda e: e.memset(kmx[:, :], 0.0), reads=["kmx"], writes=["kmx"])
        nst = NW // ST
        for st in range(nst):
            hT = hTw[:, :, st * ST:(st + 1) * ST]
            C, ck = rc[st % 2], "rc%d" % (st % 2)
            S, sk = rs_[st % 2], "rs%d" % (st % 2)
            P.dma("sp", C[:, :], ropeC[:, st * ST:(st + 1) * ST], writes=[ck])
            P.dma("sp", S[:, :], ropeS[:, st * ST:(st + 1) * ST], writes=[sk])
            for which, dst in ((0, QT), (1, KT)):
                dk = "QT" if which == 0 else "KT"
                c0 = which * 256
                for k in range(8):
                    P.op("pe", lambda e, k=k, c0=c0, hT=hT: e.matmul(pA[:, :], lhsT=wb[:, k, c0:c0 + 128], rhs=hT[:, k, :], start=(k == 0), stop=(k == 7)),
                         reads=["wb", "hTw"], writes=["pA"])
                for k in range(8):
                    P.op("pe", lambda e, k=k, c0=c0, hT=hT: e.matmul(pB[:, :], lhsT=wb[:, k, c0 + 128:c0 + 256], rhs=hT[:, k, :], start=(k == 0), stop=(k == 7)),
                         reads=["wb", "hTw"], writes=["pB"])
                P.op("dve", lambda e, C=C: e.tensor_tensor(out=t1[:, :], in0=pA[:, :], in1=C[:, :], op=ALU.mult), reads=["pA", ck], writes=["t1"])
                P.op("dve", lambda e, S=S: e.tensor_tensor(out=t2[:, :], in0=pB[:, :], in1=S[:, :], op=ALU.mult), reads=["pB", sk], writes=["t2"])
                P.op("pool", lambda e, dst=dst, st=st: e.tensor_tensor(out=dst[:, st * ST:(st + 1) * ST], in0=t1[:, :], in1=t2[:, :], op=ALU.add),
                     reads=["t1", "t2"], writes=[dk])
            P.op("pool", lambda e, st=st: e.tensor_tensor(out=sqk[:, :], in0=KT[:, st * ST:(st + 1) * ST], in1=KT[:, st * ST:(st + 1) * ST], op=ALU.mult),
                 reads=["KT"], writes=["sqk"])
            P.op("pe", lambda e: e.matmul(pN[0:64, :], lhsT=hsel[:, :], rhs=sqk[:, :], start=True, stop=True), reads=["hsel", "sqk"], writes=["pSS"])
            P.op("dve", lambda e: e.tensor_reduce(out=kmx[:, 1:2], in_=pN[0:64, :], axis=AX.X, op=ALU.max), reads=["pSS", "kmx"], writes=["kmx"])
            P.op("dve", lambda e: e.tensor_tensor(out=kmx[:, 0:1], in0=kmx[:, 0:1], in1=kmx[:, 1:2], op=ALU.max), reads=["kmx"], writes=["kmx"])
            for tt in range(4):
                for k in range(8):
                    P.op("pe", lambda e, k=k, tt=tt, hT=hT: e.matmul(pV[:, tt * 128:(tt + 1) * 128], lhsT=hT[:, k, tt * 128:(tt + 1) * 128],
                                                              rhs=wb[:, k, 512:640], start=(k == 0), stop=(k == 7)),
                         reads=["wb", "hTw"], writes=["pV"])
            for tt in range(4):
                tile = st * 4 + tt
                P.op("act", lambda e, tt=tt, tile=tile: e.activation(out=Vaug[:, tile, :, 0:64], in_=pV[:, tt * 128:(tt + 1) * 128].rearrange("p (h d) -> p h d", h=2),
                                                                     func=AF.Copy, scale=valid[:, tile:tile + 1]),
                     reads=["pV", "valid"], writes=["Vaug"])
        P.op("act", lambda e: e.activation(out=kmx[:, 0:1], in_=kmx[:, 0:1], func=AF.Sqrt), reads=["kmx"], writes=["kmx"])
        LAG = 2
        units = []
        for jb in range(8, 24):
            for h in range(2):
                dms = list(range(-8, 9))
                groups = [dms[i:i + 4] for i in range(0, len(dms), 4)]
                base = 0
                for gi, grp in enumerate(groups):
                    units.append(dict(jb=jb, h=h, grp=grp, gi=gi, base=base, last=(gi == len(groups) - 1)))
                    base += len(grp)

        def emit_scores(u):
            jb, h, grp = u["jb"], u["h"], u["grp"]
            qs = slice(jb * 128, (jb + 1) * 128)
            hs = slice(64 * h, 64 * h + 64)
            if h == 0 and u["gi"] == 0:
                P.op("pool", lambda e, qs=qs: e.tensor_tensor(out=sqk[:, 0:128], in0=QT[:, qs], in1=QT[:, qs], op=ALU.mult), reads=["QT"], writes=["sqk"])
                P.op("pe", lambda e: e.matmul(pN[0:64, 0:128], lhsT=hsel[:, :], rhs=sqk[:, 0:128], start=True, stop=True), reads=["hsel", "sqk"], writes=["pSS"])
                P.op("act", lambda e: e.activation(out=qn[:, :], in_=pN[0:64, 0:128], func=AF.Sqrt), reads=["pSS"], writes=["qn"])
                P.op("dve", lambda e: e.tensor_scalar(out=negm[:, :], in0=qn[:, :], scalar1=kmx[:, 0:1], scalar2=-1.0, op0=ALU.mult, op1=ALU.mult),
                     reads=["qn", "kmx"], writes=["negm"])
            g = cnt[0]
            cnt[0] += 1
            psx, psk = [(pS[0], "pS0"), (pS[1], "pS1"), (pA, "pA"), (pB, "pB")][g % 4]
            ptx, ptk = PT[g % 4], "PT%d" % (g % 4)
            u["ptx"], u["ptk"] = ptx, ptk
            n = len(grp)
            for i, dm in enumerate(grp):
                kc = jb + dm
                P.op("pe", lambda e, i=i, kc=kc, hs=hs, qs=qs, psx=psx: e.matmul(psx[:, i * 128:(i + 1) * 128], lhsT=KT[hs, kc * 128:(kc + 1) * 128],
                                                                                  rhs=QT[hs, qs], start=True, stop=False),
                     reads=["KT", "QT"], writes=[psk])
                P.op("pe", lambda e, i=i, h=h, psx=psx: e.matmul(psx[:, i * 128:(i + 1) * 128], lhsT=onesrow[32 * h:32 * h + 1, :],
                                                                 rhs=negm[32 * h:32 * h + 1, :], start=False, stop=True),
                     reads=["onesrow", "negm"], writes=[psk])
            P.op("act", lambda e, psx=psx, ptx=ptx, n=n: e.activation(out=ptx[:, 0:n * 128], in_=psx[:, 0:n * 128], func=AF.Exp, scale=0.125),
                 reads=[psk], writes=[ptk])
            m0 = (grp[0] + 8) * 128
            P.op("dve" if g % 2 == 0 else "pool", lambda e, ptx=ptx, n=n, m0=m0: e.tensor_tensor(
                out=ptx[:, 0:n * 128], in0=ptx[:, 0:n * 128], in1=Mall[:, m0:m0 + n * 128], op=ALU.mult),
                reads=[ptk, "Mall"], writes=[ptk])

        def emit_pv(u):
            jb, h, grp = u["jb"], u["h"], u["grp"]
            ptx, ptk = u["ptx"], u["ptk"]
            AO, aok = ao[jb % 2], "ao%d" % (jb % 2)
            PO, pok = pO[jb % 2], "pO%d" % (jb % 2)
            for i, dm in enumerate(grp):
                kc = jb + dm
                nmm = u["base"] + i
                P.op("pe", lambda e, i=i, kc=kc, h=h, ptx=ptx, PO=PO, first=(nmm == 0), last=(nmm == 16): e.matmul(
                    PO[:, h, 0:65], lhsT=ptx[:, i * 128:(i + 1) * 128], rhs=Vaug[:, kc, h, :], start=first, stop=last),
                    reads=[ptk, "Vaug"], writes=[pok])
            if u["last"]:
                P.op("dve", lambda e, h=h, PO=PO: e.reciprocal(out=rec[:, h:h + 1], in_=PO[:, h, 64:65]), reads=[pok], writes=["rec%d" % h])
                P.op("dve", lambda e, h=h, PO=PO, AO=AO: e.tensor_scalar(out=AO[:, 64 * h:64 * h + 64], in0=PO[:, h, 0:64], scalar1=rec[:, h:h + 1],
                                                                         scalar2=None, op0=ALU.mult),
                     reads=[pok, "rec%d" % h], writes=[aok])
                if h == 1:
                    P.op("pe", lambda e, AO=AO: e.transpose(out=pV[:, 0:128], in_=AO[:, :], identity=identf[:, :]), reads=[aok, "identf"], writes=["pV"])
                    P.op("act", lambda e, hp=hp, jb=jb: e.activation(out=mixT[:, hp, (jb - 8) * 128:(jb - 7) * 128], in_=pV[:, 0:128], func=AF.Copy),
                         reads=["pV"], writes=["mixT"])

        pending = []
        for u in units:
            emit_scores(u)
            pending.append(u)
            if len(pending) > LAG:
                emit_pv(pending.pop(0))
        while pending:
            emit_pv(pending.pop(0))
    P.build()


def emit_hyproj_f(nc, outer, D):
    P = Prog(nc, sem_stack=outer, prefix="B_")
    ST = 256
    H = _p1_common_f(P, D, 1536, ("w_incH", [1024, 1536]), "xT", NTOK, ST=ST, SW=128)
    _cast_w(P, H, H["w_inc"].rearrange("(k p) n -> p k n", p=128), 1536)
    Us = D("Us", [12, 128, NTOK], kind="Internal", dt=BF16)
    sb, ps = P.sb, P.ps
    wb, hT = H["wb"], H["hT"]
    ub = [sb("ub%d" % i, [128, 12, ST], BF16) for i in range(2)]
    pU = [ps("pU%d" % i, [128, 512], F32) for i in range(4)]
    nst = NTOK // ST
    _p1_load(P, H, 0)
    nxt = _p1_norm_r(P, H, 0)
    for st in range(nst):
        hT, hk = nxt
        if st + 1 < nst:
            _p1_load(P, H, st + 1)
            nxt = _p1_norm_r(P, H, st + 1)
        UB, ubk = ub[st % 2], "ub%d" % (st % 2)
        for pr in range(6):
            pu, puk = pU[pr % 4], "pU%d" % (pr % 4)
            for half in range(2):
                i = 2 * pr + half
                for k in range(8):
                    P.op("pe", lambda e, k=k, i=i, half=half, pu=pu, hT=hT: e.matmul(pu[:, half * ST:(half + 1) * ST], lhsT=wb[:, k, i * 128:(i + 1) * 128], rhs=hT[:, k, :],
                                                                                    start=(k == 0), stop=(k == 7)), reads=["wb", hk], writes=[puk])
            eng = "act" if pr % 2 == 0 else "dve"
            if eng == "act":
                P.op("act", lambda e, pr=pr, pu=pu, UB=UB: e.activation(out=UB[:, 2 * pr:2 * pr + 2, :], in_=pu[:, :].rearrange("p (i t) -> p i t", i=2), func=AF.Copy),
                     reads=[puk], writes=[ubk])
            else:
                P.op("dve", lambda e, pr=pr, pu=pu, UB=UB: e.tensor_copy(out=UB[:, 2 * pr:2 * pr + 2, :], in_=pu[:, :].rearrange("p (i t) -> p i t", i=2)),
                     reads=[puk], writes=[ubk])
        P.dma("pool", Us[:, :, st * ST:(st + 1) * ST].rearrange("i p t -> p i t"), UB[:, :, :], reads=[ubk], writes=["Us"])
    P.build()


def emit_hyena_f(nc, outer, D, mixT, groups=(0, 1, 2, 3), prefix="C_"):
    P = Prog(nc, sem_stack=outer, prefix=prefix)
    sb, ps = P.sb, P.ps
    Us = D("Us", [12, 128, NTOK], kind="Internal", dt=BF16)
    dF1cat, dTrTr, dTiTi, dR12 = D("F1cat", [128, 512]), D("TrTr", [128, 512]), D("TiTi", [128, 512]), D("R12", [128, 512])
    dTT2r, dTT2i, dIF1, dSwin = D("TT2r", [128, 512]), D("TT2i", [128, 512]), D("IF1", [128, 512]), D("Swin", [128, 64])
    dzemb = D("zembT", [33, NTOK])
    ddelta = D("delta4", [128, 512])
    dhbcol = D("hbcol4", [128, 512])
    dfw1, dfw2, dfw3, dfpar = D("fw1", [33, 64]), D("fw2", [64, 64]), D("fw3c4", [64, 2048]), D("fpar", [64, 4])
    dcw, dcb = D("cw4", [128, 36]), D("cb4", [128, 12])
    dsel = D("seld", [128, 32])
    didn = D("identd", [128, 128])

    U = [sb("U%d" % i, [128, NTOK + 2], BF16) for i in range(3)]
    CV = sb("CV", [128, NTOK], BF16)
    Ob = sb("Ob", [128, NTOK], BF16)
    h2T = sb("h2T", [64, NTOK], BF16)
    ApF = sb("ApF", [128, 2, NPQ, 256], BF16)
    ApD = sb("ApD", [128, 2, NPQ, 256], BF16)
    HsL = [sb("Hs%d" % i, [128, 2, NPQ, 256], BF16) for i in range(2)]
    Yb = sb("Yb", [128, 2, NPQ, 256], BF16)
    Bp = sb("Bp", [128, 2, 2, NPQ, 128], BF16)
    W1 = [sb("W1_%d" % i, [128, 512], F32) for i in range(2)]
    W2 = [sb("W2_%d" % i, [128, 512], F32) for i in range(2)]
    cpy = [sb("cpy%d" % i, [128, 512], F32) for i in range(2)]
    cpy2 = sb("cpy2", [128, 512], F32)
    tmpc = sb("tmpc", [128, 1024], F32)
    arg = tmpc[0:64, 0:512]
    h1 = tmpc[0:64, 512:1024]
    argi = sb("argi", [64, 512], mybir.dt.int32)
    F1b, R12b, IF1b = sb("F1b", [128, 512], BF16), sb("R12b", [128, 512], BF16), sb("IF1b", [128, 512], BF16)
    TrTr, TiTi = sb("TrTr", [128, 512], F32), sb("TiTi", [128, 512], F32)
    TT2r, TT2i = sb("TT2r", [128, 512], F32), sb("TT2i", [128, 512], F32)
    Swin = sb("Swin", [128, 64], F32)
    delta = sb("delta", [128, 512], F32)
    hbcol = sb("hbcol", [128, 512], F32)
    fw1, fw2 = sb("fw1", [33, 64], F32), sb("fw2", [64, 64], F32)
    fw3b = sb("fw3b", [64, 2048], BF16)
    fpar = sb("fpar", [64, 8], F32)
    cw, cb = sb("cw", [128, 36], F32), sb("cb", [128, 12], F32)
    selb = sb("selb", [128, 32], BF16)
    ident = sb("ident", [128, 128], BF16)
    zc = [sb("zc%d" % i, [33, 512], F32) for i in range(2)]
    win = [sb("win%d" % i, [128, 128], F32) for i in range(2)]
    eoc = [sb("eoc%d" % i, [128, 256], F32) for i in range(2)]
    fw3f = sb("fw3f", [64, 1024], BF16)

    pSS = ps("pSS", [128, 512], F32)
    pA1 = [ps("pA1_%d" % i, [128, 512], F32) for i in range(2)]
    pXr, pXi = ps("pXr", [128, 512], F32), ps("pXi", [128, 512], F32)
    pB = ps("pB", [128, 512], F32)
    pY = ps("pY", [128, 512], F32)
    pTz = ps("pTz", [128, 8, 128], BF16)

    def ldcast(dst, dkey, src, n=512, parts=128):
        P.dma("sp", cpy2[0:parts, 0:n], src, writes=["cpy2"])
        P.op("dve", lambda e: e.tensor_copy(out=dst, in_=cpy2[0:parts, 0:n]), reads=["cpy2"], writes=[dkey])
    ldcast(F1b[:, :], "F1b", dF1cat[:, :])
    ldcast(R12b[:, :], "R12b", dR12[:, :])
    ldcast(IF1b[:, :], "IF1b", dIF1[:, :])
    ldcast(ident[:, :], "ident", didn[:, :], n=128)
    ldcast(selb[:, :], "selb", dsel[:, :], n=32)
    for q4 in range(4):
        P.dma("sp", cpy2[0:64, :], dfw3[:, q4 * 512:(q4 + 1) * 512], writes=["cpy2"])
        for o_ in range(2):
            wf = cpy2[0:64, o_ * 256:o_ * 256 + 128]
            wbk = cpy2[0:64, o_ * 256 + 128:o_ * 256 + 256]
            c0_ = q4 * 512 + o_ * 256
            P.op("dve", lambda e, wf=wf, wbk=wbk, c0_=c0_: e.tensor_tensor(out=fw3b[:, c0_:c0_ + 128], in0=wf, in1=wbk, op=ALU.add), reads=["cpy2"], writes=["fw3b"])
            P.op("dve", lambda e, wf=wf, wbk=wbk, c0_=c0_: e.tensor_tensor(out=fw3b[:, c0_ + 128:c0_ + 256], in0=wf, in1=wbk, op=ALU.subtract), reads=["cpy2"], writes=["fw3b"])
            P.op("dve", lambda e, wf=wf, q4=q4, o_=o_: e.tensor_copy(out=fw3f[:, (2 * q4 + o_) * 128:(2 * q4 + o_ + 1) * 128], in_=wf), reads=["cpy2"], writes=["fw3f"])
    for dst, key, src in ((TrTr, "TrTr", dTrTr), (TiTi, "TiTi", dTiTi), (TT2r, "TT2r", dTT2r), (TT2i, "TT2i", dTT2i), (Swin, "Swin", dSwin),
                          (delta, "delta", ddelta), (hbcol, "hbcol", dhbcol), (fw1, "fw1", dfw1), (fw2, "fw2", dfw2), (cw, "cw", dcw), (cb, "cb", dcb)):
        P.dma("sp", dst[:, :], src[:, :], writes=[key])
    P.dma("sp", fpar[:, 0:4], dfpar[:, :], writes=["fpar"])
    i2p = 1.0 / (2.0 * math.pi)
    for (bc, fc, o0) in ((0, 1, 4), (2, 3, 6)):
        P.op("dve", lambda e, bc=bc, fc=fc, o0=o0: e.tensor_tensor(out=fpar[:, o0 + 1:o0 + 2], in0=fpar[:, bc:bc + 1], in1=fpar[:, fc:fc + 1], op=ALU.mult),
             reads=["fpar"], writes=["fpar"])
        P.op("dve", lambda e, o0=o0: e.tensor_scalar(out=fpar[:, o0 + 1:o0 + 2], in0=fpar[:, o0 + 1:o0 + 2], scalar1=i2p, scalar2=16.0, op0=ALU.mult, op1=ALU.add),
             reads=["fpar"], writes=["fpar"])
        P.op("dve", lambda e, fc=fc, o0=o0: e.tensor_scalar(out=fpar[:, o0:o0 + 1], in0=fpar[:, fc:fc + 1], scalar1=i2p, scalar2=None, op0=ALU.mult),
             reads=["fpar"], writes=["fpar"])

    for ch in range(16):
        zt, zk = zc[ch % 2], "zc%d" % (ch % 2)
        P.dma("sp", zt[:, :], dzemb[:, ch * 512:(ch + 1) * 512], writes=[zk])
        for layer in range(2):
            if layer == 0:
                P.op("pe", lambda e, zt=zt: e.matmul(pSS[0:64, :], lhsT=fw1[:, :], rhs=zt[:, :], start=True, stop=True), reads=["fw1", zk], writes=["pSS"])
            else:
                P.op("pe", lambda e: e.matmul(pSS[0:64, :], lhsT=fw2[:, :], rhs=h1[:, :], start=True, stop=True), reads=["fw2", "tmpc"], writes=["pSS"])
            fr, fb = (4, 5) if layer == 0 else (6, 7)
            P.op("dve", lambda e, fr=fr, fb=fb: e.tensor_scalar(out=arg[:, :], in0=pSS[0:64, :], scalar1=fpar[:, fr:fr + 1], scalar2=fpar[:, fb:fb + 1],
                                                                op0=ALU.mult, op1=ALU.add), reads=["pSS", "fpar"], writes=["tmpc"])
            P.op("dve", lambda e: e.tensor_copy(out=argi[:, :], in_=arg[:, :]), reads=["tmpc"], writes=["argi"])
            P.op("dve", lambda e: e.tensor_copy(out=h1[:, :], in_=argi[:, :]), reads=["argi", "tmpc"], writes=["tmpc"])
            P.op("dve", lambda e: e.tensor_tensor(out=arg[:, :], in0=arg[:, :], in1=h1[:, :], op=ALU.subtract), reads=["tmpc"], writes=["tmpc"])
            P.op("dve", lambda e: e.scalar_tensor_tensor(out=arg[:, :], in0=arg[:, :], scalar=0.5, in1=arg[:, :], op0=ALU.is_gt, op1=ALU.subtract),
                 reads=["tmpc"], writes=["tmpc"])
            if layer == 0:
                P.op("act", lambda e: e.activation(out=h1[:, :], in_=arg[:, :], func=AF.Sin, scale=-2.0 * math.pi), reads=["tmpc"], writes=["tmpc"])
            else:
                P.op("act", lambda e, ch=ch: e.activation(out=h2T[:, ch * 512:(ch + 1) * 512], in_=arg[:, :], func=AF.Sin, scale=-2.0 * math.pi),
                     reads=["tmpc"], writes=["h2T"])

    Zkeys = lambda i: ["Z%d_%d" % (i, s) for s in range(128 // (2 * NPQ))]
    Z = [U[i][:, 0:NTOK].rearrange("a (c p) -> a c p", p=64) for i in range(3)]
    CVs = CV[:, :].rearrange("c (a p) -> c a p", p=64)
    E3 = CV[:, :].rearrange("a (c p) -> a c p", p=64)
    O3 = Ob[:, :].rearrange("a (c p) -> a c p", p=64)
    h2s = h2T[:, :].rearrange("j (a p) -> j a p", p=64)
    cnt = [0]

    def twiddle(psrc, pkey, Tr_, Ti_, trk, tik, outr, outi, okey, view):
        g = cnt[0]
        cnt[0] += 1
        w1, w1k = W1[g % 2], "W1_%d" % (g % 2)
        w2, w2k = W2[g % 2], "W2_%d" % (g % 2)
        cp_, cpk = cpy[g % 2], "cpy%d" % (g % 2)
        P.op("act", lambda e: e.activation(out=cp_[:, :], in_=psrc[:, :], func=AF.Copy), reads=[pkey], writes=[cpk])
        P.op("dve", lambda e: e.tensor_tensor(out=w1[:, :], in0=psrc[:, :], in1=Tr_[:, :], op=ALU.mult), reads=[pkey, trk], writes=[w1k])
        P.op("pool", lambda e: e.tensor_tensor(out=w2[:, :], in0=cp_[:, :], in1=Ti_[:, :], op=ALU.mult), reads=[cpk, tik], writes=[w2k])
        w1r, w1i = view(w1)
        w2r, w2i = view(w2)
        P.op("dve", lambda e: e.tensor_tensor(out=outr, in0=w1r, in1=w2i, op=ALU.subtract), reads=[w1k, w2k], writes=[okey])
        P.op("dve", lambda e: e.tensor_tensor(out=outi, in0=w2r, in1=w1i, op=ALU.add), reads=[w1k, w2k], writes=[okey])

    v_fwd = lambda t: (t[:, 0:256], t[:, 256:512])
    v_inv = lambda t: (t[:, :].rearrange("k (c r x) -> k c r x", c=2, r=2)[:, :, 0, :], t[:, :].rearrange("k (c r x) -> k c r x", c=2, r=2)[:, :, 1, :])

    def fwd_stage1(src3, skey, c0, Ap, apk):
        for q in range(NPQ):
            g = cnt[0]
            pa, pak = pA1[g % 2], "pA1_%d" % (g % 2)
            c = c0 + 2 * q
            P.op("pe", lambda e, c=c, pa=pa: e.matmul(pa[:, :], lhsT=src3[:, c:c + 2, :], rhs=F1b[:, :], start=True, stop=True),
                 reads=[skey, "F1b"], writes=[pak])
            twiddle(pa, pak, TrTr, TiTi, "TrTr", "TiTi", Ap[:, 0, q, :], Ap[:, 1, q, :], apk, v_fwd)

    G2r, G2in, G2i = R12b[:, 0:128], R12b[:, 128:256], R12b[:, 256:384]
    R1, R2 = R12b[:, 0:256], R12b[:, 256:512]

    for grp4 in groups:
        for i in range(3):
            P.dma("sp", U[i][:, 1:NTOK + 1], Us[3 * grp4 + i], reads=["Us"], writes=["U%d" % i] + Zkeys(i), key="U%d" % i)
            P.op("pool", lambda e, i=i: e.memset(U[i][:, 0:1], 0.0), reads=["U%d" % i], writes=["U%d" % i] + Zkeys(i))
            P.op("pool", lambda e, i=i: e.memset(U[i][:, NTOK + 1:NTOK + 2], 0.0), reads=["U%d" % i], writes=["U%d" % i])
        for i in range(3):
            ci = 9 * grp4 + 3 * i
            bi = 3 * grp4 + i
            for ch in range(8):
                j0 = ch * 1024
                P.op("dve", lambda e, i=i, j0=j0, ci=ci, bi=bi: e.tensor_scalar(out=tmpc[:, :], in0=U[i][:, j0:j0 + 1024], scalar1=cw[:, ci:ci + 1],
                                                                                scalar2=cb[:, bi:bi + 1], op0=ALU.mult, op1=ALU.add),
                     reads=["U%d" % i, "cw", "cb"], writes=["tmpc"])
                P.op("dve", lambda e, i=i, j0=j0, ci=ci: e.scalar_tensor_tensor(out=tmpc[:, :], in0=U[i][:, j0 + 1:j0 + 1025], scalar=cw[:, ci + 1:ci + 2],
                                                                                in1=tmpc[:, :], op0=ALU.mult, op1=ALU.add),
                     reads=["U%d" % i, "cw", "tmpc"], writes=["tmpc"])
                P.op("dve", lambda e, i=i, j0=j0, ci=ci: e.scalar_tensor_tensor(out=CV[:, j0:j0 + 1024], in0=U[i][:, j0 + 2:j0 + 1026], scalar=cw[:, ci + 2:ci + 3],
                                                                                in1=tmpc[:, :], op0=ALU.mult, op1=ALU.add),
                     reads=["U%d" % i, "cw", "tmpc"], writes=["CV"])
            for pg in range(8):
                for pi in range(8):
                    p = pg * 8 + pi
                    P.op("pe", lambda e, p=p, pi=pi: e.transpose(out=pTz[:, pi, :], in_=CVs[:, :, p], identity=ident[:, :]),
                         reads=["CV", "ident"], writes=["pTz"])
                P.op("act", lambda e, i=i, pg=pg: e.activation(out=Z[i][:, :, pg * 8:(pg + 1) * 8].rearrange("a c p -> a p c"), in_=pTz[:, :, :], func=AF.Copy),
                     reads=["pTz"], writes=["U%d" % i] + Zkeys(i))
        for o in range(2):
            w3c0 = grp4 * 512 + o * 256
            for p in range(64):
                pe_, pek = (pSS, "pSS") if p % 2 == 0 else (pB, "pB")
                wn, wnk = win[p % 2], "win%d" % (p % 2)
                ec, eck = eoc[p % 2], "eoc%d" % (p % 2)
                P.op("pe", lambda e, p=p, w3c0=w3c0, pe_=pe_: e.matmul(pe_[:, 0:256], lhsT=h2s[:, :, p], rhs=fw3b[:, w3c0:w3c0 + 256], start=True, stop=True),
                     reads=["h2T", "fw3b"], writes=[pek])
                P.op("act", lambda e, p=p, grp4=grp4, wn=wn: e.activation(out=wn[:, :], in_=delta[:, grp4 * 128:(grp4 + 1) * 128], func=AF.Exp, scale=Swin[:, p:p + 1]),
                     reads=["delta", "Swin"], writes=[wnk])
                P.op("act", lambda e, pe_=pe_, ec=ec: e.activation(out=ec[:, :], in_=pe_[:, 0:256], func=AF.Copy), reads=[pek], writes=[eck])
                P.op("pool", lambda e, p=p, ec=ec, wn=wn: e.tensor_tensor(out=E3[:, :, p], in0=ec[:, 0:128], in1=wn[:, :], op=ALU.mult), reads=[eck, wnk], writes=["CV"])
                P.op("pool", lambda e, p=p, ec=ec, wn=wn: e.tensor_tensor(out=O3[:, :, p], in0=ec[:, 128:256], in1=wn[:, :], op=ALU.mult), reads=[eck, wnk], writes=["Ob"])
                if p == 0:
                    fc = (2 * grp4 + o) * 128
                    P.op("pe", lambda e, fc=fc: e.matmul(pY[0:1, 0:128], lhsT=h2T[:, 0:1], rhs=fw3f[:, fc:fc + 128], start=True, stop=True),
                         reads=["h2T", "fw3f"], writes=["pY"])
                    P.op("dve", lambda e, wn=wn: e.tensor_tensor(out=E3[0:1, :, 0], in0=pY[0:1, 0:128], in1=wn[0:1, :], op=ALU.mult), reads=["pY", wnk], writes=["CV"])
                    P.op("dve", lambda e, wn=wn: e.tensor_tensor(out=O3[0:1, :, 0], in0=pY[0:1, 0:128], in1=wn[0:1, :], op=ALU.mult), reads=["pY", wnk], writes=["Ob"])
            nsbt = 128 // (2 * NPQ)

            def filt_gen(sbt):
                c0 = sbt * 2 * NPQ
                Hs, hk = HsL[sbt % 2], "Hs%d" % (sbt % 2)
                for which, (src3, skey) in enumerate(((E3, "CV"), (O3, "Ob"))):
                    fwd_stage1(src3, skey, c0, ApF, "ApF")
                    yield
                    for qq in range(NPQ // 2):
                        rr = ApF[:, 0, 2 * qq:2 * qq + 2, :]
                        ri = ApF[:, 1, 2 * qq:2 * qq + 2, :]
                        if which == 0:
                            P.op("pe", lambda e, rr=rr: e.matmul(pSS[:, :], lhsT=G2r, rhs=rr, start=True, stop=False), reads=["R12b", "ApF"], writes=["pSS"])
                            P.op("pe", lambda e, ri=ri: e.matmul(pSS[:, :], lhsT=G2in, rhs=ri, start=False, stop=True), reads=["R12b", "ApF"], writes=["pSS"])
                            for j in range(2):
                                hcol = grp4 * 128 + o * 64 + sbt * NPQ + 2 * qq + j
                                P.op("act", lambda e, j=j, qq=qq, hcol=hcol, Hs=Hs: e.activation(out=Hs[:, 0, 2 * qq + j, :], in_=pSS[:, j * 256:(j + 1) * 256], func=AF.Identity,
                                                                                                 bias=hbcol[:, hcol:hcol + 1], scale=1.0),
                                     reads=["pSS", "hbcol"], writes=[hk])
                        else:
                            P.op("pe", lambda e, rr=rr: e.matmul(pSS[:, :], lhsT=G2i, rhs=rr, start=True, stop=False), reads=["R12b", "ApF"], writes=["pSS"])
                            P.op("pe", lambda e, ri=ri: e.matmul(pSS[:, :], lhsT=G2r, rhs=ri, start=False, stop=True), reads=["R12b", "ApF"], writes=["pSS"])
                            P.op("act", lambda e, qq=qq, Hs=Hs: e.activation(out=Hs[:, 1, 2 * qq:2 * qq + 2, :], in_=pSS[:, :].rearrange("k (q x) -> k q x", q=2), func=AF.Copy),
                                 reads=["pSS"], writes=[hk])
                    yield

            def data_gen(sbt):
                c0 = sbt * 2 * NPQ
                zk = "Z0_%d" % sbt
                Hs, hk = HsL[sbt % 2], "Hs%d" % (sbt % 2)
                fwd_stage1(Z[0], zk, c0, ApD, "ApD")
                yield
                for qq in range(NPQ // 2):
                    rr = ApD[:, 0, 2 * qq:2 * qq + 2, :]
                    ri = ApD[:, 1, 2 * qq:2 * qq + 2, :]
                    P.op("pe", lambda e, rr=rr: e.matmul(pXr[:, :], lhsT=G2r, rhs=rr, start=True, stop=False), reads=["R12b", "ApD"], writes=["pXr"])
                    P.op("pe", lambda e, ri=ri: e.matmul(pXr[:, :], lhsT=G2in, rhs=ri, start=False, stop=True), reads=["R12b", "ApD"], writes=["pXr"])
                    P.op("pe", lambda e, rr=rr: e.matmul(pXi[:, :], lhsT=G2i, rhs=rr, start=True, stop=False), reads=["R12b", "ApD"], writes=["pXi"])
                    P.op("pe", lambda e, ri=ri: e.matmul(pXi[:, :], lhsT=G2r, rhs=ri, start=False, stop=True), reads=["R12b", "ApD"], writes=["pXi"])
                    g = cnt[0]
                    cnt[0] += 1
                    w1, w1k = W1[g % 2], "W1_%d" % (g % 2)
                    w2, w2k = W2[g % 2], "W2_%d" % (g % 2)
                    cx, cxk = cpy[g % 2], "cpy%d" % (g % 2)
                    hr = Hs[:, 0, 2 * qq:2 * qq + 2, :].rearrange("k q x -> k (q x)")
                    hi = Hs[:, 1, 2 * qq:2 * qq + 2, :].rearrange("k q x -> k (q x)")
                    yr = Yb[:, 0, 2 * qq:2 * qq + 2, :].rearrange("k q x -> k (q x)")
                    yi = Yb[:, 1, 2 * qq:2 * qq + 2, :].rearrange("k q x -> k (q x)")
                    P.op("act", lambda e, cx=cx: e.activation(out=cx[:, :], in_=pXi[:, :], func=AF.Copy), reads=["pXi"], writes=[cxk])
                    P.op("act", lambda e: e.activation(out=cpy2[:, :], in_=pXr[:, :], func=AF.Copy), reads=["pXr"], writes=["cpy2"])
                    P.op("dve", lambda e, w1=w1, hr=hr: e.tensor_tensor(out=w1[:, :], in0=pXr[:, :], in1=hr, op=ALU.mult), reads=["pXr", hk], writes=[w1k])
                    P.op("pool", lambda e, w2=w2, cx=cx, hi=hi: e.tensor_tensor(out=w2[:, :], in0=cx[:, :], in1=hi, op=ALU.mult), reads=[cxk, hk], writes=[w2k])
                    P.op("dve", lambda e, w1=w1, w2=w2, yr=yr: e.tensor_tensor(out=yr, in0=w1[:, :], in1=w2[:, :], op=ALU.subtract), reads=[w1k, w2k], writes=["Yb"])
                    P.op("dve", lambda e, w1=w1, hr=hr: e.tensor_tensor(out=w1[:, :], in0=pXi[:, :], in1=hr, op=ALU.mult), reads=["pXi", hk, "Yb"], writes=[w1k])
                    P.op("pool", lambda e, w2=w2, hi=hi: e.tensor_tensor(out=w2[:, :], in0=cpy2[:, :], in1=hi, op=ALU.mult), reads=["cpy2", hk, "Yb"], writes=[w2k])
                    P.op("dve", lambda e, w1=w1, w2=w2, yi=yi: e.tensor_tensor(out=yi, in0=w1[:, :], in1=w2[:, :], op=ALU.add), reads=[w1k, w2k], writes=["Yb"])
                yield
                for q in range(NPQ):
                    for kc in range(2):
                        P.op("pe", lambda e, q=q, kc=kc: e.matmul(pB[:, kc * 256:(kc + 1) * 256], lhsT=Yb[:, 0, q, kc * 128:(kc + 1) * 128], rhs=R1, start=True, stop=False),
                             reads=["Yb", "R12b"], writes=["pB"])
                        P.op("pe", lambda e, q=q, kc=kc: e.matmul(pB[:, kc * 256:(kc + 1) * 256], lhsT=Yb[:, 1, q, kc * 128:(kc + 1) * 128], rhs=R2, start=False, stop=True),
                             reads=["Yb", "R12b"], writes=["pB"])
                    twiddle(pB, "pB", TT2r, TT2i, "TT2r", "TT2i", Bp[:, :, 0, q, :], Bp[:, :, 1, q, :], "Bp", v_inv)
                yield
                for hh in range(NPQ // 4):
                    n = 0
                    for kc in range(2):
                        for r in range(2):
                            rhs = Bp[:, kc, r, 4 * hh:4 * hh + 4, :]
                            lt = IF1b[:, (2 * kc + r) * 128:(2 * kc + r + 1) * 128]
                            P.op("pe", lambda e, rhs=rhs, lt=lt, n=n: e.matmul(pY[:, :], lhsT=lt, rhs=rhs, start=(n == 0), stop=(n == 3)),
                                 reads=["Bp", "IF1b"], writes=["pY"])
                            n += 1
                    cc = c0 + 8 * hh
                    zi = 1 if o == 0 else 2
                    zo = 0 if o == 0 else 2
                    okeys = [zk] if o == 0 else ["U2", "Z2_%d" % sbt]
                    P.op("dve", lambda e, cc=cc, zi=zi, zo=zo: e.scalar_tensor_tensor(out=Z[zo][:, cc:cc + 8, :], in0=pY[:, :].rearrange("a (c p) -> a c p", p=64),
                                                                                      scalar=1.0 / NFFT, in1=Z[zi][:, cc:cc + 8, :], op0=ALU.mult, op1=ALU.mult),
                         reads=["pY", "U%d" % zi] + Zkeys(zi), writes=okeys)
                yield

            for _ in filt_gen(0):
                pass
            for sbt in range(nsbt):
                gens = [data_gen(sbt)] + ([filt_gen(sbt + 1)] if sbt + 1 < nsbt else [])
                while gens:
                    for gq in list(gens):
                        try:
                            next(gq)
                        except StopIteration:
                            gens.remove(gq)
        mv = mixT[:, 4 + grp4, :].rearrange("c (a p) -> c a p", p=64)
        for pg in range(4):
            for pi in range(16):
                p = pg * 16 + pi
                P.op("pe", lambda e, p=p, pi=pi: e.matmul(pXr[:, pi * 32:(pi + 1) * 32], lhsT=Z[2][:, :, p], rhs=selb[:, :], start=True, stop=True),
                     reads=["U2", "selb"] + Zkeys(2), writes=["pXr"])
            P.op("act", lambda e, pg=pg, mv=mv: e.activation(out=mv[:, :, pg * 16:(pg + 1) * 16].rearrange("c a p -> c p a"),
                                                             in_=pXr[:, :].rearrange("c (p a) -> c p a", a=32), func=AF.Copy),
                 reads=["pXr"], writes=["mixT"])
    P.build()


def emit_hyena_h(nc, outer, D, mixT, groups=(0, 1, 2, 3), prefix="C_"):
    P = Prog(nc, sem_stack=outer, prefix=prefix)
    sb, ps = P.sb, P.ps
    Us = D("Us", [12, 128, NTOK], kind="Internal", dt=BF16)
    dF1cat, dTrTr, dTiTi, dR12 = D("F1cat_h", [128, 256]), D("TrTr_h", [128, 512]), D("TiTi_h", [128, 512]), D("R12", [128, 512])
    dTT2r, dTT2i, dIF1, dSwin = D("TT2r_h", [128, 512]), D("TT2i_h", [128, 512]), D("IF1_h", [128, 256]), D("Swin", [128, 64])
    dzemb = D("zembT", [33, NTOK])
    ddelta = D("delta4", [128, 512])
    dhbrow = D("hbrow", [1, 1024])
    dfw1, dfw2, dfw3, dfpar = D("fw1", [33, 64]), D("fw2", [64, 64]), D("fw3c4", [64, 2048]), D("fpar", [64, 4])
    dcw, dcb = D("cw4", [128, 36]), D("cb4", [128, 12])
    dsel = D("seld", [128, 32])
    didn = D("identd", [128, 128])

    U = [sb("U%d" % i, [128, NTOK + 2], BF16) for i in range(3)]
    CV = sb("CV", [128, NTOK], BF16)
    Ob = sb("Ob", [128, NTOK], BF16)
    h2T = sb("h2T", [64, NTOK], BF16)
    ApF = sb("ApF", [128, 2, NPQH, 128], BF16)
    ApD = sb("ApD", [128, 2, NPQH, 128], BF16)
    HsL = [sb("Hs%d" % i, [128, 2, NPQH, 128], BF16) for i in range(2)]
    Yb = sb("Yb", [128, 2, NPQH, 128], BF16)
    Bp = sb("Bp", [128, 2, NPQH, 128], BF16)
    W1 = [sb("W1_%d" % i, [128, 512], BF16) for i in range(4)]
    W2 = [sb("W2_%d" % i, [128, 512], BF16) for i in range(4)]
    cpy = [sb("cpr%d" % i, [128, 512], BF16) for i in range(4)]
    cpy2 = sb("cpy2", [128, 512], F32)
    tmpc = sb("tmpc", [128, 1024], F32)
    arg = tmpc[0:64, 0:512]
    h1 = tmpc[0:64, 512:1024]
    argi = sb("argi", [64, 512], mybir.dt.int32)
    F1b, R12b, IF1b = sb("F1b", [128, 256], BF16), sb("R12b", [128, 512], BF16), sb("IF1b", [128, 256], BF16)
    TrTr, TiTi = sb("TrTr", [128, 512], F32), sb("TiTi", [128, 512], F32)
    TT2r, TT2i = sb("TT2r", [128, 512], F32), sb("TT2i", [128, 512], F32)
    Swin = sb("Swin", [128, 64], F32)
    delta = sb("delta", [128, 512], F32)
    hbrow = sb("hbrow", [1, 1024], F32)
    etmp = sb("etmp", [1, 128], F32)
    fw1, fw2 = sb("fw1", [33, 64], F32), sb("fw2", [64, 64], F32)
    fw3b = sb("fw3b", [64, 2048], BF16)
    fpar = sb("fpar", [64, 8], F32)
    cw, cb = sb("cw", [128, 36], F32), sb("cb", [128, 12], F32)
    selb = sb("selb", [128, 32], BF16)
    ident = sb("ident", [128, 128], BF16)
    zc = [sb("zc%d" % i, [33, 512], F32) for i in range(2)]
    win = [sb("win%d" % i, [128, 128], F32) for i in range(2)]
    eoc = [sb("eoc%d" % i, [128, 256], F32) for i in range(2)]
    fw3f = sb("fw3f", [64, 1024], BF16)

    pSS = ps("pSS", [128, 512], F32)
    pA1 = [ps("pA1_%d" % i, [128, 512], F32) for i in range(2)]
    pXr, pXi = ps("pXr", [128, 512], F32), ps("pXi", [128, 512], F32)
    pB = ps("pB", [128, 512], F32)
    pY = ps("pY", [128, 512], F32)
    pTz = ps("pTz", [128, 8, 128], BF16)

    def ldcast(dst, dkey, src, n=512, parts=128):
        P.dma("sp", cpy2[0:parts, 0:n], src, writes=["cpy2"])
        P.op("dve", lambda e: e.tensor_copy(out=dst, in_=cpy2[0:parts, 0:n]), reads=["cpy2"], writes=[dkey])
    ldcast(F1b[:, :], "F1b", dF1cat[:, :], n=256)
    ldcast(R12b[:, :], "R12b", dR12[:, :])
    ldcast(IF1b[:, :], "IF1b", dIF1[:, :], n=256)
    ldcast(ident[:, :], "ident", didn[:, :], n=128)
    ldcast(selb[:, :], "selb", dsel[:, :], n=32)
    for q4 in range(4):
        P.dma("sp", cpy2[0:64, :], dfw3[:, q4 * 512:(q4 + 1) * 512], writes=["cpy2"])
        for o_ in range(2):
            wf = cpy2[0:64, o_ * 256:o_ * 256 + 128]
            wbk = cpy2[0:64, o_ * 256 + 128:o_ * 256 + 256]
            c0_ = q4 * 512 + o_ * 256
            P.op("dve", lambda e, wf=wf, wbk=wbk, c0_=c0_: e.tensor_tensor(out=fw3b[:, c0_:c0_ + 128], in0=wf, in1=wbk, op=ALU.add), reads=["cpy2"], writes=["fw3b"])
            P.op("dve", lambda e, wf=wf, wbk=wbk, c0_=c0_: e.tensor_tensor(out=fw3b[:, c0_ + 128:c0_ + 256], in0=wf, in1=wbk, op=ALU.subtract), reads=["cpy2"], writes=["fw3b"])
            P.op("dve", lambda e, wf=wf, q4=q4, o_=o_: e.tensor_copy(out=fw3f[:, (2 * q4 + o_) * 128:(2 * q4 + o_ + 1) * 128], in_=wf), reads=["cpy2"], writes=["fw3f"])
    for dst, key, src in ((TrTr, "TrTr", dTrTr), (TiTi, "TiTi", dTiTi), (TT2r, "TT2r", dTT2r), (TT2i, "TT2i", dTT2i), (Swin, "Swin", dSwin),
                          (delta, "delta", ddelta), (hbrow, "hbrow", dhbrow), (fw1, "fw1", dfw1), (fw2, "fw2", dfw2), (cw, "cw", dcw), (cb, "cb", dcb)):
        P.dma("sp", dst[:, :], src[:, :], writes=[key])
    P.dma("sp", fpar[:, 0:4], dfpar[:, :], writes=["fpar"])
    i2p = 1.0 / (2.0 * math.pi)
    for (bc, fc, o0) in ((0, 1, 4), (2, 3, 6)):
        P.op("dve", lambda e, bc=bc, fc=fc, o0=o0: e.tensor_tensor(out=fpar[:, o0 + 1:o0 + 2], in0=fpar[:, bc:bc + 1], in1=fpar[:, fc:fc + 1], op=ALU.mult),
             reads=["fpar"], writes=["fpar"])
        P.op("dve", lambda e, o0=o0: e.tensor_scalar(out=fpar[:, o0 + 1:o0 + 2], in0=fpar[:, o0 + 1:o0 + 2], scalar1=i2p, scalar2=16.0, op0=ALU.mult, op1=ALU.add),
             reads=["fpar"], writes=["fpar"])
        P.op("dve", lambda e, fc=fc, o0=o0: e.tensor_scalar(out=fpar[:, o0:o0 + 1], in0=fpar[:, fc:fc + 1], scalar1=i2p, scalar2=None, op0=ALU.mult),
             reads=["fpar"], writes=["fpar"])

    for ch in range(16):
        zt, zk = zc[ch % 2], "zc%d" % (ch % 2)
        P.dma("sp", zt[:, :], dzemb[:, ch * 512:(ch + 1) * 512], writes=[zk])
        for layer in range(2):
            if layer == 0:
                P.op("pe", lambda e, zt=zt: e.matmul(pSS[0:64, :], lhsT=fw1[:, :], rhs=zt[:, :], start=True, stop=True), reads=["fw1", zk], writes=["pSS"])
            else:
                P.op("pe", lambda e: e.matmul(pSS[0:64, :], lhsT=fw2[:, :], rhs=h1[:, :], start=True, stop=True), reads=["fw2", "tmpc"], writes=["pSS"])
            fr, fb = (4, 5) if layer == 0 else (6, 7)
            P.op("dve", lambda e, fr=fr, fb=fb: e.tensor_scalar(out=arg[:, :], in0=pSS[0:64, :], scalar1=fpar[:, fr:fr + 1], scalar2=fpar[:, fb:fb + 1],
                                                                op0=ALU.mult, op1=ALU.add), reads=["pSS", "fpar"], writes=["tmpc"])
            P.op("dve", lambda e: e.tensor_copy(out=argi[:, :], in_=arg[:, :]), reads=["tmpc"], writes=["argi"])
            P.op("dve", lambda e: e.tensor_copy(out=h1[:, :], in_=argi[:, :]), reads=["argi", "tmpc"], writes=["tmpc"])
            P.op("dve", lambda e: e.tensor_tensor(out=arg[:, :], in0=arg[:, :], in1=h1[:, :], op=ALU.subtract), reads=["tmpc"], writes=["tmpc"])
            P.op("dve", lambda e: e.scalar_tensor_tensor(out=arg[:, :], in0=arg[:, :], scalar=0.5, in1=arg[:, :], op0=ALU.is_gt, op1=ALU.subtract),
                 reads=["tmpc"], writes=["tmpc"])
            if layer == 0:
                P.op("act", lambda e: e.activation(out=h1[:, :], in_=arg[:, :], func=AF.Sin, scale=-2.0 * math.pi), reads=["tmpc"], writes=["tmpc"])
            else:
                P.op("act", lambda e, ch=ch: e.activation(out=h2T[:, ch * 512:(ch + 1) * 512], in_=arg[:, :], func=AF.Sin, scale=-2.0 * math.pi),
                     reads=["tmpc"], writes=["h2T"])

    Zkeys = lambda i: ["Z%d_%d" % (i, s) for s in range(128 // (2 * NPQH))]
    Z = [U[i][:, 0:NTOK].rearrange("a (c p) -> a c p", p=64) for i in range(3)]
    CVs = CV[:, :].rearrange("c (a p) -> c a p", p=64)
    E3 = CV[:, :].rearrange("a (c p) -> a c p", p=64)
    O3 = Ob[:, :].rearrange("a (c p) -> a c p", p=64)
    h2s = h2T[:, :].rearrange("j (a p) -> j a p", p=64)
    cnt = [0]

    def twiddle(psrc, pkey, Tr_, Ti_, trk, tik, outr, outi, okey, view, ieng="dve"):
        g = cnt[0]
        cnt[0] += 1
        w1, w1k = W1[g % 4], "W1_%d" % (g % 4)
        w2, w2k = W2[g % 4], "W2_%d" % (g % 4)
        cp_, cpk = cpy[g % 4], "cpr%d" % (g % 4)
        P.op("dve", lambda e: e.tensor_tensor(out=w1[:, :], in0=psrc[:, :], in1=Tr_[:, :], op=ALU.mult), reads=[pkey, trk], writes=[w1k])
        if g % 2 == 0:
            P.op("act", lambda e: e.activation(out=cp_[:, :], in_=psrc[:, :], func=AF.Copy), reads=[pkey], writes=[cpk])
            P.op("pool", lambda e: e.tensor_tensor(out=w2[:, :], in0=cp_[:, :], in1=Ti_[:, :], op=ALU.mult), reads=[cpk, tik], writes=[w2k])
        else:
            P.op("dve", lambda e: e.tensor_tensor(out=w2[:, :], in0=psrc[:, :], in1=Ti_[:, :], op=ALU.mult), reads=[pkey, tik], writes=[w2k])
        w1r, w1i = view(w1)
        w2r, w2i = view(w2)
        P.op("dve", lambda e: e.tensor_tensor(out=outr, in0=w1r, in1=w2i, op=ALU.subtract), reads=[w1k, w2k], writes=[okey])
        P.op("dve", lambda e: e.tensor_tensor(out=outi, in0=w2r, in1=w1i, op=ALU.add), reads=[w1k, w2k], writes=[okey])

    vv = lambda t: t[:, :].rearrange("k (j r x) -> k j r x", j=2, r=2)
    v_fwd = lambda t: (vv(t)[:, :, 0, :], vv(t)[:, :, 1, :])
    v_inv = v_fwd

    def fwd_stage1(src3, skey, c0, Ap, apk, ieng="dve"):
        for q2 in range(NPQH // 2):
            g = cnt[0]
            pa, pak = pA1[g % 2], "pA1_%d" % (g % 2)
            for jj in range(2):
                c = c0 + 2 * (2 * q2 + jj)
                P.op("pe", lambda e, c=c, pa=pa, jj=jj: e.matmul(pa[:, jj * 256:(jj + 1) * 256], lhsT=src3[:, c:c + 2, :], rhs=F1b[:, :], start=True, stop=True),
                     reads=[skey, "F1b"], writes=[pak])
            twiddle(pa, pak, TrTr, TiTi, "TrTr", "TiTi", Ap[:, 0, 2 * q2:2 * q2 + 2, :], Ap[:, 1, 2 * q2:2 * q2 + 2, :], apk, v_fwd, ieng=ieng)

    G2r, G2in, G2i = R12b[:, 0:128], R12b[:, 128:256], R12b[:, 256:384]
    R1, R2 = R12b[:, 0:256], R12b[:, 256:512]

    for grp4 in groups:
        for i in range(3):
            P.dma("sp", U[i][:, 1:NTOK + 1], Us[3 * grp4 + i], reads=["Us"], writes=["U%d" % i] + Zkeys(i), key="U%d" % i)
            P.op("pool", lambda e, i=i: e.memset(U[i][:, 0:1], 0.0), reads=["U%d" % i], writes=["U%d" % i] + Zkeys(i))
            P.op("pool", lambda e, i=i: e.memset(U[i][:, NTOK + 1:NTOK + 2], 0.0), reads=["U%d" % i], writes=["U%d" % i])
        for i in range(3):
            ci = 9 * grp4 + 3 * i
            bi = 3 * grp4 + i
            for ch in range(8):
                j0 = ch * 1024
                P.op("dve", lambda e, i=i, j0=j0, ci=ci, bi=bi: e.tensor_scalar(out=tmpc[:, :], in0=U[i][:, j0:j0 + 1024], scalar1=cw[:, ci:ci + 1],
                                                                                scalar2=cb[:, bi:bi + 1], op0=ALU.mult, op1=ALU.add),
                     reads=["U%d" % i, "cw", "cb"], writes=["tmpc"])
                P.op("dve", lambda e, i=i, j0=j0, ci=ci: e.scalar_tensor_tensor(out=tmpc[:, :], in0=U[i][:, j0 + 1:j0 + 1025], scalar=cw[:, ci + 1:ci + 2],
                                                                                in1=tmpc[:, :], op0=ALU.mult, op1=ALU.add),
                     reads=["U%d" % i, "cw", "tmpc"], writes=["tmpc"])
                P.op("dve", lambda e, i=i, j0=j0, ci=ci: e.scalar_tensor_tensor(out=CV[:, j0:j0 + 1024], in0=U[i][:, j0 + 2:j0 + 1026], scalar=cw[:, ci + 2:ci + 3],
                                                                                in1=tmpc[:, :], op0=ALU.mult, op1=ALU.add),
                     reads=["U%d" % i, "cw", "tmpc"], writes=["CV"])
            for pg in range(8):
                for pi in range(8):
                    p = pg * 8 + pi
                    P.op("pe", lambda e, p=p, pi=pi: e.transpose(out=pTz[:, pi, :], in_=CVs[:, :, p], identity=ident[:, :]),
                         reads=["CV", "ident"], writes=["pTz"])
                P.op("act", lambda e, i=i, pg=pg: e.activation(out=Z[i][:, :, pg * 8:(pg + 1) * 8].rearrange("a c p -> a p c"), in_=pTz[:, :, :], func=AF.Copy),
                     reads=["pTz"], writes=["U%d" % i] + Zkeys(i))
        for o in range(2):
            w3c0 = grp4 * 512 + o * 256
            for p in range(64):
                pe_, pek = (pSS, "pSS") if p % 2 == 0 else (pB, "pB")
                wn, wnk = win[p % 2], "win%d" % (p % 2)
                ec, eck = eoc[p % 2], "eoc%d" % (p % 2)
                P.op("pe", lambda e, p=p, w3c0=w3c0, pe_=pe_: e.matmul(pe_[:, 0:256], lhsT=h2s[:, :, p], rhs=fw3b[:, w3c0:w3c0 + 256], start=True, stop=True),
                     reads=["h2T", "fw3b"], writes=[pek])
                P.op("act", lambda e, p=p, grp4=grp4, wn=wn: e.activation(out=wn[:, :], in_=delta[:, grp4 * 128:(grp4 + 1) * 128], func=AF.Exp, scale=Swin[:, p:p + 1]),
                     reads=["delta", "Swin"], writes=[wnk])
                P.op("act", lambda e, pe_=pe_, ec=ec: e.activation(out=ec[:, :], in_=pe_[:, 0:256], func=AF.Copy), reads=[pek], writes=[eck])
                P.op("pool", lambda e, p=p, ec=ec, wn=wn: e.tensor_tensor(out=E3[:, :, p], in0=ec[:, 0:128], in1=wn[:, :], op=ALU.mult), reads=[eck, wnk], writes=["CV"])
                P.op("dve", lambda e, p=p, ec=ec, wn=wn: e.tensor_tensor(out=O3[:, :, p], in0=ec[:, 128:256], in1=wn[:, :], op=ALU.mult), reads=[eck, wnk], writes=["Ob"])
                if p == 0:
                    fc = (2 * grp4 + o) * 128
                    P.op("pe", lambda e, fc=fc: e.matmul(pY[0:1, 0:128], lhsT=h2T[:, 0:1], rhs=fw3f[:, fc:fc + 128], start=True, stop=True),
                         reads=["h2T", "fw3f"], writes=["pY"])
                    hc0 = (2 * grp4 + o) * 128
                    P.op("dve", lambda e, wn=wn: e.tensor_tensor(out=etmp[:, :], in0=pY[0:1, 0:128], in1=wn[0:1, :], op=ALU.mult), reads=["pY", wnk], writes=["etmp"])
                    P.op("dve", lambda e: e.tensor_copy(out=O3[0:1, :, 0], in_=etmp[:, :]), reads=["etmp"], writes=["Ob"])
                    P.op("dve", lambda e, hc0=hc0: e.tensor_tensor(out=E3[0:1, :, 0], in0=etmp[:, :], in1=hbrow[0:1, hc0:hc0 + 128], op=ALU.add),
                         reads=["etmp", "hbrow"], writes=["CV"])
            nsbt = 128 // (2 * NPQH)

            def filt_gen(sbt):
                c0 = sbt * 2 * NPQH
                Hs, hk = HsL[sbt % 2], "Hs%d" % (sbt % 2)
                for which, (src3, skey) in enumerate(((E3, "CV"), (O3, "Ob"))):
                    fwd_stage1(src3, skey, c0, ApF, "ApF")
                    yield
                    for qq in range(NPQH // 4):
                        rr = ApF[:, 0, 4 * qq:4 * qq + 4, :]
                        ri = ApF[:, 1, 4 * qq:4 * qq + 4, :]
                        if which == 0:
                            P.op("pe", lambda e, rr=rr: e.matmul(pSS[:, :], lhsT=G2r, rhs=rr, start=True, stop=False), reads=["R12b", "ApF"], writes=["pSS"])
                            P.op("pe", lambda e, ri=ri: e.matmul(pSS[:, :], lhsT=G2in, rhs=ri, start=False, stop=True), reads=["R12b", "ApF"], writes=["pSS"])
                        else:
                            P.op("pe", lambda e, rr=rr: e.matmul(pSS[:, :], lhsT=G2i, rhs=rr, start=True, stop=False), reads=["R12b", "ApF"], writes=["pSS"])
                            P.op("pe", lambda e, ri=ri: e.matmul(pSS[:, :], lhsT=G2r, rhs=ri, start=False, stop=True), reads=["R12b", "ApF"], writes=["pSS"])
                        P.op("act", lambda e, qq=qq, Hs=Hs, which=which: e.activation(out=Hs[:, which, 4 * qq:4 * qq + 4, :], in_=pSS[:, :].rearrange("k (q x) -> k q x", q=4), func=AF.Copy),
                             reads=["pSS"], writes=[hk])
                    yield

            def data_gen(sbt):
                c0 = sbt * 2 * NPQH
                zk = "Z0_%d" % sbt
                Hs, hk = HsL[sbt % 2], "Hs%d" % (sbt % 2)
                fwd_stage1(Z[0], zk, c0, ApD, "ApD", ieng="pool")
                yield
                for qq in range(NPQH // 4):
                    rr = ApD[:, 0, 4 * qq:4 * qq + 4, :]
                    ri = ApD[:, 1, 4 * qq:4 * qq + 4, :]
                    P.op("pe", lambda e, rr=rr: e.matmul(pXr[:, :], lhsT=G2r, rhs=rr, start=True, stop=False), reads=["R12b", "ApD"], writes=["pXr"])
                    P.op("pe", lambda e, ri=ri: e.matmul(pXr[:, :], lhsT=G2in, rhs=ri, start=False, stop=True), reads=["R12b", "ApD"], writes=["pXr"])
                    P.op("pe", lambda e, rr=rr: e.matmul(pXi[:, :], lhsT=G2i, rhs=rr, start=True, stop=False), reads=["R12b", "ApD"], writes=["pXi"])
                    P.op("pe", lambda e, ri=ri: e.matmul(pXi[:, :], lhsT=G2r, rhs=ri, start=False, stop=True), reads=["R12b", "ApD"], writes=["pXi"])
                    g = cnt[0]
                    cnt[0] += 1
                    w1, w1k = W1[g % 4], "W1_%d" % (g % 4)
                    w2, w2k = W2[g % 4], "W2_%d" % (g % 4)
                    cx, cxk = cpy[g % 4], "cpr%d" % (g % 4)
                    hr = Hs[:, 0, 4 * qq:4 * qq + 4, :].rearrange("k q x -> k (q x)")
                    hi = Hs[:, 1, 4 * qq:4 * qq + 4, :].rearrange("k q x -> k (q x)")
                    yr = Yb[:, 0, 4 * qq:4 * qq + 4, :].rearrange("k q x -> k (q x)")
                    yi = Yb[:, 1, 4 * qq:4 * qq + 4, :].rearrange("k q x -> k (q x)")
                    if g % 2 == 0:
                        P.op("act", lambda e, cx=cx: e.activation(out=cx[:, :], in_=pXi[:, :], func=AF.Copy), reads=["pXi"], writes=[cxk])
                    P.op("dve", lambda e, w1=w1, hr=hr: e.tensor_tensor(out=w1[:, :], in0=pXr[:, :], in1=hr, op=ALU.mult), reads=["pXr", hk], writes=[w1k])
                    if g % 2 == 0:
                        P.op("pool", lambda e, w2=w2, cx=cx, hi=hi: e.tensor_tensor(out=w2[:, :], in0=cx[:, :], in1=hi, op=ALU.mult), reads=[cxk, hk], writes=[w2k])
                    else:
                        P.op("dve", lambda e, w2=w2, hi=hi: e.tensor_tensor(out=w2[:, :], in0=pXi[:, :], in1=hi, op=ALU.mult), reads=["pXi", hk], writes=[w2k])
                    P.op("dve", lambda e, w1=w1, w2=w2, yr=yr: e.tensor_tensor(out=yr, in0=w1[:, :], in1=w2[:, :], op=ALU.subtract), reads=[w1k, w2k], writes=["Yb"])
                    P.op("dve", lambda e, w1=w1, hr=hr: e.tensor_tensor(out=w1[:, :], in0=pXi[:, :], in1=hr, op=ALU.mult), reads=["pXi", hk, "Yb"], writes=[w1k])
                    P.op("dve", lambda e, w2=w2, hi=hi: e.tensor_tensor(out=w2[:, :], in0=pXr[:, :], in1=hi, op=ALU.mult), reads=["pXr", hk, "Yb"], writes=[w2k])
                    P.op("dve", lambda e, w1=w1, w2=w2, yi=yi: e.tensor_tensor(out=yi, in0=w1[:, :], in1=w2[:, :], op=ALU.add), reads=[w1k, w2k], writes=["Yb"])
                yield
                for q2 in range(NPQH // 2):
                    for jj in range(2):
                        q = 2 * q2 + jj
                        P.op("pe", lambda e, q=q, jj=jj: e.matmul(pB[:, jj * 256:(jj + 1) * 256], lhsT=Yb[:, 0, q, :], rhs=R1, start=True, stop=False),
                             reads=["Yb", "R12b"], writes=["pB"])
                        P.op("pe", lambda e, q=q, jj=jj: e.matmul(pB[:, jj * 256:(jj + 1) * 256], lhsT=Yb[:, 1, q, :], rhs=R2, start=False, stop=True),
                             reads=["Yb", "R12b"], writes=["pB"])
                    twiddle(pB, "pB", TT2r, TT2i, "TT2r", "TT2i", Bp[:, 0, 2 * q2:2 * q2 + 2, :], Bp[:, 1, 2 * q2:2 * q2 + 2, :], "Bp", v_inv, ieng="pool")
                yield
                for hh in range(NPQH // 4):
                    for r in range(2):
                        rhs = Bp[:, r, 4 * hh:4 * hh + 4, :]
                        lt = IF1b[:, r * 128:(r + 1) * 128]
                        P.op("pe", lambda e, rhs=rhs, lt=lt, r=r: e.matmul(pY[:, :], lhsT=lt, rhs=rhs, start=(r == 0), stop=(r == 1)),
                             reads=["Bp", "IF1b"], writes=["pY"])
                    cc = c0 + 8 * hh
                    zi = 1 if o == 0 else 2
                    zo = 0 if o == 0 else 2
                    okeys = [zk] if o == 0 else ["U2", "Z2_%d" % sbt]
                    P.op("dve", lambda e, cc=cc, zi=zi, zo=zo: e.scalar_tensor_tensor(out=Z[zo][:, cc:cc + 8, :], in0=pY[:, :].rearrange("a (c p) -> a c p", p=64),
                                                                                      scalar=2.0 / NFFT, in1=Z[zi][:, cc:cc + 8, :], op0=ALU.mult, op1=ALU.mult),
                         reads=["pY", "U%d" % zi] + Zkeys(zi), writes=okeys)
                yield

            for _ in filt_gen(0):
                pass
            for sbt in range(nsbt):
                gens = [data_gen(sbt)] + ([filt_gen(sbt + 1)] if sbt + 1 < nsbt else [])
                while gens:
                    for gq in list(gens):
                        try:
                            next(gq)
                        except StopIteration:
                            gens.remove(gq)
        mv = mixT[:, 4 + grp4, :].rearrange("c (a p) -> c a p", p=64)
        for pg in range(4):
            for pi in range(16):
                p = pg * 16 + pi
                P.op("pe", lambda e, p=p, pi=pi: e.matmul(pXr[:, pi * 32:(pi + 1) * 32], lhsT=Z[2][:, :, p], rhs=selb[:, :], start=True, stop=True),
                     reads=["U2", "selb"] + Zkeys(2), writes=["pXr"])
            P.op("act", lambda e, pg=pg, mv=mv: e.activation(out=mv[:, :, pg * 16:(pg + 1) * 16].rearrange("c a p -> c p a"),
                                                             in_=pXr[:, :].rearrange("c (p a) -> c p a", a=32), func=AF.Copy),
                 reads=["pXr"], writes=["mixT"])
    P.build()


def emit_phase2_f(nc, outer, D, mixT, NT=2048):
    P = Prog(nc, sem_stack=outer, prefix="D_")
    x_tok = D("x_tok", [NT, 1024])
    cvec = D("cvec", [128, 8])
    w_ada = D("w_ada2", [1024, 4096])
    b_ada = D("b_ada2", [1, 4096])
    gvec = D("gvec", [3, 1024])
    gmix = D("gmix", [128, 8])
    w_out = D("w_out", [1024, 1024])
    w1 = D("w1", [1024, 4096])
    w2 = D("w2", [4096, 1024])
    out = D("out", [NT, 1024], kind="ExternalOutput")
    x1s = D("x1s", [NT, 1024], kind="Internal")

    sb, ps = P.sb, P.ps
    wout_b = sb("wout_b", [128, 8, 1024], BF16)
    mods = sb("mods", [128, 4096], F32)
    sbc = sb("sbc", [128, 8, 128], F32)
    ones = sb("ones", [128, 128], F32)
    ident = sb("ident", [128, 128], BF16)
    identf = sb("identf", [128, 128], F32)
    stage = [sb("stage%d" % i, [128, 2048], F32) for i in range(2)]
    w1b = [sb("w1b%d" % i, [128, 8, 512], BF16) for i in range(2)]
    w2b = [sb("w2b%d" % i, [128, 4, 1024], BF16) for i in range(2)]
    h2T = sb("h2T", [128, 8, 1024], BF16)
    f2acc = sb("f2acc", [128, 8, 1024], F32)
    brow = f2acc[0:1, 0:4, :].rearrange("p a b -> p (a b)")
    xt = [sb("xt%d" % i, [128, 1024], F32) for i in range(2)]
    sqm = sb("sqm", [128, 8, 128], BF16)
    onescol = sb("onescol", [128, 2], BF16)
    ysb = sb("ysb", [128, 1024], F32)
    tmp = sb("tmp", [128, 1024], F32)
    tmp2 = sb("tmp2", [128, 1024], F32)
    h2b = sb("h2b", [128, 1024], BF16)
    rl = [sb("rl%d" % i, [128, 512], F32) for i in range(2)]
    aT = [sb("aT%d" % i, [128, 4, 512], BF16) for i in range(2)]
    small = sb("small", [128, 32], F32)
    csb = sb("csb", [128, 8], F32)
    gmx = sb("gmx", [128, 8], F32)

    pA = ps("pA", [128, 1024], F32)
    pH = ps("pH", [128, 1024], F32)
    pT = ps("pT", [128, 1024], BF16)
    pM = [ps("pM%d" % i, [128, 512], F32) for i in range(2)]

    P.op("pool", lambda e: e.memset(ones[:, :], 1.0), writes=["ones"])
    P.op("pool", lambda e: e.memset(onescol[:, :], 1.0), writes=["onescol"])
    P.op("pool", lambda e: e.memset(identf[:, :], 0.0), writes=["identf"])
    identd = D("identd", [128, 128])
    P.dma("sp", identf[:, :], identd[:, :], writes=["identf"])
    P.op("dve", lambda e: e.tensor_copy(out=ident[:, :], in_=identf[:, :]), reads=["identf"], writes=["ident"])
    P.dma("sp", csb[:, :], cvec[:, :], writes=["csb"])
    P.dma("sp", gmx[:, :], gmix[:, :], writes=["gmx"])
    P.dma("sp", brow[:, :], b_ada[:, :], writes=["brow"])
    P.op("act", lambda e: e.activation(out=csb[:, :], in_=csb[:, :], func=AF.Silu), reads=["csb"], writes=["csb"])
    for k in range(8):
        P.op("dve", lambda e, k=k: e.tensor_scalar(out=sbc[:, k, :], in0=ones[:, :], scalar1=csb[:, k:k + 1],
                                                    scalar2=None, op0=ALU.mult), reads=["ones", "csb"], writes=["sbc"])
    wa = w_ada.rearrange("(k p) n -> p k n", p=128)
    for blk in range(16):
        st = stage[blk % 2]
        skey = "stage%d" % (blk % 2)
        stv = st[:, :].rearrange("p (k n) -> p k n", k=8)
        P.dma("sp", stv, wa[:, :, blk * 256:(blk + 1) * 256], writes=[skey])
        pm = pM[blk % 2]
        pkey = "pM%d" % (blk % 2)
        for k in range(8):
            P.op("pe", lambda e, k=k, pm=pm, stv=stv: e.matmul(pm[:, 0:256], lhsT=sbc[:, k, :], rhs=stv[:, k, :],
                                                               start=(k == 0), stop=False),
                 reads=["sbc", skey], writes=[pkey])
        P.op("pe", lambda e, pm=pm, blk=blk: e.matmul(pm[:, 0:256], lhsT=ones[0:1, :], rhs=brow[0:1, blk * 256:(blk + 1) * 256],
                                                     start=False, stop=True), reads=["ones", "brow"], writes=[pkey])
        P.op("act", lambda e, pm=pm, blk=blk: e.activation(out=mods[:, blk * 256:(blk + 1) * 256], in_=pm[:, 0:256], func=AF.Copy),
             reads=[pkey], writes=["mods"])
    gb = stage[0][:, :]
    P.dma("sp", gb[:, 0:1024], gvec[0:1, :].to_broadcast((128, 1024)), writes=["stage0"])
    gb1 = stage[1][:, :]
    P.dma("sp", gb1[:, 0:1024], gvec[1:2, :].to_broadcast((128, 1024)), writes=["stage1"])
    P.dma("sp", gb1[:, 1024:2048], gvec[2:3, :].to_broadcast((128, 1024)), writes=["stage1"])
    G1, SH2, G2, G3 = mods[:, 0:1024], mods[:, 1024:2048], mods[:, 2048:3072], mods[:, 3072:4096]
    P.op("dve", lambda e: e.tensor_tensor(out=G1, in0=G1, in1=gb[:, 0:1024], op=ALU.mult), reads=["mods", "stage0"], writes=["mods"])
    P.op("dve", lambda e: e.scalar_tensor_tensor(out=G2, in0=G2, scalar=1.0, in1=gb1[:, 0:1024], op0=ALU.add, op1=ALU.mult),
         reads=["mods", "stage1"], writes=["mods"])
    P.op("dve", lambda e: e.tensor_tensor(out=G3, in0=G3, in1=gb1[:, 1024:2048], op=ALU.mult), reads=["mods", "stage1"], writes=["mods"])
    wo = w_out.rearrange("(k p) n -> p k n", p=128)
    for c4 in range(4):
        st = stage[c4 % 2]
        skey = "stage%d" % (c4 % 2)
        stv = st[:, :].rearrange("p (k n) -> p k n", k=2)
        P.dma("sp", stv, wo[:, 2 * c4:2 * c4 + 2, :], writes=[skey])
        for kk in range(2):
            k = 2 * c4 + kk
            P.op("dve" if kk == 0 else "pool", lambda e, k=k, kk=kk, stv=stv: e.tensor_scalar(
                out=wout_b[:, k, :], in0=stv[:, kk, :], scalar1=gmx[:, k:k + 1], scalar2=None, op0=ALU.mult),
                reads=[skey, "gmx"], writes=["wout_b"])

    def sumsq(src, dst, reads, key, eng="dve"):
        w = src.shape[-1]
        P.op("pool", lambda e: e.tensor_tensor(out=tmp2[:, 0:w], in0=src, in1=src, op=ALU.mult), reads=reads, writes=["tmp2"])
        P.op(eng, lambda e: e.tensor_reduce(out=dst, in_=tmp2[:, 0:w], axis=AX.X, op=ALU.add), reads=["tmp2"] + list(reads), writes=[key])

    nsm = [0]

    def smallcol(n=1):
        c = nsm[0] % 16
        nsm[0] += 1
        return small[:, 2 * c:2 * c + n], "small%d" % c

    w1v = w1.rearrange("(k p) n -> p k n", p=128)
    w2v = w2.rearrange("(c p) n -> p c n", p=128)
    it = [0]
    for half in range(NT // 1024):
        for tile in range(8):
            t0 = half * 1024 + tile * 128
            i = it[0]
            it[0] += 1
            X, xk = xt[i % 2], "xt%d" % (i % 2)
            tl = t0
            MBk = lambda k, tl=tl: mixT[:, k, tl:tl + 128]
            mbk = "mixT"
            P.dma("sp", X[:, :], x_tok[t0:t0 + 128, :], writes=[xk])
            P.op("pool", lambda e, tl=tl: e.tensor_tensor(out=sqm[:, :, :], in0=mixT[:, :, tl:tl + 128], in1=mixT[:, :, tl:tl + 128], op=ALU.mult),
                 reads=["mixT"], writes=["sqm"])
            pst = pM[i % 2]
            pstk = "pM%d" % (i % 2)
            for grp in range(2):
                for k in range(4):
                    P.op("pe", lambda e, grp=grp, k=k, pst=pst: e.matmul(pst[:, grp:grp + 1], lhsT=sqm[:, 4 * grp + k, :], rhs=onescol[:, 0:1],
                                                                        start=(k == 0), stop=(k == 3)), reads=["sqm", "onescol"], writes=[pstk])
            rr, rrk = smallcol(2)
            _rs(P, None, pst[:, 0:2], rr, 512.0, "rmix", [pstk], rrk)
            for nh in range(2):
                for k in range(4):
                    P.op("pe", lambda e, k=k, nh=nh, MBk=MBk: e.matmul(pA[:, nh * 512:(nh + 1) * 512], lhsT=MBk(k),
                                                                     rhs=wout_b[:, k, nh * 512:(nh + 1) * 512],
                                                                     start=(k == 0), stop=(k == 3)),
                         reads=[mbk, "wout_b"], writes=["pA%d" % nh])
                for k in range(4, 8):
                    P.op("pe", lambda e, k=k, nh=nh, MBk=MBk: e.matmul(pH[:, nh * 512:(nh + 1) * 512], lhsT=MBk(k),
                                                                     rhs=wout_b[:, k, nh * 512:(nh + 1) * 512],
                                                                     start=(k == 4), stop=(k == 7)),
                         reads=[mbk, "wout_b"], writes=["pH%d" % nh])
            P.op("act", lambda e, rr=rr: e.activation(out=ysb[:, :], in_=pA[:, :], func=AF.Copy, scale=rr[:, 0:1]),
                 reads=["pA0", "pA1", rrk], writes=["ysb"])
            P.op("dve", lambda e, rr=rr: e.scalar_tensor_tensor(out=ysb[:, :], in0=pH[:, :], scalar=rr[:, 1:2], in1=ysb[:, :],
                                                                op0=ALU.mult, op1=ALU.add),
                 reads=["pH0", "pH1", rrk, "ysb"], writes=["ysb"])
            ss2, ss2k = smallcol(1)
            r2, r2k = smallcol(1)
            sumsq(ysb[:, :], ss2[:, 0:1], ["ysb"], ss2k)
            _rs(P, None, ss2, r2, 1024.0, "ry", [ss2k], r2k)
            P.op("dve", lambda e, r2=r2: e.scalar_tensor_tensor(out=tmp[:, :], in0=ysb[:, :], scalar=r2[:, 0:1], in1=G1,
                                                                op0=ALU.mult, op1=ALU.mult),
                 reads=["ysb", r2k, "mods"], writes=["tmp"])
            P.op("pool", lambda e, X=X: e.tensor_tensor(out=X[:, :], in0=tmp[:, :], in1=X[:, :], op=ALU.add),
                 reads=["tmp", xk], writes=[xk])
            P.dma("pool", x1s[t0:t0 + 128, :], X[:, :], reads=[xk], writes=["x1s%d" % (t0 // 128)])
            ss3, ss3k = smallcol(1)
            r3, r3k = smallcol(1)
            sumsq(X[:, :], ss3[:, 0:1], [xk], ss3k)
            _rs(P, None, ss3, r3, 1024.0, "r1", [ss3k], r3k)
            P.op("dve", lambda e, r3=r3, X=X: e.scalar_tensor_tensor(out=tmp[:, :], in0=X[:, :], scalar=r3[:, 0:1], in1=G2,
                                                                     op0=ALU.mult, op1=ALU.mult),
                 reads=[xk, r3k, "mods"], writes=["tmp"])
            P.op("pool", lambda e: e.tensor_tensor(out=h2b[:, :], in0=tmp[:, :], in1=SH2, op=ALU.add),
                 reads=["tmp", "mods"], writes=["h2b"])
            for k in range(8):
                P.op("pe", lambda e, k=k: e.transpose(out=pT[:, k * 128:(k + 1) * 128], in_=h2b[:, k * 128:(k + 1) * 128],
                                                      identity=ident[:, :]), reads=["h2b", "ident"], writes=["pT"])
            P.op("act", lambda e, tile=tile: e.activation(out=h2T[:, :, tile * 128:(tile + 1) * 128],
                                                          in_=pT[:, :].rearrange("p (k t) -> p k t", k=8), func=AF.Copy),
                 reads=["pT"], writes=["h2T"])
        for j in range(8):
            W1B, w1k = w1b[j % 2], "w1b%d" % (j % 2)
            W2B, w2k = w2b[j % 2], "w2b%d" % (j % 2)
            for hh in range(2):
                st, skey = stage[hh], "stage%d" % hh
                stv = st[:, :].rearrange("p (k n) -> p k n", k=8)
                P.dma("sp", stv, w1v[:, :, j * 512 + hh * 256:j * 512 + (hh + 1) * 256], writes=[skey])
                P.op("dve" if hh == 0 else "pool", lambda e, W1B=W1B, stv=stv, hh=hh: e.tensor_copy(
                    out=W1B[:, :, hh * 256:(hh + 1) * 256], in_=stv), reads=[skey], writes=[w1k])
            for hh in range(2):
                st, skey = stage[hh], "stage%d" % hh
                stv = st[:, :].rearrange("p (c n) -> p c n", c=2)
                P.dma("sp", stv, w2v[:, j * 4 + hh * 2:j * 4 + hh * 2 + 2, :], writes=[skey])
                P.op("dve" if hh == 0 else "pool", lambda e, W2B=W2B, stv=stv, hh=hh: e.tensor_copy(
                    out=W2B[:, hh * 2:hh * 2 + 2, :], in_=stv), reads=[skey], writes=[w2k])
            for tg in range(2):
                AT, atk = aT[tg], "aT%d" % tg
                for hc in range(4):
                    pm, pkey = pM[hc % 2], "pM%d" % (hc % 2)
                    RL, rlk = rl[hc % 2], "rl%d" % (hc % 2)
                    for k in range(8):
                        P.op("pe", lambda e, k=k, hc=hc, tg=tg, pm=pm, W1B=W1B: e.matmul(
                            pm[:, :], lhsT=W1B[:, k, hc * 128:(hc + 1) * 128], rhs=h2T[:, k, tg * 512:(tg + 1) * 512],
                            start=(k == 0), stop=(k == 7)), reads=[w1k, "h2T"], writes=[pkey])
                    P.op("act", lambda e, pm=pm, RL=RL: e.activation(out=RL[:, :], in_=pm[:, :], func=AF.Relu),
                         reads=[pkey], writes=[rlk])
                    P.op("pool", lambda e, RL=RL, AT=AT, hc=hc: e.tensor_tensor(out=AT[:, hc, :], in0=RL[:, :], in1=RL[:, :], op=ALU.mult),
                         reads=[rlk], writes=[atk])
                for tt in range(4):
                    tile = tg * 4 + tt
                    for nh in range(2):
                        pp, ppk = (pA, "pA%d" % nh) if tt % 2 == 0 else (pH, "pH%d" % nh)
                        for hc in range(4):
                            P.op("pe", lambda e, hc=hc, tt=tt, nh=nh, pp=pp, AT=AT, W2B=W2B: e.matmul(
                                pp[:, nh * 512:(nh + 1) * 512], lhsT=AT[:, hc, tt * 128:(tt + 1) * 128],
                                rhs=W2B[:, hc, nh * 512:(nh + 1) * 512], start=(hc == 0), stop=(hc == 3)),
                                reads=[atk, w2k], writes=[ppk])
                        fv = f2acc[:, tile, nh * 512:(nh + 1) * 512]
                        fk = "f2_%d_%d" % (tile, nh)
                        if j == 0:
                            P.op("dve", lambda e, fv=fv, pp=pp, nh=nh: e.tensor_copy(out=fv, in_=pp[:, nh * 512:(nh + 1) * 512]),
                                 reads=[ppk], writes=[fk])
                        else:
                            P.op("dve", lambda e, fv=fv, pp=pp, nh=nh: e.tensor_tensor(out=fv, in0=pp[:, nh * 512:(nh + 1) * 512], in1=fv, op=ALU.add),
                                 reads=[ppk, fk], writes=[fk])
        for tile in range(8):
            t0 = half * 1024 + tile * 128
            i = it[0]
            it[0] += 1
            X, xk = xt[i % 2], "xt%d" % (i % 2)
            P.dma("sp", X[:, :], x1s[t0:t0 + 128, :], reads=["x1s%d" % (t0 // 128)], writes=[xk], key=xk)
            fkeys = ["f2_%d_%d" % (tile, nh) for nh in range(2)]
            ss4, ss4k = smallcol(1)
            r4, r4k = smallcol(1)
            sumsq(f2acc[:, tile, :], ss4[:, 0:1], fkeys, ss4k)
            _rs(P, None, ss4, r4, 1024.0, "rf", [ss4k], r4k)
            P.op("dve", lambda e, r4=r4, tile=tile: e.scalar_tensor_tensor(out=tmp[:, :], in0=f2acc[:, tile, :], scalar=r4[:, 0:1], in1=G3,
                                                                          op0=ALU.mult, op1=ALU.mult),
                 reads=fkeys + [r4k, "mods"], writes=["tmp"])
            P.op("pool", lambda e, X=X: e.tensor_tensor(out=X[:, :], in0=tmp[:, :], in1=X[:, :], op=ALU.add),
                 reads=["tmp", xk], writes=[xk])
            P.dma("pool", out[t0:t0 + 128, :], X[:, :], reads=[xk], writes=["out%d" % (t0 // 128)])
    P.build()


def build_fused(upto=4, dbg=False):
    nc = bass.Bass("TRN2", target_bir_lowering=False)
    outer = ExitStack()
    D = DramReg(nc)
    mixT = outer.enter_context(nc.sbuf_tensor("mixT_keep", [128, 8, NOWN], BF16))
    with ExitStack() as es_a:
        hTw = es_a.enter_context(nc.sbuf_tensor("hTw_keep", [128, 8, NW], BF16))
        emit_attn_norm_f(nc, outer, D, hTw)
        if upto >= 1:
            emit_attn_f(nc, outer, D, mixT, hTw)
    if upto >= 2:
        emit_hyproj_f(nc, outer, D)
    if upto >= 3:
        emit_hyena_h(nc, outer, D, mixT, groups=(0, 1, 2, 3), prefix="C0_")
    if upto >= 4:
        emit_phase2_f(nc, outer, D, mixT)
    if dbg:
        P = Prog(nc, sem_stack=outer, prefix="E_")
        dd = D("dbg_mixT", [128, 8, NOWN], kind="ExternalOutput", dt=BF16)
        P.dma("sp", dd[:, :, :], mixT[:, :, :], reads=["mixT"], writes=["dbg"])
        P.build()
    outer.close()
    return nc


def fused_inputs(inp):
    f = lambda a: np.ascontiguousarray(a, dtype=np.float32)
    K = _hy_consts()
    K.update(_hy_consts_h())
    C, S = _rope_tables()
    M = _attn_masks()
    hsel = np.zeros((128, 64), np.float32)
    hsel[0:64, 0] = 1.0
    hsel[64:128, 32] = 1.0
    w_in = inp["w_in"][0]
    hy_w = w_in[:, 1536:]
    w_incA = []
    for hp in range(4):
        cs = slice(128 * hp, 128 * hp + 128)
        wq, wk, wv = w_in[:, 0:512][:, cs], w_in[:, 512:1024][:, cs], w_in[:, 1024:1536][:, cs]
        w_incA.append(np.concatenate([wq, _perm_cols(wq), wk, _perm_cols(wk), wv, np.zeros((1024, 128), np.float32)], axis=1))
    w_incA = f(np.stack(w_incA))
    w_incH = f(np.concatenate([hy_w[:, a * 512 + 128 * g:a * 512 + 128 * g + 128] for g in range(4) for a in range(3)], axis=1))
    cwv = inp["conv_w"][0].reshape(3, 3, 512)
    cbv = inp["conv_b"][0].reshape(3, 512)
    cw4 = np.zeros((128, 36), np.float32)
    cb4 = np.zeros((128, 12), np.float32)
    for g in range(4):
        for a in range(3):
            cb4[:, 3 * g + a] = cbv[a, 128 * g:128 * g + 128]
            for t in range(3):
                cw4[:, 9 * g + 3 * a + t] = cwv[t, a, 128 * g:128 * g + 128]
    w3 = inp["filt_w3"][0].reshape(64, 2, 2, 4, 128).transpose(0, 3, 1, 2, 4).reshape(64, 2048)
    hb = inp["hyena_bias"][0]
    hbcol4 = np.zeros((128, 4, 2, 64), np.float32)
    for g in range(4):
        for cp in range(2):
            hbcol4[64 * cp:64 * cp + 64, g, :, :] = hb[:, 128 * g:128 * g + 128][:, cp::2][None, :, :]
    cols2 = np.r_[2048:3072, 3072:6144]
    shared = {
        "w_ada1": f(inp["w_ada"][0][:, 0:2048]), "b_ada1": f(inp["b_ada"][0][0:2048].reshape(16, 128).T),
        "gpre": f(inp["g_pre_mix"][0].reshape(8, 128).T), "w_incA": w_incA, "w_incH": w_incH,
        "maskd": f(M), "hseld": hsel, "identd": np.eye(128, dtype=np.float32),
        "F1cat_h": K["F1cat_h"], "TrTr_h": K["TrTr_h"], "TiTi_h": K["TiTi_h"], "R12": K["R12"], "TT2r_h": K["TT2r_h"], "TT2i_h": K["TT2i_h"],
        "IF1_h": K["IF1_h"], "Swin": K["Swin"], "zembT": K["zembT"],
        "delta4": f(np.broadcast_to(K["deltas"][None, :], (128, 512))),
        "hbrow": f(np.stack([hb[o_, 128 * g_:128 * g_ + 128] for g_ in range(4) for o_ in range(2)]).reshape(1, 1024)),
        "fw1": f(inp["filt_w1"][0]), "fw2": f(inp["filt_w2"][0]), "fw3c4": f(w3),
        "fpar": f(np.stack([inp["filt_b1"][0], inp["filt_freq1"][0], inp["filt_b2"][0], inp["filt_freq2"][0]], axis=1)),
        "cw4": cw4, "cb4": cb4,
        "w_ada2": f(inp["w_ada"][0][:, cols2]), "b_ada2": f(inp["b_ada"][0][cols2][None, :]),
        "gvec": f(np.stack([inp["g_post_mix"][0], inp["g_pre_mlp"][0], inp["g_post_mlp"][0]])),
        "gmix": f(np.concatenate([inp["g_attn_out"][0], inp["g_hyena_out"][0]]).reshape(8, 128).T),
        "w_out": f(inp["w_out"][0]), "w1": f(inp["w_mlp1"][0]), "w2": f(inp["w_mlp2"][0]),
    }
    in_maps = []
    for core in range(8):
        b, j = core // 4, core % 4
        lo = NOWN * j - 1024
        idx = np.arange(lo, lo + NW)
        ok = (idx >= 0) & (idx < NTOK)
        idc = np.clip(idx, 0, NTOK - 1)
        xTw = np.where(ok[None, :], inp["x"][b].T[:, idc], 0.0)
        sel = np.zeros((128, 32), np.float32)
        sel[32 * j + np.arange(32), np.arange(32)] = 1.0
        m = dict(shared)
        m.update({
            "xTw": f(xTw), "validw": f(ok.astype(np.float32).reshape(32, 128).T),
            "ropeCw": f(C[:, idc]), "ropeSw": f(S[:, idc]),
            "xT": f(inp["x"][b].T), "cvec": f(inp["c"][b].reshape(8, 128).T),
            "seld": sel, "x_tok": f(inp["x"][b, NOWN * j:NOWN * (j + 1)]),
        })
        in_maps.append(m)
    return in_maps


def kernel(**inputs):
    inp = {k: np.asarray(v) for k, v in inputs.items()}
    nc = build_fused()
    in_maps = fused_inputs(inp)
    res = run_bass_kernel_spmd(nc, in_maps, core_ids=list(range(8)))
    out = np.zeros((2, NTOK, 1024), np.float32)
    for core in range(8):
        b, j = core // 4, core % 4
        out[b, NOWN * j:NOWN * (j + 1)] = res.results[core]["out"]
    return out
```

```python
import math
import numpy as np
from concourse.bass_utils import run_bass_kernel_spmd

from contextlib import ExitStack
import concourse.bass as bass
import concourse.mybir as mybir

F32 = mybir.dt.float32
BF16 = mybir.dt.bfloat16
ALU = mybir.AluOpType
AF = mybir.ActivationFunctionType
AX = mybir.AxisListType

ENGS = ("pe", "act", "dve", "pool", "sp")


class Prog:
    def __init__(self, nc, sem_stack=None, prefix=""):
        self.nc = nc
        self.ops = []
        self.state = {}
        self.es = ExitStack()
        self.sem_stack = sem_stack if sem_stack is not None else self.es
        self.prefix = prefix
        self.ndma = 0

    def sb(self, name, shape, dt):
        return self.es.enter_context(self.nc.sbuf_tensor(self.prefix + "sb_" + name, list(shape), dt))

    def ps(self, name, shape, dt):
        return self.es.enter_context(self.nc.psum_tensor(self.prefix + "ps_" + name, list(shape), dt))

    def op(self, eng, fn, reads=(), writes=(), dma=False, grp=None):
        deps = set()
        psum_r = [b for b in reads if len(b) > 1 and b[0] == "p" and b[1].isupper()]
        if psum_r:
            reads = [b for b in reads if b not in psum_r]
            writes = list(writes) + [b for b in psum_r if b not in writes]
        for b in reads:
            st = self.state.setdefault(b, [None, []])
            if st[0] is not None:
                deps.add(st[0])
        for b in writes:
            st = self.state.setdefault(b, [None, []])
            if st[0] is not None:
                deps.add(st[0])
            deps.update(st[1])
        idx = len(self.ops)
        if dma and grp is None:
            grp = "dma%d" % (self.ndma % 12)
            self.ndma += 1
        self.ops.append(dict(eng=eng, fn=fn, deps=deps, dma=dma, grp=grp, sig=dma))
        for b in reads:
            self.state[b][1].append(idx)
        for b in writes:
            self.state[b] = [idx, []]
        return idx

    def dma(self, eng, out, in_, reads=(), writes=(), grp=None, key=None):
        if grp is None:
            grp = "dma_" + str(key if key is not None else (reads[0] if reads else writes[0]))
        return self.op(eng, lambda e: e.dma_start(out=out, in_=in_), reads, writes, dma=True, grp=grp)

    def build(self):
        nc = self.nc
        ops = self.ops

        def needs_wait(o, d):
            if d["dma"] or o["dma"]:
                return True
            if o["eng"] != d["eng"]:
                return True
            return o["eng"] != "pe"

        for o in ops:
            for di in o["deps"]:
                if needs_wait(o, ops[di]):
                    ops[di]["sig"] = True
        cnt = {}
        sems = {}

        def getsem(key):
            if key not in sems:
                sems[key] = self.sem_stack.enter_context(nc.semaphore(self.prefix + "s_" + str(key).replace(" ", "")))
            return sems[key]

        for o in ops:
            if not o["sig"]:
                continue
            if o["dma"]:
                key = o["grp"]
                cnt[key] = cnt.get(key, 0) + 16
                o["tok"] = (key, cnt[key], 16)
            else:
                base = "e_" + o["eng"]
                gen = cnt.get(base + "_gen", 0)
                key = "%s_%d" % (base, gen)
                cnt[key] = cnt.get(key, 0) + 1
                o["tok"] = (key, cnt[key], 1)
                if cnt[key] >= 30000:
                    cnt[base + "_gen"] = gen + 1
        for key in list(cnt.keys()):
            if not key.endswith("_gen"):
                getsem(key)
        per = {e: [] for e in ENGS}
        for o in ops:
            per[o["eng"]].append(o)
        final_dma = {k: v for k, v in cnt.items() if k.startswith("dma")}
        self.n_sems = len(sems)

        def replay(eng_name, e):
            waited = {}
            for o in per[eng_name]:
                for di in sorted(o["deps"]):
                    d = ops[di]
                    if not needs_wait(o, d):
                        continue
                    key, val, _ = d["tok"]
                    if waited.get(key, 0) < val:
                        e.wait_ge(sems[key], val)
                        waited[key] = val
                inst = o["fn"](e)
                if o["sig"]:
                    key, val, inc = o["tok"]
                    inst.then_inc(sems[key], inc)
            if eng_name == "sp":
                for key, val in final_dma.items():
                    if waited.get(key, 0) < val:
                        e.wait_ge(sems[key], val)

        with nc.Block() as block:
            @block.sync
            def _(e):
                replay("sp", e)

            @block.tensor
            def _(e):
                replay("pe", e)

            @block.scalar
            def _(e):
                replay("act", e)

            @block.vector
            def _(e):
                replay("dve", e)

            @block.gpsimd
            def _(e):
                replay("pool", e)
        self.es.close()


EPS = 1e-6


def _rs(P, eng_stats, ss, r, n, nm, reads, key):
    P.op("dve", lambda e: e.tensor_scalar(out=r, in0=ss, scalar1=1.0 / n, scalar2=EPS, op0=ALU.mult, op1=ALU.add),
         reads=reads, writes=[key])
    P.op("act", lambda e: e.activation(out=r, in_=r, func=AF.Sqrt), reads=[key], writes=[key])
    P.op("dve", lambda e: e.reciprocal(out=r, in_=r), reads=[key], writes=[key])


def build_phase2(NT=2048):
    nc = bass.Bass("TRN2", target_bir_lowering=False)
    P = Prog(nc)
    D = lambda name, shape, kind="ExternalInput": nc.dram_tensor(name, list(shape), F32, kind=kind).ap()
    x_tok = D("x_tok", [NT, 1024])
    mix_tok = D("mix_tok", [NT, 1024])
    mixT = D("mixT", [1024, NT])
    cvec = D("cvec", [128, 8])
    w_ada = D("w_ada", [1024, 4096])
    b_ada = D("b_ada", [1, 4096])
    gvec = D("gvec", [3, 1024])
    gmix = D("gmix", [128, 8])
    w_out = D("w_out", [1024, 1024])
    w1 = D("w1", [1024, 4096])
    w2 = D("w2", [4096, 1024])
    out = D("out", [NT, 1024], kind="ExternalOutput")
    x1s = D("x1s", [NT, 1024], kind="Internal")

    sb, ps = P.sb, P.ps
    wout_b = sb("wout_b", [128, 8, 1024], BF16)
    mods = sb("mods", [128, 4096], F32)
    sbc = sb("sbc", [128, 8, 128], F32)
    ones = sb("ones", [128, 128], F32)
    ident = sb("ident", [128, 128], BF16)
    identf = sb("identf", [128, 128], F32)
    stage = [sb("stage%d" % i, [128, 2048], F32) for i in range(2)]
    w1b = [sb("w1b%d" % i, [128, 8, 512], BF16) for i in range(2)]
    w2b = [sb("w2b%d" % i, [128, 4, 1024], BF16) for i in range(2)]
    h2T = sb("h2T", [128, 8, 1024], BF16)
    f2acc = sb("f2acc", [128, 8, 1024], F32)
    xt = [sb("xt%d" % i, [128, 1024], F32) for i in range(2)]
    mt = sb("mt", [128, 1024], F32)
    mTs = sb("mTs", [128, 8, 128], F32)
    mTb = [sb("mTb%d" % i, [128, 8, 128], BF16) for i in range(2)]
    ysb = sb("ysb", [128, 1024], F32)
    tmp = sb("tmp", [128, 1024], F32)
    tmp2 = sb("tmp2", [128, 1024], F32)
    h2b = sb("h2b", [128, 1024], BF16)
    rl = [sb("rl%d" % i, [128, 512], F32) for i in range(2)]
    aT = [sb("aT%d" % i, [128, 4, 512], BF16) for i in range(2)]
    small = sb("small", [128, 32], F32)
    csb = sb("csb", [128, 8], F32)
    gmx = sb("gmx", [128, 8], F32)
    brow = sb("brow", [1, 4096], F32)

    pA = ps("pA", [128, 1024], F32)
    pH = ps("pH", [128, 1024], F32)
    pT = ps("pT", [128, 1024], BF16)
    pM = [ps("pM%d" % i, [128, 512], F32) for i in range(2)]

    P.op("pool", lambda e: e.memset(ones[:, :], 1.0), writes=["ones"])
    P.op("pool", lambda e: e.memset(identf[:, :], 0.0), writes=["identf"])
    identd = D("identd", [128, 128])
    P.dma("sp", identf[:, :], identd[:, :], writes=["identf"])
    P.op("dve", lambda e: e.tensor_copy(out=ident[:, :], in_=identf[:, :]), reads=["identf"], writes=["ident"])
    P.dma("sp", csb[:, :], cvec[:, :], writes=["csb"])
    P.dma("sp", gmx[:, :], gmix[:, :], writes=["gmx"])
    P.dma("sp", brow[:, :], b_ada[:, :], writes=["brow"])
    P.op("act", lambda e: e.activation(out=csb[:, :], in_=csb[:, :], func=AF.Silu), reads=["csb"], writes=["csb"])
    for k in range(8):
        P.op("dve", lambda e, k=k: e.tensor_scalar(out=sbc[:, k, :], in0=ones[:, :], scalar1=csb[:, k:k + 1],
                                                    scalar2=None, op0=ALU.mult), reads=["ones", "csb"], writes=["sbc"])
    wa = w_ada.rearrange("(k p) n -> p k n", p=128)
    for blk in range(16):
        st = stage[blk % 2]
        skey = "stage%d" % (blk % 2)
        stv = st[:, :].rearrange("p (k n) -> p k n", k=8)
        P.dma("sp", stv, wa[:, :, blk * 256:(blk + 1) * 256], writes=[skey])
        pm = pM[blk % 2]
        pkey = "pM%d" % (blk % 2)
        for k in range(8):
            P.op("pe", lambda e, k=k, pm=pm, stv=stv: e.matmul(pm[:, 0:256], lhsT=sbc[:, k, :], rhs=stv[:, k, :],
                                                               start=(k == 0), stop=False),
                 reads=["sbc", skey], writes=[pkey])
        P.op("pe", lambda e, pm=pm, blk=blk: e.matmul(pm[:, 0:256], lhsT=ones[0:1, :], rhs=brow[0:1, blk * 256:(blk + 1) * 256],
                                                     start=False, stop=True), reads=["ones", "brow"], writes=[pkey])
        P.op("act", lambda e, pm=pm, blk=blk: e.activation(out=mods[:, blk * 256:(blk + 1) * 256], in_=pm[:, 0:256], func=AF.Copy),
             reads=[pkey], writes=["mods"])
    gb = stage[0][:, :]
    P.dma("sp", gb[:, 0:1024], gvec[0:1, :].to_broadcast((128, 1024)), writes=["stage0"])
    gb1 = stage[1][:, :]
    P.dma("sp", gb1[:, 0:1024], gvec[1:2, :].to_broadcast((128, 1024)), writes=["stage1"])
    P.dma("sp", gb1[:, 1024:2048], gvec[2:3, :].to_broadcast((128, 1024)), writes=["stage1"])
    G1, SH2, G2, G3 = mods[:, 0:1024], mods[:, 1024:2048], mods[:, 2048:3072], mods[:, 3072:4096]
    P.op("dve", lambda e: e.tensor_tensor(out=G1, in0=G1, in1=gb[:, 0:1024], op=ALU.mult), reads=["mods", "stage0"], writes=["mods"])
    P.op("dve", lambda e: e.scalar_tensor_tensor(out=G2, in0=G2, scalar=1.0, in1=gb1[:, 0:1024], op0=ALU.add, op1=ALU.mult),
         reads=["mods", "stage1"], writes=["mods"])
    P.op("dve", lambda e: e.tensor_tensor(out=G3, in0=G3, in1=gb1[:, 1024:2048], op=ALU.mult), reads=["mods", "stage1"], writes=["mods"])
    wo = w_out.rearrange("(k p) n -> p k n", p=128)
    for c4 in range(4):
        st = stage[c4 % 2]
        skey = "stage%d" % (c4 % 2)
        stv = st[:, :].rearrange("p (k n) -> p k n", k=2)
        P.dma("sp", stv, wo[:, 2 * c4:2 * c4 + 2, :], writes=[skey])
        for kk in range(2):
            k = 2 * c4 + kk
            P.op("dve" if kk == 0 else "pool", lambda e, k=k, kk=kk, stv=stv: e.tensor_scalar(
                out=wout_b[:, k, :], in0=stv[:, kk, :], scalar1=gmx[:, k:k + 1], scalar2=None, op0=ALU.mult),
                reads=[skey, "gmx"], writes=["wout_b"])

    def sumsq(src, dst, reads, key, eng="dve"):
        w = src.shape[-1]
        P.op("pool", lambda e: e.tensor_tensor(out=tmp2[:, 0:w], in0=src, in1=src, op=ALU.mult), reads=reads, writes=["tmp2"])
        P.op(eng, lambda e: e.tensor_reduce(out=dst, in_=tmp2[:, 0:w], axis=AX.X, op=ALU.add), reads=["tmp2"] + list(reads), writes=[key])

    nsm = [0]

    def smallcol(n=1):
        c = nsm[0] % 16
        nsm[0] += 1
        return small[:, 2 * c:2 * c + n], "small%d" % c

    w1v = w1.rearrange("(k p) n -> p k n", p=128)
    w2v = w2.rearrange("(c p) n -> p c n", p=128)
    mTv = mixT.rearrange("(k p) t -> p k t", p=128)
    it = [0]
    for half in range(NT // 1024):
        for tile in range(8):
            t0 = half * 1024 + tile * 128
            i = it[0]
            it[0] += 1
            X, xk = xt[i % 2], "xt%d" % (i % 2)
            MB, mbk = mTb[i % 2], "mTb%d" % (i % 2)
            P.dma("sp", X[:, :], x_tok[t0:t0 + 128, :], writes=[xk])
            P.dma("sp", mt[:, :], mix_tok[t0:t0 + 128, :], writes=["mt"])
            P.dma("sp", mTs[:, :, :], mTv[:, :, t0:t0 + 128], writes=["mTs"])
            P.op("pool", lambda e, MB=MB: e.tensor_copy(out=MB[:, :, :], in_=mTs[:, :, :]), reads=["mTs"], writes=[mbk])
            ss, ssk = smallcol(2)
            rr, rrk = smallcol(2)
            sumsq(mt[:, 0:512], ss[:, 0:1], ["mt"], ssk)
            sumsq(mt[:, 512:1024], ss[:, 1:2], ["mt", ssk], ssk)
            _rs(P, None, ss, rr, 512.0, "rmix", [ssk], rrk)
            for nh in range(2):
                for k in range(4):
                    P.op("pe", lambda e, k=k, nh=nh, MB=MB: e.matmul(pA[:, nh * 512:(nh + 1) * 512], lhsT=MB[:, k, :],
                                                                     rhs=wout_b[:, k, nh * 512:(nh + 1) * 512],
                                                                     start=(k == 0), stop=(k == 3)),
                         reads=[mbk, "wout_b"], writes=["pA%d" % nh])
                for k in range(4, 8):
                    P.op("pe", lambda e, k=k, nh=nh, MB=MB: e.matmul(pH[:, nh * 512:(nh + 1) * 512], lhsT=MB[:, k, :],
                                                                     rhs=wout_b[:, k, nh * 512:(nh + 1) * 512],
                                                                     start=(k == 4), stop=(k == 7)),
                         reads=[mbk, "wout_b"], writes=["pH%d" % nh])
            P.op("act", lambda e, rr=rr: e.activation(out=ysb[:, :], in_=pA[:, :], func=AF.Copy, scale=rr[:, 0:1]),
                 reads=["pA0", "pA1", rrk], writes=["ysb"])
            P.op("dve", lambda e, rr=rr: e.scalar_tensor_tensor(out=ysb[:, :], in0=pH[:, :], scalar=rr[:, 1:2], in1=ysb[:, :],
                                                                op0=ALU.mult, op1=ALU.add),
                 reads=["pH0", "pH1", rrk, "ysb"], writes=["ysb"])
            ss2, ss2k = smallcol(1)
            r2, r2k = smallcol(1)
            sumsq(ysb[:, :], ss2[:, 0:1], ["ysb"], ss2k)
            _rs(P, None, ss2, r2, 1024.0, "ry", [ss2k], r2k)
            P.op("dve", lambda e, r2=r2: e.scalar_tensor_tensor(out=tmp[:, :], in0=ysb[:, :], scalar=r2[:, 0:1], in1=G1,
                                                                op0=ALU.mult, op1=ALU.mult),
                 reads=["ysb", r2k, "mods"], writes=["tmp"])
            P.op("pool", lambda e, X=X: e.tensor_tensor(out=X[:, :], in0=tmp[:, :], in1=X[:, :], op=ALU.add),
                 reads=["tmp", xk], writes=[xk])
            P.dma("pool", x1s[t0:t0 + 128, :], X[:, :], reads=[xk], writes=["x1s%d" % (t0 // 128)])
            ss3, ss3k = smallcol(1)
            r3, r3k = smallcol(1)
            sumsq(X[:, :], ss3[:, 0:1], [xk], ss3k)
            _rs(P, None, ss3, r3, 1024.0, "r1", [ss3k], r3k)
            P.op("dve", lambda e, r3=r3, X=X: e.scalar_tensor_tensor(out=tmp[:, :], in0=X[:, :], scalar=r3[:, 0:1], in1=G2,
                                                                     op0=ALU.mult, op1=ALU.mult),
                 reads=[xk, r3k, "mods"], writes=["tmp"])
            P.op("pool", lambda e: e.tensor_tensor(out=h2b[:, :], in0=tmp[:, :], in1=SH2, op=ALU.add),
                 reads=["tmp", "mods"], writes=["h2b"])
            for k in range(8):
                P.op("pe", lambda e, k=k: e.transpose(out=pT[:, k * 128:(k + 1) * 128], in_=h2b[:, k * 128:(k + 1) * 128],
                                                      identity=ident[:, :]), reads=["h2b", "ident"], writes=["pT"])
            P.op("act", lambda e, tile=tile: e.activation(out=h2T[:, :, tile * 128:(tile + 1) * 128],
                                                          in_=pT[:, :].rearrange("p (k t) -> p k t", k=8), func=AF.Copy),
                 reads=["pT"], writes=["h2T"])
        for j in range(8):
            W1B, w1k = w1b[j % 2], "w1b%d" % (j % 2)
            W2B, w2k = w2b[j % 2], "w2b%d" % (j % 2)
            for hh in range(2):
                st, skey = stage[hh], "stage%d" % hh
                stv = st[:, :].rearrange("p (k n) -> p k n", k=8)
                P.dma("sp", stv, w1v[:, :, j * 512 + hh * 256:j * 512 + (hh + 1) * 256], writes=[skey])
                P.op("dve" if hh == 0 else "pool", lambda e, W1B=W1B, stv=stv, hh=hh: e.tensor_copy(
                    out=W1B[:, :, hh * 256:(hh + 1) * 256], in_=stv), reads=[skey], writes=[w1k])
            for hh in range(2):
                st, skey = stage[hh], "stage%d" % hh
                stv = st[:, :].rearrange("p (c n) -> p c n", c=2)
                P.dma("sp", stv, w2v[:, j * 4 + hh * 2:j * 4 + hh * 2 + 2, :], writes=[skey])
                P.op("dve" if hh == 0 else "pool", lambda e, W2B=W2B, stv=stv, hh=hh: e.tensor_copy(
                    out=W2B[:, hh * 2:hh * 2 + 2, :], in_=stv), reads=[skey], writes=[w2k])
            for tg in range(2):
                AT, atk = aT[tg], "aT%d" % tg
                for hc in range(4):
                    pm, pkey = pM[hc % 2], "pM%d" % (hc % 2)
                    RL, rlk = rl[hc % 2], "rl%d" % (hc % 2)
                    for k in range(8):
                        P.op("pe", lambda e, k=k, hc=hc, tg=tg, pm=pm, W1B=W1B: e.matmul(
                            pm[:, :], lhsT=W1B[:, k, hc * 128:(hc + 1) * 128], rhs=h2T[:, k, tg * 512:(tg + 1) * 512],
                            start=(k == 0), stop=(k == 7)), reads=[w1k, "h2T"], writes=[pkey])
                    P.op("act", lambda e, pm=pm, RL=RL: e.activation(out=RL[:, :], in_=pm[:, :], func=AF.Relu),
                         reads=[pkey], writes=[rlk])
                    P.op("pool", lambda e, RL=RL, AT=AT, hc=hc: e.tensor_tensor(out=AT[:, hc, :], in0=RL[:, :], in1=RL[:, :], op=ALU.mult),
                         reads=[rlk], writes=[atk])
                for tt in range(4):
                    tile = tg * 4 + tt
                    for nh in range(2):
                        pp, ppk = (pA, "pA%d" % nh) if tt % 2 == 0 else (pH, "pH%d" % nh)
                        for hc in range(4):
                            P.op("pe", lambda e, hc=hc, tt=tt, nh=nh, pp=pp, AT=AT, W2B=W2B: e.matmul(
                                pp[:, nh * 512:(nh + 1) * 512], lhsT=AT[:, hc, tt * 128:(tt + 1) * 128],
                                rhs=W2B[:, hc, nh * 512:(nh + 1) * 512], start=(hc == 0), stop=(hc == 3)),
                                reads=[atk, w2k], writes=[ppk])
                        fv = f2acc[:, tile, nh * 512:(nh + 1) * 512]
                        fk = "f2_%d_%d" % (tile, nh)
                        if j == 0:
                            P.op("dve", lambda e, fv=fv, pp=pp, nh=nh: e.tensor_copy(out=fv, in_=pp[:, nh * 512:(nh + 1) * 512]),
                                 reads=[ppk], writes=[fk])
                        else:
                            P.op("dve", lambda e, fv=fv, pp=pp, nh=nh: e.tensor_tensor(out=fv, in0=pp[:, nh * 512:(nh + 1) * 512], in1=fv, op=ALU.add),
                                 reads=[ppk, fk], writes=[fk])
        for tile in range(8):
            t0 = half * 1024 + tile * 128
            i = it[0]
            it[0] += 1
            X, xk = xt[i % 2], "xt%d" % (i % 2)
            P.dma("sp", X[:, :], x1s[t0:t0 + 128, :], reads=["x1s%d" % (t0 // 128)], writes=[xk], key=xk)
            fkeys = ["f2_%d_%d" % (tile, nh) for nh in range(2)]
            ss4, ss4k = smallcol(1)
            r4, r4k = smallcol(1)
            sumsq(f2acc[:, tile, :], ss4[:, 0:1], fkeys, ss4k)
            _rs(P, None, ss4, r4, 1024.0, "rf", [ss4k], r4k)
            P.op("dve", lambda e, r4=r4, tile=tile: e.scalar_tensor_tensor(out=tmp[:, :], in0=f2acc[:, tile, :], scalar=r4[:, 0:1], in1=G3,
                                                                          op0=ALU.mult, op1=ALU.mult),
                 reads=fkeys + [r4k, "mods"], writes=["tmp"])
            P.op("pool", lambda e, X=X: e.tensor_tensor(out=X[:, :], in0=tmp[:, :], in1=X[:, :], op=ALU.add),
                 reads=["tmp", xk], writes=[xk])
            P.dma("pool", out[t0:t0 + 128, :], X[:, :], reads=[xk], writes=["out%d" % (t0 // 128)])
    P.build()
    return nc


def run_phase2(inp, mixed):
    f = lambda a: np.ascontiguousarray(a, dtype=np.float32)
    nc = build_phase2()
    in_maps = []
    cols = np.r_[2048:3072, 3072:6144]
    for core in range(8):
        b, j = core // 4, core % 4
        sl = slice(j * 2048, (j + 1) * 2048)
        in_maps.append({
            "x_tok": f(inp["x"][b, sl]),
            "mix_tok": f(mixed[b, sl]),
            "mixT": f(mixed[b, sl].T),
            "cvec": f(inp["c"][b].reshape(8, 128).T),
            "w_ada": f(inp["w_ada"][0][:, cols]),
            "b_ada": f(inp["b_ada"][0][cols][None, :]),
            "gvec": f(np.stack([inp["g_post_mix"][0], inp["g_pre_mlp"][0], inp["g_post_mlp"][0]])),
            "gmix": f(np.concatenate([inp["g_attn_out"][0], inp["g_hyena_out"][0]]).reshape(8, 128).T),
            "w_out": f(inp["w_out"][0]),
            "w1": f(inp["w_mlp1"][0]),
            "w2": f(inp["w_mlp2"][0]),
            "identd": np.eye(128, dtype=np.float32),
        })
    res = run_bass_kernel_spmd(nc, in_maps, core_ids=list(range(8)))
    out = np.zeros((2, 8192, 1024), np.float32)
    for core in range(8):
        b, j = core // 4, core % 4
        out[b, j * 2048:(j + 1) * 2048] = res.results[core]["out"]
    return out


ST = 512
NTOK = 8192
NST = NTOK // ST


def _p1_common(P, nc, ncols_w, ST=512, SW=256):
    D = lambda name, shape, kind="ExternalInput": nc.dram_tensor(name, list(shape), F32, kind=kind).ap()
    H = {}
    H["xT"] = D("xT", [1024, NTOK])
    cvec = D("cvec", [128, 8])
    w_ada1 = D("w_ada1", [1024, 2048])
    b_ada1 = D("b_ada1", [128, 16])
    gpre = D("gpre", [128, 8])
    w_inc = D("w_inc", [1024, ncols_w])
    sb, ps = P.sb, P.ps
    H["wb"] = wb = sb("wb", [128, 8, ncols_w], BF16)
    stage = [sb("stage%d" % i, [128, 8, SW], F32) for i in range(2)]
    H["ST"] = ST
    H["stage"] = stage
    csb = sb("csb", [128, 8], F32)
    gp = sb("gp", [128, 8], F32)
    bsb = sb("bsb", [128, 16], F32)
    H["modc"] = modc = sb("modc", [128, 16], F32)
    H["G0"] = G0 = sb("G0", [128, 8], F32)
    H["onesb"] = onesb = sb("onesb", [128, 128], BF16)
    H["xs"] = [sb("xs%d" % i, [128, 8, ST], F32) for i in range(2)]
    H["sq"] = sb("sq", [128, 8, ST], BF16)
    H["hT"] = sb("hT", [128, 8, ST], BF16)
    H["rbc"] = sb("rbc", [128, ST], F32)
    H["pSS"] = pSS = ps("pSS", [128, 512], F32)

    P.op("pool", lambda e: e.memset(onesb[:, :], 1.0), writes=["onesb"])
    P.dma("sp", csb[:, :], cvec[:, :], writes=["csb"])
    P.dma("sp", gp[:, :], gpre[:, :], writes=["gp"])
    P.dma("sp", bsb[:, :], b_ada1[:, :], writes=["bsb"])
    P.op("act", lambda e: e.activation(out=csb[:, :], in_=csb[:, :], func=AF.Silu), reads=["csb"], writes=["csb"])
    wa = w_ada1.rearrange("(k p) n -> p k n", p=128)
    for blk in range(2048 // SW):
        st, skey = stage[blk % 2], "stage%d" % (blk % 2)
        P.dma("sp", st[:, :, :], wa[:, :, blk * SW:(blk + 1) * SW], writes=[skey])
        for jj in range(SW // 128):
            j = (SW // 128) * blk + jj
            for k in range(8):
                P.op("pe", lambda e, k=k, j=j, jj=jj, st=st: e.matmul(pSS[:, j:j + 1], lhsT=st[:, k, jj * 128:(jj + 1) * 128],
                                                                      rhs=csb[:, k:k + 1], start=(k == 0), stop=(k == 7)),
                     reads=[skey, "csb"], writes=["pSS"])
    P.op("dve", lambda e: e.tensor_tensor(out=modc[:, :], in0=pSS[:, 0:16], in1=bsb[:, :], op=ALU.add),
         reads=["pSS", "bsb"], writes=["modc"])
    P.op("dve", lambda e: e.scalar_tensor_tensor(out=G0[:, :], in0=modc[:, 8:16], scalar=1.0, in1=gp[:, :], op0=ALU.add, op1=ALU.mult),
         reads=["modc", "gp"], writes=["G0"])
    wv = w_inc.rearrange("(k p) n -> p k n", p=128)
    nb = ncols_w // SW
    for blk in range(nb):
        st, skey = stage[blk % 2], "stage%d" % (blk % 2)
        P.dma("sp", st[:, :, :], wv[:, :, blk * SW:(blk + 1) * SW], writes=[skey])
        P.op("dve" if blk % 2 == 0 else "pool", lambda e, st=st, blk=blk: e.tensor_copy(out=wb[:, :, blk * SW:(blk + 1) * SW], in_=st[:, :, :]),
             reads=[skey], writes=["wb"])
    return H


def _p1_load(P, H, st):
    ST = H["ST"]
    xs, xk = H["xs"][st % 2], "xs%d" % (st % 2)
    xTv = H["xT"].rearrange("(k p) t -> p k t", p=128)
    P.dma("sp", xs[:, :, :], xTv[:, :, st * ST:(st + 1) * ST], writes=[xk])


def _p1_norm(P, H, st):
    ST = H["ST"]
    xs, xk = H["xs"][st % 2], "xs%d" % (st % 2)
    sq, hT, rbc, pSS, onesb, modc, G0 = H["sq"], H["hT"], H["rbc"], H["pSS"], H["onesb"], H["modc"], H["G0"]
    P.op("act", lambda e: e.activation(out=sq[:, :, :], in_=xs[:, :, :], func=AF.Square), reads=[xk], writes=["sq"])
    for k in range(8):
        P.op("pe", lambda e, k=k: e.matmul(pSS[:, 0:ST], lhsT=onesb[:, :], rhs=sq[:, k, :], start=(k == 0), stop=(k == 7)),
             reads=["onesb", "sq"], writes=["pSS"])
    P.op("dve", lambda e: e.tensor_scalar(out=rbc[:, :], in0=pSS[:, 0:ST], scalar1=1.0 / 1024, scalar2=EPS, op0=ALU.mult, op1=ALU.add),
         reads=["pSS"], writes=["rbc"])
    P.op("act", lambda e: e.activation(out=rbc[:, :], in_=rbc[:, :], func=AF.Sqrt), reads=["rbc"], writes=["rbc"])
    P.op("dve", lambda e: e.reciprocal(out=rbc[:, :], in_=rbc[:, :]), reads=["rbc"], writes=["rbc"])
    for k in range(8):
        P.op("dve", lambda e, k=k: e.scalar_tensor_tensor(out=xs[:, k, :], in0=xs[:, k, :], scalar=G0[:, k:k + 1], in1=rbc[:, :],
                                                          op0=ALU.mult, op1=ALU.mult), reads=[xk, "G0", "rbc"], writes=[xk])
        P.op("act", lambda e, k=k: e.activation(out=hT[:, k, :], in_=xs[:, k, :], func=AF.Identity, bias=modc[:, k:k + 1], scale=1.0),
             reads=[xk, "modc"], writes=["hT"])


def build_attn():
    nc = bass.Bass("TRN2", target_bir_lowering=False)
    P = Prog(nc)
    D = lambda name, shape, kind="ExternalInput": nc.dram_tensor(name, list(shape), F32, kind=kind).ap()
    H = _p1_common(P, nc, 768)
    ropeC = D("ropeC", [128, NTOK])
    ropeS = D("ropeS", [128, NTOK])
    maskd = D("maskd", [128, 17 * 128])
    hseld = D("hseld", [128, 64])
    attn_o = D("attn_o", [NTOK, 128], kind="ExternalOutput")
    sb, ps = P.sb, P.ps
    wb, hT = H["wb"], H["hT"]
    QT = sb("QT", [128, NTOK], BF16)
    KT = sb("KT", [128, NTOK], BF16)
    Vaug = sb("Vaug", [128, 64, 2, 65], BF16)
    Mall = sb("Mall", [128, 17 * 128], BF16)
    mst = sb("mst", [128, 17 * 128], F32)
    hself = sb("hself", [128, 64], F32)
    hsel = sb("hsel", [128, 64], BF16)
    onesrow = sb("onesrow", [64, 128], BF16)
    rc = [sb("rc%d" % i, [128, ST], F32) for i in range(2)]
    rs_ = [sb("rs%d" % i, [128, ST], F32) for i in range(2)]
    t1 = sb("t1", [128, ST], F32)
    t2 = sb("t2", [128, ST], F32)
    sqk = sb("sqk", [128, ST], BF16)
    kmx = sb("kmx", [64, 2], F32)
    qn = sb("qn", [64, 128], F32)
    negm = sb("negm", [64, 128], BF16)
    PT = [sb("PT%d" % i, [128, 512], BF16) for i in range(2)]
    ao = [sb("ao%d" % i, [128, 128], F32) for i in range(2)]
    rec = sb("rec", [128, 4], F32)
    pA = ps("pA", [128, 512], F32)
    pB = ps("pB", [128, 512], F32)
    pV = ps("pV", [128, 512], F32)
    pN = H["pSS"]
    pS = [ps("pS%d" % i, [128, 512], F32) for i in range(2)]
    pO = [ps("pO%d" % i, [128, 2, 128], F32) for i in range(2)]

    P.dma("sp", mst[:, :], maskd[:, :], writes=["mst"])
    P.op("pool", lambda e: e.tensor_copy(out=Mall[:, :], in_=mst[:, :]), reads=["mst"], writes=["Mall"])
    P.dma("sp", hself[:, :], hseld[:, :], writes=["hself"])
    P.op("pool", lambda e: e.tensor_copy(out=hsel[:, :], in_=hself[:, :]), reads=["hself"], writes=["hsel"])
    P.op("pool", lambda e: e.memset(onesrow[:, :], 1.0), writes=["onesrow"])
    P.op("pool", lambda e: e.memset(kmx[:, :], 0.0), writes=["kmx"])
    P.op("pool", lambda e: e.memset(Vaug[:, :, :, 64:65], 1.0), writes=["Vaug"])

    _p1_load(P, H, 0)
    for st in range(NST):
        if st + 1 < NST:
            _p1_load(P, H, st + 1)
        C, ck = rc[st % 2], "rc%d" % (st % 2)
        S, sk = rs_[st % 2], "rs%d" % (st % 2)
        P.dma("sp", C[:, :], ropeC[:, st * ST:(st + 1) * ST], writes=[ck])
        P.dma("sp", S[:, :], ropeS[:, st * ST:(st + 1) * ST], writes=[sk])
        _p1_norm(P, H, st)
        for which, dst in ((0, QT), (1, KT)):
            dk = "QT" if which == 0 else "KT"
            c0 = which * 256
            for k in range(8):
                P.op("pe", lambda e, k=k, c0=c0: e.matmul(pA[:, :], lhsT=wb[:, k, c0:c0 + 128], rhs=hT[:, k, :], start=(k == 0), stop=(k == 7)),
                     reads=["wb", "hT"], writes=["pA"])
            for k in range(8):
                P.op("pe", lambda e, k=k, c0=c0: e.matmul(pB[:, :], lhsT=wb[:, k, c0 + 128:c0 + 256], rhs=hT[:, k, :], start=(k == 0), stop=(k == 7)),
                     reads=["wb", "hT"], writes=["pB"])
            P.op("dve", lambda e, C=C: e.tensor_tensor(out=t1[:, :], in0=pA[:, :], in1=C[:, :], op=ALU.mult), reads=["pA", ck], writes=["t1"])
            P.op("dve", lambda e, S=S: e.tensor_tensor(out=t2[:, :], in0=pB[:, :], in1=S[:, :], op=ALU.mult), reads=["pB", sk], writes=["t2"])
            P.op("pool", lambda e, dst=dst, st=st: e.tensor_tensor(out=dst[:, st * ST:(st + 1) * ST], in0=t1[:, :], in1=t2[:, :], op=ALU.add),
                 reads=["t1", "t2"], writes=[dk])
        P.op("pool", lambda e, st=st: e.tensor_tensor(out=sqk[:, :], in0=KT[:, st * ST:(st + 1) * ST], in1=KT[:, st * ST:(st + 1) * ST], op=ALU.mult),
             reads=["KT"], writes=["sqk"])
        P.op("pe", lambda e: e.matmul(pN[0:64, :], lhsT=hsel[:, :], rhs=sqk[:, :], start=True, stop=True), reads=["hsel", "sqk"], writes=["pSS"])
        P.op("dve", lambda e: e.tensor_reduce(out=kmx[:, 1:2], in_=pN[0:64, :], axis=AX.X, op=ALU.max), reads=["pSS", "kmx"], writes=["kmx"])
        P.op("dve", lambda e: e.tensor_tensor(out=kmx[:, 0:1], in0=kmx[:, 0:1], in1=kmx[:, 1:2], op=ALU.max), reads=["kmx"], writes=["kmx"])
        for tt in range(4):
            for k in range(8):
                P.op("pe", lambda e, k=k, tt=tt: e.matmul(pV[:, tt * 128:(tt + 1) * 128], lhsT=hT[:, k, tt * 128:(tt + 1) * 128],
                                                          rhs=wb[:, k, 512:640], start=(k == 0), stop=(k == 7)),
                     reads=["wb", "hT"], writes=["pV"])
        P.op("act", lambda e, st=st: e.activation(out=Vaug[:, st * 4:(st + 1) * 4, :, 0:64],
                                                  in_=pV[:, :].rearrange("p (t h d) -> p t h d", t=4, h=2), func=AF.Copy),
             reads=["pV"], writes=["Vaug"])
    P.op("act", lambda e: e.activation(out=kmx[:, 0:1], in_=kmx[:, 0:1], func=AF.Sqrt), reads=["kmx"], writes=["kmx"])
    cnt = [0]
    for jb in range(64):
        qs = slice(jb * 128, (jb + 1) * 128)
        P.op("pool", lambda e, qs=qs: e.tensor_tensor(out=sqk[:, 0:128], in0=QT[:, qs], in1=QT[:, qs], op=ALU.mult), reads=["QT"], writes=["sqk"])
        P.op("pe", lambda e: e.matmul(pN[0:64, 0:128], lhsT=hsel[:, :], rhs=sqk[:, 0:128], start=True, stop=True), reads=["hsel", "sqk"], writes=["pSS"])
        P.op("act", lambda e: e.activation(out=qn[:, :], in_=pN[0:64, 0:128], func=AF.Sqrt), reads=["pSS"], writes=["qn"])
        P.op("dve", lambda e: e.tensor_scalar(out=negm[:, :], in0=qn[:, :], scalar1=kmx[:, 0:1], scalar2=-1.0, op0=ALU.mult, op1=ALU.mult),
             reads=["qn", "kmx"], writes=["negm"])
        AO, aok = ao[jb % 2], "ao%d" % (jb % 2)
        PO, pok = pO[jb % 2], "pO%d" % (jb % 2)
        for h in range(2):
            hs = slice(64 * h, 64 * h + 64)
            dms = [dm for dm in range(-8, 9) if 0 <= jb + dm < 64]
            groups = [dms[i:i + 4] for i in range(0, len(dms), 4)]
            nmm = 0
            for grp in groups:
                g = cnt[0]
                cnt[0] += 1
                psx, psk = pS[g % 2], "pS%d" % (g % 2)
                ptx, ptk = PT[g % 2], "PT%d" % (g % 2)
                n = len(grp)
                for i, dm in enumerate(grp):
                    kc = jb + dm
                    P.op("pe", lambda e, i=i, kc=kc, hs=hs, qs=qs, psx=psx: e.matmul(psx[:, i * 128:(i + 1) * 128], lhsT=KT[hs, kc * 128:(kc + 1) * 128],
                                                                                      rhs=QT[hs, qs], start=True, stop=False),
                         reads=["KT", "QT"], writes=[psk])
                    P.op("pe", lambda e, i=i, h=h, psx=psx: e.matmul(psx[:, i * 128:(i + 1) * 128], lhsT=onesrow[32 * h:32 * h + 1, :],
                                                                     rhs=negm[32 * h:32 * h + 1, :], start=False, stop=True),
                         reads=["onesrow", "negm"], writes=[psk])
                P.op("act", lambda e, psx=psx, ptx=ptx, n=n: e.activation(out=ptx[:, 0:n * 128], in_=psx[:, 0:n * 128], func=AF.Exp, scale=0.125),
                     reads=[psk], writes=[ptk])
                m0 = (grp[0] + 8) * 128
                P.op("dve" if g % 2 == 0 else "pool", lambda e, ptx=ptx, n=n, m0=m0: e.tensor_tensor(
                    out=ptx[:, 0:n * 128], in0=ptx[:, 0:n * 128], in1=Mall[:, m0:m0 + n * 128], op=ALU.mult),
                    reads=[ptk, "Mall"], writes=[ptk])
                for i, dm in enumerate(grp):
                    kc = jb + dm
                    P.op("pe", lambda e, i=i, kc=kc, h=h, ptx=ptx, PO=PO, first=(nmm == 0), last=(nmm == len(dms) - 1): e.matmul(
                        PO[:, h, 0:65], lhsT=ptx[:, i * 128:(i + 1) * 128], rhs=Vaug[:, kc, h, :], start=first, stop=last),
                        reads=[ptk, "Vaug"], writes=[pok])
                    nmm += 1
            P.op("dve", lambda e, h=h, PO=PO: e.reciprocal(out=rec[:, h:h + 1], in_=PO[:, h, 64:65]), reads=[pok], writes=["rec%d" % h])
            P.op("dve", lambda e, h=h, PO=PO, AO=AO: e.tensor_scalar(out=AO[:, 64 * h:64 * h + 64], in0=PO[:, h, 0:64], scalar1=rec[:, h:h + 1],
                                                                     scalar2=None, op0=ALU.mult),
                 reads=[pok, "rec%d" % h], writes=[aok])
        P.dma("pool", attn_o[qs, :], AO[:, :], reads=[aok], writes=["attn_o%d" % jb])
    P.build()
    return nc


def _rope_tables():
    half = 32
    inv = (10000.0 ** (-np.arange(half, dtype=np.float32) / half)).astype(np.float32)
    pos = np.arange(NTOK, dtype=np.float32)
    ang = (pos[:, None] * inv[None, :]).astype(np.float32)
    cos = np.cos(ang).astype(np.float32).T
    sin = np.sin(ang).astype(np.float32).T
    C = np.concatenate([cos, cos, cos, cos], axis=0)
    S = np.concatenate([-sin, sin, -sin, sin], axis=0)
    return np.ascontiguousarray(C), np.ascontiguousarray(S)


def _attn_masks():
    o = np.arange(-8 * 128 - 127, 8 * 128 + 128)
    mult = ((np.abs(o) <= 64).astype(np.float32) + ((np.abs(o) <= 256) & (o % 4 == 0)).astype(np.float32)
            + ((np.abs(o) <= 1024) & (o % 16 == 0)).astype(np.float32))
    off0 = -(8 * 128 + 127)
    M = np.zeros((128, 17, 128), np.float32)
    kl = np.arange(128)[:, None]
    ql = np.arange(128)[None, :]
    for i, dm in enumerate(range(-8, 9)):
        M[:, i, :] = mult[(128 * dm + kl - ql) - off0]
    return M.reshape(128, 17 * 128)


def _perm_cols(w):
    w = w.reshape(w.shape[0], -1, 2, 32)
    return w[:, :, ::-1, :].reshape(w.shape[0], -1)


def run_attn(inp):
    f = lambda a: np.ascontiguousarray(a, dtype=np.float32)
    nc = build_attn()
    C, S = _rope_tables()
    M = _attn_masks()
    hsel = np.zeros((128, 64), np.float32)
    hsel[0:64, 0] = 1.0
    hsel[64:128, 32] = 1.0
    w_in = inp["w_in"][0]
    in_maps = []
    for core in range(8):
        b, g = core // 4, core % 4
        cs = slice(128 * g, 128 * g + 128)
        wq, wk, wv = w_in[:, 0:512][:, cs], w_in[:, 512:1024][:, cs], w_in[:, 1024:1536][:, cs]
        w_inc = np.concatenate([wq, _perm_cols(wq), wk, _perm_cols(wk), wv, np.zeros((1024, 128), np.float32)], axis=1)
        in_maps.append({
            "xT": f(inp["x"][b].T), "cvec": f(inp["c"][b].reshape(8, 128).T),
            "w_ada1": f(inp["w_ada"][0][:, 0:2048]), "b_ada1": f(inp["b_ada"][0][0:2048].reshape(16, 128).T),
            "gpre": f(inp["g_pre_mix"][0].reshape(8, 128).T), "w_inc": f(w_inc),
            "ropeC": C, "ropeS": S, "maskd": f(M), "hseld": hsel,
        })
    res = run_bass_kernel_spmd(nc, in_maps, core_ids=list(range(8)))
    attn = np.zeros((2, NTOK, 512), np.float32)
    for core in range(8):
        b, g = core // 4, core % 4
        attn[b, :, 128 * g:128 * g + 128] = res.results[core]["attn_o"]
    return attn


NFFT = 16384
NPQ = 4
NPQH = 8


def _hy_consts():
    N = NFFT
    a = np.arange(128)[:, None].astype(np.float64)
    k1 = np.arange(256)[None, :].astype(np.float64)
    F1cat = np.concatenate([np.cos(2 * np.pi * a * k1 / 256), -np.sin(2 * np.pi * a * k1 / 256)], 1)
    th = 2 * np.pi / N
    pm = (np.arange(128) % 64)[:, None].astype(np.float64)
    Tr, Ti = np.cos(th * pm * k1), -np.sin(th * pm * k1)
    TrTr, TiTi = np.concatenate([Tr, Tr], 1), np.concatenate([Ti, Ti], 1)
    pp = np.arange(64)[:, None].astype(np.float64)
    k2 = np.arange(64)[None, :].astype(np.float64)
    g2r, g2i = np.cos(2 * np.pi * pp * k2 / 64), -np.sin(2 * np.pi * pp * k2 / 64)
    Z0 = np.zeros((64, 64))
    G2r = np.block([[g2r, Z0], [Z0, g2r]])
    G2i = np.block([[g2i, Z0], [Z0, g2i]])
    R12 = np.concatenate([G2r, -G2i, G2i, G2r], 1)
    k1c = np.arange(256)[:, None].astype(np.float64)
    pcol = (np.arange(128) % 64)[None, :].astype(np.float64)
    T2r, T2i = np.cos(th * pcol * k1c), np.sin(th * pcol * k1c)
    TT2r = np.concatenate([T2r[0:128], T2r[0:128], T2r[128:256], T2r[128:256]], 1)
    TT2i = np.concatenate([T2i[0:128], T2i[0:128], T2i[128:256], T2i[128:256]], 1)
    aa = np.arange(128)[None, :].astype(np.float64)
    IF1c, IF1s = np.cos(2 * np.pi * aa * k1c / 256), -np.sin(2 * np.pi * aa * k1c / 256)
    IF1 = np.concatenate([IF1c[0:128], IF1s[0:128], IF1c[128:256], IF1s[128:256]], 1)
    t_lin = np.linspace(0.0, 1.0, 8192, dtype=np.float32)
    Swin = -t_lin.reshape(128, 64)
    L = 8192
    t = np.linspace(0.0, 1.0, L, dtype=np.float32)[:, None]
    w = (np.float32(2.0 * math.pi) * np.arange(L, dtype=np.float32)[:, None] / np.float32(L)).astype(np.float32)
    f = np.linspace(1e-4, 15, 16, dtype=np.float32)[None, :]
    zemb = np.concatenate([t, np.cos(f * w), -np.sin(f * w)], axis=-1).astype(np.float32)
    max_decay = math.log(1e-2) / 0.3
    min_decay = math.log(1e-2) / 1.5
    deltas = np.abs(np.linspace(min_decay, max_decay, 512, dtype=np.float32)).astype(np.float32)
    f32 = lambda x: np.ascontiguousarray(x, dtype=np.float32)
    return dict(F1cat=f32(F1cat), TrTr=f32(TrTr), TiTi=f32(TiTi), R12=f32(R12), TT2r=f32(TT2r), TT2i=f32(TT2i),
                IF1=f32(IF1), Swin=f32(Swin), zembT=f32(zemb.T), deltas=deltas)


def _hy_consts_h():
    N = NFFT
    a = np.arange(128)[:, None].astype(np.float64)
    k1 = np.arange(128)[None, :].astype(np.float64) + 0.5
    F1cat = np.concatenate([np.cos(2 * np.pi * a * k1 / 256), -np.sin(2 * np.pi * a * k1 / 256)], 1)
    th = 2 * np.pi / N
    pm = (np.arange(128) % 64)[:, None].astype(np.float64)
    Tr, Ti = np.cos(th * pm * k1), -np.sin(th * pm * k1)
    TrTr, TiTi = np.concatenate([Tr] * 4, 1), np.concatenate([Ti] * 4, 1)
    k1c = (np.arange(128)[:, None].astype(np.float64) + 0.5)
    pcol = (np.arange(128) % 64)[None, :].astype(np.float64)
    T2r, T2i = np.cos(th * pcol * k1c), np.sin(th * pcol * k1c)
    TT2r, TT2i = np.concatenate([T2r] * 4, 1), np.concatenate([T2i] * 4, 1)
    aa = np.arange(128)[None, :].astype(np.float64)
    IF1 = np.concatenate([np.cos(2 * np.pi * aa * k1c / 256), -np.sin(2 * np.pi * aa * k1c / 256)], 1)
    f32 = lambda x: np.ascontiguousarray(x, dtype=np.float32)
    return dict(F1cat_h=f32(F1cat), TrTr_h=f32(TrTr), TiTi_h=f32(TiTi), TT2r_h=f32(TT2r), TT2i_h=f32(TT2i), IF1_h=f32(IF1))


def build_hyena(stop=99):
    nc = bass.Bass("TRN2", target_bir_lowering=False)
    P = Prog(nc)
    D = lambda name, shape, kind="ExternalInput": nc.dram_tensor(name, list(shape), F32, kind=kind).ap()
    ST = 256
    H = _p1_common(P, nc, 512, ST=ST, SW=128)
    sb, ps = P.sb, P.ps
    wb, hT, pSS = H["wb"], H["hT"], H["pSS"]
    dF1cat, dTrTr, dTiTi, dR12 = D("F1cat", [128, 512]), D("TrTr", [128, 512]), D("TiTi", [128, 512]), D("R12", [128, 512])
    dTT2r, dTT2i, dIF1, dSwin = D("TT2r", [128, 512]), D("TT2i", [128, 512]), D("IF1", [128, 512]), D("Swin", [128, 64])
    dzemb = D("zembT", [33, NTOK])
    ddelta = D("delta", [128, 128])
    dhbcol = D("hbcol", [128, 128])
    dfw1, dfw2, dfw3, dfpar = D("fw1", [33, 64]), D("fw2", [64, 64]), D("fw3c", [64, 512]), D("fpar", [64, 4])
    dcw, dcb = D("cw", [128, 9]), D("cb", [128, 3])
    hy_oz = D("hy_oz", [128, 128, 64], kind="ExternalOutput")

    U = [sb("U%d" % i, [128, NTOK + 2], BF16) for i in range(3)]
    CV = sb("CV", [128, NTOK], BF16)
    Ob = sb("Ob", [128, NTOK], BF16)
    h2T = sb("h2T", [64, NTOK], BF16)
    Ap = sb("Ap", [128, 2, NPQ, 256], BF16)
    Hs = sb("Hs", [128, 2, NPQ, 256], BF16)
    Yb = sb("Yb", [128, 2, NPQ, 256], BF16)
    Bp = sb("Bp", [128, 2, 2, NPQ, 128], BF16)
    W1 = [sb("W1_%d" % i, [128, 512], F32) for i in range(2)]
    W2 = [sb("W2_%d" % i, [128, 512], F32) for i in range(2)]
    cpy = [sb("cpy%d" % i, [128, 512], F32) for i in range(2)]
    cpy2 = sb("cpy2", [128, 512], F32)
    tmpc = sb("tmpc", [128, 1024], F32)
    arg = tmpc[0:64, 0:512]
    argi = sb("argi", [64, 512], mybir.dt.int32)
    h1 = tmpc[0:64, 512:1024]
    F1b = sb("F1b", [128, 512], BF16)
    R12b = sb("R12b", [128, 512], BF16)
    IF1b = sb("IF1b", [128, 512], BF16)
    TrTr, TiTi = sb("TrTr", [128, 512], F32), sb("TiTi", [128, 512], F32)
    TT2r, TT2i = sb("TT2r", [128, 512], F32), sb("TT2i", [128, 512], F32)
    Swin = sb("Swin", [128, 64], F32)
    delta = sb("delta", [128, 128], F32)
    hbcol = sb("hbcol", [128, 128], F32)
    fw1, fw2 = sb("fw1", [33, 64], F32), sb("fw2", [64, 64], F32)
    fw3b = sb("fw3b", [64, 512], BF16)
    fpar = sb("fpar", [64, 8], F32)
    cw, cb = sb("cw", [128, 9], F32), sb("cb", [128, 3], F32)
    zc = [H["stage"][i][0:33, 0:4, :].rearrange("p k n -> p (k n)") for i in range(2)]
    win = sb("win", [128, 128], F32)
    fwbw = sb("fwbw", [128, 2, 128], F32)
    ot = [H["xs"][i][:, 0:2, :].rearrange("p k n -> p (k n)") for i in range(2)]

    pA1 = [ps("pA1_%d" % i, [128, 512], F32) for i in range(2)]
    pXr, pXi = ps("pXr", [128, 512], F32), ps("pXi", [128, 512], F32)
    pB = ps("pB", [128, 512], F32)
    pY = ps("pY", [128, 512], F32)
    pTz = ps("pTz", [128, 8, 128], BF16)
    ident = sb("ident", [128, 128], BF16)
    identf = W1[0]
    didn = D("identd", [128, 128])

    def ldcast(dst, dkey, src, n=512, parts=128):
        P.dma("sp", cpy2[0:parts, 0:n], src, writes=["cpy2"])
        P.op("dve", lambda e: e.tensor_copy(out=dst, in_=cpy2[0:parts, 0:n]), reads=["cpy2"], writes=[dkey])
    ldcast(F1b[:, :], "F1b", dF1cat[:, :])
    ldcast(R12b[:, :], "R12b", dR12[:, :])
    ldcast(IF1b[:, :], "IF1b", dIF1[:, :])
    ldcast(ident[:, :], "ident", didn[:, :], n=128)
    ldcast(fw3b[:, :], "fw3b", dfw3[:, :], parts=64)
    for dst, key, src in ((TrTr, "TrTr", dTrTr), (TiTi, "TiTi", dTiTi), (TT2r, "TT2r", dTT2r), (TT2i, "TT2i", dTT2i), (Swin, "Swin", dSwin),
                          (delta, "delta", ddelta), (hbcol, "hbcol", dhbcol), (fw1, "fw1", dfw1), (fw2, "fw2", dfw2), (cw, "cw", dcw), (cb, "cb", dcb)):
        P.dma("sp", dst[:, :], src[:, :], writes=[key])
    P.dma("sp", fpar[:, 0:4], dfpar[:, :], writes=["fpar"])
    i2p = 1.0 / (2.0 * math.pi)
    for (bc, fc, o0) in ((0, 1, 4), (2, 3, 6)):
        P.op("dve", lambda e, bc=bc, fc=fc, o0=o0: e.tensor_tensor(out=fpar[:, o0 + 1:o0 + 2], in0=fpar[:, bc:bc + 1], in1=fpar[:, fc:fc + 1], op=ALU.mult),
             reads=["fpar"], writes=["fpar"])
        P.op("dve", lambda e, o0=o0: e.tensor_scalar(out=fpar[:, o0 + 1:o0 + 2], in0=fpar[:, o0 + 1:o0 + 2], scalar1=i2p, scalar2=16.0, op0=ALU.mult, op1=ALU.add),
             reads=["fpar"], writes=["fpar"])
        P.op("dve", lambda e, fc=fc, o0=o0: e.tensor_scalar(out=fpar[:, o0:o0 + 1], in0=fpar[:, fc:fc + 1], scalar1=i2p, scalar2=None, op0=ALU.mult),
             reads=["fpar"], writes=["fpar"])
    for i in range(3):
        P.op("pool", lambda e, i=i: e.memset(U[i][:, 0:1], 0.0), writes=["U%d" % i])
        P.op("pool", lambda e, i=i: e.memset(U[i][:, NTOK + 1:NTOK + 2], 0.0), writes=["U%d" % i])

    nst = NTOK // ST
    _p1_load(P, H, 0)
    pU = [pXr, pXi, pY]
    for st in range(nst):
        if st + 1 < nst:
            _p1_load(P, H, st + 1)
        _p1_norm(P, H, st)
        for i in range(3):
            pk = ["pXr", "pXi", "pY"][i]
            for k in range(8):
                P.op("pe", lambda e, k=k, i=i: e.matmul(pU[i][:, 0:ST], lhsT=wb[:, k, i * 128:(i + 1) * 128], rhs=hT[:, k, :],
                                                        start=(k == 0), stop=(k == 7)), reads=["wb", "hT"], writes=[pk])
            P.op("act", lambda e, i=i, st=st: e.activation(out=U[i][:, 1 + st * ST:1 + (st + 1) * ST], in_=pU[i][:, 0:ST], func=AF.Copy),
                 reads=[pk], writes=["U%d" % i])

    if stop <= 1:
        P.dma("pool", hy_oz[:, 0:4, :], U[0][:, 1:257].rearrange("a (c p) -> a c p", p=64).bitcast(F32) if False else H["xs"][0][:, 0, :].rearrange("a (c p) -> a c p", p=64), reads=["U0", "U1", "U2", "xs0"], writes=["dbg"])
        P.build()
        return nc
    Zkeys = lambda i: ["Z%d_%d" % (i, s) for s in range(128 // (2 * NPQ))]
    Z = [U[i][:, 0:NTOK].rearrange("a (c p) -> a c p", p=64) for i in range(3)]
    CVs = CV[:, :].rearrange("c (a p) -> c a p", p=64)
    for i in range(3):
        for ch in range(8):
            j0 = ch * 1024
            P.op("dve", lambda e, i=i, j0=j0: e.tensor_scalar(out=tmpc[:, :], in0=U[i][:, j0:j0 + 1024], scalar1=cw[:, 3 * i:3 * i + 1],
                                                              scalar2=cb[:, i:i + 1], op0=ALU.mult, op1=ALU.add),
                 reads=["U%d" % i, "cw", "cb"], writes=["tmpc"])
            P.op("dve", lambda e, i=i, j0=j0: e.scalar_tensor_tensor(out=tmpc[:, :], in0=U[i][:, j0 + 1:j0 + 1025], scalar=cw[:, 3 * i + 1:3 * i + 2],
                                                                     in1=tmpc[:, :], op0=ALU.mult, op1=ALU.add),
                 reads=["U%d" % i, "cw", "tmpc"], writes=["tmpc"])
            P.op("dve", lambda e, i=i, j0=j0: e.scalar_tensor_tensor(out=CV[:, j0:j0 + 1024], in0=U[i][:, j0 + 2:j0 + 1026], scalar=cw[:, 3 * i + 2:3 * i + 3],
                                                                     in1=tmpc[:, :], op0=ALU.mult, op1=ALU.add),
                 reads=["U%d" % i, "cw", "tmpc"], writes=["CV"])
        for pg in range(8):
            for pi in range(8):
                p = pg * 8 + pi
                P.op("pe", lambda e, p=p, pi=pi: e.transpose(out=pTz[:, pi, :], in_=CVs[:, :, p], identity=ident[:, :]),
                     reads=["CV", "ident"], writes=["pTz"])
            P.op("act", lambda e, i=i, pg=pg: e.activation(out=Z[i][:, :, pg * 8:(pg + 1) * 8].rearrange("a c p -> a p c"), in_=pTz[:, :, :], func=AF.Copy),
                 reads=["pTz"], writes=["U%d" % i] + Zkeys(i))

    if stop <= 2:
        P.dma("pool", hy_oz[:, 0:4, :], H["xs"][0][:, 0, :].rearrange("a (c p) -> a c p", p=64), reads=["U0", "U1", "U2", "xs0"], writes=["dbg"])
        P.build()
        return nc
    for ch in range(16):
        zt, zk = zc[ch % 2], "stage%d" % (ch % 2)
        P.dma("sp", zt[:, :], dzemb[:, ch * 512:(ch + 1) * 512], writes=[zk])
        for layer in range(2):
            if layer == 0:
                P.op("pe", lambda e, zt=zt: e.matmul(pSS[0:64, :], lhsT=fw1[:, :], rhs=zt[:, :], start=True, stop=True), reads=["fw1", zk], writes=["pSS"])
            else:
                P.op("pe", lambda e: e.matmul(pSS[0:64, :], lhsT=fw2[:, :], rhs=h1[:, :], start=True, stop=True), reads=["fw2", "tmpc"], writes=["pSS"])
            fr, fb = (4, 5) if layer == 0 else (6, 7)
            P.op("dve", lambda e, fr=fr, fb=fb: e.tensor_scalar(out=arg[:, :], in0=pSS[0:64, :], scalar1=fpar[:, fr:fr + 1], scalar2=fpar[:, fb:fb + 1],
                                                                op0=ALU.mult, op1=ALU.add), reads=["pSS", "fpar"], writes=["tmpc"])
            P.op("dve", lambda e: e.tensor_copy(out=argi[:, :], in_=arg[:, :]), reads=["tmpc"], writes=["argi"])
            P.op("dve", lambda e: e.tensor_copy(out=h1[:, :], in_=argi[:, :]), reads=["argi", "tmpc"], writes=["tmpc"])
            P.op("dve", lambda e: e.tensor_tensor(out=arg[:, :], in0=arg[:, :], in1=h1[:, :], op=ALU.subtract), reads=["tmpc"], writes=["tmpc"])
            P.op("dve", lambda e: e.scalar_tensor_tensor(out=arg[:, :], in0=arg[:, :], scalar=0.5, in1=arg[:, :], op0=ALU.is_gt, op1=ALU.subtract),
                 reads=["tmpc"], writes=["tmpc"])
            if layer == 0:
                P.op("act", lambda e: e.activation(out=h1[:, :], in_=arg[:, :], func=AF.Sin, scale=-2.0 * math.pi), reads=["tmpc"], writes=["tmpc"])
            else:
                P.op("act", lambda e, ch=ch: e.activation(out=h2T[:, ch * 512:(ch + 1) * 512], in_=arg[:, :], func=AF.Sin, scale=-2.0 * math.pi),
                     reads=["tmpc"], writes=["h2T"])

    if stop <= 3:
        P.dma("pool", hy_oz[:, 0:4, :], H["xs"][0][:, 0, :].rearrange("a (c p) -> a c p", p=64), reads=["h2T", "xs0"], writes=["dbg"])
        P.build()
        return nc
    E3 = CV[:, :].rearrange("a (c p) -> a c p", p=64)
    O3 = Ob[:, :].rearrange("a (c p) -> a c p", p=64)
    h2s = h2T[:, :].rearrange("j (a p) -> j a p", p=64)
    cnt = [0]

    def twiddle(psrc, pkey, Tr_, Ti_, trk, tik, outr, outi, okey, view):
        g = cnt[0]
        cnt[0] += 1
        w1, w1k = W1[g % 2], "W1_%d" % (g % 2)
        w2, w2k = W2[g % 2], "W2_%d" % (g % 2)
        cp_, cpk = cpy[g % 2], "cpy%d" % (g % 2)
        P.op("act", lambda e: e.activation(out=cp_[:, :], in_=psrc[:, :], func=AF.Copy), reads=[pkey], writes=[cpk])
        P.op("dve", lambda e: e.tensor_tensor(out=w1[:, :], in0=psrc[:, :], in1=Tr_[:, :], op=ALU.mult), reads=[pkey, trk], writes=[w1k])
        P.op("pool", lambda e: e.tensor_tensor(out=w2[:, :], in0=cp_[:, :], in1=Ti_[:, :], op=ALU.mult), reads=[cpk, tik], writes=[w2k])
        w1r, w1i = view(w1)
        w2r, w2i = view(w2)
        P.op("dve", lambda e: e.tensor_tensor(out=outr, in0=w1r, in1=w2i, op=ALU.subtract), reads=[w1k, w2k], writes=[okey])
        P.op("pool", lambda e: e.tensor_tensor(out=outi, in0=w2r, in1=w1i, op=ALU.add), reads=[w1k, w2k], writes=[okey])

    v_fwd = lambda t: (t[:, 0:256], t[:, 256:512])
    v_inv = lambda t: (t[:, :].rearrange("k (c r x) -> k c r x", c=2, r=2)[:, :, 0, :], t[:, :].rearrange("k (c r x) -> k c r x", c=2, r=2)[:, :, 1, :])

    def fwd_stage1(src3, skey, c0):
        for q in range(NPQ):
            g = cnt[0]
            pa, pak = pA1[g % 2], "pA1_%d" % (g % 2)
            c = c0 + 2 * q
            P.op("pe", lambda e, c=c, pa=pa: e.matmul(pa[:, :], lhsT=src3[:, c:c + 2, :], rhs=F1b[:, :], start=True, stop=True),
                 reads=[skey, "F1b"], writes=[pak])
            twiddle(pa, pak, TrTr, TiTi, "TrTr", "TiTi", Ap[:, 0, q, :], Ap[:, 1, q, :], "Ap", v_fwd)

    G2r, G2in, G2i = R12b[:, 0:128], R12b[:, 128:256], R12b[:, 256:384]
    R1, R2 = R12b[:, 0:256], R12b[:, 256:512]

    for o in range(2):
        for p in range(64):
            P.op("pe", lambda e, p=p, o=o: e.matmul(pSS[:, 0:256], lhsT=h2s[:, :, p], rhs=fw3b[:, o * 256:(o + 1) * 256], start=True, stop=True),
                 reads=["h2T", "fw3b"], writes=["pSS"])
            P.op("act", lambda e, p=p: e.activation(out=win[:, :], in_=delta[:, :], func=AF.Exp, scale=Swin[:, p:p + 1]), reads=["delta", "Swin"], writes=["win"])
            for d in range(2):
                P.op("dve", lambda e, d=d: e.tensor_tensor(out=fwbw[:, d, :], in0=pSS[:, d * 128:(d + 1) * 128], in1=win[:, :], op=ALU.mult),
                     reads=["pSS", "win"], writes=["fwbw"])
            P.op("pool", lambda e, p=p: e.tensor_tensor(out=E3[:, :, p], in0=fwbw[:, 0, :], in1=fwbw[:, 1, :], op=ALU.add), reads=["fwbw"], writes=["CV"])
            P.op("pool", lambda e, p=p: e.tensor_tensor(out=O3[:, :, p], in0=fwbw[:, 0, :], in1=fwbw[:, 1, :], op=ALU.subtract), reads=["fwbw"], writes=["Ob"])
            if p == 0:
                P.op("pool", lambda e: e.tensor_copy(out=E3[0:1, :, 0], in_=fwbw[0:1, 0, :]), reads=["fwbw"], writes=["CV"])
                P.op("pool", lambda e: e.tensor_copy(out=O3[0:1, :, 0], in_=fwbw[0:1, 0, :]), reads=["fwbw"], writes=["Ob"])
        if stop <= 4:
            P.dma("pool", hy_oz[:, 0:4, :], H["xs"][0][:, 0, :].rearrange("a (c p) -> a c p", p=64), reads=["CV", "Ob", "xs0"], writes=["dbg"])
            P.build()
            return nc
        for sbt in range(128 // (2 * NPQ)):
            c0 = sbt * 2 * NPQ
            zk = "Z0_%d" % sbt
            for which, (src3, skey) in enumerate(((E3, "CV"), (O3, "Ob"))):
                fwd_stage1(src3, skey, c0)
                for qq in range(NPQ // 2):
                    rr = Ap[:, 0, 2 * qq:2 * qq + 2, :]
                    ri = Ap[:, 1, 2 * qq:2 * qq + 2, :]
                    if which == 0:
                        P.op("pe", lambda e, rr=rr: e.matmul(pXr[:, :], lhsT=G2r, rhs=rr, start=True, stop=False), reads=["R12b", "Ap"], writes=["pXr"])
                        P.op("pe", lambda e, ri=ri: e.matmul(pXr[:, :], lhsT=G2in, rhs=ri, start=False, stop=True), reads=["R12b", "Ap"], writes=["pXr"])
                        for j in range(2):
                            qg = sbt * NPQ + 2 * qq + j
                            P.op("act", lambda e, j=j, qq=qq, qg=qg, o=o: e.activation(out=Hs[:, 0, 2 * qq + j, :], in_=pXr[:, j * 256:(j + 1) * 256], func=AF.Identity,
                                                                                        bias=hbcol[:, o * 64 + qg:o * 64 + qg + 1], scale=1.0),
                                 reads=["pXr", "hbcol"], writes=["Hs"])
                    else:
                        P.op("pe", lambda e, rr=rr: e.matmul(pXi[:, :], lhsT=G2i, rhs=rr, start=True, stop=False), reads=["R12b", "Ap"], writes=["pXi"])
                        P.op("pe", lambda e, ri=ri: e.matmul(pXi[:, :], lhsT=G2r, rhs=ri, start=False, stop=True), reads=["R12b", "Ap"], writes=["pXi"])
                        P.op("act", lambda e, qq=qq: e.activation(out=Hs[:, 1, 2 * qq:2 * qq + 2, :], in_=pXi[:, :].rearrange("k (q x) -> k q x", q=2), func=AF.Copy),
                             reads=["pXi"], writes=["Hs"])
            if stop <= 6:
                P.dma("pool", hy_oz[:, 0:4, :], H["xs"][0][:, 0, :].rearrange("a (c p) -> a c p", p=64), reads=["CV", "Ob", "xs0", "Ap", "Hs", "Yb", "Bp", "pY"], writes=["dbg"])
                P.build()
                return nc
            fwd_stage1(Z[0], zk, c0)
            for qq in range(NPQ // 2):
                rr = Ap[:, 0, 2 * qq:2 * qq + 2, :]
                ri = Ap[:, 1, 2 * qq:2 * qq + 2, :]
                P.op("pe", lambda e, rr=rr: e.matmul(pXr[:, :], lhsT=G2r, rhs=rr, start=True, stop=False), reads=["R12b", "Ap"], writes=["pXr"])
                P.op("pe", lambda e, ri=ri: e.matmul(pXr[:, :], lhsT=G2in, rhs=ri, start=False, stop=True), reads=["R12b", "Ap"], writes=["pXr"])
                P.op("pe", lambda e, rr=rr: e.matmul(pXi[:, :], lhsT=G2i, rhs=rr, start=True, stop=False), reads=["R12b", "Ap"], writes=["pXi"])
                P.op("pe", lambda e, ri=ri: e.matmul(pXi[:, :], lhsT=G2r, rhs=ri, start=False, stop=True), reads=["R12b", "Ap"], writes=["pXi"])
                g = cnt[0]
                cnt[0] += 1
                w1, w1k = W1[g % 2], "W1_%d" % (g % 2)
                w2, w2k = W2[g % 2], "W2_%d" % (g % 2)
                cx, cxk = cpy[g % 2], "cpy%d" % (g % 2)
                hr = Hs[:, 0, 2 * qq:2 * qq + 2, :].rearrange("k q x -> k (q x)")
                hi = Hs[:, 1, 2 * qq:2 * qq + 2, :].rearrange("k q x -> k (q x)")
                yr = Yb[:, 0, 2 * qq:2 * qq + 2, :].rearrange("k q x -> k (q x)")
                yi = Yb[:, 1, 2 * qq:2 * qq + 2, :].rearrange("k q x -> k (q x)")
                P.op("act", lambda e, cx=cx: e.activation(out=cx[:, :], in_=pXi[:, :], func=AF.Copy), reads=["pXi"], writes=[cxk])
                P.op("act", lambda e: e.activation(out=cpy2[:, :], in_=pXr[:, :], func=AF.Copy), reads=["pXr"], writes=["cpy2"])
                P.op("dve", lambda e, w1=w1, hr=hr: e.tensor_tensor(out=w1[:, :], in0=pXr[:, :], in1=hr, op=ALU.mult), reads=["pXr", "Hs"], writes=[w1k])
                P.op("pool", lambda e, w2=w2, cx=cx, hi=hi: e.tensor_tensor(out=w2[:, :], in0=cx[:, :], in1=hi, op=ALU.mult), reads=[cxk, "Hs"], writes=[w2k])
                P.op("dve", lambda e, w1=w1, w2=w2, yr=yr: e.tensor_tensor(out=yr, in0=w1[:, :], in1=w2[:, :], op=ALU.subtract), reads=[w1k, w2k], writes=["Yb"])
                P.op("dve", lambda e, w1=w1, hr=hr: e.tensor_tensor(out=w1[:, :], in0=pXi[:, :], in1=hr, op=ALU.mult), reads=["pXi", "Hs", "Yb"], writes=[w1k])
                P.op("pool", lambda e, w2=w2, hi=hi: e.tensor_tensor(out=w2[:, :], in0=cpy2[:, :], in1=hi, op=ALU.mult), reads=["cpy2", "Hs", "Yb"], writes=[w2k])
                P.op("pool", lambda e, w1=w1, w2=w2, yi=yi: e.tensor_tensor(out=yi, in0=w1[:, :], in1=w2[:, :], op=ALU.add), reads=[w1k, w2k], writes=["Yb"])
            if stop <= 7:
                P.dma("pool", hy_oz[:, 0:4, :], H["xs"][0][:, 0, :].rearrange("a (c p) -> a c p", p=64), reads=["CV", "Ob", "xs0", "Ap", "Hs", "Yb", "Bp", "pY"], writes=["dbg"])
                P.build()
                return nc
            for q in range(NPQ):
                for kc in range(2):
                    P.op("pe", lambda e, q=q, kc=kc: e.matmul(pB[:, kc * 256:(kc + 1) * 256], lhsT=Yb[:, 0, q, kc * 128:(kc + 1) * 128], rhs=R1, start=True, stop=False),
                         reads=["Yb", "R12b"], writes=["pB"])
                    P.op("pe", lambda e, q=q, kc=kc: e.matmul(pB[:, kc * 256:(kc + 1) * 256], lhsT=Yb[:, 1, q, kc * 128:(kc + 1) * 128], rhs=R2, start=False, stop=True),
                         reads=["Yb", "R12b"], writes=["pB"])
                twiddle(pB, "pB", TT2r, TT2i, "TT2r", "TT2i", Bp[:, :, 0, q, :], Bp[:, :, 1, q, :], "Bp", v_inv)
            if stop <= 8:
                P.dma("pool", hy_oz[:, 0:4, :], H["xs"][0][:, 0, :].rearrange("a (c p) -> a c p", p=64), reads=["CV", "Ob", "xs0", "Ap", "Hs", "Yb", "Bp", "pY"], writes=["dbg"])
                P.build()
                return nc
            for hh in range(NPQ // 4):
                n = 0
                for kc in range(2):
                    for r in range(2):
                        rhs = Bp[:, kc, r, 4 * hh:4 * hh + 4, :]
                        lt = IF1b[:, (2 * kc + r) * 128:(2 * kc + r + 1) * 128]
                        P.op("pe", lambda e, rhs=rhs, lt=lt, n=n: e.matmul(pY[:, :], lhsT=lt, rhs=rhs, start=(n == 0), stop=(n == 3)),
                             reads=["Bp", "IF1b"], writes=["pY"])
                        n += 1
                cc = c0 + 8 * hh
                if o == 0:
                    P.op("dve", lambda e, cc=cc: e.scalar_tensor_tensor(out=Z[0][:, cc:cc + 8, :], in0=pY[:, :].rearrange("a (c p) -> a c p", p=64), scalar=1.0 / NFFT,
                                                                        in1=Z[1][:, cc:cc + 8, :], op0=ALU.mult, op1=ALU.mult),
                         reads=["pY", "U1"] + Zkeys(1), writes=[zk])
                else:
                    g = cnt[0]
                    cnt[0] += 1
                    OT, otk = ot[g % 2], "xs%d" % (g % 2)
                    P.op("dve", lambda e, cc=cc, OT=OT: e.scalar_tensor_tensor(out=OT[:, :].rearrange("a (c p) -> a c p", p=64), in0=pY[:, :].rearrange("a (c p) -> a c p", p=64),
                                                                               scalar=1.0 / NFFT, in1=Z[2][:, cc:cc + 8, :], op0=ALU.mult, op1=ALU.mult),
                         reads=["pY", "U2"] + Zkeys(2), writes=[otk])
                    P.dma("pool", hy_oz[:, cc:cc + 8, :], OT[:, :].rearrange("a (c p) -> a c p", p=64), reads=[otk], writes=["hy_%d" % cc])
    P.build()
    return nc


def run_hyena(inp, stop=99):
    f = lambda a: np.ascontiguousarray(a, dtype=np.float32)
    nc = build_hyena(stop)
    K = _hy_consts()
    w_in = inp["w_in"][0]
    in_maps = []
    for core in range(8):
        b, g = core // 4, core % 4
        cs = slice(128 * g, 128 * g + 128)
        hy_w = w_in[:, 1536:]
        w_inc = np.concatenate([hy_w[:, 0:512][:, cs], hy_w[:, 512:1024][:, cs], hy_w[:, 1024:1536][:, cs], np.zeros((1024, 128), np.float32)], axis=1)
        cwv = inp["conv_w"][0].reshape(3, 3, 512)[:, :, cs]
        cbv = inp["conv_b"][0].reshape(3, 512)[:, cs]
        w3 = inp["filt_w3"][0].reshape(64, 2, 2, 512)[:, :, :, cs].reshape(64, 512)
        hb = inp["hyena_bias"][0][:, cs]
        hbcol = np.zeros((128, 2, 64), np.float32)
        for cp in range(2):
            hbcol[64 * cp:64 * cp + 64, :, :] = hb[:, cp::2][None, :, :]
        in_maps.append({
            "xT": f(inp["x"][b].T), "cvec": f(inp["c"][b].reshape(8, 128).T),
            "w_ada1": f(inp["w_ada"][0][:, 0:2048]), "b_ada1": f(inp["b_ada"][0][0:2048].reshape(16, 128).T),
            "gpre": f(inp["g_pre_mix"][0].reshape(8, 128).T), "w_inc": f(w_inc),
            "F1cat": K["F1cat"], "TrTr": K["TrTr"], "TiTi": K["TiTi"], "R12": K["R12"], "TT2r": K["TT2r"], "TT2i": K["TT2i"],
            "IF1": K["IF1"], "Swin": K["Swin"], "zembT": K["zembT"],
            "delta": f(np.broadcast_to(K["deltas"][cs][None, :], (128, 128))),
            "hbcol": f(hbcol.reshape(128, 128)),
            "fw1": f(inp["filt_w1"][0]), "fw2": f(inp["filt_w2"][0]), "fw3c": f(w3),
            "fpar": f(np.stack([inp["filt_b1"][0], inp["filt_freq1"][0], inp["filt_b2"][0], inp["filt_freq2"][0]], axis=1)),
            "cw": f(cwv.transpose(2, 1, 0).reshape(128, 9)), "cb": f(cbv.T),
            "identd": np.eye(128, dtype=np.float32),
        })
    res = run_bass_kernel_spmd(nc, in_maps, core_ids=list(range(8)))
    hy = np.zeros((2, NTOK, 512), np.float32)
    for core in range(8):
        b, g = core // 4, core % 4
        oz = res.results[core]["hy_oz"]
        hy[b, :, 128 * g:128 * g + 128] = oz.transpose(0, 2, 1).reshape(NTOK, 128)
    return hy


NW = 4096
NOWN = 2048


class DramReg:
    def __init__(self, nc):
        self.nc = nc
        self.t = {}

    def __call__(self, name, shape, kind="ExternalInput", dt=None):
        if name not in self.t:
            self.t[name] = self.nc.dram_tensor(name, list(shape), dt or F32, kind=kind).ap()
        return self.t[name]


def _p1_common_f(P, D, ncols_w, wname, xname, ntok, ST=512, SW=256, cast_w=True):
    H = {}
    H["xT"] = D(xname, [1024, ntok])
    cvec = D("cvec", [128, 8])
    w_ada1 = D("w_ada1", [1024, 2048])
    b_ada1 = D("b_ada1", [128, 16])
    gpre = D("gpre", [128, 8])
    H["w_inc"] = D(wname[0], wname[1])
    sb, ps = P.sb, P.ps
    H["wb"] = sb("wb", [128, 8, ncols_w], BF16)
    H["stage"] = stage = [sb("stage%d" % i, [128, 8, SW], F32) for i in range(2)]
    H["ST"], H["SW"] = ST, SW
    csb = sb("csb", [128, 8], F32)
    gp = sb("gp", [128, 8], F32)
    bsb = sb("bsb", [128, 16], F32)
    H["modc"] = modc = sb("modc", [128, 16], F32)
    H["G0"] = G0 = sb("G0", [128, 8], F32)
    H["onesb"] = onesb = sb("onesb", [128, 128], BF16)
    H["xs"] = [sb("xs%d" % i, [128, 8, ST], F32) for i in range(2)]
    H["sqR"] = [sb("sq%d" % i, [128, 8, ST], BF16) for i in range(2)]
    H["hTR"] = [sb("hT%d" % i, [128, 8, ST], BF16) for i in range(2)]
    H["rbcR"] = [sb("rbc%d" % i, [128, ST], F32) for i in range(2)]
    H["sq"], H["hT"], H["rbc"] = H["sqR"][0], H["hTR"][0], H["rbcR"][0]
    H["pSS"] = pSS = ps("pSS", [128, 512], F32)
    P.op("pool", lambda e: e.memset(onesb[:, :], 1.0), writes=["onesb"])
    P.dma("sp", csb[:, :], cvec[:, :], writes=["csb"])
    P.dma("sp", gp[:, :], gpre[:, :], writes=["gp"])
    P.dma("sp", bsb[:, :], b_ada1[:, :], writes=["bsb"])
    P.op("act", lambda e: e.activation(out=csb[:, :], in_=csb[:, :], func=AF.Silu), reads=["csb"], writes=["csb"])
    wa = w_ada1.rearrange("(k p) n -> p k n", p=128)
    for blk in range(2048 // SW):
        st, skey = stage[blk % 2], "stage%d" % (blk % 2)
        P.dma("sp", st[:, :, :], wa[:, :, blk * SW:(blk + 1) * SW], writes=[skey])
        for jj in range(SW // 128):
            j = (SW // 128) * blk + jj
            for k in range(8):
                P.op("pe", lambda e, k=k, j=j, jj=jj, st=st: e.matmul(pSS[:, j:j + 1], lhsT=st[:, k, jj * 128:(jj + 1) * 128],
                                                                      rhs=csb[:, k:k + 1], start=(k == 0), stop=(k == 7)),
                     reads=[skey, "csb"], writes=["pSS"])
    P.op("dve", lambda e: e.tensor_tensor(out=modc[:, :], in0=pSS[:, 0:16], in1=bsb[:, :], op=ALU.add),
         reads=["pSS", "bsb"], writes=["modc"])
    P.op("dve", lambda e: e.scalar_tensor_tensor(out=G0[:, :], in0=modc[:, 8:16], scalar=1.0, in1=gp[:, :], op0=ALU.add, op1=ALU.mult),
         reads=["modc", "gp"], writes=["G0"])
    return H


def _p1_norm_r(P, H, st):
    ST = H["ST"]
    r = st % 2
    xs, xk = H["xs"][r], "xs%d" % r
    sq, hT, rbc = H["sqR"][r], H["hTR"][r], H["rbcR"][r]
    sqk, hk, rk = "sq%d" % r, "hT%d" % r, "rbc%d" % r
    pSS, onesb, modc, G0 = H["pSS"], H["onesb"], H["modc"], H["G0"]
    P.op("act", lambda e: e.activation(out=sq[:, :, :], in_=xs[:, :, :], func=AF.Square), reads=[xk], writes=[sqk])
    for k in range(8):
        P.op("pe", lambda e, k=k: e.matmul(pSS[:, 0:ST], lhsT=onesb[:, :], rhs=sq[:, k, :], start=(k == 0), stop=(k == 7)),
             reads=["onesb", sqk], writes=["pSS"])
    P.op("dve", lambda e: e.tensor_scalar(out=rbc[:, :], in0=pSS[:, 0:ST], scalar1=1.0 / 1024, scalar2=EPS, op0=ALU.mult, op1=ALU.add),
         reads=["pSS"], writes=[rk])
    P.op("act", lambda e: e.activation(out=rbc[:, :], in_=rbc[:, :], func=AF.Sqrt), reads=[rk], writes=[rk])
    P.op("dve", lambda e: e.reciprocal(out=rbc[:, :], in_=rbc[:, :]), reads=[rk], writes=[rk])
    for k in range(8):
        P.op("dve", lambda e, k=k: e.scalar_tensor_tensor(out=xs[:, k, :], in0=xs[:, k, :], scalar=G0[:, k:k + 1], in1=rbc[:, :],
                                                          op0=ALU.mult, op1=ALU.mult), reads=[xk, "G0", rk], writes=[xk])
        P.op("act", lambda e, k=k: e.activation(out=hT[:, k, :], in_=xs[:, k, :], func=AF.Identity, bias=modc[:, k:k + 1], scale=1.0),
             reads=[xk, "modc"], writes=[hk])
    return hT, hk


def _cast_w(P, H, wv, ncols, col0=0):
    SW, stage, wb = H["SW"], H["stage"], H["wb"]
    for blk in range(ncols // SW):
        st, skey = stage[blk % 2], "stage%d" % (blk % 2)
        P.dma("sp", st[:, :, :], wv[:, :, blk * SW:(blk + 1) * SW], writes=[skey])
        P.op("dve" if blk % 2 == 0 else "pool", lambda e, st=st, blk=blk: e.tensor_copy(
            out=wb[:, :, col0 + blk * SW:col0 + (blk + 1) * SW], in_=st[:, :, :]), reads=[skey], writes=["wb"])


def emit_attn_norm_f(nc, outer, D, hTw):
    P = Prog(nc, sem_stack=outer, prefix="A0_")
    H = _p1_common_f(P, D, 128, ("w_incA", [4, 1024, 768]), "xTw", NW)
    ST = 512
    for st in range(NW // ST):
        _p1_load(P, H, st)
        hT, hk = _p1_norm_r(P, H, st)
        P.op("pool", lambda e, st=st, hT=hT: e.tensor_copy(out=hTw[:, :, st * ST:(st + 1) * ST], in_=hT[:, :, :]), reads=[hk], writes=["hTw"])
    P.build()


def emit_attn_f(nc, outer, D, mixT, hTw):
    P = Prog(nc, sem_stack=outer, prefix="A_")
    ST = 512
    H = {"SW": 256, "w_inc": D("w_incA", [4, 1024, 768])}
    ropeC, ropeS = D("ropeCw", [128, NW]), D("ropeSw", [128, NW])
    maskd, hseld, validd, identd = D("maskd", [128, 17 * 128]), D("hseld", [128, 64]), D("validw", [128, 32]), D("identd", [128, 128])
    sb, ps = P.sb, P.ps
    H["wb"] = wb = sb("wb", [128, 8, 768], BF16)
    H["stage"] = [sb("stage%d" % i, [128, 8, 256], F32) for i in range(2)]
    H["pSS"] = ps("pSS", [128, 512], F32)
    QT, KT = sb("QT", [128, NW], BF16), sb("KT", [128, NW], BF16)
    Vaug = sb("Vaug", [128, 32, 2, 65], BF16)
    Mall = sb("Mall", [128, 17 * 128], BF16)
    mst = sb("mst", [128, 17 * 128], F32)
    hself, hsel = sb("hself", [128, 64], F32), sb("hsel", [128, 64], BF16)
    valid = sb("valid", [128, 32], F32)
    identf = sb("identf", [128, 128], F32)
    onesrow = sb("onesrow", [64, 128], BF16)
    rc = [sb("rc%d" % i, [128, ST], F32) for i in range(2)]
    rs_ = [sb("rs%d" % i, [128, ST], F32) for i in range(2)]
    t1, t2 = sb("t1", [128, ST], F32), sb("t2", [128, ST], F32)
    sqk = sb("sqk", [128, ST], BF16)
    kmx = sb("kmx", [64, 2], F32)
    qn = sb("qn", [64, 128], F32)
    negm = sb("negm", [64, 128], BF16)
    PT = [sb("PT%d" % i, [128, 512], BF16) for i in range(4)]
    ao = [sb("ao%d" % i, [128, 128], F32) for i in range(2)]
    rec = sb("rec", [128, 4], F32)
    pA, pB, pV = ps("pA", [128, 512], F32), ps("pB", [128, 512], F32), ps("pV", [128, 512], F32)
    pN = H["pSS"]
    pS = [ps("pS%d" % i, [128, 512], F32) for i in range(2)]
    pO = [ps("pO%d" % i, [128, 2, 128], F32) for i in range(2)]

    P.dma("sp", mst[:, :], maskd[:, :], writes=["mst"])
    P.op("pool", lambda e: e.tensor_copy(out=Mall[:, :], in_=mst[:, :]), reads=["mst"], writes=["Mall"])
    P.dma("sp", hself[:, :], hseld[:, :], writes=["hself"])
    P.op("pool", lambda e: e.tensor_copy(out=hsel[:, :], in_=hself[:, :]), reads=["hself"], writes=["hsel"])
    P.dma("sp", valid[:, :], validd[:, :], writes=["valid"])
    P.dma("sp", identf[:, :], identd[:, :], writes=["identf"])
    P.op("pool", lambda e: e.memset(onesrow[:, :], 1.0), writes=["onesrow"])
    for h in range(2):
        P.op("pool", lambda e, h=h: e.tensor_copy(out=Vaug[:, :, h, 64], in_=valid[:, :]), reads=["valid"], writes=["Vaug"])
    cnt = [0]
    xl = [0]
    wv_all = H["w_inc"]
    for hp in range(4):
        _cast_w(P, H, wv_all[hp].rearrange("(k p) n -> p k n", p=128), 768)
        P.op("pool", lambda e: e.memset(kmx[:, :], 0.0), reads=["kmx"], writes=["kmx"])
        nst = NW // ST
        for st in range(nst):
            hT = hTw[:, :, st * ST:(st + 1) * ST]
            C, ck = rc[st % 2], "rc%d" % (st % 2)
            S, sk = rs_[st % 2], "rs%d" % (st % 2)
            P.dma("sp", C[:, :], ropeC[:, st * ST:(st + 1) * ST], writes=[ck])
            P.dma("sp", S[:, :], ropeS[:, st * ST:(st + 1) * ST], writes=[sk])
            for which, dst in ((0, QT), (1, KT)):
                dk = "QT" if which == 0 else "KT"
                c0 = which * 256
                for k in range(8):
                    P.op("pe", lambda e, k=k, c0=c0, hT=hT: e.matmul(pA[:, :], lhsT=wb[:, k, c0:c0 + 128], rhs=hT[:, k, :], start=(k == 0), stop=(k == 7)),
                         reads=["wb", "hTw"], writes=["pA"])
                for k in range(8):
                    P.op("pe", lambda e, k=k, c0=c0, hT=hT: e.matmul(pB[:, :], lhsT=wb[:, k, c0 + 128:c0 + 256], rhs=hT[:, k, :], start=(k == 0), stop=(k == 7)),
                         reads=["wb", "hTw"], writes=["pB"])
                P.op("dve", lambda e, C=C: e.tensor_tensor(out=t1[:, :], in0=pA[:, :], in1=C[:, :], op=ALU.mult), reads=["pA", ck], writes=["t1"])
                P.op("dve", lambda e, S=S: e.tensor_tensor(out=t2[:, :], in0=pB[:, :], in1=S[:, :], op=ALU.mult), reads=["pB", sk], writes=["t2"])
                P.op("pool", lambda e, dst=dst, st=st: e.tensor_tensor(out=dst[:, st * ST:(st + 1) * ST], in0=t1[:, :], in1=t2[:, :], op=ALU.add),
                     reads=["t1", "t2"], writes=[dk])
            P.op("pool", lambda e, st=st: e.tensor_tensor(out=sqk[:, :], in0=KT[:, st * ST:(st + 1) * ST], in1=KT[:, st * ST:(st + 1) * ST], op=ALU.mult),
                 reads=["KT"], writes=["sqk"])
            P.op("pe", lambda e: e.matmul(pN[0:64, :], lhsT=hsel[:, :], rhs=sqk[:, :], start=True, stop=True), reads=["hsel", "sqk"], writes=["pSS"])
            P.op("dve", lambda e: e.tensor_reduce(out=kmx[:, 1:2], in_=pN[0:64, :], axis=AX.X, op=ALU.max), reads=["pSS", "kmx"], writes=["kmx"])
            P.op("dve", lambda e: e.tensor_tensor(out=kmx[:, 0:1], in0=kmx[:, 0:1], in1=kmx[:, 1:2], op=ALU.max), reads=["kmx"], writes=["kmx"])
            for tt in range(4):
                for k in range(8):
                    P.op("pe", lambda e, k=k, tt=tt, hT=hT: e.matmul(pV[:, tt * 128:(tt + 1) * 128], lhsT=hT[:, k, tt * 128:(tt + 1) * 128],
                                                              rhs=wb[:, k, 512:640], start=(k == 0), stop=(k == 7)),
                         reads=["wb", "hTw"], writes=["pV"])
            for tt in range(4):
                tile = st * 4 + tt
                P.op("act", lambda e, tt=tt, tile=tile: e.activation(out=Vaug[:, tile, :, 0:64], in_=pV[:, tt * 128:(tt + 1) * 128].rearrange("p (h d) -> p h d", h=2),
                                                                     func=AF.Copy, scale=valid[:, tile:tile + 1]),
                     reads=["pV", "valid"], writes=["Vaug"])
        P.op("act", lambda e: e.activation(out=kmx[:, 0:1], in_=kmx[:, 0:1], func=AF.Sqrt), reads=["kmx"], writes=["kmx"])
        LAG = 2
        units = []
        for jb in range(8, 24):
            for h in range(2):
                dms = list(range(-8, 9))
                groups = [dms[i:i + 4] for i in range(0, len(dms), 4)]
                base = 0
                for gi, grp in enumerate(groups):
                    units.append(dict(jb=jb, h=h, grp=grp, gi=gi, base=base, last=(gi == len(groups) - 1)))
                    base += len(grp)

        def emit_scores(u):
            jb, h, grp = u["jb"], u["h"], u["grp"]
            qs = slice(jb * 128, (jb + 1) * 128)
            hs = slice(64 * h, 64 * h + 64)
            if h == 0 and u["gi"] == 0:
                P.op("pool", lambda e, qs=qs: e.tensor_tensor(out=sqk[:, 0:128], in0=QT[:, qs], in1=QT[:, qs], op=ALU.mult), reads=["QT"], writes=["sqk"])
                P.op("pe", lambda e: e.matmul(pN[0:64, 0:128], lhsT=hsel[:, :], rhs=sqk[:, 0:128], start=True, stop=True), reads=["hsel", "sqk"], writes=["pSS"])
                P.op("act", lambda e: e.activation(out=qn[:, :], in_=pN[0:64, 0:128], func=AF.Sqrt), reads=["pSS"], writes=["qn"])
                P.op("dve", lambda e: e.tensor_scalar(out=negm[:, :], in0=qn[:, :], scalar1=kmx[:, 0:1], scalar2=-1.0, op0=ALU.mult, op1=ALU.mult),
                     reads=["qn", "kmx"], writes=["negm"])
            g = cnt[0]
            cnt[0] += 1
            psx, psk = [(pS[0], "pS0"), (pS[1], "pS1"), (pA, "pA"), (pB, "pB")][g % 4]
            ptx, ptk = PT[g % 4], "PT%d" % (g % 4)
            u["ptx"], u["ptk"] = ptx, ptk
            n = len(grp)
            for i, dm in enumerate(grp):
                kc = jb + dm
                P.op("pe", lambda e, i=i, kc=kc, hs=hs, qs=qs, psx=psx: e.matmul(psx[:, i * 128:(i + 1) * 128], lhsT=KT[hs, kc * 128:(kc + 1) * 128],
                                                                                  rhs=QT[hs, qs], start=True, stop=False),
                     reads=["KT", "QT"], writes=[psk])
                P.op("pe", lambda e, i=i, h=h, psx=psx: e.matmul(psx[:, i * 128:(i + 1) * 128], lhsT=onesrow[32 * h:32 * h + 1, :],
                                                                 rhs=negm[32 * h:32 * h + 1, :], start=False, stop=True),
                     reads=["onesrow", "negm"], writes=[psk])
            P.op("act", lambda e, psx=psx, ptx=ptx, n=n: e.activation(out=ptx[:, 0:n * 128], in_=psx[:, 0:n * 128], func=AF.Exp, scale=0.125),
                 reads=[psk], writes=[ptk])
            m0 = (grp[0] + 8) * 128
            P.op("dve" if g % 2 == 0 else "pool", lambda e, ptx=ptx, n=n, m0=m0: e.tensor_tensor(
                out=ptx[:, 0:n * 128], in0=ptx[:, 0:n * 128], in1=Mall[:, m0:m0 + n * 128], op=ALU.mult),
                reads=[ptk, "Mall"], writes=[ptk])

        def emit_pv(u):
            jb, h, grp = u["jb"], u["h"], u["grp"]
            ptx, ptk = u["ptx"], u["ptk"]
            AO, aok = ao[jb % 2], "ao%d" % (jb % 2)
            PO, pok = pO[jb % 2], "pO%d" % (jb % 2)
            for i, dm in enumerate(grp):
                kc = jb + dm
                nmm = u["base"] + i
                P.op("pe", lambda e, i=i, kc=kc, h=h, ptx=ptx, PO=PO, first=(nmm == 0), last=(nmm == 16): e.matmul(
                    PO[:, h, 0:65], lhsT=ptx[:, i * 128:(i + 1) * 128], rhs=Vaug[:, kc, h, :], start=first, stop=last),
                    reads=[ptk, "Vaug"], writes=[pok])
            if u["last"]:
                P.op("dve", lambda e, h=h, PO=PO: e.reciprocal(out=rec[:, h:h + 1], in_=PO[:, h, 64:65]), reads=[pok], writes=["rec%d" % h])
                P.op("dve", lambda e, h=h, PO=PO, AO=AO: e.tensor_scalar(out=AO[:, 64 * h:64 * h + 64], in0=PO[:, h, 0:64], scalar1=rec[:, h:h + 1],
                                                                         scalar2=None, op0=ALU.mult),
                     reads=[pok, "rec%d" % h], writes=[aok])
                if h == 1:
                    P.op("pe", lambda e, AO=AO: e.transpose(out=pV[:, 0:128], in_=AO[:, :], identity=identf[:, :]), reads=[aok, "identf"], writes=["pV"])
                    P.op("act", lambda e, hp=hp, jb=jb: e.activation(out=mixT[:, hp, (jb - 8) * 128:(jb - 7) * 128], in_=pV[:, 0:128], func=AF.Copy),
                         reads=["pV"], writes=["mixT"])

        pending = []
        for u in units:
            emit_scores(u)
            pending.append(u)
            if len(pending) > LAG:
                emit_pv(pending.pop(0))
        while pending:
            emit_pv(pending.pop(0))
    P.build()


def emit_hyproj_f(nc, outer, D):
    P = Prog(nc, sem_stack=outer, prefix="B_")
    ST = 256
    H = _p1_common_f(P, D, 1536, ("w_incH", [1024, 1536]), "xT", NTOK, ST=ST, SW=128)
    _cast_w(P, H, H["w_inc"].rearrange("(k p) n -> p k n", p=128), 1536)
    Us = D("Us", [12, 128, NTOK], kind="Internal", dt=BF16)
    sb, ps = P.sb, P.ps
    wb, hT = H["wb"], H["hT"]
    ub = [sb("ub%d" % i, [128, 12, ST], BF16) for i in range(2)]
    pU = [ps("pU%d" % i, [128, 512], F32) for i in range(4)]
    nst = NTOK // ST
    _p1_load(P, H, 0)
    nxt = _p1_norm_r(P, H, 0)
    for st in range(nst):
        hT, hk = nxt
        if st + 1 < nst:
            _p1_load(P, H, st + 1)
            nxt = _p1_norm_r(P, H, st + 1)
        UB, ubk = ub[st % 2], "ub%d" % (st % 2)
        for pr in range(6):
            pu, puk = pU[pr % 4], "pU%d" % (pr % 4)
            for half in range(2):
                i = 2 * pr + half
                for k in range(8):
                    P.op("pe", lambda e, k=k, i=i, half=half, pu=pu, hT=hT: e.matmul(pu[:, half * ST:(half + 1) * ST], lhsT=wb[:, k, i * 128:(i + 1) * 128], rhs=hT[:, k, :],
                                                                                    start=(k == 0), stop=(k == 7)), reads=["wb", hk], writes=[puk])
            eng = "act" if pr % 2 == 0 else "dve"
            if eng == "act":
                P.op("act", lambda e, pr=pr, pu=pu, UB=UB: e.activation(out=UB[:, 2 * pr:2 * pr + 2, :], in_=pu[:, :].rearrange("p (i t) -> p i t", i=2), func=AF.Copy),
                     reads=[puk], writes=[ubk])
            else:
                P.op("dve", lambda e, pr=pr, pu=pu, UB=UB: e.tensor_copy(out=UB[:, 2 * pr:2 * pr + 2, :], in_=pu[:, :].rearrange("p (i t) -> p i t", i=2)),
                     reads=[puk], writes=[ubk])
        P.dma("pool", Us[:, :, st * ST:(st + 1) * ST].rearrange("i p t -> p i t"), UB[:, :, :], reads=[ubk], writes=["Us"])
    P.build()


def emit_hyena_f(nc, outer, D, mixT, groups=(0, 1, 2, 3), prefix="C_"):
    P = Prog(nc, sem_stack=outer, prefix=prefix)
    sb, ps = P.sb, P.ps
    Us = D("Us", [12, 128, NTOK], kind="Internal", dt=BF16)
    dF1cat, dTrTr, dTiTi, dR12 = D("F1cat", [128, 512]), D("TrTr", [128, 512]), D("TiTi", [128, 512]), D("R12", [128, 512])
    dTT2r, dTT2i, dIF1, dSwin = D("TT2r", [128, 512]), D("TT2i", [128, 512]), D("IF1", [128, 512]), D("Swin", [128, 64])
    dzemb = D("zembT", [33, NTOK])
    ddelta = D("delta4", [128, 512])
    dhbcol = D("hbcol4", [128, 512])
    dfw1, dfw2, dfw3, dfpar = D("fw1", [33, 64]), D("fw2", [64, 64]), D("fw3c4", [64, 2048]), D("fpar", [64, 4])
    dcw, dcb = D("cw4", [128, 36]), D("cb4", [128, 12])
    dsel = D("seld", [128, 32])
    didn = D("identd", [128, 128])

    U = [sb("U%d" % i, [128, NTOK + 2], BF16) for i in range(3)]
    CV = sb("CV", [128, NTOK], BF16)
    Ob = sb("Ob", [128, NTOK], BF16)
    h2T = sb("h2T", [64, NTOK], BF16)
    ApF = sb("ApF", [128, 2, NPQ, 256], BF16)
    ApD = sb("ApD", [128, 2, NPQ, 256], BF16)
    HsL = [sb("Hs%d" % i, [128, 2, NPQ, 256], BF16) for i in range(2)]
    Yb = sb("Yb", [128, 2, NPQ, 256], BF16)
    Bp = sb("Bp", [128, 2, 2, NPQ, 128], BF16)
    W1 = [sb("W1_%d" % i, [128, 512], F32) for i in range(2)]
    W2 = [sb("W2_%d" % i, [128, 512], F32) for i in range(2)]
    cpy = [sb("cpy%d" % i, [128, 512], F32) for i in range(2)]
    cpy2 = sb("cpy2", [128, 512], F32)
    tmpc = sb("tmpc", [128, 1024], F32)
    arg = tmpc[0:64, 0:512]
    h1 = tmpc[0:64, 512:1024]
    argi = sb("argi", [64, 512], mybir.dt.int32)
    F1b, R12b, IF1b = sb("F1b", [128, 512], BF16), sb("R12b", [128, 512], BF16), sb("IF1b", [128, 512], BF16)
    TrTr, TiTi = sb("TrTr", [128, 512], F32), sb("TiTi", [128, 512], F32)
    TT2r, TT2i = sb("TT2r", [128, 512], F32), sb("TT2i", [128, 512], F32)
    Swin = sb("Swin", [128, 64], F32)
    delta = sb("delta", [128, 512], F32)
    hbcol = sb("hbcol", [128, 512], F32)
    fw1, fw2 = sb("fw1", [33, 64], F32), sb("fw2", [64, 64], F32)
    fw3b = sb("fw3b", [64, 2048], BF16)
    fpar = sb("fpar", [64, 8], F32)
    cw, cb = sb("cw", [128, 36], F32), sb("cb", [128, 12], F32)
    selb = sb("selb", [128, 32], BF16)
    ident = sb("ident", [128, 128], BF16)
    zc = [sb("zc%d" % i, [33, 512], F32) for i in range(2)]
    win = [sb("win%d" % i, [128, 128], F32) for i in range(2)]
    eoc = [sb("eoc%d" % i, [128, 256], F32) for i in range(2)]
    fw3f = sb("fw3f", [64, 1024], BF16)

    pSS = ps("pSS", [128, 512], F32)
    pA1 = [ps("pA1_%d" % i, [128, 512], F32) for i in range(2)]
    pXr, pXi = ps("pXr", [128, 512], F32), ps("pXi", [128, 512], F32)
    pB = ps("pB", [128, 512], F32)
    pY = ps("pY", [128, 512], F32)
    pTz = ps("pTz", [128, 8, 128], BF16)

    def ldcast(dst, dkey, src, n=512, parts=128):
        P.dma("sp", cpy2[0:parts, 0:n], src, writes=["cpy2"])
        P.op("dve", lambda e: e.tensor_copy(out=dst, in_=cpy2[0:parts, 0:n]), reads=["cpy2"], writes=[dkey])
    ldcast(F1b[:, :], "F1b", dF1cat[:, :])
    ldcast(R12b[:, :], "R12b", dR12[:, :])
    ldcast(IF1b[:, :], "IF1b", dIF1[:, :])
    ldcast(ident[:, :], "ident", didn[:, :], n=128)
    ldcast(selb[:, :], "selb", dsel[:, :], n=32)
    for q4 in range(4):
        P.dma("sp", cpy2[0:64, :], dfw3[:, q4 * 512:(q4 + 1) * 512], writes=["cpy2"])
        for o_ in range(2):
            wf = cpy2[0:64, o_ * 256:o_ * 256 + 128]
            wbk = cpy2[0:64, o_ * 256 + 128:o_ * 256 + 256]
            c0_ = q4 * 512 + o_ * 256
            P.op("dve", lambda e, wf=wf, wbk=wbk, c0_=c0_: e.tensor_tensor(out=fw3b[:, c0_:c0_ + 128], in0=wf, in1=wbk, op=ALU.add), reads=["cpy2"], writes=["fw3b"])
            P.op("dve", lambda e, wf=wf, wbk=wbk, c0_=c0_: e.tensor_tensor(out=fw3b[:, c0_ + 128:c0_ + 256], in0=wf, in1=wbk, op=ALU.subtract), reads=["cpy2"], writes=["fw3b"])
            P.op("dve", lambda e, wf=wf, q4=q4, o_=o_: e.tensor_copy(out=fw3f[:, (2 * q4 + o_) * 128:(2 * q4 + o_ + 1) * 128], in_=wf), reads=["cpy2"], writes=["fw3f"])
    for dst, key, src in ((TrTr, "TrTr", dTrTr), (TiTi, "TiTi", dTiTi), (TT2r, "TT2r", dTT2r), (TT2i, "TT2i", dTT2i), (Swin, "Swin", dSwin),
                          (delta, "delta", ddelta), (hbcol, "hbcol", dhbcol), (fw1, "fw1", dfw1), (fw2, "fw2", dfw2), (cw, "cw", dcw), (cb, "cb", dcb)):
        P.dma("sp", dst[:, :], src[:, :], writes=[key])
    P.dma("sp", fpar[:, 0:4], dfpar[:, :], writes=["fpar"])
    i2p = 1.0 / (2.0 * math.pi)
    for (bc, fc, o0) in ((0, 1, 4), (2, 3, 6)):
        P.op("dve", lambda e, bc=bc, fc=fc, o0=o0: e.tensor_tensor(out=fpar[:, o0 + 1:o0 + 2], in0=fpar[:, bc:bc + 1], in1=fpar[:, fc:fc + 1], op=ALU.mult),
             reads=["fpar"], writes=["fpar"])
        P.op("dve", lambda e, o0=o0: e.tensor_scalar(out=fpar[:, o0 + 1:o0 + 2], in0=fpar[:, o0 + 1:o0 + 2], scalar1=i2p, scalar2=16.0, op0=ALU.mult, op1=ALU.add),
             reads=["fpar"], writes=["fpar"])
        P.op("dve", lambda e, fc=fc, o0=o0: e.tensor_scalar(out=fpar[:, o0:o0 + 1], in0=fpar[:, fc:fc + 1], scalar1=i2p, scalar2=None, op0=ALU.mult),
             reads=["fpar"], writes=["fpar"])

    for ch in range(16):
        zt, zk = zc[ch % 2], "zc%d" % (ch % 2)
        P.dma("sp", zt[:, :], dzemb[:, ch * 512:(ch + 1) * 512], writes=[zk])
        for layer in range(2):
            if layer == 0:
                P.op("pe", lambda e, zt=zt: e.matmul(pSS[0:64, :], lhsT=fw1[:, :], rhs=zt[:, :], start=True, stop=True), reads=["fw1", zk], writes=["pSS"])
            else:
                P.op("pe", lambda e: e.matmul(pSS[0:64, :], lhsT=fw2[:, :], rhs=h1[:, :], start=True, stop=True), reads=["fw2", "tmpc"], writes=["pSS"])
            fr, fb = (4, 5) if layer == 0 else (6, 7)
            P.op("dve", lambda e, fr=fr, fb=fb: e.tensor_scalar(out=arg[:, :], in0=pSS[0:64, :], scalar1=fpar[:, fr:fr + 1], scalar2=fpar[:, fb:fb + 1],
                                                                op0=ALU.mult, op1=ALU.add), reads=["pSS", "fpar"], writes=["tmpc"])
            P.op("dve", lambda e: e.tensor_copy(out=argi[:, :], in_=arg[:, :]), reads=["tmpc"], writes=["argi"])
            P.op("dve", lambda e: e.tensor_copy(out=h1[:, :], in_=argi[:, :]), reads=["argi", "tmpc"], writes=["tmpc"])
            P.op("dve", lambda e: e.tensor_tensor(out=arg[:, :], in0=arg[:, :], in1=h1[:, :], op=ALU.subtract), reads=["tmpc"], writes=["tmpc"])
            P.op("dve", lambda e: e.scalar_tensor_tensor(out=arg[:, :], in0=arg[:, :], scalar=0.5, in1=arg[:, :], op0=ALU.is_gt, op1=ALU.subtract),
                 reads=["tmpc"], writes=["tmpc"])
            if layer == 0:
                P.op("act", lambda e: e.activation(out=h1[:, :], in_=arg[:, :], func=AF.Sin, scale=-2.0 * math.pi), reads=["tmpc"], writes=["tmpc"])
            else:
                P.op("act", lambda e, ch=ch: e.activation(out=h2T[:, ch * 512:(ch + 1) * 512], in_=arg[:, :], func=AF.Sin, scale=-2.0 * math.pi),
                     reads=["tmpc"], writes=["h2T"])

    Zkeys = lambda i: ["Z%d_%d" % (i, s) for s in range(128 // (2 * NPQ))]
    Z = [U[i][:, 0:NTOK].rearrange("a (c p) -> a c p", p=64) for i in range(3)]
    CVs = CV[:, :].rearrange("c (a p) -> c a p", p=64)
    E3 = CV[:, :].rearrange("a (c p) -> a c p", p=64)
    O3 = Ob[:, :].rearrange("a (c p) -> a c p", p=64)
    h2s = h2T[:, :].rearrange("j (a p) -> j a p", p=64)
    cnt = [0]

    def twiddle(psrc, pkey, Tr_, Ti_, trk, tik, outr, outi, okey, view):
        g = cnt[0]
        cnt[0] += 1
        w1, w1k = W1[g % 2], "W1_%d" % (g % 2)
        w2, w2k = W2[g % 2], "W2_%d" % (g % 2)
        cp_, cpk = cpy[g % 2], "cpy%d" % (g % 2)
        P.op("act", lambda e: e.activation(out=cp_[:, :], in_=psrc[:, :], func=AF.Copy), reads=[pkey], writes=[cpk])
        P.op("dve", lambda e: e.tensor_tensor(out=w1[:, :], in0=psrc[:, :], in1=Tr_[:, :], op=ALU.mult), reads=[pkey, trk], writes=[w1k])
        P.op("pool", lambda e: e.tensor_tensor(out=w2[:, :], in0=cp_[:, :], in1=Ti_[:, :], op=ALU.mult), reads=[cpk, tik], writes=[w2k])
        w1r, w1i = view(w1)
        w2r, w2i = view(w2)
        P.op("dve", lambda e: e.tensor_tensor(out=outr, in0=w1r, in1=w2i, op=ALU.subtract), reads=[w1k, w2k], writes=[okey])
        P.op("dve", lambda e: e.tensor_tensor(out=outi, in0=w2r, in1=w1i, op=ALU.add), reads=[w1k, w2k], writes=[okey])

    v_fwd = lambda t: (t[:, 0:256], t[:, 256:512])
    v_inv = lambda t: (t[:, :].rearrange("k (c r x) -> k c r x", c=2, r=2)[:, :, 0, :], t[:, :].rearrange("k (c r x) -> k c r x", c=2, r=2)[:, :, 1, :])

    def fwd_stage1(src3, skey, c0, Ap, apk):
        for q in range(NPQ):
            g = cnt[0]
            pa, pak = pA1[g % 2], "pA1_%d" % (g % 2)
            c = c0 + 2 * q
            P.op("pe", lambda e, c=c, pa=pa: e.matmul(pa[:, :], lhsT=src3[:, c:c + 2, :], rhs=F1b[:, :], start=True, stop=True),
                 reads=[skey, "F1b"], writes=[pak])
            twiddle(pa, pak, TrTr, TiTi, "TrTr", "TiTi", Ap[:, 0, q, :], Ap[:, 1, q, :], apk, v_fwd)

    G2r, G2in, G2i = R12b[:, 0:128], R12b[:, 128:256], R12b[:, 256:384]
    R1, R2 = R12b[:, 0:256], R12b[:, 256:512]

    for grp4 in groups:
        for i in range(3):
            P.dma("sp", U[i][:, 1:NTOK + 1], Us[3 * grp4 + i], reads=["Us"], writes=["U%d" % i] + Zkeys(i), key="U%d" % i)
            P.op("pool", lambda e, i=i: e.memset(U[i][:, 0:1], 0.0), reads=["U%d" % i], writes=["U%d" % i] + Zkeys(i))
            P.op("pool", lambda e, i=i: e.memset(U[i][:, NTOK + 1:NTOK + 2], 0.0), reads=["U%d" % i], writes=["U%d" % i])
        for i in range(3):
            ci = 9 * grp4 + 3 * i
            bi = 3 * grp4 + i
            for ch in range(8):
                j0 = ch * 1024
                P.op("dve", lambda e, i=i, j0=j0, ci=ci, bi=bi: e.tensor_scalar(out=tmpc[:, :], in0=U[i][:, j0:j0 + 1024], scalar1=cw[:, ci:ci + 1],
                                                                                scalar2=cb[:, bi:bi + 1], op0=ALU.mult, op1=ALU.add),
                     reads=["U%d" % i, "cw", "cb"], writes=["tmpc"])
                P.op("dve", lambda e, i=i, j0=j0, ci=ci: e.scalar_tensor_tensor(out=tmpc[:, :], in0=U[i][:, j0 + 1:j0 + 1025], scalar=cw[:, ci + 1:ci + 2],
                                                                                in1=tmpc[:, :], op0=ALU.mult, op1=ALU.add),
                     reads=["U%d" % i, "cw", "tmpc"], writes=["tmpc"])
                P.op("dve", lambda e, i=i, j0=j0, ci=ci: e.scalar_tensor_tensor(out=CV[:, j0:j0 + 1024], in0=U[i][:, j0 + 2:j0 + 1026], scalar=cw[:, ci + 2:ci + 3],
                                                                                in1=tmpc[:, :], op0=ALU.mult, op1=ALU.add),
                     reads=["U%d" % i, "cw", "tmpc"], writes=["CV"])
            for pg in range(8):
                for pi in range(8):
                    p = pg * 8 + pi
                    P.op("pe", lambda e, p=p, pi=pi: e.transpose(out=pTz[:, pi, :], in_=CVs[:, :, p], identity=ident[:, :]),
                         reads=["CV", "ident"], writes=["pTz"])
                P.op("act", lambda e, i=i, pg=pg: e.activation(out=Z[i][:, :, pg * 8:(pg + 1) * 8].rearrange("a c p -> a p c"), in_=pTz[:, :, :], func=AF.Copy),
                     reads=["pTz"], writes=["U%d" % i] + Zkeys(i))
        for o in range(2):
            w3c0 = grp4 * 512 + o * 256
            for p in range(64):
                pe_, pek = (pSS, "pSS") if p % 2 == 0 else (pB, "pB")
                wn, wnk = win[p % 2], "win%d" % (p % 2)
                ec, eck = eoc[p % 2], "eoc%d" % (p % 2)
                P.op("pe", lambda e, p=p, w3c0=w3c0, pe_=pe_: e.matmul(pe_[:, 0:256], lhsT=h2s[:, :, p], rhs=fw3b[:, w3c0:w3c0 + 256], start=True, stop=True),
                     reads=["h2T", "fw3b"], writes=[pek])
                P.op("act", lambda e, p=p, grp4=grp4, wn=wn: e.activation(out=wn[:, :], in_=delta[:, grp4 * 128:(grp4 + 1) * 128], func=AF.Exp, scale=Swin[:, p:p + 1]),
                     reads=["delta", "Swin"], writes=[wnk])
                P.op("act", lambda e, pe_=pe_, ec=ec: e.activation(out=ec[:, :], in_=pe_[:, 0:256], func=AF.Copy), reads=[pek], writes=[eck])
                P.op("pool", lambda e, p=p, ec=ec, wn=wn: e.tensor_tensor(out=E3[:, :, p], in0=ec[:, 0:128], in1=wn[:, :], op=ALU.mult), reads=[eck, wnk], writes=["CV"])
                P.op("pool", lambda e, p=p, ec=ec, wn=wn: e.tensor_tensor(out=O3[:, :, p], in0=ec[:, 128:256], in1=wn[:, :], op=ALU.mult), reads=[eck, wnk], writes=["Ob"])
                if p == 0:
                    fc = (2 * grp4 + o) * 128
                    P.op("pe", lambda e, fc=fc: e.matmul(pY[0:1, 0:128], lhsT=h2T[:, 0:1], rhs=fw3f[:, fc:fc + 128], start=True, stop=True),
                         reads=["h2T", "fw3f"], writes=["pY"])
                    P.op("dve", lambda e, wn=wn: e.tensor_tensor(out=E3[0:1, :, 0], in0=pY[0:1, 0:128], in1=wn[0:1, :], op=ALU.mult), reads=["pY", wnk], writes=["CV"])
                    P.op("dve", lambda e, wn=wn: e.tensor_tensor(out=O3[0:1, :, 0], in0=pY[0:1, 0:128], in1=wn[0:1, :], op=ALU.mult), reads=["pY", wnk], writes=["Ob"])
            nsbt = 128 // (2 * NPQ)

            def filt_gen(sbt):
                c0 = sbt * 2 * NPQ
                Hs, hk = HsL[sbt % 2], "Hs%d" % (sbt % 2)
                for which, (src3, skey) in enumerate(((E3, "CV"), (O3, "Ob"))):
                    fwd_stage1(src3, skey, c0, ApF, "ApF")
                    yield
                    for qq in range(NPQ // 2):
                        rr = ApF[:, 0, 2 * qq:2 * qq + 2, :]
                        ri = ApF[:, 1, 2 * qq:2 * qq + 2, :]
                        if which == 0:
                            P.op("pe", lambda e, rr=rr: e.matmul(pSS[:, :], lhsT=G2r, rhs=rr, start=True, stop=False), reads=["R12b", "ApF"], writes=["pSS"])
                            P.op("pe", lambda e, ri=ri: e.matmul(pSS[:, :], lhsT=G2in, rhs=ri, start=False, stop=True), reads=["R12b", "ApF"], writes=["pSS"])
                            for j in range(2):
                                hcol = grp4 * 128 + o * 64 + sbt * NPQ + 2 * qq + j
                                P.op("act", lambda e, j=j, qq=qq, hcol=hcol, Hs=Hs: e.activation(out=Hs[:, 0, 2 * qq + j, :], in_=pSS[:, j * 256:(j + 1) * 256], func=AF.Identity,
                                                                                                 bias=hbcol[:, hcol:hcol + 1], scale=1.0),
                                     reads=["pSS", "hbcol"], writes=[hk])
                        else:
                            P.op("pe", lambda e, rr=rr: e.matmul(pSS[:, :], lhsT=G2i, rhs=rr, start=True, stop=False), reads=["R12b", "ApF"], writes=["pSS"])
                            P.op("pe", lambda e, ri=ri: e.matmul(pSS[:, :], lhsT=G2r, rhs=ri, start=False, stop=True), reads=["R12b", "ApF"], writes=["pSS"])
                            P.op("act", lambda e, qq=qq, Hs=Hs: e.activation(out=Hs[:, 1, 2 * qq:2 * qq + 2, :], in_=pSS[:, :].rearrange("k (q x) -> k q x", q=2), func=AF.Copy),
                                 reads=["pSS"], writes=[hk])
                    yield

            def data_gen(sbt):
                c0 = sbt * 2 * NPQ
                zk = "Z0_%d" % sbt
                Hs, hk = HsL[sbt % 2], "Hs%d" % (sbt % 2)
                fwd_stage1(Z[0], zk, c0, ApD, "ApD")
                yield
                for qq in range(NPQ // 2):
                    rr = ApD[:, 0, 2 * qq:2 * qq + 2, :]
                    ri = ApD[:, 1, 2 * qq:2 * qq + 2, :]
                    P.op("pe", lambda e, rr=rr: e.matmul(pXr[:, :], lhsT=G2r, rhs=rr, start=True, stop=False), reads=["R12b", "ApD"], writes=["pXr"])
                    P.op("pe", lambda e, ri=ri: e.matmul(pXr[:, :], lhsT=G2in, rhs=ri, start=False, stop=True), reads=["R12b", "ApD"], writes=["pXr"])
                    P.op("pe", lambda e, rr=rr: e.matmul(pXi[:, :], lhsT=G2i, rhs=rr, start=True, stop=False), reads=["R12b", "ApD"], writes=["pXi"])
                    P.op("pe", lambda e, ri=ri: e.matmul(pXi[:, :], lhsT=G2r, rhs=ri, start=False, stop=True), reads=["R12b", "ApD"], writes=["pXi"])
                    g = cnt[0]
                    cnt[0] += 1
                    w1, w1k = W1[g % 2], "W1_%d" % (g % 2)
                    w2, w2k = W2[g % 2], "W2_%d" % (g % 2)
                    cx, cxk = cpy[g % 2], "cpy%d" % (g % 2)
                    hr = Hs[:, 0, 2 * qq:2 * qq + 2, :].rearrange("k q x -> k (q x)")
                    hi = Hs[:, 1, 2 * qq:2 * qq + 2, :].rearrange("k q x -> k (q x)")
                    yr = Yb[:, 0, 2 * qq:2 * qq + 2, :].rearrange("k q x -> k (q x)")
                    yi = Yb[:, 1, 2 * qq:2 * qq + 2, :].rearrange("k q x -> k (q x)")
                    P.op("act", lambda e, cx=cx: e.activation(out=cx[:, :], in_=pXi[:, :], func=AF.Copy), reads=["pXi"], writes=[cxk])
                    P.op("act", lambda e: e.activation(out=cpy2[:, :], in_=pXr[:, :], func=AF.Copy), reads=["pXr"], writes=["cpy2"])
                    P.op("dve", lambda e, w1=w1, hr=hr: e.tensor_tensor(out=w1[:, :], in0=pXr[:, :], in1=hr, op=ALU.mult), reads=["pXr", hk], writes=[w1k])
                    P.op("pool", lambda e, w2=w2, cx=cx, hi=hi: e.tensor_tensor(out=w2[:, :], in0=cx[:, :], in1=hi, op=ALU.mult), reads=[cxk, hk], writes=[w2k])
                    P.op("dve", lambda e, w1=w1, w2=w2, yr=yr: e.tensor_tensor(out=yr, in0=w1[:, :], in1=w2[:, :], op=ALU.subtract), reads=[w1k, w2k], writes=["Yb"])
                    P.op("dve", lambda e, w1=w1, hr=hr: e.tensor_tensor(out=w1[:, :], in0=pXi[:, :], in1=hr, op=ALU.mult), reads=["pXi", hk, "Yb"], writes=[w1k])
                    P.op("pool", lambda e, w2=w2, hi=hi: e.tensor_tensor(out=w2[:, :], in0=cpy2[:, :], in1=hi, op=ALU.mult), reads=["cpy2", hk, "Yb"], writes=[w2k])
                    P.op("dve", lambda e, w1=w1, w2=w2, yi=yi: e.tensor_tensor(out=yi, in0=w1[:, :], in1=w2[:, :], op=ALU.add), reads=[w1k, w2k], writes=["Yb"])
                yield
                for q in range(NPQ):
                    for kc in range(2):
                        P.op("pe", lambda e, q=q, kc=kc: e.matmul(pB[:, kc * 256:(kc + 1) * 256], lhsT=Yb[:, 0, q, kc * 128:(kc + 1) * 128], rhs=R1, start=True, stop=False),
                             reads=["Yb", "R12b"], writes=["pB"])
                        P.op("pe", lambda e, q=q, kc=kc: e.matmul(pB[:, kc * 256:(kc + 1) * 256], lhsT=Yb[:, 1, q, kc * 128:(kc + 1) * 128], rhs=R2, start=False, stop=True),
                             reads=["Yb", "R12b"], writes=["pB"])
                    twiddle(pB, "pB", TT2r, TT2i, "TT2r", "TT2i", Bp[:, :, 0, q, :], Bp[:, :, 1, q, :], "Bp", v_inv)
                yield
                for hh in range(NPQ // 4):
                    n = 0
                    for kc in range(2):
                        for r in range(2):
                            rhs = Bp[:, kc, r, 4 * hh:4 * hh + 4, :]
                            lt = IF1b[:, (2 * kc + r) * 128:(2 * kc + r + 1) * 128]
                            P.op("pe", lambda e, rhs=rhs, lt=lt, n=n: e.matmul(pY[:, :], lhsT=lt, rhs=rhs, start=(n == 0), stop=(n == 3)),
                                 reads=["Bp", "IF1b"], writes=["pY"])
                            n += 1
                    cc = c0 + 8 * hh
                    zi = 1 if o == 0 else 2
                    zo = 0 if o == 0 else 2
                    okeys = [zk] if o == 0 else ["U2", "Z2_%d" % sbt]
                    P.op("dve", lambda e, cc=cc, zi=zi, zo=zo: e.scalar_tensor_tensor(out=Z[zo][:, cc:cc + 8, :], in0=pY[:, :].rearrange("a (c p) -> a c p", p=64),
                                                                                      scalar=1.0 / NFFT, in1=Z[zi][:, cc:cc + 8, :], op0=ALU.mult, op1=ALU.mult),
                         reads=["pY", "U%d" % zi] + Zkeys(zi), writes=okeys)
                yield

            for _ in filt_gen(0):
                pass
            for sbt in range(nsbt):
                gens = [data_gen(sbt)] + ([filt_gen(sbt + 1)] if sbt + 1 < nsbt else [])
                while gens:
                    for gq in list(gens):
                        try:
                            next(gq)
                        except StopIteration:
                            gens.remove(gq)
        mv = mixT[:, 4 + grp4, :].rearrange("c (a p) -> c a p", p=64)
        for pg in range(4):
            for pi in range(16):
                p = pg * 16 + pi
                P.op("pe", lambda e, p=p, pi=pi: e.matmul(pXr[:, pi * 32:(pi + 1) * 32], lhsT=Z[2][:, :, p], rhs=selb[:, :], start=True, stop=True),
                     reads=["U2", "selb"] + Zkeys(2), writes=["pXr"])
            P.op("act", lambda e, pg=pg, mv=mv: e.activation(out=mv[:, :, pg * 16:(pg + 1) * 16].rearrange("c a p -> c p a"),
                                                             in_=pXr[:, :].rearrange("c (p a) -> c p a", a=32), func=AF.Copy),
                 reads=["pXr"], writes=["mixT"])
    P.build()


def emit_hyena_h(nc, outer, D, mixT, groups=(0, 1, 2, 3), prefix="C_"):
    P = Prog(nc, sem_stack=outer, prefix=prefix)
    sb, ps = P.sb, P.ps
    Us = D("Us", [12, 128, NTOK], kind="Internal", dt=BF16)
    dF1cat, dTrTr, dTiTi, dR12 = D("F1cat_h", [128, 256]), D("TrTr_h", [128, 512]), D("TiTi_h", [128, 512]), D("R12", [128, 512])
    dTT2r, dTT2i, dIF1, dSwin = D("TT2r_h", [128, 512]), D("TT2i_h", [128, 512]), D("IF1_h", [128, 256]), D("Swin", [128, 64])
    dzemb = D("zembT", [33, NTOK])
    ddelta = D("delta4", [128, 512])
    dhbrow = D("hbrow", [1, 1024])
    dfw1, dfw2, dfw3, dfpar = D("fw1", [33, 64]), D("fw2", [64, 64]), D("fw3c4", [64, 2048]), D("fpar", [64, 4])
    dcw, dcb = D("cw4", [128, 36]), D("cb4", [128, 12])
    dsel = D("seld", [128, 32])
    didn = D("identd", [128, 128])

    U = [sb("U%d" % i, [128, NTOK + 2], BF16) for i in range(3)]
    CV = sb("CV", [128, NTOK], BF16)
    Ob = sb("Ob", [128, NTOK], BF16)
    h2T = sb("h2T", [64, NTOK], BF16)
    ApF = sb("ApF", [128, 2, NPQH, 128], BF16)
    ApD = sb("ApD", [128, 2, NPQH, 128], BF16)
    HsL = [sb("Hs%d" % i, [128, 2, NPQH, 128], BF16) for i in range(2)]
    Yb = sb("Yb", [128, 2, NPQH, 128], BF16)
    Bp = sb("Bp", [128, 2, NPQH, 128], BF16)
    W1 = [sb("W1_%d" % i, [128, 512], BF16) for i in range(4)]
    W2 = [sb("W2_%d" % i, [128, 512], BF16) for i in range(4)]
    cpy = [sb("cpr%d" % i, [128, 512], BF16) for i in range(4)]
    cpy2 = sb("cpy2", [128, 512], F32)
    tmpc = sb("tmpc", [128, 1024], F32)
    arg = tmpc[0:64, 0:512]
    h1 = tmpc[0:64, 512:1024]
    argi = sb("argi", [64, 512], mybir.dt.int32)
    F1b, R12b, IF1b = sb("F1b", [128, 256], BF16), sb("R12b", [128, 512], BF16), sb("IF1b", [128, 256], BF16)
    TrTr, TiTi = sb("TrTr", [128, 512], F32), sb("TiTi", [128, 512], F32)
    TT2r, TT2i = sb("TT2r", [128, 512], F32), sb("TT2i", [128, 512], F32)
    Swin = sb("Swin", [128, 64], F32)
    delta = sb("delta", [128, 512], F32)
    hbrow = sb("hbrow", [1, 1024], F32)
    etmp = sb("etmp", [1, 128], F32)
    fw1, fw2 = sb("fw1", [33, 64], F32), sb("fw2", [64, 64], F32)
    fw3b = sb("fw3b", [64, 2048], BF16)
    fpar = sb("fpar", [64, 8], F32)
    cw, cb = sb("cw", [128, 36], F32), sb("cb", [128, 12], F32)
    selb = sb("selb", [128, 32], BF16)
    ident = sb("ident", [128, 128], BF16)
    zc = [sb("zc%d" % i, [33, 512], F32) for i in range(2)]
    win = [sb("win%d" % i, [128, 128], F32) for i in range(2)]
    eoc = [sb("eoc%d" % i, [128, 256], F32) for i in range(2)]
    fw3f = sb("fw3f", [64, 1024], BF16)

    pSS = ps("pSS", [128, 512], F32)
    pA1 = [ps("pA1_%d" % i, [128, 512], F32) for i in range(2)]
    pXr, pXi = ps("pXr", [128, 512], F32), ps("pXi", [128, 512], F32)
    pB = ps("pB", [128, 512], F32)
    pY = ps("pY", [128, 512], F32)
    pTz = ps("pTz", [128, 8, 128], BF16)

    def ldcast(dst, dkey, src, n=512, parts=128):
        P.dma("sp", cpy2[0:parts, 0:n], src, writes=["cpy2"])
        P.op("dve", lambda e: e.tensor_copy(out=dst, in_=cpy2[0:parts, 0:n]), reads=["cpy2"], writes=[dkey])
    ldcast(F1b[:, :], "F1b", dF1cat[:, :], n=256)
    ldcast(R12b[:, :], "R12b", dR12[:, :])
    ldcast(IF1b[:, :], "IF1b", dIF1[:, :], n=256)
    ldcast(ident[:, :], "ident", didn[:, :], n=128)
    ldcast(selb[:, :], "selb", dsel[:, :], n=32)
    for q4 in range(4):
        P.dma("sp", cpy2[0:64, :], dfw3[:, q4 * 512:(q4 + 1) * 512], writes=["cpy2"])
        for o_ in range(2):
            wf = cpy2[0:64, o_ * 256:o_ * 256 + 128]
            wbk = cpy2[0:64, o_ * 256 + 128:o_ * 256 + 256]
            c0_ = q4 * 512 + o_ * 256
            P.op("dve", lambda e, wf=wf, wbk=wbk, c0_=c0_: e.tensor_tensor(out=fw3b[:, c0_:c0_ + 128], in0=wf, in1=wbk, op=ALU.add), reads=["cpy2"], writes=["fw3b"])
            P.op("dve", lambda e, wf=wf, wbk=wbk, c0_=c0_: e.tensor_tensor(out=fw3b[:, c0_ + 128:c0_ + 256], in0=wf, in1=wbk, op=ALU.subtract), reads=["cpy2"], writes=["fw3b"])
            P.op("dve", lambda e, wf=wf, q4=q4, o_=o_: e.tensor_copy(out=fw3f[:, (2 * q4 + o_) * 128:(2 * q4 + o_ + 1) * 128], in_=wf), reads=["cpy2"], writes=["fw3f"])
    for dst, key, src in ((TrTr, "TrTr", dTrTr), (TiTi, "TiTi", dTiTi), (TT2r, "TT2r", dTT2r), (TT2i, "TT2i", dTT2i), (Swin, "Swin", dSwin),
                          (delta, "delta", ddelta), (hbrow, "hbrow", dhbrow), (fw1, "fw1", dfw1), (fw2, "fw2", dfw2), (cw, "cw", dcw), (cb, "cb", dcb)):
        P.dma("sp", dst[:, :], src[:, :], writes=[key])
    P.dma("sp", fpar[:, 0:4], dfpar[:, :], writes=["fpar"])
    i2p = 1.0 / (2.0 * math.pi)
    for (bc, fc, o0) in ((0, 1, 4), (2, 3, 6)):
        P.op("dve", lambda e, bc=bc, fc=fc, o0=o0: e.tensor_tensor(out=fpar[:, o0 + 1:o0 + 2], in0=fpar[:, bc:bc + 1], in1=fpar[:, fc:fc + 1], op=ALU.mult),
             reads=["fpar"], writes=["fpar"])
        P.op("dve", lambda e, o0=o0: e.tensor_scalar(out=fpar[:, o0 + 1:o0 + 2], in0=fpar[:, o0 + 1:o0 + 2], scalar1=i2p, scalar2=16.0, op0=ALU.mult, op1=ALU.add),
             reads=["fpar"], writes=["fpar"])
        P.op("dve", lambda e, fc=fc, o0=o0: e.tensor_scalar(out=fpar[:, o0:o0 + 1], in0=fpar[:, fc:fc + 1], scalar1=i2p, scalar2=None, op0=ALU.mult),
             reads=["fpar"], writes=["fpar"])

    for ch in range(16):
        zt, zk = zc[ch % 2], "zc%d" % (ch % 2)
        P.dma("sp", zt[:, :], dzemb[:, ch * 512:(ch + 1) * 512], writes=[zk])
        for layer in range(2):
            if layer == 0:
                P.op("pe", lambda e, zt=zt: e.matmul(pSS[0:64, :], lhsT=fw1[:, :], rhs=zt[:, :], start=True, stop=True), reads=["fw1", zk], writes=["pSS"])
            else:
                P.op("pe", lambda e: e.matmul(pSS[0:64, :], lhsT=fw2[:, :], rhs=h1[:, :], start=True, stop=True), reads=["fw2", "tmpc"], writes=["pSS"])
            fr, fb = (4, 5) if layer == 0 else (6, 7)
            P.op("dve", lambda e, fr=fr, fb=fb: e.tensor_scalar(out=arg[:, :], in0=pSS[0:64, :], scalar1=fpar[:, fr:fr + 1], scalar2=fpar[:, fb:fb + 1],
                                                                op0=ALU.mult, op1=ALU.add), reads=["pSS", "fpar"], writes=["tmpc"])
            P.op("dve", lambda e: e.tensor_copy(out=argi[:, :], in_=arg[:, :]), reads=["tmpc"], writes=["argi"])
            P.op("dve", lambda e: e.tensor_copy(out=h1[:, :], in_=argi[:, :]), reads=["argi", "tmpc"], writes=["tmpc"])
            P.op("dve", lambda e: e.tensor_tensor(out=arg[:, :], in0=arg[:, :], in1=h1[:, :], op=ALU.subtract), reads=["tmpc"], writes=["tmpc"])
            P.op("dve", lambda e: e.scalar_tensor_tensor(out=arg[:, :], in0=arg[:, :], scalar=0.5, in1=arg[:, :], op0=ALU.is_gt, op1=ALU.subtract),
                 reads=["tmpc"], writes=["tmpc"])
            if layer == 0:
                P.op("act", lambda e: e.activation(out=h1[:, :], in_=arg[:, :], func=AF.Sin, scale=-2.0 * math.pi), reads=["tmpc"], writes=["tmpc"])
            else:
                P.op("act", lambda e, ch=ch: e.activation(out=h2T[:, ch * 512:(ch + 1) * 512], in_=arg[:, :], func=AF.Sin, scale=-2.0 * math.pi),
                     reads=["tmpc"], writes=["h2T"])

    Zkeys = lambda i: ["Z%d_%d" % (i, s) for s in range(128 // (2 * NPQH))]
    Z = [U[i][:, 0:NTOK].rearrange("a (c p) -> a c p", p=64) for i in range(3)]
    CVs = CV[:, :].rearrange("c (a p) -> c a p", p=64)
    E3 = CV[:, :].rearrange("a (c p) -> a c p", p=64)
    O3 = Ob[:, :].rearrange("a (c p) -> a c p", p=64)
    h2s = h2T[:, :].rearrange("j (a p) -> j a p", p=64)
    cnt = [0]

    def twiddle(psrc, pkey, Tr_, Ti_, trk, tik, outr, outi, okey, view, ieng="dve"):
        g = cnt[0]
        cnt[0] += 1
        w1, w1k = W1[g % 4], "W1_%d" % (g % 4)
        w2, w2k = W2[g % 4], "W2_%d" % (g % 4)
        cp_, cpk = cpy[g % 4], "cpr%d" % (g % 4)
        P.op("act", lambda e: e.activation(out=cp_[:, :], in_=psrc[:, :], func=AF.Copy), reads=[pkey], writes=[cpk])
        P.op("dve", lambda e: e.tensor_tensor(out=w1[:, :], in0=psrc[:, :], in1=Tr_[:, :], op=ALU.mult), reads=[pkey, trk], writes=[w1k])
        P.op("pool", lambda e: e.tensor_tensor(out=w2[:, :], in0=cp_[:, :], in1=Ti_[:, :], op=ALU.mult), reads=[cpk, tik], writes=[w2k])
        w1r, w1i = view(w1)
        w2r, w2i = view(w2)
        P.op("dve", lambda e: e.tensor_tensor(out=outr, in0=w1r, in1=w2i, op=ALU.subtract), reads=[w1k, w2k], writes=[okey])
        P.op(ieng, lambda e: e.tensor_tensor(out=outi, in0=w2r, in1=w1i, op=ALU.add), reads=[w1k, w2k], writes=[okey])

    vv = lambda t: t[:, :].rearrange("k (j r x) -> k j r x", j=2, r=2)
    v_fwd = lambda t: (vv(t)[:, :, 0, :], vv(t)[:, :, 1, :])
    v_inv = v_fwd

    def fwd_stage1(src3, skey, c0, Ap, apk, ieng="dve"):
        for q2 in range(NPQH // 2):
            g = cnt[0]
            pa, pak = pA1[g % 2], "pA1_%d" % (g % 2)
            for jj in range(2):
                c = c0 + 2 * (2 * q2 + jj)
                P.op("pe", lambda e, c=c, pa=pa, jj=jj: e.matmul(pa[:, jj * 256:(jj + 1) * 256], lhsT=src3[:, c:c + 2, :], rhs=F1b[:, :], start=True, stop=True),
                     reads=[skey, "F1b"], writes=[pak])
            twiddle(pa, pak, TrTr, TiTi, "TrTr", "TiTi", Ap[:, 0, 2 * q2:2 * q2 + 2, :], Ap[:, 1, 2 * q2:2 * q2 + 2, :], apk, v_fwd, ieng=ieng)

    G2r, G2in, G2i = R12b[:, 0:128], R12b[:, 128:256], R12b[:, 256:384]
    R1, R2 = R12b[:, 0:256], R12b[:, 256:512]

    for grp4 in groups:
        for i in range(3):
            P.dma("sp", U[i][:, 1:NTOK + 1], Us[3 * grp4 + i], reads=["Us"], writes=["U%d" % i] + Zkeys(i), key="U%d" % i)
            P.op("pool", lambda e, i=i: e.memset(U[i][:, 0:1], 0.0), reads=["U%d" % i], writes=["U%d" % i] + Zkeys(i))
            P.op("pool", lambda e, i=i: e.memset(U[i][:, NTOK + 1:NTOK + 2], 0.0), reads=["U%d" % i], writes=["U%d" % i])
        for i in range(3):
            ci = 9 * grp4 + 3 * i
            bi = 3 * grp4 + i
            for ch in range(8):
                j0 = ch * 1024
                P.op("dve", lambda e, i=i, j0=j0, ci=ci, bi=bi: e.tensor_scalar(out=tmpc[:, :], in0=U[i][:, j0:j0 + 1024], scalar1=cw[:, ci:ci + 1],
                                                                                scalar2=cb[:, bi:bi + 1], op0=ALU.mult, op1=ALU.add),
                     reads=["U%d" % i, "cw", "cb"], writes=["tmpc"])
                P.op("dve", lambda e, i=i, j0=j0, ci=ci: e.scalar_tensor_tensor(out=tmpc[:, :], in0=U[i][:, j0 + 1:j0 + 1025], scalar=cw[:, ci + 1:ci + 2],
                                                                                in1=tmpc[:, :], op0=ALU.mult, op1=ALU.add),
                     reads=["U%d" % i, "cw", "tmpc"], writes=["tmpc"])
                P.op("dve", lambda e, i=i, j0=j0, ci=ci: e.scalar_tensor_tensor(out=CV[:, j0:j0 + 1024], in0=U[i][:, j0 + 2:j0 + 1026], scalar=cw[:, ci + 2:ci + 3],
                                                                                in1=tmpc[:, :], op0=ALU.mult, op1=ALU.add),
                     reads=["U%d" % i, "cw", "tmpc"], writes=["CV"])
            for pg in range(8):
                for pi in range(8):
                    p = pg * 8 + pi
                    P.op("pe", lambda e, p=p, pi=pi: e.transpose(out=pTz[:, pi, :], in_=CVs[:, :, p], identity=ident[:, :]),
                         reads=["CV", "ident"], writes=["pTz"])
                P.op("act", lambda e, i=i, pg=pg: e.activation(out=Z[i][:, :, pg * 8:(pg + 1) * 8].rearrange("a c p -> a p c"), in_=pTz[:, :, :], func=AF.Copy),
                     reads=["pTz"], writes=["U%d" % i] + Zkeys(i))
        for o in range(2):
            w3c0 = grp4 * 512 + o * 256
            for p in range(64):
                pe_, pek = (pSS, "pSS") if p % 2 == 0 else (pB, "pB")
                wn, wnk = win[p % 2], "win%d" % (p % 2)
                ec, eck = eoc[p % 2], "eoc%d" % (p % 2)
                P.op("pe", lambda e, p=p, w3c0=w3c0, pe_=pe_: e.matmul(pe_[:, 0:256], lhsT=h2s[:, :, p], rhs=fw3b[:, w3c0:w3c0 + 256], start=True, stop=True),
                     reads=["h2T", "fw3b"], writes=[pek])
                P.op("act", lambda e, p=p, grp4=grp4, wn=wn: e.activation(out=wn[:, :], in_=delta[:, grp4 * 128:(grp4 + 1) * 128], func=AF.Exp, scale=Swin[:, p:p + 1]),
                     reads=["delta", "Swin"], writes=[wnk])
                P.op("act", lambda e, pe_=pe_, ec=ec: e.activation(out=ec[:, :], in_=pe_[:, 0:256], func=AF.Copy), reads=[pek], writes=[eck])
                P.op("pool", lambda e, p=p, ec=ec, wn=wn: e.tensor_tensor(out=E3[:, :, p], in0=ec[:, 0:128], in1=wn[:, :], op=ALU.mult), reads=[eck, wnk], writes=["CV"])
                P.op("dve", lambda e, p=p, ec=ec, wn=wn: e.tensor_tensor(out=O3[:, :, p], in0=ec[:, 128:256], in1=wn[:, :], op=ALU.mult), reads=[eck, wnk], writes=["Ob"])
                if p == 0:
                    fc = (2 * grp4 + o) * 128
                    P.op("pe", lambda e, fc=fc: e.matmul(pY[0:1, 0:128], lhsT=h2T[:, 0:1], rhs=fw3f[:, fc:fc + 128], start=True, stop=True),
                         reads=["h2T", "fw3f"], writes=["pY"])
                    hc0 = (2 * grp4 + o) * 128
                    P.op("dve", lambda e, wn=wn: e.tensor_tensor(out=etmp[:, :], in0=pY[0:1, 0:128], in1=wn[0:1, :], op=ALU.mult), reads=["pY", wnk], writes=["etmp"])
                    P.op("dve", lambda e: e.tensor_copy(out=O3[0:1, :, 0], in_=etmp[:, :]), reads=["etmp"], writes=["Ob"])
                    P.op("dve", lambda e, hc0=hc0: e.tensor_tensor(out=E3[0:1, :, 0], in0=etmp[:, :], in1=hbrow[0:1, hc0:hc0 + 128], op=ALU.add),
                         reads=["etmp", "hbrow"], writes=["CV"])
            nsbt = 128 // (2 * NPQH)

            def filt_gen(sbt):
                c0 = sbt * 2 * NPQH
                Hs, hk = HsL[sbt % 2], "Hs%d" % (sbt % 2)
                for which, (src3, skey) in enumerate(((E3, "CV"), (O3, "Ob"))):
                    fwd_stage1(src3, skey, c0, ApF, "ApF")
                    yield
                    for qq in range(NPQH // 4):
                        rr = ApF[:, 0, 4 * qq:4 * qq + 4, :]
                        ri = ApF[:, 1, 4 * qq:4 * qq + 4, :]
                        if which == 0:
                            P.op("pe", lambda e, rr=rr: e.matmul(pSS[:, :], lhsT=G2r, rhs=rr, start=True, stop=False), reads=["R12b", "ApF"], writes=["pSS"])
                            P.op("pe", lambda e, ri=ri: e.matmul(pSS[:, :], lhsT=G2in, rhs=ri, start=False, stop=True), reads=["R12b", "ApF"], writes=["pSS"])
                        else:
                            P.op("pe", lambda e, rr=rr: e.matmul(pSS[:, :], lhsT=G2i, rhs=rr, start=True, stop=False), reads=["R12b", "ApF"], writes=["pSS"])
                            P.op("pe", lambda e, ri=ri: e.matmul(pSS[:, :], lhsT=G2r, rhs=ri, start=False, stop=True), reads=["R12b", "ApF"], writes=["pSS"])
                        P.op("act", lambda e, qq=qq, Hs=Hs, which=which: e.activation(out=Hs[:, which, 4 * qq:4 * qq + 4, :], in_=pSS[:, :].rearrange("k (q x) -> k q x", q=4), func=AF.Copy),
                             reads=["pSS"], writes=[hk])
                    yield

            def data_gen(sbt):
                c0 = sbt * 2 * NPQH
                zk = "Z0_%d" % sbt
                Hs, hk = HsL[sbt % 2], "Hs%d" % (sbt % 2)
                fwd_stage1(Z[0], zk, c0, ApD, "ApD", ieng="pool")
                yield
                for qq in range(NPQH // 4):
                    rr = ApD[:, 0, 4 * qq:4 * qq + 4, :]
                    ri = ApD[:, 1, 4 * qq:4 * qq + 4, :]
                    P.op("pe", lambda e, rr=rr: e.matmul(pXr[:, :], lhsT=G2r, rhs=rr, start=True, stop=False), reads=["R12b", "ApD"], writes=["pXr"])
                    P.op("pe", lambda e, ri=ri: e.matmul(pXr[:, :], lhsT=G2in, rhs=ri, start=False, stop=True), reads=["R12b", "ApD"], writes=["pXr"])
                    P.op("pe", lambda e, rr=rr: e.matmul(pXi[:, :], lhsT=G2i, rhs=rr, start=True, stop=False), reads=["R12b", "ApD"], writes=["pXi"])
                    P.op("pe", lambda e, ri=ri: e.matmul(pXi[:, :], lhsT=G2r, rhs=ri, start=False, stop=True), reads=["R12b", "ApD"], writes=["pXi"])
                    g = cnt[0]
                    cnt[0] += 1
                    w1, w1k = W1[g % 4], "W1_%d" % (g % 4)
                    w2, w2k = W2[g % 4], "W2_%d" % (g % 4)
                    cx, cxk = cpy[g % 4], "cpr%d" % (g % 4)
                    hr = Hs[:, 0, 4 * qq:4 * qq + 4, :].rearrange("k q x -> k (q x)")
                    hi = Hs[:, 1, 4 * qq:4 * qq + 4, :].rearrange("k q x -> k (q x)")
                    yr = Yb[:, 0, 4 * qq:4 * qq + 4, :].rearrange("k q x -> k (q x)")
                    yi = Yb[:, 1, 4 * qq:4 * qq + 4, :].rearrange("k q x -> k (q x)")
                    P.op("act", lambda e, cx=cx: e.activation(out=cx[:, :], in_=pXi[:, :], func=AF.Copy), reads=["pXi"], writes=[cxk])
                    P.op("act", lambda e: e.activation(out=cpy2[:, :], in_=pXr[:, :], func=AF.Copy), reads=["pXr"], writes=["cpy2"])
                    P.op("dve", lambda e, w1=w1, hr=hr: e.tensor_tensor(out=w1[:, :], in0=pXr[:, :], in1=hr, op=ALU.mult), reads=["pXr", hk], writes=[w1k])
                    P.op("pool", lambda e, w2=w2, cx=cx, hi=hi: e.tensor_tensor(out=w2[:, :], in0=cx[:, :], in1=hi, op=ALU.mult), reads=[cxk, hk], writes=[w2k])
                    P.op("dve", lambda e, w1=w1, w2=w2, yr=yr: e.tensor_tensor(out=yr, in0=w1[:, :], in1=w2[:, :], op=ALU.subtract), reads=[w1k, w2k], writes=["Yb"])
                    P.op("pool", lambda e, w1=w1, hr=hr, cx=cx: e.tensor_tensor(out=w1[:, :], in0=cx[:, :], in1=hr, op=ALU.mult), reads=[cxk, hk, "Yb"], writes=[w1k])
                    P.op("pool", lambda e, w2=w2, hi=hi: e.tensor_tensor(out=w2[:, :], in0=cpy2[:, :], in1=hi, op=ALU.mult), reads=["cpy2", hk, "Yb"], writes=[w2k])
                    P.op("dve", lambda e, w1=w1, w2=w2, yi=yi: e.tensor_tensor(out=yi, in0=w1[:, :], in1=w2[:, :], op=ALU.add), reads=[w1k, w2k], writes=["Yb"])
                yield
                for q2 in range(NPQH // 2):
                    for jj in range(2):
                        q = 2 * q2 + jj
                        P.op("pe", lambda e, q=q, jj=jj: e.matmul(pB[:, jj * 256:(jj + 1) * 256], lhsT=Yb[:, 0, q, :], rhs=R1, start=True, stop=False),
                             reads=["Yb", "R12b"], writes=["pB"])
                        P.op("pe", lambda e, q=q, jj=jj: e.matmul(pB[:, jj * 256:(jj + 1) * 256], lhsT=Yb[:, 1, q, :], rhs=R2, start=False, stop=True),
                             reads=["Yb", "R12b"], writes=["pB"])
                    twiddle(pB, "pB", TT2r, TT2i, "TT2r", "TT2i", Bp[:, 0, 2 * q2:2 * q2 + 2, :], Bp[:, 1, 2 * q2:2 * q2 + 2, :], "Bp", v_inv, ieng="pool")
                yield
                for hh in range(NPQH // 4):
                    for r in range(2):
                        rhs = Bp[:, r, 4 * hh:4 * hh + 4, :]
                        lt = IF1b[:, r * 128:(r + 1) * 128]
                        P.op("pe", lambda e, rhs=rhs, lt=lt, r=r: e.matmul(pY[:, :], lhsT=lt, rhs=rhs, start=(r == 0), stop=(r == 1)),
                             reads=["Bp", "IF1b"], writes=["pY"])
                    cc = c0 + 8 * hh
                    zi = 1 if o == 0 else 2
                    zo = 0 if o == 0 else 2
                    okeys = [zk] if o == 0 else ["U2", "Z2_%d" % sbt]
                    P.op("dve", lambda e, cc=cc, zi=zi, zo=zo: e.scalar_tensor_tensor(out=Z[zo][:, cc:cc + 8, :], in0=pY[:, :].rearrange("a (c p) -> a c p", p=64),
                                                                                      scalar=2.0 / NFFT, in1=Z[zi][:, cc:cc + 8, :], op0=ALU.mult, op1=ALU.mult),
                         reads=["pY", "U%d" % zi] + Zkeys(zi), writes=okeys)
                yield

            for _ in filt_gen(0):
                pass
            for sbt in range(nsbt):
                gens = [data_gen(sbt)] + ([filt_gen(sbt + 1)] if sbt + 1 < nsbt else [])
                while gens:
                    for gq in list(gens):
                        try:
                            next(gq)
                        except StopIteration:
                            gens.remove(gq)
        mv = mixT[:, 4 + grp4, :].rearrange("c (a p) -> c a p", p=64)
        for pg in range(4):
            for pi in range(16):
                p = pg * 16 + pi
                P.op("pe", lambda e, p=p, pi=pi: e.matmul(pXr[:, pi * 32:(pi + 1) * 32], lhsT=Z[2][:, :, p], rhs=selb[:, :], start=True, stop=True),
                     reads=["U2", "selb"] + Zkeys(2), writes=["pXr"])
            P.op("act", lambda e, pg=pg, mv=mv: e.activation(out=mv[:, :, pg * 16:(pg + 1) * 16].rearrange("c a p -> c p a"),
                                                             in_=pXr[:, :].rearrange("c (p a) -> c p a", a=32), func=AF.Copy),
                 reads=["pXr"], writes=["mixT"])
    P.build()


def emit_phase2_f(nc, outer, D, mixT, NT=2048):
    P = Prog(nc, sem_stack=outer, prefix="D_")
    x_tok = D("x_tok", [NT, 1024])
    cvec = D("cvec", [128, 8])
    w_ada = D("w_ada2", [1024, 4096])
    b_ada = D("b_ada2", [1, 4096])
    gvec = D("gvec", [3, 1024])
    gmix = D("gmix", [128, 8])
    w_out = D("w_out", [1024, 1024])
    w1 = D("w1", [1024, 4096])
    w2 = D("w2", [4096, 1024])
    out = D("out", [NT, 1024], kind="ExternalOutput")
    x1s = D("x1s", [NT, 1024], kind="Internal")

    sb, ps = P.sb, P.ps
    wout_b = sb("wout_b", [128, 8, 1024], BF16)
    mods = sb("mods", [128, 4096], F32)
    sbc = sb("sbc", [128, 8, 128], F32)
    ones = sb("ones", [128, 128], F32)
    ident = sb("ident", [128, 128], BF16)
    identf = sb("identf", [128, 128], F32)
    stage = [sb("stage%d" % i, [128, 2048], F32) for i in range(2)]
    w1b = [sb("w1b%d" % i, [128, 8, 512], BF16) for i in range(2)]
    w2b = [sb("w2b%d" % i, [128, 4, 1024], BF16) for i in range(2)]
    h2T = sb("h2T", [128, 8, 1024], BF16)
    f2acc = sb("f2acc", [128, 8, 1024], F32)
    brow = f2acc[0:1, 0:4, :].rearrange("p a b -> p (a b)")
    xt = [sb("xt%d" % i, [128, 1024], F32) for i in range(2)]
    sqm = sb("sqm", [128, 8, 128], BF16)
    onescol = sb("onescol", [128, 2], BF16)
    ysb = sb("ysb", [128, 1024], F32)
    tmp = sb("tmp", [128, 1024], F32)
    tmp2 = sb("tmp2", [128, 1024], F32)
    h2b = sb("h2b", [128, 1024], BF16)
    rl = [sb("rl%d" % i, [128, 512], F32) for i in range(2)]
    aT = [sb("aT%d" % i, [128, 4, 512], BF16) for i in range(2)]
    small = sb("small", [128, 32], F32)
    csb = sb("csb", [128, 8], F32)
    gmx = sb("gmx", [128, 8], F32)

    pA = ps("pA", [128, 1024], F32)
    pH = ps("pH", [128, 1024], F32)
    pT = ps("pT", [128, 1024], BF16)
    pM = [ps("pM%d" % i, [128, 512], F32) for i in range(2)]

    P.op("pool", lambda e: e.memset(ones[:, :], 1.0), writes=["ones"])
    P.op("pool", lambda e: e.memset(onescol[:, :], 1.0), writes=["onescol"])
    P.op("pool", lambda e: e.memset(identf[:, :], 0.0), writes=["identf"])
    identd = D("identd", [128, 128])
    P.dma("sp", identf[:, :], identd[:, :], writes=["identf"])
    P.op("dve", lambda e: e.tensor_copy(out=ident[:, :], in_=identf[:, :]), reads=["identf"], writes=["ident"])
    P.dma("sp", csb[:, :], cvec[:, :], writes=["csb"])
    P.dma("sp", gmx[:, :], gmix[:, :], writes=["gmx"])
    P.dma("sp", brow[:, :], b_ada[:, :], writes=["brow"])
    P.op("act", lambda e: e.activation(out=csb[:, :], in_=csb[:, :], func=AF.Silu), reads=["csb"], writes=["csb"])
    for k in range(8):
        P.op("dve", lambda e, k=k: e.tensor_scalar(out=sbc[:, k, :], in0=ones[:, :], scalar1=csb[:, k:k + 1],
                                                    scalar2=None, op0=ALU.mult), reads=["ones", "csb"], writes=["sbc"])
    wa = w_ada.rearrange("(k p) n -> p k n", p=128)
    for blk in range(16):
        st = stage[blk % 2]
        skey = "stage%d" % (blk % 2)
        stv = st[:, :].rearrange("p (k n) -> p k n", k=8)
        P.dma("sp", stv, wa[:, :, blk * 256:(blk + 1) * 256], writes=[skey])
        pm = pM[blk % 2]
        pkey = "pM%d" % (blk % 2)
        for k in range(8):
            P.op("pe", lambda e, k=k, pm=pm, stv=stv: e.matmul(pm[:, 0:256], lhsT=sbc[:, k, :], rhs=stv[:, k, :],
                                                               start=(k == 0), stop=False),
                 reads=["sbc", skey], writes=[pkey])
        P.op("pe", lambda e, pm=pm, blk=blk: e.matmul(pm[:, 0:256], lhsT=ones[0:1, :], rhs=brow[0:1, blk * 256:(blk + 1) * 256],
                                                     start=False, stop=True), reads=["ones", "brow"], writes=[pkey])
        P.op("act", lambda e, pm=pm, blk=blk: e.activation(out=mods[:, blk * 256:(blk + 1) * 256], in_=pm[:, 0:256], func=AF.Copy),
             reads=[pkey], writes=["mods"])
    gb = stage[0][:, :]
    P.dma("sp", gb[:, 0:1024], gvec[0:1, :].to_broadcast((128, 1024)), writes=["stage0"])
    gb1 = stage[1][:, :]
    P.dma("sp", gb1[:, 0:1024], gvec[1:2, :].to_broadcast((128, 1024)), writes=["stage1"])
    P.dma("sp", gb1[:, 1024:2048], gvec[2:3, :].to_broadcast((128, 1024)), writes=["stage1"])
    G1, SH2, G2, G3 = mods[:, 0:1024], mods[:, 1024:2048], mods[:, 2048:3072], mods[:, 3072:4096]
    P.op("dve", lambda e: e.tensor_tensor(out=G1, in0=G1, in1=gb[:, 0:1024], op=ALU.mult), reads=["mods", "stage0"], writes=["mods"])
    P.op("dve", lambda e: e.scalar_tensor_tensor(out=G2, in0=G2, scalar=1.0, in1=gb1[:, 0:1024], op0=ALU.add, op1=ALU.mult),
         reads=["mods", "stage1"], writes=["mods"])
    P.op("dve", lambda e: e.tensor_tensor(out=G3, in0=G3, in1=gb1[:, 1024:2048], op=ALU.mult), reads=["mods", "stage1"], writes=["mods"])
    wo = w_out.rearrange("(k p) n -> p k n", p=128)
    for c4 in range(4):
        st = stage[c4 % 2]
        skey = "stage%d" % (c4 % 2)
        stv = st[:, :].rearrange("p (k n) -> p k n", k=2)
        P.dma("sp", stv, wo[:, 2 * c4:2 * c4 + 2, :], writes=[skey])
        for kk in range(2):
            k = 2 * c4 + kk
            P.op("dve" if kk == 0 else "pool", lambda e, k=k, kk=kk, stv=stv: e.tensor_scalar(
                out=wout_b[:, k, :], in0=stv[:, kk, :], scalar1=gmx[:, k:k + 1], scalar2=None, op0=ALU.mult),
                reads=[skey, "gmx"], writes=["wout_b"])

    def sumsq(src, dst, reads, key, eng="dve"):
        w = src.shape[-1]
        P.op("pool", lambda e: e.tensor_tensor(out=tmp2[:, 0:w], in0=src, in1=src, op=ALU.mult), reads=reads, writes=["tmp2"])
        P.op(eng, lambda e: e.tensor_reduce(out=dst, in_=tmp2[:, 0:w], axis=AX.X, op=ALU.add), reads=["tmp2"] + list(reads), writes=[key])

    nsm = [0]

    def smallcol(n=1):
        c = nsm[0] % 16
        nsm[0] += 1
        return small[:, 2 * c:2 * c + n], "small%d" % c

    w1v = w1.rearrange("(k p) n -> p k n", p=128)
    w2v = w2.rearrange("(c p) n -> p c n", p=128)
    it = [0]
    for half in range(NT // 1024):
        for tile in range(8):
            t0 = half * 1024 + tile * 128
            i = it[0]
            it[0] += 1
            X, xk = xt[i % 2], "xt%d" % (i % 2)
            tl = t0
            MBk = lambda k, tl=tl: mixT[:, k, tl:tl + 128]
            mbk = "mixT"
            P.dma("sp", X[:, :], x_tok[t0:t0 + 128, :], writes=[xk])
            P.op("pool", lambda e, tl=tl: e.tensor_tensor(out=sqm[:, :, :], in0=mixT[:, :, tl:tl + 128], in1=mixT[:, :, tl:tl + 128], op=ALU.mult),
                 reads=["mixT"], writes=["sqm"])
            pst = pM[i % 2]
            pstk = "pM%d" % (i % 2)
            for grp in range(2):
                for k in range(4):
                    P.op("pe", lambda e, grp=grp, k=k, pst=pst: e.matmul(pst[:, grp:grp + 1], lhsT=sqm[:, 4 * grp + k, :], rhs=onescol[:, 0:1],
                                                                        start=(k == 0), stop=(k == 3)), reads=["sqm", "onescol"], writes=[pstk])
            rr, rrk = smallcol(2)
            _rs(P, None, pst[:, 0:2], rr, 512.0, "rmix", [pstk], rrk)
            for nh in range(2):
                for k in range(4):
                    P.op("pe", lambda e, k=k, nh=nh, MBk=MBk: e.matmul(pA[:, nh * 512:(nh + 1) * 512], lhsT=MBk(k),
                                                                     rhs=wout_b[:, k, nh * 512:(nh + 1) * 512],
                                                                     start=(k == 0), stop=(k == 3)),
                         reads=[mbk, "wout_b"], writes=["pA%d" % nh])
                for k in range(4, 8):
                    P.op("pe", lambda e, k=k, nh=nh, MBk=MBk: e.matmul(pH[:, nh * 512:(nh + 1) * 512], lhsT=MBk(k),
                                                                     rhs=wout_b[:, k, nh * 512:(nh + 1) * 512],
                                                                     start=(k == 4), stop=(k == 7)),
                         reads=[mbk, "wout_b"], writes=["pH%d" % nh])
            P.op("act", lambda e, rr=rr: e.activation(out=ysb[:, :], in_=pA[:, :], func=AF.Copy, scale=rr[:, 0:1]),
                 reads=["pA0", "pA1", rrk], writes=["ysb"])
            P.op("dve", lambda e, rr=rr: e.scalar_tensor_tensor(out=ysb[:, :], in0=pH[:, :], scalar=rr[:, 1:2], in1=ysb[:, :],
                                                                op0=ALU.mult, op1=ALU.add),
                 reads=["pH0", "pH1", rrk, "ysb"], writes=["ysb"])
            ss2, ss2k = smallcol(1)
            r2, r2k = smallcol(1)
            sumsq(ysb[:, :], ss2[:, 0:1], ["ysb"], ss2k)
            _rs(P, None, ss2, r2, 1024.0, "ry", [ss2k], r2k)
            P.op("dve", lambda e, r2=r2: e.scalar_tensor_tensor(out=tmp[:, :], in0=ysb[:, :], scalar=r2[:, 0:1], in1=G1,
                                                                op0=ALU.mult, op1=ALU.mult),
                 reads=["ysb", r2k, "mods"], writes=["tmp"])
            P.op("pool", lambda e, X=X: e.tensor_tensor(out=X[:, :], in0=tmp[:, :], in1=X[:, :], op=ALU.add),
                 reads=["tmp", xk], writes=[xk])
            P.dma("pool", x1s[t0:t0 + 128, :], X[:, :], reads=[xk], writes=["x1s%d" % (t0 // 128)])
            ss3, ss3k = smallcol(1)
            r3, r3k = smallcol(1)
            sumsq(X[:, :], ss3[:, 0:1], [xk], ss3k)
            _rs(P, None, ss3, r3, 1024.0, "r1", [ss3k], r3k)
            P.op("dve", lambda e, r3=r3, X=X: e.scalar_tensor_tensor(out=tmp[:, :], in0=X[:, :], scalar=r3[:, 0:1], in1=G2,
                                                                     op0=ALU.mult, op1=ALU.mult),
                 reads=[xk, r3k, "mods"], writes=["tmp"])
            P.op("pool", lambda e: e.tensor_tensor(out=h2b[:, :], in0=tmp[:, :], in1=SH2, op=ALU.add),
                 reads=["tmp", "mods"], writes=["h2b"])
            for k in range(8):
                P.op("pe", lambda e, k=k: e.transpose(out=pT[:, k * 128:(k + 1) * 128], in_=h2b[:, k * 128:(k + 1) * 128],
                                                      identity=ident[:, :]), reads=["h2b", "ident"], writes=["pT"])
            P.op("act", lambda e, tile=tile: e.activation(out=h2T[:, :, tile * 128:(tile + 1) * 128],
                                                          in_=pT[:, :].rearrange("p (k t) -> p k t", k=8), func=AF.Copy),
                 reads=["pT"], writes=["h2T"])
        for j in range(8):
            W1B, w1k = w1b[j % 2], "w1b%d" % (j % 2)
            W2B, w2k = w2b[j % 2], "w2b%d" % (j % 2)
            for hh in range(2):
                st, skey = stage[hh], "stage%d" % hh
                stv = st[:, :].rearrange("p (k n) -> p k n", k=8)
                P.dma("sp", stv, w1v[:, :, j * 512 + hh * 256:j * 512 + (hh + 1) * 256], writes=[skey])
                P.op("dve" if hh == 0 else "pool", lambda e, W1B=W1B, stv=stv, hh=hh: e.tensor_copy(
                    out=W1B[:, :, hh * 256:(hh + 1) * 256], in_=stv), reads=[skey], writes=[w1k])
            for hh in range(2):
                st, skey = stage[hh], "stage%d" % hh
                stv = st[:, :].rearrange("p (c n) -> p c n", c=2)
                P.dma("sp", stv, w2v[:, j * 4 + hh * 2:j * 4 + hh * 2 + 2, :], writes=[skey])
                P.op("dve" if hh == 0 else "pool", lambda e, W2B=W2B, stv=stv, hh=hh: e.tensor_copy(
                    out=W2B[:, hh * 2:hh * 2 + 2, :], in_=stv), reads=[skey], writes=[w2k])
            for tg in range(2):
                AT, atk = aT[tg], "aT%d" % tg
                for hc in range(4):
                    pm, pkey = pM[hc % 2], "pM%d" % (hc % 2)
                    RL, rlk = rl[hc % 2], "rl%d" % (hc % 2)
                    for k in range(8):
                        P.op("pe", lambda e, k=k, hc=hc, tg=tg, pm=pm, W1B=W1B: e.matmul(
                            pm[:, :], lhsT=W1B[:, k, hc * 128:(hc + 1) * 128], rhs=h2T[:, k, tg * 512:(tg + 1) * 512],
                            start=(k == 0), stop=(k == 7)), reads=[w1k, "h2T"], writes=[pkey])
                    P.op("act", lambda e, pm=pm, RL=RL: e.activation(out=RL[:, :], in_=pm[:, :], func=AF.Relu),
                         reads=[pkey], writes=[rlk])
                    P.op("act", lambda e, RL=RL, AT=AT, hc=hc: e.activation(out=AT[:, hc, :], in_=RL[:, :], func=AF.Square),
                         reads=[rlk], writes=[atk])
                for tt in range(4):
                    tile = tg * 4 + tt
                    for nh in range(2):
                        pp, ppk = (pA, "pA%d" % nh) if tt % 2 == 0 else (pH, "pH%d" % nh)
                        for hc in range(4):
                            P.op("pe", lambda e, hc=hc, tt=tt, nh=nh, pp=pp, AT=AT, W2B=W2B: e.matmul(
                                pp[:, nh * 512:(nh + 1) * 512], lhsT=AT[:, hc, tt * 128:(tt + 1) * 128],
                                rhs=W2B[:, hc, nh * 512:(nh + 1) * 512], start=(hc == 0), stop=(hc == 3)),
                                reads=[atk, w2k], writes=[ppk])
                        fv = f2acc[:, tile, nh * 512:(nh + 1) * 512]
                        fk = "f2_%d_%d" % (tile, nh)
                        if j == 0:
                            P.op("dve", lambda e, fv=fv, pp=pp, nh=nh: e.tensor_copy(out=fv, in_=pp[:, nh * 512:(nh + 1) * 512]),
                                 reads=[ppk], writes=[fk])
                        else:
                            P.op("dve", lambda e, fv=fv, pp=pp, nh=nh: e.tensor_tensor(out=fv, in0=pp[:, nh * 512:(nh + 1) * 512], in1=fv, op=ALU.add),
                                 reads=[ppk, fk], writes=[fk])
        for tile in range(8):
            t0 = half * 1024 + tile * 128
            i = it[0]
            it[0] += 1
            X, xk = xt[i % 2], "xt%d" % (i % 2)
            P.dma("sp", X[:, :], x1s[t0:t0 + 128, :], reads=["x1s%d" % (t0 // 128)], writes=[xk], key=xk)
            fkeys = ["f2_%d_%d" % (tile, nh) for nh in range(2)]
            ss4, ss4k = smallcol(1)
            r4, r4k = smallcol(1)
            sumsq(f2acc[:, tile, :], ss4[:, 0:1], fkeys, ss4k)
            _rs(P, None, ss4, r4, 1024.0, "rf", [ss4k], r4k)
            P.op("dve", lambda e, r4=r4, tile=tile: e.scalar_tensor_tensor(out=tmp[:, :], in0=f2acc[:, tile, :], scalar=r4[:, 0:1], in1=G3,
                                                                          op0=ALU.mult, op1=ALU.mult),
                 reads=fkeys + [r4k, "mods"], writes=["tmp"])
            P.op("pool", lambda e, X=X: e.tensor_tensor(out=X[:, :], in0=tmp[:, :], in1=X[:, :], op=ALU.add),
                 reads=["tmp", xk], writes=[xk])
            P.dma("pool", out[t0:t0 + 128, :], X[:, :], reads=[xk], writes=["out%d" % (t0 // 128)])
    P.build()


def build_fused(upto=4, dbg=False):
    nc = bass.Bass("TRN2", target_bir_lowering=False)
    outer = ExitStack()
    D = DramReg(nc)
    mixT = outer.enter_context(nc.sbuf_tensor("mixT_keep", [128, 8, NOWN], BF16))
    with ExitStack() as es_a:
        hTw = es_a.enter_context(nc.sbuf_tensor("hTw_keep", [128, 8, NW], BF16))
        emit_attn_norm_f(nc, outer, D, hTw)
        if upto >= 1:
            emit_attn_f(nc, outer, D, mixT, hTw)
    if upto >= 2:
        emit_hyproj_f(nc, outer, D)
    if upto >= 3:
        emit_hyena_h(nc, outer, D, mixT, groups=(0, 1, 2, 3), prefix="C0_")
    if upto >= 4:
        emit_phase2_f(nc, outer, D, mixT)
    if dbg:
        P = Prog(nc, sem_stack=outer, prefix="E_")
        dd = D("dbg_mixT", [128, 8, NOWN], kind="ExternalOutput", dt=BF16)
        P.dma("sp", dd[:, :, :], mixT[:, :, :], reads=["mixT"], writes=["dbg"])
        P.build()
    outer.close()
    return nc


def fused_inputs(inp):
    f = lambda a: np.ascontiguousarray(a, dtype=np.float32)
    K = _hy_consts()
    K.update(_hy_consts_h())
    C, S = _rope_tables()
    M = _attn_masks()
    hsel = np.zeros((128, 64), np.float32)
    hsel[0:64, 0] = 1.0
    hsel[64:128, 32] = 1.0
    w_in = inp["w_in"][0]
    hy_w = w_in[:, 1536:]
    w_incA = []
    for hp in range(4):
        cs = slice(128 * hp, 128 * hp + 128)
        wq, wk, wv = w_in[:, 0:512][:, cs], w_in[:, 512:1024][:, cs], w_in[:, 1024:1536][:, cs]
        w_incA.append(np.concatenate([wq, _perm_cols(wq), wk, _perm_cols(wk), wv, np.zeros((1024, 128), np.float32)], axis=1))
    w_incA = f(np.stack(w_incA))
    w_incH = f(np.concatenate([hy_w[:, a * 512 + 128 * g:a * 512 + 128 * g + 128] for g in range(4) for a in range(3)], axis=1))
    cwv = inp["conv_w"][0].reshape(3, 3, 512)
    cbv = inp["conv_b"][0].reshape(3, 512)
    cw4 = np.zeros((128, 36), np.float32)
    cb4 = np.zeros((128, 12), np.float32)
    for g in range(4):
        for a in range(3):
            cb4[:, 3 * g + a] = cbv[a, 128 * g:128 * g + 128]
            for t in range(3):
                cw4[:, 9 * g + 3 * a + t] = cwv[t, a, 128 * g:128 * g + 128]
    w3 = inp["filt_w3"][0].reshape(64, 2, 2, 4, 128).transpose(0, 3, 1, 2, 4).reshape(64, 2048)
    hb = inp["hyena_bias"][0]
    hbcol4 = np.zeros((128, 4, 2, 64), np.float32)
    for g in range(4):
        for cp in range(2):
            hbcol4[64 * cp:64 * cp + 64, g, :, :] = hb[:, 128 * g:128 * g + 128][:, cp::2][None, :, :]
    cols2 = np.r_[2048:3072, 3072:6144]
    shared = {
        "w_ada1": f(inp["w_ada"][0][:, 0:2048]), "b_ada1": f(inp["b_ada"][0][0:2048].reshape(16, 128).T),
        "gpre": f(inp["g_pre_mix"][0].reshape(8, 128).T), "w_incA": w_incA, "w_incH": w_incH,
        "maskd": f(M), "hseld": hsel, "identd": np.eye(128, dtype=np.float32),
        "F1cat_h": K["F1cat_h"], "TrTr_h": K["TrTr_h"], "TiTi_h": K["TiTi_h"], "R12": K["R12"], "TT2r_h": K["TT2r_h"], "TT2i_h": K["TT2i_h"],
        "IF1_h": K["IF1_h"], "Swin": K["Swin"], "zembT": K["zembT"],
        "delta4": f(np.broadcast_to(K["deltas"][None, :], (128, 512))),
        "hbrow": f(np.stack([hb[o_, 128 * g_:128 * g_ + 128] for g_ in range(4) for o_ in range(2)]).reshape(1, 1024)),
        "fw1": f(inp["filt_w1"][0]), "fw2": f(inp["filt_w2"][0]), "fw3c4": f(w3),
        "fpar": f(np.stack([inp["filt_b1"][0], inp["filt_freq1"][0], inp["filt_b2"][0], inp["filt_freq2"][0]], axis=1)),
        "cw4": cw4, "cb4": cb4,
        "w_ada2": f(inp["w_ada"][0][:, cols2]), "b_ada2": f(inp["b_ada"][0][cols2][None, :]),
        "gvec": f(np.stack([inp["g_post_mix"][0], inp["g_pre_mlp"][0], inp["g_post_mlp"][0]])),
        "gmix": f(np.concatenate([inp["g_attn_out"][0], inp["g_hyena_out"][0]]).reshape(8, 128).T),
        "w_out": f(inp["w_out"][0]), "w1": f(inp["w_mlp1"][0]), "w2": f(inp["w_mlp2"][0]),
    }
    in_maps = []
    for core in range(8):
        b, j = core // 4, core % 4
        lo = NOWN * j - 1024
        idx = np.arange(lo, lo + NW)
        ok = (idx >= 0) & (idx < NTOK)
        idc = np.clip(idx, 0, NTOK - 1)
        xTw = np.where(ok[None, :], inp["x"][b].T[:, idc], 0.0)
        sel = np.zeros((128, 32), np.float32)
        sel[32 * j + np.arange(32), np.arange(32)] = 1.0
        m = dict(shared)
        m.update({
            "xTw": f(xTw), "validw": f(ok.astype(np.float32).reshape(32, 128).T),
            "ropeCw": f(C[:, idc]), "ropeSw": f(S[:, idc]),
            "xT": f(inp["x"][b].T), "cvec": f(inp["c"][b].reshape(8, 128).T),
            "seld": sel, "x_tok": f(inp["x"][b, NOWN * j:NOWN * (j + 1)]),
        })
        in_maps.append(m)
    return in_maps


def kernel(**inputs):
    inp = {k: np.asarray(v) for k, v in inputs.items()}
    nc = build_fused()
    in_maps = fused_inputs(inp)
    res = run_bass_kernel_spmd(nc, in_maps, core_ids=list(range(8)))
    out = np.zeros((2, NTOK, 1024), np.float32)
    for core in range(8):
        b, j = core // 4, core % 4
        out[b, NOWN * j:NOWN * (j + 1)] = res.results[core]["out"]
    return out
```

```python
import math
import numpy as np
from concourse.bass_utils import run_bass_kernel_spmd

from contextlib import ExitStack
import concourse.bass as bass
import concourse.mybir as mybir

F32 = mybir.dt.float32
BF16 = mybir.dt.bfloat16
ALU = mybir.AluOpType
AF = mybir.ActivationFunctionType
AX = mybir.AxisListType

ENGS = ("pe", "act", "dve", "pool", "sp")


class Prog:
    def __init__(self, nc, sem_stack=None, prefix=""):
        self.nc = nc
        self.ops = []
        self.state = {}
        self.es = ExitStack()
        self.sem_stack = sem_stack if sem_stack is not None else self.es
        self.prefix = prefix
        self.ndma = 0

    def sb(self, name, shape, dt):
        return self.es.enter_context(self.nc.sbuf_tensor(self.prefix + "sb_" + name, list(shape), dt))

    def ps(self, name, shape, dt):
        return self.es.enter_context(self.nc.psum_tensor(self.prefix + "ps_" + name, list(shape), dt))

    def op(self, eng, fn, reads=(), writes=(), dma=False, grp=None):
        deps = set()
        psum_r = [b for b in reads if len(b) > 1 and b[0] == "p" and b[1].isupper()]
        if psum_r:
            reads = [b for b in reads if b not in psum_r]
            writes = list(writes) + [b for b in psum_r if b not in writes]
        for b in reads:
            st = self.state.setdefault(b, [None, []])
            if st[0] is not None:
                deps.add(st[0])
        for b in writes:
            st = self.state.setdefault(b, [None, []])
            if st[0] is not None:
                deps.add(st[0])
            deps.update(st[1])
        idx = len(self.ops)
        if dma and grp is None:
            grp = "dma%d" % (self.ndma % 12)
            self.ndma += 1
        self.ops.append(dict(eng=eng, fn=fn, deps=deps, dma=dma, grp=grp, sig=dma))
        for b in reads:
            self.state[b][1].append(idx)
        for b in writes:
            self.state[b] = [idx, []]
        return idx

    def dma(self, eng, out, in_, reads=(), writes=(), grp=None, key=None):
        if grp is None:
            grp = "dma_" + str(key if key is not None else (reads[0] if reads else writes[0]))
        return self.op(eng, lambda e: e.dma_start(out=out, in_=in_), reads, writes, dma=True, grp=grp)

    def build(self):
        nc = self.nc
        ops = self.ops

        def needs_wait(o, d):
            if d["dma"] or o["dma"]:
                return True
            if o["eng"] != d["eng"]:
                return True
            return o["eng"] != "pe"

        for o in ops:
            for di in o["deps"]:
                if needs_wait(o, ops[di]):
                    ops[di]["sig"] = True
        cnt = {}
        sems = {}

        def getsem(key):
            if key not in sems:
                sems[key] = self.sem_stack.enter_context(nc.semaphore(self.prefix + "s_" + str(key).replace(" ", "")))
            return sems[key]

        for o in ops:
            if not o["sig"]:
                continue
            if o["dma"]:
                key = o["grp"]
                cnt[key] = cnt.get(key, 0) + 16
                o["tok"] = (key, cnt[key], 16)
            else:
                base = "e_" + o["eng"]
                gen = cnt.get(base + "_gen", 0)
                key = "%s_%d" % (base, gen)
                cnt[key] = cnt.get(key, 0) + 1
                o["tok"] = (key, cnt[key], 1)
                if cnt[key] >= 30000:
                    cnt[base + "_gen"] = gen + 1
        for key in list(cnt.keys()):
            if not key.endswith("_gen"):
                getsem(key)
        per = {e: [] for e in ENGS}
        for o in ops:
            per[o["eng"]].append(o)
        final_dma = {k: v for k, v in cnt.items() if k.startswith("dma")}
        self.n_sems = len(sems)

        def replay(eng_name, e):
            waited = {}
            for o in per[eng_name]:
                for di in sorted(o["deps"]):
                    d = ops[di]
                    if not needs_wait(o, d):
                        continue
                    key, val, _ = d["tok"]
                    if waited.get(key, 0) < val:
                        e.wait_ge(sems[key], val)
                        waited[key] = val
                inst = o["fn"](e)
                if o["sig"]:
                    key, val, inc = o["tok"]
                    inst.then_inc(sems[key], inc)
            if eng_name == "sp":
                for key, val in final_dma.items():
                    if waited.get(key, 0) < val:
                        e.wait_ge(sems[key], val)

        with nc.Block() as block:
            @block.sync
            def _(e):
                replay("sp", e)

            @block.tensor
            def _(e):
                replay("pe", e)

            @block.scalar
            def _(e):
                replay("act", e)

            @block.vector
            def _(e):
                replay("dve", e)

            @block.gpsimd
            def _(e):
                replay("pool", e)
        self.es.close()


EPS = 1e-6


def _rs(P, eng_stats, ss, r, n, nm, reads, key):
    P.op("dve", lambda e: e.tensor_scalar(out=r, in0=ss, scalar1=1.0 / n, scalar2=EPS, op0=ALU.mult, op1=ALU.add),
         reads=reads, writes=[key])
    P.op("act", lambda e: e.activation(out=r, in_=r, func=AF.Sqrt), reads=[key], writes=[key])
    P.op("dve", lambda e: e.reciprocal(out=r, in_=r), reads=[key], writes=[key])


def build_phase2(NT=2048):
    nc = bass.Bass("TRN2", target_bir_lowering=False)
    P = Prog(nc)
    D = lambda name, shape, kind="ExternalInput": nc.dram_tensor(name, list(shape), F32, kind=kind).ap()
    x_tok = D("x_tok", [NT, 1024])
    mix_tok = D("mix_tok", [NT, 1024])
    mixT = D("mixT", [1024, NT])
    cvec = D("cvec", [128, 8])
    w_ada = D("w_ada", [1024, 4096])
    b_ada = D("b_ada", [1, 4096])
    gvec = D("gvec", [3, 1024])
    gmix = D("gmix", [128, 8])
    w_out = D("w_out", [1024, 1024])
    w1 = D("w1", [1024, 4096])
    w2 = D("w2", [4096, 1024])
    out = D("out", [NT, 1024], kind="ExternalOutput")
    x1s = D("x1s", [NT, 1024], kind="Internal")

    sb, ps = P.sb, P.ps
    wout_b = sb("wout_b", [128, 8, 1024], BF16)
    mods = sb("mods", [128, 4096], F32)
    sbc = sb("sbc", [128, 8, 128], F32)
    ones = sb("ones", [128, 128], F32)
    ident = sb("ident", [128, 128], BF16)
    identf = sb("identf", [128, 128], F32)
    stage = [sb("stage%d" % i, [128, 2048], F32) for i in range(2)]
    w1b = [sb("w1b%d" % i, [128, 8, 512], BF16) for i in range(2)]
    w2b = [sb("w2b%d" % i, [128, 4, 1024], BF16) for i in range(2)]
    h2T = sb("h2T", [128, 8, 1024], BF16)
    f2acc = sb("f2acc", [128, 8, 1024], F32)
    xt = [sb("xt%d" % i, [128, 1024], F32) for i in range(2)]
    mt = sb("mt", [128, 1024], F32)
    mTs = sb("mTs", [128, 8, 128], F32)
    mTb = [sb("mTb%d" % i, [128, 8, 128], BF16) for i in range(2)]
    ysb = sb("ysb", [128, 1024], F32)
    tmp = sb("tmp", [128, 1024], F32)
    tmp2 = sb("tmp2", [128, 1024], F32)
    h2b = sb("h2b", [128, 1024], BF16)
    rl = [sb("rl%d" % i, [128, 512], F32) for i in range(2)]
    aT = [sb("aT%d" % i, [128, 4, 512], BF16) for i in range(2)]
    small = sb("small", [128, 32], F32)
    csb = sb("csb", [128, 8], F32)
    gmx = sb("gmx", [128, 8], F32)
    brow = sb("brow", [1, 4096], F32)

    pA = ps("pA", [128, 1024], F32)
    pH = ps("pH", [128, 1024], F32)
    pT = ps("pT", [128, 1024], BF16)
    pM = [ps("pM%d" % i, [128, 512], F32) for i in range(2)]

    P.op("pool", lambda e: e.memset(ones[:, :], 1.0), writes=["ones"])
    P.op("pool", lambda e: e.memset(identf[:, :], 0.0), writes=["identf"])
    identd = D("identd", [128, 128])
    P.dma("sp", identf[:, :], identd[:, :], writes=["identf"])
    P.op("dve", lambda e: e.tensor_copy(out=ident[:, :], in_=identf[:, :]), reads=["identf"], writes=["ident"])
    P.dma("sp", csb[:, :], cvec[:, :], writes=["csb"])
    P.dma("sp", gmx[:, :], gmix[:, :], writes=["gmx"])
    P.dma("sp", brow[:, :], b_ada[:, :], writes=["brow"])
    P.op("act", lambda e: e.activation(out=csb[:, :], in_=csb[:, :], func=AF.Silu), reads=["csb"], writes=["csb"])
    for k in range(8):
        P.op("dve", lambda e, k=k: e.tensor_scalar(out=sbc[:, k, :], in0=ones[:, :], scalar1=csb[:, k:k + 1],
                                                    scalar2=None, op0=ALU.mult), reads=["ones", "csb"], writes=["sbc"])
    wa = w_ada.rearrange("(k p) n -> p k n", p=128)
    for blk in range(16):
        st = stage[blk % 2]
        skey = "stage%d" % (blk % 2)
        stv = st[:, :].rearrange("p (k n) -> p k n", k=8)
        P.dma("sp", stv, wa[:, :, blk * 256:(blk + 1) * 256], writes=[skey])
        pm = pM[blk % 2]
        pkey = "pM%d" % (blk % 2)
        for k in range(8):
            P.op("pe", lambda e, k=k, pm=pm, stv=stv: e.matmul(pm[:, 0:256], lhsT=sbc[:, k, :], rhs=stv[:, k, :],
                                                               start=(k == 0), stop=False),
                 reads=["sbc", skey], writes=[pkey])
        P.op("pe", lambda e, pm=pm, blk=blk: e.matmul(pm[:, 0:256], lhsT=ones[0:1, :], rhs=brow[0:1, blk * 256:(blk + 1) * 256],
                                                     start=False, stop=True), reads=["ones", "brow"], writes=[pkey])
        P.op("act", lambda e, pm=pm, blk=blk: e.activation(out=mods[:, blk * 256:(blk + 1) * 256], in_=pm[:, 0:256], func=AF.Copy),
             reads=[pkey], writes=["mods"])
    gb = stage[0][:, :]
    P.dma("sp", gb[:, 0:1024], gvec[0:1, :].to_broadcast((128, 1024)), writes=["stage0"])
    gb1 = stage[1][:, :]
    P.dma("sp", gb1[:, 0:1024], gvec[1:2, :].to_broadcast((128, 1024)), writes=["stage1"])
    P.dma("sp", gb1[:, 1024:2048], gvec[2:3, :].to_broadcast((128, 1024)), writes=["stage1"])
    G1, SH2, G2, G3 = mods[:, 0:1024], mods[:, 1024:2048], mods[:, 2048:3072], mods[:, 3072:4096]
    P.op("dve", lambda e: e.tensor_tensor(out=G1, in0=G1, in1=gb[:, 0:1024], op=ALU.mult), reads=["mods", "stage0"], writes=["mods"])
    P.op("dve", lambda e: e.scalar_tensor_tensor(out=G2, in0=G2, scalar=1.0, in1=gb1[:, 0:1024], op0=ALU.add, op1=ALU.mult),
         reads=["mods", "stage1"], writes=["mods"])
    P.op("dve", lambda e: e.tensor_tensor(out=G3, in0=G3, in1=gb1[:, 1024:2048], op=ALU.mult), reads=["mods", "stage1"], writes=["mods"])
    wo = w_out.rearrange("(k p) n -> p k n", p=128)
    for c4 in range(4):
        st = stage[c4 % 2]
        skey = "stage%d" % (c4 % 2)
        stv = st[:, :].rearrange("p (k n) -> p k n", k=2)
        P.dma("sp", stv, wo[:, 2 * c4:2 * c4 + 2, :], writes=[skey])
        for kk in range(2):
            k = 2 * c4 + kk
            P.op("dve" if kk == 0 else "pool", lambda e, k=k, kk=kk, stv=stv: e.tensor_scalar(
                out=wout_b[:, k, :], in0=stv[:, kk, :], scalar1=gmx[:, k:k + 1], scalar2=None, op0=ALU.mult),
                reads=[skey, "gmx"], writes=["wout_b"])

    def sumsq(src, dst, reads, key, eng="dve"):
        w = src.shape[-1]
        P.op("pool", lambda e: e.tensor_tensor(out=tmp2[:, 0:w], in0=src, in1=src, op=ALU.mult), reads=reads, writes=["tmp2"])
        P.op(eng, lambda e: e.tensor_reduce(out=dst, in_=tmp2[:, 0:w], axis=AX.X, op=ALU.add), reads=["tmp2"] + list(reads), writes=[key])

    nsm = [0]

    def smallcol(n=1):
        c = nsm[0] % 16
        nsm[0] += 1
        return small[:, 2 * c:2 * c + n], "small%d" % c

    w1v = w1.rearrange("(k p) n -> p k n", p=128)
    w2v = w2.rearrange("(c p) n -> p c n", p=128)
    mTv = mixT.rearrange("(k p) t -> p k t", p=128)
    it = [0]
    for half in range(NT // 1024):
        for tile in range(8):
            t0 = half * 1024 + tile * 128
            i = it[0]
            it[0] += 1
            X, xk = xt[i % 2], "xt%d" % (i % 2)
            MB, mbk = mTb[i % 2], "mTb%d" % (i % 2)
            P.dma("sp", X[:, :], x_tok[t0:t0 + 128, :], writes=[xk])
            P.dma("sp", mt[:, :], mix_tok[t0:t0 + 128, :], writes=["mt"])
            P.dma("sp", mTs[:, :, :], mTv[:, :, t0:t0 + 128], writes=["mTs"])
            P.op("pool", lambda e, MB=MB: e.tensor_copy(out=MB[:, :, :], in_=mTs[:, :, :]), reads=["mTs"], writes=[mbk])
            ss, ssk = smallcol(2)
            rr, rrk = smallcol(2)
            sumsq(mt[:, 0:512], ss[:, 0:1], ["mt"], ssk)
            sumsq(mt[:, 512:1024], ss[:, 1:2], ["mt", ssk], ssk)
            _rs(P, None, ss, rr, 512.0, "rmix", [ssk], rrk)
            for nh in range(2):
                for k in range(4):
                    P.op("pe", lambda e, k=k, nh=nh, MB=MB: e.matmul(pA[:, nh * 512:(nh + 1) * 512], lhsT=MB[:, k, :],
                                                                     rhs=wout_b[:, k, nh * 512:(nh + 1) * 512],
                                                                     start=(k == 0), stop=(k == 3)),
                         reads=[mbk, "wout_b"], writes=["pA%d" % nh])
                for k in range(4, 8):
                    P.op("pe", lambda e, k=k, nh=nh, MB=MB: e.matmul(pH[:, nh * 512:(nh + 1) * 512], lhsT=MB[:, k, :],
                                                                     rhs=wout_b[:, k, nh * 512:(nh + 1) * 512],
                                                                     start=(k == 4), stop=(k == 7)),
                         reads=[mbk, "wout_b"], writes=["pH%d" % nh])
            P.op("act", lambda e, rr=rr: e.activation(out=ysb[:, :], in_=pA[:, :], func=AF.Copy, scale=rr[:, 0:1]),
                 reads=["pA0", "pA1", rrk], writes=["ysb"])
            P.op("dve", lambda e, rr=rr: e.scalar_tensor_tensor(out=ysb[:, :], in0=pH[:, :], scalar=rr[:, 1:2], in1=ysb[:, :],
                                                                op0=ALU.mult, op1=ALU.add),
                 reads=["pH0", "pH1", rrk, "ysb"], writes=["ysb"])
            ss2, ss2k = smallcol(1)
            r2, r2k = smallcol(1)
            sumsq(ysb[:, :], ss2[:, 0:1], ["ysb"], ss2k)
            _rs(P, None, ss2, r2, 1024.0, "ry", [ss2k], r2k)
            P.op("dve", lambda e, r2=r2: e.scalar_tensor_tensor(out=tmp[:, :], in0=ysb[:, :], scalar=r2[:, 0:1], in1=G1,
                                                                op0=ALU.mult, op1=ALU.mult),
                 reads=["ysb", r2k, "mods"], writes=["tmp"])
            P.op("pool", lambda e, X=X: e.tensor_tensor(out=X[:, :], in0=tmp[:, :], in1=X[:, :], op=ALU.add),
                 reads=["tmp", xk], writes=[xk])
            P.dma("pool", x1s[t0:t0 + 128, :], X[:, :], reads=[xk], writes=["x1s%d" % (t0 // 128)])
            ss3, ss3k = smallcol(1)
            r3, r3k = smallcol(1)
            sumsq(X[:, :], ss3[:, 0:1], [xk], ss3k)
            _rs(P, None, ss3, r3, 1024.0, "r1", [ss3k], r3k)
            P.op("dve", lambda e, r3=r3, X=X: e.scalar_tensor_tensor(out=tmp[:, :], in0=X[:, :], scalar=r3[:, 0:1], in1=G2,
                                                                     op0=ALU.mult, op1=ALU.mult),
                 reads=[xk, r3k, "mods"], writes=["tmp"])
            P.op("pool", lambda e: e.tensor_tensor(out=h2b[:, :], in0=tmp[:, :], in1=SH2, op=ALU.add),
                 reads=["tmp", "mods"], writes=["h2b"])
            for k in range(8):
                P.op("pe", lambda e, k=k: e.transpose(out=pT[:, k * 128:(k + 1) * 128], in_=h2b[:, k * 128:(k + 1) * 128],
                                                      identity=ident[:, :]), reads=["h2b", "ident"], writes=["pT"])
            P.op("act", lambda e, tile=tile: e.activation(out=h2T[:, :, tile * 128:(tile + 1) * 128],
                                                          in_=pT[:, :].rearrange("p (k t) -> p k t", k=8), func=AF.Copy),
                 reads=["pT"], writes=["h2T"])
        for j in range(8):
            W1B, w1k = w1b[j % 2], "w1b%d" % (j % 2)
            W2B, w2k = w2b[j % 2], "w2b%d" % (j % 2)
            for hh in range(2):
                st, skey = stage[hh], "stage%d" % hh
                stv = st[:, :].rearrange("p (k n) -> p k n", k=8)
                P.dma("sp", stv, w1v[:, :, j * 512 + hh * 256:j * 512 + (hh + 1) * 256], writes=[skey])
                P.op("dve" if hh == 0 else "pool", lambda e, W1B=W1B, stv=stv, hh=hh: e.tensor_copy(
                    out=W1B[:, :, hh * 256:(hh + 1) * 256], in_=stv), reads=[skey], writes=[w1k])
            for hh in range(2):
                st, skey = stage[hh], "stage%d" % hh
                stv = st[:, :].rearrange("p (c n) -> p c n", c=2)
                P.dma("sp", stv, w2v[:, j * 4 + hh * 2:j * 4 + hh * 2 + 2, :], writes=[skey])
                P.op("dve" if hh == 0 else "pool", lambda e, W2B=W2B, stv=stv, hh=hh: e.tensor_copy(
                    out=W2B[:, hh * 2:hh * 2 + 2, :], in_=stv), reads=[skey], writes=[w2k])
            for tg in range(2):
                AT, atk = aT[tg], "aT%d" % tg
                for hc in range(4):
                    pm, pkey = pM[hc % 2], "pM%d" % (hc % 2)
                    RL, rlk = rl[hc % 2], "rl%d" % (hc % 2)
                    for k in range(8):
                        P.op("pe", lambda e, k=k, hc=hc, tg=tg, pm=pm, W1B=W1B: e.matmul(
                            pm[:, :], lhsT=W1B[:, k, hc * 128:(hc + 1) * 128], rhs=h2T[:, k, tg * 512:(tg + 1) * 512],
                            start=(k == 0), stop=(k == 7)), reads=[w1k, "h2T"], writes=[pkey])
                    P.op("act", lambda e, pm=pm, RL=RL: e.activation(out=RL[:, :], in_=pm[:, :], func=AF.Relu),
                         reads=[pkey], writes=[rlk])
                    P.op("pool", lambda e, RL=RL, AT=AT, hc=hc: e.tensor_tensor(out=AT[:, hc, :], in0=RL[:, :], in1=RL[:, :], op=ALU.mult),
                         reads=[rlk], writes=[atk])
                for tt in range(4):
                    tile = tg * 4 + tt
                    for nh in range(2):
                        pp, ppk = (pA, "pA%d" % nh) if tt % 2 == 0 else (pH, "pH%d" % nh)
                        for hc in range(4):
                            P.op("pe", lambda e, hc=hc, tt=tt, nh=nh, pp=pp, AT=AT, W2B=W2B: e.matmul(
                                pp[:, nh * 512:(nh + 1) * 512], lhsT=AT[:, hc, tt * 128:(tt + 1) * 128],
                                rhs=W2B[:, hc, nh * 512:(nh + 1) * 512], start=(hc == 0), stop=(hc == 3)),
                                reads=[atk, w2k], writes=[ppk])
                        fv = f2acc[:, tile, nh * 512:(nh + 1) * 512]
                        fk = "f2_%d_%d" % (tile, nh)
                        if j == 0:
                            P.op("dve", lambda e, fv=fv, pp=pp, nh=nh: e.tensor_copy(out=fv, in_=pp[:, nh * 512:(nh + 1) * 512]),
                                 reads=[ppk], writes=[fk])
                        else:
                            P.op("dve", lambda e, fv=fv, pp=pp, nh=nh: e.tensor_tensor(out=fv, in0=pp[:, nh * 512:(nh + 1) * 512], in1=fv, op=ALU.add),
                                 reads=[ppk, fk], writes=[fk])
        for tile in range(8):
            t0 = half * 1024 + tile * 128
            i = it[0]
            it[0] += 1
            X, xk = xt[i % 2], "xt%d" % (i % 2)
            P.dma("sp", X[:, :], x1s[t0:t0 + 128, :], reads=["x1s%d" % (t0 // 128)], writes=[xk], key=xk)
            fkeys = ["f2_%d_%d" % (tile, nh) for nh in range(2)]
            ss4, ss4k = smallcol(1)
            r4, r4k = smallcol(1)
            sumsq(f2acc[:, tile, :], ss4[:, 0:1], fkeys, ss4k)
            _rs(P, None, ss4, r4, 1024.0, "rf", [ss4k], r4k)
            P.op("dve", lambda e, r4=r4, tile=tile: e.scalar_tensor_tensor(out=tmp[:, :], in0=f2acc[:, tile, :], scalar=r4[:, 0:1], in1=G3,
                                                                          op0=ALU.mult, op1=ALU.mult),
                 reads=fkeys + [r4k, "mods"], writes=["tmp"])
            P.op("pool", lambda e, X=X: e.tensor_tensor(out=X[:, :], in0=tmp[:, :], in1=X[:, :], op=ALU.add),
                 reads=["tmp", xk], writes=[xk])
            P.dma("pool", out[t0:t0 + 128, :], X[:, :], reads=[xk], writes=["out%d" % (t0 // 128)])
    P.build()
    return nc


def run_phase2(inp, mixed):
    f = lambda a: np.ascontiguousarray(a, dtype=np.float32)
    nc = build_phase2()
    in_maps = []
    cols = np.r_[2048:3072, 3072:6144]
    for core in range(8):
        b, j = core // 4, core % 4
        sl = slice(j * 2048, (j + 1) * 2048)
        in_maps.append({
            "x_tok": f(inp["x"][b, sl]),
            "mix_tok": f(mixed[b, sl]),
            "mixT": f(mixed[b, sl].T),
            "cvec": f(inp["c"][b].reshape(8, 128).T),
            "w_ada": f(inp["w_ada"][0][:, cols]),
            "b_ada": f(inp["b_ada"][0][cols][None, :]),
            "gvec": f(np.stack([inp["g_post_mix"][0], inp["g_pre_mlp"][0], inp["g_post_mlp"][0]])),
            "gmix": f(np.concatenate([inp["g_attn_out"][0], inp["g_hyena_out"][0]]).reshape(8, 128).T),
            "w_out": f(inp["w_out"][0]),
            "w1": f(inp["w_mlp1"][0]),
            "w2": f(inp["w_mlp2"][0]),
            "identd": np.eye(128, dtype=np.float32),
        })
    res = run_bass_kernel_spmd(nc, in_maps, core_ids=list(range(8)))
    out = np.zeros((2, 8192, 1024), np.float32)
    for core in range(8):
        b, j = core // 4, core % 4
        out[b, j * 2048:(j + 1) * 2048] = res.results[core]["out"]
    return out


ST = 512
NTOK = 8192
NST = NTOK // ST


def _p1_common(P, nc, ncols_w, ST=512, SW=256):
    D = lambda name, shape, kind="ExternalInput": nc.dram_tensor(name, list(shape), F32, kind=kind).ap()
    H = {}
    H["xT"] = D("xT", [1024, NTOK])
    cvec = D("cvec", [128, 8])
    w_ada1 = D("w_ada1", [1024, 2048])
    b_ada1 = D("b_ada1", [128, 16])
    gpre = D("gpre", [128, 8])
    w_inc = D("w_inc", [1024, ncols_w])
    sb, ps = P.sb, P.ps
    H["wb"] = wb = sb("wb", [128, 8, ncols_w], BF16)
    stage = [sb("stage%d" % i, [128, 8, SW], F32) for i in range(2)]
    H["ST"] = ST
    H["stage"] = stage
    csb = sb("csb", [128, 8], F32)
    gp = sb("gp", [128, 8], F32)
    bsb = sb("bsb", [128, 16], F32)
    H["modc"] = modc = sb("modc", [128, 16], F32)
    H["G0"] = G0 = sb("G0", [128, 8], F32)
    H["onesb"] = onesb = sb("onesb", [128, 128], BF16)
    H["xs"] = [sb("xs%d" % i, [128, 8, ST], F32) for i in range(2)]
    H["sq"] = sb("sq", [128, 8, ST], BF16)
    H["hT"] = sb("hT", [128, 8, ST], BF16)
    H["rbc"] = sb("rbc", [128, ST], F32)
    H["pSS"] = pSS = ps("pSS", [128, 512], F32)

    P.op("pool", lambda e: e.memset(onesb[:, :], 1.0), writes=["onesb"])
    P.dma("sp", csb[:, :], cvec[:, :], writes=["csb"])
    P.dma("sp", gp[:, :], gpre[:, :], writes=["gp"])
    P.dma("sp", bsb[:, :], b_ada1[:, :], writes=["bsb"])
    P.op("act", lambda e: e.activation(out=csb[:, :], in_=csb[:, :], func=AF.Silu), reads=["csb"], writes=["csb"])
    wa = w_ada1.rearrange("(k p) n -> p k n", p=128)
    for blk in range(2048 // SW):
        st, skey = stage[blk % 2], "stage%d" % (blk % 2)
        P.dma("sp", st[:, :, :], wa[:, :, blk * SW:(blk + 1) * SW], writes=[skey])
        for jj in range(SW // 128):
            j = (SW // 128) * blk + jj
            for k in range(8):
                P.op("pe", lambda e, k=k, j=j, jj=jj, st=st: e.matmul(pSS[:, j:j + 1], lhsT=st[:, k, jj * 128:(jj + 1) * 128],
                                                                      rhs=csb[:, k:k + 1], start=(k == 0), stop=(k == 7)),
                     reads=[skey, "csb"], writes=["pSS"])
    P.op("dve", lambda e: e.tensor_tensor(out=modc[:, :], in0=pSS[:, 0:16], in1=bsb[:, :], op=ALU.add),
         reads=["pSS", "bsb"], writes=["modc"])
    P.op("dve", lambda e: e.scalar_tensor_tensor(out=G0[:, :], in0=modc[:, 8:16], scalar=1.0, in1=gp[:, :], op0=ALU.add, op1=ALU.mult),
         reads=["modc", "gp"], writes=["G0"])
    wv = w_inc.rearrange("(k p) n -> p k n", p=128)
    nb = ncols_w // SW
    for blk in range(nb):
        st, skey = stage[blk % 2], "stage%d" % (blk % 2)
        P.dma("sp", st[:, :, :], wv[:, :, blk * SW:(blk + 1) * SW], writes=[skey])
        P.op("dve" if blk % 2 == 0 else "pool", lambda e, st=st, blk=blk: e.tensor_copy(out=wb[:, :, blk * SW:(blk + 1) * SW], in_=st[:, :, :]),
             reads=[skey], writes=["wb"])
    return H


def _p1_load(P, H, st):
    ST = H["ST"]
    xs, xk = H["xs"][st % 2], "xs%d" % (st % 2)
    xTv = H["xT"].rearrange("(k p) t -> p k t", p=128)
    P.dma("sp", xs[:, :, :], xTv[:, :, st * ST:(st + 1) * ST], writes=[xk])


def _p1_norm(P, H, st):
    ST = H["ST"]
    xs, xk = H["xs"][st % 2], "xs%d" % (st % 2)
    sq, hT, rbc, pSS, onesb, modc, G0 = H["sq"], H["hT"], H["rbc"], H["pSS"], H["onesb"], H["modc"], H["G0"]
    P.op("act", lambda e: e.activation(out=sq[:, :, :], in_=xs[:, :, :], func=AF.Square), reads=[xk], writes=["sq"])
    for k in range(8):
        P.op("pe", lambda e, k=k: e.matmul(pSS[:, 0:ST], lhsT=onesb[:, :], rhs=sq[:, k, :], start=(k == 0), stop=(k == 7)),
             reads=["onesb", "sq"], writes=["pSS"])
    P.op("dve", lambda e: e.tensor_scalar(out=rbc[:, :], in0=pSS[:, 0:ST], scalar1=1.0 / 1024, scalar2=EPS, op0=ALU.mult, op1=ALU.add),
         reads=["pSS"], writes=["rbc"])
    P.op("act", lambda e: e.activation(out=rbc[:, :], in_=rbc[:, :], func=AF.Sqrt), reads=["rbc"], writes=["rbc"])
    P.op("dve", lambda e: e.reciprocal(out=rbc[:, :], in_=rbc[:, :]), reads=["rbc"], writes=["rbc"])
    for k in range(8):
        P.op("dve", lambda e, k=k: e.scalar_tensor_tensor(out=xs[:, k, :], in0=xs[:, k, :], scalar=G0[:, k:k + 1], in1=rbc[:, :],
                                                          op0=ALU.mult, op1=ALU.mult), reads=[xk, "G0", "rbc"], writes=[xk])
        P.op("act", lambda e, k=k: e.activation(out=hT[:, k, :], in_=xs[:, k, :], func=AF.Identity, bias=modc[:, k:k + 1], scale=1.0),
             reads=[xk, "modc"], writes=["hT"])


def build_attn():
    nc = bass.Bass("TRN2", target_bir_lowering=False)
    P = Prog(nc)
    D = lambda name, shape, kind="ExternalInput": nc.dram_tensor(name, list(shape), F32, kind=kind).ap()
    H = _p1_common(P, nc, 768)
    ropeC = D("ropeC", [128, NTOK])
    ropeS = D("ropeS", [128, NTOK])
    maskd = D("maskd", [128, 17 * 128])
    hseld = D("hseld", [128, 64])
    attn_o = D("attn_o", [NTOK, 128], kind="ExternalOutput")
    sb, ps = P.sb, P.ps
    wb, hT = H["wb"], H["hT"]
    QT = sb("QT", [128, NTOK], BF16)
    KT = sb("KT", [128, NTOK], BF16)
    Vaug = sb("Vaug", [128, 64, 2, 65], BF16)
    Mall = sb("Mall", [128, 17 * 128], BF16)
    mst = sb("mst", [128, 17 * 128], F32)
    hself = sb("hself", [128, 64], F32)
    hsel = sb("hsel", [128, 64], BF16)
    onesrow = sb("onesrow", [64, 128], BF16)
    rc = [sb("rc%d" % i, [128, ST], F32) for i in range(2)]
    rs_ = [sb("rs%d" % i, [128, ST], F32) for i in range(2)]
    t1 = sb("t1", [128, ST], F32)
    t2 = sb("t2", [128, ST], F32)
    sqk = sb("sqk", [128, ST], BF16)
    kmx = sb("kmx", [64, 2], F32)
    qn = sb("qn", [64, 128], F32)
    negm = sb("negm", [64, 128], BF16)
    PT = [sb("PT%d" % i, [128, 512], BF16) for i in range(2)]
    ao = [sb("ao%d" % i, [128, 128], F32) for i in range(2)]
    rec = sb("rec", [128, 4], F32)
    pA = ps("pA", [128, 512], F32)
    pB = ps("pB", [128, 512], F32)
    pV = ps("pV", [128, 512], F32)
    pN = H["pSS"]
    pS = [ps("pS%d" % i, [128, 512], F32) for i in range(2)]
    pO = [ps("pO%d" % i, [128, 2, 128], F32) for i in range(2)]

    P.dma("sp", mst[:, :], maskd[:, :], writes=["mst"])
    P.op("pool", lambda e: e.tensor_copy(out=Mall[:, :], in_=mst[:, :]), reads=["mst"], writes=["Mall"])
    P.dma("sp", hself[:, :], hseld[:, :], writes=["hself"])
    P.op("pool", lambda e: e.tensor_copy(out=hsel[:, :], in_=hself[:, :]), reads=["hself"], writes=["hsel"])
    P.op("pool", lambda e: e.memset(onesrow[:, :], 1.0), writes=["onesrow"])
    P.op("pool", lambda e: e.memset(kmx[:, :], 0.0), writes=["kmx"])
    P.op("pool", lambda e: e.memset(Vaug[:, :, :, 64:65], 1.0), writes=["Vaug"])

    _p1_load(P, H, 0)
    for st in range(NST):
        if st + 1 < NST:
            _p1_load(P, H, st + 1)
        C, ck = rc[st % 2], "rc%d" % (st % 2)
        S, sk = rs_[st % 2], "rs%d" % (st % 2)
        P.dma("sp", C[:, :], ropeC[:, st * ST:(st + 1) * ST], writes=[ck])
        P.dma("sp", S[:, :], ropeS[:, st * ST:(st + 1) * ST], writes=[sk])
        _p1_norm(P, H, st)
        for which, dst in ((0, QT), (1, KT)):
            dk = "QT" if which == 0 else "KT"
            c0 = which * 256
            for k in range(8):
                P.op("pe", lambda e, k=k, c0=c0: e.matmul(pA[:, :], lhsT=wb[:, k, c0:c0 + 128], rhs=hT[:, k, :], start=(k == 0), stop=(k == 7)),
                     reads=["wb", "hT"], writes=["pA"])
            for k in range(8):
                P.op("pe", lambda e, k=k, c0=c0: e.matmul(pB[:, :], lhsT=wb[:, k, c0 + 128:c0 + 256], rhs=hT[:, k, :], start=(k == 0), stop=(k == 7)),
                     reads=["wb", "hT"], writes=["pB"])
            P.op("dve", lambda e, C=C: e.tensor_tensor(out=t1[:, :], in0=pA[:, :], in1=C[:, :], op=ALU.mult), reads=["pA", ck], writes=["t1"])
            P.op("dve", lambda e, S=S: e.tensor_tensor(out=t2[:, :], in0=pB[:, :], in1=S[:, :], op=ALU.mult), reads=["pB", sk], writes=["t2"])
            P.op("pool", lambda e, dst=dst, st=st: e.tensor_tensor(out=dst[:, st * ST:(st + 1) * ST], in0=t1[:, :], in1=t2[:, :], op=ALU.add),
                 reads=["t1", "t2"], writes=[dk])
        P.op("pool", lambda e, st=st: e.tensor_tensor(out=sqk[:, :], in0=KT[:, st * ST:(st + 1) * ST], in1=KT[:, st * ST:(st + 1) * ST], op=ALU.mult),
             reads=["KT"], writes=["sqk"])
        P.op("pe", lambda e: e.matmul(pN[0:64, :], lhsT=hsel[:, :], rhs=sqk[:, :], start=True, stop=True), reads=["hsel", "sqk"], writes=["pSS"])
        P.op("dve", lambda e: e.tensor_reduce(out=kmx[:, 1:2], in_=pN[0:64, :], axis=AX.X, op=ALU.max), reads=["pSS", "kmx"], writes=["kmx"])
        P.op("dve", lambda e: e.tensor_tensor(out=kmx[:, 0:1], in0=kmx[:, 0:1], in1=kmx[:, 1:2], op=ALU.max), reads=["kmx"], writes=["kmx"])
        for tt in range(4):
            for k in range(8):
                P.op("pe", lambda e, k=k, tt=tt: e.matmul(pV[:, tt * 128:(tt + 1) * 128], lhsT=hT[:, k, tt * 128:(tt + 1) * 128],
                                                          rhs=wb[:, k, 512:640], start=(k == 0), stop=(k == 7)),
                     reads=["wb", "hT"], writes=["pV"])
        P.op("act", lambda e, st=st: e.activation(out=Vaug[:, st * 4:(st + 1) * 4, :, 0:64],
                                                  in_=pV[:, :].rearrange("p (t h d) -> p t h d", t=4, h=2), func=AF.Copy),
             reads=["pV"], writes=["Vaug"])
    P.op("act", lambda e: e.activation(out=kmx[:, 0:1], in_=kmx[:, 0:1], func=AF.Sqrt), reads=["kmx"], writes=["kmx"])
    cnt = [0]
    for jb in range(64):
        qs = slice(jb * 128, (jb + 1) * 128)
        P.op("pool", lambda e, qs=qs: e.tensor_tensor(out=sqk[:, 0:128], in0=QT[:, qs], in1=QT[:, qs], op=ALU.mult), reads=["QT"], writes=["sqk"])
        P.op("pe", lambda e: e.matmul(pN[0:64, 0:128], lhsT=hsel[:, :], rhs=sqk[:, 0:128], start=True, stop=True), reads=["hsel", "sqk"], writes=["pSS"])
        P.op("act", lambda e: e.activation(out=qn[:, :], in_=pN[0:64, 0:128], func=AF.Sqrt), reads=["pSS"], writes=["qn"])
        P.op("dve", lambda e: e.tensor_scalar(out=negm[:, :], in0=qn[:, :], scalar1=kmx[:, 0:1], scalar2=-1.0, op0=ALU.mult, op1=ALU.mult),
             reads=["qn", "kmx"], writes=["negm"])
        AO, aok = ao[jb % 2], "ao%d" % (jb % 2)
        PO, pok = pO[jb % 2], "pO%d" % (jb % 2)
        for h in range(2):
            hs = slice(64 * h, 64 * h + 64)
            dms = [dm for dm in range(-8, 9) if 0 <= jb + dm < 64]
            groups = [dms[i:i + 4] for i in range(0, len(dms), 4)]
            nmm = 0
            for grp in groups:
                g = cnt[0]
                cnt[0] += 1
                psx, psk = pS[g % 2], "pS%d" % (g % 2)
                ptx, ptk = PT[g % 2], "PT%d" % (g % 2)
                n = len(grp)
                for i, dm in enumerate(grp):
                    kc = jb + dm
                    P.op("pe", lambda e, i=i, kc=kc, hs=hs, qs=qs, psx=psx: e.matmul(psx[:, i * 128:(i + 1) * 128], lhsT=KT[hs, kc * 128:(kc + 1) * 128],
                                                                                      rhs=QT[hs, qs], start=True, stop=False),
                         reads=["KT", "QT"], writes=[psk])
                    P.op("pe", lambda e, i=i, h=h, psx=psx: e.matmul(psx[:, i * 128:(i + 1) * 128], lhsT=onesrow[32 * h:32 * h + 1, :],
                                                                     rhs=negm[32 * h:32 * h + 1, :], start=False, stop=True),
                         reads=["onesrow", "negm"], writes=[psk])
                P.op("act", lambda e, psx=psx, ptx=ptx, n=n: e.activation(out=ptx[:, 0:n * 128], in_=psx[:, 0:n * 128], func=AF.Exp, scale=0.125),
                     reads=[psk], writes=[ptk])
                m0 = (grp[0] + 8) * 128
                P.op("dve" if g % 2 == 0 else "pool", lambda e, ptx=ptx, n=n, m0=m0: e.tensor_tensor(
                    out=ptx[:, 0:n * 128], in0=ptx[:, 0:n * 128], in1=Mall[:, m0:m0 + n * 128], op=ALU.mult),
                    reads=[ptk, "Mall"], writes=[ptk])
                for i, dm in enumerate(grp):
                    kc = jb + dm
                    P.op("pe", lambda e, i=i, kc=kc, h=h, ptx=ptx, PO=PO, first=(nmm == 0), last=(nmm == len(dms) - 1): e.matmul(
                        PO[:, h, 0:65], lhsT=ptx[:, i * 128:(i + 1) * 128], rhs=Vaug[:, kc, h, :], start=first, stop=last),
                        reads=[ptk, "Vaug"], writes=[pok])
                    nmm += 1
            P.op("dve", lambda e, h=h, PO=PO: e.reciprocal(out=rec[:, h:h + 1], in_=PO[:, h, 64:65]), reads=[pok], writes=["rec%d" % h])
            P.op("dve", lambda e, h=h, PO=PO, AO=AO: e.tensor_scalar(out=AO[:, 64 * h:64 * h + 64], in0=PO[:, h, 0:64], scalar1=rec[:, h:h + 1],
                                                                     scalar2=None, op0=ALU.mult),
                 reads=[pok, "rec%d" % h], writes=[aok])
        P.dma("pool", attn_o[qs, :], AO[:, :], reads=[aok], writes=["attn_o%d" % jb])
    P.build()
    return nc


def _rope_tables():
    half = 32
    inv = (10000.0 ** (-np.arange(half, dtype=np.float32) / half)).astype(np.float32)
    pos = np.arange(NTOK, dtype=np.float32)
    ang = (pos[:, None] * inv[None, :]).astype(np.float32)
    cos = np.cos(ang).astype(np.float32).T
    sin = np.sin(ang).astype(np.float32).T
    C = np.concatenate([cos, cos, cos, cos], axis=0)
    S = np.concatenate([-sin, sin, -sin, sin], axis=0)
    return np.ascontiguousarray(C), np.ascontiguousarray(S)


def _attn_masks():
    o = np.arange(-8 * 128 - 127, 8 * 128 + 128)
    mult = ((np.abs(o) <= 64).astype(np.float32) + ((np.abs(o) <= 256) & (o % 4 == 0)).astype(np.float32)
            + ((np.abs(o) <= 1024) & (o % 16 == 0)).astype(np.float32))
    off0 = -(8 * 128 + 127)
    M = np.zeros((128, 17, 128), np.float32)
    kl = np.arange(128)[:, None]
    ql = np.arange(128)[None, :]
    for i, dm in enumerate(range(-8, 9)):
        M[:, i, :] = mult[(128 * dm + kl - ql) - off0]
    return M.reshape(128, 17 * 128)


def _perm_cols(w):
    w = w.reshape(w.shape[0], -1, 2, 32)
    return w[:, :, ::-1, :].reshape(w.shape[0], -1)


def run_attn(inp):
    f = lambda a: np.ascontiguousarray(a, dtype=np.float32)
    nc = build_attn()
    C, S = _rope_tables()
    M = _attn_masks()
    hsel = np.zeros((128, 64), np.float32)
    hsel[0:64, 0] = 1.0
    hsel[64:128, 32] = 1.0
    w_in = inp["w_in"][0]
    in_maps = []
    for core in range(8):
        b, g = core // 4, core % 4
        cs = slice(128 * g, 128 * g + 128)
        wq, wk, wv = w_in[:, 0:512][:, cs], w_in[:, 512:1024][:, cs], w_in[:, 1024:1536][:, cs]
        w_inc = np.concatenate([wq, _perm_cols(wq), wk, _perm_cols(wk), wv, np.zeros((1024, 128), np.float32)], axis=1)
        in_maps.append({
            "xT": f(inp["x"][b].T), "cvec": f(inp["c"][b].reshape(8, 128).T),
            "w_ada1": f(inp["w_ada"][0][:, 0:2048]), "b_ada1": f(inp["b_ada"][0][0:2048].reshape(16, 128).T),
            "gpre": f(inp["g_pre_mix"][0].reshape(8, 128).T), "w_inc": f(w_inc),
            "ropeC": C, "ropeS": S, "maskd": f(M), "hseld": hsel,
        })
    res = run_bass_kernel_spmd(nc, in_maps, core_ids=list(range(8)))
    attn = np.zeros((2, NTOK, 512), np.float32)
    for core in range(8):
        b, g = core // 4, core % 4
        attn[b, :, 128 * g:128 * g + 128] = res.results[core]["attn_o"]
    return attn


NFFT = 16384
NPQ = 4
NPQH = 8


def _hy_consts():
    N = NFFT
    a = np.arange(128)[:, None].astype(np.float64)
    k1 = np.arange(256)[None, :].astype(np.float64)
    F1cat = np.concatenate([np.cos(2 * np.pi * a * k1 / 256), -np.sin(2 * np.pi * a * k1 / 256)], 1)
    th = 2 * np.pi / N
    pm = (np.arange(128) % 64)[:, None].astype(np.float64)
    Tr, Ti = np.cos(th * pm * k1), -np.sin(th * pm * k1)
    TrTr, TiTi = np.concatenate([Tr, Tr], 1), np.concatenate([Ti, Ti], 1)
    pp = np.arange(64)[:, None].astype(np.float64)
    k2 = np.arange(64)[None, :].astype(np.float64)
    g2r, g2i = np.cos(2 * np.pi * pp * k2 / 64), -np.sin(2 * np.pi * pp * k2 / 64)
    Z0 = np.zeros((64, 64))
    G2r = np.block([[g2r, Z0], [Z0, g2r]])
    G2i = np.block([[g2i, Z0], [Z0, g2i]])
    R12 = np.concatenate([G2r, -G2i, G2i, G2r], 1)
    k1c = np.arange(256)[:, None].astype(np.float64)
    pcol = (np.arange(128) % 64)[None, :].astype(np.float64)
    T2r, T2i = np.cos(th * pcol * k1c), np.sin(th * pcol * k1c)
    TT2r = np.concatenate([T2r[0:128], T2r[0:128], T2r[128:256], T2r[128:256]], 1)
    TT2i = np.concatenate([T2i[0:128], T2i[0:128], T2i[128:256], T2i[128:256]], 1)
    aa = np.arange(128)[None, :].astype(np.float64)
    IF1c, IF1s = np.cos(2 * np.pi * aa * k1c / 256), -np.sin(2 * np.pi * aa * k1c / 256)
    IF1 = np.concatenate([IF1c[0:128], IF1s[0:128], IF1c[128:256], IF1s[128:256]], 1)
    t_lin = np.linspace(0.0, 1.0, 8192, dtype=np.float32)
    Swin = -t_lin.reshape(128, 64)
    L = 8192
    t = np.linspace(0.0, 1.0, L, dtype=np.float32)[:, None]
    w = (np.float32(2.0 * math.pi) * np.arange(L, dtype=np.float32)[:, None] / np.float32(L)).astype(np.float32)
    f = np.linspace(1e-4, 15, 16, dtype=np.float32)[None, :]
    zemb = np.concatenate([t, np.cos(f * w), -np.sin(f * w)], axis=-1).astype(np.float32)
    max_decay = math.log(1e-2) / 0.3
    min_decay = math.log(1e-2) / 1.5
    deltas = np.abs(np.linspace(min_decay, max_decay, 512, dtype=np.float32)).astype(np.float32)
    f32 = lambda x: np.ascontiguousarray(x, dtype=np.float32)
    return dict(F1cat=f32(F1cat), TrTr=f32(TrTr), TiTi=f32(TiTi), R12=f32(R12), TT2r=f32(TT2r), TT2i=f32(TT2i),
                IF1=f32(IF1), Swin=f32(Swin), zembT=f32(zemb.T), deltas=deltas)


def _hy_consts_h():
    N = NFFT
    a = np.arange(128)[:, None].astype(np.float64)
    k1 = np.arange(128)[None, :].astype(np.float64) + 0.5
    F1cat = np.concatenate([np.cos(2 * np.pi * a * k1 / 256), -np.sin(2 * np.pi * a * k1 / 256)], 1)
    th = 2 * np.pi / N
    pm = (np.arange(128) % 64)[:, None].astype(np.float64)
    Tr, Ti = np.cos(th * pm * k1), -np.sin(th * pm * k1)
    TrTr, TiTi = np.concatenate([Tr] * 4, 1), np.concatenate([Ti] * 4, 1)
    k1c = (np.arange(128)[:, None].astype(np.float64) + 0.5)
    pcol = (np.arange(128) % 64)[None, :].astype(np.float64)
    T2r, T2i = np.cos(th * pcol * k1c), np.sin(th * pcol * k1c)
    TT2r, TT2i = np.concatenate([T2r] * 4, 1), np.concatenate([T2i] * 4, 1)
    aa = np.arange(128)[None, :].astype(np.float64)
    IF1 = np.concatenate([np.cos(2 * np.pi * aa * k1c / 256), -np.sin(2 * np.pi * aa * k1c / 256)], 1)
    f32 = lambda x: np.ascontiguousarray(x, dtype=np.float32)
    return dict(F1cat_h=f32(F1cat), TrTr_h=f32(TrTr), TiTi_h=f32(TiTi), TT2r_h=f32(TT2r), TT2i_h=f32(TT2i), IF1_h=f32(IF1))


def build_hyena(stop=99):
    nc = bass.Bass("TRN2", target_bir_lowering=False)
    P = Prog(nc)
    D = lambda name, shape, kind="ExternalInput": nc.dram_tensor(name, list(shape), F32, kind=kind).ap()
    ST = 256
    H = _p1_common(P, nc, 512, ST=ST, SW=128)
    sb, ps = P.sb, P.ps
    wb, hT, pSS = H["wb"], H["hT"], H["pSS"]
    dF1cat, dTrTr, dTiTi, dR12 = D("F1cat", [128, 512]), D("TrTr", [128, 512]), D("TiTi", [128, 512]), D("R12", [128, 512])
    dTT2r, dTT2i, dIF1, dSwin = D("TT2r", [128, 512]), D("TT2i", [128, 512]), D("IF1", [128, 512]), D("Swin", [128, 64])
    dzemb = D("zembT", [33, NTOK])
    ddelta = D("delta", [128, 128])
    dhbcol = D("hbcol", [128, 128])
    dfw1, dfw2, dfw3, dfpar = D("fw1", [33, 64]), D("fw2", [64, 64]), D("fw3c", [64, 512]), D("fpar", [64, 4])
    dcw, dcb = D("cw", [128, 9]), D("cb", [128, 3])
    hy_oz = D("hy_oz", [128, 128, 64], kind="ExternalOutput")

    U = [sb("U%d" % i, [128, NTOK + 2], BF16) for i in range(3)]
    CV = sb("CV", [128, NTOK], BF16)
    Ob = sb("Ob", [128, NTOK], BF16)
    h2T = sb("h2T", [64, NTOK], BF16)
    Ap = sb("Ap", [128, 2, NPQ, 256], BF16)
    Hs = sb("Hs", [128, 2, NPQ, 256], BF16)
    Yb = sb("Yb", [128, 2, NPQ, 256], BF16)
    Bp = sb("Bp", [128, 2, 2, NPQ, 128], BF16)
    W1 = [sb("W1_%d" % i, [128, 512], F32) for i in range(2)]
    W2 = [sb("W2_%d" % i, [128, 512], F32) for i in range(2)]
    cpy = [sb("cpy%d" % i, [128, 512], F32) for i in range(2)]
    cpy2 = sb("cpy2", [128, 512], F32)
    tmpc = sb("tmpc", [128, 1024], F32)
    arg = tmpc[0:64, 0:512]
    argi = sb("argi", [64, 512], mybir.dt.int32)
    h1 = tmpc[0:64, 512:1024]
    F1b = sb("F1b", [128, 512], BF16)
    R12b = sb("R12b", [128, 512], BF16)
    IF1b = sb("IF1b", [128, 512], BF16)
    TrTr, TiTi = sb("TrTr", [128, 512], F32), sb("TiTi", [128, 512], F32)
    TT2r, TT2i = sb("TT2r", [128, 512], F32), sb("TT2i", [128, 512], F32)
    Swin = sb("Swin", [128, 64], F32)
    delta = sb("delta", [128, 128], F32)
    hbcol = sb("hbcol", [128, 128], F32)
    fw1, fw2 = sb("fw1", [33, 64], F32), sb("fw2", [64, 64], F32)
    fw3b = sb("fw3b", [64, 512], BF16)
    fpar = sb("fpar", [64, 8], F32)
    cw, cb = sb("cw", [128, 9], F32), sb("cb", [128, 3], F32)
    zc = [H["stage"][i][0:33, 0:4, :].rearrange("p k n -> p (k n)") for i in range(2)]
    win = sb("win", [128, 128], F32)
    fwbw = sb("fwbw", [128, 2, 128], F32)
    ot = [H["xs"][i][:, 0:2, :].rearrange("p k n -> p (k n)") for i in range(2)]

    pA1 = [ps("pA1_%d" % i, [128, 512], F32) for i in range(2)]
    pXr, pXi = ps("pXr", [128, 512], F32), ps("pXi", [128, 512], F32)
    pB = ps("pB", [128, 512], F32)
    pY = ps("pY", [128, 512], F32)
    pTz = ps("pTz", [128, 8, 128], BF16)
    ident = sb("ident", [128, 128], BF16)
    identf = W1[0]
    didn = D("identd", [128, 128])

    def ldcast(dst, dkey, src, n=512, parts=128):
        P.dma("sp", cpy2[0:parts, 0:n], src, writes=["cpy2"])
        P.op("dve", lambda e: e.tensor_copy(out=dst, in_=cpy2[0:parts, 0:n]), reads=["cpy2"], writes=[dkey])
    ldcast(F1b[:, :], "F1b", dF1cat[:, :])
    ldcast(R12b[:, :], "R12b", dR12[:, :])
    ldcast(IF1b[:, :], "IF1b", dIF1[:, :])
    ldcast(ident[:, :], "ident", didn[:, :], n=128)
    ldcast(fw3b[:, :], "fw3b", dfw3[:, :], parts=64)
    for dst, key, src in ((TrTr, "TrTr", dTrTr), (TiTi, "TiTi", dTiTi), (TT2r, "TT2r", dTT2r), (TT2i, "TT2i", dTT2i), (Swin, "Swin", dSwin),
                          (delta, "delta", ddelta), (hbcol, "hbcol", dhbcol), (fw1, "fw1", dfw1), (fw2, "fw2", dfw2), (cw, "cw", dcw), (cb, "cb", dcb)):
        P.dma("sp", dst[:, :], src[:, :], writes=[key])
    P.dma("sp", fpar[:, 0:4], dfpar[:, :], writes=["fpar"])
    i2p = 1.0 / (2.0 * math.pi)
    for (bc, fc, o0) in ((0, 1, 4), (2, 3, 6)):
        P.op("dve", lambda e, bc=bc, fc=fc, o0=o0: e.tensor_tensor(out=fpar[:, o0 + 1:o0 + 2], in0=fpar[:, bc:bc + 1], in1=fpar[:, fc:fc + 1], op=ALU.mult),
             reads=["fpar"], writes=["fpar"])
        P.op("dve", lambda e, o0=o0: e.tensor_scalar(out=fpar[:, o0 + 1:o0 + 2], in0=fpar[:, o0 + 1:o0 + 2], scalar1=i2p, scalar2=16.0, op0=ALU.mult, op1=ALU.add),
             reads=["fpar"], writes=["fpar"])
        P.op("dve", lambda e, fc=fc, o0=o0: e.tensor_scalar(out=fpar[:, o0:o0 + 1], in0=fpar[:, fc:fc + 1], scalar1=i2p, scalar2=None, op0=ALU.mult),
             reads=["fpar"], writes=["fpar"])
    for i in range(3):
        P.op("pool", lambda e, i=i: e.memset(U[i][:, 0:1], 0.0), writes=["U%d" % i])
        P.op("pool", lambda e, i=i: e.memset(U[i][:, NTOK + 1:NTOK + 2], 0.0), writes=["U%d" % i])

    nst = NTOK // ST
    _p1_load(P, H, 0)
    pU = [pXr, pXi, pY]
    for st in range(nst):
        if st + 1 < nst:
            _p1_load(P, H, st + 1)
        _p1_norm(P, H, st)
        for i in range(3):
            pk = ["pXr", "pXi", "pY"][i]
            for k in range(8):
                P.op("pe", lambda e, k=k, i=i: e.matmul(pU[i][:, 0:ST], lhsT=wb[:, k, i * 128:(i + 1) * 128], rhs=hT[:, k, :],
                                                        start=(k == 0), stop=(k == 7)), reads=["wb", "hT"], writes=[pk])
            P.op("act", lambda e, i=i, st=st: e.activation(out=U[i][:, 1 + st * ST:1 + (st + 1) * ST], in_=pU[i][:, 0:ST], func=AF.Copy),
                 reads=[pk], writes=["U%d" % i])

    if stop <= 1:
        P.dma("pool", hy_oz[:, 0:4, :], U[0][:, 1:257].rearrange("a (c p) -> a c p", p=64).bitcast(F32) if False else H["xs"][0][:, 0, :].rearrange("a (c p) -> a c p", p=64), reads=["U0", "U1", "U2", "xs0"], writes=["dbg"])
        P.build()
        return nc
    Zkeys = lambda i: ["Z%d_%d" % (i, s) for s in range(128 // (2 * NPQ))]
    Z = [U[i][:, 0:NTOK].rearrange("a (c p) -> a c p", p=64) for i in range(3)]
    CVs = CV[:, :].rearrange("c (a p) -> c a p", p=64)
    for i in range(3):
        for ch in range(8):
            j0 = ch * 1024
            P.op("dve", lambda e, i=i, j0=j0: e.tensor_scalar(out=tmpc[:, :], in0=U[i][:, j0:j0 + 1024], scalar1=cw[:, 3 * i:3 * i + 1],
                                                              scalar2=cb[:, i:i + 1], op0=ALU.mult, op1=ALU.add),
                 reads=["U%d" % i, "cw", "cb"], writes=["tmpc"])
            P.op("dve", lambda e, i=i, j0=j0: e.scalar_tensor_tensor(out=tmpc[:, :], in0=U[i][:, j0 + 1:j0 + 1025], scalar=cw[:, 3 * i + 1:3 * i + 2],
                                                                     in1=tmpc[:, :], op0=ALU.mult, op1=ALU.add),
                 reads=["U%d" % i, "cw", "tmpc"], writes=["tmpc"])
            P.op("dve", lambda e, i=i, j0=j0: e.scalar_tensor_tensor(out=CV[:, j0:j0 + 1024], in0=U[i][:, j0 + 2:j0 + 1026], scalar=cw[:, 3 * i + 2:3 * i + 3],
                                                                     in1=tmpc[:, :], op0=ALU.mult, op1=ALU.add),
                 reads=["U%d" % i, "cw", "tmpc"], writes=["CV"])
        for pg in range(8):
            for pi in range(8):
                p = pg * 8 + pi
                P.op("pe", lambda e, p=p, pi=pi: e.transpose(out=pTz[:, pi, :], in_=CVs[:, :, p], identity=ident[:, :]),
                     reads=["CV", "ident"], writes=["pTz"])
            P.op("act", lambda e, i=i, pg=pg: e.activation(out=Z[i][:, :, pg * 8:(pg + 1) * 8].rearrange("a c p -> a p c"), in_=pTz[:, :, :], func=AF.Copy),
                 reads=["pTz"], writes=["U%d" % i] + Zkeys(i))

    if stop <= 2:
        P.dma("pool", hy_oz[:, 0:4, :], H["xs"][0][:, 0, :].rearrange("a (c p) -> a c p", p=64), reads=["U0", "U1", "U2", "xs0"], writes=["dbg"])
        P.build()
        return nc
    for ch in range(16):
        zt, zk = zc[ch % 2], "stage%d" % (ch % 2)
        P.dma("sp", zt[:, :], dzemb[:, ch * 512:(ch + 1) * 512], writes=[zk])
        for layer in range(2):
            if layer == 0:
                P.op("pe", lambda e, zt=zt: e.matmul(pSS[0:64, :], lhsT=fw1[:, :], rhs=zt[:, :], start=True, stop=True), reads=["fw1", zk], writes=["pSS"])
            else:
                P.op("pe", lambda e: e.matmul(pSS[0:64, :], lhsT=fw2[:, :], rhs=h1[:, :], start=True, stop=True), reads=["fw2", "tmpc"], writes=["pSS"])
            fr, fb = (4, 5) if layer == 0 else (6, 7)
            P.op("dve", lambda e, fr=fr, fb=fb: e.tensor_scalar(out=arg[:, :], in0=pSS[0:64, :], scalar1=fpar[:, fr:fr + 1], scalar2=fpar[:, fb:fb + 1],
                                                                op0=ALU.mult, op1=ALU.add), reads=["pSS", "fpar"], writes=["tmpc"])
            P.op("dve", lambda e: e.tensor_copy(out=argi[:, :], in_=arg[:, :]), reads=["tmpc"], writes=["argi"])
            P.op("dve", lambda e: e.tensor_copy(out=h1[:, :], in_=argi[:, :]), reads=["argi", "tmpc"], writes=["tmpc"])
            P.op("dve", lambda e: e.tensor_tensor(out=arg[:, :], in0=arg[:, :], in1=h1[:, :], op=ALU.subtract), reads=["tmpc"], writes=["tmpc"])
            P.op("dve", lambda e: e.scalar_tensor_tensor(out=arg[:, :], in0=arg[:, :], scalar=0.5, in1=arg[:, :], op0=ALU.is_gt, op1=ALU.subtract),
                 reads=["tmpc"], writes=["tmpc"])
            if layer == 0:
                P.op("act", lambda e: e.activation(out=h1[:, :], in_=arg[:, :], func=AF.Sin, scale=-2.0 * math.pi), reads=["tmpc"], writes=["tmpc"])
            else:
                P.op("act", lambda e, ch=ch: e.activation(out=h2T[:, ch * 512:(ch + 1) * 512], in_=arg[:, :], func=AF.Sin, scale=-2.0 * math.pi),
                     reads=["tmpc"], writes=["h2T"])

    if stop <= 3:
        P.dma("pool", hy_oz[:, 0:4, :], H["xs"][0][:, 0, :].rearrange("a (c p) -> a c p", p=64), reads=["h2T", "xs0"], writes=["dbg"])
        P.build()
        return nc
    E3 = CV[:, :].rearrange("a (c p) -> a c p", p=64)
    O3 = Ob[:, :].rearrange("a (c p) -> a c p", p=64)
    h2s = h2T[:, :].rearrange("j (a p) -> j a p", p=64)
    cnt = [0]

    def twiddle(psrc, pkey, Tr_, Ti_, trk, tik, outr, outi, okey, view):
        g = cnt[0]
        cnt[0] += 1
        w1, w1k = W1[g % 2], "W1_%d" % (g % 2)
        w2, w2k = W2[g % 2], "W2_%d" % (g % 2)
        cp_, cpk = cpy[g % 2], "cpy%d" % (g % 2)
        P.op("act", lambda e: e.activation(out=cp_[:, :], in_=psrc[:, :], func=AF.Copy), reads=[pkey], writes=[cpk])
        P.op("dve", lambda e: e.tensor_tensor(out=w1[:, :], in0=psrc[:, :], in1=Tr_[:, :], op=ALU.mult), reads=[pkey, trk], writes=[w1k])
        P.op("pool", lambda e: e.tensor_tensor(out=w2[:, :], in0=cp_[:, :], in1=Ti_[:, :], op=ALU.mult), reads=[cpk, tik], writes=[w2k])
        w1r, w1i = view(w1)
        w2r, w2i = view(w2)
        P.op("dve", lambda e: e.tensor_tensor(out=outr, in0=w1r, in1=w2i, op=ALU.subtract), reads=[w1k, w2k], writes=[okey])
        P.op("pool", lambda e: e.tensor_tensor(out=outi, in0=w2r, in1=w1i, op=ALU.add), reads=[w1k, w2k], writes=[okey])

    v_fwd = lambda t: (t[:, 0:256], t[:, 256:512])
    v_inv = lambda t: (t[:, :].rearrange("k (c r x) -> k c r x", c=2, r=2)[:, :, 0, :], t[:, :].rearrange("k (c r x) -> k c r x", c=2, r=2)[:, :, 1, :])

    def fwd_stage1(src3, skey, c0):
        for q in range(NPQ):
            g = cnt[0]
            pa, pak = pA1[g % 2], "pA1_%d" % (g % 2)
            c = c0 + 2 * q
            P.op("pe", lambda e, c=c, pa=pa: e.matmul(pa[:, :], lhsT=src3[:, c:c + 2, :], rhs=F1b[:, :], start=True, stop=True),
                 reads=[skey, "F1b"], writes=[pak])
            twiddle(pa, pak, TrTr, TiTi, "TrTr", "TiTi", Ap[:, 0, q, :], Ap[:, 1, q, :], "Ap", v_fwd)

    G2r, G2in, G2i = R12b[:, 0:128], R12b[:, 128:256], R12b[:, 256:384]
    R1, R2 = R12b[:, 0:256], R12b[:, 256:512]

    for o in range(2):
        for p in range(64):
            P.op("pe", lambda e, p=p, o=o: e.matmul(pSS[:, 0:256], lhsT=h2s[:, :, p], rhs=fw3b[:, o * 256:(o + 1) * 256], start=True, stop=True),
                 reads=["h2T", "fw3b"], writes=["pSS"])
            P.op("act", lambda e, p=p: e.activation(out=win[:, :], in_=delta[:, :], func=AF.Exp, scale=Swin[:, p:p + 1]), reads=["delta", "Swin"], writes=["win"])
            for d in range(2):
                P.op("dve", lambda e, d=d: e.tensor_tensor(out=fwbw[:, d, :], in0=pSS[:, d * 128:(d + 1) * 128], in1=win[:, :], op=ALU.mult),
                     reads=["pSS", "win"], writes=["fwbw"])
            P.op("pool", lambda e, p=p: e.tensor_tensor(out=E3[:, :, p], in0=fwbw[:, 0, :], in1=fwbw[:, 1, :], op=ALU.add), reads=["fwbw"], writes=["CV"])
            P.op("pool", lambda e, p=p: e.tensor_tensor(out=O3[:, :, p], in0=fwbw[:, 0, :], in1=fwbw[:, 1, :], op=ALU.subtract), reads=["fwbw"], writes=["Ob"])
            if p == 0:
                P.op("pool", lambda e: e.tensor_copy(out=E3[0:1, :, 0], in_=fwbw[0:1, 0, :]), reads=["fwbw"], writes=["CV"])
                P.op("pool", lambda e: e.tensor_copy(out=O3[0:1, :, 0], in_=fwbw[0:1, 0, :]), reads=["fwbw"], writes=["Ob"])
        if stop <= 4:
            P.dma("pool", hy_oz[:, 0:4, :], H["xs"][0][:, 0, :].rearrange("a (c p) -> a c p", p=64), reads=["CV", "Ob", "xs0"], writes=["dbg"])
            P.build()
            return nc
        for sbt in range(128 // (2 * NPQ)):
            c0 = sbt * 2 * NPQ
            zk = "Z0_%d" % sbt
            for which, (src3, skey) in enumerate(((E3, "CV"), (O3, "Ob"))):
                fwd_stage1(src3, skey, c0)
                for qq in range(NPQ // 2):
                    rr = Ap[:, 0, 2 * qq:2 * qq + 2, :]
                    ri = Ap[:, 1, 2 * qq:2 * qq + 2, :]
                    if which == 0:
                        P.op("pe", lambda e, rr=rr: e.matmul(pXr[:, :], lhsT=G2r, rhs=rr, start=True, stop=False), reads=["R12b", "Ap"], writes=["pXr"])
                        P.op("pe", lambda e, ri=ri: e.matmul(pXr[:, :], lhsT=G2in, rhs=ri, start=False, stop=True), reads=["R12b", "Ap"], writes=["pXr"])
                        for j in range(2):
                            qg = sbt * NPQ + 2 * qq + j
                            P.op("act", lambda e, j=j, qq=qq, qg=qg, o=o: e.activation(out=Hs[:, 0, 2 * qq + j, :], in_=pXr[:, j * 256:(j + 1) * 256], func=AF.Identity,
                                                                                        bias=hbcol[:, o * 64 + qg:o * 64 + qg + 1], scale=1.0),
                                 reads=["pXr", "hbcol"], writes=["Hs"])
                    else:
                        P.op("pe", lambda e, rr=rr: e.matmul(pXi[:, :], lhsT=G2i, rhs=rr, start=True, stop=False), reads=["R12b", "Ap"], writes=["pXi"])
                        P.op("pe", lambda e, ri=ri: e.matmul(pXi[:, :], lhsT=G2r, rhs=ri, start=False, stop=True), reads=["R12b", "Ap"], writes=["pXi"])
                        P.op("act", lambda e, qq=qq: e.activation(out=Hs[:, 1, 2 * qq:2 * qq + 2, :], in_=pXi[:, :].rearrange("k (q x) -> k q x", q=2), func=AF.Copy),
                             reads=["pXi"], writes=["Hs"])
            if stop <= 6:
                P.dma("pool", hy_oz[:, 0:4, :], H["xs"][0][:, 0, :].rearrange("a (c p) -> a c p", p=64), reads=["CV", "Ob", "xs0", "Ap", "Hs", "Yb", "Bp", "pY"], writes=["dbg"])
                P.build()
                return nc
            fwd_stage1(Z[0], zk, c0)
            for qq in range(NPQ // 2):
                rr = Ap[:, 0, 2 * qq:2 * qq + 2, :]
                ri = Ap[:, 1, 2 * qq:2 * qq + 2, :]
                P.op("pe", lambda e, rr=rr: e.matmul(pXr[:, :], lhsT=G2r, rhs=rr, start=True, stop=False), reads=["R12b", "Ap"], writes=["pXr"])
                P.op("pe", lambda e, ri=ri: e.matmul(pXr[:, :], lhsT=G2in, rhs=ri, start=False, stop=True), reads=["R12b", "Ap"], writes=["pXr"])
                P.op("pe", lambda e, rr=rr: e.matmul(pXi[:, :], lhsT=G2i, rhs=rr, start=True, stop=False), reads=["R12b", "Ap"], writes=["pXi"])
                P.op("pe", lambda e, ri=ri: e.matmul(pXi[:, :], lhsT=G2r, rhs=ri, start=False, stop=True), reads=["R12b", "Ap"], writes=["pXi"])
                g = cnt[0]
                cnt[0] += 1
                w1, w1k = W1[g % 2], "W1_%d" % (g % 2)
                w2, w2k = W2[g % 2], "W2_%d" % (g % 2)
                cx, cxk = cpy[g % 2], "cpy%d" % (g % 2)
                hr = Hs[:, 0, 2 * qq:2 * qq + 2, :].rearrange("k q x -> k (q x)")
                hi = Hs[:, 1, 2 * qq:2 * qq + 2, :].rearrange("k q x -> k (q x)")
                yr = Yb[:, 0, 2 * qq:2 * qq + 2, :].rearrange("k q x -> k (q x)")
                yi = Yb[:, 1, 2 * qq:2 * qq + 2, :].rearrange("k q x -> k (q x)")
                P.op("act", lambda e, cx=cx: e.activation(out=cx[:, :], in_=pXi[:, :], func=AF.Copy), reads=["pXi"], writes=[cxk])
                P.op("act", lambda e: e.activation(out=cpy2[:, :], in_=pXr[:, :], func=AF.Copy), reads=["pXr"], writes=["cpy2"])
                P.op("dve", lambda e, w1=w1, hr=hr: e.tensor_tensor(out=w1[:, :], in0=pXr[:, :], in1=hr, op=ALU.mult), reads=["pXr", "Hs"], writes=[w1k])
                P.op("pool", lambda e, w2=w2, cx=cx, hi=hi: e.tensor_tensor(out=w2[:, :], in0=cx[:, :], in1=hi, op=ALU.mult), reads=[cxk, "Hs"], writes=[w2k])
                P.op("dve", lambda e, w1=w1, w2=w2, yr=yr: e.tensor_tensor(out=yr, in0=w1[:, :], in1=w2[:, :], op=ALU.subtract), reads=[w1k, w2k], writes=["Yb"])
                P.op("dve", lambda e, w1=w1, hr=hr: e.tensor_tensor(out=w1[:, :], in0=pXi[:, :], in1=hr, op=ALU.mult), reads=["pXi", "Hs", "Yb"], writes=[w1k])
                P.op("pool", lambda e, w2=w2, hi=hi: e.tensor_tensor(out=w2[:, :], in0=cpy2[:, :], in1=hi, op=ALU.mult), reads=["cpy2", "Hs", "Yb"], writes=[w2k])
                P.op("pool", lambda e, w1=w1, w2=w2, yi=yi: e.tensor_tensor(out=yi, in0=w1[:, :], in1=w2[:, :], op=ALU.add), reads=[w1k, w2k], writes=["Yb"])
            if stop <= 7:
                P.dma("pool", hy_oz[:, 0:4, :], H["xs"][0][:, 0, :].rearrange("a (c p) -> a c p", p=64), reads=["CV", "Ob", "xs0", "Ap", "Hs", "Yb", "Bp", "pY"], writes=["dbg"])
                P.build()
                return nc
            for q in range(NPQ):
                for kc in range(2):
                    P.op("pe", lambda e, q=q, kc=kc: e.matmul(pB[:, kc * 256:(kc + 1) * 256], lhsT=Yb[:, 0, q, kc * 128:(kc + 1) * 128], rhs=R1, start=True, stop=False),
                         reads=["Yb", "R12b"], writes=["pB"])
                    P.op("pe", lambda e, q=q, kc=kc: e.matmul(pB[:, kc * 256:(kc + 1) * 256], lhsT=Yb[:, 1, q, kc * 128:(kc + 1) * 128], rhs=R2, start=False, stop=True),
                         reads=["Yb", "R12b"], writes=["pB"])
                twiddle(pB, "pB", TT2r, TT2i, "TT2r", "TT2i", Bp[:, :, 0, q, :], Bp[:, :, 1, q, :], "Bp", v_inv)
            if stop <= 8:
                P.dma("pool", hy_oz[:, 0:4, :], H["xs"][0][:, 0, :].rearrange("a (c p) -> a c p", p=64), reads=["CV", "Ob", "xs0", "Ap", "Hs", "Yb", "Bp", "pY"], writes=["dbg"])
                P.build()
                return nc
            for hh in range(NPQ // 4):
                n = 0
                for kc in range(2):
                    for r in range(2):
                        rhs = Bp[:, kc, r, 4 * hh:4 * hh + 4, :]
                        lt = IF1b[:, (2 * kc + r) * 128:(2 * kc + r + 1) * 128]
                        P.op("pe", lambda e, rhs=rhs, lt=lt, n=n: e.matmul(pY[:, :], lhsT=lt, rhs=rhs, start=(n == 0), stop=(n == 3)),
                             reads=["Bp", "IF1b"], writes=["pY"])
                        n += 1
                cc = c0 + 8 * hh
                if o == 0:
                    P.op("dve", lambda e, cc=cc: e.scalar_tensor_tensor(out=Z[0][:, cc:cc + 8, :], in0=pY[:, :].rearrange("a (c p) -> a c p", p=64), scalar=1.0 / NFFT,
                                                                        in1=Z[1][:, cc:cc + 8, :], op0=ALU.mult, op1=ALU.mult),
                         reads=["pY", "U1"] + Zkeys(1), writes=[zk])
                else:
                    g = cnt[0]
                    cnt[0] += 1
                    OT, otk = ot[g % 2], "xs%d" % (g % 2)
                    P.op("dve", lambda e, cc=cc, OT=OT: e.scalar_tensor_tensor(out=OT[:, :].rearrange("a (c p) -> a c p", p=64), in0=pY[:, :].rearrange("a (c p) -> a c p", p=64),
                                                                               scalar=1.0 / NFFT, in1=Z[2][:, cc:cc + 8, :], op0=ALU.mult, op1=ALU.mult),
                         reads=["pY", "U2"] + Zkeys(2), writes=[otk])
                    P.dma("pool", hy_oz[:, cc:cc + 8, :], OT[:, :].rearrange("a (c p) -> a c p", p=64), reads=[otk], writes=["hy_%d" % cc])
    P.build()
    return nc


def run_hyena(inp, stop=99):
    f = lambda a: np.ascontiguousarray(a, dtype=np.float32)
    nc = build_hyena(stop)
    K = _hy_consts()
    w_in = inp["w_in"][0]
    in_maps = []
    for core in range(8):
        b, g = core // 4, core % 4
        cs = slice(128 * g, 128 * g + 128)
        hy_w = w_in[:, 1536:]
        w_inc = np.concatenate([hy_w[:, 0:512][:, cs], hy_w[:, 512:1024][:, cs], hy_w[:, 1024:1536][:, cs], np.zeros((1024, 128), np.float32)], axis=1)
        cwv = inp["conv_w"][0].reshape(3, 3, 512)[:, :, cs]
        cbv = inp["conv_b"][0].reshape(3, 512)[:, cs]
        w3 = inp["filt_w3"][0].reshape(64, 2, 2, 512)[:, :, :, cs].reshape(64, 512)
        hb = inp["hyena_bias"][0][:, cs]
        hbcol = np.zeros((128, 2, 64), np.float32)
        for cp in range(2):
            hbcol[64 * cp:64 * cp + 64, :, :] = hb[:, cp::2][None, :, :]
        in_maps.append({
            "xT": f(inp["x"][b].T), "cvec": f(inp["c"][b].reshape(8, 128).T),
            "w_ada1": f(inp["w_ada"][0][:, 0:2048]), "b_ada1": f(inp["b_ada"][0][0:2048].reshape(16, 128).T),
            "gpre": f(inp["g_pre_mix"][0].reshape(8, 128).T), "w_inc": f(w_inc),
            "F1cat": K["F1cat"], "TrTr": K["TrTr"], "TiTi": K["TiTi"], "R12": K["R12"], "TT2r": K["TT2r"], "TT2i": K["TT2i"],
            "IF1": K["IF1"], "Swin": K["Swin"], "zembT": K["zembT"],
            "delta": f(np.broadcast_to(K["deltas"][cs][None, :], (128, 128))),
            "hbcol": f(hbcol.reshape(128, 128)),
            "fw1": f(inp["filt_w1"][0]), "fw2": f(inp["filt_w2"][0]), "fw3c": f(w3),
            "fpar": f(np.stack([inp["filt_b1"][0], inp["filt_freq1"][0], inp["filt_b2"][0], inp["filt_freq2"][0]], axis=1)),
            "cw": f(cwv.transpose(2, 1, 0).reshape(128, 9)), "cb": f(cbv.T),
            "identd": np.eye(128, dtype=np.float32),
        })
    res = run_bass_kernel_spmd(nc, in_maps, core_ids=list(range(8)))
    hy = np.zeros((2, NTOK, 512), np.float32)
    for core in range(8):
        b, g = core // 4, core % 4
        oz = res.results[core]["hy_oz"]
        hy[b, :, 128 * g:128 * g + 128] = oz.transpose(0, 2, 1).reshape(NTOK, 128)
    return hy


NW = 4096
NOWN = 2048


class DramReg:
    def __init__(self, nc):
        self.nc = nc
        self.t = {}

    def __call__(self, name, shape, kind="ExternalInput", dt=None):
        if name not in self.t:
            self.t[name] = self.nc.dram_tensor(name, list(shape), dt or F32, kind=kind).ap()
        return self.t[name]


def _p1_common_f(P, D, ncols_w, wname, xname, ntok, ST=512, SW=256, cast_w=True):
    H = {}
    H["xT"] = D(xname, [1024, ntok])
    cvec = D("cvec", [128, 8])
    w_ada1 = D("w_ada1", [1024, 2048])
    b_ada1 = D("b_ada1", [128, 16])
    gpre = D("gpre", [128, 8])
    H["w_inc"] = D(wname[0], wname[1])
    sb, ps = P.sb, P.ps
    H["wb"] = sb("wb", [128, 8, ncols_w], BF16)
    H["stage"] = stage = [sb("stage%d" % i, [128, 8, SW], F32) for i in range(2)]
    H["ST"], H["SW"] = ST, SW
    csb = sb("csb", [128, 8], F32)
    gp = sb("gp", [128, 8], F32)
    bsb = sb("bsb", [128, 16], F32)
    H["modc"] = modc = sb("modc", [128, 16], F32)
    H["G0"] = G0 = sb("G0", [128, 8], F32)
    H["onesb"] = onesb = sb("onesb", [128, 128], BF16)
    H["xs"] = [sb("xs%d" % i, [128, 8, ST], F32) for i in range(2)]
    H["sqR"] = [sb("sq%d" % i, [128, 8, ST], BF16) for i in range(2)]
    H["hTR"] = [sb("hT%d" % i, [128, 8, ST], BF16) for i in range(2)]
    H["rbcR"] = [sb("rbc%d" % i, [128, ST], F32) for i in range(2)]
    H["sq"], H["hT"], H["rbc"] = H["sqR"][0], H["hTR"][0], H["rbcR"][0]
    H["pSS"] = pSS = ps("pSS", [128, 512], F32)
    P.op("pool", lambda e: e.memset(onesb[:, :], 1.0), writes=["onesb"])
    P.dma("sp", csb[:, :], cvec[:, :], writes=["csb"])
    P.dma("sp", gp[:, :], gpre[:, :], writes=["gp"])
    P.dma("sp", bsb[:, :], b_ada1[:, :], writes=["bsb"])
    P.op("act", lambda e: e.activation(out=csb[:, :], in_=csb[:, :], func=AF.Silu), reads=["csb"], writes=["csb"])
    wa = w_ada1.rearrange("(k p) n -> p k n", p=128)
    for blk in range(2048 // SW):
        st, skey = stage[blk % 2], "stage%d" % (blk % 2)
        P.dma("sp", st[:, :, :], wa[:, :, blk * SW:(blk + 1) * SW], writes=[skey])
        for jj in range(SW // 128):
            j = (SW // 128) * blk + jj
            for k in range(8):
                P.op("pe", lambda e, k=k, j=j, jj=jj, st=st: e.matmul(pSS[:, j:j + 1], lhsT=st[:, k, jj * 128:(jj + 1) * 128],
                                                                      rhs=csb[:, k:k + 1], start=(k == 0), stop=(k == 7)),
                     reads=[skey, "csb"], writes=["pSS"])
    P.op("dve", lambda e: e.tensor_tensor(out=modc[:, :], in0=pSS[:, 0:16], in1=bsb[:, :], op=ALU.add),
         reads=["pSS", "bsb"], writes=["modc"])
    P.op("dve", lambda e: e.scalar_tensor_tensor(out=G0[:, :], in0=modc[:, 8:16], scalar=1.0, in1=gp[:, :], op0=ALU.add, op1=ALU.mult),
         reads=["modc", "gp"], writes=["G0"])
    return H


def _p1_norm_r(P, H, st):
    ST = H["ST"]
    r = st % 2
    xs, xk = H["xs"][r], "xs%d" % r
    sq, hT, rbc = H["sqR"][r], H["hTR"][r], H["rbcR"][r]
    sqk, hk, rk = "sq%d" % r, "hT%d" % r, "rbc%d" % r
    pSS, onesb, modc, G0 = H["pSS"], H["onesb"], H["modc"], H["G0"]
    P.op("act", lambda e: e.activation(out=sq[:, :, :], in_=xs[:, :, :], func=AF.Square), reads=[xk], writes=[sqk])
    for k in range(8):
        P.op("pe", lambda e, k=k: e.matmul(pSS[:, 0:ST], lhsT=onesb[:, :], rhs=sq[:, k, :], start=(k == 0), stop=(k == 7)),
             reads=["onesb", sqk], writes=["pSS"])
    P.op("dve", lambda e: e.tensor_scalar(out=rbc[:, :], in0=pSS[:, 0:ST], scalar1=1.0 / 1024, scalar2=EPS, op0=ALU.mult, op1=ALU.add),
         reads=["pSS"], writes=[rk])
    P.op("act", lambda e: e.activation(out=rbc[:, :], in_=rbc[:, :], func=AF.Sqrt), reads=[rk], writes=[rk])
    P.op("dve", lambda e: e.reciprocal(out=rbc[:, :], in_=rbc[:, :]), reads=[rk], writes=[rk])
    for k in range(8):
        P.op("dve", lambda e, k=k: e.scalar_tensor_tensor(out=xs[:, k, :], in0=xs[:, k, :], scalar=G0[:, k:k + 1], in1=rbc[:, :],
                                                          op0=ALU.mult, op1=ALU.mult), reads=[xk, "G0", rk], writes=[xk])
        P.op("act", lambda e, k=k: e.activation(out=hT[:, k, :], in_=xs[:, k, :], func=AF.Identity, bias=modc[:, k:k + 1], scale=1.0),
             reads=[xk, "modc"], writes=[hk])
    return hT, hk


def _cast_w(P, H, wv, ncols, col0=0):
    SW, stage, wb = H["SW"], H["stage"], H["wb"]
    for blk in range(ncols // SW):
        st, skey = stage[blk % 2], "stage%d" % (blk % 2)
        P.dma("sp", st[:, :, :], wv[:, :, blk * SW:(blk + 1) * SW], writes=[skey])
        P.op("dve" if blk % 2 == 0 else "pool", lambda e, st=st, blk=blk: e.tensor_copy(
            out=wb[:, :, col0 + blk * SW:col0 + (blk + 1) * SW], in_=st[:, :, :]), reads=[skey], writes=["wb"])


def emit_attn_norm_f(nc, outer, D, hTw):
    P = Prog(nc, sem_stack=outer, prefix="A0_")
    H = _p1_common_f(P, D, 128, ("w_incA", [4, 1024, 768]), "xTw", NW)
    ST = 512
    for st in range(NW // ST):
        _p1_load(P, H, st)
        hT, hk = _p1_norm_r(P, H, st)
        P.op("pool", lambda e, st=st, hT=hT: e.tensor_copy(out=hTw[:, :, st * ST:(st + 1) * ST], in_=hT[:, :, :]), reads=[hk], writes=["hTw"])
    P.build()


def emit_attn_f(nc, outer, D, mixT, hTw):
    P = Prog(nc, sem_stack=outer, prefix="A_")
    ST = 512
    H = {"SW": 256, "w_inc": D("w_incA", [4, 1024, 768])}
    ropeC, ropeS = D("ropeCw", [128, NW]), D("ropeSw", [128, NW])
    maskd, hseld, validd, identd = D("maskd", [128, 17 * 128]), D("hseld", [128, 64]), D("validw", [128, 32]), D("identd", [128, 128])
    sb, ps = P.sb, P.ps
    H["wb"] = wb = sb("wb", [128, 8, 768], BF16)
    H["stage"] = [sb("stage%d" % i, [128, 8, 256], F32) for i in range(2)]
    H["pSS"] = ps("pSS", [128, 512], F32)
    QT, KT = sb("QT", [128, NW], BF16), sb("KT", [128, NW], BF16)
    Vaug = sb("Vaug", [128, 32, 2, 65], BF16)
    Mall = sb("Mall", [128, 17 * 128], BF16)
    mst = sb("mst", [128, 17 * 128], F32)
    hself, hsel = sb("hself", [128, 64], F32), sb("hsel", [128, 64], BF16)
    valid = sb("valid", [128, 32], F32)
    identf = sb("identf", [128, 128], F32)
    onesrow = sb("onesrow", [64, 128], BF16)
    rc = [sb("rc%d" % i, [128, ST], F32) for i in range(2)]
    rs_ = [sb("rs%d" % i, [128, ST], F32) for i in range(2)]
    t1, t2 = sb("t1", [128, ST], F32), sb("t2", [128, ST], F32)
    sqk = sb("sqk", [128, ST], BF16)
    kmx = sb("kmx", [64, 2], F32)
    qn = sb("qn", [64, 128], F32)
    negm = sb("negm", [64, 128], BF16)
    PT = [sb("PT%d" % i, [128, 512], BF16) for i in range(4)]
    ao = [sb("ao%d" % i, [128, 128], F32) for i in range(2)]
    rec = sb("rec", [128, 4], F32)
    pA, pB, pV = ps("pA", [128, 512], F32), ps("pB", [128, 512], F32), ps("pV", [128, 512], F32)
    pN = H["pSS"]
    pS = [ps("pS%d" % i, [128, 512], F32) for i in range(2)]
    pO = [ps("pO%d" % i, [128, 2, 128], F32) for i in range(2)]

    P.dma("sp", mst[:, :], maskd[:, :], writes=["mst"])
    P.op("pool", lambda e: e.tensor_copy(out=Mall[:, :], in_=mst[:, :]), reads=["mst"], writes=["Mall"])
    P.dma("sp", hself[:, :], hseld[:, :], writes=["hself"])
    P.op("pool", lambda e: e.tensor_copy(out=hsel[:, :], in_=hself[:, :]), reads=["hself"], writes=["hsel"])
    P.dma("sp", valid[:, :], validd[:, :], writes=["valid"])
    P.dma("sp", identf[:, :], identd[:, :], writes=["identf"])
    P.op("pool", lambda e: e.memset(onesrow[:, :], 1.0), writes=["onesrow"])
    for h in range(2):
        P.op("pool", lambda e, h=h: e.tensor_copy(out=Vaug[:, :, h, 64], in_=valid[:, :]), reads=["valid"], writes=["Vaug"])
    cnt = [0]
    xl = [0]
    wv_all = H["w_inc"]
    for hp in range(4):
        _cast_w(P, H, wv_all[hp].rearrange("(k p) n -> p k n", p=128), 768)
        P.op("pool", lambda e: e.memset(kmx[:, :], 0.0), reads=["kmx"], writes=["kmx"])
        nst = NW // ST
        for st in range(nst):
            hT = hTw[:, :, st * ST:(st + 1) * ST]
            C, ck = rc[st % 2], "rc%d" % (st % 2)
            S, sk = rs_[st % 2], "rs%d" % (st % 2)
            P.dma("sp", C[:, :], ropeC[:, st * ST:(st + 1) * ST], writes=[ck])
            P.dma("sp", S[:, :], ropeS[:, st * ST:(st + 1) * ST], writes=[sk])
            for which, dst in ((0, QT), (1, KT)):
                dk = "QT" if which == 0 else "KT"
                c0 = which * 256
                for k in range(8):
                    P.op("pe", lambda e, k=k, c0=c0, hT=hT: e.matmul(pA[:, :], lhsT=wb[:, k, c0:c0 + 128], rhs=hT[:, k, :], start=(k == 0), stop=(k == 7)),
                         reads=["wb", "hTw"], writes=["pA"])
                for k in range(8):
                    P.op("pe", lambda e, k=k, c0=c0, hT=hT: e.matmul(pB[:, :], lhsT=wb[:, k, c0 + 128:c0 + 256], rhs=hT[:, k, :], start=(k == 0), stop=(k == 7)),
                         reads=["wb", "hTw"], writes=["pB"])
                P.op("dve", lambda e, C=C: e.tensor_tensor(out=t1[:, :], in0=pA[:, :], in1=C[:, :], op=ALU.mult), reads=["pA", ck], writes=["t1"])
                P.op("dve", lambda e, S=S: e.tensor_tensor(out=t2[:, :], in0=pB[:, :], in1=S[:, :], op=ALU.mult), reads=["pB", sk], writes=["t2"])
                P.op("pool", lambda e, dst=dst, st=st: e.tensor_tensor(out=dst[:, st * ST:(st + 1) * ST], in0=t1[:, :], in1=t2[:, :], op=ALU.add),
                     reads=["t1", "t2"], writes=[dk])
            P.op("pool", lambda e, st=st: e.tensor_tensor(out=sqk[:, :], in0=KT[:, st * ST:(st + 1) * ST], in1=KT[:, st * ST:(st + 1) * ST], op=ALU.mult),
                 reads=["KT"], writes=["sqk"])
            P.op("pe", lambda e: e.matmul(pN[0:64, :], lhsT=hsel[:, :], rhs=sqk[:, :], start=True, stop=True), reads=["hsel", "sqk"], writes=["pSS"])
            P.op("dve", lambda e: e.tensor_reduce(out=kmx[:, 1:2], in_=pN[0:64, :], axis=AX.X, op=ALU.max), reads=["pSS", "kmx"], writes=["kmx"])
            P.op("dve", lambda e: e.tensor_tensor(out=kmx[:, 0:1], in0=kmx[:, 0:1], in1=kmx[:, 1:2], op=ALU.max), reads=["kmx"], writes=["kmx"])
            for tt in range(4):
                for k in range(8):
                    P.op("pe", lambda e, k=k, tt=tt, hT=hT: e.matmul(pV[:, tt * 128:(tt + 1) * 128], lhsT=hT[:, k, tt * 128:(tt + 1) * 128],
                                                              rhs=wb[:, k, 512:640], start=(k == 0), stop=(k == 7)),
                         reads=["wb", "hTw"], writes=["pV"])
            for tt in range(4):
                tile = st * 4 + tt
                P.op("act", lambda e, tt=tt, tile=tile: e.activation(out=Vaug[:, tile, :, 0:64], in_=pV[:, tt * 128:(tt + 1) * 128].rearrange("p (h d) -> p h d", h=2),
                                                                     func=AF.Copy, scale=valid[:, tile:tile + 1]),
                     reads=["pV", "valid"], writes=["Vaug"])
        P.op("act", lambda e: e.activation(out=kmx[:, 0:1], in_=kmx[:, 0:1], func=AF.Sqrt), reads=["kmx"], writes=["kmx"])
        LAG = 2
        units = []
        for jb in range(8, 24):
            for h in range(2):
                dms = list(range(-8, 9))
                groups = [dms[i:i + 4] for i in range(0, len(dms), 4)]
                base = 0
                for gi, grp in enumerate(groups):
                    units.append(dict(jb=jb, h=h, grp=grp, gi=gi, base=base, last=(gi == len(groups) - 1)))
                    base += len(grp)

        def emit_scores(u):
            jb, h, grp = u["jb"], u["h"], u["grp"]
            qs = slice(jb * 128, (jb + 1) * 128)
            hs = slice(64 * h, 64 * h + 64)
            if h == 0 and u["gi"] == 0:
                P.op("pool", lambda e, qs=qs: e.tensor_tensor(out=sqk[:, 0:128], in0=QT[:, qs], in1=QT[:, qs], op=ALU.mult), reads=["QT"], writes=["sqk"])
                P.op("pe", lambda e: e.matmul(pN[0:64, 0:128], lhsT=hsel[:, :], rhs=sqk[:, 0:128], start=True, stop=True), reads=["hsel", "sqk"], writes=["pSS"])
                P.op("act", lambda e: e.activation(out=qn[:, :], in_=pN[0:64, 0:128], func=AF.Sqrt), reads=["pSS"], writes=["qn"])
                P.op("dve", lambda e: e.tensor_scalar(out=negm[:, :], in0=qn[:, :], scalar1=kmx[:, 0:1], scalar2=-1.0, op0=ALU.mult, op1=ALU.mult),
                     reads=["qn", "kmx"], writes=["negm"])
            g = cnt[0]
            cnt[0] += 1
            psx, psk = [(pS[0], "pS0"), (pS[1], "pS1"), (pA, "pA"), (pB, "pB")][g % 4]
            ptx, ptk = PT[g % 4], "PT%d" % (g % 4)
            u["ptx"], u["ptk"] = ptx, ptk
            n = len(grp)
            for i, dm in enumerate(grp):
                kc = jb + dm
                P.op("pe", lambda e, i=i, kc=kc, hs=hs, qs=qs, psx=psx: e.matmul(psx[:, i * 128:(i + 1) * 128], lhsT=KT[hs, kc * 128:(kc + 1) * 128],
                                                                                  rhs=QT[hs, qs], start=True, stop=False),
                     reads=["KT", "QT"], writes=[psk])
                P.op("pe", lambda e, i=i, h=h, psx=psx: e.matmul(psx[:, i * 128:(i + 1) * 128], lhsT=onesrow[32 * h:32 * h + 1, :],
                                                                 rhs=negm[32 * h:32 * h + 1, :], start=False, stop=True),
                     reads=["onesrow", "negm"], writes=[psk])
            P.op("act", lambda e, psx=psx, ptx=ptx, n=n: e.activation(out=ptx[:, 0:n * 128], in_=psx[:, 0:n * 128], func=AF.Exp, scale=0.125),
                 reads=[psk], writes=[ptk])
            m0 = (grp[0] + 8) * 128
            P.op("dve" if g % 2 == 0 else "pool", lambda e, ptx=ptx, n=n, m0=m0: e.tensor_tensor(
                out=ptx[:, 0:n * 128], in0=ptx[:, 0:n * 128], in1=Mall[:, m0:m0 + n * 128], op=ALU.mult),
                reads=[ptk, "Mall"], writes=[ptk])

        def emit_pv(u):
            jb, h, grp = u["jb"], u["h"], u["grp"]
            ptx, ptk = u["ptx"], u["ptk"]
            AO, aok = ao[jb % 2], "ao%d" % (jb % 2)
            PO, pok = pO[jb % 2], "pO%d" % (jb % 2)
            for i, dm in enumerate(grp):
                kc = jb + dm
                nmm = u["base"] + i
                P.op("pe", lambda e, i=i, kc=kc, h=h, ptx=ptx, PO=PO, first=(nmm == 0), last=(nmm == 16): e.matmul(
                    PO[:, h, 0:65], lhsT=ptx[:, i * 128:(i + 1) * 128], rhs=Vaug[:, kc, h, :], start=first, stop=last),
                    reads=[ptk, "Vaug"], writes=[pok])
            if u["last"]:
                P.op("dve", lambda e, h=h, PO=PO: e.reciprocal(out=rec[:, h:h + 1], in_=PO[:, h, 64:65]), reads=[pok], writes=["rec%d" % h])
                P.op("dve", lambda e, h=h, PO=PO, AO=AO: e.tensor_scalar(out=AO[:, 64 * h:64 * h + 64], in0=PO[:, h, 0:64], scalar1=rec[:, h:h + 1],
                                                                         scalar2=None, op0=ALU.mult),
                     reads=[pok, "rec%d" % h], writes=[aok])
                if h == 1:
                    P.op("pe", lambda e, AO=AO: e.transpose(out=pV[:, 0:128], in_=AO[:, :], identity=identf[:, :]), reads=[aok, "identf"], writes=["pV"])
                    P.op("act", lambda e, hp=hp, jb=jb: e.activation(out=mixT[:, hp, (jb - 8) * 128:(jb - 7) * 128], in_=pV[:, 0:128], func=AF.Copy),
                         reads=["pV"], writes=["mixT"])

        pending = []
        for u in units:
            emit_scores(u)
            pending.append(u)
            if len(pending) > LAG:
                emit_pv(pending.pop(0))
        while pending:
            emit_pv(pending.pop(0))
    P.build()


def emit_hyproj_f(nc, outer, D):
    P = Prog(nc, sem_stack=outer, prefix="B_")
    ST = 256
    H = _p1_common_f(P, D, 1536, ("w_incH", [1024, 1536]), "xT", NTOK, ST=ST, SW=128)
    _cast_w(P, H, H["w_inc"].rearrange("(k p) n -> p k n", p=128), 1536)
    Us = D("Us", [12, 128, NTOK], kind="Internal", dt=BF16)
    sb, ps = P.sb, P.ps
    wb, hT = H["wb"], H["hT"]
    ub = [sb("ub%d" % i, [128, 12, ST], BF16) for i in range(2)]
    pU = [ps("pU%d" % i, [128, 512], F32) for i in range(4)]
    nst = NTOK // ST
    _p1_load(P, H, 0)
    nxt = _p1_norm_r(P, H, 0)
    for st in range(nst):
        hT, hk = nxt
        if st + 1 < nst:
            _p1_load(P, H, st + 1)
            nxt = _p1_norm_r(P, H, st + 1)
        UB, ubk = ub[st % 2], "ub%d" % (st % 2)
        for pr in range(6):
            pu, puk = pU[pr % 4], "pU%d" % (pr % 4)
            for half in range(2):
                i = 2 * pr + half
                for k in range(8):
                    P.op("pe", lambda e, k=k, i=i, half=half, pu=pu, hT=hT: e.matmul(pu[:, half * ST:(half + 1) * ST], lhsT=wb[:, k, i * 128:(i + 1) * 128], rhs=hT[:, k, :],
                                                                                    start=(k == 0), stop=(k == 7)), reads=["wb", hk], writes=[puk])
            eng = "act" if pr % 2 == 0 else "dve"
            if eng == "act":
                P.op("act", lambda e, pr=pr, pu=pu, UB=UB: e.activation(out=UB[:, 2 * pr:2 * pr + 2, :], in_=pu[:, :].rearrange("p (i t) -> p i t", i=2), func=AF.Copy),
                     reads=[puk], writes=[ubk])
            else:
                P.op("dve", lambda e, pr=pr, pu=pu, UB=UB: e.tensor_copy(out=UB[:, 2 * pr:2 * pr + 2, :], in_=pu[:, :].rearrange("p (i t) -> p i t", i=2)),
                     reads=[puk], writes=[ubk])
        P.dma("pool", Us[:, :, st * ST:(st + 1) * ST].rearrange("i p t -> p i t"), UB[:, :, :], reads=[ubk], writes=["Us"])
    P.build()


def emit_hyena_f(nc, outer, D, mixT, groups=(0, 1, 2, 3), prefix="C_"):
    P = Prog(nc, sem_stack=outer, prefix=prefix)
    sb, ps = P.sb, P.ps
    Us = D("Us", [12, 128, NTOK], kind="Internal", dt=BF16)
    dF1cat, dTrTr, dTiTi, dR12 = D("F1cat", [128, 512]), D("TrTr", [128, 512]), D("TiTi", [128, 512]), D("R12", [128, 512])
    dTT2r, dTT2i, dIF1, dSwin = D("TT2r", [128, 512]), D("TT2i", [128, 512]), D("IF1", [128, 512]), D("Swin", [128, 64])
    dzemb = D("zembT", [33, NTOK])
    ddelta = D("delta4", [128, 512])
    dhbcol = D("hbcol4", [128, 512])
    dfw1, dfw2, dfw3, dfpar = D("fw1", [33, 64]), D("fw2", [64, 64]), D("fw3c4", [64, 2048]), D("fpar", [64, 4])
    dcw, dcb = D("cw4", [128, 36]), D("cb4", [128, 12])
    dsel = D("seld", [128, 32])
    didn = D("identd", [128, 128])

    U = [sb("U%d" % i, [128, NTOK + 2], BF16) for i in range(3)]
    CV = sb("CV", [128, NTOK], BF16)
    Ob = sb("Ob", [128, NTOK], BF16)
    h2T = sb("h2T", [64, NTOK], BF16)
    ApF = sb("ApF", [128, 2, NPQ, 256], BF16)
    ApD = sb("ApD", [128, 2, NPQ, 256], BF16)
    HsL = [sb("Hs%d" % i, [128, 2, NPQ, 256], BF16) for i in range(2)]
    Yb = sb("Yb", [128, 2, NPQ, 256], BF16)
    Bp = sb("Bp", [128, 2, 2, NPQ, 128], BF16)
    W1 = [sb("W1_%d" % i, [128, 512], F32) for i in range(2)]
    W2 = [sb("W2_%d" % i, [128, 512], F32) for i in range(2)]
    cpy = [sb("cpy%d" % i, [128, 512], F32) for i in range(2)]
    cpy2 = sb("cpy2", [128, 512], F32)
    tmpc = sb("tmpc", [128, 1024], F32)
    arg = tmpc[0:64, 0:512]
    h1 = tmpc[0:64, 512:1024]
    argi = sb("argi", [64, 512], mybir.dt.int32)
    F1b, R12b, IF1b = sb("F1b", [128, 512], BF16), sb("R12b", [128, 512], BF16), sb("IF1b", [128, 512], BF16)
    TrTr, TiTi = sb("TrTr", [128, 512], F32), sb("TiTi", [128, 512], F32)
    TT2r, TT2i = sb("TT2r", [128, 512], F32), sb("TT2i", [128, 512], F32)
    Swin = sb("Swin", [128, 64], F32)
    delta = sb("delta", [128, 512], F32)
    hbcol = sb("hbcol", [128, 512], F32)
    fw1, fw2 = sb("fw1", [33, 64], F32), sb("fw2", [64, 64], F32)
    fw3b = sb("fw3b", [64, 2048], BF16)
    fpar = sb("fpar", [64, 8], F32)
    cw, cb = sb("cw", [128, 36], F32), sb("cb", [128, 12], F32)
    selb = sb("selb", [128, 32], BF16)
    ident = sb("ident", [128, 128], BF16)
    zc = [sb("zc%d" % i, [33, 512], F32) for i in range(2)]
    win = [sb("win%d" % i, [128, 128], F32) for i in range(2)]
    eoc = [sb("eoc%d" % i, [128, 256], F32) for i in range(2)]
    fw3f = sb("fw3f", [64, 1024], BF16)

    pSS = ps("pSS", [128, 512], F32)
    pA1 = [ps("pA1_%d" % i, [128, 512], F32) for i in range(2)]
    pXr, pXi = ps("pXr", [128, 512], F32), ps("pXi", [128, 512], F32)
    pB = ps("pB", [128, 512], F32)
    pY = ps("pY", [128, 512], F32)
    pTz = ps("pTz", [128, 8, 128], BF16)

    def ldcast(dst, dkey, src, n=512, parts=128):
        P.dma("sp", cpy2[0:parts, 0:n], src, writes=["cpy2"])
        P.op("dve", lambda e: e.tensor_copy(out=dst, in_=cpy2[0:parts, 0:n]), reads=["cpy2"], writes=[dkey])
    ldcast(F1b[:, :], "F1b", dF1cat[:, :])
    ldcast(R12b[:, :], "R12b", dR12[:, :])
    ldcast(IF1b[:, :], "IF1b", dIF1[:, :])
    ldcast(ident[:, :], "ident", didn[:, :], n=128)
    ldcast(selb[:, :], "selb", dsel[:, :], n=32)
    for q4 in range(4):
        P.dma("sp", cpy2[0:64, :], dfw3[:, q4 * 512:(q4 + 1) * 512], writes=["cpy2"])
        for o_ in range(2):
            wf = cpy2[0:64, o_ * 256:o_ * 256 + 128]
            wbk = cpy2[0:64, o_ * 256 + 128:o_ * 256 + 256]
            c0_ = q4 * 512 + o_ * 256
            P.op("dve", lambda e, wf=wf, wbk=wbk, c0_=c0_: e.tensor_tensor(out=fw3b[:, c0_:c0_ + 128], in0=wf, in1=wbk, op=ALU.add), reads=["cpy2"], writes=["fw3b"])
            P.op("dve", lambda e, wf=wf, wbk=wbk, c0_=c0_: e.tensor_tensor(out=fw3b[:, c0_ + 128:c0_ + 256], in0=wf, in1=wbk, op=ALU.subtract), reads=["cpy2"], writes=["fw3b"])
            P.op("dve", lambda e, wf=wf, q4=q4, o_=o_: e.tensor_copy(out=fw3f[:, (2 * q4 + o_) * 128:(2 * q4 + o_ + 1) * 128], in_=wf), reads=["cpy2"], writes=["fw3f"])
    for dst, key, src in ((TrTr, "TrTr", dTrTr), (TiTi, "TiTi", dTiTi), (TT2r, "TT2r", dTT2r), (TT2i, "TT2i", dTT2i), (Swin, "Swin", dSwin),
                          (delta, "delta", ddelta), (hbcol, "hbcol", dhbcol), (fw1, "fw1", dfw1), (fw2, "fw2", dfw2), (cw, "cw", dcw), (cb, "cb", dcb)):
        P.dma("sp", dst[:, :], src[:, :], writes=[key])
    P.dma("sp", fpar[:, 0:4], dfpar[:, :], writes=["fpar"])
    i2p = 1.0 / (2.0 * math.pi)
    for (bc, fc, o0) in ((0, 1, 4), (2, 3, 6)):
        P.op("dve", lambda e, bc=bc, fc=fc, o0=o0: e.tensor_tensor(out=fpar[:, o0 + 1:o0 + 2], in0=fpar[:, bc:bc + 1], in1=fpar[:, fc:fc + 1], op=ALU.mult),
             reads=["fpar"], writes=["fpar"])
        P.op("dve", lambda e, o0=o0: e.tensor_scalar(out=fpar[:, o0 + 1:o0 + 2], in0=fpar[:, o0 + 1:o0 + 2], scalar1=i2p, scalar2=16.0, op0=ALU.mult, op1=ALU.add),
             reads=["fpar"], writes=["fpar"])
        P.op("dve", lambda e, fc=fc, o0=o0: e.tensor_scalar(out=fpar[:, o0:o0 + 1], in0=fpar[:, fc:fc + 1], scalar1=i2p, scalar2=None, op0=ALU.mult),
             reads=["fpar"], writes=["fpar"])

    for ch in range(16):
        zt, zk = zc[ch % 2], "zc%d" % (ch % 2)
        P.dma("sp", zt[:, :], dzemb[:, ch * 512:(ch + 1) * 512], writes=[zk])
        for layer in range(2):
            if layer == 0:
                P.op("pe", lambda e, zt=zt: e.matmul(pSS[0:64, :], lhsT=fw1[:, :], rhs=zt[:, :], start=True, stop=True), reads=["fw1", zk], writes=["pSS"])
            else:
                P.op("pe", lambda e: e.matmul(pSS[0:64, :], lhsT=fw2[:, :], rhs=h1[:, :], start=True, stop=True), reads=["fw2", "tmpc"], writes=["pSS"])
            fr, fb = (4, 5) if layer == 0 else (6, 7)
            P.op("dve", lambda e, fr=fr, fb=fb: e.tensor_scalar(out=arg[:, :], in0=pSS[0:64, :], scalar1=fpar[:, fr:fr + 1], scalar2=fpar[:, fb:fb + 1],
                                                                op0=ALU.mult, op1=ALU.add), reads=["pSS", "fpar"], writes=["tmpc"])
            P.op("dve", lambda e: e.tensor_copy(out=argi[:, :], in_=arg[:, :]), reads=["tmpc"], writes=["argi"])
            P.op("dve", lambda e: e.tensor_copy(out=h1[:, :], in_=argi[:, :]), reads=["argi", "tmpc"], writes=["tmpc"])
            P.op("dve", lambda e: e.tensor_tensor(out=arg[:, :], in0=arg[:, :], in1=h1[:, :], op=ALU.subtract), reads=["tmpc"], writes=["tmpc"])
            P.op("dve", lambda e: e.scalar_tensor_tensor(out=arg[:, :], in0=arg[:, :], scalar=0.5, in1=arg[:, :], op0=ALU.is_gt, op1=ALU.subtract),
                 reads=["tmpc"], writes=["tmpc"])
            if layer == 0:
                P.op("act", lambda e: e.activation(out=h1[:, :], in_=arg[:, :], func=AF.Sin, scale=-2.0 * math.pi), reads=["tmpc"], writes=["tmpc"])
            else:
                P.op("act", lambda e, ch=ch: e.activation(out=h2T[:, ch * 512:(ch + 1) * 512], in_=arg[:, :], func=AF.Sin, scale=-2.0 * math.pi),
                     reads=["tmpc"], writes=["h2T"])

    Zkeys = lambda i: ["Z%d_%d" % (i, s) for s in range(128 // (2 * NPQ))]
    Z = [U[i][:, 0:NTOK].rearrange("a (c p) -> a c p", p=64) for i in range(3)]
    CVs = CV[:, :].rearrange("c (a p) -> c a p", p=64)
    E3 = CV[:, :].rearrange("a (c p) -> a c p", p=64)
    O3 = Ob[:, :].rearrange("a (c p) -> a c p", p=64)
    h2s = h2T[:, :].rearrange("j (a p) -> j a p", p=64)
    cnt = [0]

    def twiddle(psrc, pkey, Tr_, Ti_, trk, tik, outr, outi, okey, view):
        g = cnt[0]
        cnt[0] += 1
        w1, w1k = W1[g % 2], "W1_%d" % (g % 2)
        w2, w2k = W2[g % 2], "W2_%d" % (g % 2)
        cp_, cpk = cpy[g % 2], "cpy%d" % (g % 2)
        P.op("act", lambda e: e.activation(out=cp_[:, :], in_=psrc[:, :], func=AF.Copy), reads=[pkey], writes=[cpk])
        P.op("dve", lambda e: e.tensor_tensor(out=w1[:, :], in0=psrc[:, :], in1=Tr_[:, :], op=ALU.mult), reads=[pkey, trk], writes=[w1k])
        P.op("pool", lambda e: e.tensor_tensor(out=w2[:, :], in0=cp_[:, :], in1=Ti_[:, :], op=ALU.mult), reads=[cpk, tik], writes=[w2k])
        w1r, w1i = view(w1)
        w2r, w2i = view(w2)
        P.op("dve", lambda e: e.tensor_tensor(out=outr, in0=w1r, in1=w2i, op=ALU.subtract), reads=[w1k, w2k], writes=[okey])
        P.op("dve", lambda e: e.tensor_tensor(out=outi, in0=w2r, in1=w1i, op=ALU.add), reads=[w1k, w2k], writes=[okey])

    v_fwd = lambda t: (t[:, 0:256], t[:, 256:512])
    v_inv = lambda t: (t[:, :].rearrange("k (c r x) -> k c r x", c=2, r=2)[:, :, 0, :], t[:, :].rearrange("k (c r x) -> k c r x", c=2, r=2)[:, :, 1, :])

    def fwd_stage1(src3, skey, c0, Ap, apk):
        for q in range(NPQ):
            g = cnt[0]
            pa, pak = pA1[g % 2], "pA1_%d" % (g % 2)
            c = c0 + 2 * q
            P.op("pe", lambda e, c=c, pa=pa: e.matmul(pa[:, :], lhsT=src3[:, c:c + 2, :], rhs=F1b[:, :], start=True, stop=True),
                 reads=[skey, "F1b"], writes=[pak])
            twiddle(pa, pak, TrTr, TiTi, "TrTr", "TiTi", Ap[:, 0, q, :], Ap[:, 1, q, :], apk, v_fwd)

    G2r, G2in, G2i = R12b[:, 0:128], R12b[:, 128:256], R12b[:, 256:384]
    R1, R2 = R12b[:, 0:256], R12b[:, 256:512]

    for grp4 in groups:
        for i in range(3):
            P.dma("sp", U[i][:, 1:NTOK + 1], Us[3 * grp4 + i], reads=["Us"], writes=["U%d" % i] + Zkeys(i), key="U%d" % i)
            P.op("pool", lambda e, i=i: e.memset(U[i][:, 0:1], 0.0), reads=["U%d" % i], writes=["U%d" % i] + Zkeys(i))
            P.op("pool", lambda e, i=i: e.memset(U[i][:, NTOK + 1:NTOK + 2], 0.0), reads=["U%d" % i], writes=["U%d" % i])
        for i in range(3):
            ci = 9 * grp4 + 3 * i
            bi = 3 * grp4 + i
            for ch in range(8):
                j0 = ch * 1024
                P.op("dve", lambda e, i=i, j0=j0, ci=ci, bi=bi: e.tensor_scalar(out=tmpc[:, :], in0=U[i][:, j0:j0 + 1024], scalar1=cw[:, ci:ci + 1],
                                                                                scalar2=cb[:, bi:bi + 1], op0=ALU.mult, op1=ALU.add),
                     reads=["U%d" % i, "cw", "cb"], writes=["tmpc"])
                P.op("dve", lambda e, i=i, j0=j0, ci=ci: e.scalar_tensor_tensor(out=tmpc[:, :], in0=U[i][:, j0 + 1:j0 + 1025], scalar=cw[:, ci + 1:ci + 2],
                                                                                in1=tmpc[:, :], op0=ALU.mult, op1=ALU.add),
                     reads=["U%d" % i, "cw", "tmpc"], writes=["tmpc"])
                P.op("dve", lambda e, i=i, j0=j0, ci=ci: e.scalar_tensor_tensor(out=CV[:, j0:j0 + 1024], in0=U[i][:, j0 + 2:j0 + 1026], scalar=cw[:, ci + 2:ci + 3],
                                                                                in1=tmpc[:, :], op0=ALU.mult, op1=ALU.add),
                     reads=["U%d" % i, "cw", "tmpc"], writes=["CV"])
            for pg in range(8):
                for pi in range(8):
                    p = pg * 8 + pi
                    P.op("pe", lambda e, p=p, pi=pi: e.transpose(out=pTz[:, pi, :], in_=CVs[:, :, p], identity=ident[:, :]),
                         reads=["CV", "ident"], writes=["pTz"])
                P.op("act", lambda e, i=i, pg=pg: e.activation(out=Z[i][:, :, pg * 8:(pg + 1) * 8].rearrange("a c p -> a p c"), in_=pTz[:, :, :], func=AF.Copy),
                     reads=["pTz"], writes=["U%d" % i] + Zkeys(i))
        for o in range(2):
            w3c0 = grp4 * 512 + o * 256
            for p in range(64):
                pe_, pek = (pSS, "pSS") if p % 2 == 0 else (pB, "pB")
                wn, wnk = win[p % 2], "win%d" % (p % 2)
                ec, eck = eoc[p % 2], "eoc%d" % (p % 2)
                P.op("pe", lambda e, p=p, w3c0=w3c0, pe_=pe_: e.matmul(pe_[:, 0:256], lhsT=h2s[:, :, p], rhs=fw3b[:, w3c0:w3c0 + 256], start=True, stop=True),
                     reads=["h2T", "fw3b"], writes=[pek])
                P.op("act", lambda e, p=p, grp4=grp4, wn=wn: e.activation(out=wn[:, :], in_=delta[:, grp4 * 128:(grp4 + 1) * 128], func=AF.Exp, scale=Swin[:, p:p + 1]),
                     reads=["delta", "Swin"], writes=[wnk])
                P.op("act", lambda e, pe_=pe_, ec=ec: e.activation(out=ec[:, :], in_=pe_[:, 0:256], func=AF.Copy), reads=[pek], writes=[eck])
                P.op("pool", lambda e, p=p, ec=ec, wn=wn: e.tensor_tensor(out=E3[:, :, p], in0=ec[:, 0:128], in1=wn[:, :], op=ALU.mult), reads=[eck, wnk], writes=["CV"])
                P.op("pool", lambda e, p=p, ec=ec, wn=wn: e.tensor_tensor(out=O3[:, :, p], in0=ec[:, 128:256], in1=wn[:, :], op=ALU.mult), reads=[eck, wnk], writes=["Ob"])
                if p == 0:
                    fc = (2 * grp4 + o) * 128
                    P.op("pe", lambda e, fc=fc: e.matmul(pY[0:1, 0:128], lhsT=h2T[:, 0:1], rhs=fw3f[:, fc:fc + 128], start=True, stop=True),
                         reads=["h2T", "fw3f"], writes=["pY"])
                    P.op("dve", lambda e, wn=wn: e.tensor_tensor(out=E3[0:1, :, 0], in0=pY[0:1, 0:128], in1=wn[0:1, :], op=ALU.mult), reads=["pY", wnk], writes=["CV"])
                    P.op("dve", lambda e, wn=wn: e.tensor_tensor(out=O3[0:1, :, 0], in0=pY[0:1, 0:128], in1=wn[0:1, :], op=ALU.mult), reads=["pY", wnk], writes=["Ob"])
            nsbt = 128 // (2 * NPQ)

            def filt_gen(sbt):
                c0 = sbt * 2 * NPQ
                Hs, hk = HsL[sbt % 2], "Hs%d" % (sbt % 2)
                for which, (src3, skey) in enumerate(((E3, "CV"), (O3, "Ob"))):
                    fwd_stage1(src3, skey, c0, ApF, "ApF")
                    yield
                    for qq in range(NPQ // 2):
                        rr = ApF[:, 0, 2 * qq:2 * qq + 2, :]
                        ri = ApF[:, 1, 2 * qq:2 * qq + 2, :]
                        if which == 0:
                            P.op("pe", lambda e, rr=rr: e.matmul(pSS[:, :], lhsT=G2r, rhs=rr, start=True, stop=False), reads=["R12b", "ApF"], writes=["pSS"])
                            P.op("pe", lambda e, ri=ri: e.matmul(pSS[:, :], lhsT=G2in, rhs=ri, start=False, stop=True), reads=["R12b", "ApF"], writes=["pSS"])
                            for j in range(2):
                                hcol = grp4 * 128 + o * 64 + sbt * NPQ + 2 * qq + j
                                P.op("act", lambda e, j=j, qq=qq, hcol=hcol, Hs=Hs: e.activation(out=Hs[:, 0, 2 * qq + j, :], in_=pSS[:, j * 256:(j + 1) * 256], func=AF.Identity,
                                                                                                 bias=hbcol[:, hcol:hcol + 1], scale=1.0),
                                     reads=["pSS", "hbcol"], writes=[hk])
                        else:
                            P.op("pe", lambda e, rr=rr: e.matmul(pSS[:, :], lhsT=G2i, rhs=rr, start=True, stop=False), reads=["R12b", "ApF"], writes=["pSS"])
                            P.op("pe", lambda e, ri=ri: e.matmul(pSS[:, :], lhsT=G2r, rhs=ri, start=False, stop=True), reads=["R12b", "ApF"], writes=["pSS"])
                            P.op("act", lambda e, qq=qq, Hs=Hs: e.activation(out=Hs[:, 1, 2 * qq:2 * qq + 2, :], in_=pSS[:, :].rearrange("k (q x) -> k q x", q=2), func=AF.Copy),
                                 reads=["pSS"], writes=[hk])
                    yield

            def data_gen(sbt):
                c0 = sbt * 2 * NPQ
                zk = "Z0_%d" % sbt
                Hs, hk = HsL[sbt % 2], "Hs%d" % (sbt % 2)
                fwd_stage1(Z[0], zk, c0, ApD, "ApD")
                yield
                for qq in range(NPQ // 2):
                    rr = ApD[:, 0, 2 * qq:2 * qq + 2, :]
                    ri = ApD[:, 1, 2 * qq:2 * qq + 2, :]
                    P.op("pe", lambda e, rr=rr: e.matmul(pXr[:, :], lhsT=G2r, rhs=rr, start=True, stop=False), reads=["R12b", "ApD"], writes=["pXr"])
                    P.op("pe", lambda e, ri=ri: e.matmul(pXr[:, :], lhsT=G2in, rhs=ri, start=False, stop=True), reads=["R12b", "ApD"], writes=["pXr"])
                    P.op("pe", lambda e, rr=rr: e.matmul(pXi[:, :], lhsT=G2i, rhs=rr, start=True, stop=False), reads=["R12b", "ApD"], writes=["pXi"])
                    P.op("pe", lambda e, ri=ri: e.matmul(pXi[:, :], lhsT=G2r, rhs=ri, start=False, stop=True), reads=["R12b", "ApD"], writes=["pXi"])
                    g = cnt[0]
                    cnt[0] += 1
                    w1, w1k = W1[g % 2], "W1_%d" % (g % 2)
                    w2, w2k = W2[g % 2], "W2_%d" % (g % 2)
                    cx, cxk = cpy[g % 2], "cpy%d" % (g % 2)
                    hr = Hs[:, 0, 2 * qq:2 * qq + 2, :].rearrange("k q x -> k (q x)")
                    hi = Hs[:, 1, 2 * qq:2 * qq + 2, :].rearrange("k q x -> k (q x)")
                    yr = Yb[:, 0, 2 * qq:2 * qq + 2, :].rearrange("k q x -> k (q x)")
                    yi = Yb[:, 1, 2 * qq:2 * qq + 2, :].rearrange("k q x -> k (q x)")
                    P.op("act", lambda e, cx=cx: e.activation(out=cx[:, :], in_=pXi[:, :], func=AF.Copy), reads=["pXi"], writes=[cxk])
                    P.op("act", lambda e: e.activation(out=cpy2[:, :], in_=pXr[:, :], func=AF.Copy), reads=["pXr"], writes=["cpy2"])
                    P.op("dve", lambda e, w1=w1, hr=hr: e.tensor_tensor(out=w1[:, :], in0=pXr[:, :], in1=hr, op=ALU.mult), reads=["pXr", hk], writes=[w1k])
                    P.op("pool", lambda e, w2=w2, cx=cx, hi=hi: e.tensor_tensor(out=w2[:, :], in0=cx[:, :], in1=hi, op=ALU.mult), reads=[cxk, hk], writes=[w2k])
                    P.op("dve", lambda e, w1=w1, w2=w2, yr=yr: e.tensor_tensor(out=yr, in0=w1[:, :], in1=w2[:, :], op=ALU.subtract), reads=[w1k, w2k], writes=["Yb"])
                    P.op("dve", lambda e, w1=w1, hr=hr: e.tensor_tensor(out=w1[:, :], in0=pXi[:, :], in1=hr, op=ALU.mult), reads=["pXi", hk, "Yb"], writes=[w1k])
                    P.op("pool", lambda e, w2=w2, hi=hi: e.tensor_tensor(out=w2[:, :], in0=cpy2[:, :], in1=hi, op=ALU.mult), reads=["cpy2", hk, "Yb"], writes=[w2k])
                    P.op("dve", lambda e, w1=w1, w2=w2, yi=yi: e.tensor_tensor(out=yi, in0=w1[:, :], in1=w2[:, :], op=ALU.add), reads=[w1k, w2k], writes=["Yb"])
                yield
                for q in range(NPQ):
                    for kc in range(2):
                        P.op("pe", lambda e, q=q, kc=kc: e.matmul(pB[:, kc * 256:(kc + 1) * 256], lhsT=Yb[:, 0, q, kc * 128:(kc + 1) * 128], rhs=R1, start=True, stop=False),
                             reads=["Yb", "R12b"], writes=["pB"])
                        P.op("pe", lambda e, q=q, kc=kc: e.matmul(pB[:, kc * 256:(kc + 1) * 256], lhsT=Yb[:, 1, q, kc * 128:(kc + 1) * 128], rhs=R2, start=False, stop=True),
                             reads=["Yb", "R12b"], writes=["pB"])
                    twiddle(pB, "pB", TT2r, TT2i, "TT2r", "TT2i", Bp[:, :, 0, q, :], Bp[:, :, 1, q, :], "Bp", v_inv)
                yield
                for hh in range(NPQ // 4):
                    n = 0
                    for kc in range(2):
                        for r in range(2):
                            rhs = Bp[:, kc, r, 4 * hh:4 * hh + 4, :]
                            lt = IF1b[:, (2 * kc + r) * 128:(2 * kc + r + 1) * 128]
                            P.op("pe", lambda e, rhs=rhs, lt=lt, n=n: e.matmul(pY[:, :], lhsT=lt, rhs=rhs, start=(n == 0), stop=(n == 3)),
                                 reads=["Bp", "IF1b"], writes=["pY"])
                            n += 1
                    cc = c0 + 8 * hh
                    zi = 1 if o == 0 else 2
                    zo = 0 if o == 0 else 2
                    okeys = [zk] if o == 0 else ["U2", "Z2_%d" % sbt]
                    P.op("dve", lambda e, cc=cc, zi=zi, zo=zo: e.scalar_tensor_tensor(out=Z[zo][:, cc:cc + 8, :], in0=pY[:, :].rearrange("a (c p) -> a c p", p=64),
                                                                                      scalar=1.0 / NFFT, in1=Z[zi][:, cc:cc + 8, :], op0=ALU.mult, op1=ALU.mult),
                         reads=["pY", "U%d" % zi] + Zkeys(zi), writes=okeys)
                yield

            for _ in filt_gen(0):
                pass
            for sbt in range(nsbt):
                gens = [data_gen(sbt)] + ([filt_gen(sbt + 1)] if sbt + 1 < nsbt else [])
                while gens:
                    for gq in list(gens):
                        try:
                            next(gq)
                        except StopIteration:
                            gens.remove(gq)
        mv = mixT[:, 4 + grp4, :].rearrange("c (a p) -> c a p", p=64)
        for pg in range(4):
            for pi in range(16):
                p = pg * 16 + pi
                P.op("pe", lambda e, p=p, pi=pi: e.matmul(pXr[:, pi * 32:(pi + 1) * 32], lhsT=Z[2][:, :, p], rhs=selb[:, :], start=True, stop=True),
                     reads=["U2", "selb"] + Zkeys(2), writes=["pXr"])
            P.op("act", lambda e, pg=pg, mv=mv: e.activation(out=mv[:, :, pg * 16:(pg + 1) * 16].rearrange("c a p -> c p a"),
                                                             in_=pXr[:, :].rearrange("c (p a) -> c p a", a=32), func=AF.Copy),
                 reads=["pXr"], writes=["mixT"])
    P.build()


def emit_hyena_h(nc, outer, D, mixT, groups=(0, 1, 2, 3), prefix="C_"):
    P = Prog(nc, sem_stack=outer, prefix=prefix)
    sb, ps = P.sb, P.ps
    Us = D("Us", [12, 128, NTOK], kind="Internal", dt=BF16)
    dF1cat, dTrTr, dTiTi, dR12 = D("F1cat_h", [128, 256]), D("TrTr_h", [128, 512]), D("TiTi_h", [128, 512]), D("R12", [128, 512])
    dTT2r, dTT2i, dIF1, dSwin = D("TT2r_h", [128, 512]), D("TT2i_h", [128, 512]), D("IF1_h", [128, 256]), D("Swin", [128, 64])
    dzemb = D("zembT", [33, NTOK])
    ddelta = D("delta4", [128, 512])
    dhbrow = D("hbrow", [1, 1024])
    dfw1, dfw2, dfw3, dfpar = D("fw1", [33, 64]), D("fw2", [64, 64]), D("fw3c4", [64, 2048]), D("fpar", [64, 4])
    dcw, dcb = D("cw4", [128, 36]), D("cb4", [128, 12])
    dsel = D("seld", [128, 32])
    didn = D("identd", [128, 128])

    U = [sb("U%d" % i, [128, NTOK + 2], BF16) for i in range(3)]
    CV = sb("CV", [128, NTOK], BF16)
    Ob = sb("Ob", [128, NTOK], BF16)
    h2T = sb("h2T", [64, NTOK], BF16)
    ApF = sb("ApF", [128, 2, NPQH, 128], BF16)
    ApD = sb("ApD", [128, 2, NPQH, 128], BF16)
    HsL = [sb("Hs%d" % i, [128, 2, NPQH, 128], BF16) for i in range(2)]
    Yb = sb("Yb", [128, 2, NPQH, 128], BF16)
    Bp = sb("Bp", [128, 2, NPQH, 128], BF16)
    W1 = [sb("W1_%d" % i, [128, 512], BF16) for i in range(4)]
    W2 = [sb("W2_%d" % i, [128, 512], BF16) for i in range(4)]
    cpy = [sb("cpr%d" % i, [128, 512], BF16) for i in range(4)]
    cpy2 = sb("cpy2", [128, 512], F32)
    tmpc = sb("tmpc", [128, 1024], F32)
    arg = tmpc[0:64, 0:512]
    h1 = tmpc[0:64, 512:1024]
    argi = sb("argi", [64, 512], mybir.dt.int32)
    F1b, R12b, IF1b = sb("F1b", [128, 256], BF16), sb("R12b", [128, 512], BF16), sb("IF1b", [128, 256], BF16)
    TrTr, TiTi = sb("TrTr", [128, 512], F32), sb("TiTi", [128, 512], F32)
    TT2r, TT2i = sb("TT2r", [128, 512], F32), sb("TT2i", [128, 512], F32)
    Swin = sb("Swin", [128, 64], F32)
    delta = sb("delta", [128, 512], F32)
    hbrow = sb("hbrow", [1, 1024], F32)
    etmp = sb("etmp", [1, 128], F32)
    fw1, fw2 = sb("fw1", [33, 64], F32), sb("fw2", [64, 64], F32)
    fw3b = sb("fw3b", [64, 2048], BF16)
    fpar = sb("fpar", [64, 8], F32)
    cw, cb = sb("cw", [128, 36], F32), sb("cb", [128, 12], F32)
    selb = sb("selb", [128, 32], BF16)
    ident = sb("ident", [128, 128], BF16)
    zc = [sb("zc%d" % i, [33, 512], F32) for i in range(2)]
    win = [sb("win%d" % i, [128, 128], F32) for i in range(2)]
    eoc = [sb("eoc%d" % i, [128, 256], F32) for i in range(2)]
    fw3f = sb("fw3f", [64, 1024], BF16)

    pSS = ps("pSS", [128, 512], F32)
    pA1 = [ps("pA1_%d" % i, [128, 512], F32) for i in range(2)]
    pXr, pXi = ps("pXr", [128, 512], F32), ps("pXi", [128, 512], F32)
    pB = ps("pB", [128, 512], F32)
    pY = ps("pY", [128, 512], F32)
    pTz = ps("pTz", [128, 8, 128], BF16)

    def ldcast(dst, dkey, src, n=512, parts=128):
        P.dma("sp", cpy2[0:parts, 0:n], src, writes=["cpy2"])
        P.op("dve", lambda e: e.tensor_copy(out=dst, in_=cpy2[0:parts, 0:n]), reads=["cpy2"], writes=[dkey])
    ldcast(F1b[:, :], "F1b", dF1cat[:, :], n=256)
    ldcast(R12b[:, :], "R12b", dR12[:, :])
    ldcast(IF1b[:, :], "IF1b", dIF1[:, :], n=256)
    ldcast(ident[:, :], "ident", didn[:, :], n=128)
    ldcast(selb[:, :], "selb", dsel[:, :], n=32)
    for q4 in range(4):
        P.dma("sp", cpy2[0:64, :], dfw3[:, q4 * 512:(q4 + 1) * 512], writes=["cpy2"])
        for o_ in range(2):
            wf = cpy2[0:64, o_ * 256:o_ * 256 + 128]
            wbk = cpy2[0:64, o_ * 256 + 128:o_ * 256 + 256]
            c0_ = q4 * 512 + o_ * 256
            P.op("dve", lambda e, wf=wf, wbk=wbk, c0_=c0_: e.tensor_tensor(out=fw3b[:, c0_:c0_ + 128], in0=wf, in1=wbk, op=ALU.add), reads=["cpy2"], writes=["fw3b"])
            P.op("dve", lambda e, wf=wf, wbk=wbk, c0_=c0_: e.tensor_tensor(out=fw3b[:, c0_ + 128:c0_ + 256], in0=wf, in1=wbk, op=ALU.subtract), reads=["cpy2"], writes=["fw3b"])
            P.op("dve", lambda e, wf=wf, q4=q4, o_=o_: e.tensor_copy(out=fw3f[:, (2 * q4 + o_) * 128:(2 * q4 + o_ + 1) * 128], in_=wf), reads=["cpy2"], writes=["fw3f"])
    for dst, key, src in ((TrTr, "TrTr", dTrTr), (TiTi, "TiTi", dTiTi), (TT2r, "TT2r", dTT2r), (TT2i, "TT2i", dTT2i), (Swin, "Swin", dSwin),
                          (delta, "delta", ddelta), (hbrow, "hbrow", dhbrow), (fw1, "fw1", dfw1), (fw2, "fw2", dfw2), (cw, "cw", dcw), (cb, "cb", dcb)):
        P.dma("sp", dst[:, :], src[:, :], writes=[key])
    P.dma("sp", fpar[:, 0:4], dfpar[:, :], writes=["fpar"])
    i2p = 1.0 / (2.0 * math.pi)
    for (bc, fc, o0) in ((0, 1, 4), (2, 3, 6)):
        P.op("dve", lambda e, bc=bc, fc=fc, o0=o0: e.tensor_tensor(out=fpar[:, o0 + 1:o0 + 2], in0=fpar[:, bc:bc + 1], in1=fpar[:, fc:fc + 1], op=ALU.mult),
             reads=["fpar"], writes=["fpar"])
        P.op("dve", lambda e, o0=o0: e.tensor_scalar(out=fpar[:, o0 + 1:o0 + 2], in0=fpar[:, o0 + 1:o0 + 2], scalar1=i2p, scalar2=16.0, op0=ALU.mult, op1=ALU.add),
             reads=["fpar"], writes=["fpar"])
        P.op("dve", lambda e, fc=fc, o0=o0: e.tensor_scalar(out=fpar[:, o0:o0 + 1], in0=fpar[:, fc:fc + 1], scalar1=i2p, scalar2=None, op0=ALU.mult),
             reads=["fpar"], writes=["fpar"])

    for ch in range(16):
        zt, zk = zc[ch % 2], "zc%d" % (ch % 2)
        P.dma("sp", zt[:, :], dzemb[:, ch * 512:(ch + 1) * 512], writes=[zk])
        for layer in range(2):
            if layer == 0:
                P.op("pe", lambda e, zt=zt: e.matmul(pSS[0:64, :], lhsT=fw1[:, :], rhs=zt[:, :], start=True, stop=True), reads=["fw1", zk], writes=["pSS"])
            else:
                P.op("pe", lambda e: e.matmul(pSS[0:64, :], lhsT=fw2[:, :], rhs=h1[:, :], start=True, stop=True), reads=["fw2", "tmpc"], writes=["pSS"])
            fr, fb = (4, 5) if layer == 0 else (6, 7)
            P.op("dve", lambda e, fr=fr, fb=fb: e.tensor_scalar(out=arg[:, :], in0=pSS[0:64, :], scalar1=fpar[:, fr:fr + 1], scalar2=fpar[:, fb:fb + 1],
                                                                op0=ALU.mult, op1=ALU.add), reads=["pSS", "fpar"], writes=["tmpc"])
            P.op("dve", lambda e: e.tensor_copy(out=argi[:, :], in_=arg[:, :]), reads=["tmpc"], writes=["argi"])
            P.op("dve", lambda e: e.tensor_copy(out=h1[:, :], in_=argi[:, :]), reads=["argi", "tmpc"], writes=["tmpc"])
            P.op("dve", lambda e: e.tensor_tensor(out=arg[:, :], in0=arg[:, :], in1=h1[:, :], op=ALU.subtract), reads=["tmpc"], writes=["tmpc"])
            P.op("dve", lambda e: e.scalar_tensor_tensor(out=arg[:, :], in0=arg[:, :], scalar=0.5, in1=arg[:, :], op0=ALU.is_gt, op1=ALU.subtract),
                 reads=["tmpc"], writes=["tmpc"])
            if layer == 0:
                P.op("act", lambda e: e.activation(out=h1[:, :], in_=arg[:, :], func=AF.Sin, scale=-2.0 * math.pi), reads=["tmpc"], writes=["tmpc"])
            else:
                P.op("act", lambda e, ch=ch: e.activation(out=h2T[:, ch * 512:(ch + 1) * 512], in_=arg[:, :], func=AF.Sin, scale=-2.0 * math.pi),
                     reads=["tmpc"], writes=["h2T"])

    Zkeys = lambda i: ["Z%d_%d" % (i, s) for s in range(128 // (2 * NPQH))]
    Z = [U[i][:, 0:NTOK].rearrange("a (c p) -> a c p", p=64) for i in range(3)]
    CVs = CV[:, :].rearrange("c (a p) -> c a p", p=64)
    E3 = CV[:, :].rearrange("a (c p) -> a c p", p=64)
    O3 = Ob[:, :].rearrange("a (c p) -> a c p", p=64)
    h2s = h2T[:, :].rearrange("j (a p) -> j a p", p=64)
    cnt = [0]

    def twiddle(psrc, pkey, Tr_, Ti_, trk, tik, outr, outi, okey, view, ieng="dve"):
        g = cnt[0]
        cnt[0] += 1
        w1, w1k = W1[g % 4], "W1_%d" % (g % 4)
        w2, w2k = W2[g % 4], "W2_%d" % (g % 4)
        cp_, cpk = cpy[g % 4], "cpr%d" % (g % 4)
        P.op("act", lambda e: e.activation(out=cp_[:, :], in_=psrc[:, :], func=AF.Copy), reads=[pkey], writes=[cpk])
        P.op("dve", lambda e: e.tensor_tensor(out=w1[:, :], in0=psrc[:, :], in1=Tr_[:, :], op=ALU.mult), reads=[pkey, trk], writes=[w1k])
        P.op("pool", lambda e: e.tensor_tensor(out=w2[:, :], in0=cp_[:, :], in1=Ti_[:, :], op=ALU.mult), reads=[cpk, tik], writes=[w2k])
        w1r, w1i = view(w1)
        w2r, w2i = view(w2)
        P.op("dve", lambda e: e.tensor_tensor(out=outr, in0=w1r, in1=w2i, op=ALU.subtract), reads=[w1k, w2k], writes=[okey])
        P.op(ieng, lambda e: e.tensor_tensor(out=outi, in0=w2r, in1=w1i, op=ALU.add), reads=[w1k, w2k], writes=[okey])

    vv = lambda t: t[:, :].rearrange("k (j r x) -> k j r x", j=2, r=2)
    v_fwd = lambda t: (vv(t)[:, :, 0, :], vv(t)[:, :, 1, :])
    v_inv = v_fwd

    def fwd_stage1(src3, skey, c0, Ap, apk, ieng="dve"):
        for q2 in range(NPQH // 2):
            g = cnt[0]
            pa, pak = pA1[g % 2], "pA1_%d" % (g % 2)
            for jj in range(2):
                c = c0 + 2 * (2 * q2 + jj)
                P.op("pe", lambda e, c=c, pa=pa, jj=jj: e.matmul(pa[:, jj * 256:(jj + 1) * 256], lhsT=src3[:, c:c + 2, :], rhs=F1b[:, :], start=True, stop=True),
                     reads=[skey, "F1b"], writes=[pak])
            twiddle(pa, pak, TrTr, TiTi, "TrTr", "TiTi", Ap[:, 0, 2 * q2:2 * q2 + 2, :], Ap[:, 1, 2 * q2:2 * q2 + 2, :], apk, v_fwd, ieng=ieng)

    G2r, G2in, G2i = R12b[:, 0:128], R12b[:, 128:256], R12b[:, 256:384]
    R1, R2 = R12b[:, 0:256], R12b[:, 256:512]

    for grp4 in groups:
        for i in range(3):
            P.dma("sp", U[i][:, 1:NTOK + 1], Us[3 * grp4 + i], reads=["Us"], writes=["U%d" % i] + Zkeys(i), key="U%d" % i)
            P.op("pool", lambda e, i=i: e.memset(U[i][:, 0:1], 0.0), reads=["U%d" % i], writes=["U%d" % i] + Zkeys(i))
            P.op("pool", lambda e, i=i: e.memset(U[i][:, NTOK + 1:NTOK + 2], 0.0), reads=["U%d" % i], writes=["U%d" % i])
        for i in range(3):
            ci = 9 * grp4 + 3 * i
            bi = 3 * grp4 + i
            for ch in range(8):
                j0 = ch * 1024
                P.op("dve", lambda e, i=i, j0=j0, ci=ci, bi=bi: e.tensor_scalar(out=tmpc[:, :], in0=U[i][:, j0:j0 + 1024], scalar1=cw[:, ci:ci + 1],
                                                                                scalar2=cb[:, bi:bi + 1], op0=ALU.mult, op1=ALU.add),
                     reads=["U%d" % i, "cw", "cb"], writes=["tmpc"])
                P.op("dve", lambda e, i=i, j0=j0, ci=ci: e.scalar_tensor_tensor(out=tmpc[:, :], in0=U[i][:, j0 + 1:j0 + 1025], scalar=cw[:, ci + 1:ci + 2],
                                                                                in1=tmpc[:, :], op0=ALU.mult, op1=ALU.add),
                     reads=["U%d" % i, "cw", "tmpc"], writes=["tmpc"])
                P.op("dve", lambda e, i=i, j0=j0, ci=ci: e.scalar_tensor_tensor(out=CV[:, j0:j0 + 1024], in0=U[i][:, j0 + 2:j0 + 1026], scalar=cw[:, ci + 2:ci + 3],
                                                                                in1=tmpc[:, :], op0=ALU.mult, op1=ALU.add),
                     reads=["U%d" % i, "cw", "tmpc"], writes=["CV"])
            for pg in range(8):
                for pi in range(8):
                    p = pg * 8 + pi
                    P.op("pe", lambda e, p=p, pi=pi: e.transpose(out=pTz[:, pi, :], in_=CVs[:, :, p], identity=ident[:, :]),
                         reads=["CV", "ident"], writes=["pTz"])
                P.op("act", lambda e, i=i, pg=pg: e.activation(out=Z[i][:, :, pg * 8:(pg + 1) * 8].rearrange("a c p -> a p c"), in_=pTz[:, :, :], func=AF.Copy),
                     reads=["pTz"], writes=["U%d" % i] + Zkeys(i))
        for o in range(2):
            w3c0 = grp4 * 512 + o * 256
            for p in range(64):
                pe_, pek = (pSS, "pSS") if p % 2 == 0 else (pB, "pB")
                wn, wnk = win[p % 2], "win%d" % (p % 2)
                ec, eck = eoc[p % 2], "eoc%d" % (p % 2)
                P.op("pe", lambda e, p=p, w3c0=w3c0, pe_=pe_: e.matmul(pe_[:, 0:256], lhsT=h2s[:, :, p], rhs=fw3b[:, w3c0:w3c0 + 256], start=True, stop=True),
                     reads=["h2T", "fw3b"], writes=[pek])
                P.op("act", lambda e, p=p, grp4=grp4, wn=wn: e.activation(out=wn[:, :], in_=delta[:, grp4 * 128:(grp4 + 1) * 128], func=AF.Exp, scale=Swin[:, p:p + 1]),
                     reads=["delta", "Swin"], writes=[wnk])
                P.op("act", lambda e, pe_=pe_, ec=ec: e.activation(out=ec[:, :], in_=pe_[:, 0:256], func=AF.Copy), reads=[pek], writes=[eck])
                P.op("pool", lambda e, p=p, ec=ec, wn=wn: e.tensor_tensor(out=E3[:, :, p], in0=ec[:, 0:128], in1=wn[:, :], op=ALU.mult), reads=[eck, wnk], writes=["CV"])
                P.op("dve", lambda e, p=p, ec=ec, wn=wn: e.tensor_tensor(out=O3[:, :, p], in0=ec[:, 128:256], in1=wn[:, :], op=ALU.mult), reads=[eck, wnk], writes=["Ob"])
                if p == 0:
                    fc = (2 * grp4 + o) * 128
                    P.op("pe", lambda e, fc=fc: e.matmul(pY[0:1, 0:128], lhsT=h2T[:, 0:1], rhs=fw3f[:, fc:fc + 128], start=True, stop=True),
                         reads=["h2T", "fw3f"], writes=["pY"])
                    hc0 = (2 * grp4 + o) * 128
                    P.op("dve", lambda e, wn=wn: e.tensor_tensor(out=etmp[:, :], in0=pY[0:1, 0:128], in1=wn[0:1, :], op=ALU.mult), reads=["pY", wnk], writes=["etmp"])
                    P.op("dve", lambda e: e.tensor_copy(out=O3[0:1, :, 0], in_=etmp[:, :]), reads=["etmp"], writes=["Ob"])
                    P.op("dve", lambda e, hc0=hc0: e.tensor_tensor(out=E3[0:1, :, 0], in0=etmp[:, :], in1=hbrow[0:1, hc0:hc0 + 128], op=ALU.add),
                         reads=["etmp", "hbrow"], writes=["CV"])
            nsbt = 128 // (2 * NPQH)

            def filt_gen(sbt):
                c0 = sbt * 2 * NPQH
                Hs, hk = HsL[sbt % 2], "Hs%d" % (sbt % 2)
                for which, (src3, skey) in enumerate(((E3, "CV"), (O3, "Ob"))):
                    fwd_stage1(src3, skey, c0, ApF, "ApF")
                    yield
                    for qq in range(NPQH // 4):
                        rr = ApF[:, 0, 4 * qq:4 * qq + 4, :]
                        ri = ApF[:, 1, 4 * qq:4 * qq + 4, :]
                        if which == 0:
                            P.op("pe", lambda e, rr=rr: e.matmul(pSS[:, :], lhsT=G2r, rhs=rr, start=True, stop=False), reads=["R12b", "ApF"], writes=["pSS"])
                            P.op("pe", lambda e, ri=ri: e.matmul(pSS[:, :], lhsT=G2in, rhs=ri, start=False, stop=True), reads=["R12b", "ApF"], writes=["pSS"])
                        else:
                            P.op("pe", lambda e, rr=rr: e.matmul(pSS[:, :], lhsT=G2i, rhs=rr, start=True, stop=False), reads=["R12b", "ApF"], writes=["pSS"])
                            P.op("pe", lambda e, ri=ri: e.matmul(pSS[:, :], lhsT=G2r, rhs=ri, start=False, stop=True), reads=["R12b", "ApF"], writes=["pSS"])
                        P.op("act", lambda e, qq=qq, Hs=Hs, which=which: e.activation(out=Hs[:, which, 4 * qq:4 * qq + 4, :], in_=pSS[:, :].rearrange("k (q x) -> k q x", q=4), func=AF.Copy),
                             reads=["pSS"], writes=[hk])
                    yield

            def data_gen(sbt):
                c0 = sbt * 2 * NPQH
                zk = "Z0_%d" % sbt
                Hs, hk = HsL[sbt % 2], "Hs%d" % (sbt % 2)
                fwd_stage1(Z[0], zk, c0, ApD, "ApD", ieng="pool")
                yield
                for qq in range(NPQH // 4):
                    rr = ApD[:, 0, 4 * qq:4 * qq + 4, :]
                    ri = ApD[:, 1, 4 * qq:4 * qq + 4, :]
                    P.op("pe", lambda e, rr=rr: e.matmul(pXr[:, :], lhsT=G2r, rhs=rr, start=True, stop=False), reads=["R12b", "ApD"], writes=["pXr"])
                    P.op("pe", lambda e, ri=ri: e.matmul(pXr[:, :], lhsT=G2in, rhs=ri, start=False, stop=True), reads=["R12b", "ApD"], writes=["pXr"])
                    P.op("pe", lambda e, rr=rr: e.matmul(pXi[:, :], lhsT=G2i, rhs=rr, start=True, stop=False), reads=["R12b", "ApD"], writes=["pXi"])
                    P.op("pe", lambda e, ri=ri: e.matmul(pXi[:, :], lhsT=G2r, rhs=ri, start=False, stop=True), reads=["R12b", "ApD"], writes=["pXi"])
                    g = cnt[0]
                    cnt[0] += 1
                    w1, w1k = W1[g % 4], "W1_%d" % (g % 4)
                    w2, w2k = W2[g % 4], "W2_%d" % (g % 4)
                    cx, cxk = cpy[g % 4], "cpr%d" % (g % 4)
                    hr = Hs[:, 0, 4 * qq:4 * qq + 4, :].rearrange("k q x -> k (q x)")
                    hi = Hs[:, 1, 4 * qq:4 * qq + 4, :].rearrange("k q x -> k (q x)")
                    yr = Yb[:, 0, 4 * qq:4 * qq + 4, :].rearrange("k q x -> k (q x)")
                    yi = Yb[:, 1, 4 * qq:4 * qq + 4, :].rearrange("k q x -> k (q x)")
                    P.op("act", lambda e, cx=cx: e.activation(out=cx[:, :], in_=pXi[:, :], func=AF.Copy), reads=["pXi"], writes=[cxk])
                    P.op("act", lambda e: e.activation(out=cpy2[:, :], in_=pXr[:, :], func=AF.Copy), reads=["pXr"], writes=["cpy2"])
                    P.op("dve", lambda e, w1=w1, hr=hr: e.tensor_tensor(out=w1[:, :], in0=pXr[:, :], in1=hr, op=ALU.mult), reads=["pXr", hk], writes=[w1k])
                    P.op("pool", lambda e, w2=w2, cx=cx, hi=hi: e.tensor_tensor(out=w2[:, :], in0=cx[:, :], in1=hi, op=ALU.mult), reads=[cxk, hk], writes=[w2k])
                    P.op("dve", lambda e, w1=w1, w2=w2, yr=yr: e.tensor_tensor(out=yr, in0=w1[:, :], in1=w2[:, :], op=ALU.subtract), reads=[w1k, w2k], writes=["Yb"])
                    P.op("pool", lambda e, w1=w1, hr=hr, cx=cx: e.tensor_tensor(out=w1[:, :], in0=cx[:, :], in1=hr, op=ALU.mult), reads=[cxk, hk, "Yb"], writes=[w1k])
                    P.op("pool", lambda e, w2=w2, hi=hi: e.tensor_tensor(out=w2[:, :], in0=cpy2[:, :], in1=hi, op=ALU.mult), reads=["cpy2", hk, "Yb"], writes=[w2k])
                    P.op("dve", lambda e, w1=w1, w2=w2, yi=yi: e.tensor_tensor(out=yi, in0=w1[:, :], in1=w2[:, :], op=ALU.add), reads=[w1k, w2k], writes=["Yb"])
                yield
                for q2 in range(NPQH // 2):
                    for jj in range(2):
                        q = 2 * q2 + jj
                        P.op("pe", lambda e, q=q, jj=jj: e.matmul(pB[:, jj * 256:(jj + 1) * 256], lhsT=Yb[:, 0, q, :], rhs=R1, start=True, stop=False),
                             reads=["Yb", "R12b"], writes=["pB"])
                        P.op("pe", lambda e, q=q, jj=jj: e.matmul(pB[:, jj * 256:(jj + 1) * 256], lhsT=Yb[:, 1, q, :], rhs=R2, start=False, stop=True),
                             reads=["Yb", "R12b"], writes=["pB"])
                    twiddle(pB, "pB", TT2r, TT2i, "TT2r", "TT2i", Bp[:, 0, 2 * q2:2 * q2 + 2, :], Bp[:, 1, 2 * q2:2 * q2 + 2, :], "Bp", v_inv, ieng="pool")
                yield
                for hh in range(NPQH // 4):
                    for r in range(2):
                        rhs = Bp[:, r, 4 * hh:4 * hh + 4, :]
                        lt = IF1b[:, r * 128:(r + 1) * 128]
                        P.op("pe", lambda e, rhs=rhs, lt=lt, r=r: e.matmul(pY[:, :], lhsT=lt, rhs=rhs, start=(r == 0), stop=(r == 1)),
                             reads=["Bp", "IF1b"], writes=["pY"])
                    cc = c0 + 8 * hh
                    zi = 1 if o == 0 else 2
                    zo = 0 if o == 0 else 2
                    okeys = [zk] if o == 0 else ["U2", "Z2_%d" % sbt]
                    P.op("dve", lambda e, cc=cc, zi=zi, zo=zo: e.scalar_tensor_tensor(out=Z[zo][:, cc:cc + 8, :], in0=pY[:, :].rearrange("a (c p) -> a c p", p=64),
                                                                                      scalar=2.0 / NFFT, in1=Z[zi][:, cc:cc + 8, :], op0=ALU.mult, op1=ALU.mult),
                         reads=["pY", "U%d" % zi] + Zkeys(zi), writes=okeys)
                yield

            for _ in filt_gen(0):
                pass
            for sbt in range(nsbt):
                gens = [data_gen(sbt)] + ([filt_gen(sbt + 1)] if sbt + 1 < nsbt else [])
                while gens:
                    for gq in list(gens):
                        try:
                            next(gq)
                        except StopIteration:
                            gens.remove(gq)
        mv = mixT[:, 4 + grp4, :].rearrange("c (a p) -> c a p", p=64)
        for pg in range(4):
            for pi in range(16):
                p = pg * 16 + pi
                P.op("pe", lambda e, p=p, pi=pi: e.matmul(pXr[:, pi * 32:(pi + 1) * 32], lhsT=Z[2][:, :, p], rhs=selb[:, :], start=True, stop=True),
                     reads=["U2", "selb"] + Zkeys(2), writes=["pXr"])
            P.op("act", lambda e, pg=pg, mv=mv: e.activation(out=mv[:, :, pg * 16:(pg + 1) * 16].rearrange("c a p -> c p a"),
                                                             in_=pXr[:, :].rearrange("c (p a) -> c p a", a=32), func=AF.Copy),
                 reads=["pXr"], writes=["mixT"])
    P.build()


def emit_phase2_f(nc, outer, D, mixT, NT=2048):
    P = Prog(nc, sem_stack=outer, prefix="D_")
    x_tok = D("x_tok", [NT, 1024])
    cvec = D("cvec", [128, 8])
    w_ada = D("w_ada2", [1024, 4096])
    b_ada = D("b_ada2", [1, 4096])
    gvec = D("gvec", [3, 1024])
    gmix = D("gmix", [128, 8])
    w_out = D("w_out", [1024, 1024])
    w1 = D("w1", [1024, 4096])
    w2 = D("w2", [4096, 1024])
    out = D("out", [NT, 1024], kind="ExternalOutput")
    x1s = D("x1s", [NT, 1024], kind="Internal")

    sb, ps = P.sb, P.ps
    wout_b = sb("wout_b", [128, 8, 1024], BF16)
    mods = sb("mods", [128, 4096], F32)
    sbc = sb("sbc", [128, 8, 128], F32)
    ones = sb("ones", [128, 128], F32)
    ident = sb("ident", [128, 128], BF16)
    identf = sb("identf", [128, 128], F32)
    stage = [sb("stage%d" % i, [128, 2048], F32) for i in range(2)]
    w1b = [sb("w1b%d" % i, [128, 8, 512], BF16) for i in range(2)]
    w2b = [sb("w2b%d" % i, [128, 4, 1024], BF16) for i in range(2)]
    h2T = sb("h2T", [128, 8, 1024], BF16)
    f2acc = sb("f2acc", [128, 8, 1024], F32)
    brow = f2acc[0:1, 0:4, :].rearrange("p a b -> p (a b)")
    xt = [sb("xt%d" % i, [128, 1024], F32) for i in range(2)]
    sqm = sb("sqm", [128, 8, 128], BF16)
    onescol = sb("onescol", [128, 2], BF16)
    ysb = sb("ysb", [128, 1024], F32)
    tmp = sb("tmp", [128, 1024], F32)
    tmp2 = sb("tmp2", [128, 1024], F32)
    h2b = sb("h2b", [128, 1024], BF16)
    rl = [sb("rl%d" % i, [128, 512], F32) for i in range(2)]
    aT = [sb("aT%d" % i, [128, 4, 512], BF16) for i in range(2)]
    small = sb("small", [128, 32], F32)
    csb = sb("csb", [128, 8], F32)
    gmx = sb("gmx", [128, 8], F32)

    pA = ps("pA", [128, 1024], F32)
    pH = ps("pH", [128, 1024], F32)
    pT = ps("pT", [128, 1024], BF16)
    pM = [ps("pM%d" % i, [128, 512], F32) for i in range(2)]

    P.op("pool", lambda e: e.memset(ones[:, :], 1.0), writes=["ones"])
    P.op("pool", lambda e: e.memset(onescol[:, :], 1.0), writes=["onescol"])
    P.op("pool", lambda e: e.memset(identf[:, :], 0.0), writes=["identf"])
    identd = D("identd", [128, 128])
    P.dma("sp", identf[:, :], identd[:, :], writes=["identf"])
    P.op("dve", lambda e: e.tensor_copy(out=ident[:, :], in_=identf[:, :]), reads=["identf"], writes=["ident"])
    P.dma("sp", csb[:, :], cvec[:, :], writes=["csb"])
    P.dma("sp", gmx[:, :], gmix[:, :], writes=["gmx"])
    P.dma("sp", brow[:, :], b_ada[:, :], writes=["brow"])
    P.op("act", lambda e: e.activation(out=csb[:, :], in_=csb[:, :], func=AF.Silu), reads=["csb"], writes=["csb"])
    for k in range(8):
        P.op("dve", lambda e, k=k: e.tensor_scalar(out=sbc[:, k, :], in0=ones[:, :], scalar1=csb[:, k:k + 1],
                                                    scalar2=None, op0=ALU.mult), reads=["ones", "csb"], writes=["sbc"])
    wa = w_ada.rearrange("(k p) n -> p k n", p=128)
    for blk in range(16):
        st = stage[blk % 2]
        skey = "stage%d" % (blk % 2)
        stv = st[:, :].rearrange("p (k n) -> p k n", k=8)
        P.dma("sp", stv, wa[:, :, blk * 256:(blk + 1) * 256], writes=[skey])
        pm = pM[blk % 2]
        pkey = "pM%d" % (blk % 2)
        for k in range(8):
            P.op("pe", lambda e, k=k, pm=pm, stv=stv: e.matmul(pm[:, 0:256], lhsT=sbc[:, k, :], rhs=stv[:, k, :],
                                                               start=(k == 0), stop=False),
                 reads=["sbc", skey], writes=[pkey])
        P.op("pe", lambda e, pm=pm, blk=blk: e.matmul(pm[:, 0:256], lhsT=ones[0:1, :], rhs=brow[0:1, blk * 256:(blk + 1) * 256],
                                                     start=False, stop=True), reads=["ones", "brow"], writes=[pkey])
        P.op("act", lambda e, pm=pm, blk=blk: e.activation(out=mods[:, blk * 256:(blk + 1) * 256], in_=pm[:, 0:256], func=AF.Copy),
             reads=[pkey], writes=["mods"])
    gb = stage[0][:, :]
    P.dma("sp", gb[:, 0:1024], gvec[0:1, :].to_broadcast((128, 1024)), writes=["stage0"])
    gb1 = stage[1][:, :]
    P.dma("sp", gb1[:, 0:1024], gvec[1:2, :].to_broadcast((128, 1024)), writes=["stage1"])
    P.dma("sp", gb1[:, 1024:2048], gvec[2:3, :].to_broadcast((128, 1024)), writes=["stage1"])
    G1, SH2, G2, G3 = mods[:, 0:1024], mods[:, 1024:2048], mods[:, 2048:3072], mods[:, 3072:4096]
    P.op("dve", lambda e: e.tensor_tensor(out=G1, in0=G1, in1=gb[:, 0:1024], op=ALU.mult), reads=["mods", "stage0"], writes=["mods"])
    P.op("dve", lambda e: e.scalar_tensor_tensor(out=G2, in0=G2, scalar=1.0, in1=gb1[:, 0:1024], op0=ALU.add, op1=ALU.mult),
         reads=["mods", "stage1"], writes=["mods"])
    P.op("dve", lambda e: e.tensor_tensor(out=G3, in0=G3, in1=gb1[:, 1024:2048], op=ALU.mult), reads=["mods", "stage1"], writes=["mods"])
    wo = w_out.rearrange("(k p) n -> p k n", p=128)
    for c4 in range(4):
        st = stage[c4 % 2]
        skey = "stage%d" % (c4 % 2)
        stv = st[:, :].rearrange("p (k n) -> p k n", k=2)
        P.dma("sp", stv, wo[:, 2 * c4:2 * c4 + 2, :], writes=[skey])
        for kk in range(2):
            k = 2 * c4 + kk
            P.op("dve" if kk == 0 else "pool", lambda e, k=k, kk=kk, stv=stv: e.tensor_scalar(
                out=wout_b[:, k, :], in0=stv[:, kk, :], scalar1=gmx[:, k:k + 1], scalar2=None, op0=ALU.mult),
                reads=[skey, "gmx"], writes=["wout_b"])

    def sumsq(src, dst, reads, key, eng="dve"):
        w = src.shape[-1]
        P.op("act", lambda e: e.activation(out=tmp2[:, 0:w], in_=src, func=AF.Square), reads=reads, writes=["tmp2"])
        P.op(eng, lambda e: e.tensor_reduce(out=dst, in_=tmp2[:, 0:w], axis=AX.X, op=ALU.add), reads=["tmp2"] + list(reads), writes=[key])

    nsm = [0]

    def smallcol(n=1):
        c = nsm[0] % 16
        nsm[0] += 1
        return small[:, 2 * c:2 * c + n], "small%d" % c

    w1v = w1.rearrange("(k p) n -> p k n", p=128)
    w2v = w2.rearrange("(c p) n -> p c n", p=128)
    it = [0]
    for half in range(NT // 1024):
        for tile in range(8):
            t0 = half * 1024 + tile * 128
            i = it[0]
            it[0] += 1
            X, xk = xt[i % 2], "xt%d" % (i % 2)
            tl = t0
            MBk = lambda k, tl=tl: mixT[:, k, tl:tl + 128]
            mbk = "mixT"
            P.dma("sp", X[:, :], x_tok[t0:t0 + 128, :], writes=[xk])
            P.op("pool", lambda e, tl=tl: e.tensor_tensor(out=sqm[:, :, :], in0=mixT[:, :, tl:tl + 128], in1=mixT[:, :, tl:tl + 128], op=ALU.mult),
                 reads=["mixT"], writes=["sqm"])
            pst = pM[i % 2]
            pstk = "pM%d" % (i % 2)
            for grp in range(2):
                for k in range(4):
                    P.op("pe", lambda e, grp=grp, k=k, pst=pst: e.matmul(pst[:, grp:grp + 1], lhsT=sqm[:, 4 * grp + k, :], rhs=onescol[:, 0:1],
                                                                        start=(k == 0), stop=(k == 3)), reads=["sqm", "onescol"], writes=[pstk])
            rr, rrk = smallcol(2)
            _rs(P, None, pst[:, 0:2], rr, 512.0, "rmix", [pstk], rrk)
            for nh in range(2):
                for k in range(4):
                    P.op("pe", lambda e, k=k, nh=nh, MBk=MBk: e.matmul(pA[:, nh * 512:(nh + 1) * 512], lhsT=MBk(k),
                                                                     rhs=wout_b[:, k, nh * 512:(nh + 1) * 512],
                                                                     start=(k == 0), stop=(k == 3)),
                         reads=[mbk, "wout_b"], writes=["pA%d" % nh])
                for k in range(4, 8):
                    P.op("pe", lambda e, k=k, nh=nh, MBk=MBk: e.matmul(pH[:, nh * 512:(nh + 1) * 512], lhsT=MBk(k),
                                                                     rhs=wout_b[:, k, nh * 512:(nh + 1) * 512],
                                                                     start=(k == 4), stop=(k == 7)),
                         reads=[mbk, "wout_b"], writes=["pH%d" % nh])
            P.op("act", lambda e, rr=rr: e.activation(out=ysb[:, :], in_=pA[:, :], func=AF.Copy, scale=rr[:, 0:1]),
                 reads=["pA0", "pA1", rrk], writes=["ysb"])
            P.op("dve", lambda e, rr=rr: e.scalar_tensor_tensor(out=ysb[:, :], in0=pH[:, :], scalar=rr[:, 1:2], in1=ysb[:, :],
                                                                op0=ALU.mult, op1=ALU.add),
                 reads=["pH0", "pH1", rrk, "ysb"], writes=["ysb"])
            ss2, ss2k = smallcol(1)
            r2, r2k = smallcol(1)
            sumsq(ysb[:, :], ss2[:, 0:1], ["ysb"], ss2k)
            _rs(P, None, ss2, r2, 1024.0, "ry", [ss2k], r2k)
            P.op("dve", lambda e, r2=r2: e.scalar_tensor_tensor(out=tmp[:, :], in0=ysb[:, :], scalar=r2[:, 0:1], in1=G1,
                                                                op0=ALU.mult, op1=ALU.mult),
                 reads=["ysb", r2k, "mods"], writes=["tmp"])
            P.op("pool", lambda e, X=X: e.tensor_tensor(out=X[:, :], in0=tmp[:, :], in1=X[:, :], op=ALU.add),
                 reads=["tmp", xk], writes=[xk])
            P.dma("pool", x1s[t0:t0 + 128, :], X[:, :], reads=[xk], writes=["x1s%d" % (t0 // 128)])
            ss3, ss3k = smallcol(1)
            r3, r3k = smallcol(1)
            sumsq(X[:, :], ss3[:, 0:1], [xk], ss3k)
            _rs(P, None, ss3, r3, 1024.0, "r1", [ss3k], r3k)
            P.op("dve", lambda e, r3=r3, X=X: e.scalar_tensor_tensor(out=tmp[:, :], in0=X[:, :], scalar=r3[:, 0:1], in1=G2,
                                                                     op0=ALU.mult, op1=ALU.mult),
                 reads=[xk, r3k, "mods"], writes=["tmp"])
            P.op("pool", lambda e: e.tensor_tensor(out=h2b[:, :], in0=tmp[:, :], in1=SH2, op=ALU.add),
                 reads=["tmp", "mods"], writes=["h2b"])
            for k in range(8):
                P.op("pe", lambda e, k=k: e.transpose(out=pT[:, k * 128:(k + 1) * 128], in_=h2b[:, k * 128:(k + 1) * 128],
                                                      identity=ident[:, :]), reads=["h2b", "ident"], writes=["pT"])
            P.op("act", lambda e, tile=tile: e.activation(out=h2T[:, :, tile * 128:(tile + 1) * 128],
                                                          in_=pT[:, :].rearrange("p (k t) -> p k t", k=8), func=AF.Copy),
                 reads=["pT"], writes=["h2T"])
        for j in range(8):
            W1B, w1k = w1b[j % 2], "w1b%d" % (j % 2)
            W2B, w2k = w2b[j % 2], "w2b%d" % (j % 2)
            for hh in range(2):
                st, skey = stage[hh], "stage%d" % hh
                stv = st[:, :].rearrange("p (k n) -> p k n", k=8)
                P.dma("sp", stv, w1v[:, :, j * 512 + hh * 256:j * 512 + (hh + 1) * 256], writes=[skey])
                P.op("dve" if hh == 0 else "pool", lambda e, W1B=W1B, stv=stv, hh=hh: e.tensor_copy(
                    out=W1B[:, :, hh * 256:(hh + 1) * 256], in_=stv), reads=[skey], writes=[w1k])
            for hh in range(2):
                st, skey = stage[hh], "stage%d" % hh
                stv = st[:, :].rearrange("p (c n) -> p c n", c=2)
                P.dma("sp", stv, w2v[:, j * 4 + hh * 2:j * 4 + hh * 2 + 2, :], writes=[skey])
                P.op("dve" if hh == 0 else "pool", lambda e, W2B=W2B, stv=stv, hh=hh: e.tensor_copy(
                    out=W2B[:, hh * 2:hh * 2 + 2, :], in_=stv), reads=[skey], writes=[w2k])
            for tg in range(2):
                AT, atk = aT[tg], "aT%d" % tg
                for hc in range(4):
                    pm, pkey = pM[hc % 2], "pM%d" % (hc % 2)
                    RL, rlk = rl[hc % 2], "rl%d" % (hc % 2)
                    for k in range(8):
                        P.op("pe", lambda e, k=k, hc=hc, tg=tg, pm=pm, W1B=W1B: e.matmul(
                            pm[:, :], lhsT=W1B[:, k, hc * 128:(hc + 1) * 128], rhs=h2T[:, k, tg * 512:(tg + 1) * 512],
                            start=(k == 0), stop=(k == 7)), reads=[w1k, "h2T"], writes=[pkey])
                    P.op("act", lambda e, pm=pm, RL=RL: e.activation(out=RL[:, :], in_=pm[:, :], func=AF.Relu),
                         reads=[pkey], writes=[rlk])
                    P.op("act", lambda e, RL=RL, AT=AT, hc=hc: e.activation(out=AT[:, hc, :], in_=RL[:, :], func=AF.Square),
                         reads=[rlk], writes=[atk])
                for tt in range(4):
                    tile = tg * 4 + tt
                    for nh in range(2):
                        pp, ppk = (pA, "pA%d" % nh) if tt % 2 == 0 else (pH, "pH%d" % nh)
                        for hc in range(4):
                            P.op("pe", lambda e, hc=hc, tt=tt, nh=nh, pp=pp, AT=AT, W2B=W2B: e.matmul(
                                pp[:, nh * 512:(nh + 1) * 512], lhsT=AT[:, hc, tt * 128:(tt + 1) * 128],
                                rhs=W2B[:, hc, nh * 512:(nh + 1) * 512], start=(hc == 0), stop=(hc == 3)),
                                reads=[atk, w2k], writes=[ppk])
                        fv = f2acc[:, tile, nh * 512:(nh + 1) * 512]
                        fk = "f2_%d_%d" % (tile, nh)
                        if j == 0:
                            P.op("dve", lambda e, fv=fv, pp=pp, nh=nh: e.tensor_copy(out=fv, in_=pp[:, nh * 512:(nh + 1) * 512]),
                                 reads=[ppk], writes=[fk])
                        else:
                            P.op("dve", lambda e, fv=fv, pp=pp, nh=nh: e.tensor_tensor(out=fv, in0=pp[:, nh * 512:(nh + 1) * 512], in1=fv, op=ALU.add),
                                 reads=[ppk, fk], writes=[fk])
        for tile in range(8):
            t0 = half * 1024 + tile * 128
            i = it[0]
            it[0] += 1
            X, xk = xt[i % 2], "xt%d" % (i % 2)
            P.dma("sp", X[:, :], x1s[t0:t0 + 128, :], reads=["x1s%d" % (t0 // 128)], writes=[xk], key=xk)
            fkeys = ["f2_%d_%d" % (tile, nh) for nh in range(2)]
            ss4, ss4k = smallcol(1)
            r4, r4k = smallcol(1)
            sumsq(f2acc[:, tile, :], ss4[:, 0:1], fkeys, ss4k)
            _rs(P, None, ss4, r4, 1024.0, "rf", [ss4k], r4k)
            P.op("dve", lambda e, r4=r4, tile=tile: e.scalar_tensor_tensor(out=tmp[:, :], in0=f2acc[:, tile, :], scalar=r4[:, 0:1], in1=G3,
                                                                          op0=ALU.mult, op1=ALU.mult),
                 reads=fkeys + [r4k, "mods"], writes=["tmp"])
            P.op("pool", lambda e, X=X: e.tensor_tensor(out=X[:, :], in0=tmp[:, :], in1=X[:, :], op=ALU.add),
                 reads=["tmp", xk], writes=[xk])
            P.dma("pool", out[t0:t0 + 128, :], X[:, :], reads=[xk], writes=["out%d" % (t0 // 128)])
    P.build()


def build_fused(upto=4, dbg=False):
    nc = bass.Bass("TRN2", target_bir_lowering=False)
    outer = ExitStack()
    D = DramReg(nc)
    mixT = outer.enter_context(nc.sbuf_tensor("mixT_keep", [128, 8, NOWN], BF16))
    with ExitStack() as es_a:
        hTw = es_a.enter_context(nc.sbuf_tensor("hTw_keep", [128, 8, NW], BF16))
        emit_attn_norm_f(nc, outer, D, hTw)
        if upto >= 1:
            emit_attn_f(nc, outer, D, mixT, hTw)
    if upto >= 2:
        emit_hyproj_f(nc, outer, D)
    if upto >= 3:
        emit_hyena_h(nc, outer, D, mixT, groups=(0, 1, 2, 3), prefix="C0_")
    if upto >= 4:
        emit_phase2_f(nc, outer, D, mixT)
    if dbg:
        P = Prog(nc, sem_stack=outer, prefix="E_")
        dd = D("dbg_mixT", [128, 8, NOWN], kind="ExternalOutput", dt=BF16)
        P.dma("sp", dd[:, :, :], mixT[:, :, :], reads=["mixT"], writes=["dbg"])
        P.build()
    outer.close()
    return nc


def fused_inputs(inp):
    f = lambda a: np.ascontiguousarray(a, dtype=np.float32)
    K = _hy_consts()
    K.update(_hy_consts_h())
    C, S = _rope_tables()
    M = _attn_masks()
    hsel = np.zeros((128, 64), np.float32)
    hsel[0:64, 0] = 1.0
    hsel[64:128, 32] = 1.0
    w_in = inp["w_in"][0]
    hy_w = w_in[:, 1536:]
    w_incA = []
    for hp in range(4):
        cs = slice(128 * hp, 128 * hp + 128)
        wq, wk, wv = w_in[:, 0:512][:, cs], w_in[:, 512:1024][:, cs], w_in[:, 1024:1536][:, cs]
        w_incA.append(np.concatenate([wq, _perm_cols(wq), wk, _perm_cols(wk), wv, np.zeros((1024, 128), np.float32)], axis=1))
    w_incA = f(np.stack(w_incA))
    w_incH = f(np.concatenate([hy_w[:, a * 512 + 128 * g:a * 512 + 128 * g + 128] for g in range(4) for a in range(3)], axis=1))
    cwv = inp["conv_w"][0].reshape(3, 3, 512)
    cbv = inp["conv_b"][0].reshape(3, 512)
    cw4 = np.zeros((128, 36), np.float32)
    cb4 = np.zeros((128, 12), np.float32)
    for g in range(4):
        for a in range(3):
            cb4[:, 3 * g + a] = cbv[a, 128 * g:128 * g + 128]
            for t in range(3):
                cw4[:, 9 * g + 3 * a + t] = cwv[t, a, 128 * g:128 * g + 128]
    w3 = inp["filt_w3"][0].reshape(64, 2, 2, 4, 128).transpose(0, 3, 1, 2, 4).reshape(64, 2048)
    hb = inp["hyena_bias"][0]
    hbcol4 = np.zeros((128, 4, 2, 64), np.float32)
    for g in range(4):
        for cp in range(2):
            hbcol4[64 * cp:64 * cp + 64, g, :, :] = hb[:, 128 * g:128 * g + 128][:, cp::2][None, :, :]
    cols2 = np.r_[2048:3072, 3072:6144]
    shared = {
        "w_ada1": f(inp["w_ada"][0][:, 0:2048]), "b_ada1": f(inp["b_ada"][0][0:2048].reshape(16, 128).T),
        "gpre": f(inp["g_pre_mix"][0].reshape(8, 128).T), "w_incA": w_incA, "w_incH": w_incH,
        "maskd": f(M), "hseld": hsel, "identd": np.eye(128, dtype=np.float32),
        "F1cat_h": K["F1cat_h"], "TrTr_h": K["TrTr_h"], "TiTi_h": K["TiTi_h"], "R12": K["R12"], "TT2r_h": K["TT2r_h"], "TT2i_h": K["TT2i_h"],
        "IF1_h": K["IF1_h"], "Swin": K["Swin"], "zembT": K["zembT"],
        "delta4": f(np.broadcast_to(K["deltas"][None, :], (128, 512))),
        "hbrow": f(np.stack([hb[o_, 128 * g_:128 * g_ + 128] for g_ in range(4) for o_ in range(2)]).reshape(1, 1024)),
        "fw1": f(inp["filt_w1"][0]), "fw2": f(inp["filt_w2"][0]), "fw3c4": f(w3),
        "fpar": f(np.stack([inp["filt_b1"][0], inp["filt_freq1"][0], inp["filt_b2"][0], inp["filt_freq2"][0]], axis=1)),
        "cw4": cw4, "cb4": cb4,
        "w_ada2": f(inp["w_ada"][0][:, cols2]), "b_ada2": f(inp["b_ada"][0][cols2][None, :]),
        "gvec": f(np.stack([inp["g_post_mix"][0], inp["g_pre_mlp"][0], inp["g_post_mlp"][0]])),
        "gmix": f(np.concatenate([inp["g_attn_out"][0], inp["g_hyena_out"][0]]).reshape(8, 128).T),
        "w_out": f(inp["w_out"][0]), "w1": f(inp["w_mlp1"][0]), "w2": f(inp["w_mlp2"][0]),
    }
    in_maps = []
    for core in range(8):
        b, j = core // 4, core % 4
        lo = NOWN * j - 1024
        idx = np.arange(lo, lo + NW)
        ok = (idx >= 0) & (idx < NTOK)
        idc = np.clip(idx, 0, NTOK - 1)
        xTw = np.where(ok[None, :], inp["x"][b].T[:, idc], 0.0)
        sel = np.zeros((128, 32), np.float32)
        sel[32 * j + np.arange(32), np.arange(32)] = 1.0
        m = dict(shared)
        m.update({
            "xTw": f(xTw), "validw": f(ok.astype(np.float32).reshape(32, 128).T),
            "ropeCw": f(C[:, idc]), "ropeSw": f(S[:, idc]),
            "xT": f(inp["x"][b].T), "cvec": f(inp["c"][b].reshape(8, 128).T),
            "seld": sel, "x_tok": f(inp["x"][b, NOWN * j:NOWN * (j + 1)]),
        })
        in_maps.append(m)
    return in_maps


def kernel(**inputs):
    inp = {k: np.asarray(v) for k, v in inputs.items()}
    nc = build_fused()
    in_maps = fused_inputs(inp)
    res = run_bass_kernel_spmd(nc, in_maps, core_ids=list(range(8)))
    out = np.zeros((2, NTOK, 1024), np.float32)
    for core in range(8):
        b, j = core // 4, core % 4
        out[b, NOWN * j:NOWN * (j + 1)] = res.results[core]["out"]
    return out
```

```python
import math
import numpy as np
from concourse.bass_utils import run_bass_kernel_spmd

from contextlib import ExitStack
import concourse.bass as bass
import concourse.mybir as mybir

F32 = mybir.dt.float32
BF16 = mybir.dt.bfloat16
ALU = mybir.AluOpType
AF = mybir.ActivationFunctionType
AX = mybir.AxisListType

ENGS = ("pe", "act", "dve", "pool", "sp")


class Prog:
    def __init__(self, nc, sem_stack=None, prefix=""):
        self.nc = nc
        self.ops = []
        self.state = {}
        self.es = ExitStack()
        self.sem_stack = sem_stack if sem_stack is not None else self.es
        self.prefix = prefix
        self.ndma = 0

    def sb(self, name, shape, dt):
        return self.es.enter_context(self.nc.sbuf_tensor(self.prefix + "sb_" + name, list(shape), dt))

    def ps(self, name, shape, dt):
        return self.es.enter_context(self.nc.psum_tensor(self.prefix + "ps_" + name, list(shape), dt))

    def op(self, eng, fn, reads=(), writes=(), dma=False, grp=None):
        deps = set()
        psum_r = [b for b in reads if len(b) > 1 and b[0] == "p" and b[1].isupper()]
        if psum_r:
            reads = [b for b in reads if b not in psum_r]
            writes = list(writes) + [b for b in psum_r if b not in writes]
        for b in reads:
            st = self.state.setdefault(b, [None, []])
            if st[0] is not None:
                deps.add(st[0])
        for b in writes:
            st = self.state.setdefault(b, [None, []])
            if st[0] is not None:
                deps.add(st[0])
            deps.update(st[1])
        idx = len(self.ops)
        if dma and grp is None:
            grp = "dma%d" % (self.ndma % 12)
            self.ndma += 1
        self.ops.append(dict(eng=eng, fn=fn, deps=deps, dma=dma, grp=grp, sig=dma))
        for b in reads:
            self.state[b][1].append(idx)
        for b in writes:
            self.state[b] = [idx, []]
        return idx

    def dma(self, eng, out, in_, reads=(), writes=(), grp=None, key=None):
        if grp is None:
            grp = "dma_" + str(key if key is not None else (reads[0] if reads else writes[0]))
        return self.op(eng, lambda e: e.dma_start(out=out, in_=in_), reads, writes, dma=True, grp=grp)

    def build(self):
        nc = self.nc
        ops = self.ops

        def needs_wait(o, d):
            if d["dma"] or o["dma"]:
                return True
            if o["eng"] != d["eng"]:
                return True
            return o["eng"] != "pe"

        for o in ops:
            for di in o["deps"]:
                if needs_wait(o, ops[di]):
                    ops[di]["sig"] = True
        cnt = {}
        sems = {}

        def getsem(key):
            if key not in sems:
                sems[key] = self.sem_stack.enter_context(nc.semaphore(self.prefix + "s_" + str(key).replace(" ", "")))
            return sems[key]

        for o in ops:
            if not o["sig"]:
                continue
            if o["dma"]:
                key = o["grp"]
                cnt[key] = cnt.get(key, 0) + 16
                o["tok"] = (key, cnt[key], 16)
            else:
                base = "e_" + o["eng"]
                gen = cnt.get(base + "_gen", 0)
                key = "%s_%d" % (base, gen)
                cnt[key] = cnt.get(key, 0) + 1
                o["tok"] = (key, cnt[key], 1)
                if cnt[key] >= 30000:
                    cnt[base + "_gen"] = gen + 1
        for key in list(cnt.keys()):
            if not key.endswith("_gen"):
                getsem(key)
        per = {e: [] for e in ENGS}
        for o in ops:
            per[o["eng"]].append(o)
        final_dma = {k: v for k, v in cnt.items() if k.startswith("dma")}
        self.n_sems = len(sems)

        def replay(eng_name, e):
            waited = {}
            for o in per[eng_name]:
                for di in sorted(o["deps"]):
                    d = ops[di]
                    if not needs_wait(o, d):
                        continue
                    key, val, _ = d["tok"]
                    if waited.get(key, 0) < val:
                        e.wait_ge(sems[key], val)
                        waited[key] = val
                inst = o["fn"](e)
                if o["sig"]:
                    key, val, inc = o["tok"]
                    inst.then_inc(sems[key], inc)
            if eng_name == "sp":
                for key, val in final_dma.items():
                    if waited.get(key, 0) < val:
                        e.wait_ge(sems[key], val)

        with nc.Block() as block:
            @block.sync
            def _(e):
                replay("sp", e)

            @block.tensor
            def _(e):
                replay("pe", e)

            @block.scalar
            def _(e):
                replay("act", e)

            @block.vector
            def _(e):
                replay("dve", e)

            @block.gpsimd
            def _(e):
                replay("pool", e)
        self.es.close()


EPS = 1e-6


def _rs(P, eng_stats, ss, r, n, nm, reads, key):
    P.op("dve", lambda e: e.tensor_scalar(out=r, in0=ss, scalar1=1.0 / n, scalar2=EPS, op0=ALU.mult, op1=ALU.add),
         reads=reads, writes=[key])
    P.op("act", lambda e: e.activation(out=r, in_=r, func=AF.Sqrt), reads=[key], writes=[key])
    P.op("dve", lambda e: e.reciprocal(out=r, in_=r), reads=[key], writes=[key])


def build_phase2(NT=2048):
    nc = bass.Bass("TRN2", target_bir_lowering=False)
    P = Prog(nc)
    D = lambda name, shape, kind="ExternalInput": nc.dram_tensor(name, list(shape), F32, kind=kind).ap()
    x_tok = D("x_tok", [NT, 1024])
    mix_tok = D("mix_tok", [NT, 1024])
    mixT = D("mixT", [1024, NT])
    cvec = D("cvec", [128, 8])
    w_ada = D("w_ada", [1024, 4096])
    b_ada = D("b_ada", [1, 4096])
    gvec = D("gvec", [3, 1024])
    gmix = D("gmix", [128, 8])
    w_out = D("w_out", [1024, 1024])
    w1 = D("w1", [1024, 4096])
    w2 = D("w2", [4096, 1024])
    out = D("out", [NT, 1024], kind="ExternalOutput")
    x1s = D("x1s", [NT, 1024], kind="Internal")

    sb, ps = P.sb, P.ps
    wout_b = sb("wout_b", [128, 8, 1024], BF16)
    mods = sb("mods", [128, 4096], F32)
    sbc = sb("sbc", [128, 8, 128], F32)
    ones = sb("ones", [128, 128], F32)
    ident = sb("ident", [128, 128], BF16)
    identf = sb("identf", [128, 128], F32)
    stage = [sb("stage%d" % i, [128, 2048], F32) for i in range(2)]
    w1b = [sb("w1b%d" % i, [128, 8, 512], BF16) for i in range(2)]
    w2b = [sb("w2b%d" % i, [128, 4, 1024], BF16) for i in range(2)]
    h2T = sb("h2T", [128, 8, 1024], BF16)
    f2acc = sb("f2acc", [128, 8, 1024], F32)
    xt = [sb("xt%d" % i, [128, 1024], F32) for i in range(2)]
    mt = sb("mt", [128, 1024], F32)
    mTs = sb("mTs", [128, 8, 128], F32)
    mTb = [sb("mTb%d" % i, [128, 8, 128], BF16) for i in range(2)]
    ysb = sb("ysb", [128, 1024], F32)
    tmp = sb("tmp", [128, 1024], F32)
    tmp2 = sb("tmp2", [128, 1024], F32)
    h2b = sb("h2b", [128, 1024], BF16)
    rl = [sb("rl%d" % i, [128, 512], F32) for i in range(2)]
    aT = [sb("aT%d" % i, [128, 4, 512], BF16) for i in range(2)]
    small = sb("small", [128, 32], F32)
    csb = sb("csb", [128, 8], F32)
    gmx = sb("gmx", [128, 8], F32)
    brow = sb("brow", [1, 4096], F32)

    pA = ps("pA", [128, 1024], F32)
    pH = ps("pH", [128, 1024], F32)
    pT = ps("pT", [128, 1024], BF16)
    pM = [ps("pM%d" % i, [128, 512], F32) for i in range(2)]

    P.op("pool", lambda e: e.memset(ones[:, :], 1.0), writes=["ones"])
    P.op("pool", lambda e: e.memset(identf[:, :], 0.0), writes=["identf"])
    identd = D("identd", [128, 128])
    P.dma("sp", identf[:, :], identd[:, :], writes=["identf"])
    P.op("dve", lambda e: e.tensor_copy(out=ident[:, :], in_=identf[:, :]), reads=["identf"], writes=["ident"])
    P.dma("sp", csb[:, :], cvec[:, :], writes=["csb"])
    P.dma("sp", gmx[:, :], gmix[:, :], writes=["gmx"])
    P.dma("sp", brow[:, :], b_ada[:, :], writes=["brow"])
    P.op("act", lambda e: e.activation(out=csb[:, :], in_=csb[:, :], func=AF.Silu), reads=["csb"], writes=["csb"])
    for k in range(8):
        P.op("dve", lambda e, k=k: e.tensor_scalar(out=sbc[:, k, :], in0=ones[:, :], scalar1=csb[:, k:k + 1],
                                                    scalar2=None, op0=ALU.mult), reads=["ones", "csb"], writes=["sbc"])
    wa = w_ada.rearrange("(k p) n -> p k n", p=128)
    for blk in range(16):
        st = stage[blk % 2]
        skey = "stage%d" % (blk % 2)
        stv = st[:, :].rearrange("p (k n) -> p k n", k=8)
        P.dma("sp", stv, wa[:, :, blk * 256:(blk + 1) * 256], writes=[skey])
        pm = pM[blk % 2]
        pkey = "pM%d" % (blk % 2)
        for k in range(8):
            P.op("pe", lambda e, k=k, pm=pm, stv=stv: e.matmul(pm[:, 0:256], lhsT=sbc[:, k, :], rhs=stv[:, k, :],
                                                               start=(k == 0), stop=False),
                 reads=["sbc", skey], writes=[pkey])
        P.op("pe", lambda e, pm=pm, blk=blk: e.matmul(pm[:, 0:256], lhsT=ones[0:1, :], rhs=brow[0:1, blk * 256:(blk + 1) * 256],
                                                     start=False, stop=True), reads=["ones", "brow"], writes=[pkey])
        P.op("act", lambda e, pm=pm, blk=blk: e.activation(out=mods[:, blk * 256:(blk + 1) * 256], in_=pm[:, 0:256], func=AF.Copy),
             reads=[pkey], writes=["mods"])
    gb = stage[0][:, :]
    P.dma("sp", gb[:, 0:1024], gvec[0:1, :].to_broadcast((128, 1024)), writes=["stage0"])
    gb1 = stage[1][:, :]
    P.dma("sp", gb1[:, 0:1024], gvec[1:2, :].to_broadcast((128, 1024)), writes=["stage1"])
    P.dma("sp", gb1[:, 1024:2048], gvec[2:3, :].to_broadcast((128, 1024)), writes=["stage1"])
    G1, SH2, G2, G3 = mods[:, 0:1024], mods[:, 1024:2048], mods[:, 2048:3072], mods[:, 3072:4096]
    P.op("dve", lambda e: e.tensor_tensor(out=G1, in0=G1, in1=gb[:, 0:1024], op=ALU.mult), reads=["mods", "stage0"], writes=["mods"])
    P.op("dve", lambda e: e.scalar_tensor_tensor(out=G2, in0=G2, scalar=1.0, in1=gb1[:, 0:1024], op0=ALU.add, op1=ALU.mult),
         reads=["mods", "stage1"], writes=["mods"])
    P.op("dve", lambda e: e.tensor_tensor(out=G3, in0=G3, in1=gb1[:, 1024:2048], op=ALU.mult), reads=["mods", "stage1"], writes=["mods"])
    wo = w_out.rearrange("(k p) n -> p k n", p=128)
    for c4 in range(4):
        st = stage[c4 % 2]
        skey = "stage%d" % (c4 % 2)
        stv = st[:, :].rearrange("p (k n) -> p k n", k=2)
        P.dma("sp", stv, wo[:, 2 * c4:2 * c4 + 2, :], writes=[skey])
        for kk in range(2):
            k = 2 * c4 + kk
            P.op("dve" if kk == 0 else "pool", lambda e, k=k, kk=kk, stv=stv: e.tensor_scalar(
                out=wout_b[:, k, :], in0=stv[:, kk, :], scalar1=gmx[:, k:k + 1], scalar2=None, op0=ALU.mult),
                reads=[skey, "gmx"], writes=["wout_b"])

    def sumsq(src, dst, reads, key, eng="dve"):
        w = src.shape[-1]
        P.op("pool", lambda e: e.tensor_tensor(out=tmp2[:, 0:w], in0=src, in1=src, op=ALU.mult), reads=reads, writes=["tmp2"])
        P.op(eng, lambda e: e.tensor_reduce(out=dst, in_=tmp2[:, 0:w], axis=AX.X, op=ALU.add), reads=["tmp2"] + list(reads), writes=[key])

    nsm = [0]

    def smallcol(n=1):
        c = nsm[0] % 16
        nsm[0] += 1
        return small[:, 2 * c:2 * c + n], "small%d" % c

    w1v = w1.rearrange("(k p) n -> p k n", p=128)
    w2v = w2.rearrange("(c p) n -> p c n", p=128)
    mTv = mixT.rearrange("(k p) t -> p k t", p=128)
    it = [0]
    for half in range(NT // 1024):
        for tile in range(8):
            t0 = half * 1024 + tile * 128
            i = it[0]
            it[0] += 1
            X, xk = xt[i % 2], "xt%d" % (i % 2)
            MB, mbk = mTb[i % 2], "mTb%d" % (i % 2)
            P.dma("sp", X[:, :], x_tok[t0:t0 + 128, :], writes=[xk])
            P.dma("sp", mt[:, :], mix_tok[t0:t0 + 128, :], writes=["mt"])
            P.dma("sp", mTs[:, :, :], mTv[:, :, t0:t0 + 128], writes=["mTs"])
            P.op("pool", lambda e, MB=MB: e.tensor_copy(out=MB[:, :, :], in_=mTs[:, :, :]), reads=["mTs"], writes=[mbk])
            ss, ssk = smallcol(2)
            rr, rrk = smallcol(2)
            sumsq(mt[:, 0:512], ss[:, 0:1], ["mt"], ssk)
            sumsq(mt[:, 512:1024], ss[:, 1:2], ["mt", ssk], ssk)
            _rs(P, None, ss, rr, 512.0, "rmix", [ssk], rrk)
            for nh in range(2):
                for k in range(4):
                    P.op("pe", lambda e, k=k, nh=nh, MB=MB: e.matmul(pA[:, nh * 512:(nh + 1) * 512], lhsT=MB[:, k, :],
                                                                     rhs=wout_b[:, k, nh * 512:(nh + 1) * 512],
                                                                     start=(k == 0), stop=(k == 3)),
                         reads=[mbk, "wout_b"], writes=["pA%d" % nh])
                for k in range(4, 8):
                    P.op("pe", lambda e, k=k, nh=nh, MB=MB: e.matmul(pH[:, nh * 512:(nh + 1) * 512], lhsT=MB[:, k, :],
                                                                     rhs=wout_b[:, k, nh * 512:(nh + 1) * 512],
                                                                     start=(k == 4), stop=(k == 7)),
                         reads=[mbk, "wout_b"], writes=["pH%d" % nh])
            P.op("act", lambda e, rr=rr: e.activation(out=ysb[:, :], in_=pA[:, :], func=AF.Copy, scale=rr[:, 0:1]),
                 reads=["pA0", "pA1", rrk], writes=["ysb"])
            P.op("dve", lambda e, rr=rr: e.scalar_tensor_tensor(out=ysb[:, :], in0=pH[:, :], scalar=rr[:, 1:2], in1=ysb[:, :],
                                                                op0=ALU.mult, op1=ALU.add),
                 reads=["pH0", "pH1", rrk, "ysb"], writes=["ysb"])
            ss2, ss2k = smallcol(1)
            r2, r2k = smallcol(1)
            sumsq(ysb[:, :], ss2[:, 0:1], ["ysb"], ss2k)
            _rs(P, None, ss2, r2, 1024.0, "ry", [ss2k], r2k)
            P.op("dve", lambda e, r2=r2: e.scalar_tensor_tensor(out=tmp[:, :], in0=ysb[:, :], scalar=r2[:, 0:1], in1=G1,
                                                                op0=ALU.mult, op1=ALU.mult),
                 reads=["ysb", r2k, "mods"], writes=["tmp"])
            P.op("pool", lambda e, X=X: e.tensor_tensor(out=X[:, :], in0=tmp[:, :], in1=X[:, :], op=ALU.add),
                 reads=["tmp", xk], writes=[xk])
            P.dma("pool", x1s[t0:t0 + 128, :], X[:, :], reads=[xk], writes=["x1s%d" % (t0 // 128)])
            ss3, ss3k = smallcol(1)
            r3, r3k = smallcol(1)
            sumsq(X[:, :], ss3[:, 0:1], [xk], ss3k)
            _rs(P, None, ss3, r3, 1024.0, "r1", [ss3k], r3k)
            P.op("dve", lambda e, r3=r3, X=X: e.scalar_tensor_tensor(out=tmp[:, :], in0=X[:, :], scalar=r3[:, 0:1], in1=G2,
                                                                     op0=ALU.mult, op1=ALU.mult),
                 reads=[xk, r3k, "mods"], writes=["tmp"])
            P.op("pool", lambda e: e.tensor_tensor(out=h2b[:, :], in0=tmp[:, :], in1=SH2, op=ALU.add),
                 reads=["tmp", "mods"], writes=["h2b"])
            for k in range(8):
                P.op("pe", lambda e, k=k: e.transpose(out=pT[:, k * 128:(k + 1) * 128], in_=h2b[:, k * 128:(k + 1) * 128],
                                                      identity=ident[:, :]), reads=["h2b", "ident"], writes=["pT"])
            P.op("act", lambda e, tile=tile: e.activation(out=h2T[:, :, tile * 128:(tile + 1) * 128],
                                                          in_=pT[:, :].rearrange("p (k t) -> p k t", k=8), func=AF.Copy),
                 reads=["pT"], writes=["h2T"])
        for j in range(8):
            W1B, w1k = w1b[j % 2], "w1b%d" % (j % 2)
            W2B, w2k = w2b[j % 2], "w2b%d" % (j % 2)
            for hh in range(2):
                st, skey = stage[hh], "stage%d" % hh
                stv = st[:, :].rearrange("p (k n) -> p k n", k=8)
                P.dma("sp", stv, w1v[:, :, j * 512 + hh * 256:j * 512 + (hh + 1) * 256], writes=[skey])
                P.op("dve" if hh == 0 else "pool", lambda e, W1B=W1B, stv=stv, hh=hh: e.tensor_copy(
                    out=W1B[:, :, hh * 256:(hh + 1) * 256], in_=stv), reads=[skey], writes=[w1k])
            for hh in range(2):
                st, skey = stage[hh], "stage%d" % hh
                stv = st[:, :].rearrange("p (c n) -> p c n", c=2)
                P.dma("sp", stv, w2v[:, j * 4 + hh * 2:j * 4 + hh * 2 + 2, :], writes=[skey])
                P.op("dve" if hh == 0 else "pool", lambda e, W2B=W2B, stv=stv, hh=hh: e.tensor_copy(
                    out=W2B[:, hh * 2:hh * 2 + 2, :], in_=stv), reads=[skey], writes=[w2k])
            for tg in range(2):
                AT, atk = aT[tg], "aT%d" % tg
                for hc in range(4):
                    pm, pkey = pM[hc % 2], "pM%d" % (hc % 2)
                    RL, rlk = rl[hc % 2], "rl%d" % (hc % 2)
                    for k in range(8):
                        P.op("pe", lambda e, k=k, hc=hc, tg=tg, pm=pm, W1B=W1B: e.matmul(
                            pm[:, :], lhsT=W1B[:, k, hc * 128:(hc + 1) * 128], rhs=h2T[:, k, tg * 512:(tg + 1) * 512],
                            start=(k == 0), stop=(k == 7)), reads=[w1k, "h2T"], writes=[pkey])
                    P.op("act", lambda e, pm=pm, RL=RL: e.activation(out=RL[:, :], in_=pm[:, :], func=AF.Relu),
                         reads=[pkey], writes=[rlk])
                    P.op("pool", lambda e, RL=RL, AT=AT, hc=hc: e.tensor_tensor(out=AT[:, hc, :], in0=RL[:, :], in1=RL[:, :], op=ALU.mult),
                         reads=[rlk], writes=[atk])
                for tt in range(4):
                    tile = tg * 4 + tt
                    for nh in range(2):
                        pp, ppk = (pA, "pA%d" % nh) if tt % 2 == 0 else (pH, "pH%d" % nh)
                        for hc in range(4):
                            P.op("pe", lambda e, hc=hc, tt=tt, nh=nh, pp=pp, AT=AT, W2B=W2B: e.matmul(
                                pp[:, nh * 512:(nh + 1) * 512], lhsT=AT[:, hc, tt * 128:(tt + 1) * 128],
                                rhs=W2B[:, hc, nh * 512:(nh + 1) * 512], start=(hc == 0), stop=(hc == 3)),
                                reads=[atk, w2k], writes=[ppk])
                        fv = f2acc[:, tile, nh * 512:(nh + 1) * 512]
                        fk = "f2_%d_%d" % (tile, nh)
                        if j == 0:
                            P.op("dve", lambda e, fv=fv, pp=pp, nh=nh: e.tensor_copy(out=fv, in_=pp[:, nh * 512:(nh + 1) * 512]),
                                 reads=[ppk], writes=[fk])
                        else:
                            P.op("dve", lambda e, fv=fv, pp=pp, nh=nh: e.tensor_tensor(out=fv, in0=pp[:, nh * 512:(nh + 1) * 512], in1=fv, op=ALU.add),
                                 reads=[ppk, fk], writes=[fk])
        for tile in range(8):
            t0 = half * 1024 + tile * 128
            i = it[0]
            it[0] += 1
            X, xk = xt[i % 2], "xt%d" % (i % 2)
            P.dma("sp", X[:, :], x1s[t0:t0 + 128, :], reads=["x1s%d" % (t0 // 128)], writes=[xk], key=xk)
            fkeys = ["f2_%d_%d" % (tile, nh) for nh in range(2)]
            ss4, ss4k = smallcol(1)
            r4, r4k = smallcol(1)
            sumsq(f2acc[:, tile, :], ss4[:, 0:1], fkeys, ss4k)
            _rs(P, None, ss4, r4, 1024.0, "rf", [ss4k], r4k)
            P.op("dve", lambda e, r4=r4, tile=tile: e.scalar_tensor_tensor(out=tmp[:, :], in0=f2acc[:, tile, :], scalar=r4[:, 0:1], in1=G3,
                                                                          op0=ALU.mult, op1=ALU.mult),
                 reads=fkeys + [r4k, "mods"], writes=["tmp"])
            P.op("pool", lambda e, X=X: e.tensor_tensor(out=X[:, :], in0=tmp[:, :], in1=X[:, :], op=ALU.add),
                 reads=["tmp", xk], writes=[xk])
            P.dma("pool", out[t0:t0 + 128, :], X[:, :], reads=[xk], writes=["out%d" % (t0 // 128)])
    P.build()
    return nc


def run_phase2(inp, mixed):
    f = lambda a: np.ascontiguousarray(a, dtype=np.float32)
    nc = build_phase2()
    in_maps = []
    cols = np.r_[2048:3072, 3072:6144]
    for core in range(8):
        b, j = core // 4, core % 4
        sl = slice(j * 2048, (j + 1) * 2048)
        in_maps.append({
            "x_tok": f(inp["x"][b, sl]),
            "mix_tok": f(mixed[b, sl]),
            "mixT": f(mixed[b, sl].T),
            "cvec": f(inp["c"][b].reshape(8, 128).T),
            "w_ada": f(inp["w_ada"][0][:, cols]),
            "b_ada": f(inp["b_ada"][0][cols][None, :]),
            "gvec": f(np.stack([inp["g_post_mix"][0], inp["g_pre_mlp"][0], inp["g_post_mlp"][0]])),
            "gmix": f(np.concatenate([inp["g_attn_out"][0], inp["g_hyena_out"][0]]).reshape(8, 128).T),
            "w_out": f(inp["w_out"][0]),
            "w1": f(inp["w_mlp1"][0]),
            "w2": f(inp["w_mlp2"][0]),
            "identd": np.eye(128, dtype=np.float32),
        })
    res = run_bass_kernel_spmd(nc, in_maps, core_ids=list(range(8)))
    out = np.zeros((2, 8192, 1024), np.float32)
    for core in range(8):
        b, j = core // 4, core % 4
        out[b, j * 2048:(j + 1) * 2048] = res.results[core]["out"]
    return out


ST = 512
NTOK = 8192
NST = NTOK // ST


def _p1_common(P, nc, ncols_w, ST=512, SW=256):
    D = lambda name, shape, kind="ExternalInput": nc.dram_tensor(name, list(shape), F32, kind=kind).ap()
    H = {}
    H["xT"] = D("xT", [1024, NTOK])
    cvec = D("cvec", [128, 8])
    w_ada1 = D("w_ada1", [1024, 2048])
    b_ada1 = D("b_ada1", [128, 16])
    gpre = D("gpre", [128, 8])
    w_inc = D("w_inc", [1024, ncols_w])
    sb, ps = P.sb, P.ps
    H["wb"] = wb = sb("wb", [128, 8, ncols_w], BF16)
    stage = [sb("stage%d" % i, [128, 8, SW], F32) for i in range(2)]
    H["ST"] = ST
    H["stage"] = stage
    csb = sb("csb", [128, 8], F32)
    gp = sb("gp", [128, 8], F32)
    bsb = sb("bsb", [128, 16], F32)
    H["modc"] = modc = sb("modc", [128, 16], F32)
    H["G0"] = G0 = sb("G0", [128, 8], F32)
    H["onesb"] = onesb = sb("onesb", [128, 128], BF16)
    H["xs"] = [sb("xs%d" % i, [128, 8, ST], F32) for i in range(2)]
    H["sq"] = sb("sq", [128, 8, ST], BF16)
    H["hT"] = sb("hT", [128, 8, ST], BF16)
    H["rbc"] = sb("rbc", [128, ST], F32)
    H["pSS"] = pSS = ps("pSS", [128, 512], F32)

    P.op("pool", lambda e: e.memset(onesb[:, :], 1.0), writes=["onesb"])
    P.dma("sp", csb[:, :], cvec[:, :], writes=["csb"])
    P.dma("sp", gp[:, :], gpre[:, :], writes=["gp"])
    P.dma("sp", bsb[:, :], b_ada1[:, :], writes=["bsb"])
    P.op("act", lambda e: e.activation(out=csb[:, :], in_=csb[:, :], func=AF.Silu), reads=["csb"], writes=["csb"])
    wa = w_ada1.rearrange("(k p) n -> p k n", p=128)
    for blk in range(2048 // SW):
        st, skey = stage[blk % 2], "stage%d" % (blk % 2)
        P.dma("sp", st[:, :, :], wa[:, :, blk * SW:(blk + 1) * SW], writes=[skey])
        for jj in range(SW // 128):
            j = (SW // 128) * blk + jj
            for k in range(8):
                P.op("pe", lambda e, k=k, j=j, jj=jj, st=st: e.matmul(pSS[:, j:j + 1], lhsT=st[:, k, jj * 128:(jj + 1) * 128],
                                                                      rhs=csb[:, k:k + 1], start=(k == 0), stop=(k == 7)),
                     reads=[skey, "csb"], writes=["pSS"])
    P.op("dve", lambda e: e.tensor_tensor(out=modc[:, :], in0=pSS[:, 0:16], in1=bsb[:, :], op=ALU.add),
         reads=["pSS", "bsb"], writes=["modc"])
    P.op("dve", lambda e: e.scalar_tensor_tensor(out=G0[:, :], in0=modc[:, 8:16], scalar=1.0, in1=gp[:, :], op0=ALU.add, op1=ALU.mult),
         reads=["modc", "gp"], writes=["G0"])
    wv = w_inc.rearrange("(k p) n -> p k n", p=128)
    nb = ncols_w // SW
    for blk in range(nb):
        st, skey = stage[blk % 2], "stage%d" % (blk % 2)
        P.dma("sp", st[:, :, :], wv[:, :, blk * SW:(blk + 1) * SW], writes=[skey])
        P.op("dve" if blk % 2 == 0 else "pool", lambda e, st=st, blk=blk: e.tensor_copy(out=wb[:, :, blk * SW:(blk + 1) * SW], in_=st[:, :, :]),
             reads=[skey], writes=["wb"])
    return H


def _p1_load(P, H, st):
    ST = H["ST"]
    xs, xk = H["xs"][st % 2], "xs%d" % (st % 2)
    xTv = H["xT"].rearrange("(k p) t -> p k t", p=128)
    P.dma("sp", xs[:, :, :], xTv[:, :, st * ST:(st + 1) * ST], writes=[xk])


def _p1_norm(P, H, st):
    ST = H["ST"]
    xs, xk = H["xs"][st % 2], "xs%d" % (st % 2)
    sq, hT, rbc, pSS, onesb, modc, G0 = H["sq"], H["hT"], H["rbc"], H["pSS"], H["onesb"], H["modc"], H["G0"]
    P.op("act", lambda e: e.activation(out=sq[:, :, :], in_=xs[:, :, :], func=AF.Square), reads=[xk], writes=["sq"])
    for k in range(8):
        P.op("pe", lambda e, k=k: e.matmul(pSS[:, 0:ST], lhsT=onesb[:, :], rhs=sq[:, k, :], start=(k == 0), stop=(k == 7)),
             reads=["onesb", "sq"], writes=["pSS"])
    P.op("dve", lambda e: e.tensor_scalar(out=rbc[:, :], in0=pSS[:, 0:ST], scalar1=1.0 / 1024, scalar2=EPS, op0=ALU.mult, op1=ALU.add),
         reads=["pSS"], writes=["rbc"])
    P.op("act", lambda e: e.activation(out=rbc[:, :], in_=rbc[:, :], func=AF.Sqrt), reads=["rbc"], writes=["rbc"])
    P.op("dve", lambda e: e.reciprocal(out=rbc[:, :], in_=rbc[:, :]), reads=["rbc"], writes=["rbc"])
    for k in range(8):
        P.op("dve", lambda e, k=k: e.scalar_tensor_tensor(out=xs[:, k, :], in0=xs[:, k, :], scalar=G0[:, k:k + 1], in1=rbc[:, :],
                                                          op0=ALU.mult, op1=ALU.mult), reads=[xk, "G0", "rbc"], writes=[xk])
        P.op("act", lambda e, k=k: e.activation(out=hT[:, k, :], in_=xs[:, k, :], func=AF.Identity, bias=modc[:, k:k + 1], scale=1.0),
             reads=[xk, "modc"], writes=["hT"])


def build_attn():
    nc = bass.Bass("TRN2", target_bir_lowering=False)
    P = Prog(nc)
    D = lambda name, shape, kind="ExternalInput": nc.dram_tensor(name, list(shape), F32, kind=kind).ap()
    H = _p1_common(P, nc, 768)
    ropeC = D("ropeC", [128, NTOK])
    ropeS = D("ropeS", [128, NTOK])
    maskd = D("maskd", [128, 17 * 128])
    hseld = D("hseld", [128, 64])
    attn_o = D("attn_o", [NTOK, 128], kind="ExternalOutput")
    sb, ps = P.sb, P.ps
    wb, hT = H["wb"], H["hT"]
    QT = sb("QT", [128, NTOK], BF16)
    KT = sb("KT", [128, NTOK], BF16)
    Vaug = sb("Vaug", [128, 64, 2, 65], BF16)
    Mall = sb("Mall", [128, 17 * 128], BF16)
    mst = sb("mst", [128, 17 * 128], F32)
    hself = sb("hself", [128, 64], F32)
    hsel = sb("hsel", [128, 64], BF16)
    onesrow = sb("onesrow", [64, 128], BF16)
    rc = [sb("rc%d" % i, [128, ST], F32) for i in range(2)]
    rs_ = [sb("rs%d" % i, [128, ST], F32) for i in range(2)]
    t1 = sb("t1", [128, ST], F32)
    t2 = sb("t2", [128, ST], F32)
    sqk = sb("sqk", [128, ST], BF16)
    kmx = sb("kmx", [64, 2], F32)
    qn = sb("qn", [64, 128], F32)
    negm = sb("negm", [64, 128], BF16)
    PT = [sb("PT%d" % i, [128, 512], BF16) for i in range(2)]
    ao = [sb("ao%d" % i, [128, 128], F32) for i in range(2)]
    rec = sb("rec", [128, 4], F32)
    pA = ps("pA", [128, 512], F32)
    pB = ps("pB", [128, 512], F32)
    pV = ps("pV", [128, 512], F32)
    pN = H["pSS"]
    pS = [ps("pS%d" % i, [128, 512], F32) for i in range(2)]
    pO = [ps("pO%d" % i, [128, 2, 128], F32) for i in range(2)]

    P.dma("sp", mst[:, :], maskd[:, :], writes=["mst"])
    P.op("pool", lambda e: e.tensor_copy(out=Mall[:, :], in_=mst[:, :]), reads=["mst"], writes=["Mall"])
    P.dma("sp", hself[:, :], hseld[:, :], writes=["hself"])
    P.op("pool", lambda e: e.tensor_copy(out=hsel[:, :], in_=hself[:, :]), reads=["hself"], writes=["hsel"])
    P.op("pool", lambda e: e.memset(onesrow[:, :], 1.0), writes=["onesrow"])
    P.op("pool", lambda e: e.memset(kmx[:, :], 0.0), writes=["kmx"])
    P.op("pool", lambda e: e.memset(Vaug[:, :, :, 64:65], 1.0), writes=["Vaug"])

    _p1_load(P, H, 0)
    for st in range(NST):
        if st + 1 < NST:
            _p1_load(P, H, st + 1)
        C, ck = rc[st % 2], "rc%d" % (st % 2)
        S, sk = rs_[st % 2], "rs%d" % (st % 2)
        P.dma("sp", C[:, :], ropeC[:, st * ST:(st + 1) * ST], writes=[ck])
        P.dma("sp", S[:, :], ropeS[:, st * ST:(st + 1) * ST], writes=[sk])
        _p1_norm(P, H, st)
        for which, dst in ((0, QT), (1, KT)):
            dk = "QT" if which == 0 else "KT"
            c0 = which * 256
            for k in range(8):
                P.op("pe", lambda e, k=k, c0=c0: e.matmul(pA[:, :], lhsT=wb[:, k, c0:c0 + 128], rhs=hT[:, k, :], start=(k == 0), stop=(k == 7)),
                     reads=["wb", "hT"], writes=["pA"])
            for k in range(8):
                P.op("pe", lambda e, k=k, c0=c0: e.matmul(pB[:, :], lhsT=wb[:, k, c0 + 128:c0 + 256], rhs=hT[:, k, :], start=(k == 0), stop=(k == 7)),
                     reads=["wb", "hT"], writes=["pB"])
            P.op("dve", lambda e, C=C: e.tensor_tensor(out=t1[:, :], in0=pA[:, :], in1=C[:, :], op=ALU.mult), reads=["pA", ck], writes=["t1"])
            P.op("dve", lambda e, S=S: e.tensor_tensor(out=t2[:, :], in0=pB[:, :], in1=S[:, :], op=ALU.mult), reads=["pB", sk], writes=["t2"])
            P.op("pool", lambda e, dst=dst, st=st: e.tensor_tensor(out=dst[:, st * ST:(st + 1) * ST], in0=t1[:, :], in1=t2[:, :], op=ALU.add),
                 reads=["t1", "t2"], writes=[dk])
        P.op("pool", lambda e, st=st: e.tensor_tensor(out=sqk[:, :], in0=KT[:, st * ST:(st + 1) * ST], in1=KT[:, st * ST:(st + 1) * ST], op=ALU.mult),
             reads=["KT"], writes=["sqk"])
        P.op("pe", lambda e: e.matmul(pN[0:64, :], lhsT=hsel[:, :], rhs=sqk[:, :], start=True, stop=True), reads=["hsel", "sqk"], writes=["pSS"])
        P.op("dve", lambda e: e.tensor_reduce(out=kmx[:, 1:2], in_=pN[0:64, :], axis=AX.X, op=ALU.max), reads=["pSS", "kmx"], writes=["kmx"])
        P.op("dve", lambda e: e.tensor_tensor(out=kmx[:, 0:1], in0=kmx[:, 0:1], in1=kmx[:, 1:2], op=ALU.max), reads=["kmx"], writes=["kmx"])
        for tt in range(4):
            for k in range(8):
                P.op("pe", lambda e, k=k, tt=tt: e.matmul(pV[:, tt * 128:(tt + 1) * 128], lhsT=hT[:, k, tt * 128:(tt + 1) * 128],
                                                          rhs=wb[:, k, 512:640], start=(k == 0), stop=(k == 7)),
                     reads=["wb", "hT"], writes=["pV"])
        P.op("act", lambda e, st=st: e.activation(out=Vaug[:, st * 4:(st + 1) * 4, :, 0:64],
                                                  in_=pV[:, :].rearrange("p (t h d) -> p t h d", t=4, h=2), func=AF.Copy),
             reads=["pV"], writes=["Vaug"])
    P.op("act", lambda e: e.activation(out=kmx[:, 0:1], in_=kmx[:, 0:1], func=AF.Sqrt), reads=["kmx"], writes=["kmx"])
    cnt = [0]
    for jb in range(64):
        qs = slice(jb * 128, (jb + 1) * 128)
        P.op("pool", lambda e, qs=qs: e.tensor_tensor(out=sqk[:, 0:128], in0=QT[:, qs], in1=QT[:, qs], op=ALU.mult), reads=["QT"], writes=["sqk"])
        P.op("pe", lambda e: e.matmul(pN[0:64, 0:128], lhsT=hsel[:, :], rhs=sqk[:, 0:128], start=True, stop=True), reads=["hsel", "sqk"], writes=["pSS"])
        P.op("act", lambda e: e.activation(out=qn[:, :], in_=pN[0:64, 0:128], func=AF.Sqrt), reads=["pSS"], writes=["qn"])
        P.op("dve", lambda e: e.tensor_scalar(out=negm[:, :], in0=qn[:, :], scalar1=kmx[:, 0:1], scalar2=-1.0, op0=ALU.mult, op1=ALU.mult),
             reads=["qn", "kmx"], writes=["negm"])
        AO, aok = ao[jb % 2], "ao%d" % (jb % 2)
        PO, pok = pO[jb % 2], "pO%d" % (jb % 2)
        for h in range(2):
            hs = slice(64 * h, 64 * h + 64)
            dms = [dm for dm in range(-8, 9) if 0 <= jb + dm < 64]
            groups = [dms[i:i + 4] for i in range(0, len(dms), 4)]
            nmm = 0
            for grp in groups:
                g = cnt[0]
                cnt[0] += 1
                psx, psk = pS[g % 2], "pS%d" % (g % 2)
                ptx, ptk = PT[g % 2], "PT%d" % (g % 2)
                n = len(grp)
                for i, dm in enumerate(grp):
                    kc = jb + dm
                    P.op("pe", lambda e, i=i, kc=kc, hs=hs, qs=qs, psx=psx: e.matmul(psx[:, i * 128:(i + 1) * 128], lhsT=KT[hs, kc * 128:(kc + 1) * 128],
                                                                                      rhs=QT[hs, qs], start=True, stop=False),
                         reads=["KT", "QT"], writes=[psk])
                    P.op("pe", lambda e, i=i, h=h, psx=psx: e.matmul(psx[:, i * 128:(i + 1) * 128], lhsT=onesrow[32 * h:32 * h + 1, :],
                                                                     rhs=negm[32 * h:32 * h + 1, :], start=False, stop=True),
                         reads=["onesrow", "negm"], writes=[psk])
                P.op("act", lambda e, psx=psx, ptx=ptx, n=n: e.activation(out=ptx[:, 0:n * 128], in_=psx[:, 0:n * 128], func=AF.Exp, scale=0.125),
                     reads=[psk], writes=[ptk])
                m0 = (grp[0] + 8) * 128
                P.op("dve" if g % 2 == 0 else "pool", lambda e, ptx=ptx, n=n, m0=m0: e.tensor_tensor(
                    out=ptx[:, 0:n * 128], in0=ptx[:, 0:n * 128], in1=Mall[:, m0:m0 + n * 128], op=ALU.mult),
                    reads=[ptk, "Mall"], writes=[ptk])
                for i, dm in enumerate(grp):
                    kc = jb + dm
                    P.op("pe", lambda e, i=i, kc=kc, h=h, ptx=ptx, PO=PO, first=(nmm == 0), last=(nmm == len(dms) - 1): e.matmul(
                        PO[:, h, 0:65], lhsT=ptx[:, i * 128:(i + 1) * 128], rhs=Vaug[:, kc, h, :], start=first, stop=last),
                        reads=[ptk, "Vaug"], writes=[pok])
                    nmm += 1
            P.op("dve", lambda e, h=h, PO=PO: e.reciprocal(out=rec[:, h:h + 1], in_=PO[:, h, 64:65]), reads=[pok], writes=["rec%d" % h])
            P.op("dve", lambda e, h=h, PO=PO, AO=AO: e.tensor_scalar(out=AO[:, 64 * h:64 * h + 64], in0=PO[:, h, 0:64], scalar1=rec[:, h:h + 1],
                                                                     scalar2=None, op0=ALU.mult),
                 reads=[pok, "rec%d" % h], writes=[aok])
        P.dma("pool", attn_o[qs, :], AO[:, :], reads=[aok], writes=["attn_o%d" % jb])
    P.build()
    return nc


def _rope_tables():
    half = 32
    inv = (10000.0 ** (-np.arange(half, dtype=np.float32) / half)).astype(np.float32)
    pos = np.arange(NTOK, dtype=np.float32)
    ang = (pos[:, None] * inv[None, :]).astype(np.float32)
    cos = np.cos(ang).astype(np.float32).T
    sin = np.sin(ang).astype(np.float32).T
    C = np.concatenate([cos, cos, cos, cos], axis=0)
    S = np.concatenate([-sin, sin, -sin, sin], axis=0)
    return np.ascontiguousarray(C), np.ascontiguousarray(S)


def _attn_masks():
    o = np.arange(-8 * 128 - 127, 8 * 128 + 128)
    mult = ((np.abs(o) <= 64).astype(np.float32) + ((np.abs(o) <= 256) & (o % 4 == 0)).astype(np.float32)
            + ((np.abs(o) <= 1024) & (o % 16 == 0)).astype(np.float32))
    off0 = -(8 * 128 + 127)
    M = np.zeros((128, 17, 128), np.float32)
    kl = np.arange(128)[:, None]
    ql = np.arange(128)[None, :]
    for i, dm in enumerate(range(-8, 9)):
        M[:, i, :] = mult[(128 * dm + kl - ql) - off0]
    return M.reshape(128, 17 * 128)


def _perm_cols(w):
    w = w.reshape(w.shape[0], -1, 2, 32)
    return w[:, :, ::-1, :].reshape(w.shape[0], -1)


def run_attn(inp):
    f = lambda a: np.ascontiguousarray(a, dtype=np.float32)
    nc = build_attn()
    C, S = _rope_tables()
    M = _attn_masks()
    hsel = np.zeros((128, 64), np.float32)
    hsel[0:64, 0] = 1.0
    hsel[64:128, 32] = 1.0
    w_in = inp["w_in"][0]
    in_maps = []
    for core in range(8):
        b, g = core // 4, core % 4
        cs = slice(128 * g, 128 * g + 128)
        wq, wk, wv = w_in[:, 0:512][:, cs], w_in[:, 512:1024][:, cs], w_in[:, 1024:1536][:, cs]
        w_inc = np.concatenate([wq, _perm_cols(wq), wk, _perm_cols(wk), wv, np.zeros((1024, 128), np.float32)], axis=1)
        in_maps.append({
            "xT": f(inp["x"][b].T), "cvec": f(inp["c"][b].reshape(8, 128).T),
            "w_ada1": f(inp["w_ada"][0][:, 0:2048]), "b_ada1": f(inp["b_ada"][0][0:2048].reshape(16, 128).T),
            "gpre": f(inp["g_pre_mix"][0].reshape(8, 128).T), "w_inc": f(w_inc),
            "ropeC": C, "ropeS": S, "maskd": f(M), "hseld": hsel,
        })
    res = run_bass_kernel_spmd(nc, in_maps, core_ids=list(range(8)))
    attn = np.zeros((2, NTOK, 512), np.float32)
    for core in range(8):
        b, g = core // 4, core % 4
        attn[b, :, 128 * g:128 * g + 128] = res.results[core]["attn_o"]
    return attn


NFFT = 16384
NPQ = 4
NPQH = 8


def _hy_consts():
    N = NFFT
    a = np.arange(128)[:, None].astype(np.float64)
    k1 = np.arange(256)[None, :].astype(np.float64)
    F1cat = np.concatenate([np.cos(2 * np.pi * a * k1 / 256), -np.sin(2 * np.pi * a * k1 / 256)], 1)
    th = 2 * np.pi / N
    pm = (np.arange(128) % 64)[:, None].astype(np.float64)
    Tr, Ti = np.cos(th * pm * k1), -np.sin(th * pm * k1)
    TrTr, TiTi = np.concatenate([Tr, Tr], 1), np.concatenate([Ti, Ti], 1)
    pp = np.arange(64)[:, None].astype(np.float64)
    k2 = np.arange(64)[None, :].astype(np.float64)
    g2r, g2i = np.cos(2 * np.pi * pp * k2 / 64), -np.sin(2 * np.pi * pp * k2 / 64)
    Z0 = np.zeros((64, 64))
    G2r = np.block([[g2r, Z0], [Z0, g2r]])
    G2i = np.block([[g2i, Z0], [Z0, g2i]])
    R12 = np.concatenate([G2r, -G2i, G2i, G2r], 1)
    k1c = np.arange(256)[:, None].astype(np.float64)
    pcol = (np.arange(128) % 64)[None, :].astype(np.float64)
    T2r, T2i = np.cos(th * pcol * k1c), np.sin(th * pcol * k1c)
    TT2r = np.concatenate([T2r[0:128], T2r[0:128], T2r[128:256], T2r[128:256]], 1)
    TT2i = np.concatenate([T2i[0:128], T2i[0:128], T2i[128:256], T2i[128:256]], 1)
    aa = np.arange(128)[None, :].astype(np.float64)
    IF1c, IF1s = np.cos(2 * np.pi * aa * k1c / 256), -np.sin(2 * np.pi * aa * k1c / 256)
    IF1 = np.concatenate([IF1c[0:128], IF1s[0:128], IF1c[128:256], IF1s[128:256]], 1)
    t_lin = np.linspace(0.0, 1.0, 8192, dtype=np.float32)
    Swin = -t_lin.reshape(128, 64)
    L = 8192
    t = np.linspace(0.0, 1.0, L, dtype=np.float32)[:, None]
    w = (np.float32(2.0 * math.pi) * np.arange(L, dtype=np.float32)[:, None] / np.float32(L)).astype(np.float32)
    f = np.linspace(1e-4, 15, 16, dtype=np.float32)[None, :]
    zemb = np.concatenate([t, np.cos(f * w), -np.sin(f * w)], axis=-1).astype(np.float32)
    max_decay = math.log(1e-2) / 0.3
    min_decay = math.log(1e-2) / 1.5
    deltas = np.abs(np.linspace(min_decay, max_decay, 512, dtype=np.float32)).astype(np.float32)
    f32 = lambda x: np.ascontiguousarray(x, dtype=np.float32)
    return dict(F1cat=f32(F1cat), TrTr=f32(TrTr), TiTi=f32(TiTi), R12=f32(R12), TT2r=f32(TT2r), TT2i=f32(TT2i),
                IF1=f32(IF1), Swin=f32(Swin), zembT=f32(zemb.T), deltas=deltas)


def _hy_consts_h():
    N = NFFT
    a = np.arange(128)[:, None].astype(np.float64)
    k1 = np.arange(128)[None, :].astype(np.float64) + 0.5
    F1cat = np.concatenate([np.cos(2 * np.pi * a * k1 / 256), -np.sin(2 * np.pi * a * k1 / 256)], 1)
    th = 2 * np.pi / N
    pm = (np.arange(128) % 64)[:, None].astype(np.float64)
    Tr, Ti = np.cos(th * pm * k1), -np.sin(th * pm * k1)
    TrTr, TiTi = np.concatenate([Tr] * 4, 1), np.concatenate([Ti] * 4, 1)
    k1c = (np.arange(128)[:, None].astype(np.float64) + 0.5)
    pcol = (np.arange(128) % 64)[None, :].astype(np.float64)
    T2r, T2i = np.cos(th * pcol * k1c), np.sin(th * pcol * k1c)
    TT2r, TT2i = np.concatenate([T2r] * 4, 1), np.concatenate([T2i] * 4, 1)
    aa = np.arange(128)[None, :].astype(np.float64)
    IF1 = np.concatenate([np.cos(2 * np.pi * aa * k1c / 256), -np.sin(2 * np.pi * aa * k1c / 256)], 1)
    f32 = lambda x: np.ascontiguousarray(x, dtype=np.float32)
    return dict(F1cat_h=f32(F1cat), TrTr_h=f32(TrTr), TiTi_h=f32(TiTi), TT2r_h=f32(TT2r), TT2i_h=f32(TT2i), IF1_h=f32(IF1))


def build_hyena(stop=99):
    nc = bass.Bass("TRN2", target_bir_lowering=False)
    P = Prog(nc)
    D = lambda name, shape, kind="ExternalInput": nc.dram_tensor(name, list(shape), F32, kind=kind).ap()
    ST = 256
    H = _p1_common(P, nc, 512, ST=ST, SW=128)
    sb, ps = P.sb, P.ps
    wb, hT, pSS = H["wb"], H["hT"], H["pSS"]
    dF1cat, dTrTr, dTiTi, dR12 = D("F1cat", [128, 512]), D("TrTr", [128, 512]), D("TiTi", [128, 512]), D("R12", [128, 512])
    dTT2r, dTT2i, dIF1, dSwin = D("TT2r", [128, 512]), D("TT2i", [128, 512]), D("IF1", [128, 512]), D("Swin", [128, 64])
    dzemb = D("zembT", [33, NTOK])
    ddelta = D("delta", [128, 128])
    dhbcol = D("hbcol", [128, 128])
    dfw1, dfw2, dfw3, dfpar = D("fw1", [33, 64]), D("fw2", [64, 64]), D("fw3c", [64, 512]), D("fpar", [64, 4])
    dcw, dcb = D("cw", [128, 9]), D("cb", [128, 3])
    hy_oz = D("hy_oz", [128, 128, 64], kind="ExternalOutput")

    U = [sb("U%d" % i, [128, NTOK + 2], BF16) for i in range(3)]
    CV = sb("CV", [128, NTOK], BF16)
    Ob = sb("Ob", [128, NTOK], BF16)
    h2T = sb("h2T", [64, NTOK], BF16)
    Ap = sb("Ap", [128, 2, NPQ, 256], BF16)
    Hs = sb("Hs", [128, 2, NPQ, 256], BF16)
    Yb = sb("Yb", [128, 2, NPQ, 256], BF16)
    Bp = sb("Bp", [128, 2, 2, NPQ, 128], BF16)
    W1 = [sb("W1_%d" % i, [128, 512], F32) for i in range(2)]
    W2 = [sb("W2_%d" % i, [128, 512], F32) for i in range(2)]
    cpy = [sb("cpy%d" % i, [128, 512], F32) for i in range(2)]
    cpy2 = sb("cpy2", [128, 512], F32)
    tmpc = sb("tmpc", [128, 1024], F32)
    arg = tmpc[0:64, 0:512]
    argi = sb("argi", [64, 512], mybir.dt.int32)
    h1 = tmpc[0:64, 512:1024]
    F1b = sb("F1b", [128, 512], BF16)
    R12b = sb("R12b", [128, 512], BF16)
    IF1b = sb("IF1b", [128, 512], BF16)
    TrTr, TiTi = sb("TrTr", [128, 512], F32), sb("TiTi", [128, 512], F32)
    TT2r, TT2i = sb("TT2r", [128, 512], F32), sb("TT2i", [128, 512], F32)
    Swin = sb("Swin", [128, 64], F32)
    delta = sb("delta", [128, 128], F32)
    hbcol = sb("hbcol", [128, 128], F32)
    fw1, fw2 = sb("fw1", [33, 64], F32), sb("fw2", [64, 64], F32)
    fw3b = sb("fw3b", [64, 512], BF16)
    fpar = sb("fpar", [64, 8], F32)
    cw, cb = sb("cw", [128, 9], F32), sb("cb", [128, 3], F32)
    zc = [H["stage"][i][0:33, 0:4, :].rearrange("p k n -> p (k n)") for i in range(2)]
    win = sb("win", [128, 128], F32)
    fwbw = sb("fwbw", [128, 2, 128], F32)
    ot = [H["xs"][i][:, 0:2, :].rearrange("p k n -> p (k n)") for i in range(2)]

    pA1 = [ps("pA1_%d" % i, [128, 512], F32) for i in range(2)]
    pXr, pXi = ps("pXr", [128, 512], F32), ps("pXi", [128, 512], F32)
    pB = ps("pB", [128, 512], F32)
    pY = ps("pY", [128, 512], F32)
    pTz = ps("pTz", [128, 8, 128], BF16)
    ident = sb("ident", [128, 128], BF16)
    identf = W1[0]
    didn = D("identd", [128, 128])

    def ldcast(dst, dkey, src, n=512, parts=128):
        P.dma("sp", cpy2[0:parts, 0:n], src, writes=["cpy2"])
        P.op("dve", lambda e: e.tensor_copy(out=dst, in_=cpy2[0:parts, 0:n]), reads=["cpy2"], writes=[dkey])
    ldcast(F1b[:, :], "F1b", dF1cat[:, :])
    ldcast(R12b[:, :], "R12b", dR12[:, :])
    ldcast(IF1b[:, :], "IF1b", dIF1[:, :])
    ldcast(ident[:, :], "ident", didn[:, :], n=128)
    ldcast(fw3b[:, :], "fw3b", dfw3[:, :], parts=64)
    for dst, key, src in ((TrTr, "TrTr", dTrTr), (TiTi, "TiTi", dTiTi), (TT2r, "TT2r", dTT2r), (TT2i, "TT2i", dTT2i), (Swin, "Swin", dSwin),
                          (delta, "delta", ddelta), (hbcol, "hbcol", dhbcol), (fw1, "fw1", dfw1), (fw2, "fw2", dfw2), (cw, "cw", dcw), (cb, "cb", dcb)):
        P.dma("sp", dst[:, :], src[:, :], writes=[key])
    P.dma("sp", fpar[:, 0:4], dfpar[:, :], writes=["fpar"])
    i2p = 1.0 / (2.0 * math.pi)
    for (bc, fc, o0) in ((0, 1, 4), (2, 3, 6)):
        P.op("dve", lambda e, bc=bc, fc=fc, o0=o0: e.tensor_tensor(out=fpar[:, o0 + 1:o0 + 2], in0=fpar[:, bc:bc + 1], in1=fpar[:, fc:fc + 1], op=ALU.mult),
             reads=["fpar"], writes=["fpar"])
        P.op("dve", lambda e, o0=o0: e.tensor_scalar(out=fpar[:, o0 + 1:o0 + 2], in0=fpar[:, o0 + 1:o0 + 2], scalar1=i2p, scalar2=16.0, op0=ALU.mult, op1=ALU.add),
             reads=["fpar"], writes=["fpar"])
        P.op("dve", lambda e, fc=fc, o0=o0: e.tensor_scalar(out=fpar[:, o0:o0 + 1], in0=fpar[:, fc:fc + 1], scalar1=i2p, scalar2=None, op0=ALU.mult),
             reads=["fpar"], writes=["fpar"])
    for i in range(3):
        P.op("pool", lambda e, i=i: e.memset(U[i][:, 0:1], 0.0), writes=["U%d" % i])
        P.op("pool", lambda e, i=i: e.memset(U[i][:, NTOK + 1:NTOK + 2], 0.0), writes=["U%d" % i])

    nst = NTOK // ST
    _p1_load(P, H, 0)
    pU = [pXr, pXi, pY]
    for st in range(nst):
        if st + 1 < nst:
            _p1_load(P, H, st + 1)
        _p1_norm(P, H, st)
        for i in range(3):
            pk = ["pXr", "pXi", "pY"][i]
            for k in range(8):
                P.op("pe", lambda e, k=k, i=i: e.matmul(pU[i][:, 0:ST], lhsT=wb[:, k, i * 128:(i + 1) * 128], rhs=hT[:, k, :],
                                                        start=(k == 0), stop=(k == 7)), reads=["wb", "hT"], writes=[pk])
            P.op("act", lambda e, i=i, st=st: e.activation(out=U[i][:, 1 + st * ST:1 + (st + 1) * ST], in_=pU[i][:, 0:ST], func=AF.Copy),
                 reads=[pk], writes=["U%d" % i])

    if stop <= 1:
        P.dma("pool", hy_oz[:, 0:4, :], U[0][:, 1:257].rearrange("a (c p) -> a c p", p=64).bitcast(F32) if False else H["xs"][0][:, 0, :].rearrange("a (c p) -> a c p", p=64), reads=["U0", "U1", "U2", "xs0"], writes=["dbg"])
        P.build()
        return nc
    Zkeys = lambda i: ["Z%d_%d" % (i, s) for s in range(128 // (2 * NPQ))]
    Z = [U[i][:, 0:NTOK].rearrange("a (c p) -> a c p", p=64) for i in range(3)]
    CVs = CV[:, :].rearrange("c (a p) -> c a p", p=64)
    for i in range(3):
        for ch in range(8):
            j0 = ch * 1024
            P.op("dve", lambda e, i=i, j0=j0: e.tensor_scalar(out=tmpc[:, :], in0=U[i][:, j0:j0 + 1024], scalar1=cw[:, 3 * i:3 * i + 1],
                                                              scalar2=cb[:, i:i + 1], op0=ALU.mult, op1=ALU.add),
                 reads=["U%d" % i, "cw", "cb"], writes=["tmpc"])
            P.op("dve", lambda e, i=i, j0=j0: e.scalar_tensor_tensor(out=tmpc[:, :], in0=U[i][:, j0 + 1:j0 + 1025], scalar=cw[:, 3 * i + 1:3 * i + 2],
                                                                     in1=tmpc[:, :], op0=ALU.mult, op1=ALU.add),
                 reads=["U%d" % i, "cw", "tmpc"], writes=["tmpc"])
            P.op("dve", lambda e, i=i, j0=j0: e.scalar_tensor_tensor(out=CV[:, j0:j0 + 1024], in0=U[i][:, j0 + 2:j0 + 1026], scalar=cw[:, 3 * i + 2:3 * i + 3],
                                                                     in1=tmpc[:, :], op0=ALU.mult, op1=ALU.add),
                 reads=["U%d" % i, "cw", "tmpc"], writes=["CV"])
        for pg in range(8):
            for pi in range(8):
                p = pg * 8 + pi
                P.op("pe", lambda e, p=p, pi=pi: e.transpose(out=pTz[:, pi, :], in_=CVs[:, :, p], identity=ident[:, :]),
                     reads=["CV", "ident"], writes=["pTz"])
            P.op("act", lambda e, i=i, pg=pg: e.activation(out=Z[i][:, :, pg * 8:(pg + 1) * 8].rearrange("a c p -> a p c"), in_=pTz[:, :, :], func=AF.Copy),
                 reads=["pTz"], writes=["U%d" % i] + Zkeys(i))

    if stop <= 2:
        P.dma("pool", hy_oz[:, 0:4, :], H["xs"][0][:, 0, :].rearrange("a (c p) -> a c p", p=64), reads=["U0", "U1", "U2", "xs0"], writes=["dbg"])
        P.build()
        return nc
    for ch in range(16):
        zt, zk = zc[ch % 2], "stage%d" % (ch % 2)
        P.dma("sp", zt[:, :], dzemb[:, ch * 512:(ch + 1) * 512], writes=[zk])
        for layer in range(2):
            if layer == 0:
                P.op("pe", lambda e, zt=zt: e.matmul(pSS[0:64, :], lhsT=fw1[:, :], rhs=zt[:, :], start=True, stop=True), reads=["fw1", zk], writes=["pSS"])
            else:
                P.op("pe", lambda e: e.matmul(pSS[0:64, :], lhsT=fw2[:, :], rhs=h1[:, :], start=True, stop=True), reads=["fw2", "tmpc"], writes=["pSS"])
            fr, fb = (4, 5) if layer == 0 else (6, 7)
            P.op("dve", lambda e, fr=fr, fb=fb: e.tensor_scalar(out=arg[:, :], in0=pSS[0:64, :], scalar1=fpar[:, fr:fr + 1], scalar2=fpar[:, fb:fb + 1],
                                                                op0=ALU.mult, op1=ALU.add), reads=["pSS", "fpar"], writes=["tmpc"])
            P.op("dve", lambda e: e.tensor_copy(out=argi[:, :], in_=arg[:, :]), reads=["tmpc"], writes=["argi"])
            P.op("dve", lambda e: e.tensor_copy(out=h1[:, :], in_=argi[:, :]), reads=["argi", "tmpc"], writes=["tmpc"])
            P.op("dve", lambda e: e.tensor_tensor(out=arg[:, :], in0=arg[:, :], in1=h1[:, :], op=ALU.subtract), reads=["tmpc"], writes=["tmpc"])
            P.op("dve", lambda e: e.scalar_tensor_tensor(out=arg[:, :], in0=arg[:, :], scalar=0.5, in1=arg[:, :], op0=ALU.is_gt, op1=ALU.subtract),
                 reads=["tmpc"], writes=["tmpc"])
            if layer == 0:
                P.op("act", lambda e: e.activation(out=h1[:, :], in_=arg[:, :], func=AF.Sin, scale=-2.0 * math.pi), reads=["tmpc"], writes=["tmpc"])
            else:
                P.op("act", lambda e, ch=ch: e.activation(out=h2T[:, ch * 512:(ch + 1) * 512], in_=arg[:, :], func=AF.Sin, scale=-2.0 * math.pi),
                     reads=["tmpc"], writes=["h2T"])

    if stop <= 3:
        P.dma("pool", hy_oz[:, 0:4, :], H["xs"][0][:, 0, :].rearrange("a (c p) -> a c p", p=64), reads=["h2T", "xs0"], writes=["dbg"])
        P.build()
        return nc
    E3 = CV[:, :].rearrange("a (c p) -> a c p", p=64)
    O3 = Ob[:, :].rearrange("a (c p) -> a c p", p=64)
    h2s = h2T[:, :].rearrange("j (a p) -> j a p", p=64)
    cnt = [0]

    def twiddle(psrc, pkey, Tr_, Ti_, trk, tik, outr, outi, okey, view):
        g = cnt[0]
        cnt[0] += 1
        w1, w1k = W1[g % 2], "W1_%d" % (g % 2)
        w2, w2k = W2[g % 2], "W2_%d" % (g % 2)
        cp_, cpk = cpy[g % 2], "cpy%d" % (g % 2)
        P.op("act", lambda e: e.activation(out=cp_[:, :], in_=psrc[:, :], func=AF.Copy), reads=[pkey], writes=[cpk])
        P.op("dve", lambda e: e.tensor_tensor(out=w1[:, :], in0=psrc[:, :], in1=Tr_[:, :], op=ALU.mult), reads=[pkey, trk], writes=[w1k])
        P.op("pool", lambda e: e.tensor_tensor(out=w2[:, :], in0=cp_[:, :], in1=Ti_[:, :], op=ALU.mult), reads=[cpk, tik], writes=[w2k])
        w1r, w1i = view(w1)
        w2r, w2i = view(w2)
        P.op("dve", lambda e: e.tensor_tensor(out=outr, in0=w1r, in1=w2i, op=ALU.subtract), reads=[w1k, w2k], writes=[okey])
        P.op("pool", lambda e: e.tensor_tensor(out=outi, in0=w2r, in1=w1i, op=ALU.add), reads=[w1k, w2k], writes=[okey])

    v_fwd = lambda t: (t[:, 0:256], t[:, 256:512])
    v_inv = lambda t: (t[:, :].rearrange("k (c r x) -> k c r x", c=2, r=2)[:, :, 0, :], t[:, :].rearrange("k (c r x) -> k c r x", c=2, r=2)[:, :, 1, :])

    def fwd_stage1(src3, skey, c0):
        for q in range(NPQ):
            g = cnt[0]
            pa, pak = pA1[g % 2], "pA1_%d" % (g % 2)
            c = c0 + 2 * q
            P.op("pe", lambda e, c=c, pa=pa: e.matmul(pa[:, :], lhsT=src3[:, c:c + 2, :], rhs=F1b[:, :], start=True, stop=True),
                 reads=[skey, "F1b"], writes=[pak])
            twiddle(pa, pak, TrTr, TiTi, "TrTr", "TiTi", Ap[:, 0, q, :], Ap[:, 1, q, :], "Ap", v_fwd)

    G2r, G2in, G2i = R12b[:, 0:128], R12b[:, 128:256], R12b[:, 256:384]
    R1, R2 = R12b[:, 0:256], R12b[:, 256:512]

    for o in range(2):
        for p in range(64):
            P.op("pe", lambda e, p=p, o=o: e.matmul(pSS[:, 0:256], lhsT=h2s[:, :, p], rhs=fw3b[:, o * 256:(o + 1) * 256], start=True, stop=True),
                 reads=["h2T", "fw3b"], writes=["pSS"])
            P.op("act", lambda e, p=p: e.activation(out=win[:, :], in_=delta[:, :], func=AF.Exp, scale=Swin[:, p:p + 1]), reads=["delta", "Swin"], writes=["win"])
            for d in range(2):
                P.op("dve", lambda e, d=d: e.tensor_tensor(out=fwbw[:, d, :], in0=pSS[:, d * 128:(d + 1) * 128], in1=win[:, :], op=ALU.mult),
                     reads=["pSS", "win"], writes=["fwbw"])
            P.op("pool", lambda e, p=p: e.tensor_tensor(out=E3[:, :, p], in0=fwbw[:, 0, :], in1=fwbw[:, 1, :], op=ALU.add), reads=["fwbw"], writes=["CV"])
            P.op("pool", lambda e, p=p: e.tensor_tensor(out=O3[:, :, p], in0=fwbw[:, 0, :], in1=fwbw[:, 1, :], op=ALU.subtract), reads=["fwbw"], writes=["Ob"])
            if p == 0:
                P.op("pool", lambda e: e.tensor_copy(out=E3[0:1, :, 0], in_=fwbw[0:1, 0, :]), reads=["fwbw"], writes=["CV"])
                P.op("pool", lambda e: e.tensor_copy(out=O3[0:1, :, 0], in_=fwbw[0:1, 0, :]), reads=["fwbw"], writes=["Ob"])
        if stop <= 4:
            P.dma("pool", hy_oz[:, 0:4, :], H["xs"][0][:, 0, :].rearrange("a (c p) -> a c p", p=64), reads=["CV", "Ob", "xs0"], writes=["dbg"])
            P.build()
            return nc
        for sbt in range(128 // (2 * NPQ)):
            c0 = sbt * 2 * NPQ
            zk = "Z0_%d" % sbt
            for which, (src3, skey) in enumerate(((E3, "CV"), (O3, "Ob"))):
                fwd_stage1(src3, skey, c0)
                for qq in range(NPQ // 2):
                    rr = Ap[:, 0, 2 * qq:2 * qq + 2, :]
                    ri = Ap[:, 1, 2 * qq:2 * qq + 2, :]
                    if which == 0:
                        P.op("pe", lambda e, rr=rr: e.matmul(pXr[:, :], lhsT=G2r, rhs=rr, start=True, stop=False), reads=["R12b", "Ap"], writes=["pXr"])
                        P.op("pe", lambda e, ri=ri: e.matmul(pXr[:, :], lhsT=G2in, rhs=ri, start=False, stop=True), reads=["R12b", "Ap"], writes=["pXr"])
                        for j in range(2):
                            qg = sbt * NPQ + 2 * qq + j
                            P.op("act", lambda e, j=j, qq=qq, qg=qg, o=o: e.activation(out=Hs[:, 0, 2 * qq + j, :], in_=pXr[:, j * 256:(j + 1) * 256], func=AF.Identity,
                                                                                        bias=hbcol[:, o * 64 + qg:o * 64 + qg + 1], scale=1.0),
                                 reads=["pXr", "hbcol"], writes=["Hs"])
                    else:
                        P.op("pe", lambda e, rr=rr: e.matmul(pXi[:, :], lhsT=G2i, rhs=rr, start=True, stop=False), reads=["R12b", "Ap"], writes=["pXi"])
                        P.op("pe", lambda e, ri=ri: e.matmul(pXi[:, :], lhsT=G2r, rhs=ri, start=False, stop=True), reads=["R12b", "Ap"], writes=["pXi"])
                        P.op("act", lambda e, qq=qq: e.activation(out=Hs[:, 1, 2 * qq:2 * qq + 2, :], in_=pXi[:, :].rearrange("k (q x) -> k q x", q=2), func=AF.Copy),
                             reads=["pXi"], writes=["Hs"])
            if stop <= 6:
                P.dma("pool", hy_oz[:, 0:4, :], H["xs"][0][:, 0, :].rearrange("a (c p) -> a c p", p=64), reads=["CV", "Ob", "xs0", "Ap", "Hs", "Yb", "Bp", "pY"], writes=["dbg"])
                P.build()
                return nc
            fwd_stage1(Z[0], zk, c0)
            for qq in range(NPQ // 2):
                rr = Ap[:, 0, 2 * qq:2 * qq + 2, :]
                ri = Ap[:, 1, 2 * qq:2 * qq + 2, :]
                P.op("pe", lambda e, rr=rr: e.matmul(pXr[:, :], lhsT=G2r, rhs=rr, start=True, stop=False), reads=["R12b", "Ap"], writes=["pXr"])
                P.op("pe", lambda e, ri=ri: e.matmul(pXr[:, :], lhsT=G2in, rhs=ri, start=False, stop=True), reads=["R12b", "Ap"], writes=["pXr"])
                P.op("pe", lambda e, rr=rr: e.matmul(pXi[:, :], lhsT=G2i, rhs=rr, start=True, stop=False), reads=["R12b", "Ap"], writes=["pXi"])
                P.op("pe", lambda e, ri=ri: e.matmul(pXi[:, :], lhsT=G2r, rhs=ri, start=False, stop=True), reads=["R12b", "Ap"], writes=["pXi"])
                g = cnt[0]
                cnt[0] += 1
                w1, w1k = W1[g % 2], "W1_%d" % (g % 2)
                w2, w2k = W2[g % 2], "W2_%d" % (g % 2)
                cx, cxk = cpy[g % 2], "cpy%d" % (g % 2)
                hr = Hs[:, 0, 2 * qq:2 * qq + 2, :].rearrange("k q x -> k (q x)")
                hi = Hs[:, 1, 2 * qq:2 * qq + 2, :].rearrange("k q x -> k (q x)")
                yr = Yb[:, 0, 2 * qq:2 * qq + 2, :].rearrange("k q x -> k (q x)")
                yi = Yb[:, 1, 2 * qq:2 * qq + 2, :].rearrange("k q x -> k (q x)")
                P.op("act", lambda e, cx=cx: e.activation(out=cx[:, :], in_=pXi[:, :], func=AF.Copy), reads=["pXi"], writes=[cxk])
                P.op("act", lambda e: e.activation(out=cpy2[:, :], in_=pXr[:, :], func=AF.Copy), reads=["pXr"], writes=["cpy2"])
                P.op("dve", lambda e, w1=w1, hr=hr: e.tensor_tensor(out=w1[:, :], in0=pXr[:, :], in1=hr, op=ALU.mult), reads=["pXr", "Hs"], writes=[w1k])
                P.op("pool", lambda e, w2=w2, cx=cx, hi=hi: e.tensor_tensor(out=w2[:, :], in0=cx[:, :], in1=hi, op=ALU.mult), reads=[cxk, "Hs"], writes=[w2k])
                P.op("dve", lambda e, w1=w1, w2=w2, yr=yr: e.tensor_tensor(out=yr, in0=w1[:, :], in1=w2[:, :], op=ALU.subtract), reads=[w1k, w2k], writes=["Yb"])
                P.op("dve", lambda e, w1=w1, hr=hr: e.tensor_tensor(out=w1[:, :], in0=pXi[:, :], in1=hr, op=ALU.mult), reads=["pXi", "Hs", "Yb"], writes=[w1k])
                P.op("pool", lambda e, w2=w2, hi=hi: e.tensor_tensor(out=w2[:, :], in0=cpy2[:, :], in1=hi, op=ALU.mult), reads=["cpy2", "Hs", "Yb"], writes=[w2k])
                P.op("pool", lambda e, w1=w1, w2=w2, yi=yi: e.tensor_tensor(out=yi, in0=w1[:, :], in1=w2[:, :], op=ALU.add), reads=[w1k, w2k], writes=["Yb"])
            if stop <= 7:
                P.dma("pool", hy_oz[:, 0:4, :], H["xs"][0][:, 0, :].rearrange("a (c p) -> a c p", p=64), reads=["CV", "Ob", "xs0", "Ap", "Hs", "Yb", "Bp", "pY"], writes=["dbg"])
                P.build()
                return nc
            for q in range(NPQ):
                for kc in range(2):
                    P.op("pe", lambda e, q=q, kc=kc: e.matmul(pB[:, kc * 256:(kc + 1) * 256], lhsT=Yb[:, 0, q, kc * 128:(kc + 1) * 128], rhs=R1, start=True, stop=False),
                         reads=["Yb", "R12b"], writes=["pB"])
                    P.op("pe", lambda e, q=q, kc=kc: e.matmul(pB[:, kc * 256:(kc + 1) * 256], lhsT=Yb[:, 1, q, kc * 128:(kc + 1) * 128], rhs=R2, start=False, stop=True),
                         reads=["Yb", "R12b"], writes=["pB"])
                twiddle(pB, "pB", TT2r, TT2i, "TT2r", "TT2i", Bp[:, :, 0, q, :], Bp[:, :, 1, q, :], "Bp", v_inv)
            if stop <= 8:
                P.dma("pool", hy_oz[:, 0:4, :], H["xs"][0][:, 0, :].rearrange("a (c p) -> a c p", p=64), reads=["CV", "Ob", "xs0", "Ap", "Hs", "Yb", "Bp", "pY"], writes=["dbg"])
                P.build()
                return nc
            for hh in range(NPQ // 4):
                n = 0
                for kc in range(2):
                    for r in range(2):
                        rhs = Bp[:, kc, r, 4 * hh:4 * hh + 4, :]
                        lt = IF1b[:, (2 * kc + r) * 128:(2 * kc + r + 1) * 128]
                        P.op("pe", lambda e, rhs=rhs, lt=lt, n=n: e.matmul(pY[:, :], lhsT=lt, rhs=rhs, start=(n == 0), stop=(n == 3)),
                             reads=["Bp", "IF1b"], writes=["pY"])
                        n += 1
                cc = c0 + 8 * hh
                if o == 0:
                    P.op("dve", lambda e, cc=cc: e.scalar_tensor_tensor(out=Z[0][:, cc:cc + 8, :], in0=pY[:, :].rearrange("a (c p) -> a c p", p=64), scalar=1.0 / NFFT,
                                                                        in1=Z[1][:, cc:cc + 8, :], op0=ALU.mult, op1=ALU.mult),
                         reads=["pY", "U1"] + Zkeys(1), writes=[zk])
                else:
                    g = cnt[0]
                    cnt[0] += 1
                    OT, otk = ot[g % 2], "xs%d" % (g % 2)
                    P.op("dve", lambda e, cc=cc, OT=OT: e.scalar_tensor_tensor(out=OT[:, :].rearrange("a (c p) -> a c p", p=64), in0=pY[:, :].rearrange("a (c p) -> a c p", p=64),
                                                                               scalar=1.0 / NFFT, in1=Z[2][:, cc:cc + 8, :], op0=ALU.mult, op1=ALU.mult),
                         reads=["pY", "U2"] + Zkeys(2), writes=[otk])
                    P.dma("pool", hy_oz[:, cc:cc + 8, :], OT[:, :].rearrange("a (c p) -> a c p", p=64), reads=[otk], writes=["hy_%d" % cc])
    P.build()
    return nc


def run_hyena(inp, stop=99):
    f = lambda a: np.ascontiguousarray(a, dtype=np.float32)
    nc = build_hyena(stop)
    K = _hy_consts()
    w_in = inp["w_in"][0]
    in_maps = []
    for core in range(8):
        b, g = core // 4, core % 4
        cs = slice(128 * g, 128 * g + 128)
        hy_w = w_in[:, 1536:]
        w_inc = np.concatenate([hy_w[:, 0:512][:, cs], hy_w[:, 512:1024][:, cs], hy_w[:, 1024:1536][:, cs], np.zeros((1024, 128), np.float32)], axis=1)
        cwv = inp["conv_w"][0].reshape(3, 3, 512)[:, :, cs]
        cbv = inp["conv_b"][0].reshape(3, 512)[:, cs]
        w3 = inp["filt_w3"][0].reshape(64, 2, 2, 512)[:, :, :, cs].reshape(64, 512)
        hb = inp["hyena_bias"][0][:, cs]
        hbcol = np.zeros((128, 2, 64), np.float32)
        for cp in range(2):
            hbcol[64 * cp:64 * cp + 64, :, :] = hb[:, cp::2][None, :, :]
        in_maps.append({
            "xT": f(inp["x"][b].T), "cvec": f(inp["c"][b].reshape(8, 128).T),
            "w_ada1": f(inp["w_ada"][0][:, 0:2048]), "b_ada1": f(inp["b_ada"][0][0:2048].reshape(16, 128).T),
            "gpre": f(inp["g_pre_mix"][0].reshape(8, 128).T), "w_inc": f(w_inc),
            "F1cat": K["F1cat"], "TrTr": K["TrTr"], "TiTi": K["TiTi"], "R12": K["R12"], "TT2r": K["TT2r"], "TT2i": K["TT2i"],
            "IF1": K["IF1"], "Swin": K["Swin"], "zembT": K["zembT"],
            "delta": f(np.broadcast_to(K["deltas"][cs][None, :], (128, 128))),
            "hbcol": f(hbcol.reshape(128, 128)),
            "fw1": f(inp["filt_w1"][0]), "fw2": f(inp["filt_w2"][0]), "fw3c": f(w3),
            "fpar": f(np.stack([inp["filt_b1"][0], inp["filt_freq1"][0], inp["filt_b2"][0], inp["filt_freq2"][0]], axis=1)),
            "cw": f(cwv.transpose(2, 1, 0).reshape(128, 9)), "cb": f(cbv.T),
            "identd": np.eye(128, dtype=np.float32),
        })
    res = run_bass_kernel_spmd(nc, in_maps, core_ids=list(range(8)))
    hy = np.zeros((2, NTOK, 512), np.float32)
    for core in range(8):
        b, g = core // 4, core % 4
        oz = res.results[core]["hy_oz"]
        hy[b, :, 128 * g:128 * g + 128] = oz.transpose(0, 2, 1).reshape(NTOK, 128)
    return hy


NW = 4096
NOWN = 2048


class DramReg:
    def __init__(self, nc):
        self.nc = nc
        self.t = {}

    def __call__(self, name, shape, kind="ExternalInput", dt=None):
        if name not in self.t:
            self.t[name] = self.nc.dram_tensor(name, list(shape), dt or F32, kind=kind).ap()
        return self.t[name]


def _p1_common_f(P, D, ncols_w, wname, xname, ntok, ST=512, SW=256, cast_w=True):
    H = {}
    H["xT"] = D(xname, [1024, ntok])
    cvec = D("cvec", [128, 8])
    w_ada1 = D("w_ada1", [1024, 2048])
    b_ada1 = D("b_ada1", [128, 16])
    gpre = D("gpre", [128, 8])
    H["w_inc"] = D(wname[0], wname[1])
    sb, ps = P.sb, P.ps
    H["wb"] = sb("wb", [128, 8, ncols_w], BF16)
    H["stage"] = stage = [sb("stage%d" % i, [128, 8, SW], F32) for i in range(2)]
    H["ST"], H["SW"] = ST, SW
    csb = sb("csb", [128, 8], F32)
    gp = sb("gp", [128, 8], F32)
    bsb = sb("bsb", [128, 16], F32)
    H["modc"] = modc = sb("modc", [128, 16], F32)
    H["G0"] = G0 = sb("G0", [128, 8], F32)
    H["onesb"] = onesb = sb("onesb", [128, 128], BF16)
    H["xs"] = [sb("xs%d" % i, [128, 8, ST], F32) for i in range(2)]
    H["sqR"] = [sb("sq%d" % i, [128, 8, ST], BF16) for i in range(2)]
    H["hTR"] = [sb("hT%d" % i, [128, 8, ST], BF16) for i in range(2)]
    H["rbcR"] = [sb("rbc%d" % i, [128, ST], F32) for i in range(2)]
    H["sq"], H["hT"], H["rbc"] = H["sqR"][0], H["hTR"][0], H["rbcR"][0]
    H["pSS"] = pSS = ps("pSS", [128, 512], F32)
    P.op("pool", lambda e: e.memset(onesb[:, :], 1.0), writes=["onesb"])
    P.dma("sp", csb[:, :], cvec[:, :], writes=["csb"])
    P.dma("sp", gp[:, :], gpre[:, :], writes=["gp"])
    P.dma("sp", bsb[:, :], b_ada1[:, :], writes=["bsb"])
    P.op("act", lambda e: e.activation(out=csb[:, :], in_=csb[:, :], func=AF.Silu), reads=["csb"], writes=["csb"])
    wa = w_ada1.rearrange("(k p) n -> p k n", p=128)
    for blk in range(2048 // SW):
        st, skey = stage[blk % 2], "stage%d" % (blk % 2)
        P.dma("sp", st[:, :, :], wa[:, :, blk * SW:(blk + 1) * SW], writes=[skey])
        for jj in range(SW // 128):
            j = (SW // 128) * blk + jj
            for k in range(8):
                P.op("pe", lambda e, k=k, j=j, jj=jj, st=st: e.matmul(pSS[:, j:j + 1], lhsT=st[:, k, jj * 128:(jj + 1) * 128],
                                                                      rhs=csb[:, k:k + 1], start=(k == 0), stop=(k == 7)),
                     reads=[skey, "csb"], writes=["pSS"])
    P.op("dve", lambda e: e.tensor_tensor(out=modc[:, :], in0=pSS[:, 0:16], in1=bsb[:, :], op=ALU.add),
         reads=["pSS", "bsb"], writes=["modc"])
    P.op("dve", lambda e: e.scalar_tensor_tensor(out=G0[:, :], in0=modc[:, 8:16], scalar=1.0, in1=gp[:, :], op0=ALU.add, op1=ALU.mult),
         reads=["modc", "gp"], writes=["G0"])
    return H


def _p1_norm_r(P, H, st):
    ST = H["ST"]
    r = st % 2
    xs, xk = H["xs"][r], "xs%d" % r
    sq, hT, rbc = H["sqR"][r], H["hTR"][r], H["rbcR"][r]
    sqk, hk, rk = "sq%d" % r, "hT%d" % r, "rbc%d" % r
    pSS, onesb, modc, G0 = H["pSS"], H["onesb"], H["modc"], H["G0"]
    P.op("act", lambda e: e.activation(out=sq[:, :, :], in_=xs[:, :, :], func=AF.Square), reads=[xk], writes=[sqk])
    for k in range(8):
        P.op("pe", lambda e, k=k: e.matmul(pSS[:, 0:ST], lhsT=onesb[:, :], rhs=sq[:, k, :], start=(k == 0), stop=(k == 7)),
             reads=["onesb", sqk], writes=["pSS"])
    P.op("dve", lambda e: e.tensor_scalar(out=rbc[:, :], in0=pSS[:, 0:ST], scalar1=1.0 / 1024, scalar2=EPS, op0=ALU.mult, op1=ALU.add),
         reads=["pSS"], writes=[rk])
    P.op("act", lambda e: e.activation(out=rbc[:, :], in_=rbc[:, :], func=AF.Sqrt), reads=[rk], writes=[rk])
    P.op("dve", lambda e: e.reciprocal(out=rbc[:, :], in_=rbc[:, :]), reads=[rk], writes=[rk])
    for k in range(8):
        P.op("dve", lambda e, k=k: e.scalar_tensor_tensor(out=xs[:, k, :], in0=xs[:, k, :], scalar=G0[:, k:k + 1], in1=rbc[:, :],
                                                          op0=ALU.mult, op1=ALU.mult), reads=[xk, "G0", rk], writes=[xk])
        P.op("act", lambda e, k=k: e.activation(out=hT[:, k, :], in_=xs[:, k, :], func=AF.Identity, bias=modc[:, k:k + 1], scale=1.0),
             reads=[xk, "modc"], writes=[hk])
    return hT, hk


def _cast_w(P, H, wv, ncols, col0=0):
    SW, stage, wb = H["SW"], H["stage"], H["wb"]
    for blk in range(ncols // SW):
        st, skey = stage[blk % 2], "stage%d" % (blk % 2)
        P.dma("sp", st[:, :, :], wv[:, :, blk * SW:(blk + 1) * SW], writes=[skey])
        P.op("dve" if blk % 2 == 0 else "pool", lambda e, st=st, blk=blk: e.tensor_copy(
            out=wb[:, :, col0 + blk * SW:col0 + (blk + 1) * SW], in_=st[:, :, :]), reads=[skey], writes=["wb"])


def emit_attn_norm_f(nc, outer, D, hTw):
    P = Prog(nc, sem_stack=outer, prefix="A0_")
    H = _p1_common_f(P, D, 128, ("w_incA", [4, 1024, 768]), "xTw", NW)
    ST = 512
    for st in range(NW // ST):
        _p1_load(P, H, st)
        hT, hk = _p1_norm_r(P, H, st)
        P.op("pool", lambda e, st=st, hT=hT: e.tensor_copy(out=hTw[:, :, st * ST:(st + 1) * ST], in_=hT[:, :, :]), reads=[hk], writes=["hTw"])
    P.build()


def emit_attn_f(nc, outer, D, mixT, hTw):
    P = Prog(nc, sem_stack=outer, prefix="A_")
    ST = 512
    H = {"SW": 256, "w_inc": D("w_incA", [4, 1024, 768])}
    ropeC, ropeS = D("ropeCw", [128, NW]), D("ropeSw", [128, NW])
    maskd, hseld, validd, identd = D("maskd", [128, 17 * 128]), D("hseld", [128, 64]), D("validw", [128, 32]), D("identd", [128, 128])
    sb, ps = P.sb, P.ps
    H["wb"] = wb = sb("wb", [128, 8, 768], BF16)
    H["stage"] = [sb("stage%d" % i, [128, 8, 256], F32) for i in range(2)]
    H["pSS"] = ps("pSS", [128, 512], F32)
    QT, KT = sb("QT", [128, NW], BF16), sb("KT", [128, NW], BF16)
    Vaug = sb("Vaug", [128, 32, 2, 65], BF16)
    Mall = sb("Mall", [128, 17 * 128], BF16)
    mst = sb("mst", [128, 17 * 128], F32)
    hself, hsel = sb("hself", [128, 64], F32), sb("hsel", [128, 64], BF16)
    valid = sb("valid", [128, 32], F32)
    identf = sb("identf", [128, 128], F32)
    onesrow = sb("onesrow", [64, 128], BF16)
    rc = [sb("rc%d" % i, [128, ST], F32) for i in range(2)]
    rs_ = [sb("rs%d" % i, [128, ST], F32) for i in range(2)]
    t1, t2 = sb("t1", [128, ST], F32), sb("t2", [128, ST], F32)
    sqk = sb("sqk", [128, ST], BF16)
    kmx = sb("kmx", [64, 2], F32)
    qn = sb("qn", [64, 128], F32)
    negm = sb("negm", [64, 128], BF16)
    PT = [sb("PT%d" % i, [128, 512], BF16) for i in range(4)]
    ao = [sb("ao%d" % i, [128, 128], F32) for i in range(2)]
    rec = sb("rec", [128, 4], F32)
    pA, pB, pV = ps("pA", [128, 512], F32), ps("pB", [128, 512], F32), ps("pV", [128, 512], F32)
    pN = H["pSS"]
    pS = [ps("pS%d" % i, [128, 512], F32) for i in range(2)]
    pO = [ps("pO%d" % i, [128, 2, 128], F32) for i in range(2)]

    P.dma("sp", mst[:, :], maskd[:, :], writes=["mst"])
    P.op("pool", lambda e: e.tensor_copy(out=Mall[:, :], in_=mst[:, :]), reads=["mst"], writes=["Mall"])
    P.dma("sp", hself[:, :], hseld[:, :], writes=["hself"])
    P.op("pool", lambda e: e.tensor_copy(out=hsel[:, :], in_=hself[:, :]), reads=["hself"], writes=["hsel"])
    P.dma("sp", valid[:, :], validd[:, :], writes=["valid"])
    P.dma("sp", identf[:, :], identd[:, :], writes=["identf"])
    P.op("pool", lambda e: e.memset(onesrow[:, :], 1.0), writes=["onesrow"])
    for h in range(2):
        P.op("pool", lambda e, h=h: e.tensor_copy(out=Vaug[:, :, h, 64], in_=valid[:, :]), reads=["valid"], writes=["Vaug"])
    cnt = [0]
    xl = [0]
    wv_all = H["w_inc"]
    for hp in range(4):
        _cast_w(P, H, wv_all[hp].rearrange("(k p) n -> p k n", p=128), 768)
        P.op("pool", lambda e: e.memset(kmx[:, :], 0.0), reads=["kmx"], writes=["kmx"])
        nst = NW // ST
        for st in range(nst):
            hT = hTw[:, :, st * ST:(st + 1) * ST]
            C, ck = rc[st % 2], "rc%d" % (st % 2)
            S, sk = rs_[st % 2], "rs%d" % (st % 2)
            P.dma("sp", C[:, :], ropeC[:, st * ST:(st + 1) * ST], writes=[ck])
            P.dma("sp", S[:, :], ropeS[:, st * ST:(st + 1) * ST], writes=[sk])
            for which, dst in ((0, QT), (1, KT)):
                dk = "QT" if which == 0 else "KT"
                c0 = which * 256
                for k in range(8):
                    P.op("pe", lambda e, k=k, c0=c0, hT=hT: e.matmul(pA[:, :], lhsT=wb[:, k, c0:c0 + 128], rhs=hT[:, k, :], start=(k == 0), stop=(k == 7)),
                         reads=["wb", "hTw"], writes=["pA"])
                for k in range(8):
                    P.op("pe", lambda e, k=k, c0=c0, hT=hT: e.matmul(pB[:, :], lhsT=wb[:, k, c0 + 128:c0 + 256], rhs=hT[:, k, :], start=(k == 0), stop=(k == 7)),
                         reads=["wb", "hTw"], writes=["pB"])
                P.op("dve", lambda e, C=C: e.tensor_tensor(out=t1[:, :], in0=pA[:, :], in1=C[:, :], op=ALU.mult), reads=["pA", ck], writes=["t1"])
                P.op("dve", lambda e, S=S: e.tensor_tensor(out=t2[:, :], in0=pB[:, :], in1=S[:, :], op=ALU.mult), reads=["pB", sk], writes=["t2"])
                P.op("pool", lambda e, dst=dst, st=st: e.tensor_tensor(out=dst[:, st * ST:(st + 1) * ST], in0=t1[:, :], in1=t2[:, :], op=ALU.add),
                     reads=["t1", "t2"], writes=[dk])
            P.op("pool", lambda e, st=st: e.tensor_tensor(out=sqk[:, :], in0=KT[:, st * ST:(st + 1) * ST], in1=KT[:, st * ST:(st + 1) * ST], op=ALU.mult),
                 reads=["KT"], writes=["sqk"])
            P.op("pe", lambda e: e.matmul(pN[0:64, :], lhsT=hsel[:, :], rhs=sqk[:, :], start=True, stop=True), reads=["hsel", "sqk"], writes=["pSS"])
            P.op("dve", lambda e: e.tensor_reduce(out=kmx[:, 1:2], in_=pN[0:64, :], axis=AX.X, op=ALU.max), reads=["pSS", "kmx"], writes=["kmx"])
            P.op("dve", lambda e: e.tensor_tensor(out=kmx[:, 0:1], in0=kmx[:, 0:1], in1=kmx[:, 1:2], op=ALU.max), reads=["kmx"], writes=["kmx"])
            for tt in range(4):
                for k in range(8):
                    P.op("pe", lambda e, k=k, tt=tt, hT=hT: e.matmul(pV[:, tt * 128:(tt + 1) * 128], lhsT=hT[:, k, tt * 128:(tt + 1) * 128],
                                                              rhs=wb[:, k, 512:640], start=(k == 0), stop=(k == 7)),
                         reads=["wb", "hTw"], writes=["pV"])
            for tt in range(4):
                tile = st * 4 + tt
                P.op("act", lambda e, tt=tt, tile=tile: e.activation(out=Vaug[:, tile, :, 0:64], in_=pV[:, tt * 128:(tt + 1) * 128].rearrange("p (h d) -> p h d", h=2),
                                                                     func=AF.Copy, scale=valid[:, tile:tile + 1]),
                     reads=["pV", "valid"], writes=["Vaug"])
        P.op("act", lambda e: e.activation(out=kmx[:, 0:1], in_=kmx[:, 0:1], func=AF.Sqrt), reads=["kmx"], writes=["kmx"])
        LAG = 2
        units = []
        for jb in range(8, 24):
            for h in range(2):
                dms = list(range(-8, 9))
                groups = [dms[i:i + 4] for i in range(0, len(dms), 4)]
                base = 0
                for gi, grp in enumerate(groups):
                    units.append(dict(jb=jb, h=h, grp=grp, gi=gi, base=base, last=(gi == len(groups) - 1)))
                    base += len(grp)

        def emit_scores(u):
            jb, h, grp = u["jb"], u["h"], u["grp"]
            qs = slice(jb * 128, (jb + 1) * 128)
            hs = slice(64 * h, 64 * h + 64)
            if h == 0 and u["gi"] == 0:
                P.op("pool", lambda e, qs=qs: e.tensor_tensor(out=sqk[:, 0:128], in0=QT[:, qs], in1=QT[:, qs], op=ALU.mult), reads=["QT"], writes=["sqk"])
                P.op("pe", lambda e: e.matmul(pN[0:64, 0:128], lhsT=hsel[:, :], rhs=sqk[:, 0:128], start=True, stop=True), reads=["hsel", "sqk"], writes=["pSS"])
                P.op("act", lambda e: e.activation(out=qn[:, :], in_=pN[0:64, 0:128], func=AF.Sqrt), reads=["pSS"], writes=["qn"])
                P.op("dve", lambda e: e.tensor_scalar(out=negm[:, :], in0=qn[:, :], scalar1=kmx[:, 0:1], scalar2=-1.0, op0=ALU.mult, op1=ALU.mult),
                     reads=["qn", "kmx"], writes=["negm"])
            g = cnt[0]
            cnt[0] += 1
            psx, psk = [(pS[0], "pS0"), (pS[1], "pS1"), (pA, "pA"), (pB, "pB")][g % 4]
            ptx, ptk = PT[g % 4], "PT%d" % (g % 4)
            u["ptx"], u["ptk"] = ptx, ptk
            n = len(grp)
            for i, dm in enumerate(grp):
                kc = jb + dm
                P.op("pe", lambda e, i=i, kc=kc, hs=hs, qs=qs, psx=psx: e.matmul(psx[:, i * 128:(i + 1) * 128], lhsT=KT[hs, kc * 128:(kc + 1) * 128],
                                                                                  rhs=QT[hs, qs], start=True, stop=False),
                     reads=["KT", "QT"], writes=[psk])
                P.op("pe", lambda e, i=i, h=h, psx=psx: e.matmul(psx[:, i * 128:(i + 1) * 128], lhsT=onesrow[32 * h:32 * h + 1, :],
                                                                 rhs=negm[32 * h:32 * h + 1, :], start=False, stop=True),
                     reads=["onesrow", "negm"], writes=[psk])
            P.op("act", lambda e, psx=psx, ptx=ptx, n=n: e.activation(out=ptx[:, 0:n * 128], in_=psx[:, 0:n * 128], func=AF.Exp, scale=0.125),
                 reads=[psk], writes=[ptk])
            m0 = (grp[0] + 8) * 128
            P.op("dve" if g % 2 == 0 else "pool", lambda e, ptx=ptx, n=n, m0=m0: e.tensor_tensor(
                out=ptx[:, 0:n * 128], in0=ptx[:, 0:n * 128], in1=Mall[:, m0:m0 + n * 128], op=ALU.mult),
                reads=[ptk, "Mall"], writes=[ptk])

        def emit_pv(u):
            jb, h, grp = u["jb"], u["h"], u["grp"]
            ptx, ptk = u["ptx"], u["ptk"]
            AO, aok = ao[jb % 2], "ao%d" % (jb % 2)
            PO, pok = pO[jb % 2], "pO%d" % (jb % 2)
            for i, dm in enumerate(grp):
                kc = jb + dm
                nmm = u["base"] + i
                P.op("pe", lambda e, i=i, kc=kc, h=h, ptx=ptx, PO=PO, first=(nmm == 0), last=(nmm == 16): e.matmul(
                    PO[:, h, 0:65], lhsT=ptx[:, i * 128:(i + 1) * 128], rhs=Vaug[:, kc, h, :], start=first, stop=last),
                    reads=[ptk, "Vaug"], writes=[pok])
            if u["last"]:
                P.op("dve", lambda e, h=h, PO=PO: e.reciprocal(out=rec[:, h:h + 1], in_=PO[:, h, 64:65]), reads=[pok], writes=["rec%d" % h])
                P.op("dve", lambda e, h=h, PO=PO, AO=AO: e.tensor_scalar(out=AO[:, 64 * h:64 * h + 64], in0=PO[:, h, 0:64], scalar1=rec[:, h:h + 1],
                                                                         scalar2=None, op0=ALU.mult),
                     reads=[pok, "rec%d" % h], writes=[aok])
                if h == 1:
                    P.op("pe", lambda e, AO=AO: e.transpose(out=pV[:, 0:128], in_=AO[:, :], identity=identf[:, :]), reads=[aok, "identf"], writes=["pV"])
                    P.op("act", lambda e, hp=hp, jb=jb: e.activation(out=mixT[:, hp, (jb - 8) * 128:(jb - 7) * 128], in_=pV[:, 0:128], func=AF.Copy),
                         reads=["pV"], writes=["mixT"])

        pending = []
        for u in units:
            emit_scores(u)
            pending.append(u)
            if len(pending) > LAG:
                emit_pv(pending.pop(0))
        while pending:
            emit_pv(pending.pop(0))
    P.build()


def emit_hyproj_f(nc, outer, D):
    P = Prog(nc, sem_stack=outer, prefix="B_")
    ST = 256
    H = _p1_common_f(P, D, 1536, ("w_incH", [1024, 1536]), "xT", NTOK, ST=ST, SW=128)
    _cast_w(P, H, H["w_inc"].rearrange("(k p) n -> p k n", p=128), 1536)
    Us = D("Us", [12, 128, NTOK], kind="Internal", dt=BF16)
    sb, ps = P.sb, P.ps
    wb, hT = H["wb"], H["hT"]
    ub = [sb("ub%d" % i, [128, 12, ST], BF16) for i in range(2)]
    pU = [ps("pU%d" % i, [128, 512], F32) for i in range(4)]
    nst = NTOK // ST
    _p1_load(P, H, 0)
    nxt = _p1_norm_r(P, H, 0)
    for st in range(nst):
        hT, hk = nxt
        if st + 1 < nst:
            _p1_load(P, H, st + 1)
            nxt = _p1_norm_r(P, H, st + 1)
        UB, ubk = ub[st % 2], "ub%d" % (st % 2)
        for pr in range(6):
            pu, puk = pU[pr % 4], "pU%d" % (pr % 4)
            for half in range(2):
                i = 2 * pr + half
                for k in range(8):
                    P.op("pe", lambda e, k=k, i=i, half=half, pu=pu, hT=hT: e.matmul(pu[:, half * ST:(half + 1) * ST], lhsT=wb[:, k, i * 128:(i + 1) * 128], rhs=hT[:, k, :],
                                                                                    start=(k == 0), stop=(k == 7)), reads=["wb", hk], writes=[puk])
            eng = "act" if pr % 2 == 0 else "dve"
            if eng == "act":
                P.op("act", lambda e, pr=pr, pu=pu, UB=UB: e.activation(out=UB[:, 2 * pr:2 * pr + 2, :], in_=pu[:, :].rearrange("p (i t) -> p i t", i=2), func=AF.Copy),
                     reads=[puk], writes=[ubk])
            else:
                P.op("dve", lambda e, pr=pr, pu=pu, UB=UB: e.tensor_copy(out=UB[:, 2 * pr:2 * pr + 2, :], in_=pu[:, :].rearrange("p (i t) -> p i t", i=2)),
                     reads=[puk], writes=[ubk])
        P.dma("pool", Us[:, :, st * ST:(st + 1) * ST].rearrange("i p t -> p i t"), UB[:, :, :], reads=[ubk], writes=["Us"])
    P.build()


def emit_hyena_f(nc, outer, D, mixT, groups=(0, 1, 2, 3), prefix="C_"):
    P = Prog(nc, sem_stack=outer, prefix=prefix)
    sb, ps = P.sb, P.ps
    Us = D("Us", [12, 128, NTOK], kind="Internal", dt=BF16)
    dF1cat, dTrTr, dTiTi, dR12 = D("F1cat", [128, 512]), D("TrTr", [128, 512]), D("TiTi", [128, 512]), D("R12", [128, 512])
    dTT2r, dTT2i, dIF1, dSwin = D("TT2r", [128, 512]), D("TT2i", [128, 512]), D("IF1", [128, 512]), D("Swin", [128, 64])
    dzemb = D("zembT", [33, NTOK])
    ddelta = D("delta4", [128, 512])
    dhbcol = D("hbcol4", [128, 512])
    dfw1, dfw2, dfw3, dfpar = D("fw1", [33, 64]), D("fw2", [64, 64]), D("fw3c4", [64, 2048]), D("fpar", [64, 4])
    dcw, dcb = D("cw4", [128, 36]), D("cb4", [128, 12])
    dsel = D("seld", [128, 32])
    didn = D("identd", [128, 128])

    U = [sb("U%d" % i, [128, NTOK + 2], BF16) for i in range(3)]
    CV = sb("CV", [128, NTOK], BF16)
    Ob = sb("Ob", [128, NTOK], BF16)
    h2T = sb("h2T", [64, NTOK], BF16)
    ApF = sb("ApF", [128, 2, NPQ, 256], BF16)
    ApD = sb("ApD", [128, 2, NPQ, 256], BF16)
    HsL = [sb("Hs%d" % i, [128, 2, NPQ, 256], BF16) for i in range(2)]
    Yb = sb("Yb", [128, 2, NPQ, 256], BF16)
    Bp = sb("Bp", [128, 2, 2, NPQ, 128], BF16)
    W1 = [sb("W1_%d" % i, [128, 512], F32) for i in range(2)]
    W2 = [sb("W2_%d" % i, [128, 512], F32) for i in range(2)]
    cpy = [sb("cpy%d" % i, [128, 512], F32) for i in range(2)]
    cpy2 = sb("cpy2", [128, 512], F32)
    tmpc = sb("tmpc", [128, 1024], F32)
    arg = tmpc[0:64, 0:512]
    h1 = tmpc[0:64, 512:1024]
    argi = sb("argi", [64, 512], mybir.dt.int32)
    F1b, R12b, IF1b = sb("F1b", [128, 512], BF16), sb("R12b", [128, 512], BF16), sb("IF1b", [128, 512], BF16)
    TrTr, TiTi = sb("TrTr", [128, 512], F32), sb("TiTi", [128, 512], F32)
    TT2r, TT2i = sb("TT2r", [128, 512], F32), sb("TT2i", [128, 512], F32)
    Swin = sb("Swin", [128, 64], F32)
    delta = sb("delta", [128, 512], F32)
    hbcol = sb("hbcol", [128, 512], F32)
    fw1, fw2 = sb("fw1", [33, 64], F32), sb("fw2", [64, 64], F32)
    fw3b = sb("fw3b", [64, 2048], BF16)
    fpar = sb("fpar", [64, 8], F32)
    cw, cb = sb("cw", [128, 36], F32), sb("cb", [128, 12], F32)
    selb = sb("selb", [128, 32], BF16)
    ident = sb("ident", [128, 128], BF16)
    zc = [sb("zc%d" % i, [33, 512], F32) for i in range(2)]
    win = [sb("win%d" % i, [128, 128], F32) for i in range(2)]
    eoc = [sb("eoc%d" % i, [128, 256], F32) for i in range(2)]
    fw3f = sb("fw3f", [64, 1024], BF16)

    pSS = ps("pSS", [128, 512], F32)
    pA1 = [ps("pA1_%d" % i, [128, 512], F32) for i in range(2)]
    pXr, pXi = ps("pXr", [128, 512], F32), ps("pXi", [128, 512], F32)
    pB = ps("pB", [128, 512], F32)
    pY = ps("pY", [128, 512], F32)
    pTz = ps("pTz", [128, 8, 128], BF16)

    def ldcast(dst, dkey, src, n=512, parts=128):
        P.dma("sp", cpy2[0:parts, 0:n], src, writes=["cpy2"])
        P.op("dve", lambda e: e.tensor_copy(out=dst, in_=cpy2[0:parts, 0:n]), reads=["cpy2"], writes=[dkey])
    ldcast(F1b[:, :], "F1b", dF1cat[:, :])
    ldcast(R12b[:, :], "R12b", dR12[:, :])
    ldcast(IF1b[:, :], "IF1b", dIF1[:, :])
    ldcast(ident[:, :], "ident", didn[:, :], n=128)
    ldcast(selb[:, :], "selb", dsel[:, :], n=32)
    for q4 in range(4):
        P.dma("sp", cpy2[0:64, :], dfw3[:, q4 * 512:(q4 + 1) * 512], writes=["cpy2"])
        for o_ in range(2):
            wf = cpy2[0:64, o_ * 256:o_ * 256 + 128]
            wbk = cpy2[0:64, o_ * 256 + 128:o_ * 256 + 256]
            c0_ = q4 * 512 + o_ * 256
            P.op("dve", lambda e, wf=wf, wbk=wbk, c0_=c0_: e.tensor_tensor(out=fw3b[:, c0_:c0_ + 128], in0=wf, in1=wbk, op=ALU.add), reads=["cpy2"], writes=["fw3b"])
            P.op("dve", lambda e, wf=wf, wbk=wbk, c0_=c0_: e.tensor_tensor(out=fw3b[:, c0_ + 128:c0_ + 256], in0=wf, in1=wbk, op=ALU.subtract), reads=["cpy2"], writes=["fw3b"])
            P.op("dve", lambda e, wf=wf, q4=q4, o_=o_: e.tensor_copy(out=fw3f[:, (2 * q4 + o_) * 128:(2 * q4 + o_ + 1) * 128], in_=wf), reads=["cpy2"], writes=["fw3f"])
    for dst, key, src in ((TrTr, "TrTr", dTrTr), (TiTi, "TiTi", dTiTi), (TT2r, "TT2r", dTT2r), (TT2i, "TT2i", dTT2i), (Swin, "Swin", dSwin),
                          (delta, "delta", ddelta), (hbcol, "hbcol", dhbcol), (fw1, "fw1", dfw1), (fw2, "fw2", dfw2), (cw, "cw", dcw), (cb, "cb", dcb)):
        P.dma("sp", dst[:, :], src[:, :], writes=[key])
    P.dma("sp", fpar[:, 0:4], dfpar[:, :], writes=["fpar"])
    i2p = 1.0 / (2.0 * math.pi)
    for (bc, fc, o0) in ((0, 1, 4), (2, 3, 6)):
        P.op("dve", lambda e, bc=bc, fc=fc, o0=o0: e.tensor_tensor(out=fpar[:, o0 + 1:o0 + 2], in0=fpar[:, bc:bc + 1], in1=fpar[:, fc:fc + 1], op=ALU.mult),
             reads=["fpar"], writes=["fpar"])
        P.op("dve", lambda e, o0=o0: e.tensor_scalar(out=fpar[:, o0 + 1:o0 + 2], in0=fpar[:, o0 + 1:o0 + 2], scalar1=i2p, scalar2=16.0, op0=ALU.mult, op1=ALU.add),
             reads=["fpar"], writes=["fpar"])
        P.op("dve", lambda e, fc=fc, o0=o0: e.tensor_scalar(out=fpar[:, o0:o0 + 1], in0=fpar[:, fc:fc + 1], scalar1=i2p, scalar2=None, op0=ALU.mult),
             reads=["fpar"], writes=["fpar"])

    for ch in range(16):
        zt, zk = zc[ch % 2], "zc%d" % (ch % 2)
        P.dma("sp", zt[:, :], dzemb[:, ch * 512:(ch + 1) * 512], writes=[zk])
        for layer in range(2):
            if layer == 0:
                P.op("pe", lambda e, zt=zt: e.matmul(pSS[0:64, :], lhsT=fw1[:, :], rhs=zt[:, :], start=True, stop=True), reads=["fw1", zk], writes=["pSS"])
            else:
                P.op("pe", lambda e: e.matmul(pSS[0:64, :], lhsT=fw2[:, :], rhs=h1[:, :], start=True, stop=True), reads=["fw2", "tmpc"], writes=["pSS"])
            fr, fb = (4, 5) if layer == 0 else (6, 7)
            P.op("dve", lambda e, fr=fr, fb=fb: e.tensor_scalar(out=arg[:, :], in0=pSS[0:64, :], scalar1=fpar[:, fr:fr + 1], scalar2=fpar[:, fb:fb + 1],
                                                                op0=ALU.mult, op1=ALU.add), reads=["pSS", "fpar"], writes=["tmpc"])
            P.op("dve", lambda e: e.tensor_copy(out=argi[:, :], in_=arg[:, :]), reads=["tmpc"], writes=["argi"])
            P.op("dve", lambda e: e.tensor_copy(out=h1[:, :], in_=argi[:, :]), reads=["argi", "tmpc"], writes=["tmpc"])
            P.op("dve", lambda e: e.tensor_tensor(out=arg[:, :], in0=arg[:, :], in1=h1[:, :], op=ALU.subtract), reads=["tmpc"], writes=["tmpc"])
            P.op("dve", lambda e: e.scalar_tensor_tensor(out=arg[:, :], in0=arg[:, :], scalar=0.5, in1=arg[:, :], op0=ALU.is_gt, op1=ALU.subtract),
                 reads=["tmpc"], writes=["tmpc"])
            if layer == 0:
                P.op("act", lambda e: e.activation(out=h1[:, :], in_=arg[:, :], func=AF.Sin, scale=-2.0 * math.pi), reads=["tmpc"], writes=["tmpc"])
            else:
                P.op("act", lambda e, ch=ch: e.activation(out=h2T[:, ch * 512:(ch + 1) * 512], in_=arg[:, :], func=AF.Sin, scale=-2.0 * math.pi),
                     reads=["tmpc"], writes=["h2T"])

    Zkeys = lambda i: ["Z%d_%d" % (i, s) for s in range(128 // (2 * NPQ))]
    Z = [U[i][:, 0:NTOK].rearrange("a (c p) -> a c p", p=64) for i in range(3)]
    CVs = CV[:, :].rearrange("c (a p) -> c a p", p=64)
    E3 = CV[:, :].rearrange("a (c p) -> a c p", p=64)
    O3 = Ob[:, :].rearrange("a (c p) -> a c p", p=64)
    h2s = h2T[:, :].rearrange("j (a p) -> j a p", p=64)
    cnt = [0]

    def twiddle(psrc, pkey, Tr_, Ti_, trk, tik, outr, outi, okey, view):
        g = cnt[0]
        cnt[0] += 1
        w1, w1k = W1[g % 2], "W1_%d" % (g % 2)
        w2, w2k = W2[g % 2], "W2_%d" % (g % 2)
        cp_, cpk = cpy[g % 2], "cpy%d" % (g % 2)
        P.op("act", lambda e: e.activation(out=cp_[:, :], in_=psrc[:, :], func=AF.Copy), reads=[pkey], writes=[cpk])
        P.op("dve", lambda e: e.tensor_tensor(out=w1[:, :], in0=psrc[:, :], in1=Tr_[:, :], op=ALU.mult), reads=[pkey, trk], writes=[w1k])
        P.op("pool", lambda e: e.tensor_tensor(out=w2[:, :], in0=cp_[:, :], in1=Ti_[:, :], op=ALU.mult), reads=[cpk, tik], writes=[w2k])
        w1r, w1i = view(w1)
        w2r, w2i = view(w2)
        P.op("dve", lambda e: e.tensor_tensor(out=outr, in0=w1r, in1=w2i, op=ALU.subtract), reads=[w1k, w2k], writes=[okey])
        P.op("dve", lambda e: e.tensor_tensor(out=outi, in0=w2r, in1=w1i, op=ALU.add), reads=[w1k, w2k], writes=[okey])

    v_fwd = lambda t: (t[:, 0:256], t[:, 256:512])
    v_inv = lambda t: (t[:, :].rearrange("k (c r x) -> k c r x", c=2, r=2)[:, :, 0, :], t[:, :].rearrange("k (c r x) -> k c r x", c=2, r=2)[:, :, 1, :])

    def fwd_stage1(src3, skey, c0, Ap, apk):
        for q in range(NPQ):
            g = cnt[0]
            pa, pak = pA1[g % 2], "pA1_%d" % (g % 2)
            c = c0 + 2 * q
            P.op("pe", lambda e, c=c, pa=pa: e.matmul(pa[:, :], lhsT=src3[:, c:c + 2, :], rhs=F1b[:, :], start=True, stop=True),
                 reads=[skey, "F1b"], writes=[pak])
            twiddle(pa, pak, TrTr, TiTi, "TrTr", "TiTi", Ap[:, 0, q, :], Ap[:, 1, q, :], apk, v_fwd)

    G2r, G2in, G2i = R12b[:, 0:128], R12b[:, 128:256], R12b[:, 256:384]
    R1, R2 = R12b[:, 0:256], R12b[:, 256:512]

    for grp4 in groups:
        for i in range(3):
            P.dma("sp", U[i][:, 1:NTOK + 1], Us[3 * grp4 + i], reads=["Us"], writes=["U%d" % i] + Zkeys(i), key="U%d" % i)
            P.op("pool", lambda e, i=i: e.memset(U[i][:, 0:1], 0.0), reads=["U%d" % i], writes=["U%d" % i] + Zkeys(i))
            P.op("pool", lambda e, i=i: e.memset(U[i][:, NTOK + 1:NTOK + 2], 0.0), reads=["U%d" % i], writes=["U%d" % i])
        for i in range(3):
            ci = 9 * grp4 + 3 * i
            bi = 3 * grp4 + i
            for ch in range(8):
                j0 = ch * 1024
                P.op("dve", lambda e, i=i, j0=j0, ci=ci, bi=bi: e.tensor_scalar(out=tmpc[:, :], in0=U[i][:, j0:j0 + 1024], scalar1=cw[:, ci:ci + 1],
                                                                                scalar2=cb[:, bi:bi + 1], op0=ALU.mult, op1=ALU.add),
                     reads=["U%d" % i, "cw", "cb"], writes=["tmpc"])
                P.op("dve", lambda e, i=i, j0=j0, ci=ci: e.scalar_tensor_tensor(out=tmpc[:, :], in0=U[i][:, j0 + 1:j0 + 1025], scalar=cw[:, ci + 1:ci + 2],
                                                                                in1=tmpc[:, :], op0=ALU.mult, op1=ALU.add),
                     reads=["U%d" % i, "cw", "tmpc"], writes=["tmpc"])
                P.op("dve", lambda e, i=i, j0=j0, ci=ci: e.scalar_tensor_tensor(out=CV[:, j0:j0 + 1024], in0=U[i][:, j0 + 2:j0 + 1026], scalar=cw[:, ci + 2:ci + 3],
                                                                                in1=tmpc[:, :], op0=ALU.mult, op1=ALU.add),
                     reads=["U%d" % i, "cw", "tmpc"], writes=["CV"])
            for pg in range(8):
                for pi in range(8):
                    p = pg * 8 + pi
                    P.op("pe", lambda e, p=p, pi=pi: e.transpose(out=pTz[:, pi, :], in_=CVs[:, :, p], identity=ident[:, :]),
                         reads=["CV", "ident"], writes=["pTz"])
                P.op("act", lambda e, i=i, pg=pg: e.activation(out=Z[i][:, :, pg * 8:(pg + 1) * 8].rearrange("a c p -> a p c"), in_=pTz[:, :, :], func=AF.Copy),
                     reads=["pTz"], writes=["U%d" % i] + Zkeys(i))
        for o in range(2):
            w3c0 = grp4 * 512 + o * 256
            for p in range(64):
                pe_, pek = (pSS, "pSS") if p % 2 == 0 else (pB, "pB")
                wn, wnk = win[p % 2], "win%d" % (p % 2)
                ec, eck = eoc[p % 2], "eoc%d" % (p % 2)
                P.op("pe", lambda e, p=p, w3c0=w3c0, pe_=pe_: e.matmul(pe_[:, 0:256], lhsT=h2s[:, :, p], rhs=fw3b[:, w3c0:w3c0 + 256], start=True, stop=True),
                     reads=["h2T", "fw3b"], writes=[pek])
                P.op("act", lambda e, p=p, grp4=grp4, wn=wn: e.activation(out=wn[:, :], in_=delta[:, grp4 * 128:(grp4 + 1) * 128], func=AF.Exp, scale=Swin[:, p:p + 1]),
                     reads=["delta", "Swin"], writes=[wnk])
                P.op("act", lambda e, pe_=pe_, ec=ec: e.activation(out=ec[:, :], in_=pe_[:, 0:256], func=AF.Copy), reads=[pek], writes=[eck])
                P.op("pool", lambda e, p=p, ec=ec, wn=wn: e.tensor_tensor(out=E3[:, :, p], in0=ec[:, 0:128], in1=wn[:, :], op=ALU.mult), reads=[eck, wnk], writes=["CV"])
                P.op("pool", lambda e, p=p, ec=ec, wn=wn: e.tensor_tensor(out=O3[:, :, p], in0=ec[:, 128:256], in1=wn[:, :], op=ALU.mult), reads=[eck, wnk], writes=["Ob"])
                if p == 0:
                    fc = (2 * grp4 + o) * 128
                    P.op("pe", lambda e, fc=fc: e.matmul(pY[0:1, 0:128], lhsT=h2T[:, 0:1], rhs=fw3f[:, fc:fc + 128], start=True, stop=True),
                         reads=["h2T", "fw3f"], writes=["pY"])
                    P.op("dve", lambda e, wn=wn: e.tensor_tensor(out=E3[0:1, :, 0], in0=pY[0:1, 0:128], in1=wn[0:1, :], op=ALU.mult), reads=["pY", wnk], writes=["CV"])
                    P.op("dve", lambda e, wn=wn: e.tensor_tensor(out=O3[0:1, :, 0], in0=pY[0:1, 0:128], in1=wn[0:1, :], op=ALU.mult), reads=["pY", wnk], writes=["Ob"])
            nsbt = 128 // (2 * NPQ)

            def filt_gen(sbt):
                c0 = sbt * 2 * NPQ
                Hs, hk = HsL[sbt % 2], "Hs%d" % (sbt % 2)
                for which, (src3, skey) in enumerate(((E3, "CV"), (O3, "Ob"))):
                    fwd_stage1(src3, skey, c0, ApF, "ApF")
                    yield
                    for qq in range(NPQ // 2):
                        rr = ApF[:, 0, 2 * qq:2 * qq + 2, :]
                        ri = ApF[:, 1, 2 * qq:2 * qq + 2, :]
                        if which == 0:
                            P.op("pe", lambda e, rr=rr: e.matmul(pSS[:, :], lhsT=G2r, rhs=rr, start=True, stop=False), reads=["R12b", "ApF"], writes=["pSS"])
                            P.op("pe", lambda e, ri=ri: e.matmul(pSS[:, :], lhsT=G2in, rhs=ri, start=False, stop=True), reads=["R12b", "ApF"], writes=["pSS"])
                            for j in range(2):
                                hcol = grp4 * 128 + o * 64 + sbt * NPQ + 2 * qq + j
                                P.op("act", lambda e, j=j, qq=qq, hcol=hcol, Hs=Hs: e.activation(out=Hs[:, 0, 2 * qq + j, :], in_=pSS[:, j * 256:(j + 1) * 256], func=AF.Identity,
                                                                                                 bias=hbcol[:, hcol:hcol + 1], scale=1.0),
                                     reads=["pSS", "hbcol"], writes=[hk])
                        else:
                            P.op("pe", lambda e, rr=rr: e.matmul(pSS[:, :], lhsT=G2i, rhs=rr, start=True, stop=False), reads=["R12b", "ApF"], writes=["pSS"])
                            P.op("pe", lambda e, ri=ri: e.matmul(pSS[:, :], lhsT=G2r, rhs=ri, start=False, stop=True), reads=["R12b", "ApF"], writes=["pSS"])
                            P.op("act", lambda e, qq=qq, Hs=Hs: e.activation(out=Hs[:, 1, 2 * qq:2 * qq + 2, :], in_=pSS[:, :].rearrange("k (q x) -> k q x", q=2), func=AF.Copy),
                                 reads=["pSS"], writes=[hk])
                    yield

            def data_gen(sbt):
                c0 = sbt * 2 * NPQ
                zk = "Z0_%d" % sbt
                Hs, hk = HsL[sbt % 2], "Hs%d" % (sbt % 2)
                fwd_stage1(Z[0], zk, c0, ApD, "ApD")
                yield
                for qq in range(NPQ // 2):
                    rr = ApD[:, 0, 2 * qq:2 * qq + 2, :]
                    ri = ApD[:, 1, 2 * qq:2 * qq + 2, :]
                    P.op("pe", lambda e, rr=rr: e.matmul(pXr[:, :], lhsT=G2r, rhs=rr, start=True, stop=False), reads=["R12b", "ApD"], writes=["pXr"])
                    P.op("pe", lambda e, ri=ri: e.matmul(pXr[:, :], lhsT=G2in, rhs=ri, start=False, stop=True), reads=["R12b", "ApD"], writes=["pXr"])
                    P.op("pe", lambda e, rr=rr: e.matmul(pXi[:, :], lhsT=G2i, rhs=rr, start=True, stop=False), reads=["R12b", "ApD"], writes=["pXi"])
                    P.op("pe", lambda e, ri=ri: e.matmul(pXi[:, :], lhsT=G2r, rhs=ri, start=False, stop=True), reads=["R12b", "ApD"], writes=["pXi"])
                    g = cnt[0]
                    cnt[0] += 1
                    w1, w1k = W1[g % 2], "W1_%d" % (g % 2)
                    w2, w2k = W2[g % 2], "W2_%d" % (g % 2)
                    cx, cxk = cpy[g % 2], "cpy%d" % (g % 2)
                    hr = Hs[:, 0, 2 * qq:2 * qq + 2, :].rearrange("k q x -> k (q x)")
                    hi = Hs[:, 1, 2 * qq:2 * qq + 2, :].rearrange("k q x -> k (q x)")
                    yr = Yb[:, 0, 2 * qq:2 * qq + 2, :].rearrange("k q x -> k (q x)")
                    yi = Yb[:, 1, 2 * qq:2 * qq + 2, :].rearrange("k q x -> k (q x)")
                    P.op("act", lambda e, cx=cx: e.activation(out=cx[:, :], in_=pXi[:, :], func=AF.Copy), reads=["pXi"], writes=[cxk])
                    P.op("act", lambda e: e.activation(out=cpy2[:, :], in_=pXr[:, :], func=AF.Copy), reads=["pXr"], writes=["cpy2"])
                    P.op("dve", lambda e, w1=w1, hr=hr: e.tensor_tensor(out=w1[:, :], in0=pXr[:, :], in1=hr, op=ALU.mult), reads=["pXr", hk], writes=[w1k])
                    P.op("pool", lambda e, w2=w2, cx=cx, hi=hi: e.tensor_tensor(out=w2[:, :], in0=cx[:, :], in1=hi, op=ALU.mult), reads=[cxk, hk], writes=[w2k])
                    P.op("dve", lambda e, w1=w1, w2=w2, yr=yr: e.tensor_tensor(out=yr, in0=w1[:, :], in1=w2[:, :], op=ALU.subtract), reads=[w1k, w2k], writes=["Yb"])
                    P.op("dve", lambda e, w1=w1, hr=hr: e.tensor_tensor(out=w1[:, :], in0=pXi[:, :], in1=hr, op=ALU.mult), reads=["pXi", hk, "Yb"], writes=[w1k])
                    P.op("pool", lambda e, w2=w2, hi=hi: e.tensor_tensor(out=w2[:, :], in0=cpy2[:, :], in1=hi, op=ALU.mult), reads=["cpy2", hk, "Yb"], writes=[w2k])
                    P.op("dve", lambda e, w1=w1, w2=w2, yi=yi: e.tensor_tensor(out=yi, in0=w1[:, :], in1=w2[:, :], op=ALU.add), reads=[w1k, w2k], writes=["Yb"])
                yield
                for q in range(NPQ):
                    for kc in range(2):
                        P.op("pe", lambda e, q=q, kc=kc: e.matmul(pB[:, kc * 256:(kc + 1) * 256], lhsT=Yb[:, 0, q, kc * 128:(kc + 1) * 128], rhs=R1, start=True, stop=False),
                             reads=["Yb", "R12b"], writes=["pB"])
                        P.op("pe", lambda e, q=q, kc=kc: e.matmul(pB[:, kc * 256:(kc + 1) * 256], lhsT=Yb[:, 1, q, kc * 128:(kc + 1) * 128], rhs=R2, start=False, stop=True),
                             reads=["Yb", "R12b"], writes=["pB"])
                    twiddle(pB, "pB", TT2r, TT2i, "TT2r", "TT2i", Bp[:, :, 0, q, :], Bp[:, :, 1, q, :], "Bp", v_inv)
                yield
                for hh in range(NPQ // 4):
                    n = 0
                    for kc in range(2):
                        for r in range(2):
                            rhs = Bp[:, kc, r, 4 * hh:4 * hh + 4, :]
                            lt = IF1b[:, (2 * kc + r) * 128:(2 * kc + r + 1) * 128]
                            P.op("pe", lambda e, rhs=rhs, lt=lt, n=n: e.matmul(pY[:, :], lhsT=lt, rhs=rhs, start=(n == 0), stop=(n == 3)),
                                 reads=["Bp", "IF1b"], writes=["pY"])
                            n += 1
                    cc = c0 + 8 * hh
                    zi = 1 if o == 0 else 2
                    zo = 0 if o == 0 else 2
                    okeys = [zk] if o == 0 else ["U2", "Z2_%d" % sbt]
                    P.op("dve", lambda e, cc=cc, zi=zi, zo=zo: e.scalar_tensor_tensor(out=Z[zo][:, cc:cc + 8, :], in0=pY[:, :].rearrange("a (c p) -> a c p", p=64),
                                                                                      scalar=1.0 / NFFT, in1=Z[zi][:, cc:cc + 8, :], op0=ALU.mult, op1=ALU.mult),
                         reads=["pY", "U%d" % zi] + Zkeys(zi), writes=okeys)
                yield

            for _ in filt_gen(0):
                pass
            for sbt in range(nsbt):
                gens = [data_gen(sbt)] + ([filt_gen(sbt + 1)] if sbt + 1 < nsbt else [])
                while gens:
                    for gq in list(gens):
                        try:
                            next(gq)
                        except StopIteration:
                            gens.remove(gq)
        mv = mixT[:, 4 + grp4, :].rearrange("c (a p) -> c a p", p=64)
        for pg in range(4):
            for pi in range(16):
                p = pg * 16 + pi
                P.op("pe", lambda e, p=p, pi=pi: e.matmul(pXr[:, pi * 32:(pi + 1) * 32], lhsT=Z[2][:, :, p], rhs=selb[:, :], start=True, stop=True),
                     reads=["U2", "selb"] + Zkeys(2), writes=["pXr"])
            P.op("act", lambda e, pg=pg, mv=mv: e.activation(out=mv[:, :, pg * 16:(pg + 1) * 16].rearrange("c a p -> c p a"),
                                                             in_=pXr[:, :].rearrange("c (p a) -> c p a", a=32), func=AF.Copy),
                 reads=["pXr"], writes=["mixT"])
    P.build()


def emit_hyena_h(nc, outer, D, mixT, groups=(0, 1, 2, 3), prefix="C_"):
    P = Prog(nc, sem_stack=outer, prefix=prefix)
    sb, ps = P.sb, P.ps
    Us = D("Us", [12, 128, NTOK], kind="Internal", dt=BF16)
    dF1cat, dTrTr, dTiTi, dR12 = D("F1cat_h", [128, 256]), D("TrTr_h", [128, 512]), D("TiTi_h", [128, 512]), D("R12", [128, 512])
    dTT2r, dTT2i, dIF1, dSwin = D("TT2r_h", [128, 512]), D("TT2i_h", [128, 512]), D("IF1_h", [128, 256]), D("Swin", [128, 64])
    dzemb = D("zembT", [33, NTOK])
    ddelta = D("delta4", [128, 512])
    dhbrow = D("hbrow", [1, 1024])
    dfw1, dfw2, dfw3, dfpar = D("fw1", [33, 64]), D("fw2", [64, 64]), D("fw3c4", [64, 2048]), D("fpar", [64, 4])
    dcw, dcb = D("cw4", [128, 36]), D("cb4", [128, 12])
    dsel = D("seld", [128, 32])
    didn = D("identd", [128, 128])

    U = [sb("U%d" % i, [128, NTOK + 2], BF16) for i in range(3)]
    CV = sb("CV", [128, NTOK], BF16)
    Ob = sb("Ob", [128, NTOK], BF16)
    h2T = sb("h2T", [64, NTOK], BF16)
    ApF = sb("ApF", [128, 2, NPQH, 128], BF16)
    ApD = sb("ApD", [128, 2, NPQH, 128], BF16)
    HsL = [sb("Hs%d" % i, [128, 2, NPQH, 128], BF16) for i in range(2)]
    Yb = sb("Yb", [128, 2, NPQH, 128], BF16)
    Bp = sb("Bp", [128, 2, NPQH, 128], BF16)
    W1 = [sb("W1_%d" % i, [128, 512], BF16) for i in range(4)]
    W2 = [sb("W2_%d" % i, [128, 512], BF16) for i in range(4)]
    cpy = [sb("cpr%d" % i, [128, 512], BF16) for i in range(4)]
    cpy2 = sb("cpy2", [128, 512], F32)
    tmpc = sb("tmpc", [128, 1024], F32)
    arg = tmpc[0:64, 0:512]
    h1 = tmpc[0:64, 512:1024]
    argi = sb("argi", [64, 512], mybir.dt.int32)
    F1b, R12b, IF1b = sb("F1b", [128, 256], BF16), sb("R12b", [128, 512], BF16), sb("IF1b", [128, 256], BF16)
    TrTr, TiTi = sb("TrTr", [128, 512], F32), sb("TiTi", [128, 512], F32)
    TT2r, TT2i = sb("TT2r", [128, 512], F32), sb("TT2i", [128, 512], F32)
    Swin = sb("Swin", [128, 64], F32)
    delta = sb("delta", [128, 512], F32)
    hbrow = sb("hbrow", [1, 1024], F32)
    etmp = sb("etmp", [1, 128], F32)
    fw1, fw2 = sb("fw1", [33, 64], F32), sb("fw2", [64, 64], F32)
    fw3b = sb("fw3b", [64, 2048], BF16)
    fpar = sb("fpar", [64, 8], F32)
    cw, cb = sb("cw", [128, 36], F32), sb("cb", [128, 12], F32)
    selb = sb("selb", [128, 32], BF16)
    ident = sb("ident", [128, 128], BF16)
    zc = [sb("zc%d" % i, [33, 512], F32) for i in range(2)]
    win = [sb("win%d" % i, [128, 128], F32) for i in range(2)]
    eoc = [sb("eoc%d" % i, [128, 256], F32) for i in range(2)]
    fw3f = sb("fw3f", [64, 1024], BF16)

    pSS = ps("pSS", [128, 512], F32)
    pA1 = [ps("pA1_%d" % i, [128, 512], F32) for i in range(2)]
    pXr, pXi = ps("pXr", [128, 512], F32), ps("pXi", [128, 512], F32)
    pB = ps("pB", [128, 512], F32)
    pY = ps("pY", [128, 512], F32)
    pTz = ps("pTz", [128, 8, 128], BF16)

    def ldcast(dst, dkey, src, n=512, parts=128):
        P.dma("sp", cpy2[0:parts, 0:n], src, writes=["cpy2"])
        P.op("dve", lambda e: e.tensor_copy(out=dst, in_=cpy2[0:parts, 0:n]), reads=["cpy2"], writes=[dkey])
    ldcast(F1b[:, :], "F1b", dF1cat[:, :], n=256)
    ldcast(R12b[:, :], "R12b", dR12[:, :])
    ldcast(IF1b[:, :], "IF1b", dIF1[:, :], n=256)
    ldcast(ident[:, :], "ident", didn[:, :], n=128)
    ldcast(selb[:, :], "selb", dsel[:, :], n=32)
    for q4 in range(4):
        P.dma("sp", cpy2[0:64, :], dfw3[:, q4 * 512:(q4 + 1) * 512], writes=["cpy2"])
        for o_ in range(2):
            wf = cpy2[0:64, o_ * 256:o_ * 256 + 128]
            wbk = cpy2[0:64, o_ * 256 + 128:o_ * 256 + 256]
            c0_ = q4 * 512 + o_ * 256
            P.op("dve", lambda e, wf=wf, wbk=wbk, c0_=c0_: e.tensor_tensor(out=fw3b[:, c0_:c0_ + 128], in0=wf, in1=wbk, op=ALU.add), reads=["cpy2"], writes=["fw3b"])
            P.op("dve", lambda e, wf=wf, wbk=wbk, c0_=c0_: e.tensor_tensor(out=fw3b[:, c0_ + 128:c0_ + 256], in0=wf, in1=wbk, op=ALU.subtract), reads=["cpy2"], writes=["fw3b"])
            P.op("dve", lambda e, wf=wf, q4=q4, o_=o_: e.tensor_copy(out=fw3f[:, (2 * q4 + o_) * 128:(2 * q4 + o_ + 1) * 128], in_=wf), reads=["cpy2"], writes=["fw3f"])
    for dst, key, src in ((TrTr, "TrTr", dTrTr), (TiTi, "TiTi", dTiTi), (TT2r, "TT2r", dTT2r), (TT2i, "TT2i", dTT2i), (Swin, "Swin", dSwin),
                          (delta, "delta", ddelta), (hbrow, "hbrow", dhbrow), (fw1, "fw1", dfw1), (fw2, "fw2", dfw2), (cw, "cw", dcw), (cb, "cb", dcb)):
        P.dma("sp", dst[:, :], src[:, :], writes=[key])
    P.dma("sp", fpar[:, 0:4], dfpar[:, :], writes=["fpar"])
    i2p = 1.0 / (2.0 * math.pi)
    for (bc, fc, o0) in ((0, 1, 4), (2, 3, 6)):
        P.op("dve", lambda e, bc=bc, fc=fc, o0=o0: e.tensor_tensor(out=fpar[:, o0 + 1:o0 + 2], in0=fpar[:, bc:bc + 1], in1=fpar[:, fc:fc + 1], op=ALU.mult),
             reads=["fpar"], writes=["fpar"])
        P.op("dve", lambda e, o0=o0: e.tensor_scalar(out=fpar[:, o0 + 1:o0 + 2], in0=fpar[:, o0 + 1:o0 + 2], scalar1=i2p, scalar2=16.0, op0=ALU.mult, op1=ALU.add),
             reads=["fpar"], writes=["fpar"])
        P.op("dve", lambda e, fc=fc, o0=o0: e.tensor_scalar(out=fpar[:, o0:o0 + 1], in0=fpar[:, fc:fc + 1], scalar1=i2p, scalar2=None, op0=ALU.mult),
             reads=["fpar"], writes=["fpar"])

    for ch in range(16):
        zt, zk = zc[ch % 2], "zc%d" % (ch % 2)
        P.dma("sp", zt[:, :], dzemb[:, ch * 512:(ch + 1) * 512], writes=[zk])
        for layer in range(2):
            if layer == 0:
                P.op("pe", lambda e, zt=zt: e.matmul(pSS[0:64, :], lhsT=fw1[:, :], rhs=zt[:, :], start=True, stop=True), reads=["fw1", zk], writes=["pSS"])
            else:
                P.op("pe", lambda e: e.matmul(pSS[0:64, :], lhsT=fw2[:, :], rhs=h1[:, :], start=True, stop=True), reads=["fw2", "tmpc"], writes=["pSS"])
            fr, fb = (4, 5) if layer == 0 else (6, 7)
            P.op("dve", lambda e, fr=fr, fb=fb: e.tensor_scalar(out=arg[:, :], in0=pSS[0:64, :], scalar1=fpar[:, fr:fr + 1], scalar2=fpar[:, fb:fb + 1],
                                                                op0=ALU.mult, op1=ALU.add), reads=["pSS", "fpar"], writes=["tmpc"])
            P.op("dve", lambda e: e.tensor_copy(out=argi[:, :], in_=arg[:, :]), reads=["tmpc"], writes=["argi"])
            P.op("dve", lambda e: e.tensor_copy(out=h1[:, :], in_=argi[:, :]), reads=["argi", "tmpc"], writes=["tmpc"])
            P.op("dve", lambda e: e.tensor_tensor(out=arg[:, :], in0=arg[:, :], in1=h1[:, :], op=ALU.subtract), reads=["tmpc"], writes=["tmpc"])
            P.op("dve", lambda e: e.scalar_tensor_tensor(out=arg[:, :], in0=arg[:, :], scalar=0.5, in1=arg[:, :], op0=ALU.is_gt, op1=ALU.subtract),
                 reads=["tmpc"], writes=["tmpc"])
            if layer == 0:
                P.op("act", lambda e: e.activation(out=h1[:, :], in_=arg[:, :], func=AF.Sin, scale=-2.0 * math.pi), reads=["tmpc"], writes=["tmpc"])
            else:
                P.op("act", lambda e, ch=ch: e.activation(out=h2T[:, ch * 512:(ch + 1) * 512], in_=arg[:, :], func=AF.Sin, scale=-2.0 * math.pi),
                     reads=["tmpc"], writes=["h2T"])

    Zkeys = lambda i: ["Z%d_%d" % (i, s) for s in range(128 // (2 * NPQH))]
    Z = [U[i][:, 0:NTOK].rearrange("a (c p) -> a c p", p=64) for i in range(3)]
    CVs = CV[:, :].rearrange("c (a p) -> c a p", p=64)
    E3 = CV[:, :].rearrange("a (c p) -> a c p", p=64)
    O3 = Ob[:, :].rearrange("a (c p) -> a c p", p=64)
    h2s = h2T[:, :].rearrange("j (a p) -> j a p", p=64)
    cnt = [0]

    def twiddle(psrc, pkey, Tr_, Ti_, trk, tik, outr, outi, okey, view, ieng="dve"):
        g = cnt[0]
        cnt[0] += 1
        w1, w1k = W1[g % 4], "W1_%d" % (g % 4)
        w2, w2k = W2[g % 4], "W2_%d" % (g % 4)
        cp_, cpk = cpy[g % 4], "cpr%d" % (g % 4)
        P.op("act", lambda e: e.activation(out=cp_[:, :], in_=psrc[:, :], func=AF.Copy), reads=[pkey], writes=[cpk])
        P.op("dve", lambda e: e.tensor_tensor(out=w1[:, :], in0=psrc[:, :], in1=Tr_[:, :], op=ALU.mult), reads=[pkey, trk], writes=[w1k])
        P.op("pool", lambda e: e.tensor_tensor(out=w2[:, :], in0=cp_[:, :], in1=Ti_[:, :], op=ALU.mult), reads=[cpk, tik], writes=[w2k])
        w1r, w1i = view(w1)
        w2r, w2i = view(w2)
        P.op("dve", lambda e: e.tensor_tensor(out=outr, in0=w1r, in1=w2i, op=ALU.subtract), reads=[w1k, w2k], writes=[okey])
        P.op(ieng, lambda e: e.tensor_tensor(out=outi, in0=w2r, in1=w1i, op=ALU.add), reads=[w1k, w2k], writes=[okey])

    vv = lambda t: t[:, :].rearrange("k (j r x) -> k j r x", j=2, r=2)
    v_fwd = lambda t: (vv(t)[:, :, 0, :], vv(t)[:, :, 1, :])
    v_inv = v_fwd

    def fwd_stage1(src3, skey, c0, Ap, apk, ieng="dve"):
        for q2 in range(NPQH // 2):
            g = cnt[0]
            pa, pak = pA1[g % 2], "pA1_%d" % (g % 2)
            for jj in range(2):
                c = c0 + 2 * (2 * q2 + jj)
                P.op("pe", lambda e, c=c, pa=pa, jj=jj: e.matmul(pa[:, jj * 256:(jj + 1) * 256], lhsT=src3[:, c:c + 2, :], rhs=F1b[:, :], start=True, stop=True),
                     reads=[skey, "F1b"], writes=[pak])
            twiddle(pa, pak, TrTr, TiTi, "TrTr", "TiTi", Ap[:, 0, 2 * q2:2 * q2 + 2, :], Ap[:, 1, 2 * q2:2 * q2 + 2, :], apk, v_fwd, ieng=ieng)

    G2r, G2in, G2i = R12b[:, 0:128], R12b[:, 128:256], R12b[:, 256:384]
    R1, R2 = R12b[:, 0:256], R12b[:, 256:512]

    for grp4 in groups:
        for i in range(3):
            P.dma("sp", U[i][:, 1:NTOK + 1], Us[3 * grp4 + i], reads=["Us"], writes=["U%d" % i] + Zkeys(i), key="U%d" % i)
            P.op("pool", lambda e, i=i: e.memset(U[i][:, 0:1], 0.0), reads=["U%d" % i], writes=["U%d" % i] + Zkeys(i))
            P.op("pool", lambda e, i=i: e.memset(U[i][:, NTOK + 1:NTOK + 2], 0.0), reads=["U%d" % i], writes=["U%d" % i])
        for i in range(3):
            ci = 9 * grp4 + 3 * i
            bi = 3 * grp4 + i
            for ch in range(8):
                j0 = ch * 1024
                P.op("dve", lambda e, i=i, j0=j0, ci=ci, bi=bi: e.tensor_scalar(out=tmpc[:, :], in0=U[i][:, j0:j0 + 1024], scalar1=cw[:, ci:ci + 1],
                                                                                scalar2=cb[:, bi:bi + 1], op0=ALU.mult, op1=ALU.add),
                     reads=["U%d" % i, "cw", "cb"], writes=["tmpc"])
                P.op("dve", lambda e, i=i, j0=j0, ci=ci: e.scalar_tensor_tensor(out=tmpc[:, :], in0=U[i][:, j0 + 1:j0 + 1025], scalar=cw[:, ci + 1:ci + 2],
                                                                                in1=tmpc[:, :], op0=ALU.mult, op1=ALU.add),
                     reads=["U%d" % i, "cw", "tmpc"], writes=["tmpc"])
                P.op("dve", lambda e, i=i, j0=j0, ci=ci: e.scalar_tensor_tensor(out=CV[:, j0:j0 + 1024], in0=U[i][:, j0 + 2:j0 + 1026], scalar=cw[:, ci + 2:ci + 3],
                                                                                in1=tmpc[:, :], op0=ALU.mult, op1=ALU.add),
                     reads=["U%d" % i, "cw", "tmpc"], writes=["CV"])
            for pg in range(8):
                for pi in range(8):
                    p = pg * 8 + pi
                    P.op("pe", lambda e, p=p, pi=pi: e.transpose(out=pTz[:, pi, :], in_=CVs[:, :, p], identity=ident[:, :]),
                         reads=["CV", "ident"], writes=["pTz"])
                P.op("act", lambda e, i=i, pg=pg: e.activation(out=Z[i][:, :, pg * 8:(pg + 1) * 8].rearrange("a c p -> a p c"), in_=pTz[:, :, :], func=AF.Copy),
                     reads=["pTz"], writes=["U%d" % i] + Zkeys(i))
        for o in range(2):
            w3c0 = grp4 * 512 + o * 256
            for p in range(64):
                pe_, pek = (pSS, "pSS") if p % 2 == 0 else (pB, "pB")
                wn, wnk = win[p % 2], "win%d" % (p % 2)
                ec, eck = eoc[p % 2], "eoc%d" % (p % 2)
                P.op("pe", lambda e, p=p, w3c0=w3c0, pe_=pe_: e.matmul(pe_[:, 0:256], lhsT=h2s[:, :, p], rhs=fw3b[:, w3c0:w3c0 + 256], start=True, stop=True),
                     reads=["h2T", "fw3b"], writes=[pek])
                P.op("act", lambda e, p=p, grp4=grp4, wn=wn: e.activation(out=wn[:, :], in_=delta[:, grp4 * 128:(grp4 + 1) * 128], func=AF.Exp, scale=Swin[:, p:p + 1]),
                     reads=["delta", "Swin"], writes=[wnk])
                P.op("act", lambda e, pe_=pe_, ec=ec: e.activation(out=ec[:, :], in_=pe_[:, 0:256], func=AF.Copy), reads=[pek], writes=[eck])
                P.op("pool", lambda e, p=p, ec=ec, wn=wn: e.tensor_tensor(out=E3[:, :, p], in0=ec[:, 0:128], in1=wn[:, :], op=ALU.mult), reads=[eck, wnk], writes=["CV"])
                P.op("dve", lambda e, p=p, ec=ec, wn=wn: e.tensor_tensor(out=O3[:, :, p], in0=ec[:, 128:256], in1=wn[:, :], op=ALU.mult), reads=[eck, wnk], writes=["Ob"])
                if p == 0:
                    fc = (2 * grp4 + o) * 128
                    P.op("pe", lambda e, fc=fc: e.matmul(pY[0:1, 0:128], lhsT=h2T[:, 0:1], rhs=fw3f[:, fc:fc + 128], start=True, stop=True),
                         reads=["h2T", "fw3f"], writes=["pY"])
                    hc0 = (2 * grp4 + o) * 128
                    P.op("dve", lambda e, wn=wn: e.tensor_tensor(out=etmp[:, :], in0=pY[0:1, 0:128], in1=wn[0:1, :], op=ALU.mult), reads=["pY", wnk], writes=["etmp"])
                    P.op("dve", lambda e: e.tensor_copy(out=O3[0:1, :, 0], in_=etmp[:, :]), reads=["etmp"], writes=["Ob"])
                    P.op("dve", lambda e, hc0=hc0: e.tensor_tensor(out=E3[0:1, :, 0], in0=etmp[:, :], in1=hbrow[0:1, hc0:hc0 + 128], op=ALU.add),
                         reads=["etmp", "hbrow"], writes=["CV"])
            nsbt = 128 // (2 * NPQH)

            def filt_gen(sbt):
                c0 = sbt * 2 * NPQH
                Hs, hk = HsL[sbt % 2], "Hs%d" % (sbt % 2)
                for which, (src3, skey) in enumerate(((E3, "CV"), (O3, "Ob"))):
                    fwd_stage1(src3, skey, c0, ApF, "ApF")
                    yield
                    for qq in range(NPQH // 4):
                        rr = ApF[:, 0, 4 * qq:4 * qq + 4, :]
                        ri = ApF[:, 1, 4 * qq:4 * qq + 4, :]
                        if which == 0:
                            P.op("pe", lambda e, rr=rr: e.matmul(pSS[:, :], lhsT=G2r, rhs=rr, start=True, stop=False), reads=["R12b", "ApF"], writes=["pSS"])
                            P.op("pe", lambda e, ri=ri: e.matmul(pSS[:, :], lhsT=G2in, rhs=ri, start=False, stop=True), reads=["R12b", "ApF"], writes=["pSS"])
                        else:
                            P.op("pe", lambda e, rr=rr: e.matmul(pSS[:, :], lhsT=G2i, rhs=rr, start=True, stop=False), reads=["R12b", "ApF"], writes=["pSS"])
                            P.op("pe", lambda e, ri=ri: e.matmul(pSS[:, :], lhsT=G2r, rhs=ri, start=False, stop=True), reads=["R12b", "ApF"], writes=["pSS"])
                        P.op("act", lambda e, qq=qq, Hs=Hs, which=which: e.activation(out=Hs[:, which, 4 * qq:4 * qq + 4, :], in_=pSS[:, :].rearrange("k (q x) -> k q x", q=4), func=AF.Copy),
                             reads=["pSS"], writes=[hk])
                    yield

            def data_gen(sbt):
                c0 = sbt * 2 * NPQH
                zk = "Z0_%d" % sbt
                Hs, hk = HsL[sbt % 2], "Hs%d" % (sbt % 2)
                fwd_stage1(Z[0], zk, c0, ApD, "ApD", ieng="pool")
                yield
                for qq in range(NPQH // 4):
                    rr = ApD[:, 0, 4 * qq:4 * qq + 4, :]
                    ri = ApD[:, 1, 4 * qq:4 * qq + 4, :]
                    P.op("pe", lambda e, rr=rr: e.matmul(pXr[:, :], lhsT=G2r, rhs=rr, start=True, stop=False), reads=["R12b", "ApD"], writes=["pXr"])
                    P.op("pe", lambda e, ri=ri: e.matmul(pXr[:, :], lhsT=G2in, rhs=ri, start=False, stop=True), reads=["R12b", "ApD"], writes=["pXr"])
                    P.op("pe", lambda e, rr=rr: e.matmul(pXi[:, :], lhsT=G2i, rhs=rr, start=True, stop=False), reads=["R12b", "ApD"], writes=["pXi"])
                    P.op("pe", lambda e, ri=ri: e.matmul(pXi[:, :], lhsT=G2r, rhs=ri, start=False, stop=True), reads=["R12b", "ApD"], writes=["pXi"])
                    g = cnt[0]
                    cnt[0] += 1
                    w1, w1k = W1[g % 4], "W1_%d" % (g % 4)
                    w2, w2k = W2[g % 4], "W2_%d" % (g % 4)
                    cx, cxk = cpy[g % 4], "cpr%d" % (g % 4)
                    hr = Hs[:, 0, 4 * qq:4 * qq + 4, :].rearrange("k q x -> k (q x)")
                    hi = Hs[:, 1, 4 * qq:4 * qq + 4, :].rearrange("k q x -> k (q x)")
                    yr = Yb[:, 0, 4 * qq:4 * qq + 4, :].rearrange("k q x -> k (q x)")
                    yi = Yb[:, 1, 4 * qq:4 * qq + 4, :].rearrange("k q x -> k (q x)")
                    P.op("act", lambda e, cx=cx: e.activation(out=cx[:, :], in_=pXi[:, :], func=AF.Copy), reads=["pXi"], writes=[cxk])
                    P.op("act", lambda e: e.activation(out=cpy2[:, :], in_=pXr[:, :], func=AF.Copy), reads=["pXr"], writes=["cpy2"])
                    P.op("dve", lambda e, w1=w1, hr=hr: e.tensor_tensor(out=w1[:, :], in0=pXr[:, :], in1=hr, op=ALU.mult), reads=["pXr", hk], writes=[w1k])
                    P.op("pool", lambda e, w2=w2, cx=cx, hi=hi: e.tensor_tensor(out=w2[:, :], in0=cx[:, :], in1=hi, op=ALU.mult), reads=[cxk, hk], writes=[w2k])
                    P.op("dve", lambda e, w1=w1, w2=w2, yr=yr: e.tensor_tensor(out=yr, in0=w1[:, :], in1=w2[:, :], op=ALU.subtract), reads=[w1k, w2k], writes=["Yb"])
                    P.op("pool", lambda e, w1=w1, hr=hr, cx=cx: e.tensor_tensor(out=w1[:, :], in0=cx[:, :], in1=hr, op=ALU.mult), reads=[cxk, hk, "Yb"], writes=[w1k])
                    P.op("pool", lambda e, w2=w2, hi=hi: e.tensor_tensor(out=w2[:, :], in0=cpy2[:, :], in1=hi, op=ALU.mult), reads=["cpy2", hk, "Yb"], writes=[w2k])
                    P.op("dve", lambda e, w1=w1, w2=w2, yi=yi: e.tensor_tensor(out=yi, in0=w1[:, :], in1=w2[:, :], op=ALU.add), reads=[w1k, w2k], writes=["Yb"])
                yield
                for q2 in range(NPQH // 2):
                    for jj in range(2):
                        q = 2 * q2 + jj
                        P.op("pe", lambda e, q=q, jj=jj: e.matmul(pB[:, jj * 256:(jj + 1) * 256], lhsT=Yb[:, 0, q, :], rhs=R1, start=True, stop=False),
                             reads=["Yb", "R12b"], writes=["pB"])
                        P.op("pe", lambda e, q=q, jj=jj: e.matmul(pB[:, jj * 256:(jj + 1) * 256], lhsT=Yb[:, 1, q, :], rhs=R2, start=False, stop=True),
                             reads=["Yb", "R12b"], writes=["pB"])
                    twiddle(pB, "pB", TT2r, TT2i, "TT2r", "TT2i", Bp[:, 0, 2 * q2:2 * q2 + 2, :], Bp[:, 1, 2 * q2:2 * q2 + 2, :], "Bp", v_inv, ieng="pool")
                yield
                for hh in range(NPQH // 4):
                    for r in range(2):
                        rhs = Bp[:, r, 4 * hh:4 * hh + 4, :]
                        lt = IF1b[:, r * 128:(r + 1) * 128]
                        P.op("pe", lambda e, rhs=rhs, lt=lt, r=r: e.matmul(pY[:, :], lhsT=lt, rhs=rhs, start=(r == 0), stop=(r == 1)),
                             reads=["Bp", "IF1b"], writes=["pY"])
                    cc = c0 + 8 * hh
                    zi = 1 if o == 0 else 2
                    zo = 0 if o == 0 else 2
                    okeys = [zk] if o == 0 else ["U2", "Z2_%d" % sbt]
                    P.op("dve", lambda e, cc=cc, zi=zi, zo=zo: e.scalar_tensor_tensor(out=Z[zo][:, cc:cc + 8, :], in0=pY[:, :].rearrange("a (c p) -> a c p", p=64),
                                                                                      scalar=2.0 / NFFT, in1=Z[zi][:, cc:cc + 8, :], op0=ALU.mult, op1=ALU.mult),
                         reads=["pY", "U%d" % zi] + Zkeys(zi), writes=okeys)
                yield

            for _ in filt_gen(0):
                pass
            for sbt in range(nsbt):
                gens = [data_gen(sbt)] + ([filt_gen(sbt + 1)] if sbt + 1 < nsbt else [])
                while gens:
                    for gq in list(gens):
                        try:
                            next(gq)
                        except StopIteration:
                            gens.remove(gq)
        mv = mixT[:, 4 + grp4, :].rearrange("c (a p) -> c a p", p=64)
        for pg in range(4):
            for pi in range(16):
                p = pg * 16 + pi
                P.op("pe", lambda e, p=p, pi=pi: e.matmul(pXr[:, pi * 32:(pi + 1) * 32], lhsT=Z[2][:, :, p], rhs=selb[:, :], start=True, stop=True),
                     reads=["U2", "selb"] + Zkeys(2), writes=["pXr"])
            P.op("act", lambda e, pg=pg, mv=mv: e.activation(out=mv[:, :, pg * 16:(pg + 1) * 16].rearrange("c a p -> c p a"),
                                                             in_=pXr[:, :].rearrange("c (p a) -> c p a", a=32), func=AF.Copy),
                 reads=["pXr"], writes=["mixT"])
    P.build()


def emit_phase2_f(nc, outer, D, mixT, NT=2048):
    P = Prog(nc, sem_stack=outer, prefix="D_")
    x_tok = D("x_tok", [NT, 1024])
    cvec = D("cvec", [128, 8])
    w_ada = D("w_ada2", [1024, 4096])
    b_ada = D("b_ada2", [1, 4096])
    gvec = D("gvec", [3, 1024])
    gmix = D("gmix", [128, 8])
    w_out = D("w_out", [1024, 1024])
    w1 = D("w1", [1024, 4096])
    w2 = D("w2", [4096, 1024])
    out = D("out", [NT, 1024], kind="ExternalOutput")
    x1s = D("x1s", [NT, 1024], kind="Internal")

    sb, ps = P.sb, P.ps
    wout_b = sb("wout_b", [128, 8, 1024], BF16)
    mods = sb("mods", [128, 4096], F32)
    sbc = sb("sbc", [128, 8, 128], F32)
    ones = sb("ones", [128, 128], F32)
    ident = sb("ident", [128, 128], BF16)
    identf = sb("identf", [128, 128], F32)
    stage = [sb("stage%d" % i, [128, 2048], F32) for i in range(2)]
    w1b = [sb("w1b%d" % i, [128, 8, 512], BF16) for i in range(2)]
    w2b = [sb("w2b%d" % i, [128, 4, 1024], BF16) for i in range(2)]
    h2T = sb("h2T", [128, 8, 1024], BF16)
    f2acc = sb("f2acc", [128, 8, 1024], F32)
    brow = f2acc[0:1, 0:4, :].rearrange("p a b -> p (a b)")
    xt = [sb("xt%d" % i, [128, 1024], F32) for i in range(2)]
    sqm = sb("sqm", [128, 8, 128], BF16)
    onescol = sb("onescol", [128, 2], BF16)
    ysb = sb("ysb", [128, 1024], F32)
    tmp = sb("tmp", [128, 1024], F32)
    tmp2 = sb("tmp2", [128, 1024], F32)
    h2b = sb("h2b", [128, 1024], BF16)
    rl = [sb("rl%d" % i, [128, 512], F32) for i in range(2)]
    aT = [sb("aT%d" % i, [128, 4, 512], BF16) for i in range(2)]
    small = sb("small", [128, 32], F32)
    csb = sb("csb", [128, 8], F32)
    gmx = sb("gmx", [128, 8], F32)

    pA = ps("pA", [128, 1024], F32)
    pH = ps("pH", [128, 1024], F32)
    pT = ps("pT", [128, 1024], BF16)
    pM = [ps("pM%d" % i, [128, 512], F32) for i in range(2)]

    P.op("pool", lambda e: e.memset(ones[:, :], 1.0), writes=["ones"])
    P.op("pool", lambda e: e.memset(onescol[:, :], 1.0), writes=["onescol"])
    P.op("pool", lambda e: e.memset(identf[:, :], 0.0), writes=["identf"])
    identd = D("identd", [128, 128])
    P.dma("sp", identf[:, :], identd[:, :], writes=["identf"])
    P.op("dve", lambda e: e.tensor_copy(out=ident[:, :], in_=identf[:, :]), reads=["identf"], writes=["ident"])
    P.dma("sp", csb[:, :], cvec[:, :], writes=["csb"])
    P.dma("sp", gmx[:, :], gmix[:, :], writes=["gmx"])
    P.dma("sp", brow[:, :], b_ada[:, :], writes=["brow"])
    P.op("act", lambda e: e.activation(out=csb[:, :], in_=csb[:, :], func=AF.Silu), reads=["csb"], writes=["csb"])
    for k in range(8):
        P.op("dve", lambda e, k=k: e.tensor_scalar(out=sbc[:, k, :], in0=ones[:, :], scalar1=csb[:, k:k + 1],
                                                    scalar2=None, op0=ALU.mult), reads=["ones", "csb"], writes=["sbc"])
    wa = w_ada.rearrange("(k p) n -> p k n", p=128)
    for blk in range(16):
        st = stage[blk % 2]
        skey = "stage%d" % (blk % 2)
        stv = st[:, :].rearrange("p (k n) -> p k n", k=8)
        P.dma("sp", stv, wa[:, :, blk * 256:(blk + 1) * 256], writes=[skey])
        pm = pM[blk % 2]
        pkey = "pM%d" % (blk % 2)
        for k in range(8):
            P.op("pe", lambda e, k=k, pm=pm, stv=stv: e.matmul(pm[:, 0:256], lhsT=sbc[:, k, :], rhs=stv[:, k, :],
                                                               start=(k == 0), stop=False),
                 reads=["sbc", skey], writes=[pkey])
        P.op("pe", lambda e, pm=pm, blk=blk: e.matmul(pm[:, 0:256], lhsT=ones[0:1, :], rhs=brow[0:1, blk * 256:(blk + 1) * 256],
                                                     start=False, stop=True), reads=["ones", "brow"], writes=[pkey])
        P.op("act", lambda e, pm=pm, blk=blk: e.activation(out=mods[:, blk * 256:(blk + 1) * 256], in_=pm[:, 0:256], func=AF.Copy),
             reads=[pkey], writes=["mods"])
    gb = stage[0][:, :]
    P.dma("sp", gb[:, 0:1024], gvec[0:1, :].to_broadcast((128, 1024)), writes=["stage0"])
    gb1 = stage[1][:, :]
    P.dma("sp", gb1[:, 0:1024], gvec[1:2, :].to_broadcast((128, 1024)), writes=["stage1"])
    P.dma("sp", gb1[:, 1024:2048], gvec[2:3, :].to_broadcast((128, 1024)), writes=["stage1"])
    G1, SH2, G2, G3 = mods[:, 0:1024], mods[:, 1024:2048], mods[:, 2048:3072], mods[:, 3072:4096]
    P.op("dve", lambda e: e.tensor_tensor(out=G1, in0=G1, in1=gb[:, 0:1024], op=ALU.mult), reads=["mods", "stage0"], writes=["mods"])
    P.op("dve", lambda e: e.scalar_tensor_tensor(out=G2, in0=G2, scalar=1.0, in1=gb1[:, 0:1024], op0=ALU.add, op1=ALU.mult),
         reads=["mods", "stage1"], writes=["mods"])
    P.op("dve", lambda e: e.tensor_tensor(out=G3, in0=G3, in1=gb1[:, 1024:2048], op=ALU.mult), reads=["mods", "stage1"], writes=["mods"])
    wo = w_out.rearrange("(k p) n -> p k n", p=128)
    for c4 in range(4):
        st = stage[c4 % 2]
        skey = "stage%d" % (c4 % 2)
        stv = st[:, :].rearrange("p (k n) -> p k n", k=2)
        P.dma("sp", stv, wo[:, 2 * c4:2 * c4 + 2, :], writes=[skey])
        for kk in range(2):
            k = 2 * c4 + kk
            P.op("dve" if kk == 0 else "pool", lambda e, k=k, kk=kk, stv=stv: e.tensor_scalar(
                out=wout_b[:, k, :], in0=stv[:, kk, :], scalar1=gmx[:, k:k + 1], scalar2=None, op0=ALU.mult),
                reads=[skey, "gmx"], writes=["wout_b"])

    def sumsq(src, dst, reads, key, eng="dve"):
        w = src.shape[-1]
        P.op("act", lambda e: e.activation(out=tmp2[:, 0:w], in_=src, func=AF.Square), reads=reads, writes=["tmp2"])
        P.op(eng, lambda e: e.tensor_reduce(out=dst, in_=tmp2[:, 0:w], axis=AX.X, op=ALU.add), reads=["tmp2"] + list(reads), writes=[key])

    nsm = [0]

    def smallcol(n=1):
        c = nsm[0] % 16
        nsm[0] += 1
        return small[:, 2 * c:2 * c + n], "small%d" % c

    w1v = w1.rearrange("(k p) n -> p k n", p=128)
    w2v = w2.rearrange("(c p) n -> p c n", p=128)
    it = [0]
    for half in range(NT // 1024):
        for tile in range(8):
            t0 = half * 1024 + tile * 128
            i = it[0]
            it[0] += 1
            X, xk = xt[i % 2], "xt%d" % (i % 2)
            tl = t0
            MBk = lambda k, tl=tl: mixT[:, k, tl:tl + 128]
            mbk = "mixT"
            P.dma("sp", X[:, :], x_tok[t0:t0 + 128, :], writes=[xk])
            P.op("act", lambda e, tl=tl: e.activation(out=sqm[:, :, :], in_=mixT[:, :, tl:tl + 128], func=AF.Square),
                 reads=["mixT"], writes=["sqm"])
            pst = pM[i % 2]
            pstk = "pM%d" % (i % 2)
            for grp in range(2):
                for k in range(4):
                    P.op("pe", lambda e, grp=grp, k=k, pst=pst: e.matmul(pst[:, grp:grp + 1], lhsT=sqm[:, 4 * grp + k, :], rhs=onescol[:, 0:1],
                                                                        start=(k == 0), stop=(k == 3)), reads=["sqm", "onescol"], writes=[pstk])
            rr, rrk = smallcol(2)
            _rs(P, None, pst[:, 0:2], rr, 512.0, "rmix", [pstk], rrk)
            for nh in range(2):
                for k in range(4):
                    P.op("pe", lambda e, k=k, nh=nh, MBk=MBk: e.matmul(pA[:, nh * 512:(nh + 1) * 512], lhsT=MBk(k),
                                                                     rhs=wout_b[:, k, nh * 512:(nh + 1) * 512],
                                                                     start=(k == 0), stop=(k == 3)),
                         reads=[mbk, "wout_b"], writes=["pA%d" % nh])
                for k in range(4, 8):
                    P.op("pe", lambda e, k=k, nh=nh, MBk=MBk: e.matmul(pH[:, nh * 512:(nh + 1) * 512], lhsT=MBk(k),
                                                                     rhs=wout_b[:, k, nh * 512:(nh + 1) * 512],
                                                                     start=(k == 4), stop=(k == 7)),
                         reads=[mbk, "wout_b"], writes=["pH%d" % nh])
            P.op("act", lambda e, rr=rr: e.activation(out=ysb[:, :], in_=pA[:, :], func=AF.Copy, scale=rr[:, 0:1]),
                 reads=["pA0", "pA1", rrk], writes=["ysb"])
            P.op("dve", lambda e, rr=rr: e.scalar_tensor_tensor(out=ysb[:, :], in0=pH[:, :], scalar=rr[:, 1:2], in1=ysb[:, :],
                                                                op0=ALU.mult, op1=ALU.add),
                 reads=["pH0", "pH1", rrk, "ysb"], writes=["ysb"])
            ss2, ss2k = smallcol(1)
            r2, r2k = smallcol(1)
            sumsq(ysb[:, :], ss2[:, 0:1], ["ysb"], ss2k)
            _rs(P, None, ss2, r2, 1024.0, "ry", [ss2k], r2k)
            P.op("dve", lambda e, r2=r2: e.scalar_tensor_tensor(out=tmp[:, :], in0=ysb[:, :], scalar=r2[:, 0:1], in1=G1,
                                                                op0=ALU.mult, op1=ALU.mult),
                 reads=["ysb", r2k, "mods"], writes=["tmp"])
            P.op("dve", lambda e, X=X: e.tensor_tensor(out=X[:, :], in0=tmp[:, :], in1=X[:, :], op=ALU.add),
                 reads=["tmp", xk], writes=[xk])
            P.dma("pool", x1s[t0:t0 + 128, :], X[:, :], reads=[xk], writes=["x1s%d" % (t0 // 128)])
            ss3, ss3k = smallcol(1)
            r3, r3k = smallcol(1)
            sumsq(X[:, :], ss3[:, 0:1], [xk], ss3k)
            _rs(P, None, ss3, r3, 1024.0, "r1", [ss3k], r3k)
            P.op("dve", lambda e, r3=r3, X=X: e.scalar_tensor_tensor(out=tmp[:, :], in0=X[:, :], scalar=r3[:, 0:1], in1=G2,
                                                                     op0=ALU.mult, op1=ALU.mult),
                 reads=[xk, r3k, "mods"], writes=["tmp"])
            P.op("dve", lambda e: e.tensor_tensor(out=h2b[:, :], in0=tmp[:, :], in1=SH2, op=ALU.add),
                 reads=["tmp", "mods"], writes=["h2b"])
            for k in range(8):
                P.op("pe", lambda e, k=k: e.transpose(out=pT[:, k * 128:(k + 1) * 128], in_=h2b[:, k * 128:(k + 1) * 128],
                                                      identity=ident[:, :]), reads=["h2b", "ident"], writes=["pT"])
            P.op("act", lambda e, tile=tile: e.activation(out=h2T[:, :, tile * 128:(tile + 1) * 128],
                                                          in_=pT[:, :].rearrange("p (k t) -> p k t", k=8), func=AF.Copy),
                 reads=["pT"], writes=["h2T"])
        for j in range(8):
            W1B, w1k = w1b[j % 2], "w1b%d" % (j % 2)
            W2B, w2k = w2b[j % 2], "w2b%d" % (j % 2)
            for hh in range(2):
                st, skey = stage[hh], "stage%d" % hh
                stv = st[:, :].rearrange("p (k n) -> p k n", k=8)
                P.dma("sp", stv, w1v[:, :, j * 512 + hh * 256:j * 512 + (hh + 1) * 256], writes=[skey])
                P.op("dve" if hh == 0 else "pool", lambda e, W1B=W1B, stv=stv, hh=hh: e.tensor_copy(
                    out=W1B[:, :, hh * 256:(hh + 1) * 256], in_=stv), reads=[skey], writes=[w1k])
            for hh in range(2):
                st, skey = stage[hh], "stage%d" % hh
                stv = st[:, :].rearrange("p (c n) -> p c n", c=2)
                P.dma("sp", stv, w2v[:, j * 4 + hh * 2:j * 4 + hh * 2 + 2, :], writes=[skey])
                P.op("dve" if hh == 0 else "pool", lambda e, W2B=W2B, stv=stv, hh=hh: e.tensor_copy(
                    out=W2B[:, hh * 2:hh * 2 + 2, :], in_=stv), reads=[skey], writes=[w2k])
            for tg in range(2):
                AT, atk = aT[tg], "aT%d" % tg
                for hc in range(4):
                    pm, pkey = pM[hc % 2], "pM%d" % (hc % 2)
                    RL, rlk = rl[hc % 2], "rl%d" % (hc % 2)
                    for k in range(8):
                        P.op("pe", lambda e, k=k, hc=hc, tg=tg, pm=pm, W1B=W1B: e.matmul(
                            pm[:, :], lhsT=W1B[:, k, hc * 128:(hc + 1) * 128], rhs=h2T[:, k, tg * 512:(tg + 1) * 512],
                            start=(k == 0), stop=(k == 7)), reads=[w1k, "h2T"], writes=[pkey])
                    P.op("act", lambda e, pm=pm, RL=RL: e.activation(out=RL[:, :], in_=pm[:, :], func=AF.Relu),
                         reads=[pkey], writes=[rlk])
                    P.op("act", lambda e, RL=RL, AT=AT, hc=hc: e.activation(out=AT[:, hc, :], in_=RL[:, :], func=AF.Square),
                         reads=[rlk], writes=[atk])
                for tt in range(4):
                    tile = tg * 4 + tt
                    for nh in range(2):
                        pp, ppk = (pA, "pA%d" % nh) if tt % 2 == 0 else (pH, "pH%d" % nh)
                        for hc in range(4):
                            P.op("pe", lambda e, hc=hc, tt=tt, nh=nh, pp=pp, AT=AT, W2B=W2B: e.matmul(
                                pp[:, nh * 512:(nh + 1) * 512], lhsT=AT[:, hc, tt * 128:(tt + 1) * 128],
                                rhs=W2B[:, hc, nh * 512:(nh + 1) * 512], start=(hc == 0), stop=(hc == 3)),
                                reads=[atk, w2k], writes=[ppk])
                        fv = f2acc[:, tile, nh * 512:(nh + 1) * 512]
                        fk = "f2_%d_%d" % (tile, nh)
                        if j == 0:
                            P.op("dve", lambda e, fv=fv, pp=pp, nh=nh: e.tensor_copy(out=fv, in_=pp[:, nh * 512:(nh + 1) * 512]),
                                 reads=[ppk], writes=[fk])
                        else:
                            P.op("dve", lambda e, fv=fv, pp=pp, nh=nh: e.tensor_tensor(out=fv, in0=pp[:, nh * 512:(nh + 1) * 512], in1=fv, op=ALU.add),
                                 reads=[ppk, fk], writes=[fk])
        for tile in range(8):
            t0 = half * 1024 + tile * 128
            i = it[0]
            it[0] += 1
            X, xk = xt[i % 2], "xt%d" % (i % 2)
            P.dma("sp", X[:, :], x1s[t0:t0 + 128, :], reads=["x1s%d" % (t0 // 128)], writes=[xk], key=xk)
            fkeys = ["f2_%d_%d" % (tile, nh) for nh in range(2)]
            ss4, ss4k = smallcol(1)
            r4, r4k = smallcol(1)
            sumsq(f2acc[:, tile, :], ss4[:, 0:1], fkeys, ss4k)
            _rs(P, None, ss4, r4, 1024.0, "rf", [ss4k], r4k)
            P.op("dve", lambda e, r4=r4, tile=tile: e.scalar_tensor_tensor(out=tmp[:, :], in0=f2acc[:, tile, :], scalar=r4[:, 0:1], in1=G3,
                                                                          op0=ALU.mult, op1=ALU.mult),
                 reads=fkeys + [r4k, "mods"], writes=["tmp"])
            P.op("dve", lambda e, X=X: e.tensor_tensor(out=X[:, :], in0=tmp[:, :], in1=X[:, :], op=ALU.add),
                 reads=["tmp", xk], writes=[xk])
            P.dma("pool", out[t0:t0 + 128, :], X[:, :], reads=[xk], writes=["out%d" % (t0 // 128)])
    P.build()


def build_fused(upto=4, dbg=False):
    nc = bass.Bass("TRN2", target_bir_lowering=False)
    outer = ExitStack()
    D = DramReg(nc)
    mixT = outer.enter_context(nc.sbuf_tensor("mixT_keep", [128, 8, NOWN], BF16))
    with ExitStack() as es_a:
        hTw = es_a.enter_context(nc.sbuf_tensor("hTw_keep", [128, 8, NW], BF16))
        emit_attn_norm_f(nc, outer, D, hTw)
        if upto >= 1:
            emit_attn_f(nc, outer, D, mixT, hTw)
    if upto >= 2:
        emit_hyproj_f(nc, outer, D)
    if upto >= 3:
        emit_hyena_h(nc, outer, D, mixT, groups=(0, 1, 2, 3), prefix="C0_")
    if upto >= 4:
        emit_phase2_f(nc, outer, D, mixT)
    if dbg:
        P = Prog(nc, sem_stack=outer, prefix="E_")
        dd = D("dbg_mixT", [128, 8, NOWN], kind="ExternalOutput", dt=BF16)
        P.dma("sp", dd[:, :, :], mixT[:, :, :], reads=["mixT"], writes=["dbg"])
        P.build()
    outer.close()
    return nc


def fused_inputs(inp):
    f = lambda a: np.ascontiguousarray(a, dtype=np.float32)
    K = _hy_consts()
    K.update(_hy_consts_h())
    C, S = _rope_tables()
    M = _attn_masks()
    hsel = np.zeros((128, 64), np.float32)
    hsel[0:64, 0] = 1.0
    hsel[64:128, 32] = 1.0
    w_in = inp["w_in"][0]
    hy_w = w_in[:, 1536:]
    w_incA = []
    for hp in range(4):
        cs = slice(128 * hp, 128 * hp + 128)
        wq, wk, wv = w_in[:, 0:512][:, cs], w_in[:, 512:1024][:, cs], w_in[:, 1024:1536][:, cs]
        w_incA.append(np.concatenate([wq, _perm_cols(wq), wk, _perm_cols(wk), wv, np.zeros((1024, 128), np.float32)], axis=1))
    w_incA = f(np.stack(w_incA))
    w_incH = f(np.concatenate([hy_w[:, a * 512 + 128 * g:a * 512 + 128 * g + 128] for g in range(4) for a in range(3)], axis=1))
    cwv = inp["conv_w"][0].reshape(3, 3, 512)
    cbv = inp["conv_b"][0].reshape(3, 512)
    cw4 = np.zeros((128, 36), np.float32)
    cb4 = np.zeros((128, 12), np.float32)
    for g in range(4):
        for a in range(3):
            cb4[:, 3 * g + a] = cbv[a, 128 * g:128 * g + 128]
            for t in range(3):
                cw4[:, 9 * g + 3 * a + t] = cwv[t, a, 128 * g:128 * g + 128]
    w3 = inp["filt_w3"][0].reshape(64, 2, 2, 4, 128).transpose(0, 3, 1, 2, 4).reshape(64, 2048)
    hb = inp["hyena_bias"][0]
    hbcol4 = np.zeros((128, 4, 2, 64), np.float32)
    for g in range(4):
        for cp in range(2):
            hbcol4[64 * cp:64 * cp + 64, g, :, :] = hb[:, 128 * g:128 * g + 128][:, cp::2][None, :, :]
    cols2 = np.r_[2048:3072, 3072:6144]
    shared = {
        "w_ada1": f(inp["w_ada"][0][:, 0:2048]), "b_ada1": f(inp["b_ada"][0][0:2048].reshape(16, 128).T),
        "gpre": f(inp["g_pre_mix"][0].reshape(8, 128).T), "w_incA": w_incA, "w_incH": w_incH,
        "maskd": f(M), "hseld": hsel, "identd": np.eye(128, dtype=np.float32),
        "F1cat_h": K["F1cat_h"], "TrTr_h": K["TrTr_h"], "TiTi_h": K["TiTi_h"], "R12": K["R12"], "TT2r_h": K["TT2r_h"], "TT2i_h": K["TT2i_h"],
        "IF1_h": K["IF1_h"], "Swin": K["Swin"], "zembT": K["zembT"],
        "delta4": f(np.broadcast_to(K["deltas"][None, :], (128, 512))),
        "hbrow": f(np.stack([hb[o_, 128 * g_:128 * g_ + 128] for g_ in range(4) for o_ in range(2)]).reshape(1, 1024)),
        "fw1": f(inp["filt_w1"][0]), "fw2": f(inp["filt_w2"][0]), "fw3c4": f(w3),
        "fpar": f(np.stack([inp["filt_b1"][0], inp["filt_freq1"][0], inp["filt_b2"][0], inp["filt_freq2"][0]], axis=1)),
        "cw4": cw4, "cb4": cb4,
        "w_ada2": f(inp["w_ada"][0][:, cols2]), "b_ada2": f(inp["b_ada"][0][cols2][None, :]),
        "gvec": f(np.stack([inp["g_post_mix"][0], inp["g_pre_mlp"][0], inp["g_post_mlp"][0]])),
        "gmix": f(np.concatenate([inp["g_attn_out"][0], inp["g_hyena_out"][0]]).reshape(8, 128).T),
        "w_out": f(inp["w_out"][0]), "w1": f(inp["w_mlp1"][0]), "w2": f(inp["w_mlp2"][0]),
    }
    in_maps = []
    for core in range(8):
        b, j = core // 4, core % 4
        lo = NOWN * j - 1024
        idx = np.arange(lo, lo + NW)
        ok = (idx >= 0) & (idx < NTOK)
        idc = np.clip(idx, 0, NTOK - 1)
        xTw = np.where(ok[None, :], inp["x"][b].T[:, idc], 0.0)
        sel = np.zeros((128, 32), np.float32)
        sel[32 * j + np.arange(32), np.arange(32)] = 1.0
        m = dict(shared)
        m.update({
            "xTw": f(xTw), "validw": f(ok.astype(np.float32).reshape(32, 128).T),
            "ropeCw": f(C[:, idc]), "ropeSw": f(S[:, idc]),
            "xT": f(inp["x"][b].T), "cvec": f(inp["c"][b].reshape(8, 128).T),
            "seld": sel, "x_tok": f(inp["x"][b, NOWN * j:NOWN * (j + 1)]),
        })
        in_maps.append(m)
    return in_maps


def kernel(**inputs):
    inp = {k: np.asarray(v) for k, v in inputs.items()}
    nc = build_fused()
    in_maps = fused_inputs(inp)
    res = run_bass_kernel_spmd(nc, in_maps, core_ids=list(range(8)))
    out = np.zeros((2, NTOK, 1024), np.float32)
    for core in range(8):
        b, j = core // 4, core % 4
        out[b, NOWN * j:NOWN * (j + 1)] = res.results[core]["out"]
    return out
```

```python
import math
import numpy as np
from concourse.bass_utils import run_bass_kernel_spmd

from contextlib import ExitStack
import concourse.bass as bass
import concourse.mybir as mybir

F32 = mybir.dt.float32
BF16 = mybir.dt.bfloat16
ALU = mybir.AluOpType
AF = mybir.ActivationFunctionType
AX = mybir.AxisListType

ENGS = ("pe", "act", "dve", "pool", "sp")


class Prog:
    def __init__(self, nc, sem_stack=None, prefix=""):
        self.nc = nc
        self.ops = []
        self.state = {}
        self.es = ExitStack()
        self.sem_stack = sem_stack if sem_stack is not None else self.es
        self.prefix = prefix
        self.ndma = 0

    def sb(self, name, shape, dt):
        return self.es.enter_context(self.nc.sbuf_tensor(self.prefix + "sb_" + name, list(shape), dt))

    def ps(self, name, shape, dt):
        return self.es.enter_context(self.nc.psum_tensor(self.prefix + "ps_" + name, list(shape), dt))

    def op(self, eng, fn, reads=(), writes=(), dma=False, grp=None):
        deps = set()
        psum_r = [b for b in reads if len(b) > 1 and b[0] == "p" and b[1].isupper()]
        if psum_r:
            reads = [b for b in reads if b not in psum_r]
            writes = list(writes) + [b for b in psum_r if b not in writes]
        for b in reads:
            st = self.state.setdefault(b, [None, []])
            if st[0] is not None:
                deps.add(st[0])
        for b in writes:
            st = self.state.setdefault(b, [None, []])
            if st[0] is not None:
                deps.add(st[0])
            deps.update(st[1])
        idx = len(self.ops)
        if dma and grp is None:
            grp = "dma%d" % (self.ndma % 12)
            self.ndma += 1
        self.ops.append(dict(eng=eng, fn=fn, deps=deps, dma=dma, grp=grp, sig=dma))
        for b in reads:
            self.state[b][1].append(idx)
        for b in writes:
            self.state[b] = [idx, []]
        return idx

    def dma(self, eng, out, in_, reads=(), writes=(), grp=None, key=None):
        if grp is None:
            grp = "dma_" + str(key if key is not None else (reads[0] if reads else writes[0]))
        return self.op(eng, lambda e: e.dma_start(out=out, in_=in_), reads, writes, dma=True, grp=grp)

    def build(self):
        nc = self.nc
        ops = self.ops

        def needs_wait(o, d):
            if d["dma"] or o["dma"]:
                return True
            if o["eng"] != d["eng"]:
                return True
            return o["eng"] != "pe"

        for o in ops:
            for di in o["deps"]:
                if needs_wait(o, ops[di]):
                    ops[di]["sig"] = True
        cnt = {}
        sems = {}

        def getsem(key):
            if key not in sems:
                sems[key] = self.sem_stack.enter_context(nc.semaphore(self.prefix + "s_" + str(key).replace(" ", "")))
            return sems[key]

        for o in ops:
            if not o["sig"]:
                continue
            if o["dma"]:
                key = o["grp"]
                cnt[key] = cnt.get(key, 0) + 16
                o["tok"] = (key, cnt[key], 16)
            else:
                base = "e_" + o["eng"]
                gen = cnt.get(base + "_gen", 0)
                key = "%s_%d" % (base, gen)
                cnt[key] = cnt.get(key, 0) + 1
                o["tok"] = (key, cnt[key], 1)
                if cnt[key] >= 30000:
                    cnt[base + "_gen"] = gen + 1
        for key in list(cnt.keys()):
            if not key.endswith("_gen"):
                getsem(key)
        per = {e: [] for e in ENGS}
        for o in ops:
            per[o["eng"]].append(o)
        final_dma = {k: v for k, v in cnt.items() if k.startswith("dma")}
        self.n_sems = len(sems)

        def replay(eng_name, e):
            waited = {}
            for o in per[eng_name]:
                for di in sorted(o["deps"]):
                    d = ops[di]
                    if not needs_wait(o, d):
                        continue
                    key, val, _ = d["tok"]
                    if waited.get(key, 0) < val:
                        e.wait_ge(sems[key], val)
                        waited[key] = val
                inst = o["fn"](e)
                if o["sig"]:
                    key, val, inc = o["tok"]
                    inst.then_inc(sems[key], inc)
            if eng_name == "sp":
                for key, val in final_dma.items():
                    if waited.get(key, 0) < val:
                        e.wait_ge(sems[key], val)

        with nc.Block() as block:
            @block.sync
            def _(e):
                replay("sp", e)

            @block.tensor
            def _(e):
                replay("pe", e)

            @block.scalar
            def _(e):
                replay("act", e)

            @block.vector
            def _(e):
                replay("dve", e)

            @block.gpsimd
            def _(e):
                replay("pool", e)
        self.es.close()


EPS = 1e-6


def _rs(P, eng_stats, ss, r, n, nm, reads, key):
    P.op("dve", lambda e: e.tensor_scalar(out=r, in0=ss, scalar1=1.0 / n, scalar2=EPS, op0=ALU.mult, op1=ALU.add),
         reads=reads, writes=[key])
    P.op("act", lambda e: e.activation(out=r, in_=r, func=AF.Sqrt), reads=[key], writes=[key])
    P.op("dve", lambda e: e.reciprocal(out=r, in_=r), reads=[key], writes=[key])


def build_phase2(NT=2048):
    nc = bass.Bass("TRN2", target_bir_lowering=False)
    P = Prog(nc)
    D = lambda name, shape, kind="ExternalInput": nc.dram_tensor(name, list(shape), F32, kind=kind).ap()
    x_tok = D("x_tok", [NT, 1024])
    mix_tok = D("mix_tok", [NT, 1024])
    mixT = D("mixT", [1024, NT])
    cvec = D("cvec", [128, 8])
    w_ada = D("w_ada", [1024, 4096])
    b_ada = D("b_ada", [1, 4096])
    gvec = D("gvec", [3, 1024])
    gmix = D("gmix", [128, 8])
    w_out = D("w_out", [1024, 1024])
    w1 = D("w1", [1024, 4096])
    w2 = D("w2", [4096, 1024])
    out = D("out", [NT, 1024], kind="ExternalOutput")
    x1s = D("x1s", [NT, 1024], kind="Internal")

    sb, ps = P.sb, P.ps
    wout_b = sb("wout_b", [128, 8, 1024], BF16)
    mods = sb("mods", [128, 4096], F32)
    sbc = sb("sbc", [128, 8, 128], F32)
    ones = sb("ones", [128, 128], F32)
    ident = sb("ident", [128, 128], BF16)
    identf = sb("identf", [128, 128], F32)
    stage = [sb("stage%d" % i, [128, 2048], F32) for i in range(2)]
    w1b = [sb("w1b%d" % i, [128, 8, 512], BF16) for i in range(2)]
    w2b = [sb("w2b%d" % i, [128, 4, 1024], BF16) for i in range(2)]
    h2T = sb("h2T", [128, 8, 1024], BF16)
    f2acc = sb("f2acc", [128, 8, 1024], F32)
    xt = [sb("xt%d" % i, [128, 1024], F32) for i in range(2)]
    mt = sb("mt", [128, 1024], F32)
    mTs = sb("mTs", [128, 8, 128], F32)
    mTb = [sb("mTb%d" % i, [128, 8, 128], BF16) for i in range(2)]
    ysb = sb("ysb", [128, 1024], F32)
    tmp = sb("tmp", [128, 1024], F32)
    tmp2 = sb("tmp2", [128, 1024], F32)
    h2b = sb("h2b", [128, 1024], BF16)
    rl = [sb("rl%d" % i, [128, 512], F32) for i in range(2)]
    aT = [sb("aT%d" % i, [128, 4, 512], BF16) for i in range(2)]
    small = sb("small", [128, 32], F32)
    csb = sb("csb", [128, 8], F32)
    gmx = sb("gmx", [128, 8], F32)
    brow = sb("brow", [1, 4096], F32)

    pA = ps("pA", [128, 1024], F32)
    pH = ps("pH", [128, 1024], F32)
    pT = ps("pT", [128, 1024], BF16)
    pM = [ps("pM%d" % i, [128, 512], F32) for i in range(2)]

    P.op("pool", lambda e: e.memset(ones[:, :], 1.0), writes=["ones"])
    P.op("pool", lambda e: e.memset(identf[:, :], 0.0), writes=["identf"])
    identd = D("identd", [128, 128])
    P.dma("sp", identf[:, :], identd[:, :], writes=["identf"])
    P.op("dve", lambda e: e.tensor_copy(out=ident[:, :], in_=identf[:, :]), reads=["identf"], writes=["ident"])
    P.dma("sp", csb[:, :], cvec[:, :], writes=["csb"])
    P.dma("sp", gmx[:, :], gmix[:, :], writes=["gmx"])
    P.dma("sp", brow[:, :], b_ada[:, :], writes=["brow"])
    P.op("act", lambda e: e.activation(out=csb[:, :], in_=csb[:, :], func=AF.Silu), reads=["csb"], writes=["csb"])
    for k in range(8):
        P.op("dve", lambda e, k=k: e.tensor_scalar(out=sbc[:, k, :], in0=ones[:, :], scalar1=csb[:, k:k + 1],
                                                    scalar2=None, op0=ALU.mult), reads=["ones", "csb"], writes=["sbc"])
    wa = w_ada.rearrange("(k p) n -> p k n", p=128)
    for blk in range(16):
        st = stage[blk % 2]
        skey = "stage%d" % (blk % 2)
        stv = st[:, :].rearrange("p (k n) -> p k n", k=8)
        P.dma("sp", stv, wa[:, :, blk * 256:(blk + 1) * 256], writes=[skey])
        pm = pM[blk % 2]
        pkey = "pM%d" % (blk % 2)
        for k in range(8):
            P.op("pe", lambda e, k=k, pm=pm, stv=stv: e.matmul(pm[:, 0:256], lhsT=sbc[:, k, :], rhs=stv[:, k, :],
                                                               start=(k == 0), stop=False),
                 reads=["sbc", skey], writes=[pkey])
        P.op("pe", lambda e, pm=pm, blk=blk: e.matmul(pm[:, 0:256], lhsT=ones[0:1, :], rhs=brow[0:1, blk * 256:(blk + 1) * 256],
                                                     start=False, stop=True), reads=["ones", "brow"], writes=[pkey])
        P.op("act", lambda e, pm=pm, blk=blk: e.activation(out=mods[:, blk * 256:(blk + 1) * 256], in_=pm[:, 0:256], func=AF.Copy),
             reads=[pkey], writes=["mods"])
    gb = stage[0][:, :]
    P.dma("sp", gb[:, 0:1024], gvec[0:1, :].to_broadcast((128, 1024)), writes=["stage0"])
    gb1 = stage[1][:, :]
    P.dma("sp", gb1[:, 0:1024], gvec[1:2, :].to_broadcast((128, 1024)), writes=["stage1"])
    P.dma("sp", gb1[:, 1024:2048], gvec[2:3, :].to_broadcast((128, 1024)), writes=["stage1"])
    G1, SH2, G2, G3 = mods[:, 0:1024], mods[:, 1024:2048], mods[:, 2048:3072], mods[:, 3072:4096]
    P.op("dve", lambda e: e.tensor_tensor(out=G1, in0=G1, in1=gb[:, 0:1024], op=ALU.mult), reads=["mods", "stage0"], writes=["mods"])
    P.op("dve", lambda e: e.scalar_tensor_tensor(out=G2, in0=G2, scalar=1.0, in1=gb1[:, 0:1024], op0=ALU.add, op1=ALU.mult),
         reads=["mods", "stage1"], writes=["mods"])
    P.op("dve", lambda e: e.tensor_tensor(out=G3, in0=G3, in1=gb1[:, 1024:2048], op=ALU.mult), reads=["mods", "stage1"], writes=["mods"])
    wo = w_out.rearrange("(k p) n -> p k n", p=128)
    for c4 in range(4):
        st = stage[c4 % 2]
        skey = "stage%d" % (c4 % 2)
        stv = st[:, :].rearrange("p (k n) -> p k n", k=2)
        P.dma("sp", stv, wo[:, 2 * c4:2 * c4 + 2, :], writes=[skey])
        for kk in range(2):
            k = 2 * c4 + kk
            P.op("dve" if kk == 0 else "pool", lambda e, k=k, kk=kk, stv=stv: e.tensor_scalar(
                out=wout_b[:, k, :], in0=stv[:, kk, :], scalar1=gmx[:, k:k + 1], scalar2=None, op0=ALU.mult),
                reads=[skey, "gmx"], writes=["wout_b"])

    def sumsq(src, dst, reads, key, eng="dve"):
        w = src.shape[-1]
        P.op("pool", lambda e: e.tensor_tensor(out=tmp2[:, 0:w], in0=src, in1=src, op=ALU.mult), reads=reads, writes=["tmp2"])
        P.op(eng, lambda e: e.tensor_reduce(out=dst, in_=tmp2[:, 0:w], axis=AX.X, op=ALU.add), reads=["tmp2"] + list(reads), writes=[key])

    nsm = [0]

    def smallcol(n=1):
        c = nsm[0] % 16
        nsm[0] += 1
        return small[:, 2 * c:2 * c + n], "small%d" % c

    w1v = w1.rearrange("(k p) n -> p k n", p=128)
    w2v = w2.rearrange("(c p) n -> p c n", p=128)
    mTv = mixT.rearrange("(k p) t -> p k t", p=128)
    it = [0]
    for half in range(NT // 1024):
        for tile in range(8):
            t0 = half * 1024 + tile * 128
            i = it[0]
            it[0] += 1
            X, xk = xt[i % 2], "xt%d" % (i % 2)
            MB, mbk = mTb[i % 2], "mTb%d" % (i % 2)
            P.dma("sp", X[:, :], x_tok[t0:t0 + 128, :], writes=[xk])
            P.dma("sp", mt[:, :], mix_tok[t0:t0 + 128, :], writes=["mt"])
            P.dma("sp", mTs[:, :, :], mTv[:, :, t0:t0 + 128], writes=["mTs"])
            P.op("pool", lambda e, MB=MB: e.tensor_copy(out=MB[:, :, :], in_=mTs[:, :, :]), reads=["mTs"], writes=[mbk])
            ss, ssk = smallcol(2)
            rr, rrk = smallcol(2)
            sumsq(mt[:, 0:512], ss[:, 0:1], ["mt"], ssk)
            sumsq(mt[:, 512:1024], ss[:, 1:2], ["mt", ssk], ssk)
            _rs(P, None, ss, rr, 512.0, "rmix", [ssk], rrk)
            for nh in range(2):
                for k in range(4):
                    P.op("pe", lambda e, k=k, nh=nh, MB=MB: e.matmul(pA[:, nh * 512:(nh + 1) * 512], lhsT=MB[:, k, :],
                                                                     rhs=wout_b[:, k, nh * 512:(nh + 1) * 512],
                                                                     start=(k == 0), stop=(k == 3)),
                         reads=[mbk, "wout_b"], writes=["pA%d" % nh])
                for k in range(4, 8):
                    P.op("pe", lambda e, k=k, nh=nh, MB=MB: e.matmul(pH[:, nh * 512:(nh + 1) * 512], lhsT=MB[:, k, :],
                                                                     rhs=wout_b[:, k, nh * 512:(nh + 1) * 512],
                                                                     start=(k == 4), stop=(k == 7)),
                         reads=[mbk, "wout_b"], writes=["pH%d" % nh])
            P.op("act", lambda e, rr=rr: e.activation(out=ysb[:, :], in_=pA[:, :], func=AF.Copy, scale=rr[:, 0:1]),
                 reads=["pA0", "pA1", rrk], writes=["ysb"])
            P.op("dve", lambda e, rr=rr: e.scalar_tensor_tensor(out=ysb[:, :], in0=pH[:, :], scalar=rr[:, 1:2], in1=ysb[:, :],
                                                                op0=ALU.mult, op1=ALU.add),
                 reads=["pH0", "pH1", rrk, "ysb"], writes=["ysb"])
            ss2, ss2k = smallcol(1)
            r2, r2k = smallcol(1)
            sumsq(ysb[:, :], ss2[:, 0:1], ["ysb"], ss2k)
            _rs(P, None, ss2, r2, 1024.0, "ry", [ss2k], r2k)
            P.op("dve", lambda e, r2=r2: e.scalar_tensor_tensor(out=tmp[:, :], in0=ysb[:, :], scalar=r2[:, 0:1], in1=G1,
                                                                op0=ALU.mult, op1=ALU.mult),
                 reads=["ysb", r2k, "mods"], writes=["tmp"])
            P.op("pool", lambda e, X=X: e.tensor_tensor(out=X[:, :], in0=tmp[:, :], in1=X[:, :], op=ALU.add),
                 reads=["tmp", xk], writes=[xk])
            P.dma("pool", x1s[t0:t0 + 128, :], X[:, :], reads=[xk], writes=["x1s%d" % (t0 // 128)])
            ss3, ss3k = smallcol(1)
            r3, r3k = smallcol(1)
            sumsq(X[:, :], ss3[:, 0:1], [xk], ss3k)
            _rs(P, None, ss3, r3, 1024.0, "r1", [ss3k], r3k)
            P.op("dve", lambda e, r3=r3, X=X: e.scalar_tensor_tensor(out=tmp[:, :], in0=X[:, :], scalar=r3[:, 0:1], in1=G2,
                                                                     op0=ALU.mult, op1=ALU.mult),
                 reads=[xk, r3k, "mods"], writes=["tmp"])
            P.op("pool", lambda e: e.tensor_tensor(out=h2b[:, :], in0=tmp[:, :], in1=SH2, op=ALU.add),
                 reads=["tmp", "mods"], writes=["h2b"])
            for k in range(8):
                P.op("pe", lambda e, k=k: e.transpose(out=pT[:, k * 128:(k + 1) * 128], in_=h2b[:, k * 128:(k + 1) * 128],
                                                      identity=ident[:, :]), reads=["h2b", "ident"], writes=["pT"])
            P.op("act", lambda e, tile=tile: e.activation(out=h2T[:, :, tile * 128:(tile + 1) * 128],
                                                          in_=pT[:, :].rearrange("p (k t) -> p k t", k=8), func=AF.Copy),
                 reads=["pT"], writes=["h2T"])
        for j in range(8):
            W1B, w1k = w1b[j % 2], "w1b%d" % (j % 2)
            W2B, w2k = w2b[j % 2], "w2b%d" % (j % 2)
            for hh in range(2):
                st, skey = stage[hh], "stage%d" % hh
                stv = st[:, :].rearrange("p (k n) -> p k n", k=8)
                P.dma("sp", stv, w1v[:, :, j * 512 + hh * 256:j * 512 + (hh + 1) * 256], writes=[skey])
                P.op("dve" if hh == 0 else "pool", lambda e, W1B=W1B, stv=stv, hh=hh: e.tensor_copy(
                    out=W1B[:, :, hh * 256:(hh + 1) * 256], in_=stv), reads=[skey], writes=[w1k])
            for hh in range(2):
                st, skey = stage[hh], "stage%d" % hh
                stv = st[:, :].rearrange("p (c n) -> p c n", c=2)
                P.dma("sp", stv, w2v[:, j * 4 + hh * 2:j * 4 + hh * 2 + 2, :], writes=[skey])
                P.op("dve" if hh == 0 else "pool", lambda e, W2B=W2B, stv=stv, hh=hh: e.tensor_copy(
                    out=W2B[:, hh * 2:hh * 2 + 2, :], in_=stv), reads=[skey], writes=[w2k])
            for tg in range(2):
                AT, atk = aT[tg], "aT%d" % tg
                for hc in range(4):
                    pm, pkey = pM[hc % 2], "pM%d" % (hc % 2)
                    RL, rlk = rl[hc % 2], "rl%d" % (hc % 2)
                    for k in range(8):
                        P.op("pe", lambda e, k=k, hc=hc, tg=tg, pm=pm, W1B=W1B: e.matmul(
                            pm[:, :], lhsT=W1B[:, k, hc * 128:(hc + 1) * 128], rhs=h2T[:, k, tg * 512:(tg + 1) * 512],
                            start=(k == 0), stop=(k == 7)), reads=[w1k, "h2T"], writes=[pkey])
                    P.op("act", lambda e, pm=pm, RL=RL: e.activation(out=RL[:, :], in_=pm[:, :], func=AF.Relu),
                         reads=[pkey], writes=[rlk])
                    P.op("pool", lambda e, RL=RL, AT=AT, hc=hc: e.tensor_tensor(out=AT[:, hc, :], in0=RL[:, :], in1=RL[:, :], op=ALU.mult),
                         reads=[rlk], writes=[atk])
                for tt in range(4):
                    tile = tg * 4 + tt
                    for nh in range(2):
                        pp, ppk = (pA, "pA%d" % nh) if tt % 2 == 0 else (pH, "pH%d" % nh)
                        for hc in range(4):
                            P.op("pe", lambda e, hc=hc, tt=tt, nh=nh, pp=pp, AT=AT, W2B=W2B: e.matmul(
                                pp[:, nh * 512:(nh + 1) * 512], lhsT=AT[:, hc, tt * 128:(tt + 1) * 128],
                                rhs=W2B[:, hc, nh * 512:(nh + 1) * 512], start=(hc == 0), stop=(hc == 3)),
                                reads=[atk, w2k], writes=[ppk])
                        fv = f2acc[:, tile, nh * 512:(nh + 1) * 512]
                        fk = "f2_%d_%d" % (tile, nh)
                        if j == 0:
                            P.op("dve", lambda e, fv=fv, pp=pp, nh=nh: e.tensor_copy(out=fv, in_=pp[:, nh * 512:(nh + 1) * 512]),
                                 reads=[ppk], writes=[fk])
                        else:
                            P.op("dve", lambda e, fv=fv, pp=pp, nh=nh: e.tensor_tensor(out=fv, in0=pp[:, nh * 512:(nh + 1) * 512], in1=fv, op=ALU.add),
                                 reads=[ppk, fk], writes=[fk])
        for tile in range(8):
            t0 = half * 1024 + tile * 128
            i = it[0]
            it[0] += 1
            X, xk = xt[i % 2], "xt%d" % (i % 2)
            P.dma("sp", X[:, :], x1s[t0:t0 + 128, :], reads=["x1s%d" % (t0 // 128)], writes=[xk], key=xk)
            fkeys = ["f2_%d_%d" % (tile, nh) for nh in range(2)]
            ss4, ss4k = smallcol(1)
            r4, r4k = smallcol(1)
            sumsq(f2acc[:, tile, :], ss4[:, 0:1], fkeys, ss4k)
            _rs(P, None, ss4, r4, 1024.0, "rf", [ss4k], r4k)
            P.op("dve", lambda e, r4=r4, tile=tile: e.scalar_tensor_tensor(out=tmp[:, :], in0=f2acc[:, tile, :], scalar=r4[:, 0:1], in1=G3,
                                                                          op0=ALU.mult, op1=ALU.mult),
                 reads=fkeys + [r4k, "mods"], writes=["tmp"])
            P.op("pool", lambda e, X=X: e.tensor_tensor(out=X[:, :], in0=tmp[:, :], in1=X[:, :], op=ALU.add),
                 reads=["tmp", xk], writes=[xk])
            P.dma("pool", out[t0:t0 + 128, :], X[:, :], reads=[xk], writes=["out%d" % (t0 // 128)])
    P.build()
    return nc


def run_phase2(inp, mixed):
    f = lambda a: np.ascontiguousarray(a, dtype=np.float32)
    nc = build_phase2()
    in_maps = []
    cols = np.r_[2048:3072, 3072:6144]
    for core in range(8):
        b, j = core // 4, core % 4
        sl = slice(j * 2048, (j + 1) * 2048)
        in_maps.append({
            "x_tok": f(inp["x"][b, sl]),
            "mix_tok": f(mixed[b, sl]),
            "mixT": f(mixed[b, sl].T),
            "cvec": f(inp["c"][b].reshape(8, 128).T),
            "w_ada": f(inp["w_ada"][0][:, cols]),
            "b_ada": f(inp["b_ada"][0][cols][None, :]),
            "gvec": f(np.stack([inp["g_post_mix"][0], inp["g_pre_mlp"][0], inp["g_post_mlp"][0]])),
            "gmix": f(np.concatenate([inp["g_attn_out"][0], inp["g_hyena_out"][0]]).reshape(8, 128).T),
            "w_out": f(inp["w_out"][0]),
            "w1": f(inp["w_mlp1"][0]),
            "w2": f(inp["w_mlp2"][0]),
            "identd": np.eye(128, dtype=np.float32),
        })
    res = run_bass_kernel_spmd(nc, in_maps, core_ids=list(range(8)))
    out = np.zeros((2, 8192, 1024), np.float32)
    for core in range(8):
        b, j = core // 4, core % 4
        out[b, j * 2048:(j + 1) * 2048] = res.results[core]["out"]
    return out


ST = 512
NTOK = 8192
NST = NTOK // ST


def _p1_common(P, nc, ncols_w, ST=512, SW=256):
    D = lambda name, shape, kind="ExternalInput": nc.dram_tensor(name, list(shape), F32, kind=kind).ap()
    H = {}
    H["xT"] = D("xT", [1024, NTOK])
    cvec = D("cvec", [128, 8])
    w_ada1 = D("w_ada1", [1024, 2048])
    b_ada1 = D("b_ada1", [128, 16])
    gpre = D("gpre", [128, 8])
    w_inc = D("w_inc", [1024, ncols_w])
    sb, ps = P.sb, P.ps
    H["wb"] = wb = sb("wb", [128, 8, ncols_w], BF16)
    stage = [sb("stage%d" % i, [128, 8, SW], F32) for i in range(2)]
    H["ST"] = ST
    H["stage"] = stage
    csb = sb("csb", [128, 8], F32)
    gp = sb("gp", [128, 8], F32)
    bsb = sb("bsb", [128, 16], F32)
    H["modc"] = modc = sb("modc", [128, 16], F32)
    H["G0"] = G0 = sb("G0", [128, 8], F32)
    H["onesb"] = onesb = sb("onesb", [128, 128], BF16)
    H["xs"] = [sb("xs%d" % i, [128, 8, ST], F32) for i in range(2)]
    H["sq"] = sb("sq", [128, 8, ST], BF16)
    H["hT"] = sb("hT", [128, 8, ST], BF16)
    H["rbc"] = sb("rbc", [128, ST], F32)
    H["pSS"] = pSS = ps("pSS", [128, 512], F32)

    P.op("pool", lambda e: e.memset(onesb[:, :], 1.0), writes=["onesb"])
    P.dma("sp", csb[:, :], cvec[:, :], writes=["csb"])
    P.dma("sp", gp[:, :], gpre[:, :], writes=["gp"])
    P.dma("sp", bsb[:, :], b_ada1[:, :], writes=["bsb"])
    P.op("act", lambda e: e.activation(out=csb[:, :], in_=csb[:, :], func=AF.Silu), reads=["csb"], writes=["csb"])
    wa = w_ada1.rearrange("(k p) n -> p k n", p=128)
    for blk in range(2048 // SW):
        st, skey = stage[blk % 2], "stage%d" % (blk % 2)
        P.dma("sp", st[:, :, :], wa[:, :, blk * SW:(blk + 1) * SW], writes=[skey])
        for jj in range(SW // 128):
            j = (SW // 128) * blk + jj
            for k in range(8):
                P.op("pe", lambda e, k=k, j=j, jj=jj, st=st: e.matmul(pSS[:, j:j + 1], lhsT=st[:, k, jj * 128:(jj + 1) * 128],
                                                                      rhs=csb[:, k:k + 1], start=(k == 0), stop=(k == 7)),
                     reads=[skey, "csb"], writes=["pSS"])
    P.op("dve", lambda e: e.tensor_tensor(out=modc[:, :], in0=pSS[:, 0:16], in1=bsb[:, :], op=ALU.add),
         reads=["pSS", "bsb"], writes=["modc"])
    P.op("dve", lambda e: e.scalar_tensor_tensor(out=G0[:, :], in0=modc[:, 8:16], scalar=1.0, in1=gp[:, :], op0=ALU.add, op1=ALU.mult),
         reads=["modc", "gp"], writes=["G0"])
    wv = w_inc.rearrange("(k p) n -> p k n", p=128)
    nb = ncols_w // SW
    for blk in range(nb):
        st, skey = stage[blk % 2], "stage%d" % (blk % 2)
        P.dma("sp", st[:, :, :], wv[:, :, blk * SW:(blk + 1) * SW], writes=[skey])
        P.op("dve" if blk % 2 == 0 else "pool", lambda e, st=st, blk=blk: e.tensor_copy(out=wb[:, :, blk * SW:(blk + 1) * SW], in_=st[:, :, :]),
             reads=[skey], writes=["wb"])
    return H


def _p1_load(P, H, st):
    ST = H["ST"]
    xs, xk = H["xs"][st % 2], "xs%d" % (st % 2)
    xTv = H["xT"].rearrange("(k p) t -> p k t", p=128)
    P.dma("sp", xs[:, :, :], xTv[:, :, st * ST:(st + 1) * ST], writes=[xk])


def _p1_norm(P, H, st):
    ST = H["ST"]
    xs, xk = H["xs"][st % 2], "xs%d" % (st % 2)
    sq, hT, rbc, pSS, onesb, modc, G0 = H["sq"], H["hT"], H["rbc"], H["pSS"], H["onesb"], H["modc"], H["G0"]
    P.op("act", lambda e: e.activation(out=sq[:, :, :], in_=xs[:, :, :], func=AF.Square), reads=[xk], writes=["sq"])
    for k in range(8):
        P.op("pe", lambda e, k=k: e.matmul(pSS[:, 0:ST], lhsT=onesb[:, :], rhs=sq[:, k, :], start=(k == 0), stop=(k == 7)),
             reads=["onesb", "sq"], writes=["pSS"])
    P.op("dve", lambda e: e.tensor_scalar(out=rbc[:, :], in0=pSS[:, 0:ST], scalar1=1.0 / 1024, scalar2=EPS, op0=ALU.mult, op1=ALU.add),
         reads=["pSS"], writes=["rbc"])
    P.op("act", lambda e: e.activation(out=rbc[:, :], in_=rbc[:, :], func=AF.Sqrt), reads=["rbc"], writes=["rbc"])
    P.op("dve", lambda e: e.reciprocal(out=rbc[:, :], in_=rbc[:, :]), reads=["rbc"], writes=["rbc"])
    for k in range(8):
        P.op("dve", lambda e, k=k: e.scalar_tensor_tensor(out=xs[:, k, :], in0=xs[:, k, :], scalar=G0[:, k:k + 1], in1=rbc[:, :],
                                                          op0=ALU.mult, op1=ALU.mult), reads=[xk, "G0", "rbc"], writes=[xk])
        P.op("act", lambda e, k=k: e.activation(out=hT[:, k, :], in_=xs[:, k, :], func=AF.Identity, bias=modc[:, k:k + 1], scale=1.0),
             reads=[xk, "modc"], writes=["hT"])


def build_attn():
    nc = bass.Bass("TRN2", target_bir_lowering=False)
    P = Prog(nc)
    D = lambda name, shape, kind="ExternalInput": nc.dram_tensor(name, list(shape), F32, kind=kind).ap()
    H = _p1_common(P, nc, 768)
    ropeC = D("ropeC", [128, NTOK])
    ropeS = D("ropeS", [128, NTOK])
    maskd = D("maskd", [128, 17 * 128])
    hseld = D("hseld", [128, 64])
    attn_o = D("attn_o", [NTOK, 128], kind="ExternalOutput")
    sb, ps = P.sb, P.ps
    wb, hT = H["wb"], H["hT"]
    QT = sb("QT", [128, NTOK], BF16)
    KT = sb("KT", [128, NTOK], BF16)
    Vaug = sb("Vaug", [128, 64, 2, 65], BF16)
    Mall = sb("Mall", [128, 17 * 128], BF16)
    mst = sb("mst", [128, 17 * 128], F32)
    hself = sb("hself", [128, 64], F32)
    hsel = sb("hsel", [128, 64], BF16)
    onesrow = sb("onesrow", [64, 128], BF16)
    rc = [sb("rc%d" % i, [128, ST], F32) for i in range(2)]
    rs_ = [sb("rs%d" % i, [128, ST], F32) for i in range(2)]
    t1 = sb("t1", [128, ST], F32)
    t2 = sb("t2", [128, ST], F32)
    sqk = sb("sqk", [128, ST], BF16)
    kmx = sb("kmx", [64, 2], F32)
    qn = sb("qn", [64, 128], F32)
    negm = sb("negm", [64, 128], BF16)
    PT = [sb("PT%d" % i, [128, 512], BF16) for i in range(2)]
    ao = [sb("ao%d" % i, [128, 128], F32) for i in range(2)]
    rec = sb("rec", [128, 4], F32)
    pA = ps("pA", [128, 512], F32)
    pB = ps("pB", [128, 512], F32)
    pV = ps("pV", [128, 512], F32)
    pN = H["pSS"]
    pS = [ps("pS%d" % i, [128, 512], F32) for i in range(2)]
    pO = [ps("pO%d" % i, [128, 2, 128], F32) for i in range(2)]

    P.dma("sp", mst[:, :], maskd[:, :], writes=["mst"])
    P.op("pool", lambda e: e.tensor_copy(out=Mall[:, :], in_=mst[:, :]), reads=["mst"], writes=["Mall"])
    P.dma("sp", hself[:, :], hseld[:, :], writes=["hself"])
    P.op("pool", lambda e: e.tensor_copy(out=hsel[:, :], in_=hself[:, :]), reads=["hself"], writes=["hsel"])
    P.op("pool", lambda e: e.memset(onesrow[:, :], 1.0), writes=["onesrow"])
    P.op("pool", lambda e: e.memset(kmx[:, :], 0.0), writes=["kmx"])
    P.op("pool", lambda e: e.memset(Vaug[:, :, :, 64:65], 1.0), writes=["Vaug"])

    _p1_load(P, H, 0)
    for st in range(NST):
        if st + 1 < NST:
            _p1_load(P, H, st + 1)
        C, ck = rc[st % 2], "rc%d" % (st % 2)
        S, sk = rs_[st % 2], "rs%d" % (st % 2)
        P.dma("sp", C[:, :], ropeC[:, st * ST:(st + 1) * ST], writes=[ck])
        P.dma("sp", S[:, :], ropeS[:, st * ST:(st + 1) * ST], writes=[sk])
        _p1_norm(P, H, st)
        for which, dst in ((0, QT), (1, KT)):
            dk = "QT" if which == 0 else "KT"
            c0 = which * 256
            for k in range(8):
                P.op("pe", lambda e, k=k, c0=c0: e.matmul(pA[:, :], lhsT=wb[:, k, c0:c0 + 128], rhs=hT[:, k, :], start=(k == 0), stop=(k == 7)),
                     reads=["wb", "hT"], writes=["pA"])
            for k in range(8):
                P.op("pe", lambda e, k=k, c0=c0: e.matmul(pB[:, :], lhsT=wb[:, k, c0 + 128:c0 + 256], rhs=hT[:, k, :], start=(k == 0), stop=(k == 7)),
                     reads=["wb", "hT"], writes=["pB"])
            P.op("dve", lambda e, C=C: e.tensor_tensor(out=t1[:, :], in0=pA[:, :], in1=C[:, :], op=ALU.mult), reads=["pA", ck], writes=["t1"])
            P.op("dve", lambda e, S=S: e.tensor_tensor(out=t2[:, :], in0=pB[:, :], in1=S[:, :], op=ALU.mult), reads=["pB", sk], writes=["t2"])
            P.op("pool", lambda e, dst=dst, st=st: e.tensor_tensor(out=dst[:, st * ST:(st + 1) * ST], in0=t1[:, :], in1=t2[:, :], op=ALU.add),
                 reads=["t1", "t2"], writes=[dk])
        P.op("pool", lambda e, st=st: e.tensor_tensor(out=sqk[:, :], in0=KT[:, st * ST:(st + 1) * ST], in1=KT[:, st * ST:(st + 1) * ST], op=ALU.mult),
             reads=["KT"], writes=["sqk"])
        P.op("pe", lambda e: e.matmul(pN[0:64, :], lhsT=hsel[:, :], rhs=sqk[:, :], start=True, stop=True), reads=["hsel", "sqk"], writes=["pSS"])
        P.op("dve", lambda e: e.tensor_reduce(out=kmx[:, 1:2], in_=pN[0:64, :], axis=AX.X, op=ALU.max), reads=["pSS", "kmx"], writes=["kmx"])
        P.op("dve", lambda e: e.tensor_tensor(out=kmx[:, 0:1], in0=kmx[:, 0:1], in1=kmx[:, 1:2], op=ALU.max), reads=["kmx"], writes=["kmx"])
        for tt in range(4):
            for k in range(8):
                P.op("pe", lambda e, k=k, tt=tt: e.matmul(pV[:, tt * 128:(tt + 1) * 128], lhsT=hT[:, k, tt * 128:(tt + 1) * 128],
                                                          rhs=wb[:, k, 512:640], start=(k == 0), stop=(k == 7)),
                     reads=["wb", "hT"], writes=["pV"])
        P.op("act", lambda e, st=st: e.activation(out=Vaug[:, st * 4:(st + 1) * 4, :, 0:64],
                                                  in_=pV[:, :].rearrange("p (t h d) -> p t h d", t=4, h=2), func=AF.Copy),
             reads=["pV"], writes=["Vaug"])
    P.op("act", lambda e: e.activation(out=kmx[:, 0:1], in_=kmx[:, 0:1], func=AF.Sqrt), reads=["kmx"], writes=["kmx"])
    cnt = [0]
    for jb in range(64):
        qs = slice(jb * 128, (jb + 1) * 128)
        P.op("pool", lambda e, qs=qs: e.tensor_tensor(out=sqk[:, 0:128], in0=QT[:, qs], in1=QT[:, qs], op=ALU.mult), reads=["QT"], writes=["sqk"])
        P.op("pe", lambda e: e.matmul(pN[0:64, 0:128], lhsT=hsel[:, :], rhs=sqk[:, 0:128], start=True, stop=True), reads=["hsel", "sqk"], writes=["pSS"])
        P.op("act", lambda e: e.activation(out=qn[:, :], in_=pN[0:64, 0:128], func=AF.Sqrt), reads=["pSS"], writes=["qn"])
        P.op("dve", lambda e: e.tensor_scalar(out=negm[:, :], in0=qn[:, :], scalar1=kmx[:, 0:1], scalar2=-1.0, op0=ALU.mult, op1=ALU.mult),
             reads=["qn", "kmx"], writes=["negm"])
        AO, aok = ao[jb % 2], "ao%d" % (jb % 2)
        PO, pok = pO[jb % 2], "pO%d" % (jb % 2)
        for h in range(2):
            hs = slice(64 * h, 64 * h + 64)
            dms = [dm for dm in range(-8, 9) if 0 <= jb + dm < 64]
            groups = [dms[i:i + 4] for i in range(0, len(dms), 4)]
            nmm = 0
            for grp in groups:
                g = cnt[0]
                cnt[0] += 1
                psx, psk = pS[g % 2], "pS%d" % (g % 2)
                ptx, ptk = PT[g % 2], "PT%d" % (g % 2)
                n = len(grp)
                for i, dm in enumerate(grp):
                    kc = jb + dm
                    P.op("pe", lambda e, i=i, kc=kc, hs=hs, qs=qs, psx=psx: e.matmul(psx[:, i * 128:(i + 1) * 128], lhsT=KT[hs, kc * 128:(kc + 1) * 128],
                                                                                      rhs=QT[hs, qs], start=True, stop=False),
                         reads=["KT", "QT"], writes=[psk])
                    P.op("pe", lambda e, i=i, h=h, psx=psx: e.matmul(psx[:, i * 128:(i + 1) * 128], lhsT=onesrow[32 * h:32 * h + 1, :],
                                                                     rhs=negm[32 * h:32 * h + 1, :], start=False, stop=True),
                         reads=["onesrow", "negm"], writes=[psk])
                P.op("act", lambda e, psx=psx, ptx=ptx, n=n: e.activation(out=ptx[:, 0:n * 128], in_=psx[:, 0:n * 128], func=AF.Exp, scale=0.125),
                     reads=[psk], writes=[ptk])
                m0 = (grp[0] + 8) * 128
                P.op("dve" if g % 2 == 0 else "pool", lambda e, ptx=ptx, n=n, m0=m0: e.tensor_tensor(
                    out=ptx[:, 0:n * 128], in0=ptx[:, 0:n * 128], in1=Mall[:, m0:m0 + n * 128], op=ALU.mult),
                    reads=[ptk, "Mall"], writes=[ptk])
                for i, dm in enumerate(grp):
                    kc = jb + dm
                    P.op("pe", lambda e, i=i, kc=kc, h=h, ptx=ptx, PO=PO, first=(nmm == 0), last=(nmm == len(dms) - 1): e.matmul(
                        PO[:, h, 0:65], lhsT=ptx[:, i * 128:(i + 1) * 128], rhs=Vaug[:, kc, h, :], start=first, stop=last),
                        reads=[ptk, "Vaug"], writes=[pok])
                    nmm += 1
            P.op("dve", lambda e, h=h, PO=PO: e.reciprocal(out=rec[:, h:h + 1], in_=PO[:, h, 64:65]), reads=[pok], writes=["rec%d" % h])
            P.op("dve", lambda e, h=h, PO=PO, AO=AO: e.tensor_scalar(out=AO[:, 64 * h:64 * h + 64], in0=PO[:, h, 0:64], scalar1=rec[:, h:h + 1],
                                                                     scalar2=None, op0=ALU.mult),
                 reads=[pok, "rec%d" % h], writes=[aok])
        P.dma("pool", attn_o[qs, :], AO[:, :], reads=[aok], writes=["attn_o%d" % jb])
    P.build()
    return nc


def _rope_tables():
    half = 32
    inv = (10000.0 ** (-np.arange(half, dtype=np.float32) / half)).astype(np.float32)
    pos = np.arange(NTOK, dtype=np.float32)
    ang = (pos[:, None] * inv[None, :]).astype(np.float32)
    cos = np.cos(ang).astype(np.float32).T
    sin = np.sin(ang).astype(np.float32).T
    C = np.concatenate([cos, cos, cos, cos], axis=0)
    S = np.concatenate([-sin, sin, -sin, sin], axis=0)
    return np.ascontiguousarray(C), np.ascontiguousarray(S)


def _attn_masks():
    o = np.arange(-8 * 128 - 127, 8 * 128 + 128)
    mult = ((np.abs(o) <= 64).astype(np.float32) + ((np.abs(o) <= 256) & (o % 4 == 0)).astype(np.float32)
            + ((np.abs(o) <= 1024) & (o % 16 == 0)).astype(np.float32))
    off0 = -(8 * 128 + 127)
    M = np.zeros((128, 17, 128), np.float32)
    kl = np.arange(128)[:, None]
    ql = np.arange(128)[None, :]
    for i, dm in enumerate(range(-8, 9)):
        M[:, i, :] = mult[(128 * dm + kl - ql) - off0]
    return M.reshape(128, 17 * 128)


def _perm_cols(w):
    w = w.reshape(w.shape[0], -1, 2, 32)
    return w[:, :, ::-1, :].reshape(w.shape[0], -1)


def run_attn(inp):
    f = lambda a: np.ascontiguousarray(a, dtype=np.float32)
    nc = build_attn()
    C, S = _rope_tables()
    M = _attn_masks()
    hsel = np.zeros((128, 64), np.float32)
    hsel[0:64, 0] = 1.0
    hsel[64:128, 32] = 1.0
    w_in = inp["w_in"][0]
    in_maps = []
    for core in range(8):
        b, g = core // 4, core % 4
        cs = slice(128 * g, 128 * g + 128)
        wq, wk, wv = w_in[:, 0:512][:, cs], w_in[:, 512:1024][:, cs], w_in[:, 1024:1536][:, cs]
        w_inc = np.concatenate([wq, _perm_cols(wq), wk, _perm_cols(wk), wv, np.zeros((1024, 128), np.float32)], axis=1)
        in_maps.append({
            "xT": f(inp["x"][b].T), "cvec": f(inp["c"][b].reshape(8, 128).T),
            "w_ada1": f(inp["w_ada"][0][:, 0:2048]), "b_ada1": f(inp["b_ada"][0][0:2048].reshape(16, 128).T),
            "gpre": f(inp["g_pre_mix"][0].reshape(8, 128).T), "w_inc": f(w_inc),
            "ropeC": C, "ropeS": S, "maskd": f(M), "hseld": hsel,
        })
    res = run_bass_kernel_spmd(nc, in_maps, core_ids=list(range(8)))
    attn = np.zeros((2, NTOK, 512), np.float32)
    for core in range(8):
        b, g = core // 4, core % 4
        attn[b, :, 128 * g:128 * g + 128] = res.results[core]["attn_o"]
    return attn


NFFT = 16384
NPQ = 4
NPQH = 8


def _hy_consts():
    N = NFFT
    a = np.arange(128)[:, None].astype(np.float64)
    k1 = np.arange(256)[None, :].astype(np.float64)
    F1cat = np.concatenate([np.cos(2 * np.pi * a * k1 / 256), -np.sin(2 * np.pi * a * k1 / 256)], 1)
    th = 2 * np.pi / N
    pm = (np.arange(128) % 64)[:, None].astype(np.float64)
    Tr, Ti = np.cos(th * pm * k1), -np.sin(th * pm * k1)
    TrTr, TiTi = np.concatenate([Tr, Tr], 1), np.concatenate([Ti, Ti], 1)
    pp = np.arange(64)[:, None].astype(np.float64)
    k2 = np.arange(64)[None, :].astype(np.float64)
    g2r, g2i = np.cos(2 * np.pi * pp * k2 / 64), -np.sin(2 * np.pi * pp * k2 / 64)
    Z0 = np.zeros((64, 64))
    G2r = np.block([[g2r, Z0], [Z0, g2r]])
    G2i = np.block([[g2i, Z0], [Z0, g2i]])
    R12 = np.concatenate([G2r, -G2i, G2i, G2r], 1)
    k1c = np.arange(256)[:, None].astype(np.float64)
    pcol = (np.arange(128) % 64)[None, :].astype(np.float64)
    T2r, T2i = np.cos(th * pcol * k1c), np.sin(th * pcol * k1c)
    TT2r = np.concatenate([T2r[0:128], T2r[0:128], T2r[128:256], T2r[128:256]], 1)
    TT2i = np.concatenate([T2i[0:128], T2i[0:128], T2i[128:256], T2i[128:256]], 1)
    aa = np.arange(128)[None, :].astype(np.float64)
    IF1c, IF1s = np.cos(2 * np.pi * aa * k1c / 256), -np.sin(2 * np.pi * aa * k1c / 256)
    IF1 = np.concatenate([IF1c[0:128], IF1s[0:128], IF1c[128:256], IF1s[128:256]], 1)
    t_lin = np.linspace(0.0, 1.0, 8192, dtype=np.float32)
    Swin = -t_lin.reshape(128, 64)
    L = 8192
    t = np.linspace(0.0, 1.0, L, dtype=np.float32)[:, None]
    w = (np.float32(2.0 * math.pi) * np.arange(L, dtype=np.float32)[:, None] / np.float32(L)).astype(np.float32)
    f = np.linspace(1e-4, 15, 16, dtype=np.float32)[None, :]
    zemb = np.concatenate([t, np.cos(f * w), -np.sin(f * w)], axis=-1).astype(np.float32)
    max_decay = math.log(1e-2) / 0.3
    min_decay = math.log(1e-2) / 1.5
    deltas = np.abs(np.linspace(min_decay, max_decay, 512, dtype=np.float32)).astype(np.float32)
    f32 = lambda x: np.ascontiguousarray(x, dtype=np.float32)
    return dict(F1cat=f32(F1cat), TrTr=f32(TrTr), TiTi=f32(TiTi), R12=f32(R12), TT2r=f32(TT2r), TT2i=f32(TT2i),
                IF1=f32(IF1), Swin=f32(Swin), zembT=f32(zemb.T), deltas=deltas)


def _hy_consts_h():
    N = NFFT
    a = np.arange(128)[:, None].astype(np.float64)
    k1 = np.arange(128)[None, :].astype(np.float64) + 0.5
    F1cat = np.concatenate([np.cos(2 * np.pi * a * k1 / 256), -np.sin(2 * np.pi * a * k1 / 256)], 1)
    th = 2 * np.pi / N
    pm = (np.arange(128) % 64)[:, None].astype(np.float64)
    Tr, Ti = np.cos(th * pm * k1), -np.sin(th * pm * k1)
    TrTr, TiTi = np.concatenate([Tr] * 4, 1), np.concatenate([Ti] * 4, 1)
    k1c = (np.arange(128)[:, None].astype(np.float64) + 0.5)
    pcol = (np.arange(128) % 64)[None, :].astype(np.float64)
    T2r, T2i = np.cos(th * pcol * k1c), np.sin(th * pcol * k1c)
    TT2r, TT2i = np.concatenate([T2r] * 4, 1), np.concatenate([T2i] * 4, 1)
    aa = np.arange(128)[None, :].astype(np.float64)
    IF1 = np.concatenate([np.cos(2 * np.pi * aa * k1c / 256), -np.sin(2 * np.pi * aa * k1c / 256)], 1)
    f32 = lambda x: np.ascontiguousarray(x, dtype=np.float32)
    return dict(F1cat_h=f32(F1cat), TrTr_h=f32(TrTr), TiTi_h=f32(TiTi), TT2r_h=f32(TT2r), TT2i_h=f32(TT2i), IF1_h=f32(IF1))


def build_hyena(stop=99):
    nc = bass.Bass("TRN2", target_bir_lowering=False)
    P = Prog(nc)
    D = lambda name, shape, kind="ExternalInput": nc.dram_tensor(name, list(shape), F32, kind=kind).ap()
    ST = 256
    H = _p1_common(P, nc, 512, ST=ST, SW=128)
    sb, ps = P.sb, P.ps
    wb, hT, pSS = H["wb"], H["hT"], H["pSS"]
    dF1cat, dTrTr, dTiTi, dR12 = D("F1cat", [128, 512]), D("TrTr", [128, 512]), D("TiTi", [128, 512]), D("R12", [128, 512])
    dTT2r, dTT2i, dIF1, dSwin = D("TT2r", [128, 512]), D("TT2i", [128, 512]), D("IF1", [128, 512]), D("Swin", [128, 64])
    dzemb = D("zembT", [33, NTOK])
    ddelta = D("delta", [128, 128])
    dhbcol = D("hbcol", [128, 128])
    dfw1, dfw2, dfw3, dfpar = D("fw1", [33, 64]), D("fw2", [64, 64]), D("fw3c", [64, 512]), D("fpar", [64, 4])
    dcw, dcb = D("cw", [128, 9]), D("cb", [128, 3])
    hy_oz = D("hy_oz", [128, 128, 64], kind="ExternalOutput")

    U = [sb("U%d" % i, [128, NTOK + 2], BF16) for i in range(3)]
    CV = sb("CV", [128, NTOK], BF16)
    Ob = sb("Ob", [128, NTOK], BF16)
    h2T = sb("h2T", [64, NTOK], BF16)
    Ap = sb("Ap", [128, 2, NPQ, 256], BF16)
    Hs = sb("Hs", [128, 2, NPQ, 256], BF16)
    Yb = sb("Yb", [128, 2, NPQ, 256], BF16)
    Bp = sb("Bp", [128, 2, 2, NPQ, 128], BF16)
    W1 = [sb("W1_%d" % i, [128, 512], F32) for i in range(2)]
    W2 = [sb("W2_%d" % i, [128, 512], F32) for i in range(2)]
    cpy = [sb("cpy%d" % i, [128, 512], F32) for i in range(2)]
    cpy2 = sb("cpy2", [128, 512], F32)
    tmpc = sb("tmpc", [128, 1024], F32)
    arg = tmpc[0:64, 0:512]
    argi = sb("argi", [64, 512], mybir.dt.int32)
    h1 = tmpc[0:64, 512:1024]
    F1b = sb("F1b", [128, 512], BF16)
    R12b = sb("R12b", [128, 512], BF16)
    IF1b = sb("IF1b", [128, 512], BF16)
    TrTr, TiTi = sb("TrTr", [128, 512], F32), sb("TiTi", [128, 512], F32)
    TT2r, TT2i = sb("TT2r", [128, 512], F32), sb("TT2i", [128, 512], F32)
    Swin = sb("Swin", [128, 64], F32)
    delta = sb("delta", [128, 128], F32)
    hbcol = sb("hbcol", [128, 128], F32)
    fw1, fw2 = sb("fw1", [33, 64], F32), sb("fw2", [64, 64], F32)
    fw3b = sb("fw3b", [64, 512], BF16)
    fpar = sb("fpar", [64, 8], F32)
    cw, cb = sb("cw", [128, 9], F32), sb("cb", [128, 3], F32)
    zc = [H["stage"][i][0:33, 0:4, :].rearrange("p k n -> p (k n)") for i in range(2)]
    win = sb("win", [128, 128], F32)
    fwbw = sb("fwbw", [128, 2, 128], F32)
    ot = [H["xs"][i][:, 0:2, :].rearrange("p k n -> p (k n)") for i in range(2)]

    pA1 = [ps("pA1_%d" % i, [128, 512], F32) for i in range(2)]
    pXr, pXi = ps("pXr", [128, 512], F32), ps("pXi", [128, 512], F32)
    pB = ps("pB", [128, 512], F32)
    pY = ps("pY", [128, 512], F32)
    pTz = ps("pTz", [128, 8, 128], BF16)
    ident = sb("ident", [128, 128], BF16)
    identf = W1[0]
    didn = D("identd", [128, 128])

    def ldcast(dst, dkey, src, n=512, parts=128):
        P.dma("sp", cpy2[0:parts, 0:n], src, writes=["cpy2"])
        P.op("dve", lambda e: e.tensor_copy(out=dst, in_=cpy2[0:parts, 0:n]), reads=["cpy2"], writes=[dkey])
    ldcast(F1b[:, :], "F1b", dF1cat[:, :])
    ldcast(R12b[:, :], "R12b", dR12[:, :])
    ldcast(IF1b[:, :], "IF1b", dIF1[:, :])
    ldcast(ident[:, :], "ident", didn[:, :], n=128)
    ldcast(fw3b[:, :], "fw3b", dfw3[:, :], parts=64)
    for dst, key, src in ((TrTr, "TrTr", dTrTr), (TiTi, "TiTi", dTiTi), (TT2r, "TT2r", dTT2r), (TT2i, "TT2i", dTT2i), (Swin, "Swin", dSwin),
                          (delta, "delta", ddelta), (hbcol, "hbcol", dhbcol), (fw1, "fw1", dfw1), (fw2, "fw2", dfw2), (cw, "cw", dcw), (cb, "cb", dcb)):
        P.dma("sp", dst[:, :], src[:, :], writes=[key])
    P.dma("sp", fpar[:, 0:4], dfpar[:, :], writes=["fpar"])
    i2p = 1.0 / (2.0 * math.pi)
    for (bc, fc, o0) in ((0, 1, 4), (2, 3, 6)):
        P.op("dve", lambda e, bc=bc, fc=fc, o0=o0: e.tensor_tensor(out=fpar[:, o0 + 1:o0 + 2], in0=fpar[:, bc:bc + 1], in1=fpar[:, fc:fc + 1], op=ALU.mult),
             reads=["fpar"], writes=["fpar"])
        P.op("dve", lambda e, o0=o0: e.tensor_scalar(out=fpar[:, o0 + 1:o0 + 2], in0=fpar[:, o0 + 1:o0 + 2], scalar1=i2p, scalar2=16.0, op0=ALU.mult, op1=ALU.add),
             reads=["fpar"], writes=["fpar"])
        P.op("dve", lambda e, fc=fc, o0=o0: e.tensor_scalar(out=fpar[:, o0:o0 + 1], in0=fpar[:, fc:fc + 1], scalar1=i2p, scalar2=None, op0=ALU.mult),
             reads=["fpar"], writes=["fpar"])
    for i in range(3):
        P.op("pool", lambda e, i=i: e.memset(U[i][:, 0:1], 0.0), writes=["U%d" % i])
        P.op("pool", lambda e, i=i: e.memset(U[i][:, NTOK + 1:NTOK + 2], 0.0), writes=["U%d" % i])

    nst = NTOK // ST
    _p1_load(P, H, 0)
    pU = [pXr, pXi, pY]
    for st in range(nst):
        if st + 1 < nst:
            _p1_load(P, H, st + 1)
        _p1_norm(P, H, st)
        for i in range(3):
            pk = ["pXr", "pXi", "pY"][i]
            for k in range(8):
                P.op("pe", lambda e, k=k, i=i: e.matmul(pU[i][:, 0:ST], lhsT=wb[:, k, i * 128:(i + 1) * 128], rhs=hT[:, k, :],
                                                        start=(k == 0), stop=(k == 7)), reads=["wb", "hT"], writes=[pk])
            P.op("act", lambda e, i=i, st=st: e.activation(out=U[i][:, 1 + st * ST:1 + (st + 1) * ST], in_=pU[i][:, 0:ST], func=AF.Copy),
                 reads=[pk], writes=["U%d" % i])

    if stop <= 1:
        P.dma("pool", hy_oz[:, 0:4, :], U[0][:, 1:257].rearrange("a (c p) -> a c p", p=64).bitcast(F32) if False else H["xs"][0][:, 0, :].rearrange("a (c p) -> a c p", p=64), reads=["U0", "U1", "U2", "xs0"], writes=["dbg"])
        P.build()
        return nc
    Zkeys = lambda i: ["Z%d_%d" % (i, s) for s in range(128 // (2 * NPQ))]
    Z = [U[i][:, 0:NTOK].rearrange("a (c p) -> a c p", p=64) for i in range(3)]
    CVs = CV[:, :].rearrange("c (a p) -> c a p", p=64)
    for i in range(3):
        for ch in range(8):
            j0 = ch * 1024
            P.op("dve", lambda e, i=i, j0=j0: e.tensor_scalar(out=tmpc[:, :], in0=U[i][:, j0:j0 + 1024], scalar1=cw[:, 3 * i:3 * i + 1],
                                                              scalar2=cb[:, i:i + 1], op0=ALU.mult, op1=ALU.add),
                 reads=["U%d" % i, "cw", "cb"], writes=["tmpc"])
            P.op("dve", lambda e, i=i, j0=j0: e.scalar_tensor_tensor(out=tmpc[:, :], in0=U[i][:, j0 + 1:j0 + 1025], scalar=cw[:, 3 * i + 1:3 * i + 2],
                                                                     in1=tmpc[:, :], op0=ALU.mult, op1=ALU.add),
                 reads=["U%d" % i, "cw", "tmpc"], writes=["tmpc"])
            P.op("dve", lambda e, i=i, j0=j0: e.scalar_tensor_tensor(out=CV[:, j0:j0 + 1024], in0=U[i][:, j0 + 2:j0 + 1026], scalar=cw[:, 3 * i + 2:3 * i + 3],
                                                                     in1=tmpc[:, :], op0=ALU.mult, op1=ALU.add),
                 reads=["U%d" % i, "cw", "tmpc"], writes=["CV"])
        for pg in range(8):
            for pi in range(8):
                p = pg * 8 + pi
                P.op("pe", lambda e, p=p, pi=pi: e.transpose(out=pTz[:, pi, :], in_=CVs[:, :, p], identity=ident[:, :]),
                     reads=["CV", "ident"], writes=["pTz"])
            P.op("act", lambda e, i=i, pg=pg: e.activation(out=Z[i][:, :, pg * 8:(pg + 1) * 8].rearrange("a c p -> a p c"), in_=pTz[:, :, :], func=AF.Copy),
                 reads=["pTz"], writes=["U%d" % i] + Zkeys(i))

    if stop <= 2:
        P.dma("pool", hy_oz[:, 0:4, :], H["xs"][0][:, 0, :].rearrange("a (c p) -> a c p", p=64), reads=["U0", "U1", "U2", "xs0"], writes=["dbg"])
        P.build()
        return nc
    for ch in range(16):
        zt, zk = zc[ch % 2], "stage%d" % (ch % 2)
        P.dma("sp", zt[:, :], dzemb[:, ch * 512:(ch + 1) * 512], writes=[zk])
        for layer in range(2):
            if layer == 0:
                P.op("pe", lambda e, zt=zt: e.matmul(pSS[0:64, :], lhsT=fw1[:, :], rhs=zt[:, :], start=True, stop=True), reads=["fw1", zk], writes=["pSS"])
            else:
                P.op("pe", lambda e: e.matmul(pSS[0:64, :], lhsT=fw2[:, :], rhs=h1[:, :], start=True, stop=True), reads=["fw2", "tmpc"], writes=["pSS"])
            fr, fb = (4, 5) if layer == 0 else (6, 7)
            P.op("dve", lambda e, fr=fr, fb=fb: e.tensor_scalar(out=arg[:, :], in0=pSS[0:64, :], scalar1=fpar[:, fr:fr + 1], scalar2=fpar[:, fb:fb + 1],
                                                                op0=ALU.mult, op1=ALU.add), reads=["pSS", "fpar"], writes=["tmpc"])
            P.op("dve", lambda e: e.tensor_copy(out=argi[:, :], in_=arg[:, :]), reads=["tmpc"], writes=["argi"])
            P.op("dve", lambda e: e.tensor_copy(out=h1[:, :], in_=argi[:, :]), reads=["argi", "tmpc"], writes=["tmpc"])
            P.op("dve", lambda e: e.tensor_tensor(out=arg[:, :], in0=arg[:, :], in1=h1[:, :], op=ALU.subtract), reads=["tmpc"], writes=["tmpc"])
            P.op("dve", lambda e: e.scalar_tensor_tensor(out=arg[:, :], in0=arg[:, :], scalar=0.5, in1=arg[:, :], op0=ALU.is_gt, op1=ALU.subtract),
                 reads=["tmpc"], writes=["tmpc"])
            if layer == 0:
                P.op("act", lambda e: e.activation(out=h1[:, :], in_=arg[:, :], func=AF.Sin, scale=-2.0 * math.pi), reads=["tmpc"], writes=["tmpc"])
            else:
                P.op("act", lambda e, ch=ch: e.activation(out=h2T[:, ch * 512:(ch + 1) * 512], in_=arg[:, :], func=AF.Sin, scale=-2.0 * math.pi),
                     reads=["tmpc"], writes=["h2T"])

    if stop <= 3:
        P.dma("pool", hy_oz[:, 0:4, :], H["xs"][0][:, 0, :].rearrange("a (c p) -> a c p", p=64), reads=["h2T", "xs0"], writes=["dbg"])
        P.build()
        return nc
    E3 = CV[:, :].rearrange("a (c p) -> a c p", p=64)
    O3 = Ob[:, :].rearrange("a (c p) -> a c p", p=64)
    h2s = h2T[:, :].rearrange("j (a p) -> j a p", p=64)
    cnt = [0]

    def twiddle(psrc, pkey, Tr_, Ti_, trk, tik, outr, outi, okey, view):
        g = cnt[0]
        cnt[0] += 1
        w1, w1k = W1[g % 2], "W1_%d" % (g % 2)
        w2, w2k = W2[g % 2], "W2_%d" % (g % 2)
        cp_, cpk = cpy[g % 2], "cpy%d" % (g % 2)
        P.op("act", lambda e: e.activation(out=cp_[:, :], in_=psrc[:, :], func=AF.Copy), reads=[pkey], writes=[cpk])
        P.op("dve", lambda e: e.tensor_tensor(out=w1[:, :], in0=psrc[:, :], in1=Tr_[:, :], op=ALU.mult), reads=[pkey, trk], writes=[w1k])
        P.op("pool", lambda e: e.tensor_tensor(out=w2[:, :], in0=cp_[:, :], in1=Ti_[:, :], op=ALU.mult), reads=[cpk, tik], writes=[w2k])
        w1r, w1i = view(w1)
        w2r, w2i = view(w2)
        P.op("dve", lambda e: e.tensor_tensor(out=outr, in0=w1r, in1=w2i, op=ALU.subtract), reads=[w1k, w2k], writes=[okey])
        P.op("pool", lambda e: e.tensor_tensor(out=outi, in0=w2r, in1=w1i, op=ALU.add), reads=[w1k, w2k], writes=[okey])

    v_fwd = lambda t: (t[:, 0:256], t[:, 256:512])
    v_inv = lambda t: (t[:, :].rearrange("k (c r x) -> k c r x", c=2, r=2)[:, :, 0, :], t[:, :].rearrange("k (c r x) -> k c r x", c=2, r=2)[:, :, 1, :])

    def fwd_stage1(src3, skey, c0):
        for q in range(NPQ):
            g = cnt[0]
            pa, pak = pA1[g % 2], "pA1_%d" % (g % 2)
            c = c0 + 2 * q
            P.op("pe", lambda e, c=c, pa=pa: e.matmul(pa[:, :], lhsT=src3[:, c:c + 2, :], rhs=F1b[:, :], start=True, stop=True),
                 reads=[skey, "F1b"], writes=[pak])
            twiddle(pa, pak, TrTr, TiTi, "TrTr", "TiTi", Ap[:, 0, q, :], Ap[:, 1, q, :], "Ap", v_fwd)

    G2r, G2in, G2i = R12b[:, 0:128], R12b[:, 128:256], R12b[:, 256:384]
    R1, R2 = R12b[:, 0:256], R12b[:, 256:512]

    for o in range(2):
        for p in range(64):
            P.op("pe", lambda e, p=p, o=o: e.matmul(pSS[:, 0:256], lhsT=h2s[:, :, p], rhs=fw3b[:, o * 256:(o + 1) * 256], start=True, stop=True),
                 reads=["h2T", "fw3b"], writes=["pSS"])
            P.op("act", lambda e, p=p: e.activation(out=win[:, :], in_=delta[:, :], func=AF.Exp, scale=Swin[:, p:p + 1]), reads=["delta", "Swin"], writes=["win"])
            for d in range(2):
                P.op("dve", lambda e, d=d: e.tensor_tensor(out=fwbw[:, d, :], in0=pSS[:, d * 128:(d + 1) * 128], in1=win[:, :], op=ALU.mult),
                     reads=["pSS", "win"], writes=["fwbw"])
            P.op("pool", lambda e, p=p: e.tensor_tensor(out=E3[:, :, p], in0=fwbw[:, 0, :], in1=fwbw[:, 1, :], op=ALU.add), reads=["fwbw"], writes=["CV"])
            P.op("pool", lambda e, p=p: e.tensor_tensor(out=O3[:, :, p], in0=fwbw[:, 0, :], in1=fwbw[:, 1, :], op=ALU.subtract), reads=["fwbw"], writes=["Ob"])
            if p == 0:
                P.op("pool", lambda e: e.tensor_copy(out=E3[0:1, :, 0], in_=fwbw[0:1, 0, :]), reads=["fwbw"], writes=["CV"])
                P.op("pool", lambda e: e.tensor_copy(out=O3[0:1, :, 0], in_=fwbw[0:1, 0, :]), reads=["fwbw"], writes=["Ob"])
        if stop <= 4:
            P.dma("pool", hy_oz[:, 0:4, :], H["xs"][0][:, 0, :].rearrange("a (c p) -> a c p", p=64), reads=["CV", "Ob", "xs0"], writes=["dbg"])
            P.build()
            return nc
        for sbt in range(128 // (2 * NPQ)):
            c0 = sbt * 2 * NPQ
            zk = "Z0_%d" % sbt
            for which, (src3, skey) in enumerate(((E3, "CV"), (O3, "Ob"))):
                fwd_stage1(src3, skey, c0)
                for qq in range(NPQ // 2):
                    rr = Ap[:, 0, 2 * qq:2 * qq + 2, :]
                    ri = Ap[:, 1, 2 * qq:2 * qq + 2, :]
                    if which == 0:
                        P.op("pe", lambda e, rr=rr: e.matmul(pXr[:, :], lhsT=G2r, rhs=rr, start=True, stop=False), reads=["R12b", "Ap"], writes=["pXr"])
                        P.op("pe", lambda e, ri=ri: e.matmul(pXr[:, :], lhsT=G2in, rhs=ri, start=False, stop=True), reads=["R12b", "Ap"], writes=["pXr"])
                        for j in range(2):
                            qg = sbt * NPQ + 2 * qq + j
                            P.op("act", lambda e, j=j, qq=qq, qg=qg, o=o: e.activation(out=Hs[:, 0, 2 * qq + j, :], in_=pXr[:, j * 256:(j + 1) * 256], func=AF.Identity,
                                                                                        bias=hbcol[:, o * 64 + qg:o * 64 + qg + 1], scale=1.0),
                                 reads=["pXr", "hbcol"], writes=["Hs"])
                    else:
                        P.op("pe", lambda e, rr=rr: e.matmul(pXi[:, :], lhsT=G2i, rhs=rr, start=True, stop=False), reads=["R12b", "Ap"], writes=["pXi"])
                        P.op("pe", lambda e, ri=ri: e.matmul(pXi[:, :], lhsT=G2r, rhs=ri, start=False, stop=True), reads=["R12b", "Ap"], writes=["pXi"])
                        P.op("act", lambda e, qq=qq: e.activation(out=Hs[:, 1, 2 * qq:2 * qq + 2, :], in_=pXi[:, :].rearrange("k (q x) -> k q x", q=2), func=AF.Copy),
                             reads=["pXi"], writes=["Hs"])
            if stop <= 6:
                P.dma("pool", hy_oz[:, 0:4, :], H["xs"][0][:, 0, :].rearrange("a (c p) -> a c p", p=64), reads=["CV", "Ob", "xs0", "Ap", "Hs", "Yb", "Bp", "pY"], writes=["dbg"])
                P.build()
                return nc
            fwd_stage1(Z[0], zk, c0)
            for qq in range(NPQ // 2):
                rr = Ap[:, 0, 2 * qq:2 * qq + 2, :]
                ri = Ap[:, 1, 2 * qq:2 * qq + 2, :]
                P.op("pe", lambda e, rr=rr: e.matmul(pXr[:, :], lhsT=G2r, rhs=rr, start=True, stop=False), reads=["R12b", "Ap"], writes=["pXr"])
                P.op("pe", lambda e, ri=ri: e.matmul(pXr[:, :], lhsT=G2in, rhs=ri, start=False, stop=True), reads=["R12b", "Ap"], writes=["pXr"])
                P.op("pe", lambda e, rr=rr: e.matmul(pXi[:, :], lhsT=G2i, rhs=rr, start=True, stop=False), reads=["R12b", "Ap"], writes=["pXi"])
                P.op("pe", lambda e, ri=ri: e.matmul(pXi[:, :], lhsT=G2r, rhs=ri, start=False, stop=True), reads=["R12b", "Ap"], writes=["pXi"])
                g = cnt[0]
                cnt[0] += 1
                w1, w1k = W1[g % 2], "W1_%d" % (g % 2)
                w2, w2k = W2[g % 2], "W2_%d" % (g % 2)
                cx, cxk = cpy[g % 2], "cpy%d" % (g % 2)
                hr = Hs[:, 0, 2 * qq:2 * qq + 2, :].rearrange("k q x -> k (q x)")
                hi = Hs[:, 1, 2 * qq:2 * qq + 2, :].rearrange("k q x -> k (q x)")
                yr = Yb[:, 0, 2 * qq:2 * qq + 2, :].rearrange("k q x -> k (q x)")
                yi = Yb[:, 1, 2 * qq:2 * qq + 2, :].rearrange("k q x -> k (q x)")
                P.op("act", lambda e, cx=cx: e.activation(out=cx[:, :], in_=pXi[:, :], func=AF.Copy), reads=["pXi"], writes=[cxk])
                P.op("act", lambda e: e.activation(out=cpy2[:, :], in_=pXr[:, :], func=AF.Copy), reads=["pXr"], writes=["cpy2"])
                P.op("dve", lambda e, w1=w1, hr=hr: e.tensor_tensor(out=w1[:, :], in0=pXr[:, :], in1=hr, op=ALU.mult), reads=["pXr", "Hs"], writes=[w1k])
                P.op("pool", lambda e, w2=w2, cx=cx, hi=hi: e.tensor_tensor(out=w2[:, :], in0=cx[:, :], in1=hi, op=ALU.mult), reads=[cxk, "Hs"], writes=[w2k])
                P.op("dve", lambda e, w1=w1, w2=w2, yr=yr: e.tensor_tensor(out=yr, in0=w1[:, :], in1=w2[:, :], op=ALU.subtract), reads=[w1k, w2k], writes=["Yb"])
                P.op("dve", lambda e, w1=w1, hr=hr: e.tensor_tensor(out=w1[:, :], in0=pXi[:, :], in1=hr, op=ALU.mult), reads=["pXi", "Hs", "Yb"], writes=[w1k])
                P.op("pool", lambda e, w2=w2, hi=hi: e.tensor_tensor(out=w2[:, :], in0=cpy2[:, :], in1=hi, op=ALU.mult), reads=["cpy2", "Hs", "Yb"], writes=[w2k])
                P.op("pool", lambda e, w1=w1, w2=w2, yi=yi: e.tensor_tensor(out=yi, in0=w1[:, :], in1=w2[:, :], op=ALU.add), reads=[w1k, w2k], writes=["Yb"])
            if stop <= 7:
                P.dma("pool", hy_oz[:, 0:4, :], H["xs"][0][:, 0, :].rearrange("a (c p) -> a c p", p=64), reads=["CV", "Ob", "xs0", "Ap", "Hs", "Yb", "Bp", "pY"], writes=["dbg"])
                P.build()
                return nc
            for q in range(NPQ):
                for kc in range(2):
                    P.op("pe", lambda e, q=q, kc=kc: e.matmul(pB[:, kc * 256:(kc + 1) * 256], lhsT=Yb[:, 0, q, kc * 128:(kc + 1) * 128], rhs=R1, start=True, stop=False),
                         reads=["Yb", "R12b"], writes=["pB"])
                    P.op("pe", lambda e, q=q, kc=kc: e.matmul(pB[:, kc * 256:(kc + 1) * 256], lhsT=Yb[:, 1, q, kc * 128:(kc + 1) * 128], rhs=R2, start=False, stop=True),
                         reads=["Yb", "R12b"], writes=["pB"])
                twiddle(pB, "pB", TT2r, TT2i, "TT2r", "TT2i", Bp[:, :, 0, q, :], Bp[:, :, 1, q, :], "Bp", v_inv)
            if stop <= 8:
                P.dma("pool", hy_oz[:, 0:4, :], H["xs"][0][:, 0, :].rearrange("a (c p) -> a c p", p=64), reads=["CV", "Ob", "xs0", "Ap", "Hs", "Yb", "Bp", "pY"], writes=["dbg"])
                P.build()
                return nc
            for hh in range(NPQ // 4):
                n = 0
                for kc in range(2):
                    for r in range(2):
                        rhs = Bp[:, kc, r, 4 * hh:4 * hh + 4, :]
                        lt = IF1b[:, (2 * kc + r) * 128:(2 * kc + r + 1) * 128]
                        P.op("pe", lambda e, rhs=rhs, lt=lt, n=n: e.matmul(pY[:, :], lhsT=lt, rhs=rhs, start=(n == 0), stop=(n == 3)),
                             reads=["Bp", "IF1b"], writes=["pY"])
                        n += 1
                cc = c0 + 8 * hh
                if o == 0:
                    P.op("dve", lambda e, cc=cc: e.scalar_tensor_tensor(out=Z[0][:, cc:cc + 8, :], in0=pY[:, :].rearrange("a (c p) -> a c p", p=64), scalar=1.0 / NFFT,
                                                                        in1=Z[1][:, cc:cc + 8, :], op0=ALU.mult, op1=ALU.mult),
                         reads=["pY", "U1"] + Zkeys(1), writes=[zk])
                else:
                    g = cnt[0]
                    cnt[0] += 1
                    OT, otk = ot[g % 2], "xs%d" % (g % 2)
                    P.op("dve", lambda e, cc=cc, OT=OT: e.scalar_tensor_tensor(out=OT[:, :].rearrange("a (c p) -> a c p", p=64), in0=pY[:, :].rearrange("a (c p) -> a c p", p=64),
                                                                               scalar=1.0 / NFFT, in1=Z[2][:, cc:cc + 8, :], op0=ALU.mult, op1=ALU.mult),
                         reads=["pY", "U2"] + Zkeys(2), writes=[otk])
                    P.dma("pool", hy_oz[:, cc:cc + 8, :], OT[:, :].rearrange("a (c p) -> a c p", p=64), reads=[otk], writes=["hy_%d" % cc])
    P.build()
    return nc


def run_hyena(inp, stop=99):
    f = lambda a: np.ascontiguousarray(a, dtype=np.float32)
    nc = build_hyena(stop)
    K = _hy_consts()
    w_in = inp["w_in"][0]
    in_maps = []
    for core in range(8):
        b, g = core // 4, core % 4
        cs = slice(128 * g, 128 * g + 128)
        hy_w = w_in[:, 1536:]
        w_inc = np.concatenate([hy_w[:, 0:512][:, cs], hy_w[:, 512:1024][:, cs], hy_w[:, 1024:1536][:, cs], np.zeros((1024, 128), np.float32)], axis=1)
        cwv = inp["conv_w"][0].reshape(3, 3, 512)[:, :, cs]
        cbv = inp["conv_b"][0].reshape(3, 512)[:, cs]
        w3 = inp["filt_w3"][0].reshape(64, 2, 2, 512)[:, :, :, cs].reshape(64, 512)
        hb = inp["hyena_bias"][0][:, cs]
        hbcol = np.zeros((128, 2, 64), np.float32)
        for cp in range(2):
            hbcol[64 * cp:64 * cp + 64, :, :] = hb[:, cp::2][None, :, :]
        in_maps.append({
            "xT": f(inp["x"][b].T), "cvec": f(inp["c"][b].reshape(8, 128).T),
            "w_ada1": f(inp["w_ada"][0][:, 0:2048]), "b_ada1": f(inp["b_ada"][0][0:2048].reshape(16, 128).T),
            "gpre": f(inp["g_pre_mix"][0].reshape(8, 128).T), "w_inc": f(w_inc),
            "F1cat": K["F1cat"], "TrTr": K["TrTr"], "TiTi": K["TiTi"], "R12": K["R12"], "TT2r": K["TT2r"], "TT2i": K["TT2i"],
            "IF1": K["IF1"], "Swin": K["Swin"], "zembT": K["zembT"],
            "delta": f(np.broadcast_to(K["deltas"][cs][None, :], (128, 128))),
            "hbcol": f(hbcol.reshape(128, 128)),
            "fw1": f(inp["filt_w1"][0]), "fw2": f(inp["filt_w2"][0]), "fw3c": f(w3),
            "fpar": f(np.stack([inp["filt_b1"][0], inp["filt_freq1"][0], inp["filt_b2"][0], inp["filt_freq2"][0]], axis=1)),
            "cw": f(cwv.transpose(2, 1, 0).reshape(128, 9)), "cb": f(cbv.T),
            "identd": np.eye(128, dtype=np.float32),
        })
    res = run_bass_kernel_spmd(nc, in_maps, core_ids=list(range(8)))
    hy = np.zeros((2, NTOK, 512), np.float32)
    for core in range(8):
        b, g = core // 4, core % 4
        oz = res.results[core]["hy_oz"]
        hy[b, :, 128 * g:128 * g + 128] = oz.transpose(0, 2, 1).reshape(NTOK, 128)
    return hy


NW = 4096
NOWN = 2048


class DramReg:
    def __init__(self, nc):
        self.nc = nc
        self.t = {}

    def __call__(self, name, shape, kind="ExternalInput", dt=None):
        if name not in self.t:
            self.t[name] = self.nc.dram_tensor(name, list(shape), dt or F32, kind=kind).ap()
        return self.t[name]


def _p1_common_f(P, D, ncols_w, wname, xname, ntok, ST=512, SW=256, cast_w=True):
    H = {}
    H["xT"] = D(xname, [1024, ntok])
    cvec = D("cvec", [128, 8])
    w_ada1 = D("w_ada1", [1024, 2048])
    b_ada1 = D("b_ada1", [128, 16])
    gpre = D("gpre", [128, 8])
    H["w_inc"] = D(wname[0], wname[1])
    sb, ps = P.sb, P.ps
    H["wb"] = sb("wb", [128, 8, ncols_w], BF16)
    H["stage"] = stage = [sb("stage%d" % i, [128, 8, SW], F32) for i in range(2)]
    H["ST"], H["SW"] = ST, SW
    csb = sb("csb", [128, 8], F32)
    gp = sb("gp", [128, 8], F32)
    bsb = sb("bsb", [128, 16], F32)
    H["modc"] = modc = sb("modc", [128, 16], F32)
    H["G0"] = G0 = sb("G0", [128, 8], F32)
    H["onesb"] = onesb = sb("onesb", [128, 128], BF16)
    H["xs"] = [sb("xs%d" % i, [128, 8, ST], F32) for i in range(2)]
    H["sqR"] = [sb("sq%d" % i, [128, 8, ST], BF16) for i in range(2)]
    H["hTR"] = [sb("hT%d" % i, [128, 8, ST], BF16) for i in range(2)]
    H["rbcR"] = [sb("rbc%d" % i, [128, ST], F32) for i in range(2)]
    H["sq"], H["hT"], H["rbc"] = H["sqR"][0], H["hTR"][0], H["rbcR"][0]
    H["pSS"] = pSS = ps("pSS", [128, 512], F32)
    P.op("pool", lambda e: e.memset(onesb[:, :], 1.0), writes=["onesb"])
    P.dma("sp", csb[:, :], cvec[:, :], writes=["csb"])
    P.dma("sp", gp[:, :], gpre[:, :], writes=["gp"])
    P.dma("sp", bsb[:, :], b_ada1[:, :], writes=["bsb"])
    P.op("act", lambda e: e.activation(out=csb[:, :], in_=csb[:, :], func=AF.Silu), reads=["csb"], writes=["csb"])
    wa = w_ada1.rearrange("(k p) n -> p k n", p=128)
    for blk in range(2048 // SW):
        st, skey = stage[blk % 2], "stage%d" % (blk % 2)
        P.dma("sp", st[:, :, :], wa[:, :, blk * SW:(blk + 1) * SW], writes=[skey])
        for jj in range(SW // 128):
            j = (SW // 128) * blk + jj
            for k in range(8):
                P.op("pe", lambda e, k=k, j=j, jj=jj, st=st: e.matmul(pSS[:, j:j + 1], lhsT=st[:, k, jj * 128:(jj + 1) * 128],
                                                                      rhs=csb[:, k:k + 1], start=(k == 0), stop=(k == 7)),
                     reads=[skey, "csb"], writes=["pSS"])
    P.op("dve", lambda e: e.tensor_tensor(out=modc[:, :], in0=pSS[:, 0:16], in1=bsb[:, :], op=ALU.add),
         reads=["pSS", "bsb"], writes=["modc"])
    P.op("dve", lambda e: e.scalar_tensor_tensor(out=G0[:, :], in0=modc[:, 8:16], scalar=1.0, in1=gp[:, :], op0=ALU.add, op1=ALU.mult),
         reads=["modc", "gp"], writes=["G0"])
    return H


def _p1_norm_r(P, H, st):
    ST = H["ST"]
    r = st % 2
    xs, xk = H["xs"][r], "xs%d" % r
    sq, hT, rbc = H["sqR"][r], H["hTR"][r], H["rbcR"][r]
    sqk, hk, rk = "sq%d" % r, "hT%d" % r, "rbc%d" % r
    pSS, onesb, modc, G0 = H["pSS"], H["onesb"], H["modc"], H["G0"]
    P.op("act", lambda e: e.activation(out=sq[:, :, :], in_=xs[:, :, :], func=AF.Square), reads=[xk], writes=[sqk])
    for k in range(8):
        P.op("pe", lambda e, k=k: e.matmul(pSS[:, 0:ST], lhsT=onesb[:, :], rhs=sq[:, k, :], start=(k == 0), stop=(k == 7)),
             reads=["onesb", sqk], writes=["pSS"])
    P.op("dve", lambda e: e.tensor_scalar(out=rbc[:, :], in0=pSS[:, 0:ST], scalar1=1.0 / 1024, scalar2=EPS, op0=ALU.mult, op1=ALU.add),
         reads=["pSS"], writes=[rk])
    P.op("act", lambda e: e.activation(out=rbc[:, :], in_=rbc[:, :], func=AF.Sqrt), reads=[rk], writes=[rk])
    P.op("dve", lambda e: e.reciprocal(out=rbc[:, :], in_=rbc[:, :]), reads=[rk], writes=[rk])
    for k in range(8):
        P.op("dve", lambda e, k=k: e.scalar_tensor_tensor(out=xs[:, k, :], in0=xs[:, k, :], scalar=G0[:, k:k + 1], in1=rbc[:, :],
                                                          op0=ALU.mult, op1=ALU.mult), reads=[xk, "G0", rk], writes=[xk])
        P.op("act", lambda e, k=k: e.activation(out=hT[:, k, :], in_=xs[:, k, :], func=AF.Identity, bias=modc[:, k:k + 1], scale=1.0),
             reads=[xk, "modc"], writes=[hk])
    return hT, hk


def _cast_w(P, H, wv, ncols, col0=0):
    SW, stage, wb = H["SW"], H["stage"], H["wb"]
    for blk in range(ncols // SW):
        st, skey = stage[blk % 2], "stage%d" % (blk % 2)
        P.dma("sp", st[:, :, :], wv[:, :, blk * SW:(blk + 1) * SW], writes=[skey])
        P.op("dve" if blk % 2 == 0 else "pool", lambda e, st=st, blk=blk: e.tensor_copy(
            out=wb[:, :, col0 + blk * SW:col0 + (blk + 1) * SW], in_=st[:, :, :]), reads=[skey], writes=["wb"])


def emit_attn_norm_f(nc, outer, D, hTw):
    P = Prog(nc, sem_stack=outer, prefix="A0_")
    H = _p1_common_f(P, D, 128, ("w_incA", [4, 1024, 768]), "xTw", NW)
    ST = 512
    for st in range(NW // ST):
        _p1_load(P, H, st)
        hT, hk = _p1_norm_r(P, H, st)
        P.op("pool", lambda e, st=st, hT=hT: e.tensor_copy(out=hTw[:, :, st * ST:(st + 1) * ST], in_=hT[:, :, :]), reads=[hk], writes=["hTw"])
    P.build()


def emit_attn_f(nc, outer, D, mixT, hTw):
    P = Prog(nc, sem_stack=outer, prefix="A_")
    ST = 512
    H = {"SW": 256, "w_inc": D("w_incA", [4, 1024, 768])}
    ropeC, ropeS = D("ropeCw", [128, NW]), D("ropeSw", [128, NW])
    maskd, hseld, validd, identd = D("maskd", [128, 17 * 128]), D("hseld", [128, 64]), D("validw", [128, 32]), D("identd", [128, 128])
    sb, ps = P.sb, P.ps
    H["wb"] = wb = sb("wb", [128, 8, 768], BF16)
    H["stage"] = [sb("stage%d" % i, [128, 8, 256], F32) for i in range(2)]
    H["pSS"] = ps("pSS", [128, 512], F32)
    QT, KT = sb("QT", [128, NW], BF16), sb("KT", [128, NW], BF16)
    Vaug = sb("Vaug", [128, 32, 2, 65], BF16)
    Mall = sb("Mall", [128, 17 * 128], BF16)
    mst = sb("mst", [128, 17 * 128], F32)
    hself, hsel = sb("hself", [128, 64], F32), sb("hsel", [128, 64], BF16)
    valid = sb("valid", [128, 32], F32)
    identf = sb("identf", [128, 128], F32)
    onesrow = sb("onesrow", [64, 128], BF16)
    rc = [sb("rc%d" % i, [128, ST], F32) for i in range(2)]
    rs_ = [sb("rs%d" % i, [128, ST], F32) for i in range(2)]
    t1, t2 = sb("t1", [128, ST], F32), sb("t2", [128, ST], F32)
    sqk = sb("sqk", [128, ST], BF16)
    kmx = sb("kmx", [64, 2], F32)
    qn = sb("qn", [64, 128], F32)
    negm = sb("negm", [64, 128], BF16)
    PT = [sb("PT%d" % i, [128, 512], BF16) for i in range(4)]
    ao = [sb("ao%d" % i, [128, 128], F32) for i in range(2)]
    rec = sb("rec", [128, 4], F32)
    pA, pB, pV = ps("pA", [128, 512], F32), ps("pB", [128, 512], F32), ps("pV", [128, 512], F32)
    pN = H["pSS"]
    pS = [ps("pS%d" % i, [128, 512], F32) for i in range(2)]
    pO = [ps("pO%d" % i, [128, 2, 128], F32) for i in range(2)]

    P.dma("sp", mst[:, :], maskd[:, :], writes=["mst"])
    P.op("pool", lambda e: e.tensor_copy(out=Mall[:, :], in_=mst[:, :]), reads=["mst"], writes=["Mall"])
    P.dma("sp", hself[:, :], hseld[:, :], writes=["hself"])
    P.op("pool", lambda e: e.tensor_copy(out=hsel[:, :], in_=hself[:, :]), reads=["hself"], writes=["hsel"])
    P.dma("sp", valid[:, :], validd[:, :], writes=["valid"])
    P.dma("sp", identf[:, :], identd[:, :], writes=["identf"])
    P.op("pool", lambda e: e.memset(onesrow[:, :], 1.0), writes=["onesrow"])
    for h in range(2):
        P.op("pool", lambda e, h=h: e.tensor_copy(out=Vaug[:, :, h, 64], in_=valid[:, :]), reads=["valid"], writes=["Vaug"])
    cnt = [0]
    xl = [0]
    wv_all = H["w_inc"]
    for hp in range(4):
        _cast_w(P, H, wv_all[hp].rearrange("(k p) n -> p k n", p=128), 768)
        P.op("pool", lambda e: e.memset(kmx[:, :], 0.0), reads=["kmx"], writes=["kmx"])
        nst = NW // ST
        for st in range(nst):
            hT = hTw[:, :, st * ST:(st + 1) * ST]
            C, ck = rc[st % 2], "rc%d" % (st % 2)
            S, sk = rs_[st % 2], "rs%d" % (st % 2)
            P.dma("sp", C[:, :], ropeC[:, st * ST:(st + 1) * ST], writes=[ck])
            P.dma("sp", S[:, :], ropeS[:, st * ST:(st + 1) * ST], writes=[sk])
            for which, dst in ((0, QT), (1, KT)):
                dk = "QT" if which == 0 else "KT"
                c0 = which * 256
                for k in range(8):
                    P.op("pe", lambda e, k=k, c0=c0, hT=hT: e.matmul(pA[:, :], lhsT=wb[:, k, c0:c0 + 128], rhs=hT[:, k, :], start=(k == 0), stop=(k == 7)),
                         reads=["wb", "hTw"], writes=["pA"])
                for k in range(8):
                    P.op("pe", lambda e, k=k, c0=c0, hT=hT: e.matmul(pB[:, :], lhsT=wb[:, k, c0 + 128:c0 + 256], rhs=hT[:, k, :], start=(k == 0), stop=(k == 7)),
                         reads=["wb", "hTw"], writes=["pB"])
                P.op("dve", lambda e, C=C: e.tensor_tensor(out=t1[:, :], in0=pA[:, :], in1=C[:, :], op=ALU.mult), reads=["pA", ck], writes=["t1"])
                P.op("dve", lambda e, S=S: e.tensor_tensor(out=t2[:, :], in0=pB[:, :], in1=S[:, :], op=ALU.mult), reads=["pB", sk], writes=["t2"])
                P.op("dve", lambda e, dst=dst, st=st: e.tensor_tensor(out=dst[:, st * ST:(st + 1) * ST], in0=t1[:, :], in1=t2[:, :], op=ALU.add),
                     reads=["t1", "t2"], writes=[dk])
            P.op("pool", lambda e, st=st: e.tensor_tensor(out=sqk[:, :], in0=KT[:, st * ST:(st + 1) * ST], in1=KT[:, st * ST:(st + 1) * ST], op=ALU.mult),
                 reads=["KT"], writes=["sqk"])
            P.op("pe", lambda e: e.matmul(pN[0:64, :], lhsT=hsel[:, :], rhs=sqk[:, :], start=True, stop=True), reads=["hsel", "sqk"], writes=["pSS"])
            P.op("dve", lambda e: e.tensor_reduce(out=kmx[:, 1:2], in_=pN[0:64, :], axis=AX.X, op=ALU.max), reads=["pSS", "kmx"], writes=["kmx"])
            P.op("dve", lambda e: e.tensor_tensor(out=kmx[:, 0:1], in0=kmx[:, 0:1], in1=kmx[:, 1:2], op=ALU.max), reads=["kmx"], writes=["kmx"])
            for tt in range(4):
                for k in range(8):
                    P.op("pe", lambda e, k=k, tt=tt, hT=hT: e.matmul(pV[:, tt * 128:(tt + 1) * 128], lhsT=hT[:, k, tt * 128:(tt + 1) * 128],
                                                              rhs=wb[:, k, 512:640], start=(k == 0), stop=(k == 7)),
                         reads=["wb", "hTw"], writes=["pV"])
            for tt in range(4):
                tile = st * 4 + tt
                P.op("act", lambda e, tt=tt, tile=tile: e.activation(out=Vaug[:, tile, :, 0:64], in_=pV[:, tt * 128:(tt + 1) * 128].rearrange("p (h d) -> p h d", h=2),
                                                                     func=AF.Copy, scale=valid[:, tile:tile + 1]),
                     reads=["pV", "valid"], writes=["Vaug"])
        P.op("act", lambda e: e.activation(out=kmx[:, 0:1], in_=kmx[:, 0:1], func=AF.Sqrt), reads=["kmx"], writes=["kmx"])
        LAG = 2
        units = []
        for jb in range(8, 24):
            for h in range(2):
                dms = list(range(-8, 9))
                groups = [dms[i:i + 4] for i in range(0, len(dms), 4)]
                base = 0
                for gi, grp in enumerate(groups):
                    units.append(dict(jb=jb, h=h, grp=grp, gi=gi, base=base, last=(gi == len(groups) - 1)))
                    base += len(grp)

        def emit_scores(u):
            jb, h, grp = u["jb"], u["h"], u["grp"]
            qs = slice(jb * 128, (jb + 1) * 128)
            hs = slice(64 * h, 64 * h + 64)
            if h == 0 and u["gi"] == 0:
                P.op("pool", lambda e, qs=qs: e.tensor_tensor(out=sqk[:, 0:128], in0=QT[:, qs], in1=QT[:, qs], op=ALU.mult), reads=["QT"], writes=["sqk"])
                P.op("pe", lambda e: e.matmul(pN[0:64, 0:128], lhsT=hsel[:, :], rhs=sqk[:, 0:128], start=True, stop=True), reads=["hsel", "sqk"], writes=["pSS"])
                P.op("act", lambda e: e.activation(out=qn[:, :], in_=pN[0:64, 0:128], func=AF.Sqrt), reads=["pSS"], writes=["qn"])
                P.op("dve", lambda e: e.tensor_scalar(out=negm[:, :], in0=qn[:, :], scalar1=kmx[:, 0:1], scalar2=-1.0, op0=ALU.mult, op1=ALU.mult),
                     reads=["qn", "kmx"], writes=["negm"])
            g = cnt[0]
            cnt[0] += 1
            psx, psk = [(pS[0], "pS0"), (pS[1], "pS1"), (pA, "pA"), (pB, "pB")][g % 4]
            ptx, ptk = PT[g % 4], "PT%d" % (g % 4)
            u["ptx"], u["ptk"] = ptx, ptk
            n = len(grp)
            for i, dm in enumerate(grp):
                kc = jb + dm
                P.op("pe", lambda e, i=i, kc=kc, hs=hs, qs=qs, psx=psx: e.matmul(psx[:, i * 128:(i + 1) * 128], lhsT=KT[hs, kc * 128:(kc + 1) * 128],
                                                                                  rhs=QT[hs, qs], start=True, stop=False),
                     reads=["KT", "QT"], writes=[psk])
                P.op("pe", lambda e, i=i, h=h, psx=psx: e.matmul(psx[:, i * 128:(i + 1) * 128], lhsT=onesrow[32 * h:32 * h + 1, :],
                                                                 rhs=negm[32 * h:32 * h + 1, :], start=False, stop=True),
                     reads=["onesrow", "negm"], writes=[psk])
            P.op("act", lambda e, psx=psx, ptx=ptx, n=n: e.activation(out=ptx[:, 0:n * 128], in_=psx[:, 0:n * 128], func=AF.Exp, scale=0.125),
                 reads=[psk], writes=[ptk])
            m0 = (grp[0] + 8) * 128
            P.op("dve" if g % 2 == 0 else "pool", lambda e, ptx=ptx, n=n, m0=m0: e.tensor_tensor(
                out=ptx[:, 0:n * 128], in0=ptx[:, 0:n * 128], in1=Mall[:, m0:m0 + n * 128], op=ALU.mult),
                reads=[ptk, "Mall"], writes=[ptk])

        def emit_pv(u):
            jb, h, grp = u["jb"], u["h"], u["grp"]
            ptx, ptk = u["ptx"], u["ptk"]
            AO, aok = ao[jb % 2], "ao%d" % (jb % 2)
            PO, pok = pO[jb % 2], "pO%d" % (jb % 2)
            for i, dm in enumerate(grp):
                kc = jb + dm
                nmm = u["base"] + i
                P.op("pe", lambda e, i=i, kc=kc, h=h, ptx=ptx, PO=PO, first=(nmm == 0), last=(nmm == 16): e.matmul(
                    PO[:, h, 0:65], lhsT=ptx[:, i * 128:(i + 1) * 128], rhs=Vaug[:, kc, h, :], start=first, stop=last),
                    reads=[ptk, "Vaug"], writes=[pok])
            if u["last"]:
                P.op("dve", lambda e, h=h, PO=PO: e.reciprocal(out=rec[:, h:h + 1], in_=PO[:, h, 64:65]), reads=[pok], writes=["rec%d" % h])
                P.op("dve", lambda e, h=h, PO=PO, AO=AO: e.tensor_scalar(out=AO[:, 64 * h:64 * h + 64], in0=PO[:, h, 0:64], scalar1=rec[:, h:h + 1],
                                                                         scalar2=None, op0=ALU.mult),
                     reads=[pok, "rec%d" % h], writes=[aok])
                if h == 1:
                    P.op("pe", lambda e, AO=AO: e.transpose(out=pV[:, 0:128], in_=AO[:, :], identity=identf[:, :]), reads=[aok, "identf"], writes=["pV"])
                    P.op("act", lambda e, hp=hp, jb=jb: e.activation(out=mixT[:, hp, (jb - 8) * 128:(jb - 7) * 128], in_=pV[:, 0:128], func=AF.Copy),
                         reads=["pV"], writes=["mixT"])

        pending = []
        for u in units:
            emit_scores(u)
            pending.append(u)
            if len(pending) > LAG:
                emit_pv(pending.pop(0))
        while pending:
            emit_pv(pending.pop(0))
    P.build()


def emit_hyproj_f(nc, outer, D):
    P = Prog(nc, sem_stack=outer, prefix="B_")
    ST = 256
    H = _p1_common_f(P, D, 1536, ("w_incH", [1024, 1536]), "xT", NTOK, ST=ST, SW=128)
    _cast_w(P, H, H["w_inc"].rearrange("(k p) n -> p k n", p=128), 1536)
    Us = D("Us", [12, 128, NTOK], kind="Internal", dt=BF16)
    sb, ps = P.sb, P.ps
    wb, hT = H["wb"], H["hT"]
    ub = [sb("ub%d" % i, [128, 12, ST], BF16) for i in range(2)]
    pU = [ps("pU%d" % i, [128, 512], F32) for i in range(4)]
    nst = NTOK // ST
    _p1_load(P, H, 0)
    nxt = _p1_norm_r(P, H, 0)
    for st in range(nst):
        hT, hk = nxt
        if st + 1 < nst:
            _p1_load(P, H, st + 1)
            nxt = _p1_norm_r(P, H, st + 1)
        UB, ubk = ub[st % 2], "ub%d" % (st % 2)
        for pr in range(6):
            pu, puk = pU[pr % 4], "pU%d" % (pr % 4)
            for half in range(2):
                i = 2 * pr + half
                for k in range(8):
                    P.op("pe", lambda e, k=k, i=i, half=half, pu=pu, hT=hT: e.matmul(pu[:, half * ST:(half + 1) * ST], lhsT=wb[:, k, i * 128:(i + 1) * 128], rhs=hT[:, k, :],
                                                                                    start=(k == 0), stop=(k == 7)), reads=["wb", hk], writes=[puk])
            eng = "act" if pr % 2 == 0 else "dve"
            if eng == "act":
                P.op("act", lambda e, pr=pr, pu=pu, UB=UB: e.activation(out=UB[:, 2 * pr:2 * pr + 2, :], in_=pu[:, :].rearrange("p (i t) -> p i t", i=2), func=AF.Copy),
                     reads=[puk], writes=[ubk])
            else:
                P.op("dve", lambda e, pr=pr, pu=pu, UB=UB: e.tensor_copy(out=UB[:, 2 * pr:2 * pr + 2, :], in_=pu[:, :].rearrange("p (i t) -> p i t", i=2)),
                     reads=[puk], writes=[ubk])
        P.dma("pool", Us[:, :, st * ST:(st + 1) * ST].rearrange("i p t -> p i t"), UB[:, :, :], reads=[ubk], writes=["Us"])
    P.build()


def emit_hyena_f(nc, outer, D, mixT, groups=(0, 1, 2, 3), prefix="C_"):
    P = Prog(nc, sem_stack=outer, prefix=prefix)
    sb, ps = P.sb, P.ps
    Us = D("Us", [12, 128, NTOK], kind="Internal", dt=BF16)
    dF1cat, dTrTr, dTiTi, dR12 = D("F1cat", [128, 512]), D("TrTr", [128, 512]), D("TiTi", [128, 512]), D("R12", [128, 512])
    dTT2r, dTT2i, dIF1, dSwin = D("TT2r", [128, 512]), D("TT2i", [128, 512]), D("IF1", [128, 512]), D("Swin", [128, 64])
    dzemb = D("zembT", [33, NTOK])
    ddelta = D("delta4", [128, 512])
    dhbcol = D("hbcol4", [128, 512])
    dfw1, dfw2, dfw3, dfpar = D("fw1", [33, 64]), D("fw2", [64, 64]), D("fw3c4", [64, 2048]), D("fpar", [64, 4])
    dcw, dcb = D("cw4", [128, 36]), D("cb4", [128, 12])
    dsel = D("seld", [128, 32])
    didn = D("identd", [128, 128])

    U = [sb("U%d" % i, [128, NTOK + 2], BF16) for i in range(3)]
    CV = sb("CV", [128, NTOK], BF16)
    Ob = sb("Ob", [128, NTOK], BF16)
    h2T = sb("h2T", [64, NTOK], BF16)
    ApF = sb("ApF", [128, 2, NPQ, 256], BF16)
    ApD = sb("ApD", [128, 2, NPQ, 256], BF16)
    HsL = [sb("Hs%d" % i, [128, 2, NPQ, 256], BF16) for i in range(2)]
    Yb = sb("Yb", [128, 2, NPQ, 256], BF16)
    Bp = sb("Bp", [128, 2, 2, NPQ, 128], BF16)
    W1 = [sb("W1_%d" % i, [128, 512], F32) for i in range(2)]
    W2 = [sb("W2_%d" % i, [128, 512], F32) for i in range(2)]
    cpy = [sb("cpy%d" % i, [128, 512], F32) for i in range(2)]
    cpy2 = sb("cpy2", [128, 512], F32)
    tmpc = sb("tmpc", [128, 1024], F32)
    arg = tmpc[0:64, 0:512]
    h1 = tmpc[0:64, 512:1024]
    argi = sb("argi", [64, 512], mybir.dt.int32)
    F1b, R12b, IF1b = sb("F1b", [128, 512], BF16), sb("R12b", [128, 512], BF16), sb("IF1b", [128, 512], BF16)
    TrTr, TiTi = sb("TrTr", [128, 512], F32), sb("TiTi", [128, 512], F32)
    TT2r, TT2i = sb("TT2r", [128, 512], F32), sb("TT2i", [128, 512], F32)
    Swin = sb("Swin", [128, 64], F32)
    delta = sb("delta", [128, 512], F32)
    hbcol = sb("hbcol", [128, 512], F32)
    fw1, fw2 = sb("fw1", [33, 64], F32), sb("fw2", [64, 64], F32)
    fw3b = sb("fw3b", [64, 2048], BF16)
    fpar = sb("fpar", [64, 8], F32)
    cw, cb = sb("cw", [128, 36], F32), sb("cb", [128, 12], F32)
    selb = sb("selb", [128, 32], BF16)
    ident = sb("ident", [128, 128], BF16)
    zc = [sb("zc%d" % i, [33, 512], F32) for i in range(2)]
    win = [sb("win%d" % i, [128, 128], F32) for i in range(2)]
    eoc = [sb("eoc%d" % i, [128, 256], F32) for i in range(2)]
    fw3f = sb("fw3f", [64, 1024], BF16)

    pSS = ps("pSS", [128, 512], F32)
    pA1 = [ps("pA1_%d" % i, [128, 512], F32) for i in range(2)]
    pXr, pXi = ps("pXr", [128, 512], F32), ps("pXi", [128, 512], F32)
    pB = ps("pB", [128, 512], F32)
    pY = ps("pY", [128, 512], F32)
    pTz = ps("pTz", [128, 8, 128], BF16)

    def ldcast(dst, dkey, src, n=512, parts=128):
        P.dma("sp", cpy2[0:parts, 0:n], src, writes=["cpy2"])
        P.op("dve", lambda e: e.tensor_copy(out=dst, in_=cpy2[0:parts, 0:n]), reads=["cpy2"], writes=[dkey])
    ldcast(F1b[:, :], "F1b", dF1cat[:, :])
    ldcast(R12b[:, :], "R12b", dR12[:, :])
    ldcast(IF1b[:, :], "IF1b", dIF1[:, :])
    ldcast(ident[:, :], "ident", didn[:, :], n=128)
    ldcast(selb[:, :], "selb", dsel[:, :], n=32)
    for q4 in range(4):
        P.dma("sp", cpy2[0:64, :], dfw3[:, q4 * 512:(q4 + 1) * 512], writes=["cpy2"])
        for o_ in range(2):
            wf = cpy2[0:64, o_ * 256:o_ * 256 + 128]
            wbk = cpy2[0:64, o_ * 256 + 128:o_ * 256 + 256]
            c0_ = q4 * 512 + o_ * 256
            P.op("dve", lambda e, wf=wf, wbk=wbk, c0_=c0_: e.tensor_tensor(out=fw3b[:, c0_:c0_ + 128], in0=wf, in1=wbk, op=ALU.add), reads=["cpy2"], writes=["fw3b"])
            P.op("dve", lambda e, wf=wf, wbk=wbk, c0_=c0_: e.tensor_tensor(out=fw3b[:, c0_ + 128:c0_ + 256], in0=wf, in1=wbk, op=ALU.subtract), reads=["cpy2"], writes=["fw3b"])
            P.op("dve", lambda e, wf=wf, q4=q4, o_=o_: e.tensor_copy(out=fw3f[:, (2 * q4 + o_) * 128:(2 * q4 + o_ + 1) * 128], in_=wf), reads=["cpy2"], writes=["fw3f"])
    for dst, key, src in ((TrTr, "TrTr", dTrTr), (TiTi, "TiTi", dTiTi), (TT2r, "TT2r", dTT2r), (TT2i, "TT2i", dTT2i), (Swin, "Swin", dSwin),
                          (delta, "delta", ddelta), (hbcol, "hbcol", dhbcol), (fw1, "fw1", dfw1), (fw2, "fw2", dfw2), (cw, "cw", dcw), (cb, "cb", dcb)):
        P.dma("sp", dst[:, :], src[:, :], writes=[key])
    P.dma("sp", fpar[:, 0:4], dfpar[:, :], writes=["fpar"])
    i2p = 1.0 / (2.0 * math.pi)
    for (bc, fc, o0) in ((0, 1, 4), (2, 3, 6)):
        P.op("dve", lambda e, bc=bc, fc=fc, o0=o0: e.tensor_tensor(out=fpar[:, o0 + 1:o0 + 2], in0=fpar[:, bc:bc + 1], in1=fpar[:, fc:fc + 1], op=ALU.mult),
             reads=["fpar"], writes=["fpar"])
        P.op("dve", lambda e, o0=o0: e.tensor_scalar(out=fpar[:, o0 + 1:o0 + 2], in0=fpar[:, o0 + 1:o0 + 2], scalar1=i2p, scalar2=16.0, op0=ALU.mult, op1=ALU.add),
             reads=["fpar"], writes=["fpar"])
        P.op("dve", lambda e, fc=fc, o0=o0: e.tensor_scalar(out=fpar[:, o0:o0 + 1], in0=fpar[:, fc:fc + 1], scalar1=i2p, scalar2=None, op0=ALU.mult),
             reads=["fpar"], writes=["fpar"])

    for ch in range(16):
        zt, zk = zc[ch % 2], "zc%d" % (ch % 2)
        P.dma("sp", zt[:, :], dzemb[:, ch * 512:(ch + 1) * 512], writes=[zk])
        for layer in range(2):
            if layer == 0:
                P.op("pe", lambda e, zt=zt: e.matmul(pSS[0:64, :], lhsT=fw1[:, :], rhs=zt[:, :], start=True, stop=True), reads=["fw1", zk], writes=["pSS"])
            else:
                P.op("pe", lambda e: e.matmul(pSS[0:64, :], lhsT=fw2[:, :], rhs=h1[:, :], start=True, stop=True), reads=["fw2", "tmpc"], writes=["pSS"])
            fr, fb = (4, 5) if layer == 0 else (6, 7)
            P.op("dve", lambda e, fr=fr, fb=fb: e.tensor_scalar(out=arg[:, :], in0=pSS[0:64, :], scalar1=fpar[:, fr:fr + 1], scalar2=fpar[:, fb:fb + 1],
                                                                op0=ALU.mult, op1=ALU.add), reads=["pSS", "fpar"], writes=["tmpc"])
            P.op("dve", lambda e: e.tensor_copy(out=argi[:, :], in_=arg[:, :]), reads=["tmpc"], writes=["argi"])
            P.op("dve", lambda e: e.tensor_copy(out=h1[:, :], in_=argi[:, :]), reads=["argi", "tmpc"], writes=["tmpc"])
            P.op("dve", lambda e: e.tensor_tensor(out=arg[:, :], in0=arg[:, :], in1=h1[:, :], op=ALU.subtract), reads=["tmpc"], writes=["tmpc"])
            P.op("dve", lambda e: e.scalar_tensor_tensor(out=arg[:, :], in0=arg[:, :], scalar=0.5, in1=arg[:, :], op0=ALU.is_gt, op1=ALU.subtract),
                 reads=["tmpc"], writes=["tmpc"])
            if layer == 0:
                P.op("act", lambda e: e.activation(out=h1[:, :], in_=arg[:, :], func=AF.Sin, scale=-2.0 * math.pi), reads=["tmpc"], writes=["tmpc"])
            else:
                P.op("act", lambda e, ch=ch: e.activation(out=h2T[:, ch * 512:(ch + 1) * 512], in_=arg[:, :], func=AF.Sin, scale=-2.0 * math.pi),
                     reads=["tmpc"], writes=["h2T"])

    Zkeys = lambda i: ["Z%d_%d" % (i, s) for s in range(128 // (2 * NPQ))]
    Z = [U[i][:, 0:NTOK].rearrange("a (c p) -> a c p", p=64) for i in range(3)]
    CVs = CV[:, :].rearrange("c (a p) -> c a p", p=64)
    E3 = CV[:, :].rearrange("a (c p) -> a c p", p=64)
    O3 = Ob[:, :].rearrange("a (c p) -> a c p", p=64)
    h2s = h2T[:, :].rearrange("j (a p) -> j a p", p=64)
    cnt = [0]

    def twiddle(psrc, pkey, Tr_, Ti_, trk, tik, outr, outi, okey, view):
        g = cnt[0]
        cnt[0] += 1
        w1, w1k = W1[g % 2], "W1_%d" % (g % 2)
        w2, w2k = W2[g % 2], "W2_%d" % (g % 2)
        cp_, cpk = cpy[g % 2], "cpy%d" % (g % 2)
        P.op("act", lambda e: e.activation(out=cp_[:, :], in_=psrc[:, :], func=AF.Copy), reads=[pkey], writes=[cpk])
        P.op("dve", lambda e: e.tensor_tensor(out=w1[:, :], in0=psrc[:, :], in1=Tr_[:, :], op=ALU.mult), reads=[pkey, trk], writes=[w1k])
        P.op("pool", lambda e: e.tensor_tensor(out=w2[:, :], in0=cp_[:, :], in1=Ti_[:, :], op=ALU.mult), reads=[cpk, tik], writes=[w2k])
        w1r, w1i = view(w1)
        w2r, w2i = view(w2)
        P.op("dve", lambda e: e.tensor_tensor(out=outr, in0=w1r, in1=w2i, op=ALU.subtract), reads=[w1k, w2k], writes=[okey])
        P.op("dve", lambda e: e.tensor_tensor(out=outi, in0=w2r, in1=w1i, op=ALU.add), reads=[w1k, w2k], writes=[okey])

    v_fwd = lambda t: (t[:, 0:256], t[:, 256:512])
    v_inv = lambda t: (t[:, :].rearrange("k (c r x) -> k c r x", c=2, r=2)[:, :, 0, :], t[:, :].rearrange("k (c r x) -> k c r x", c=2, r=2)[:, :, 1, :])

    def fwd_stage1(src3, skey, c0, Ap, apk):
        for q in range(NPQ):
            g = cnt[0]
            pa, pak = pA1[g % 2], "pA1_%d" % (g % 2)
            c = c0 + 2 * q
            P.op("pe", lambda e, c=c, pa=pa: e.matmul(pa[:, :], lhsT=src3[:, c:c + 2, :], rhs=F1b[:, :], start=True, stop=True),
                 reads=[skey, "F1b"], writes=[pak])
            twiddle(pa, pak, TrTr, TiTi, "TrTr", "TiTi", Ap[:, 0, q, :], Ap[:, 1, q, :], apk, v_fwd)

    G2r, G2in, G2i = R12b[:, 0:128], R12b[:, 128:256], R12b[:, 256:384]
    R1, R2 = R12b[:, 0:256], R12b[:, 256:512]

    for grp4 in groups:
        for i in range(3):
            P.dma("sp", U[i][:, 1:NTOK + 1], Us[3 * grp4 + i], reads=["Us"], writes=["U%d" % i] + Zkeys(i), key="U%d" % i)
            P.op("pool", lambda e, i=i: e.memset(U[i][:, 0:1], 0.0), reads=["U%d" % i], writes=["U%d" % i] + Zkeys(i))
            P.op("pool", lambda e, i=i: e.memset(U[i][:, NTOK + 1:NTOK + 2], 0.0), reads=["U%d" % i], writes=["U%d" % i])
        for i in range(3):
            ci = 9 * grp4 + 3 * i
            bi = 3 * grp4 + i
            for ch in range(8):
                j0 = ch * 1024
                P.op("dve", lambda e, i=i, j0=j0, ci=ci, bi=bi: e.tensor_scalar(out=tmpc[:, :], in0=U[i][:, j0:j0 + 1024], scalar1=cw[:, ci:ci + 1],
                                                                                scalar2=cb[:, bi:bi + 1], op0=ALU.mult, op1=ALU.add),
                     reads=["U%d" % i, "cw", "cb"], writes=["tmpc"])
                P.op("dve", lambda e, i=i, j0=j0, ci=ci: e.scalar_tensor_tensor(out=tmpc[:, :], in0=U[i][:, j0 + 1:j0 + 1025], scalar=cw[:, ci + 1:ci + 2],
                                                                                in1=tmpc[:, :], op0=ALU.mult, op1=ALU.add),
                     reads=["U%d" % i, "cw", "tmpc"], writes=["tmpc"])
                P.op("dve", lambda e, i=i, j0=j0, ci=ci: e.scalar_tensor_tensor(out=CV[:, j0:j0 + 1024], in0=U[i][:, j0 + 2:j0 + 1026], scalar=cw[:, ci + 2:ci + 3],
                                                                                in1=tmpc[:, :], op0=ALU.mult, op1=ALU.add),
                     reads=["U%d" % i, "cw", "tmpc"], writes=["CV"])
            for pg in range(8):
                for pi in range(8):
                    p = pg * 8 + pi
                    P.op("pe", lambda e, p=p, pi=pi: e.transpose(out=pTz[:, pi, :], in_=CVs[:, :, p], identity=ident[:, :]),
                         reads=["CV", "ident"], writes=["pTz"])
                P.op("act", lambda e, i=i, pg=pg: e.activation(out=Z[i][:, :, pg * 8:(pg + 1) * 8].rearrange("a c p -> a p c"), in_=pTz[:, :, :], func=AF.Copy),
                     reads=["pTz"], writes=["U%d" % i] + Zkeys(i))
        for o in range(2):
            w3c0 = grp4 * 512 + o * 256
            for p in range(64):
                pe_, pek = (pSS, "pSS") if p % 2 == 0 else (pB, "pB")
                wn, wnk = win[p % 2], "win%d" % (p % 2)
                ec, eck = eoc[p % 2], "eoc%d" % (p % 2)
                P.op("pe", lambda e, p=p, w3c0=w3c0, pe_=pe_: e.matmul(pe_[:, 0:256], lhsT=h2s[:, :, p], rhs=fw3b[:, w3c0:w3c0 + 256], start=True, stop=True),
                     reads=["h2T", "fw3b"], writes=[pek])
                P.op("act", lambda e, p=p, grp4=grp4, wn=wn: e.activation(out=wn[:, :], in_=delta[:, grp4 * 128:(grp4 + 1) * 128], func=AF.Exp, scale=Swin[:, p:p + 1]),
                     reads=["delta", "Swin"], writes=[wnk])
                P.op("act", lambda e, pe_=pe_, ec=ec: e.activation(out=ec[:, :], in_=pe_[:, 0:256], func=AF.Copy), reads=[pek], writes=[eck])
                P.op("pool", lambda e, p=p, ec=ec, wn=wn: e.tensor_tensor(out=E3[:, :, p], in0=ec[:, 0:128], in1=wn[:, :], op=ALU.mult), reads=[eck, wnk], writes=["CV"])
                P.op("pool", lambda e, p=p, ec=ec, wn=wn: e.tensor_tensor(out=O3[:, :, p], in0=ec[:, 128:256], in1=wn[:, :], op=ALU.mult), reads=[eck, wnk], writes=["Ob"])
                if p == 0:
                    fc = (2 * grp4 + o) * 128
                    P.op("pe", lambda e, fc=fc: e.matmul(pY[0:1, 0:128], lhsT=h2T[:, 0:1], rhs=fw3f[:, fc:fc + 128], start=True, stop=True),
                         reads=["h2T", "fw3f"], writes=["pY"])
                    P.op("dve", lambda e, wn=wn: e.tensor_tensor(out=E3[0:1, :, 0], in0=pY[0:1, 0:128], in1=wn[0:1, :], op=ALU.mult), reads=["pY", wnk], writes=["CV"])
                    P.op("dve", lambda e, wn=wn: e.tensor_tensor(out=O3[0:1, :, 0], in0=pY[0:1, 0:128], in1=wn[0:1, :], op=ALU.mult), reads=["pY", wnk], writes=["Ob"])
            nsbt = 128 // (2 * NPQ)

            def filt_gen(sbt):
                c0 = sbt * 2 * NPQ
                Hs, hk = HsL[sbt % 2], "Hs%d" % (sbt % 2)
                for which, (src3, skey) in enumerate(((E3, "CV"), (O3, "Ob"))):
                    fwd_stage1(src3, skey, c0, ApF, "ApF")
                    yield
                    for qq in range(NPQ // 2):
                        rr = ApF[:, 0, 2 * qq:2 * qq + 2, :]
                        ri = ApF[:, 1, 2 * qq:2 * qq + 2, :]
                        if which == 0:
                            P.op("pe", lambda e, rr=rr: e.matmul(pSS[:, :], lhsT=G2r, rhs=rr, start=True, stop=False), reads=["R12b", "ApF"], writes=["pSS"])
                            P.op("pe", lambda e, ri=ri: e.matmul(pSS[:, :], lhsT=G2in, rhs=ri, start=False, stop=True), reads=["R12b", "ApF"], writes=["pSS"])
                            for j in range(2):
                                hcol = grp4 * 128 + o * 64 + sbt * NPQ + 2 * qq + j
                                P.op("act", lambda e, j=j, qq=qq, hcol=hcol, Hs=Hs: e.activation(out=Hs[:, 0, 2 * qq + j, :], in_=pSS[:, j * 256:(j + 1) * 256], func=AF.Identity,
                                                                                                 bias=hbcol[:, hcol:hcol + 1], scale=1.0),
                                     reads=["pSS", "hbcol"], writes=[hk])
                        else:
                            P.op("pe", lambda e, rr=rr: e.matmul(pSS[:, :], lhsT=G2i, rhs=rr, start=True, stop=False), reads=["R12b", "ApF"], writes=["pSS"])
                            P.op("pe", lambda e, ri=ri: e.matmul(pSS[:, :], lhsT=G2r, rhs=ri, start=False, stop=True), reads=["R12b", "ApF"], writes=["pSS"])
                            P.op("act", lambda e, qq=qq, Hs=Hs: e.activation(out=Hs[:, 1, 2 * qq:2 * qq + 2, :], in_=pSS[:, :].rearrange("k (q x) -> k q x", q=2), func=AF.Copy),
                                 reads=["pSS"], writes=[hk])
                    yield

            def data_gen(sbt):
                c0 = sbt * 2 * NPQ
                zk = "Z0_%d" % sbt
                Hs, hk = HsL[sbt % 2], "Hs%d" % (sbt % 2)
                fwd_stage1(Z[0], zk, c0, ApD, "ApD")
                yield
                for qq in range(NPQ // 2):
                    rr = ApD[:, 0, 2 * qq:2 * qq + 2, :]
                    ri = ApD[:, 1, 2 * qq:2 * qq + 2, :]
                    P.op("pe", lambda e, rr=rr: e.matmul(pXr[:, :], lhsT=G2r, rhs=rr, start=True, stop=False), reads=["R12b", "ApD"], writes=["pXr"])
                    P.op("pe", lambda e, ri=ri: e.matmul(pXr[:, :], lhsT=G2in, rhs=ri, start=False, stop=True), reads=["R12b", "ApD"], writes=["pXr"])
                    P.op("pe", lambda e, rr=rr: e.matmul(pXi[:, :], lhsT=G2i, rhs=rr, start=True, stop=False), reads=["R12b", "ApD"], writes=["pXi"])
                    P.op("pe", lambda e, ri=ri: e.matmul(pXi[:, :], lhsT=G2r, rhs=ri, start=False, stop=True), reads=["R12b", "ApD"], writes=["pXi"])
                    g = cnt[0]
                    cnt[0] += 1
                    w1, w1k = W1[g % 2], "W1_%d" % (g % 2)
                    w2, w2k = W2[g % 2], "W2_%d" % (g % 2)
                    cx, cxk = cpy[g % 2], "cpy%d" % (g % 2)
                    hr = Hs[:, 0, 2 * qq:2 * qq + 2, :].rearrange("k q x -> k (q x)")
                    hi = Hs[:, 1, 2 * qq:2 * qq + 2, :].rearrange("k q x -> k (q x)")
                    yr = Yb[:, 0, 2 * qq:2 * qq + 2, :].rearrange("k q x -> k (q x)")
                    yi = Yb[:, 1, 2 * qq:2 * qq + 2, :].rearrange("k q x -> k (q x)")
                    P.op("act", lambda e, cx=cx: e.activation(out=cx[:, :], in_=pXi[:, :], func=AF.Copy), reads=["pXi"], writes=[cxk])
                    P.op("act", lambda e: e.activation(out=cpy2[:, :], in_=pXr[:, :], func=AF.Copy), reads=["pXr"], writes=["cpy2"])
                    P.op("dve", lambda e, w1=w1, hr=hr: e.tensor_tensor(out=w1[:, :], in0=pXr[:, :], in1=hr, op=ALU.mult), reads=["pXr", hk], writes=[w1k])
                    P.op("pool", lambda e, w2=w2, cx=cx, hi=hi: e.tensor_tensor(out=w2[:, :], in0=cx[:, :], in1=hi, op=ALU.mult), reads=[cxk, hk], writes=[w2k])
                    P.op("dve", lambda e, w1=w1, w2=w2, yr=yr: e.tensor_tensor(out=yr, in0=w1[:, :], in1=w2[:, :], op=ALU.subtract), reads=[w1k, w2k], writes=["Yb"])
                    P.op("dve", lambda e, w1=w1, hr=hr: e.tensor_tensor(out=w1[:, :], in0=pXi[:, :], in1=hr, op=ALU.mult), reads=["pXi", hk, "Yb"], writes=[w1k])
                    P.op("pool", lambda e, w2=w2, hi=hi: e.tensor_tensor(out=w2[:, :], in0=cpy2[:, :], in1=hi, op=ALU.mult), reads=["cpy2", hk, "Yb"], writes=[w2k])
                    P.op("dve", lambda e, w1=w1, w2=w2, yi=yi: e.tensor_tensor(out=yi, in0=w1[:, :], in1=w2[:, :], op=ALU.add), reads=[w1k, w2k], writes=["Yb"])
                yield
                for q in range(NPQ):
                    for kc in range(2):
                        P.op("pe", lambda e, q=q, kc=kc: e.matmul(pB[:, kc * 256:(kc + 1) * 256], lhsT=Yb[:, 0, q, kc * 128:(kc + 1) * 128], rhs=R1, start=True, stop=False),
                             reads=["Yb", "R12b"], writes=["pB"])
                        P.op("pe", lambda e, q=q, kc=kc: e.matmul(pB[:, kc * 256:(kc + 1) * 256], lhsT=Yb[:, 1, q, kc * 128:(kc + 1) * 128], rhs=R2, start=False, stop=True),
                             reads=["Yb", "R12b"], writes=["pB"])
                    twiddle(pB, "pB", TT2r, TT2i, "TT2r", "TT2i", Bp[:, :, 0, q, :], Bp[:, :, 1, q, :], "Bp", v_inv)
                yield
                for hh in range(NPQ // 4):
                    n = 0
                    for kc in range(2):
                        for r in range(2):
                            rhs = Bp[:, kc, r, 4 * hh:4 * hh + 4, :]
                            lt = IF1b[:, (2 * kc + r) * 128:(2 * kc + r + 1) * 128]
                            P.op("pe", lambda e, rhs=rhs, lt=lt, n=n: e.matmul(pY[:, :], lhsT=lt, rhs=rhs, start=(n == 0), stop=(n == 3)),
                                 reads=["Bp", "IF1b"], writes=["pY"])
                            n += 1
                    cc = c0 + 8 * hh
                    zi = 1 if o == 0 else 2
                    zo = 0 if o == 0 else 2
                    okeys = [zk] if o == 0 else ["U2", "Z2_%d" % sbt]
                    P.op("dve", lambda e, cc=cc, zi=zi, zo=zo: e.scalar_tensor_tensor(out=Z[zo][:, cc:cc + 8, :], in0=pY[:, :].rearrange("a (c p) -> a c p", p=64),
                                                                                      scalar=1.0 / NFFT, in1=Z[zi][:, cc:cc + 8, :], op0=ALU.mult, op1=ALU.mult),
                         reads=["pY", "U%d" % zi] + Zkeys(zi), writes=okeys)
                yield

            for _ in filt_gen(0):
                pass
            for sbt in range(nsbt):
                gens = [data_gen(sbt)] + ([filt_gen(sbt + 1)] if sbt + 1 < nsbt else [])
                while gens:
                    for gq in list(gens):
                        try:
                            next(gq)
                        except StopIteration:
                            gens.remove(gq)
        mv = mixT[:, 4 + grp4, :].rearrange("c (a p) -> c a p", p=64)
        for pg in range(4):
            for pi in range(16):
                p = pg * 16 + pi
                P.op("pe", lambda e, p=p, pi=pi: e.matmul(pXr[:, pi * 32:(pi + 1) * 32], lhsT=Z[2][:, :, p], rhs=selb[:, :], start=True, stop=True),
                     reads=["U2", "selb"] + Zkeys(2), writes=["pXr"])
            P.op("act", lambda e, pg=pg, mv=mv: e.activation(out=mv[:, :, pg * 16:(pg + 1) * 16].rearrange("c a p -> c p a"),
                                                             in_=pXr[:, :].rearrange("c (p a) -> c p a", a=32), func=AF.Copy),
                 reads=["pXr"], writes=["mixT"])
    P.build()


def emit_hyena_h(nc, outer, D, mixT, groups=(0, 1, 2, 3), prefix="C_"):
    P = Prog(nc, sem_stack=outer, prefix=prefix)
    sb, ps = P.sb, P.ps
    Us = D("Us", [12, 128, NTOK], kind="Internal", dt=BF16)
    dF1cat, dTrTr, dTiTi, dR12 = D("F1cat_h", [128, 256]), D("TrTr_h", [128, 512]), D("TiTi_h", [128, 512]), D("R12", [128, 512])
    dTT2r, dTT2i, dIF1, dSwin = D("TT2r_h", [128, 512]), D("TT2i_h", [128, 512]), D("IF1_h", [128, 256]), D("Swin", [128, 64])
    dzemb = D("zembT", [33, NTOK])
    ddelta = D("delta4", [128, 512])
    dhbrow = D("hbrow", [1, 1024])
    dfw1, dfw2, dfw3, dfpar = D("fw1", [33, 64]), D("fw2", [64, 64]), D("fw3c4", [64, 2048]), D("fpar", [64, 4])
    dcw, dcb = D("cw4", [128, 36]), D("cb4", [128, 12])
    dsel = D("seld", [128, 32])
    didn = D("identd", [128, 128])

    U = [sb("U%d" % i, [128, NTOK + 2], BF16) for i in range(3)]
    CV = sb("CV", [128, NTOK], BF16)
    Ob = sb("Ob", [128, NTOK], BF16)
    h2T = sb("h2T", [64, NTOK], BF16)
    ApF = sb("ApF", [128, 2, NPQH, 128], BF16)
    ApD = sb("ApD", [128, 2, NPQH, 128], BF16)
    HsL = [sb("Hs%d" % i, [128, 2, NPQH, 128], BF16) for i in range(2)]
    Yb = sb("Yb", [128, 2, NPQH, 128], BF16)
    Bp = sb("Bp", [128, 2, NPQH, 128], BF16)
    W1 = [sb("W1_%d" % i, [128, 512], BF16) for i in range(4)]
    W2 = [sb("W2_%d" % i, [128, 512], BF16) for i in range(4)]
    cpy = [sb("cpr%d" % i, [128, 512], BF16) for i in range(4)]
    cpy2 = sb("cpy2", [128, 512], F32)
    tmpc = sb("tmpc", [128, 1024], F32)
    arg = tmpc[0:64, 0:512]
    h1 = tmpc[0:64, 512:1024]
    argi = sb("argi", [64, 512], mybir.dt.int32)
    F1b, R12b, IF1b = sb("F1b", [128, 256], BF16), sb("R12b", [128, 512], BF16), sb("IF1b", [128, 256], BF16)
    TrTr, TiTi = sb("TrTr", [128, 512], F32), sb("TiTi", [128, 512], F32)
    TT2r, TT2i = sb("TT2r", [128, 512], F32), sb("TT2i", [128, 512], F32)
    Swin = sb("Swin", [128, 64], F32)
    delta = sb("delta", [128, 512], F32)
    hbrow = sb("hbrow", [1, 1024], F32)
    etmp = sb("etmp", [1, 128], F32)
    fw1, fw2 = sb("fw1", [33, 64], F32), sb("fw2", [64, 64], F32)
    fw3b = sb("fw3b", [64, 2048], BF16)
    fpar = sb("fpar", [64, 8], F32)
    cw, cb = sb("cw", [128, 36], F32), sb("cb", [128, 12], F32)
    selb = sb("selb", [128, 32], BF16)
    ident = sb("ident", [128, 128], BF16)
    zc = [sb("zc%d" % i, [33, 512], F32) for i in range(2)]
    win = [sb("win%d" % i, [128, 128], F32) for i in range(2)]
    eoc = [sb("eoc%d" % i, [128, 256], F32) for i in range(2)]
    fw3f = sb("fw3f", [64, 1024], BF16)

    pSS = ps("pSS", [128, 512], F32)
    pA1 = [ps("pA1_%d" % i, [128, 512], F32) for i in range(2)]
    pXr, pXi = ps("pXr", [128, 512], F32), ps("pXi", [128, 512], F32)
    pB = ps("pB", [128, 512], F32)
    pY = ps("pY", [128, 512], F32)
    pTz = ps("pTz", [128, 8, 128], BF16)

    def ldcast(dst, dkey, src, n=512, parts=128):
        P.dma("sp", cpy2[0:parts, 0:n], src, writes=["cpy2"])
        P.op("dve", lambda e: e.tensor_copy(out=dst, in_=cpy2[0:parts, 0:n]), reads=["cpy2"], writes=[dkey])
    ldcast(F1b[:, :], "F1b", dF1cat[:, :], n=256)
    ldcast(R12b[:, :], "R12b", dR12[:, :])
    ldcast(IF1b[:, :], "IF1b", dIF1[:, :], n=256)
    ldcast(ident[:, :], "ident", didn[:, :], n=128)
    ldcast(selb[:, :], "selb", dsel[:, :], n=32)
    for q4 in range(4):
        P.dma("sp", cpy2[0:64, :], dfw3[:, q4 * 512:(q4 + 1) * 512], writes=["cpy2"])
        for o_ in range(2):
            wf = cpy2[0:64, o_ * 256:o_ * 256 + 128]
            wbk = cpy2[0:64, o_ * 256 + 128:o_ * 256 + 256]
            c0_ = q4 * 512 + o_ * 256
            P.op("dve", lambda e, wf=wf, wbk=wbk, c0_=c0_: e.tensor_tensor(out=fw3b[:, c0_:c0_ + 128], in0=wf, in1=wbk, op=ALU.add), reads=["cpy2"], writes=["fw3b"])
            P.op("dve", lambda e, wf=wf, wbk=wbk, c0_=c0_: e.tensor_tensor(out=fw3b[:, c0_ + 128:c0_ + 256], in0=wf, in1=wbk, op=ALU.subtract), reads=["cpy2"], writes=["fw3b"])
            P.op("dve", lambda e, wf=wf, q4=q4, o_=o_: e.tensor_copy(out=fw3f[:, (2 * q4 + o_) * 128:(2 * q4 + o_ + 1) * 128], in_=wf), reads=["cpy2"], writes=["fw3f"])
    for dst, key, src in ((TrTr, "TrTr", dTrTr), (TiTi, "TiTi", dTiTi), (TT2r, "TT2r", dTT2r), (TT2i, "TT2i", dTT2i), (Swin, "Swin", dSwin),
                          (delta, "delta", ddelta), (hbrow, "hbrow", dhbrow), (fw1, "fw1", dfw1), (fw2, "fw2", dfw2), (cw, "cw", dcw), (cb, "cb", dcb)):
        P.dma("sp", dst[:, :], src[:, :], writes=[key])
    P.dma("sp", fpar[:, 0:4], dfpar[:, :], writes=["fpar"])
    i2p = 1.0 / (2.0 * math.pi)
    for (bc, fc, o0) in ((0, 1, 4), (2, 3, 6)):
        P.op("dve", lambda e, bc=bc, fc=fc, o0=o0: e.tensor_tensor(out=fpar[:, o0 + 1:o0 + 2], in0=fpar[:, bc:bc + 1], in1=fpar[:, fc:fc + 1], op=ALU.mult),
             reads=["fpar"], writes=["fpar"])
        P.op("dve", lambda e, o0=o0: e.tensor_scalar(out=fpar[:, o0 + 1:o0 + 2], in0=fpar[:, o0 + 1:o0 + 2], scalar1=i2p, scalar2=16.0, op0=ALU.mult, op1=ALU.add),
             reads=["fpar"], writes=["fpar"])
        P.op("dve", lambda e, fc=fc, o0=o0: e.tensor_scalar(out=fpar[:, o0:o0 + 1], in0=fpar[:, fc:fc + 1], scalar1=i2p, scalar2=None, op0=ALU.mult),
             reads=["fpar"], writes=["fpar"])

    for ch in range(16):
        zt, zk = zc[ch % 2], "zc%d" % (ch % 2)
        P.dma("sp", zt[:, :], dzemb[:, ch * 512:(ch + 1) * 512], writes=[zk])
        for layer in range(2):
            if layer == 0:
                P.op("pe", lambda e, zt=zt: e.matmul(pSS[0:64, :], lhsT=fw1[:, :], rhs=zt[:, :], start=True, stop=True), reads=["fw1", zk], writes=["pSS"])
            else:
                P.op("pe", lambda e: e.matmul(pSS[0:64, :], lhsT=fw2[:, :], rhs=h1[:, :], start=True, stop=True), reads=["fw2", "tmpc"], writes=["pSS"])
            fr, fb = (4, 5) if layer == 0 else (6, 7)
            P.op("dve", lambda e, fr=fr, fb=fb: e.tensor_scalar(out=arg[:, :], in0=pSS[0:64, :], scalar1=fpar[:, fr:fr + 1], scalar2=fpar[:, fb:fb + 1],
                                                                op0=ALU.mult, op1=ALU.add), reads=["pSS", "fpar"], writes=["tmpc"])
            P.op("dve", lambda e: e.tensor_copy(out=argi[:, :], in_=arg[:, :]), reads=["tmpc"], writes=["argi"])
            P.op("dve", lambda e: e.tensor_copy(out=h1[:, :], in_=argi[:, :]), reads=["argi", "tmpc"], writes=["tmpc"])
            P.op("dve", lambda e: e.tensor_tensor(out=arg[:, :], in0=arg[:, :], in1=h1[:, :], op=ALU.subtract), reads=["tmpc"], writes=["tmpc"])
            P.op("dve", lambda e: e.scalar_tensor_tensor(out=arg[:, :], in0=arg[:, :], scalar=0.5, in1=arg[:, :], op0=ALU.is_gt, op1=ALU.subtract),
                 reads=["tmpc"], writes=["tmpc"])
            if layer == 0:
                P.op("act", lambda e: e.activation(out=h1[:, :], in_=arg[:, :], func=AF.Sin, scale=-2.0 * math.pi), reads=["tmpc"], writes=["tmpc"])
            else:
                P.op("act", lambda e, ch=ch: e.activation(out=h2T[:, ch * 512:(ch + 1) * 512], in_=arg[:, :], func=AF.Sin, scale=-2.0 * math.pi),
                     reads=["tmpc"], writes=["h2T"])

    Zkeys = lambda i: ["Z%d_%d" % (i, s) for s in range(128 // (2 * NPQH))]
    Z = [U[i][:, 0:NTOK].rearrange("a (c p) -> a c p", p=64) for i in range(3)]
    CVs = CV[:, :].rearrange("c (a p) -> c a p", p=64)
    E3 = CV[:, :].rearrange("a (c p) -> a c p", p=64)
    O3 = Ob[:, :].rearrange("a (c p) -> a c p", p=64)
    h2s = h2T[:, :].rearrange("j (a p) -> j a p", p=64)
    cnt = [0]

    def twiddle(psrc, pkey, Tr_, Ti_, trk, tik, outr, outi, okey, view, ieng="dve"):
        g = cnt[0]
        cnt[0] += 1
        w1, w1k = W1[g % 4], "W1_%d" % (g % 4)
        w2, w2k = W2[g % 4], "W2_%d" % (g % 4)
        cp_, cpk = cpy[g % 4], "cpr%d" % (g % 4)
        P.op("act", lambda e: e.activation(out=cp_[:, :], in_=psrc[:, :], func=AF.Copy), reads=[pkey], writes=[cpk])
        P.op("dve", lambda e: e.tensor_tensor(out=w1[:, :], in0=psrc[:, :], in1=Tr_[:, :], op=ALU.mult), reads=[pkey, trk], writes=[w1k])
        P.op("pool", lambda e: e.tensor_tensor(out=w2[:, :], in0=cp_[:, :], in1=Ti_[:, :], op=ALU.mult), reads=[cpk, tik], writes=[w2k])
        w1r, w1i = view(w1)
        w2r, w2i = view(w2)
        P.op("dve", lambda e: e.tensor_tensor(out=outr, in0=w1r, in1=w2i, op=ALU.subtract), reads=[w1k, w2k], writes=[okey])
        P.op(ieng, lambda e: e.tensor_tensor(out=outi, in0=w2r, in1=w1i, op=ALU.add), reads=[w1k, w2k], writes=[okey])

    vv = lambda t: t[:, :].rearrange("k (j r x) -> k j r x", j=2, r=2)
    v_fwd = lambda t: (vv(t)[:, :, 0, :], vv(t)[:, :, 1, :])
    v_inv = v_fwd

    def fwd_stage1(src3, skey, c0, Ap, apk, ieng="dve"):
        for q2 in range(NPQH // 2):
            g = cnt[0]
            pa, pak = pA1[g % 2], "pA1_%d" % (g % 2)
            for jj in range(2):
                c = c0 + 2 * (2 * q2 + jj)
                P.op("pe", lambda e, c=c, pa=pa, jj=jj: e.matmul(pa[:, jj * 256:(jj + 1) * 256], lhsT=src3[:, c:c + 2, :], rhs=F1b[:, :], start=True, stop=True),
                     reads=[skey, "F1b"], writes=[pak])
            twiddle(pa, pak, TrTr, TiTi, "TrTr", "TiTi", Ap[:, 0, 2 * q2:2 * q2 + 2, :], Ap[:, 1, 2 * q2:2 * q2 + 2, :], apk, v_fwd, ieng=ieng)

    G2r, G2in, G2i = R12b[:, 0:128], R12b[:, 128:256], R12b[:, 256:384]
    R1, R2 = R12b[:, 0:256], R12b[:, 256:512]

    for grp4 in groups:
        for i in range(3):
            P.dma("sp", U[i][:, 1:NTOK + 1], Us[3 * grp4 + i], reads=["Us"], writes=["U%d" % i] + Zkeys(i), key="U%d" % i)
            P.op("pool", lambda e, i=i: e.memset(U[i][:, 0:1], 0.0), reads=["U%d" % i], writes=["U%d" % i] + Zkeys(i))
            P.op("pool", lambda e, i=i: e.memset(U[i][:, NTOK + 1:NTOK + 2], 0.0), reads=["U%d" % i], writes=["U%d" % i])
        for i in range(3):
            ci = 9 * grp4 + 3 * i
            bi = 3 * grp4 + i
            for ch in range(8):
                j0 = ch * 1024
                P.op("dve", lambda e, i=i, j0=j0, ci=ci, bi=bi: e.tensor_scalar(out=tmpc[:, :], in0=U[i][:, j0:j0 + 1024], scalar1=cw[:, ci:ci + 1],
                                                                                scalar2=cb[:, bi:bi + 1], op0=ALU.mult, op1=ALU.add),
                     reads=["U%d" % i, "cw", "cb"], writes=["tmpc"])
                P.op("dve", lambda e, i=i, j0=j0, ci=ci: e.scalar_tensor_tensor(out=tmpc[:, :], in0=U[i][:, j0 + 1:j0 + 1025], scalar=cw[:, ci + 1:ci + 2],
                                                                                in1=tmpc[:, :], op0=ALU.mult, op1=ALU.add),
                     reads=["U%d" % i, "cw", "tmpc"], writes=["tmpc"])
                P.op("dve", lambda e, i=i, j0=j0, ci=ci: e.scalar_tensor_tensor(out=CV[:, j0:j0 + 1024], in0=U[i][:, j0 + 2:j0 + 1026], scalar=cw[:, ci + 2:ci + 3],
                                                                                in1=tmpc[:, :], op0=ALU.mult, op1=ALU.add),
                     reads=["U%d" % i, "cw", "tmpc"], writes=["CV"])
            for pg in range(8):
                for pi in range(8):
                    p = pg * 8 + pi
                    P.op("pe", lambda e, p=p, pi=pi: e.transpose(out=pTz[:, pi, :], in_=CVs[:, :, p], identity=ident[:, :]),
                         reads=["CV", "ident"], writes=["pTz"])
                P.op("act", lambda e, i=i, pg=pg: e.activation(out=Z[i][:, :, pg * 8:(pg + 1) * 8].rearrange("a c p -> a p c"), in_=pTz[:, :, :], func=AF.Copy),
                     reads=["pTz"], writes=["U%d" % i] + Zkeys(i))
        for o in range(2):
            w3c0 = grp4 * 512 + o * 256
            for p in range(64):
                pe_, pek = (pSS, "pSS") if p % 2 == 0 else (pB, "pB")
                wn, wnk = win[p % 2], "win%d" % (p % 2)
                ec, eck = eoc[p % 2], "eoc%d" % (p % 2)
                P.op("pe", lambda e, p=p, w3c0=w3c0, pe_=pe_: e.matmul(pe_[:, 0:256], lhsT=h2s[:, :, p], rhs=fw3b[:, w3c0:w3c0 + 256], start=True, stop=True),
                     reads=["h2T", "fw3b"], writes=[pek])
                P.op("act", lambda e, p=p, grp4=grp4, wn=wn: e.activation(out=wn[:, :], in_=delta[:, grp4 * 128:(grp4 + 1) * 128], func=AF.Exp, scale=Swin[:, p:p + 1]),
                     reads=["delta", "Swin"], writes=[wnk])
                P.op("act", lambda e, pe_=pe_, ec=ec: e.activation(out=ec[:, :], in_=pe_[:, 0:256], func=AF.Copy), reads=[pek], writes=[eck])
                P.op("pool", lambda e, p=p, ec=ec, wn=wn: e.tensor_tensor(out=E3[:, :, p], in0=ec[:, 0:128], in1=wn[:, :], op=ALU.mult), reads=[eck, wnk], writes=["CV"])
                P.op("dve", lambda e, p=p, ec=ec, wn=wn: e.tensor_tensor(out=O3[:, :, p], in0=ec[:, 128:256], in1=wn[:, :], op=ALU.mult), reads=[eck, wnk], writes=["Ob"])
                if p == 0:
                    fc = (2 * grp4 + o) * 128
                    P.op("pe", lambda e, fc=fc: e.matmul(pY[0:1, 0:128], lhsT=h2T[:, 0:1], rhs=fw3f[:, fc:fc + 128], start=True, stop=True),
                         reads=["h2T", "fw3f"], writes=["pY"])
                    hc0 = (2 * grp4 + o) * 128
                    P.op("dve", lambda e, wn=wn: e.tensor_tensor(out=etmp[:, :], in0=pY[0:1, 0:128], in1=wn[0:1, :], op=ALU.mult), reads=["pY", wnk], writes=["etmp"])
                    P.op("dve", lambda e: e.tensor_copy(out=O3[0:1, :, 0], in_=etmp[:, :]), reads=["etmp"], writes=["Ob"])
                    P.op("dve", lambda e, hc0=hc0: e.tensor_tensor(out=E3[0:1, :, 0], in0=etmp[:, :], in1=hbrow[0:1, hc0:hc0 + 128], op=ALU.add),
                         reads=["etmp", "hbrow"], writes=["CV"])
            nsbt = 128 // (2 * NPQH)

            def filt_gen(sbt):
                c0 = sbt * 2 * NPQH
                Hs, hk = HsL[sbt % 2], "Hs%d" % (sbt % 2)
                for which, (src3, skey) in enumerate(((E3, "CV"), (O3, "Ob"))):
                    fwd_stage1(src3, skey, c0, ApF, "ApF")
                    yield
                    for qq in range(NPQH // 4):
                        rr = ApF[:, 0, 4 * qq:4 * qq + 4, :]
                        ri = ApF[:, 1, 4 * qq:4 * qq + 4, :]
                        if which == 0:
                            P.op("pe", lambda e, rr=rr: e.matmul(pSS[:, :], lhsT=G2r, rhs=rr, start=True, stop=False), reads=["R12b", "ApF"], writes=["pSS"])
                            P.op("pe", lambda e, ri=ri: e.matmul(pSS[:, :], lhsT=G2in, rhs=ri, start=False, stop=True), reads=["R12b", "ApF"], writes=["pSS"])
                        else:
                            P.op("pe", lambda e, rr=rr: e.matmul(pSS[:, :], lhsT=G2i, rhs=rr, start=True, stop=False), reads=["R12b", "ApF"], writes=["pSS"])
                            P.op("pe", lambda e, ri=ri: e.matmul(pSS[:, :], lhsT=G2r, rhs=ri, start=False, stop=True), reads=["R12b", "ApF"], writes=["pSS"])
                        P.op("act", lambda e, qq=qq, Hs=Hs, which=which: e.activation(out=Hs[:, which, 4 * qq:4 * qq + 4, :], in_=pSS[:, :].rearrange("k (q x) -> k q x", q=4), func=AF.Copy),
                             reads=["pSS"], writes=[hk])
                    yield

            def data_gen(sbt):
                c0 = sbt * 2 * NPQH
                zk = "Z0_%d" % sbt
                Hs, hk = HsL[sbt % 2], "Hs%d" % (sbt % 2)
                fwd_stage1(Z[0], zk, c0, ApD, "ApD", ieng="pool")
                yield
                for qq in range(NPQH // 4):
                    rr = ApD[:, 0, 4 * qq:4 * qq + 4, :]
                    ri = ApD[:, 1, 4 * qq:4 * qq + 4, :]
                    P.op("pe", lambda e, rr=rr: e.matmul(pXr[:, :], lhsT=G2r, rhs=rr, start=True, stop=False), reads=["R12b", "ApD"], writes=["pXr"])
                    P.op("pe", lambda e, ri=ri: e.matmul(pXr[:, :], lhsT=G2in, rhs=ri, start=False, stop=True), reads=["R12b", "ApD"], writes=["pXr"])
                    P.op("pe", lambda e, rr=rr: e.matmul(pXi[:, :], lhsT=G2i, rhs=rr, start=True, stop=False), reads=["R12b", "ApD"], writes=["pXi"])
                    P.op("pe", lambda e, ri=ri: e.matmul(pXi[:, :], lhsT=G2r, rhs=ri, start=False, stop=True), reads=["R12b", "ApD"], writes=["pXi"])
                    g = cnt[0]
                    cnt[0] += 1
                    w1, w1k = W1[g % 4], "W1_%d" % (g % 4)
                    w2, w2k = W2[g % 4], "W2_%d" % (g % 4)
                    cx, cxk = cpy[g % 4], "cpr%d" % (g % 4)
                    hr = Hs[:, 0, 4 * qq:4 * qq + 4, :].rearrange("k q x -> k (q x)")
                    hi = Hs[:, 1, 4 * qq:4 * qq + 4, :].rearrange("k q x -> k (q x)")
                    yr = Yb[:, 0, 4 * qq:4 * qq + 4, :].rearrange("k q x -> k (q x)")
                    yi = Yb[:, 1, 4 * qq:4 * qq + 4, :].rearrange("k q x -> k (q x)")
                    P.op("act", lambda e, cx=cx: e.activation(out=cx[:, :], in_=pXi[:, :], func=AF.Copy), reads=["pXi"], writes=[cxk])
                    P.op("act", lambda e: e.activation(out=cpy2[:, :], in_=pXr[:, :], func=AF.Copy), reads=["pXr"], writes=["cpy2"])
                    P.op("dve", lambda e, w1=w1, hr=hr: e.tensor_tensor(out=w1[:, :], in0=pXr[:, :], in1=hr, op=ALU.mult), reads=["pXr", hk], writes=[w1k])
                    P.op("pool", lambda e, w2=w2, cx=cx, hi=hi: e.tensor_tensor(out=w2[:, :], in0=cx[:, :], in1=hi, op=ALU.mult), reads=[cxk, hk], writes=[w2k])
                    P.op("dve", lambda e, w1=w1, w2=w2, yr=yr: e.tensor_tensor(out=yr, in0=w1[:, :], in1=w2[:, :], op=ALU.subtract), reads=[w1k, w2k], writes=["Yb"])
                    P.op("pool", lambda e, w1=w1, hr=hr, cx=cx: e.tensor_tensor(out=w1[:, :], in0=cx[:, :], in1=hr, op=ALU.mult), reads=[cxk, hk, "Yb"], writes=[w1k])
                    P.op("pool", lambda e, w2=w2, hi=hi: e.tensor_tensor(out=w2[:, :], in0=cpy2[:, :], in1=hi, op=ALU.mult), reads=["cpy2", hk, "Yb"], writes=[w2k])
                    P.op("dve", lambda e, w1=w1, w2=w2, yi=yi: e.tensor_tensor(out=yi, in0=w1[:, :], in1=w2[:, :], op=ALU.add), reads=[w1k, w2k], writes=["Yb"])
                yield
                for q2 in range(NPQH // 2):
                    for jj in range(2):
                        q = 2 * q2 + jj
                        P.op("pe", lambda e, q=q, jj=jj: e.matmul(pB[:, jj * 256:(jj + 1) * 256], lhsT=Yb[:, 0, q, :], rhs=R1, start=True, stop=False),
                             reads=["Yb", "R12b"], writes=["pB"])
                        P.op("pe", lambda e, q=q, jj=jj: e.matmul(pB[:, jj * 256:(jj + 1) * 256], lhsT=Yb[:, 1, q, :], rhs=R2, start=False, stop=True),
                             reads=["Yb", "R12b"], writes=["pB"])
                    twiddle(pB, "pB", TT2r, TT2i, "TT2r", "TT2i", Bp[:, 0, 2 * q2:2 * q2 + 2, :], Bp[:, 1, 2 * q2:2 * q2 + 2, :], "Bp", v_inv, ieng="pool")
                yield
                for hh in range(NPQH // 4):
                    for r in range(2):
                        rhs = Bp[:, r, 4 * hh:4 * hh + 4, :]
                        lt = IF1b[:, r * 128:(r + 1) * 128]
                        P.op("pe", lambda e, rhs=rhs, lt=lt, r=r: e.matmul(pY[:, :], lhsT=lt, rhs=rhs, start=(r == 0), stop=(r == 1)),
                             reads=["Bp", "IF1b"], writes=["pY"])
                    cc = c0 + 8 * hh
                    zi = 1 if o == 0 else 2
                    zo = 0 if o == 0 else 2
                    okeys = [zk] if o == 0 else ["U2", "Z2_%d" % sbt]
                    P.op("dve", lambda e, cc=cc, zi=zi, zo=zo: e.scalar_tensor_tensor(out=Z[zo][:, cc:cc + 8, :], in0=pY[:, :].rearrange("a (c p) -> a c p", p=64),
                                                                                      scalar=2.0 / NFFT, in1=Z[zi][:, cc:cc + 8, :], op0=ALU.mult, op1=ALU.mult),
                         reads=["pY", "U%d" % zi] + Zkeys(zi), writes=okeys)
                yield

            for _ in filt_gen(0):
                pass
            for sbt in range(nsbt):
                gens = [data_gen(sbt)] + ([filt_gen(sbt + 1)] if sbt + 1 < nsbt else [])
                while gens:
                    for gq in list(gens):
                        try:
                            next(gq)
                        except StopIteration:
                            gens.remove(gq)
        mv = mixT[:, 4 + grp4, :].rearrange("c (a p) -> c a p", p=64)
        for pg in range(4):
            for pi in range(16):
                p = pg * 16 + pi
                P.op("pe", lambda e, p=p, pi=pi: e.matmul(pXr[:, pi * 32:(pi + 1) * 32], lhsT=Z[2][:, :, p], rhs=selb[:, :], start=True, stop=True),
                     reads=["U2", "selb"] + Zkeys(2), writes=["pXr"])
            P.op("act", lambda e, pg=pg, mv=mv: e.activation(out=mv[:, :, pg * 16:(pg + 1) * 16].rearrange("c a p -> c p a"),
                                                             in_=pXr[:, :].rearrange("c (p a) -> c p a", a=32), func=AF.Copy),
                 reads=["pXr"], writes=["mixT"])
    P.build()


def emit_phase2_f(nc, outer, D, mixT, NT=2048):
    P = Prog(nc, sem_stack=outer, prefix="D_")
    x_tok = D("x_tok", [NT, 1024])
    cvec = D("cvec", [128, 8])
    w_ada = D("w_ada2", [1024, 4096])
    b_ada = D("b_ada2", [1, 4096])
    gvec = D("gvec", [3, 1024])
    gmix = D("gmix", [128, 8])
    w_out = D("w_out", [1024, 1024])
    w1 = D("w1", [1024, 4096])
    w2 = D("w2", [4096, 1024])
    out = D("out", [NT, 1024], kind="ExternalOutput")
    x1s = D("x1s", [NT, 1024], kind="Internal")

    sb, ps = P.sb, P.ps
    wout_b = sb("wout_b", [128, 8, 1024], BF16)
    mods = sb("mods", [128, 4096], F32)
    sbc = sb("sbc", [128, 8, 128], F32)
    ones = sb("ones", [128, 128], F32)
    ident = sb("ident", [128, 128], BF16)
    identf = sb("identf", [128, 128], F32)
    stage = [sb("stage%d" % i, [128, 2048], F32) for i in range(2)]
    w1b = [sb("w1b%d" % i, [128, 8, 512], BF16) for i in range(2)]
    w2b = [sb("w2b%d" % i, [128, 4, 1024], BF16) for i in range(2)]
    h2T = sb("h2T", [128, 8, 1024], BF16)
    f2acc = sb("f2acc", [128, 8, 1024], F32)
    brow = f2acc[0:1, 0:4, :].rearrange("p a b -> p (a b)")
    xt = [sb("xt%d" % i, [128, 1024], F32) for i in range(2)]
    sqm = sb("sqm", [128, 8, 128], BF16)
    onescol = sb("onescol", [128, 2], BF16)
    ysb = sb("ysb", [128, 1024], F32)
    tmp = sb("tmp", [128, 1024], F32)
    tmp2 = sb("tmp2", [128, 1024], F32)
    h2b = sb("h2b", [128, 1024], BF16)
    rl = [sb("rl%d" % i, [128, 512], F32) for i in range(2)]
    aT = [sb("aT%d" % i, [128, 4, 512], BF16) for i in range(2)]
    small = sb("small", [128, 32], F32)
    csb = sb("csb", [128, 8], F32)
    gmx = sb("gmx", [128, 8], F32)

    pA = ps("pA", [128, 1024], F32)
    pH = ps("pH", [128, 1024], F32)
    pT = ps("pT", [128, 1024], BF16)
    pM = [ps("pM%d" % i, [128, 512], F32) for i in range(2)]

    P.op("pool", lambda e: e.memset(ones[:, :], 1.0), writes=["ones"])
    P.op("pool", lambda e: e.memset(onescol[:, :], 1.0), writes=["onescol"])
    P.op("pool", lambda e: e.memset(identf[:, :], 0.0), writes=["identf"])
    identd = D("identd", [128, 128])
    P.dma("sp", identf[:, :], identd[:, :], writes=["identf"])
    P.op("dve", lambda e: e.tensor_copy(out=ident[:, :], in_=identf[:, :]), reads=["identf"], writes=["ident"])
    P.dma("sp", csb[:, :], cvec[:, :], writes=["csb"])
    P.dma("sp", gmx[:, :], gmix[:, :], writes=["gmx"])
    P.dma("sp", brow[:, :], b_ada[:, :], writes=["brow"])
    P.op("act", lambda e: e.activation(out=csb[:, :], in_=csb[:, :], func=AF.Silu), reads=["csb"], writes=["csb"])
    for k in range(8):
        P.op("dve", lambda e, k=k: e.tensor_scalar(out=sbc[:, k, :], in0=ones[:, :], scalar1=csb[:, k:k + 1],
                                                    scalar2=None, op0=ALU.mult), reads=["ones", "csb"], writes=["sbc"])
    wa = w_ada.rearrange("(k p) n -> p k n", p=128)
    for blk in range(16):
        st = stage[blk % 2]
        skey = "stage%d" % (blk % 2)
        stv = st[:, :].rearrange("p (k n) -> p k n", k=8)
        P.dma("sp", stv, wa[:, :, blk * 256:(blk + 1) * 256], writes=[skey])
        pm = pM[blk % 2]
        pkey = "pM%d" % (blk % 2)
        for k in range(8):
            P.op("pe", lambda e, k=k, pm=pm, stv=stv: e.matmul(pm[:, 0:256], lhsT=sbc[:, k, :], rhs=stv[:, k, :],
                                                               start=(k == 0), stop=False),
                 reads=["sbc", skey], writes=[pkey])
        P.op("pe", lambda e, pm=pm, blk=blk: e.matmul(pm[:, 0:256], lhsT=ones[0:1, :], rhs=brow[0:1, blk * 256:(blk + 1) * 256],
                                                     start=False, stop=True), reads=["ones", "brow"], writes=[pkey])
        P.op("act", lambda e, pm=pm, blk=blk: e.activation(out=mods[:, blk * 256:(blk + 1) * 256], in_=pm[:, 0:256], func=AF.Copy),
             reads=[pkey], writes=["mods"])
    gb = stage[0][:, :]
    P.dma("sp", gb[:, 0:1024], gvec[0:1, :].to_broadcast((128, 1024)), writes=["stage0"])
    gb1 = stage[1][:, :]
    P.dma("sp", gb1[:, 0:1024], gvec[1:2, :].to_broadcast((128, 1024)), writes=["stage1"])
    P.dma("sp", gb1[:, 1024:2048], gvec[2:3, :].to_broadcast((128, 1024)), writes=["stage1"])
    G1, SH2, G2, G3 = mods[:, 0:1024], mods[:, 1024:2048], mods[:, 2048:3072], mods[:, 3072:4096]
    P.op("dve", lambda e: e.tensor_tensor(out=G1, in0=G1, in1=gb[:, 0:1024], op=ALU.mult), reads=["mods", "stage0"], writes=["mods"])
    P.op("dve", lambda e: e.scalar_tensor_tensor(out=G2, in0=G2, scalar=1.0, in1=gb1[:, 0:1024], op0=ALU.add, op1=ALU.mult),
         reads=["mods", "stage1"], writes=["mods"])
    P.op("dve", lambda e: e.tensor_tensor(out=G3, in0=G3, in1=gb1[:, 1024:2048], op=ALU.mult), reads=["mods", "stage1"], writes=["mods"])
    wo = w_out.rearrange("(k p) n -> p k n", p=128)
    for c4 in range(4):
        st = stage[c4 % 2]
        skey = "stage%d" % (c4 % 2)
        stv = st[:, :].rearrange("p (k n) -> p k n", k=2)
        P.dma("sp", stv, wo[:, 2 * c4:2 * c4 + 2, :], writes=[skey])
        for kk in range(2):
            k = 2 * c4 + kk
            P.op("dve" if kk == 0 else "pool", lambda e, k=k, kk=kk, stv=stv: e.tensor_scalar(
                out=wout_b[:, k, :], in0=stv[:, kk, :], scalar1=gmx[:, k:k + 1], scalar2=None, op0=ALU.mult),
                reads=[skey, "gmx"], writes=["wout_b"])

    def sumsq(src, dst, reads, key, eng="dve"):
        w = src.shape[-1]
        P.op("act", lambda e: e.activation(out=tmp2[:, 0:w], in_=src, func=AF.Square), reads=reads, writes=["tmp2"])
        P.op(eng, lambda e: e.tensor_reduce(out=dst, in_=tmp2[:, 0:w], axis=AX.X, op=ALU.add), reads=["tmp2"] + list(reads), writes=[key])

    nsm = [0]

    def smallcol(n=1):
        c = nsm[0] % 16
        nsm[0] += 1
        return small[:, 2 * c:2 * c + n], "small%d" % c

    w1v = w1.rearrange("(k p) n -> p k n", p=128)
    w2v = w2.rearrange("(c p) n -> p c n", p=128)
    it = [0]
    for half in range(NT // 1024):
        for tile in range(8):
            t0 = half * 1024 + tile * 128
            i = it[0]
            it[0] += 1
            X, xk = xt[i % 2], "xt%d" % (i % 2)
            tl = t0
            MBk = lambda k, tl=tl: mixT[:, k, tl:tl + 128]
            mbk = "mixT"
            P.dma("sp", X[:, :], x_tok[t0:t0 + 128, :], writes=[xk])
            P.op("pool", lambda e, tl=tl: e.tensor_tensor(out=sqm[:, :, :], in0=mixT[:, :, tl:tl + 128], in1=mixT[:, :, tl:tl + 128], op=ALU.mult),
                 reads=["mixT"], writes=["sqm"])
            pst = pM[i % 2]
            pstk = "pM%d" % (i % 2)
            for grp in range(2):
                for k in range(4):
                    P.op("pe", lambda e, grp=grp, k=k, pst=pst: e.matmul(pst[:, grp:grp + 1], lhsT=sqm[:, 4 * grp + k, :], rhs=onescol[:, 0:1],
                                                                        start=(k == 0), stop=(k == 3)), reads=["sqm", "onescol"], writes=[pstk])
            rr, rrk = smallcol(2)
            _rs(P, None, pst[:, 0:2], rr, 512.0, "rmix", [pstk], rrk)
            for nh in range(2):
                for k in range(4):
                    P.op("pe", lambda e, k=k, nh=nh, MBk=MBk: e.matmul(pA[:, nh * 512:(nh + 1) * 512], lhsT=MBk(k),
                                                                     rhs=wout_b[:, k, nh * 512:(nh + 1) * 512],
                                                                     start=(k == 0), stop=(k == 3)),
                         reads=[mbk, "wout_b"], writes=["pA%d" % nh])
                for k in range(4, 8):
                    P.op("pe", lambda e, k=k, nh=nh, MBk=MBk: e.matmul(pH[:, nh * 512:(nh + 1) * 512], lhsT=MBk(k),
                                                                     rhs=wout_b[:, k, nh * 512:(nh + 1) * 512],
                                                                     start=(k == 4), stop=(k == 7)),
                         reads=[mbk, "wout_b"], writes=["pH%d" % nh])
            P.op("act", lambda e, rr=rr: e.activation(out=ysb[:, :], in_=pA[:, :], func=AF.Copy, scale=rr[:, 0:1]),
                 reads=["pA0", "pA1", rrk], writes=["ysb"])
            P.op("dve", lambda e, rr=rr: e.scalar_tensor_tensor(out=ysb[:, :], in0=pH[:, :], scalar=rr[:, 1:2], in1=ysb[:, :],
                                                                op0=ALU.mult, op1=ALU.add),
                 reads=["pH0", "pH1", rrk, "ysb"], writes=["ysb"])
            ss2, ss2k = smallcol(1)
            r2, r2k = smallcol(1)
            sumsq(ysb[:, :], ss2[:, 0:1], ["ysb"], ss2k)
            _rs(P, None, ss2, r2, 1024.0, "ry", [ss2k], r2k)
            P.op("dve", lambda e, r2=r2: e.scalar_tensor_tensor(out=tmp[:, :], in0=ysb[:, :], scalar=r2[:, 0:1], in1=G1,
                                                                op0=ALU.mult, op1=ALU.mult),
                 reads=["ysb", r2k, "mods"], writes=["tmp"])
            P.op("pool", lambda e, X=X: e.tensor_tensor(out=X[:, :], in0=tmp[:, :], in1=X[:, :], op=ALU.add),
                 reads=["tmp", xk], writes=[xk])
            P.dma("pool", x1s[t0:t0 + 128, :], X[:, :], reads=[xk], writes=["x1s%d" % (t0 // 128)])
            ss3, ss3k = smallcol(1)
            r3, r3k = smallcol(1)
            sumsq(X[:, :], ss3[:, 0:1], [xk], ss3k)
            _rs(P, None, ss3, r3, 1024.0, "r1", [ss3k], r3k)
            P.op("dve", lambda e, r3=r3, X=X: e.scalar_tensor_tensor(out=tmp[:, :], in0=X[:, :], scalar=r3[:, 0:1], in1=G2,
                                                                     op0=ALU.mult, op1=ALU.mult),
                 reads=[xk, r3k, "mods"], writes=["tmp"])
            P.op("pool", lambda e: e.tensor_tensor(out=h2b[:, :], in0=tmp[:, :], in1=SH2, op=ALU.add),
                 reads=["tmp", "mods"], writes=["h2b"])
            for k in range(8):
                P.op("pe", lambda e, k=k: e.transpose(out=pT[:, k * 128:(k + 1) * 128], in_=h2b[:, k * 128:(k + 1) * 128],
                                                      identity=ident[:, :]), reads=["h2b", "ident"], writes=["pT"])
            P.op("act", lambda e, tile=tile: e.activation(out=h2T[:, :, tile * 128:(tile + 1) * 128],
                                                          in_=pT[:, :].rearrange("p (k t) -> p k t", k=8), func=AF.Copy),
                 reads=["pT"], writes=["h2T"])
        for j in range(8):
            W1B, w1k = w1b[j % 2], "w1b%d" % (j % 2)
            W2B, w2k = w2b[j % 2], "w2b%d" % (j % 2)
            for hh in range(2):
                st, skey = stage[hh], "stage%d" % hh
                stv = st[:, :].rearrange("p (k n) -> p k n", k=8)
                P.dma("sp", stv, w1v[:, :, j * 512 + hh * 256:j * 512 + (hh + 1) * 256], writes=[skey])
                if hh == 0:
                    P.op("dve", lambda e, W1B=W1B, stv=stv, hh=hh: e.tensor_copy(out=W1B[:, :, hh * 256:(hh + 1) * 256], in_=stv), reads=[skey], writes=[w1k])
                else:
                    P.op("act", lambda e, W1B=W1B, stv=stv, hh=hh: e.activation(out=W1B[:, :, hh * 256:(hh + 1) * 256], in_=stv, func=AF.Copy), reads=[skey], writes=[w1k])
            for hh in range(2):
                st, skey = stage[hh], "stage%d" % hh
                stv = st[:, :].rearrange("p (c n) -> p c n", c=2)
                P.dma("sp", stv, w2v[:, j * 4 + hh * 2:j * 4 + hh * 2 + 2, :], writes=[skey])
                if hh == 0:
                    P.op("dve", lambda e, W2B=W2B, stv=stv, hh=hh: e.tensor_copy(out=W2B[:, hh * 2:hh * 2 + 2, :], in_=stv), reads=[skey], writes=[w2k])
                else:
                    P.op("act", lambda e, W2B=W2B, stv=stv, hh=hh: e.activation(out=W2B[:, hh * 2:hh * 2 + 2, :], in_=stv, func=AF.Copy), reads=[skey], writes=[w2k])
            for tg in range(2):
                AT, atk = aT[tg], "aT%d" % tg
                for hc in range(4):
                    pm, pkey = pM[hc % 2], "pM%d" % (hc % 2)
                    RL, rlk = rl[hc % 2], "rl%d" % (hc % 2)
                    for k in range(8):
                        P.op("pe", lambda e, k=k, hc=hc, tg=tg, pm=pm, W1B=W1B: e.matmul(
                            pm[:, :], lhsT=W1B[:, k, hc * 128:(hc + 1) * 128], rhs=h2T[:, k, tg * 512:(tg + 1) * 512],
                            start=(k == 0), stop=(k == 7)), reads=[w1k, "h2T"], writes=[pkey])
                    P.op("act", lambda e, pm=pm, RL=RL: e.activation(out=RL[:, :], in_=pm[:, :], func=AF.Relu),
                         reads=[pkey], writes=[rlk])
                    P.op("act", lambda e, RL=RL, AT=AT, hc=hc: e.activation(out=AT[:, hc, :], in_=RL[:, :], func=AF.Square),
                         reads=[rlk], writes=[atk])
                for tt in range(4):
                    tile = tg * 4 + tt
                    for nh in range(2):
                        pp, ppk = (pA, "pA%d" % nh) if tt % 2 == 0 else (pH, "pH%d" % nh)
                        for hc in range(4):
                            P.op("pe", lambda e, hc=hc, tt=tt, nh=nh, pp=pp, AT=AT, W2B=W2B: e.matmul(
                                pp[:, nh * 512:(nh + 1) * 512], lhsT=AT[:, hc, tt * 128:(tt + 1) * 128],
                                rhs=W2B[:, hc, nh * 512:(nh + 1) * 512], start=(hc == 0), stop=(hc == 3)),
                                reads=[atk, w2k], writes=[ppk])
                        fv = f2acc[:, tile, nh * 512:(nh + 1) * 512]
                        fk = "f2_%d_%d" % (tile, nh)
                        if j == 0:
                            P.op("dve", lambda e, fv=fv, pp=pp, nh=nh: e.tensor_copy(out=fv, in_=pp[:, nh * 512:(nh + 1) * 512]),
                                 reads=[ppk], writes=[fk])
                        else:
                            P.op("dve", lambda e, fv=fv, pp=pp, nh=nh: e.tensor_tensor(out=fv, in0=pp[:, nh * 512:(nh + 1) * 512], in1=fv, op=ALU.add),
                                 reads=[ppk, fk], writes=[fk])
        for tile in range(8):
            t0 = half * 1024 + tile * 128
            i = it[0]
            it[0] += 1
            X, xk = xt[i % 2], "xt%d" % (i % 2)
            P.dma("sp", X[:, :], x1s[t0:t0 + 128, :], reads=["x1s%d" % (t0 // 128)], writes=[xk], key=xk)
            fkeys = ["f2_%d_%d" % (tile, nh) for nh in range(2)]
            ss4, ss4k = smallcol(1)
            r4, r4k = smallcol(1)
            sumsq(f2acc[:, tile, :], ss4[:, 0:1], fkeys, ss4k)
            _rs(P, None, ss4, r4, 1024.0, "rf", [ss4k], r4k)
            P.op("dve", lambda e, r4=r4, tile=tile: e.scalar_tensor_tensor(out=tmp[:, :], in0=f2acc[:, tile, :], scalar=r4[:, 0:1], in1=G3,
                                                                          op0=ALU.mult, op1=ALU.mult),
                 reads=fkeys + [r4k, "mods"], writes=["tmp"])
            P.op("pool", lambda e, X=X: e.tensor_tensor(out=X[:, :], in0=tmp[:, :], in1=X[:, :], op=ALU.add),
                 reads=["tmp", xk], writes=[xk])
            P.dma("pool", out[t0:t0 + 128, :], X[:, :], reads=[xk], writes=["out%d" % (t0 // 128)])
    P.build()


def build_fused(upto=4, dbg=False):
    nc = bass.Bass("TRN2", target_bir_lowering=False)
    outer = ExitStack()
    D = DramReg(nc)
    mixT = outer.enter_context(nc.sbuf_tensor("mixT_keep", [128, 8, NOWN], BF16))
    with ExitStack() as es_a:
        hTw = es_a.enter_context(nc.sbuf_tensor("hTw_keep", [128, 8, NW], BF16))
        emit_attn_norm_f(nc, outer, D, hTw)
        if upto >= 1:
            emit_attn_f(nc, outer, D, mixT, hTw)
    if upto >= 2:
        emit_hyproj_f(nc, outer, D)
    if upto >= 3:
        emit_hyena_h(nc, outer, D, mixT, groups=(0, 1, 2, 3), prefix="C0_")
    if upto >= 4:
        emit_phase2_f(nc, outer, D, mixT)
    if dbg:
        P = Prog(nc, sem_stack=outer, prefix="E_")
        dd = D("dbg_mixT", [128, 8, NOWN], kind="ExternalOutput", dt=BF16)
        P.dma("sp", dd[:, :, :], mixT[:, :, :], reads=["mixT"], writes=["dbg"])
        P.build()
    outer.close()
    return nc


def fused_inputs(inp):
    f = lambda a: np.ascontiguousarray(a, dtype=np.float32)
    K = _hy_consts()
    K.update(_hy_consts_h())
    C, S = _rope_tables()
    M = _attn_masks()
    hsel = np.zeros((128, 64), np.float32)
    hsel[0:64, 0] = 1.0
    hsel[64:128, 32] = 1.0
    w_in = inp["w_in"][0]
    hy_w = w_in[:, 1536:]
    w_incA = []
    for hp in range(4):
        cs = slice(128 * hp, 128 * hp + 128)
        wq, wk, wv = w_in[:, 0:512][:, cs], w_in[:, 512:1024][:, cs], w_in[:, 1024:1536][:, cs]
        w_incA.append(np.concatenate([wq, _perm_cols(wq), wk, _perm_cols(wk), wv, np.zeros((1024, 128), np.float32)], axis=1))
    w_incA = f(np.stack(w_incA))
    w_incH = f(np.concatenate([hy_w[:, a * 512 + 128 * g:a * 512 + 128 * g + 128] for g in range(4) for a in range(3)], axis=1))
    cwv = inp["conv_w"][0].reshape(3, 3, 512)
    cbv = inp["conv_b"][0].reshape(3, 512)
    cw4 = np.zeros((128, 36), np.float32)
    cb4 = np.zeros((128, 12), np.float32)
    for g in range(4):
        for a in range(3):
            cb4[:, 3 * g + a] = cbv[a, 128 * g:128 * g + 128]
            for t in range(3):
                cw4[:, 9 * g + 3 * a + t] = cwv[t, a, 128 * g:128 * g + 128]
    w3 = inp["filt_w3"][0].reshape(64, 2, 2, 4, 128).transpose(0, 3, 1, 2, 4).reshape(64, 2048)
    hb = inp["hyena_bias"][0]
    hbcol4 = np.zeros((128, 4, 2, 64), np.float32)
    for g in range(4):
        for cp in range(2):
            hbcol4[64 * cp:64 * cp + 64, g, :, :] = hb[:, 128 * g:128 * g + 128][:, cp::2][None, :, :]
    cols2 = np.r_[2048:3072, 3072:6144]
    shared = {
        "w_ada1": f(inp["w_ada"][0][:, 0:2048]), "b_ada1": f(inp["b_ada"][0][0:2048].reshape(16, 128).T),
        "gpre": f(inp["g_pre_mix"][0].reshape(8, 128).T), "w_incA": w_incA, "w_incH": w_incH,
        "maskd": f(M), "hseld": hsel, "identd": np.eye(128, dtype=np.float32),
        "F1cat_h": K["F1cat_h"], "TrTr_h": K["TrTr_h"], "TiTi_h": K["TiTi_h"], "R12": K["R12"], "TT2r_h": K["TT2r_h"], "TT2i_h": K["TT2i_h"],
        "IF1_h": K["IF1_h"], "Swin": K["Swin"], "zembT": K["zembT"],
        "delta4": f(np.broadcast_to(K["deltas"][None, :], (128, 512))),
        "hbrow": f(np.stack([hb[o_, 128 * g_:128 * g_ + 128] for g_ in range(4) for o_ in range(2)]).reshape(1, 1024)),
        "fw1": f(inp["filt_w1"][0]), "fw2": f(inp["filt_w2"][0]), "fw3c4": f(w3),
        "fpar": f(np.stack([inp["filt_b1"][0], inp["filt_freq1"][0], inp["filt_b2"][0], inp["filt_freq2"][0]], axis=1)),
        "cw4": cw4, "cb4": cb4,
        "w_ada2": f(inp["w_ada"][0][:, cols2]), "b_ada2": f(inp["b_ada"][0][cols2][None, :]),
        "gvec": f(np.stack([inp["g_post_mix"][0], inp["g_pre_mlp"][0], inp["g_post_mlp"][0]])),
        "gmix": f(np.concatenate([inp["g_attn_out"][0], inp["g_hyena_out"][0]]).reshape(8, 128).T),
        "w_out": f(inp["w_out"][0]), "w1": f(inp["w_mlp1"][0]), "w2": f(inp["w_mlp2"][0]),
    }
    in_maps = []
    for core in range(8):
        b, j = core // 4, core % 4
        lo = NOWN * j - 1024
        idx = np.arange(lo, lo + NW)
        ok = (idx >= 0) & (idx < NTOK)
        idc = np.clip(idx, 0, NTOK - 1)
        xTw = np.where(ok[None, :], inp["x"][b].T[:, idc], 0.0)
        sel = np.zeros((128, 32), np.float32)
        sel[32 * j + np.arange(32), np.arange(32)] = 1.0
        m = dict(shared)
        m.update({
            "xTw": f(xTw), "validw": f(ok.astype(np.float32).reshape(32, 128).T),
            "ropeCw": f(C[:, idc]), "ropeSw": f(S[:, idc]),
            "xT": f(inp["x"][b].T), "cvec": f(inp["c"][b].reshape(8, 128).T),
            "seld": sel, "x_tok": f(inp["x"][b, NOWN * j:NOWN * (j + 1)]),
        })
        in_maps.append(m)
    return in_maps


def kernel(**inputs):
    inp = {k: np.asarray(v) for k, v in inputs.items()}
    nc = build_fused()
    in_maps = fused_inputs(inp)
    res = run_bass_kernel_spmd(nc, in_maps, core_ids=list(range(8)))
    out = np.zeros((2, NTOK, 1024), np.float32)
    for core in range(8):
        b, j = core // 4, core % 4
        out[b, NOWN * j:NOWN * (j + 1)] = res.results[core]["out"]
    return out
```
